# Optimizing a Trainium2 kernel written in Bass

```python
import math
import jax, jax.numpy as jnp
from jax import lax
import numpy as np

D_MODEL = 1024
BATCH = 8
SEQ = 2048
DEPTH = 2
DEC_BATCH = 128
DEC_SEQ = 4
PAST_LEN = 16384
PAGE_SIZE = 128

MIX_WIDTH = 2 * D_MODEL
SSD_WIDTH = MIX_WIDTH // 2
SSD_HEAD_DIM = 64
SSD_HEADS = SSD_WIDTH // SSD_HEAD_DIM
SSD_GROUPS = 2
SSD_STATE = 128
SSD_CONV_DIM = SSD_WIDTH + 2 * SSD_GROUPS * SSD_STATE
GDN_WIDTH = MIX_WIDTH - SSD_WIDTH
GDN_HEAD_DIM = 128
GDN_HEADS = GDN_WIDTH // GDN_HEAD_DIM
GDN_KEY_WIDTH = GDN_HEADS * GDN_HEAD_DIM
GDN_CONV_DIM = 2 * GDN_KEY_WIDTH + GDN_WIDTH
CONV_K = 4
CHUNK = 64
NORM_EPS = 1e-6
ADA_DIM = 3 * D_MODEL
IN_SIZES = (SSD_WIDTH, SSD_CONV_DIM, SSD_HEADS, GDN_WIDTH, GDN_CONV_DIM, GDN_HEADS, GDN_HEADS)
IN_DIM = sum(IN_SIZES)

kernel_name = "hymba_ssd_gdn_adaln_step"


def _split(a, sizes):
    offs = np.cumsum(sizes)[:-1].tolist()
    return jnp.split(a, offs, axis=-1)


def _rmsnorm(x, w):
    xf = x.astype(jnp.float32)
    return xf * lax.rsqrt(jnp.mean(xf * xf, axis=-1, keepdims=True) + NORM_EPS) * w.astype(jnp.float32)


def _l2norm(x):
    return x * lax.rsqrt(jnp.sum(x * x, axis=-1, keepdims=True) + NORM_EPS)


def _chunk_len(L):
    return CHUNK if L % CHUNK == 0 else L


def _causal_conv(x, buf, w, b):
    L = x.shape[1]
    xp = jnp.concatenate([buf.astype(x.dtype), x], axis=1)
    y = xp[:, 0:L] * w[0]
    for k in range(1, CONV_K):
        y = y + xp[:, k:k + L] * w[k]
    if b is not None:
        y = y + b
    return jax.nn.silu(y), xp[:, L:]


def _ssd(x, dt, A, Bm, Cm, h0):
    Bsz, L, H, P = x.shape
    G, N = Bm.shape[2], Bm.shape[3]
    R = H // G
    Q = _chunk_len(L)
    nc = L // Q
    xc = (x * dt[..., None]).reshape(Bsz, nc, Q, G, R, P)
    a = (dt * A).reshape(Bsz, nc, Q, G, R)
    Bc = Bm.reshape(Bsz, nc, Q, G, N)
    Cc = Cm.reshape(Bsz, nc, Q, G, N)
    acum = jnp.cumsum(a, axis=2)
    tril = jnp.tril(jnp.ones((Q, Q), dtype=bool))[None, None, :, :, None, None]
    seg = acum[:, :, :, None] - acum[:, :, None, :]
    Lmat = jnp.exp(jnp.where(tril, seg, -jnp.inf))
    scores = jnp.einsum('bclgn,bcsgn->bcgls', Cc, Bc)
    y_diag = jnp.einsum('bcgls,bclsgr,bcsgrp->bclgrp', scores, Lmat, xc)
    decay_states = jnp.exp(acum[:, :, -1:] - acum)
    chunk_states = jnp.einsum('bclgn,bclgr,bclgrp->bcgrpn', Bc, decay_states, xc)
    chunk_decay = jnp.exp(acum[:, :, -1])

    def step(h, inp):
        s, d = inp
        return h * d[..., None, None] + s, h

    hT, h_in = lax.scan(step, h0.reshape(Bsz, G, R, P, N),
                        (jnp.moveaxis(chunk_states, 1, 0), jnp.moveaxis(chunk_decay, 1, 0)))
    h_in = jnp.moveaxis(h_in, 0, 1)
    y_off = jnp.einsum('bclgn,bcgrpn,bclgr->bclgrp', Cc, h_in, jnp.exp(acum))
    return (y_diag + y_off).reshape(Bsz, L, H, P), hT.reshape(Bsz, H, P, N)


def _gated_delta(q, k, v, g, beta, S0):
    Bsz, L, H, Dk = q.shape
    Dv = v.shape[-1]
    Q = _chunk_len(L)
    nc = L // Q
    to_c = lambda t: t.reshape(Bsz, nc, Q, H, -1).transpose(0, 3, 1, 2, 4)
    qc, kc, vc = to_c(q), to_c(k), to_c(v)
    gc = g.reshape(Bsz, nc, Q, H).transpose(0, 3, 1, 2)
    bc = beta.reshape(Bsz, nc, Q, H).transpose(0, 3, 1, 2)
    gcum = jnp.cumsum(gc, axis=-1)
    incl = jnp.tril(jnp.ones((Q, Q), dtype=bool))
    strict = jnp.tril(jnp.ones((Q, Q), dtype=bool), k=-1)
    decay = jnp.exp(jnp.where(incl, gcum[..., :, None] - gcum[..., None, :], -jnp.inf))
    kb = kc * bc[..., None]
    M = jnp.where(strict, jnp.einsum('bhcik,bhcjk->bhcij', kb, kc) * decay, 0.0)
    A = M + jnp.eye(Q, dtype=M.dtype)
    rhs = jnp.concatenate([vc * bc[..., None], kb * jnp.exp(gcum)[..., None]], axis=-1)
    sol = lax.linalg.triangular_solve(A, rhs, left_side=True, lower=True, unit_diagonal=True)
    u, w = sol[..., :Dv], sol[..., Dv:]
    attn = jnp.einsum('bhcik,bhcjk->bhcij', qc, kc) * decay
    q_dec = qc * jnp.exp(gcum)[..., None]
    k_dec = kc * jnp.exp(gcum[..., -1:] - gcum)[..., None]
    chunk_decay = jnp.exp(gcum[..., -1])

    def step(S, inp):
        u_c, w_c, at_c, qd_c, kd_c, d_c = inp
        v_new = u_c - jnp.einsum('bhik,bhkv->bhiv', w_c, S)
        o = jnp.einsum('bhik,bhkv->bhiv', qd_c, S) + jnp.einsum('bhij,bhjv->bhiv', at_c, v_new)
        S = S * d_c[..., None, None] + jnp.einsum('bhik,bhiv->bhkv', kd_c, v_new)
        return S, o

    mv = lambda t: jnp.moveaxis(t, 2, 0)
    ST, o = lax.scan(step, S0, (mv(u), mv(w), mv(attn), mv(q_dec), mv(k_dec), mv(chunk_decay)))
    o = o.transpose(1, 0, 3, 2, 4).reshape(Bsz, L, H, Dv)
    return o, ST


def _layer(x, c, conv_s, h_ssm, conv_g, S_gdn, norm_w, w_ada, b_ada, w_in, ssd_conv_w, ssd_conv_b,
           ssd_dt_bias, ssd_a_log, ssd_d, ssd_norm_w, gdn_conv_w, gdn_dt_bias, gdn_a_log, gdn_norm_w, w_out):
    f32 = jnp.float32
    Bsz, L, _ = x.shape
    mod = jax.nn.silu(c.astype(f32)) @ w_ada.astype(f32) + b_ada.astype(f32)
    shift, scale, gate = jnp.split(mod, 3, axis=-1)
    h = _rmsnorm(x, norm_w) * (1.0 + scale[:, None]) + shift[:, None]
    proj = h @ w_in.astype(f32)
    z_s, xbc_raw, dt_raw, z_g, qkv_raw, a_raw, b_raw = _split(proj, IN_SIZES)

    xbc, new_conv_s = _causal_conv(xbc_raw, conv_s.astype(f32), ssd_conv_w.astype(f32), ssd_conv_b.astype(f32))
    xs, Bm, Cm = _split(xbc, (SSD_WIDTH, SSD_GROUPS * SSD_STATE, SSD_GROUPS * SSD_STATE))
    dt = jax.nn.softplus(dt_raw + ssd_dt_bias.astype(f32))
    A = -jnp.exp(ssd_a_log.astype(f32))
    xh = xs.reshape(Bsz, L, SSD_HEADS, SSD_HEAD_DIM)
    y, new_h = _ssd(xh, dt, A, Bm.reshape(Bsz, L, SSD_GROUPS, SSD_STATE),
                    Cm.reshape(Bsz, L, SSD_GROUPS, SSD_STATE), h_ssm.astype(f32))
    y = (y + xh * ssd_d.astype(f32)[:, None]).reshape(Bsz, L, SSD_WIDTH) * jax.nn.silu(z_s)
    y = _rmsnorm(y.reshape(Bsz, L, SSD_GROUPS, SSD_WIDTH // SSD_GROUPS),
                 ssd_norm_w.reshape(SSD_GROUPS, SSD_WIDTH // SSD_GROUPS)).reshape(Bsz, L, SSD_WIDTH)

    qkv, new_conv_g = _causal_conv(qkv_raw, conv_g.astype(f32), gdn_conv_w.astype(f32), None)
    q, k, v = _split(qkv, (GDN_KEY_WIDTH, GDN_KEY_WIDTH, GDN_WIDTH))
    q = _l2norm(q.reshape(Bsz, L, GDN_HEADS, GDN_HEAD_DIM)) * (GDN_HEAD_DIM ** -0.5)
    k = _l2norm(k.reshape(Bsz, L, GDN_HEADS, GDN_HEAD_DIM))
    v = v.reshape(Bsz, L, GDN_HEADS, GDN_HEAD_DIM)
    beta = jax.nn.sigmoid(b_raw)
    g = -jnp.exp(gdn_a_log.astype(f32)) * jax.nn.softplus(a_raw + gdn_dt_bias.astype(f32))
    o, new_S = _gated_delta(q, k, v, g, beta, S_gdn.astype(f32))
    o = _rmsnorm(o, gdn_norm_w) * jax.nn.silu(z_g.reshape(Bsz, L, GDN_HEADS, GDN_HEAD_DIM))
    o = o.reshape(Bsz, L, GDN_WIDTH)

    mix = jnp.concatenate([y, o], axis=-1)
    out = x.astype(f32) + gate[:, None] * (mix @ w_out.astype(f32))
    return out.astype(x.dtype), new_conv_s, new_h, new_conv_g, new_S


def _trunk(x, c, states, weights, final_norm_w):
    new = ([], [], [], [])
    for l in range(DEPTH):
        x, *st = _layer(x, c, states[0][l], states[1][l], states[2][l], states[3][l], *[w[l] for w in weights])
        for lst, s, ref in zip(new, st, states):
            lst.append(s.astype(ref.dtype))
    y = _rmsnorm(x, final_norm_w).astype(x.dtype)
    return y, [jnp.stack(lst) for lst in new]


def setup_inputs(seed: int = 0) -> dict:
    key = jax.random.key(seed)
    ks = iter(jax.random.split(key, 32))
    f32 = jnp.float32
    nrm = lambda shape, s: jax.random.normal(next(ks), shape, f32) * s
    unif = lambda shape, lo, hi: jax.random.uniform(next(ks), shape, f32, lo, hi)

    def dt_bias(n):
        dt = jnp.exp(unif((DEPTH, n), math.log(1e-3), math.log(1e-1)))
        return dt + jnp.log(-jnp.expm1(-dt))

    return {
        "x_prompt": nrm((BATCH, SEQ, D_MODEL), 1.0),
        "x_sample": nrm((DEC_BATCH, DEC_SEQ, D_MODEL), 1.0),
        "state_ssd_conv": nrm((DEPTH, DEC_BATCH, CONV_K - 1, SSD_CONV_DIM), 1.0),
        "state_ssm": nrm((DEPTH, DEC_BATCH, SSD_HEADS, SSD_HEAD_DIM, SSD_STATE), 0.5),
        "state_gdn_conv": nrm((DEPTH, DEC_BATCH, CONV_K - 1, GDN_CONV_DIM), 1.0),
        "state_gdn": nrm((DEPTH, DEC_BATCH, GDN_HEADS, GDN_HEAD_DIM, GDN_HEAD_DIM), 0.5),
        "c_prompt": nrm((BATCH, D_MODEL), 1.0),
        "c_sample": nrm((DEC_BATCH, D_MODEL), 1.0),
        "norm_w": 1.0 + nrm((DEPTH, D_MODEL), 0.02),
        "w_ada": nrm((DEPTH, D_MODEL, ADA_DIM), 0.5 * D_MODEL ** -0.5),
        "b_ada": nrm((DEPTH, ADA_DIM), 0.02),
        "w_in": nrm((DEPTH, D_MODEL, IN_DIM), D_MODEL ** -0.5),
        "ssd_conv_w": nrm((DEPTH, CONV_K, SSD_CONV_DIM), CONV_K ** -0.5),
        "ssd_conv_b": nrm((DEPTH, SSD_CONV_DIM), 0.02),
        "ssd_dt_bias": dt_bias(SSD_HEADS),
        "ssd_a_log": jnp.log(unif((DEPTH, SSD_HEADS), 1.0, 16.0)),
        "ssd_d": 1.0 + nrm((DEPTH, SSD_HEADS), 0.02),
        "ssd_norm_w": 1.0 + nrm((DEPTH, SSD_WIDTH), 0.02),
        "gdn_conv_w": nrm((DEPTH, CONV_K, GDN_CONV_DIM), CONV_K ** -0.5),
        "gdn_dt_bias": dt_bias(GDN_HEADS),
        "gdn_a_log": jnp.log(unif((DEPTH, GDN_HEADS), 1.0, 16.0)),
        "gdn_norm_w": 1.0 + nrm((DEPTH, GDN_HEAD_DIM), 0.02),
        "w_out": nrm((DEPTH, MIX_WIDTH, D_MODEL), MIX_WIDTH ** -0.5),
        "final_norm_w": 1.0 + nrm((D_MODEL,), 0.02),
    }


def reference(x_prompt, x_sample, state_ssd_conv, state_ssm, state_gdn_conv, state_gdn, c_prompt, c_sample,
              norm_w, w_ada, b_ada, w_in, ssd_conv_w, ssd_conv_b, ssd_dt_bias, ssd_a_log, ssd_d, ssd_norm_w,
              gdn_conv_w, gdn_dt_bias, gdn_a_log, gdn_norm_w, w_out, final_norm_w):
    weights = (norm_w, w_ada, b_ada, w_in, ssd_conv_w, ssd_conv_b, ssd_dt_bias, ssd_a_log, ssd_d, ssd_norm_w,
               gdn_conv_w, gdn_dt_bias, gdn_a_log, gdn_norm_w, w_out)
    bp = x_prompt.shape[0]
    zero_states = (
        jnp.zeros((DEPTH, bp) + state_ssd_conv.shape[2:], state_ssd_conv.dtype),
        jnp.zeros((DEPTH, bp) + state_ssm.shape[2:], state_ssm.dtype),
        jnp.zeros((DEPTH, bp) + state_gdn_conv.shape[2:], state_gdn_conv.dtype),
        jnp.zeros((DEPTH, bp) + state_gdn.shape[2:], state_gdn.dtype),
    )
    y_prompt, sp = _trunk(x_prompt, c_prompt, zero_states, weights, final_norm_w)
    y_sample, ss = _trunk(x_sample, c_sample, (state_ssd_conv, state_ssm, state_gdn_conv, state_gdn),
                          weights, final_norm_w)
    return (y_prompt, y_sample, sp[0], sp[1], sp[2], sp[3], ss[0], ss[1], ss[2], ss[3])
```

```python
import contextlib
import numpy as np
import concourse.bass as bass
import concourse.mybir as mybir
from concourse.bass_utils import run_bass_kernel_spmd

F32 = mybir.dt.float32
F32R = mybir.dt.float32r
BF16 = mybir.dt.bfloat16
AF = mybir.ActivationFunctionType
ALU = mybir.AluOpType
AX = mybir.AxisListType

ENGS = ("pe", "dve", "act", "pool", "sp")


class Buf:
    def __init__(self, t, name):
        self.t = t
        self.name = name
        self.st = {}
        self.dma_sem = None
        self.dma_cnt = 0

    def __getitem__(self, k):
        return self.t[k]


class Instr:
    __slots__ = ("eng", "fn", "deps", "signal", "count", "is_dma", "buf_sem", "dma_count")

    def __init__(self, eng, fn):
        self.eng = eng
        self.fn = fn
        self.deps = []
        self.signal = False
        self.count = None
        self.is_dma = False
        self.buf_sem = None
        self.dma_count = None


class Sched:
    def __init__(self, nc):
        self.nc = nc
        self.streams = {e: [] for e in ENGS}
        self.all = []

    def _states(self, buf, key):
        st = buf.st
        if key is None:
            if None not in st:
                st[None] = [None, {}]
            return list(st.values())
        if key not in st:
            if None in st:
                st[key] = [st[None][0], dict(st[None][1])]
            else:
                st[key] = [None, {}]
        res = [st[key]]
        if None in st:
            res.append(st[None])
        return res

    @staticmethod
    def _norm(lst):
        out = []
        for x in lst or []:
            out.append((x, None) if isinstance(x, Buf) else x)
        return out

    def add(self, eng, fn, reads=None, writes=None, dma_buf=None):
        ins = Instr(eng, fn)
        ins.is_dma = dma_buf is not None
        reads = self._norm(reads)
        writes = self._norm(writes)
        deps, raw = [], []
        for buf, key in reads:
            for s in self._states(buf, key):
                if s[0] is not None:
                    deps.append(s[0])
                    raw.append(s[0])
                if getattr(buf, "excl", False):
                    for r in s[1].values():
                        if r.eng != eng:
                            deps.append(r)
        for buf, key in writes:
            for s in self._states(buf, key):
                if s[0] is not None:
                    deps.append(s[0])
                deps.extend(s[1].values())
        fdeps = {}
        for d in deps:
            if d is ins:
                continue
            if (not ins.is_dma) and (not d.is_dma) and d.eng == eng:
                if eng == "pe" or not any(d is r for r in raw):
                    continue
            if d.is_dma:
                fdeps[id(d)] = (d, d.buf_sem.dma_cnt * 16)
            else:
                fdeps[id(d)] = (d, None)
        ins.deps = list(fdeps.values())
        if ins.is_dma:
            ins.buf_sem = dma_buf
            dma_buf.dma_cnt += 1
            ins.dma_count = dma_buf.dma_cnt * 16
        rkey = ("dma", id(ins)) if ins.is_dma else eng
        for buf, key in reads:
            for s in self._states(buf, key):
                s[1][rkey] = ins
        for buf, key in writes:
            for s in self._states(buf, key):
                s[0] = ins
                if key is None or s is not buf.st.get(None):
                    s[1] = {}
        self.streams[eng].append(ins)
        self.all.append(ins)
        return ins

    def barrier(self, dma_bufs_all):
        last = {}
        for e in ("pe", "dve", "act", "pool"):
            for ins in reversed(self.streams[e]):
                if not isinstance(ins, tuple) and not ins.is_dma:
                    last[e] = ins
                    ins.signal = True
                    break
        snap = [(b, b.dma_cnt * 16) for b in dma_bufs_all if b.dma_cnt > 0]
        mark = ("barrier", last, snap)
        for e in ENGS:
            self.streams[e].append(mark)

    def emit(self, final_wait_eng="sp"):
        nc = self.nc
        for ins in self.all:
            for d, _v in ins.deps:
                if not d.is_dma:
                    d.signal = True
        with contextlib.ExitStack() as es:
            eng_sem = {e: es.enter_context(nc.semaphore("s_" + e)) for e in ("pe", "dve", "act", "pool")}
            dma_bufs = []
            for ins in self.all:
                if isinstance(ins, tuple):
                    continue
                if ins.is_dma and ins.buf_sem.dma_sem is None:
                    ins.buf_sem.dma_sem = es.enter_context(nc.semaphore("d_%d" % len(dma_bufs)))
                    dma_bufs.append(ins.buf_sem)
            cnt = {e: 0 for e in eng_sem}
            for ins in self.all:
                if not ins.is_dma and ins.signal:
                    cnt[ins.eng] += 1
                    ins.count = cnt[ins.eng]
            block = es.enter_context(nc.Block())
            engobj = {"pe": block.tensor, "dve": block.vector, "act": block.scalar, "pool": block.gpsimd,
                      "sp": block.sync}
            sched = self

            def make(ename):
                def body(eng):
                    water = {}
                    for ins in sched.streams[ename]:
                        if isinstance(ins, tuple):
                            _, last, snap = ins
                            for e2, li in last.items():
                                if e2 != ename and water.get(("e", e2), 0) < li.count:
                                    water[("e", e2)] = li.count
                                    eng.wait_ge(eng_sem[e2], li.count)
                            for b_, v_ in snap:
                                if water.get(("d", id(b_)), 0) < v_:
                                    water[("d", id(b_))] = v_
                                    eng.wait_ge(b_.dma_sem, v_)
                            continue
                        need = {}
                        for d, dv in ins.deps:
                            if d.is_dma:
                                key, sem, val = ("d", id(d.buf_sem)), d.buf_sem.dma_sem, dv
                            else:
                                key, sem, val = ("e", d.eng), eng_sem[d.eng], d.count
                            if need.get(key, (None, 0))[1] < val:
                                need[key] = (sem, val)
                        for key, (sem, val) in need.items():
                            if water.get(key, 0) >= val:
                                continue
                            water[key] = val
                            eng.wait_ge(sem, val)
                        bi = ins.fn(eng)
                        if ins.is_dma:
                            bi.then_inc(ins.buf_sem.dma_sem, 16)
                        elif ins.signal:
                            bi.then_inc(eng_sem[ename], 1)
                    if ename == final_wait_eng:
                        for b in dma_bufs:
                            eng.wait_ge(b.dma_sem, b.dma_cnt * 16)
                return body

            for ename in ENGS:
                engobj[ename](make(ename))


D = 1024
SEQ = 2048
NT = SEQ // 128
NSB = 16
SL = 4
DEPTH = 2
IN_DIM = 6688
EPS = 1e-6
NEG = -30000.0
OFF_ZS, OFF_XBC, OFF_DT, OFF_ZG, OFF_QKV, OFF_A, OFF_B = 0, 1024, 2560, 2576, 3600, 6672, 6680


class Cfg:
    def __init__(self, name, T, nseq, L, Q, chain):
        self.name, self.T, self.nseq, self.L, self.Q, self.chain = name, T, nseq, L, Q, chain
        self.nch = T // Q
        self.nlev = {64: 5, 4: 1}[Q]


CP = Cfg("p", 128, 1, 128, 64, True)
CS = Cfg("s", 64, NSB, SL, 4, False)


def host_consts():
    c = {"ident": np.eye(128, dtype=np.float32), "ones": np.ones((128, 128), np.float32)}
    for cfg in (CP, CS):
        T, Q, nch = cfg.T, cfg.Q, cfg.nch
        ch = np.arange(T) // Q
        same = ch[:, None] == ch[None, :]
        idx = np.arange(T)
        le = idx[:, None] <= idx[None, :]
        lt = idx[:, None] < idx[None, :]
        n = cfg.name
        c["tri_" + n] = (same & le).astype(np.float32)
        c["blk_" + n] = same.astype(np.float32)
        c["nmI_" + n] = np.where(same & le, 0.0, NEG).astype(np.float32)
        c["nmS_" + n] = np.where(same & lt, 0.0, NEG).astype(np.float32)
        c["nmSr_" + n] = np.ascontiguousarray(c["nmS_" + n].T)
        cm = (ch[None, :] == np.arange(nch)[:, None]).astype(np.float32)
        c["cmf_" + n] = np.ascontiguousarray(np.broadcast_to(cm[None], (128, nch, T))).astype(np.float32)
        c["cmT_" + n] = np.ascontiguousarray(cm.T)
    e = np.zeros((17, 128), np.float32)
    e[0, :] = 1.0
    c["E_p"] = e
    e = np.zeros((17, 64), np.float32)
    for t in range(64):
        e[1 + t // SL, t] = 1.0
    c["E_s"] = e
    return c


def bl(ap, n):
    return ap.unsqueeze(len(ap.shape)).to_broadcast(list(ap.shape) + [n])


def bm(ap, n):
    return ap.unsqueeze(1).to_broadcast([ap.shape[0], n] + list(ap.shape[1:]))


def build(debug=(), limit=None):
    limit = limit or {}
    nc = bass.Bass("TRN2", target_bir_lowering=False)
    S = Sched(nc)
    es = contextlib.ExitStack()
    consts_np = host_consts()

    def din(name, shape):
        return nc.dram_tensor(name, list(shape), F32, kind="ExternalInput").ap()

    def dout(name, shape):
        return nc.dram_tensor(name, list(shape), F32, kind="ExternalOutput").ap()

    I = {}
    I["xp"] = din("xp", [SEQ, D])
    I["xs"] = din("xs", [NSB * SL, D])
    I["call"] = din("call", [17, D])
    I["st_sc"] = din("st_sc", [DEPTH, NSB, 3, 1536])
    I["st_ssm"] = din("st_ssm", [DEPTH, NSB, 16, 64, 128])
    I["st_gc"] = din("st_gc", [DEPTH, NSB, 3, 3072])
    I["st_gdn"] = din("st_gdn", [DEPTH, NSB, 8, 128, 128])
    wshapes = {"norm_w": [DEPTH, D], "w_ada": [DEPTH, D, 3 * D], "b_ada": [DEPTH, 3 * D], "w_in": [DEPTH, D, IN_DIM],
               "ssd_conv_w": [DEPTH, 4, 1536], "ssd_conv_b": [DEPTH, 1536], "ssd_dt_bias": [DEPTH, 16],
               "ssd_a_log": [DEPTH, 16], "ssd_d": [DEPTH, 16], "ssd_norm_w": [DEPTH, 1024],
               "gdn_conv_w": [DEPTH, 4, 3072], "gdn_dt_bias": [DEPTH, 8], "gdn_a_log": [DEPTH, 8],
               "gdn_norm_w": [DEPTH, 128], "w_out": [DEPTH, 2048, D], "final_norm_w": [1, D]}
    for k, shp in wshapes.items():
        I[k] = din(k, shp)
    for k, v in consts_np.items():
        I["c_" + k] = din("c_" + k, v.shape)
    O = {}
    O["y_p"] = dout("y_p", [SEQ, D])
    O["y_s"] = dout("y_s", [NSB * SL, D])
    O["ncs_p"] = dout("ncs_p", [DEPTH, 3, 1536])
    O["nssm_p"] = dout("nssm_p", [DEPTH, 1024, 128])
    O["ngc_p"] = dout("ngc_p", [DEPTH, 3, 3072])
    O["ngdn_p"] = dout("ngdn_p", [DEPTH, 8, 128, 128])
    O["ncs_s"] = dout("ncs_s", [DEPTH, NSB * 3, 1536])
    O["nssm_s"] = dout("nssm_s", [DEPTH, NSB, 1024, 128])
    O["ngc_s"] = dout("ngc_s", [DEPTH, NSB * 3, 3072])
    O["ngdn_s"] = dout("ngdn_s", [DEPTH, NSB, 8, 128, 128])
    NROW = SEQ + NSB * SL
    scr = [Buf(nc.dram_tensor("scr%d" % i, [NROW, D], F32, kind="Internal").ap(), "scr%d" % i) for i in range(3)]
    xin_ext = None

    allb = []

    def sb(name, shape, dt=F32):
        b_ = Buf(es.enter_context(nc.sbuf_tensor(name, list(shape), dt)), name)
        allb.append(b_)
        return b_

    banks = [Buf(es.enter_context(nc.psum_tensor("bank%d" % i, [128, 512], F32)), "bank%d" % i) for i in range(8)]
    for b_ in banks:
        b_.excl = True
    pstate = {"i": 0, "pinned": set()}

    def pget():
        while True:
            i = pstate["i"] % 8
            pstate["i"] += 1
            if i not in pstate["pinned"]:
                return banks[i]

    def ppin(b):
        pstate["pinned"].add(banks.index(b))

    def punpin(b):
        pstate["pinned"].discard(banks.index(b))

    def pf(b, shape, p=128):
        n = int(np.prod(shape))
        ap = b[:p, 0:n]
        if len(shape) == 1:
            return ap
        names = " ".join("a%d" % i for i in range(len(shape)))
        kw = {"a%d" % i: shape[i] for i in range(1, len(shape))}
        return ap.rearrange("p (%s) -> p %s" % (names, names), **kw)

    def pb(b, shape, p=128):
        n = int(np.prod(shape))
        ap = b[:p, :].bitcast(BF16)[:, 0:n]
        if len(shape) == 1:
            return ap
        names = " ".join("a%d" % i for i in range(len(shape)))
        kw = {"a%d" % i: shape[i] for i in range(1, len(shape))}
        return ap.rearrange("p (%s) -> p %s" % (names, names), **kw)

    def MM(out, lhsT, rhs, R, W, start=True, stop=True, **kw):
        S.add("pe", lambda e: e.matmul(out, lhsT=lhsT, rhs=rhs, start=start, stop=stop, **kw), reads=R, writes=W)

    def TR(out, in_, ident, R, W):
        S.add("pe", lambda e: e.transpose(out=out, in_=in_, identity=ident), reads=R, writes=W)

    def ACT(out, in_, func, R, W, **kw):
        S.add("act", lambda e: e.activation(out=out, in_=in_, func=func, **kw), reads=R, writes=W)

    def TT(out, in0, in1, op, R, W, eng="dve"):
        S.add(eng, lambda e: e.tensor_tensor(out=out, in0=in0, in1=in1, op=op), reads=R, writes=W)

    def TS(out, in0, s1, s2, op0, op1, R, W, eng="dve"):
        S.add(eng, lambda e: e.tensor_scalar(out=out, in0=in0, scalar1=s1, scalar2=s2, op0=op0, op1=op1), reads=R, writes=W)

    def STT(out, in0, scalar, in1, op0, op1, R, W):
        S.add("dve", lambda e: e.scalar_tensor_tensor(out=out, in0=in0, scalar=scalar, in1=in1, op0=op0, op1=op1), reads=R, writes=W)

    def CP_(out, in_, R, W, eng="dve"):
        S.add(eng, lambda e: e.tensor_copy(out=out, in_=in_), reads=R, writes=W)

    def RECIP(out, in_, R, W):
        S.add("dve", lambda e: e.reciprocal(out=out, in_=in_), reads=R, writes=W)

    def RED(out, in_, R, W):
        S.add("dve", lambda e: e.tensor_reduce(out=out, in_=in_, axis=AX.X, op=ALU.add), reads=R, writes=W)

    def MSET(ap, val, W, eng="pool"):
        S.add(eng, lambda e: e.memset(ap, val), writes=W)

    def LD(out, in_, buf, R=None, nonc=False):
        if nonc:
            S.add("sp", lambda e: e.dma_start(out=out, in_=in_, allow_slow_non_contiguous=True), reads=R, writes=[buf], dma_buf=buf)
        else:
            S.add("sp", lambda e: e.dma_start(out=out, in_=in_), reads=R, writes=[buf], dma_buf=buf)

    sto_n = [0]
    sto_dummy = Buf(None, "sto_dummy")

    def STO(out, in_, buf, W=None, extra_reads=None):
        S.add("sp", lambda e: e.dma_start(out=out, in_=in_), reads=[buf] + (extra_reads or []), writes=W, dma_buf=buf)

    dbg_out = {}
    mod_stacks = []
    cur_es = [es]
    all_bufs = []

    def DBG(name, buf, ap, shape):
        if name in debug and name not in dbg_out:
            o = dout("dbg_" + name, shape)
            dbg_out[name] = o
            if ap.dtype != F32:
                tmp = Buf(cur_es[0].enter_context(nc.sbuf_tensor("dbgt_" + name, list(shape), F32)), "dbgt_" + name)
                all_bufs.append(tmp)
                CP_(tmp[:], ap, [buf], [tmp])
                STO(o, tmp[:], tmp)
            else:
                STO(o, ap, buf)

    C = {}
    for k, v in consts_np.items():
        if k.startswith("cmf_"):
            continue
        C[k] = sb("k_" + k, v.shape)
        LD(C[k][:], I["c_" + k], C[k])
    identf = C["ident"]
    onesf = C["ones"]
    identb = sb("identb", [128, 128], BF16)
    CP_(identb[:], identf[:], [identf], [identb])
    onesb = sb("onesb", [128, 128], BF16)
    CP_(onesb[:], onesf[:], [onesf], [onesb])
    cfg_pending = []
    for cfg in (CP, CS):
        n = cfg.name
        cfg.tri, cfg.blk, cfg.nmI, cfg.nmS, cfg.nmSr, cfg.cmT = (C[k + n] for k in ("tri_", "blk_", "nmI_", "nmS_", "nmSr_", "cmT_"))
        cfg.cmf = sb("cmfb_" + n, [128, cfg.nch, cfg.T], BF16)
        cfg.ncmf = sb("ncmfb_" + n, [128, cfg.nch, cfg.T], BF16)
        cfg_pending.append(cfg)
        cfg.cmTb = sb("cmTb_" + n, [cfg.T, cfg.nch], BF16)
        CP_(cfg.cmTb[:], cfg.cmT[:], [cfg.cmT], [cfg.cmTb])

    SW = 128
    stage = [sb("stage%d" % i, [128, 8, SW]) for i in range(2)]
    stage_i = [0]
    WO = sb("wo", [128, 8, D], BF16)
    DIAG = sb("diag", [128, 12, 4, 128], BF16)
    cwT = sb("cwT", [128, 12, 4])
    cbrow = sb("cbrow", [1, 1536], BF16)
    nwT = sb("nwT", [128, 8])
    scT = sb("scT", [128, 8, 17])
    shT = sb("shT", [128, 8, 17])
    gate_bc = {"p": sb("gate_p", [128, D]), "s": sb("gate_s", [64, D])}
    vec16 = sb("vec16", [128, 4, 16])
    vec8 = sb("vec8", [128, 2, 8])
    scTb = sb("scTb", [128, 8, 17], BF16)
    badaT = sb("badaT", [128, 24])
    nw_in = sb("nw_in", [128, 8])
    xt_slots = [sb("xt%d" % i, [128, D]) for i in range(2)]
    xa_slots = [sb("xa%d" % i, [128, D]) for i in range(2)]
    junk = sb("junk", [128, D], BF16)
    xn = sb("xn", [128, D], BF16)
    hT_slots = [sb("hT%d" % i, [128, 8, 128], BF16) for i in range(2)]
    smp = sb("smp", [128, 2])
    cur = {"hT": hT_slots[0], "fT": None}
    sm1 = sb("sm1", [128, 8])
    raw = sb("raw", [128, 12, 132], BF16)
    tailf = sb("tailf", [128, 12, 48])
    tailo = sb("tailo", [48, 1536])
    cst_in = tailo
    fT_slots = [sb("fT%d" % i, [128, 12, 128], BF16) for i in range(2)]
    cur["fT"] = fT_slots[0]
    big = [sb("big%d" % i, [128, D]) for i in range(4)]
    bigb = [sb("bigb%d" % i, [128, D], BF16) for i in range(3)]
    xo = big[3]
    junk2 = bigb[1]
    mixT_buf = bigb[0]
    call_sb, sil_c, bgate, gate17, cbrow_f = big[0], bigb[0], big[1], big[2], tailo
    hTf_t = big[0]
    for cfg in cfg_pending:
        n_ = cfg.nch * cfg.T
        tmpv = big[0][:, 0:n_].rearrange("p (c t) -> p c t", t=cfg.T)
        LD(tmpv, I["c_cmf_" + cfg.name], big[0])
        CP_(cfg.cmf[:], tmpv, [big[0]], [cfg.cmf])
        TS(cfg.ncmf[:], tmpv, -1.0, None, ALU.mult, ALU.bypass, [big[0]], [cfg.ncmf])

    def load_cols(dst, dcol, src, ncols):
        c0 = 0
        while c0 < ncols:
            n = min(SW, ncols - c0)
            st = stage[stage_i[0] % 2]
            stage_i[0] += 1
            LD(st[:, :, 0:n], src[:, c0:c0 + n].rearrange("(k p) n -> p k n", p=128), st)
            CP_(dst[:, :, dcol + c0:dcol + c0 + n], st[:, :, 0:n], [st], [dst], eng="pool")
            c0 += n

    def load_wo(src_rows, nk, scale_ap_fn):
        for half in range(D // SW):
            st = stage[stage_i[0] % 2]
            stage_i[0] += 1
            LD(st[:, 0:nk, :], src_rows[:, half * SW:(half + 1) * SW].rearrange("(k p) n -> p k n", p=128), st)
            for k in range(nk):
                TS(WO[:, k, half * SW:(half + 1) * SW], st[:, k, :], scale_ap_fn(k), 1.0, ALU.mult, ALU.mult,
                   [st, nwT], [WO], eng="pool")

    def build_diag(convw, c0, nchunks):
        for k_ in range(4):
            LD(cwT[:, 0:nchunks, k_], convw[k_, c0 * 128:(c0 + nchunks) * 128].rearrange("(c p) -> p c", p=128), cwT, nonc=True)
        for c in range(nchunks):
            TT(DIAG[:, c, :, :], bm(identf[:], 4), bl(cwT[:, c, :], 128), ALU.mult, [identf, cwT], [DIAG], eng="pool")

    def do_mod(l):
        LD(call_sb[:17, :], I["call"], call_sb)
        ACT(sil_c[:17, :], call_sb[:17, :], AF.Silu, [call_sb], [sil_c])
        b = pget()
        for c in range(8):
            TR(pb(b, [8, 32])[:, c, 0:17], sil_c[:17, c * 128:(c + 1) * 128], identb[:17, :17], [sil_c, identb], [b])
        CP_(scTb[:], pb(b, [8, 32])[:, :, 0:17], [b], [scTb])
        LD(badaT[:], I["b_ada"][l].rearrange("(c p) -> p c", p=128), badaT, nonc=True)
        LD(nw_in[:], I["norm_w"][l].rearrange("(c p) -> p c", p=128), nw_in, nonc=True)
        LD(bgate[0:1, :], I["b_ada"][l:l + 1, 2048:3072], bgate)
        nblk = 3 * D // SW
        per = D // SW
        modes = contextlib.ExitStack()
        wb = Buf(modes.enter_context(nc.sbuf_tensor("modwb%d" % l, [128, 8, SW], BF16)), "modwb")
        all_bufs.append(wb)
        for blk in range(nblk):
            st = stage[stage_i[0] % 2]
            stage_i[0] += 1
            LD(st[:], I["w_ada"][l][:, blk * SW:(blk + 1) * SW].rearrange("(k p) n -> p k n", p=128), st)
            CP_(wb[:, :, 0:SW], st[:], [st], [wb], eng="pool")
            if blk < 2 * per:
                b = pget()
                nj = SW // 128
                for j in range(nj):
                    for k in range(8):
                        MM(pf(b, [nj, 17])[:, j, :], wb[:, k, j * 128:(j + 1) * 128], scTb[:, k, :], [wb, scTb], [b],
                           start=(k == 0), stop=(k == 7))
                dst = shT if blk < per else scT
                cc = (blk % per) * nj
                TT(dst[:, cc:cc + nj, :], pf(b, [nj, 17]), bl(badaT[:, blk * nj:blk * nj + nj], 17), ALU.add, [b, badaT], [dst])
            else:
                g = pget()
                gc0 = (blk - 2 * per) * SW
                for k in range(8):
                    MM(g[:17, 0:SW], scTb[:, k, :], wb[:, k, 0:SW], [wb, scTb], [g], start=(k == 0), stop=False)
                MM(g[:17, 0:SW], onesf[0:1, 0:17], bgate[0:1, gc0:gc0 + SW], [onesf, bgate], [g], start=False, stop=True)
                CP_(gate17[:17, gc0:gc0 + SW], g[:17, 0:SW], [g], [gate17])
        TS(scT[:], scT[:], 1.0, None, ALU.add, ALU.bypass, [scT], [scT])
        TT(scT[:], scT[:], bl(nw_in[:], 17), ALU.mult, [scT, nw_in], [scT])
        for n, E, T in (("p", C["E_p"], 128), ("s", C["E_s"], 64)):
            for half in range(2):
                b = pget()
                MM(b[:T, :], E[:, :T], gate17[:17, half * 512:(half + 1) * 512], [E, gate17], [b])
                CP_(gate_bc[n][:T, half * 512:(half + 1) * 512], b[:T, :], [b], [gate_bc[n]])
        mod_stacks.append(modes)

    def rows(ti):
        return (ti * 128, 128) if ti < NT else (SEQ, 64)

    def src_ap(layer_in, ti):
        r0, T = rows(ti)
        if layer_in is None:
            return (I["xp"][r0:r0 + T, :] if ti < NT else I["xs"][:, :]), None
        return layer_in[r0:r0 + T, :], (layer_in, ti)

    def prologue(cfg, xt):
        T = cfg.T
        hT = cur["hT"]
        ACT(junk[:T, :], xt[:T, :], AF.Square, [xt], [junk, smp], accum_out=smp[:T, 0:1])
        TS(smp[:T, 0:1], smp[:T, 0:1], 1.0 / D, EPS, ALU.mult, ALU.add, [smp], [smp])
        ACT(smp[:T, 0:1], smp[:T, 0:1], AF.Sqrt, [smp], [smp])
        RECIP(smp[:T, 0:1], smp[:T, 0:1], [smp], [smp])
        ACT(xn[:T, :], xt[:T, :], AF.Copy, [xt, smp], [xn], scale=smp[:T, 0:1])
        b = pget()
        for c in range(8):
            TR(pb(b, [8, T])[:, c, :], xn[:T, c * 128:(c + 1) * 128], identb[:T, :T], [xn, identb], [b])
        if cfg.nseq == 1:
            for c in range(8):
                ACT(hT[:, c, :T], pb(b, [8, T])[:, c, :], AF.Identity, [b, scT, shT], [hT], scale=scT[:, c, 0:1], bias=shT[:, c, 0:1])
        else:
            sc, sh = bl(scT[:, :, 1:17], SL), bl(shT[:, :, 1:17], SL)
            o1 = hTf_t[:, 0:8 * T].rearrange("p (c b l) -> p c b l", c=8, l=SL)
            o2 = hT[:, :, :T].rearrange("p c (b l) -> p c b l", l=SL)
            i1 = pb(b, [8, NSB, SL])
            TT(o1, i1, sc, ALU.mult, [b, scT], [hTf_t])
            TT(o2, o1, sh, ALU.add, [hTf_t, shT], [hT])

    def inproj_conv(cfg, l, wcol0, nchunks, bias_row, st_conv_in, conv_out, conv_out_s, ch0, is_last, first):
        T, L, nseq = cfg.T, cfg.L, cfg.nseq
        hT, fT = cur["hT"], cur["fT"]
        rawv = raw[:, :, 0:nseq * (L + 3)].rearrange("p c (b l) -> p c b l", l=L + 3)
        if cfg.nseq == 1:
            if first:
                MSET(raw[:, :, 0:3], 0.0, [raw])
            else:
                CP_(raw[:, 0:nchunks, 0:3], raw[:, 0:nchunks, L:L + 3], [raw], [raw])
        else:
            LD(cst_in[:, 0:nchunks * 128], st_conv_in[:, :, ch0 * 128:(ch0 + nchunks) * 128].rearrange("b k c -> (b k) c"), cst_in)
            for g0 in range(0, nchunks, 8):
                b = pget()
                ng = min(8, nchunks - g0)
                for c in range(ng):
                    TR(pf(b, [8, 48])[:, c, :], cst_in[:, (g0 + c) * 128:(g0 + c + 1) * 128], identf[:48, :48], [cst_in, identf], [b])
                CP_(rawv[:, g0:g0 + ng, :, 0:3], pf(b, [8, 48])[:, 0:ng, :].rearrange("p c (b k) -> p c b k", k=3), [b], [raw])
        if limit.get("stopC", 99) <= 1:
            return
        for g0 in range(0, nchunks, 4):
            b = pget()
            for c in range(4):
                for k in range(8):
                    MM(pf(b, [4, T])[:, c, :], PA["WB"][:, k, wcol0 + (g0 + c) * 128: wcol0 + (g0 + c + 1) * 128], hT[:, k, :T],
                       [PA["WB"], hT], [b], start=(k == 0), stop=(k == 7))
            if nseq == 1:
                ACT(raw[:, g0:g0 + 4, 3:3 + L], pf(b, [4, T]), AF.Copy, [b], [raw])
            else:
                for c in range(4):
                    ACT(rawv[:, g0 + c, :, 3:3 + L], pf(b, [4, T])[:, c, :].rearrange("p (b l) -> p b l", l=L), AF.Copy, [b], [raw])
            if is_last and limit.get("tail", True):
                for c in range(4):
                    ACT(tailf[:, g0 + c, 0:nseq * 3].rearrange("p (b k) -> p b k", k=3),
                        pf(b, [4, T])[:, c, :].rearrange("p (b l) -> p b l", l=L)[:, :, L - 3:L], AF.Copy, [b], [tailf])
        if limit.get("stopC", 99) <= 2:
            return
        for g0 in range(0, nchunks, 4):
            b = pget()
            for c in range(4):
                o = pf(b, [4, T])[:, c, :]
                if nseq > 1:
                    o = o.rearrange("p (b l) -> p b l", l=L)
                for k in range(4):
                    MM(o, DIAG[:, g0 + c, k, :], rawv[:, g0 + c, :, k:k + L] if nseq > 1 else raw[:, g0 + c, k:k + L],
                       [DIAG, raw], [b], start=(k == 0), stop=(k == 3 and bias_row is None))
                if bias_row is not None:
                    MM(pf(b, [4, T])[:, c, :], bias_row[0:1, (g0 + c) * 128:(g0 + c + 1) * 128], onesb[0:1, :T], [bias_row, onesb], [b],
                       start=False, stop=True)
            ACT(fT[:, g0:g0 + 4, :T], pf(b, [4, T]), AF.Silu, [b], [fT])
        if limit.get("stopC", 99) <= 3:
            return
        if is_last:
            n3 = nseq * 3
            for g0 in range(0, nchunks, 4):
                b = pget()
                for c in range(4):
                    TR(pf(b, [4, 128], p=n3)[:, c, :], tailf[:, g0 + c, 0:n3], identf[:, :], [tailf, identf], [b])
                CP_(tailo[:n3, g0 * 128:(g0 + 4) * 128], b[:n3, :], [b], [tailo])
            dst = conv_out[l][:, ch0 * 128:(ch0 + nchunks) * 128] if nseq == 1 else conv_out_s[l][:, ch0 * 128:(ch0 + nchunks) * 128]
            STO(dst, tailo[:n3, 0:nchunks * 128], tailo)

    def cum_and_tot(cfg, a_ap, a_buf, nh, out_c, out_dbc, pw):
        T, nch = cfg.T, cfg.nch
        b = pget()
        MM(b[:T, 0:nh], cfg.tri[:, :], a_ap, [cfg.tri, a_buf], [b])
        MM(b[:T, nh:2 * nh], cfg.blk[:, :], a_ap, [cfg.blk, a_buf], [b])
        ACT(out_c[:T, 0:2 * nh], b[:T, 0:2 * nh], AF.Copy, [b], [out_c])
        b2 = pget()
        MM(b2[:nch, 0:nh], cfg.cmT[:, :], a_ap, [cfg.cmT, a_buf], [b2])
        CP_(tot[:nch, 0:nh], b2[:nch, 0:nh], [b2], [tot])
        TT(totx[:nch, 0:nch * nh].rearrange("p (c h) -> p c h", h=nh), bm(tot[:nch, 0:nh], nch), bl(identf[:nch, :nch], nh), ALU.mult,
           [tot, identf], [totx])
        b3 = pget()
        MM(b3[:pw, 0:nch * nh], onesf[:nch, :pw], totx[:nch, 0:nch * nh], [onesf, totx], [b3])
        ACT(out_dbc[:pw, 0:nch * nh], b3[:pw, 0:nch * nh], AF.Exp, [b3], [out_dbc])

    tot = sb("tot", [16, 16])
    totx = sb("totx", [16, 256])
    sm = {k: sb("sm_" + k, [128, 64]) for k in ("a", "b", "c", "d", "e")}
    dbc = sb("dbc", [128, 256])
    acT = sb("acT", [16, 128])

    PA = {}

    def alloc_A(pes, tag):
        def sbp(name, shape, dt=F32):
            return Buf(pes.enter_context(nc.sbuf_tensor(name + tag, list(shape), dt)), name)
        PA.update(stf=sbp("stf", [128, D]), stb=sbp("stb", [128, D], BF16), hn=sbp("hn", [128, 8, 128]),
                  stf_s=sbp("stf_s", [128, D]), stb_s=sbp("stb_s", [128, D], BF16), Gs=sbp("Gs", [128, 2, 128]),
                  earg=sbp("earg", [128, 4, 128]), Ee=sbp("Ee", [128, 4, 128]), WT=sbp("WT", [128, 16, 128], BF16),
                  Btok=sbp("Btok", [128, 256], BF16), Btm=sbp("Btm", [128, 2, 256], BF16), CTm=sbp("CTm", [128, 2, 2, 128], BF16),
                  WB=sbp("WB", [128, 8, 2576], BF16))
    mslot = [0]

    def ssd_state_update(cfg, c, st_in, stb_in, st_out, yo, xdec, first_c, last_c):
        T, nch = cfg.T, cfg.nch
        Btm, CTm, Btok = PA["Btm"], PA["CTm"], PA["Btok"]
        fT = cur["fT"]
        sl = mslot[0] % 2
        mslot[0] += 1
        TS(Btm[:T, sl, :], Btok[:T, :], cfg.cmT[:, c:c + 1], 1.0, ALU.mult, ALU.mult, [Btok, cfg.cmT], [(Btm, sl)], eng="pool")
        TT(CTm[:, sl, :, :T], fT[:, 10:12, :T], bm(cfg.cmf[:, c, :], 2), ALU.mult, [fT, cfg.cmf], [(CTm, sl)], eng="pool")
        for g in range(2):
            MM(yo[g][:T, :], CTm[:, sl, g, :T], stb_in[:, g * 512:(g + 1) * 512], [(CTm, sl), stb_in], [yo[g]], start=first_c, stop=last_c)
        sp_ = [pget(), pget()]
        for g in range(2):
            MM(sp_[g][:, :], Btm[:T, sl, g * 128:(g + 1) * 128], xdec[:T, g * 512:(g + 1) * 512], [(Btm, sl), xdec], [sp_[g]])
        TT(big[3][:, :].rearrange("p (h q) -> p h q", q=64), st_in[:, :].rearrange("p (h q) -> p h q", q=64),
           bl(dbc[:, c * 16:(c + 1) * 16], 64), ALU.mult, [st_in, dbc], [big[3]])
        for g in range(2):
            TT(st_out[:, g * 512:(g + 1) * 512], big[3][:, g * 512:(g + 1) * 512], sp_[g][:, :], ALU.add, [big[3], sp_[g]], [st_out])

    def phaseA_tile(cfg, l, ti, xt, first, is_last, part):
        T, L, nseq, nch = cfg.T, cfg.L, cfg.nseq, cfg.nch
        hT, fT = cur["hT"], cur["fT"]
        stf, stb, hn, stf_s, stb_s, Gs, earg, Ee, WT, Btok = (PA[k] for k in ("stf", "stb", "hn", "stf_s", "stb_s", "Gs", "earg", "Ee", "WT", "Btok"))
        if part == "front":
            prologue(cfg, xt)
            inproj_conv(cfg, l, OFF_XBC, 12, cbrow, I["st_sc"][l], O["ncs_p"], O["ncs_s"], 0, is_last, first)
            return
        if ti == 0:
            DBG("fT_A", fT, fT[:, :, :T], [128, 12, T])
        if limit.get("stopA", 99) <= 1:
            return
        b = pget()
        for k in range(8):
            MM(b[:T, 0:16], hT[:, k, :T], PA["WB"][:, k, OFF_DT:OFF_DT + 16], [hT, PA["WB"]], [b], start=(k == 0), stop=(k == 7))
        A_, B_, C_, D_, E_ = (sm[k] for k in "abcde")
        TT(A_[:T, 0:16], b[:T, 0:16], vec16[:T, 0, :], ALU.add, [b, vec16], [A_])
        ACT(A_[:T, 0:16], A_[:T, 0:16], AF.Exp, [A_], [A_])
        ACT(A_[:T, 16:32], A_[:T, 0:16], AF.Ln, [A_], [A_], bias=1.0)
        TT(A_[:T, 32:48], A_[:T, 16:32], vec16[:T, 1, :], ALU.mult, [A_, vec16], [A_])
        cum_and_tot(cfg, A_[:T, 32:48], A_, 16, B_, dbc, 128)
        TT(C_[:T, 0:16], B_[:T, 16:32], B_[:T, 0:16], ALU.subtract, [B_], [C_])
        ACT(C_[:T, 0:16], C_[:T, 0:16], AF.Exp, [C_], [C_])
        ACT(C_[:T, 16:32], B_[:T, 0:16], AF.Exp, [B_], [C_])
        b = pget()
        TR(b[:16, 0:T], B_[:T, 0:16], identf[:T, :T], [B_, identf], [b])
        ACT(acT[:, :T], b[:16, 0:T], AF.Copy, [b], [acT])
        if ti == 0:
            DBG("acum", B_, B_[:T, 0:32], [T, 32])
        if limit.get("stopA", 99) <= 2:
            return
        bx, bb = pget(), pget()
        for c in range(8):
            TR(pb(bx, [8, 128], p=T)[:, c, :], fT[:, c, :T], identb[:, :], [fT, identb], [bx])
        for c in range(2):
            TR(pb(bb, [2, 128], p=T)[:, c, :], fT[:, 8 + c, :T], identb[:, :], [fT, identb], [bb])
        xc, xdec, xsD = bigb[0], bigb[1], big[0]
        TT(xc[:T, :].rearrange("p (h q) -> p h q", q=64), pb(bx, [16, 64], p=T), bl(A_[:T, 16:32], 64), ALU.mult, [bx, A_], [xc])
        TT(xsD[:T, :].rearrange("p (h q) -> p h q", q=64), pb(bx, [16, 64], p=T), bl(vec16[:T, 2, :], 64), ALU.mult, [bx, vec16], [xsD])
        TT(xdec[:T, :].rearrange("p (h q) -> p h q", q=64), xc[:T, :].rearrange("p (h q) -> p h q", q=64), bl(C_[:T, 0:16], 64), ALU.mult,
           [xc, C_], [xdec], eng="pool")
        ACT(Btok[:T, :], pb(bb, [256], p=T), AF.Copy, [bb], [Btok])
        b = pget()
        for g in range(2):
            MM(pf(b, [2, T], p=T)[:, g, :], fT[:, 8 + g, :T], fT[:, 10 + g, :T], [fT], [b])
        ACT(Gs[:T, :, :T], pf(b, [2, T], p=T), AF.Copy, [b], [Gs])
        for q in range(4):
            b = pget()
            for j in range(4):
                h = q * 4 + j
                MM(pf(b, [4, T], p=T)[:, j, :], identf[0:16, h:h + 1].to_broadcast([16, T]), acT[:, :T], [identf, acT], [b])
            for j in range(4):
                h = q * 4 + j
                STT(earg[:T, j, :T], pf(b, [4, T], p=T)[:, j, :], B_[:T, h:h + 1], cfg.nmI[:, :], ALU.subtract, ALU.add,
                    [b, B_, cfg.nmI], [earg])
            ACT(Ee[:T, :, :T], earg[:T, :, :T], AF.Exp, [earg], [Ee])
            TT(WT[:T, q * 4:q * 4 + 4, :T], Ee[:T, :, :T], bm(Gs[:T, q // 2, :T], 4), ALU.mult, [Ee, Gs], [WT])
        if limit.get("stopA", 99) <= 3:
            return
        yp = [pget(), pget()]
        for h in range(16):
            MM(yp[h // 8][:T, (h % 8) * 64:(h % 8 + 1) * 64], WT[:T, h, :T], xc[:T, h * 64:(h + 1) * 64], [WT, xc], [yp[h // 8]])
        ppin(yp[0]); ppin(yp[1])
        yo = [pget(), pget()]
        ppin(yo[0]); ppin(yo[1])
        if cfg.chain:
            if first:
                MSET(stf[:, :], 0.0, [stf])
                MSET(stb[:, :], 0.0, [stb])
            for c in range(nch):
                ssd_state_update(cfg, c, stf, stb, stf, yo, xdec, c == 0, c == nch - 1)
                ACT(stb[:, :], stf[:, :], AF.Copy, [stf], [stb])
            if is_last:
                for half in range(2):
                    b = pget()
                    for k in range(4):
                        kk = half * 4 + k
                        TR(pf(b, [4, 128])[:, k, :], stf[:, :].rearrange("n (j k) -> n k j", k=8)[:, kk, :], identf[:, :], [stf, identf], [b])
                    CP_(hn[:, half * 4:half * 4 + 4, :], pf(b, [4, 128]), [b], [hn])
                STO(O["nssm_p"][l].rearrange("(p k) n -> p k n", k=8), hn[:], hn)
        else:
            for c in range(limit.get("nchs", nch)):
                LD(hn[:], I["st_ssm"][l, c].rearrange("h q n -> (h q) n").rearrange("(p k) n -> p k n", k=8), hn)
                if limit.get("sub", 9) <= 0:
                    continue
                for half in range(2):
                    b = pget()
                    for k in range(4):
                        kk = half * 4 + k
                        TR(pf(b, [4, 128])[:, k, :], hn[:, kk, :], identf[:, :], [hn, identf], [b])
                    if limit.get("sub", 9) <= 1:
                        continue
                    ACT(stf_s[:, :].rearrange("n (j k) -> n k j", k=8)[:, half * 4:half * 4 + 4, :], pf(b, [4, 128]), AF.Copy, [b], [stf_s])
                    if limit.get("sub", 9) <= 2:
                        continue
                    CP_(stb_s[:, :].rearrange("n (j k) -> n k j", k=8)[:, half * 4:half * 4 + 4, :], pf(b, [4, 128]), [b], [stb_s])
                if limit.get("noupd"):
                    continue
                ssd_state_update(cfg, c, stf_s, stb_s, stf_s, yo, xdec, c == 0, c == limit.get("nchs", nch) - 1)
                if limit.get("noback"):
                    continue
                for half in range(2):
                    b = pget()
                    for k in range(4):
                        kk = half * 4 + k
                        TR(pf(b, [4, 128])[:, k, :], stf_s[:, :].rearrange("n (j k) -> n k j", k=8)[:, kk, :], identf[:, :], [stf_s, identf], [b])
                    CP_(hn[:, half * 4:half * 4 + 4, :], pf(b, [4, 128]), [b], [hn])
                STO(O["nssm_s"][l, c].rearrange("(p k) n -> p k n", k=8), hn[:], hn)
        if limit.get("stopA", 99) <= 4:
            punpin(yp[0]); punpin(yp[1]); punpin(yo[0]); punpin(yo[1])
            return
        t1, t2 = big[1], big[2]
        for g in range(2):
            TT(t1[:T, g * 512:(g + 1) * 512].rearrange("p (h q) -> p h q", q=64), yo[g][:T, :].rearrange("p (h q) -> p h q", q=64),
               bl(C_[:T, 16 + g * 8:16 + g * 8 + 8], 64), ALU.mult, [yo[g], C_], [t1])
            TT(t2[:T, g * 512:(g + 1) * 512], yp[g][:T, :], t1[:T, g * 512:(g + 1) * 512], ALU.add, [yp[g], t1], [t2])
        for x_ in yp + yo:
            punpin(x_)
        TT(t2[:T, :], t2[:T, :], xsD[:T, :], ALU.add, [t2, xsD], [t2], eng="pool")
        if ti == 0:
            DBG("yssd", t2, t2[:T, :], [T, D])
        for g in range(2):
            b = pget()
            for k in range(8):
                MM(b[:T, :], hT[:, k, :T], PA["WB"][:, k, OFF_ZS + g * 512:OFF_ZS + (g + 1) * 512], [hT, PA["WB"]], [b], start=(k == 0), stop=(k == 7))
            ACT(t1[:T, g * 512:(g + 1) * 512], b[:T, :], AF.Silu, [b], [t1])
        TT(t2[:T, :], t2[:T, :], t1[:T, :], ALU.mult, [t2, t1], [t2])
        for g in range(2):
            ACT(junk2[:T, g * 512:(g + 1) * 512], t2[:T, g * 512:(g + 1) * 512], AF.Square, [t2], [junk2, sm1], accum_out=sm1[:T, 2 + g:3 + g])
        TS(sm1[:T, 2:4], sm1[:T, 2:4], 1.0 / 512, EPS, ALU.mult, ALU.add, [sm1], [sm1])
        ACT(sm1[:T, 2:4], sm1[:T, 2:4], AF.Sqrt, [sm1], [sm1])
        RECIP(sm1[:T, 2:4], sm1[:T, 2:4], [sm1], [sm1])
        ynb = bigb[2]
        for g in range(2):
            ACT(ynb[:T, g * 512:(g + 1) * 512], t2[:T, g * 512:(g + 1) * 512], AF.Copy, [t2, sm1], [ynb], scale=sm1[:T, 2 + g:3 + g])
        out_proj(cfg, ynb, 8)

    def out_proj(cfg, ynb, nk):
        T = cfg.T
        b = pget()
        for c in range(nk):
            TR(pb(b, [8, T])[:, c, :], ynb[:T, c * 128:(c + 1) * 128], identb[:T, :T], [ynb, identb], [b])
        mixT = mixT_buf[:, :].rearrange("p (c t) -> p c t", t=128)
        CP_(mixT[:, 0:nk, :T], pb(b, [8, T])[:, 0:nk, :], [b], [mixT_buf])
        for half in range(2):
            b = pget()
            for k in range(nk):
                MM(b[:T, :], mixT[:, k, :T], WO[:, k, half * 512:(half + 1) * 512], [mixT_buf, WO], [b], start=(k == 0), stop=(k == nk - 1))
            TT(big[3][:T, half * 512:(half + 1) * 512], b[:T, :], gate_bc[cfg.name][:T, half * 512:(half + 1) * 512], ALU.mult,
               [b, gate_bc[cfg.name]], [big[3]])

    PB = {}

    def alloc_B(pes, tag):
        def sbp(name, shape, dt=F32):
            return Buf(pes.enter_context(nc.sbuf_tensor(name + tag, list(shape), dt)), name)
        PB.update(Sp_f=sbp("Sp_f", [128, 4, 128]), Sp_b=sbp("Sp_b", [128, 4, 128], BF16), Ss_in=sbp("Ss_in", [128, 8, 128]),
                  Ss_b=sbp("Ss_b", [128, 8, 128], BF16), kv=sbp("kv", [128, 8, 128], BF16), kbg=sbp("kbg", [128, 4, 128]),
                  vb_=sbp("vb", [128, 4, 128]), kdec=sbp("kdec", [128, 4, 128], BF16), kdm=sbp("kdm", [64, 2, 128], BF16),
                  LG=sbp("LG", [128, 12]), LGT=sbp("LGT", [12, 128]), ea=sbp("ea", [128, 4, 128]),
                  E2a=sbp("E2a", [128, 4, 128]),
                  X0=sbp("X0", [128, 4, 128]), X1=sbp("X1", [128, 4, 128]), Y0=sbp("Y0", [128, 4, 128]), Y1=sbp("Y1", [128, 4, 128]),
                  R0=sbp("R0", [128, 4, 128]), R1=sbp("R1", [128, 4, 128]), attnT=sbp("attnT", [128, 4, 128], BF16),
                  egb=sbp("egb", [128, 4, 128]), qsq=sbp("qsq", [128, 4, 128], BF16),
                  nwTm=sbp("nwTm", [128, 1024], BF16), qdm=sbp("qdm", [128, 1024], BF16), u_sb=sbp("u_sb", [128, 4, 128]),
                  vnew=sbp("vnew", [128, 4, 128], BF16), dbcS=sbp("dbcS", [128, 64]), WB=sbp("WB", [128, 8, 2056], BF16))
        PB["E2"] = None

    def r32(ap):
        return ap.bitcast(F32R)

    def phaseB_tile(cfg, l, ti, xt, h0, first, is_last, part):
        T, L, nseq, nch = cfg.T, cfg.L, cfg.nseq, cfg.nch
        hT, fT = cur["hT"], cur["fT"]
        if part == "front":
            prologue(cfg, xt)
            inproj_conv_B(cfg, l, h0, is_last, first)
            return
        (Sp_f, Sp_b, Ss_in, Ss_b, kv, kbg, vb_, kdec, kdm, LG, LGT, ea, attnT, egb, nwTm, qdm, u_sb, vnew, dbcS) = (
            PB[k] for k in ("Sp_f", "Sp_b", "Ss_in", "Ss_b", "kv", "kbg", "vb_", "kdec", "kdm", "LG", "LGT", "ea", "attnT",
                            "egb", "nwTm", "qdm", "u_sb", "vnew", "dbcS"))
        E2 = [PB["E2a"], PB["E2a"]]
        Ss_out = Ss_in
        Xs, Ys, Rs = [PB["X0"], PB["X1"]], [PB["Y0"], PB["Y1"]], [PB["R0"], PB["R1"]]
        WB = PB["WB"]
        A_, B_, C_, D_, E_ = (sm[k] for k in "abcde")
        b = pget()
        for k in range(8):
            MM(b[:T, 0:8], hT[:, k, :T], WB[:, k, 2048:2056], [hT, WB], [b], start=(k == 0), stop=(k == 7))
        TT(A_[:T, 0:4], b[:T, 0:4], vec8[:T, 0, h0:h0 + 4], ALU.add, [b, vec8], [A_])
        ACT(A_[:T, 0:4], A_[:T, 0:4], AF.Exp, [A_], [A_])
        ACT(A_[:T, 0:4], A_[:T, 0:4], AF.Ln, [A_], [A_], bias=1.0)
        TT(A_[:T, 4:8], A_[:T, 0:4], vec8[:T, 1, h0:h0 + 4], ALU.mult, [A_, vec8], [A_])
        ACT(A_[:T, 8:12], b[:T, 4:8], AF.Exp, [b], [A_], scale=-1.0)
        TS(A_[:T, 8:12], A_[:T, 8:12], 1.0, None, ALU.add, ALU.bypass, [A_], [A_])
        RECIP(A_[:T, 12:16], A_[:T, 8:12], [A_], [A_])
        ACT(A_[:T, 16:20], A_[:T, 12:16], AF.Ln, [A_], [A_])
        cum_and_tot(cfg, A_[:T, 4:8], A_, 4, B_, dbcS, 128)
        ACT(C_[:T, 0:4], B_[:T, 0:4], AF.Exp, [B_], [C_])
        TT(C_[:T, 4:8], B_[:T, 4:8], B_[:T, 0:4], ALU.subtract, [B_], [C_])
        ACT(C_[:T, 4:8], C_[:T, 4:8], AF.Exp, [C_], [C_])
        b = pget()
        for c in range(8):
            TR(pb(b, [8, 128], p=T)[:, c, :], fT[:, 4 + c, :T], identb[:, :], [fT, identb], [b])
        ACT(kv[:T, :, :], pb(b, [8, 128], p=T), AF.Copy, [b], [kv])
        ksq = big[0]
        TT(ksq[:T, 0:512].rearrange("p (h d) -> p h d", d=128), kv[:T, 0:4, :], kv[:T, 0:4, :], ALU.mult, [kv], [ksq])
        RED(D_[:T, 0:4], ksq[:T, 0:512].rearrange("p (h d) -> p h d", d=128), [ksq], [D_])
        TS(D_[:T, 0:4], D_[:T, 0:4], EPS, None, ALU.add, ALU.bypass, [D_], [D_])
        ACT(D_[:T, 4:8], D_[:T, 0:4], AF.Ln, [D_], [D_])
        ACT(D_[:T, 8:12], D_[:T, 4:8], AF.Exp, [D_], [D_], scale=-0.5)
        qsq = PB["qsq"]
        ACT(qsq[:, :, :T], fT[:, 0:4, :T], AF.Square, [fT], [qsq])
        b = pget()
        for j in range(4):
            MM(b[:T, j:j + 1], qsq[:, j, :T], onesb[:, 0:1], [qsq, onesb], [b])
        TS(D_[:T, 12:16], b[:T, 0:4], EPS, None, ALU.add, ALU.bypass, [b], [D_])
        ACT(D_[:T, 12:16], D_[:T, 12:16], AF.Ln, [D_], [D_])
        ACT(D_[:T, 16:20], D_[:T, 12:16], AF.Exp, [D_], [D_], scale=-0.5)
        TS(D_[:T, 16:20], D_[:T, 16:20], float(128 ** -0.5), None, ALU.mult, ALU.bypass, [D_], [D_])
        TT(E_[:T, 0:4], D_[:T, 8:12], A_[:T, 12:16], ALU.mult, [D_, A_], [E_])
        TT(E_[:T, 4:8], E_[:T, 0:4], C_[:T, 0:4], ALU.mult, [E_, C_], [E_])
        TT(E_[:T, 8:12], D_[:T, 8:12], C_[:T, 4:8], ALU.mult, [D_, C_], [E_])
        TS(E_[:T, 12:16], E_[:T, 0:4], -1.0, None, ALU.mult, ALU.bypass, [E_], [E_])
        TS(E_[:T, 16:20], D_[:T, 8:12], -1.0, None, ALU.mult, ALU.bypass, [D_], [E_])
        CP_(LG[:T, 0:4], B_[:T, 0:4], [B_], [LG])
        STT(LG[:T, 4:8], D_[:T, 4:8], -0.5, B_[:T, 0:4], ALU.mult, ALU.subtract, [D_, B_], [LG])
        STT(LG[:T, 8:12], D_[:T, 4:8], -0.5, B_[:T, 0:4], ALU.mult, ALU.add, [D_, B_], [LG])
        TT(LG[:T, 8:12], LG[:T, 8:12], A_[:T, 16:20], ALU.add, [LG, A_], [LG])
        b = pget()
        TR(b[:12, 0:T], LG[:T, 0:12], identf[:T, :T], [LG, identf], [b])
        ACT(LGT[:, :T], b[:12, 0:T], AF.Copy, [b], [LGT])
        TT(r32(kbg[:T, :, :]), kv[:T, 0:4, :], bl(E_[:T, 4:8], 128), ALU.mult, [kv, E_], [kbg])
        TT(kdec[:T, :, :], kv[:T, 0:4, :], bl(E_[:T, 8:12], 128), ALU.mult, [kv, E_], [kdec])
        TT(r32(vb_[:T, :, :]), kv[:T, 4:8, :], bl(A_[:T, 12:16], 128), ALU.mult, [kv, A_], [vb_])
        bG, bA = pget(), pget()
        for j in range(4):
            MM(pf(bG, [4, T], p=T)[:, j, :], fT[:, 4 + j, :T], fT[:, 4 + j, :T], [fT], [bG])
        for j in range(4):
            MM(pf(bA, [4, T], p=T)[:, j, :], fT[:, 4 + j, :T], fT[:, j, :T], [fT], [bA])
        bcs = [pget(), pget(), pget()]
        for r in range(3):
            for j in range(4):
                MM(pf(bcs[r], [4, T], p=T)[:, j, :], identf[0:12, r * 4 + j:r * 4 + j + 1].to_broadcast([12, T]), LGT[:, :T],
                   [identf, LGT], [bcs[r]])
        be = pget()
        for j in range(4):
            MM(pf(be, [4, T])[:, j, :], identf[0:12, j:j + 1].to_broadcast([12, 128]), LGT[:, :T], [identf, LGT], [be])
        X, Y, R = Xs[0], Ys[0], Rs[0]
        kinds = ((1, ALU.add, cfg.nmSr, 0), (2, ALU.subtract, cfg.nmS, 1), (0, ALU.subtract, cfg.nmI, 2))
        for r, op0, msk, ki in kinds:
            for j in range(4):
                STT(ea[:T, j, :T], pf(bcs[r], [4, T], p=T)[:, j, :], B_[:T, j:j + 1], msk[:, :], op0, ALU.add, [bcs[r], B_, msk], [ea])
            Ek = E2[ki % 2]
            ACT(Ek[:T, :, :T], ea[:T, :, :T], AF.Exp, [ea], [Ek])
            for j in range(4):
                if ki == 0:
                    STT(r32(X[:T, j, :T]), pf(bG, [4, T], p=T)[:, j, :], E_[:T, 12 + j:13 + j], Ek[:T, j, :T], ALU.mult, ALU.mult, [bG, E_, Ek], [X])
                elif ki == 1:
                    STT(r32(Y[:T, j, :T]), pf(bG, [4, T], p=T)[:, j, :], E_[:T, 16 + j:17 + j], Ek[:T, j, :T], ALU.mult, ALU.mult, [bG, E_, Ek], [Y])
                else:
                    STT(attnT[:T, j, :T], pf(bA, [4, T], p=T)[:, j, :], D_[:T, 8 + j:9 + j], Ek[:T, j, :T], ALU.mult, ALU.mult, [bA, D_, Ek], [attnT])
        ACT(egb[:, :, :T], pf(be, [4, T]), AF.Exp, [be], [egb])
        TT(r32(R[:T, :, :T]), Y[:T, :, :T], bm(identf[:T, :T], 4), ALU.add, [Y, identf], [R])
        for lev in range(cfg.nlev):
            X2, Y2, R2 = Xs[(lev + 1) % 2], Ys[(lev + 1) % 2], Rs[(lev + 1) % 2]
            lastlev = (lev == cfg.nlev - 1)
            bX, bY, bR = pget(), (None if lastlev else pget()), pget()
            for j in range(4):
                MM(pf(bX, [4, T], p=T)[:, j, :], r32(Y[:T, j, :T]), r32(X[:T, j, :T]), [X, Y], [bX])
            if not lastlev:
                for j in range(4):
                    MM(pf(bY, [4, T], p=T)[:, j, :], r32(X[:T, j, :T]), r32(Y[:T, j, :T]), [X, Y], [bY])
            ACT(r32(X2[:T, :, :T]), pf(bX, [4, T], p=T), AF.Copy, [bX], [X2])
            if not lastlev:
                CP_(r32(Y2[:T, :, :T]), pf(bY, [4, T], p=T), [bY], [Y2])
            for j in range(4):
                MM(pf(bR, [4, T], p=T)[:, j, :], r32(X2[:T, j, :T]), r32(R[:T, j, :T]), [X2, R], [bR])
            TT(r32(R2[:T, :, :T]), R[:T, :, :T], pf(bR, [4, T], p=T), ALU.add, [R, bR], [R2])
            X, Y, R = X2, Y2, R2
        bU, bW = pget(), pget()
        for j in range(4):
            MM(pf(bU, [4, 128], p=T)[:, j, :], r32(R[:T, j, :T]), r32(vb_[:T, j, :]), [R, vb_], [bU])
        for j in range(4):
            MM(pf(bW, [4, T])[:, j, :], r32(kbg[:T, j, :]), r32(R[:T, j, :T]), [kbg, R], [bW])
        ACT(u_sb[:T, :, :], pf(bU, [4, 128], p=T), AF.Copy, [bU], [u_sb])
        TT(egb[:, :, :T], fT[:, 0:4, :T], egb[:, :, :T], ALU.mult, [fT, egb], [egb])
        vnb, ob = pget(), pget()
        ppin(vnb)
        ppin(ob)
        vn4, o4 = pf(vnb, [4, 128], p=T), pf(ob, [4, 128], p=T)
        if cfg.chain:
            if first:
                MSET(Sp_f[:], 0.0, [Sp_f])
                MSET(Sp_b[:], 0.0, [Sp_b])
            nwv = nwTm[:, 0:nch * 4 * T].rearrange("p (c j t) -> p c j t", c=nch, j=4)
            qdv = qdm[:, 0:nch * 4 * T].rearrange("p (c j t) -> p c j t", c=nch, j=4)
            for c in range(nch):
                TT(nwv[:, c], pf(bW, [4, T]), bm(cfg.ncmf[:, c, :], 4), ALU.mult, [bW, cfg.ncmf], [(nwTm, c)])
                TT(qdv[:, c], egb[:, :, :T], bm(cfg.cmf[:, c, :], 4), ALU.mult, [egb, cfg.cmf], [(qdm, c)], eng="pool")
            for c in range(nch):
                for j in range(4):
                    MM(vn4[:, j, :], nwv[:, c, j, :], Sp_b[:, j, :], [(nwTm, c), Sp_b], [vnb], start=(j == 0 and c == 0), stop=False, skip_group_check=True)
                for j in range(4):
                    MM(o4[:, j, :], qdv[:, c, j, :], Sp_b[:, j, :], [(qdm, c), Sp_b], [ob], start=(j == 0 and c == 0), stop=False, skip_group_check=True)
                TT(vnew[:T, :, :], vn4, u_sb[:T, :, :], ALU.add, [vnb, u_sb], [vnew])
                bs = pget()
                for j in range(4):
                    MM(pf(bs, [4, 128])[:, j, :], kdec[c * 64:(c + 1) * 64, j, :], vnew[c * 64:(c + 1) * 64, j, :], [kdec, vnew], [bs])
                tmpS = big[0][:, 512:1024].rearrange("p (j v) -> p j v", v=128)
                TT(tmpS, Sp_f[:, :, :], bl(dbcS[:, c * 4:(c + 1) * 4], 128), ALU.mult, [Sp_f, dbcS], [big[0]])
                TT(Sp_f[:, :, :], tmpS, pf(bs, [4, 128]), ALU.add, [big[0], bs], [Sp_f])
                ACT(Sp_b[:, :, :], Sp_f[:, :, :], AF.Copy, [Sp_f], [Sp_b])
            if is_last:
                STO(O["ngdn_p"][l, h0:h0 + 4].rearrange("h k v -> k h v"), Sp_f[:], Sp_f)
        else:
            nwv = nwTm[:, 0:nch * T].rearrange("p (c t) -> p c t", t=T)
            qdv = qdm[:, 0:nch * T].rearrange("p (c t) -> p c t", t=T)
            ppin(bW)
            for j in range(4):
                h = h0 + j
                TT(nwv, bm(pf(bW, [4, T])[:, j, :], nch), cfg.ncmf[:, :, :], ALU.mult, [bW, cfg.ncmf], [nwTm])
                TT(qdv, bm(egb[:, j, :T], nch), cfg.cmf[:, :, :], ALU.mult, [egb, cfg.cmf], [qdm], eng="pool")
                tmpS = big[2][:, :].rearrange("p (c v) -> p c v", v=128)
                for hf in range(2):
                    LD(Ss_in[:], I["st_gdn"][l, hf * 8:(hf + 1) * 8, h].rearrange("b k v -> k b v"), Ss_in)
                    CP_(Ss_b[:], Ss_in[:], [Ss_in], [Ss_b], eng="pool")
                    for cc in range(8):
                        c = hf * 8 + cc
                        MM(vn4[:, j, :], nwv[:, c, :], Ss_b[:, cc, :], [nwTm, Ss_b], [vnb], start=(j == 0 and c == 0), stop=False, skip_group_check=True)
                    for cc in range(8):
                        c = hf * 8 + cc
                        MM(o4[:, j, :], qdv[:, c, :], Ss_b[:, cc, :], [qdm, Ss_b], [ob], start=(j == 0 and c == 0), stop=False, skip_group_check=True)
                    TT(vnew[:T, j, :], vn4[:, j, :], u_sb[:T, j, :], ALU.add, [vnb, u_sb], [vnew])
                    TT(tmpS, Ss_in[:, :, :],
                       bl(dbcS[:, 0:nch * 4].rearrange("p (c j) -> p c j", j=4)[:, hf * 8:(hf + 1) * 8, j], 128), ALU.mult, [Ss_in, dbcS], [big[2]])
                    for c4 in (0, 4):
                        bs = pget()
                        for cc in range(4):
                            c = hf * 8 + c4 + cc
                            sl = mslot[0] % 2
                            mslot[0] += 1
                            TS(kdm[:T, sl, :], kdec[:T, j, :], cfg.cmT[:, c:c + 1], 1.0, ALU.mult, ALU.mult, [kdec, cfg.cmT], [(kdm, sl)], eng="pool")
                            MM(pf(bs, [4, 128])[:, cc, :], kdm[:T, sl, :], vnew[:T, j, :], [(kdm, sl), vnew], [bs])
                        TT(Ss_in[:, c4:c4 + 4, :], tmpS[:, c4:c4 + 4, :], pf(bs, [4, 128]), ALU.add, [big[2], bs], [Ss_in])
                    STO(O["ngdn_s"][l, hf * 8:(hf + 1) * 8, h].rearrange("b k v -> k b v"), Ss_in[:], Ss_in)
            punpin(bW)
        for j in range(4):
            MM(o4[:, j, :], attnT[:T, j, :T], vnew[:T, j, :], [attnT, vnew], [ob], start=False, stop=True, skip_group_check=True)
        punpin(vnb)
        of = big[1]
        ACT(of[:T, 0:512], ob[:T, :], AF.Copy, [ob], [of])
        punpin(ob)
        if ti == 0 and h0 == 0:
            DBG("ogdn", of, of[:T, 0:512], [T, 512])
        TT(ksq[:T, 0:512], of[:T, 0:512], of[:T, 0:512], ALU.mult, [of], [ksq])
        RED(C_[:T, 8:12], ksq[:T, 0:512].rearrange("p (h d) -> p h d", d=128), [ksq], [C_])
        TT(C_[:T, 12:16], D_[:T, 16:20], D_[:T, 16:20], ALU.mult, [D_], [C_])
        TT(C_[:T, 8:12], C_[:T, 8:12], C_[:T, 12:16], ALU.mult, [C_], [C_])
        TS(C_[:T, 8:12], C_[:T, 8:12], 1.0 / 128, EPS, ALU.mult, ALU.add, [C_], [C_])
        ACT(C_[:T, 8:12], C_[:T, 8:12], AF.Ln, [C_], [C_])
        ACT(C_[:T, 8:12], C_[:T, 8:12], AF.Exp, [C_], [C_], scale=-0.5)
        TT(C_[:T, 8:12], C_[:T, 8:12], D_[:T, 16:20], ALU.mult, [C_, D_], [C_])
        b = pget()
        for k in range(8):
            MM(b[:T, :], hT[:, k, :T], WB[:, k, 0:512], [hT, WB], [b], start=(k == 0), stop=(k == 7))
        sz = big[2]
        ACT(sz[:T, 0:512], b[:T, :], AF.Silu, [b], [sz])
        TT(of[:T, 0:512].rearrange("p (h d) -> p h d", d=128), of[:T, 0:512].rearrange("p (h d) -> p h d", d=128), bl(C_[:T, 8:12], 128),
           ALU.mult, [of, C_], [of])
        onb = bigb[2]
        TT(onb[:T, 0:512], of[:T, 0:512], sz[:T, 0:512], ALU.mult, [of, sz], [onb])
        out_proj(cfg, onb, 4)

    def inproj_conv_B(cfg, l, h0, is_last, first):
        T, L, nseq = cfg.T, cfg.L, cfg.nseq
        hT, fT = cur["hT"], cur["fT"]
        rawv = raw[:, :, 0:nseq * (L + 3)].rearrange("p c (b l) -> p c b l", l=L + 3)
        if nseq == 1:
            if first:
                MSET(raw[:, :, 0:3], 0.0, [raw])
            else:
                CP_(raw[:, :, 0:3], raw[:, :, L:L + 3], [raw], [raw])
        else:
            for part in range(3):
                cs0 = part * 1024 + h0 * 128
                LD(cst_in[:, part * 512:(part + 1) * 512], I["st_gc"][l][:, :, cs0:cs0 + 512].rearrange("b k c -> (b k) c"), cst_in)
            for g0 in (0, 8):
                b = pget()
                ng = min(8, 12 - g0)
                for c in range(ng):
                    TR(pf(b, [8, 48])[:, c, :], cst_in[:, (g0 + c) * 128:(g0 + c + 1) * 128], identf[:48, :48], [cst_in, identf], [b])
                CP_(rawv[:, g0:g0 + ng, :, 0:3], pf(b, [8, 48])[:, 0:ng, :].rearrange("p c (b k) -> p c b k", k=3), [b], [raw])
        for g0 in range(0, 12, 4):
            b = pget()
            for c in range(4):
                for k in range(8):
                    MM(pf(b, [4, T])[:, c, :], PB["WB"][:, k, 512 + (g0 + c) * 128: 512 + (g0 + c + 1) * 128], hT[:, k, :T],
                       [PB["WB"], hT], [b], start=(k == 0), stop=(k == 7))
            if nseq == 1:
                ACT(raw[:, g0:g0 + 4, 3:3 + L], pf(b, [4, T]), AF.Copy, [b], [raw])
            else:
                for c in range(4):
                    ACT(rawv[:, g0 + c, :, 3:3 + L], pf(b, [4, T])[:, c, :].rearrange("p (b l) -> p b l", l=L), AF.Copy, [b], [raw])
            if is_last and limit.get("tail", True):
                for c in range(4):
                    ACT(tailf[:, g0 + c, 0:nseq * 3].rearrange("p (b k) -> p b k", k=3),
                        pf(b, [4, T])[:, c, :].rearrange("p (b l) -> p b l", l=L)[:, :, L - 3:L], AF.Copy, [b], [tailf])
        for g0 in range(0, 12, 4):
            b = pget()
            for c in range(4):
                o = pf(b, [4, T])[:, c, :]
                if nseq > 1:
                    o = o.rearrange("p (b l) -> p b l", l=L)
                for k in range(4):
                    MM(o, DIAG[:, g0 + c, k, :], rawv[:, g0 + c, :, k:k + L] if nseq > 1 else raw[:, g0 + c, k:k + L],
                       [DIAG, raw], [b], start=(k == 0), stop=(k == 3))
            ACT(fT[:, g0:g0 + 4, :T], pf(b, [4, T]), AF.Silu, [b], [fT])
        if is_last:
            n3 = nseq * 3
            for g0 in range(0, 12, 4):
                b = pget()
                for c in range(4):
                    TR(pf(b, [4, 128], p=n3)[:, c, :], tailf[:, g0 + c, 0:n3], identf[:, :], [tailf, identf], [b])
                CP_(tailo[:n3, g0 * 128:(g0 + 4) * 128], b[:n3, :], [b], [tailo])
            dst = O["ngc_p"] if nseq == 1 else O["ngc_s"]
            for part in range(3):
                cs0 = part * 1024 + h0 * 128
                STO(dst[l][:, cs0:cs0 + 512], tailo[:n3, part * 512:(part + 1) * 512], tailo)

    def bvec(dst_ap, src_row_ap, buf):
        LD(dst_ap, src_row_ap.partition_broadcast(128), buf)

    tiles = [(CP, ti) for ti in range(limit.get("ntiles", NT))] + ([(CS, NT)] if limit.get("sample", True) else [])
    all_bufs.extend(allb)
    layer_in = None
    for l in range(limit.get("layers", DEPTH)):
        do_mod(l)
        S.barrier(all_bufs)
        mod_stacks.pop().close()
        for ph in limit.get("phases", (0, 1, 2)):
            acc_in = layer_in if ph == 0 else scr[(ph - 1)]
            last_phase = (l == DEPTH - 1 and ph == 2)
            acc_out = scr[ph] if ph < 2 else (scr[2] if not last_phase else None)
            S.barrier(all_bufs)
            pes = contextlib.ExitStack()
            cur_es[0] = pes
            if ph == 0:
                alloc_A(pes, "_%d_%d" % (l, ph))
            else:
                alloc_B(pes, "_%d_%d" % (l, ph))
            for v_ in list(PA.values()) + list(PB.values()):
                if v_ is not None and v_ not in all_bufs:
                    all_bufs.append(v_)
            if ph == 0:
                load_cols(PA["WB"], 0, I["w_in"][l][:, 0:2576], 2576)
                LD(nwT[:], I["ssd_norm_w"][l].rearrange("(k p) -> p k", p=128), nwT, nonc=True)
                load_wo(I["w_out"][l][0:1024, :], 8, lambda k: nwT[:, k:k + 1])
                build_diag(I["ssd_conv_w"][l], 0, 12)
                LD(cbrow_f[0:1, :], I["ssd_conv_b"][l:l + 1, :], cbrow_f)
                CP_(cbrow[:], cbrow_f[0:1, :], [cbrow_f], [cbrow])
                bvec(vec16[:, 0, :], I["ssd_dt_bias"][l:l + 1, :], vec16)
                bvec(vec16[:, 1, :], I["ssd_a_log"][l:l + 1, :], vec16)
                bvec(vec16[:, 2, :], I["ssd_d"][l:l + 1, :], vec16)
                ACT(vec16[:, 1, :], vec16[:, 1, :], AF.Exp, [vec16], [vec16])
                TS(vec16[:, 1, :], vec16[:, 1, :], -1.0, None, ALU.mult, ALU.bypass, [vec16], [vec16])
            else:
                h0 = (ph - 1) * 4
                load_cols(PB["WB"], 0, I["w_in"][l][:, OFF_ZG + h0 * 128:OFF_ZG + h0 * 128 + 512], 512)
                for part in range(3):
                    c0 = OFF_QKV + part * 1024 + h0 * 128
                    load_cols(PB["WB"], 512 + part * 512, I["w_in"][l][:, c0:c0 + 512], 512)
                load_cols(PB["WB"], 2048, I["w_in"][l][:, OFF_A + h0:OFF_A + h0 + 4], 4)
                load_cols(PB["WB"], 2052, I["w_in"][l][:, OFF_B + h0:OFF_B + h0 + 4], 4)
                LD(nwT[:, 0:1], I["gdn_norm_w"][l].rearrange("(p o) -> p o", o=1), nwT, nonc=True)
                load_wo(I["w_out"][l][1024 + h0 * 128:1024 + h0 * 128 + 512, :], 4, lambda k: nwT[:, 0:1])
                for part in range(3):
                    for k_ in range(4):
                        LD(cwT[:, part * 4:part * 4 + 4, k_],
                           I["gdn_conv_w"][l][k_, part * 1024 + h0 * 128:part * 1024 + h0 * 128 + 512].rearrange("(c p) -> p c", p=128), cwT, nonc=True)
                for c in range(12):
                    TT(DIAG[:, c, :, :], bm(identf[:], 4), bl(cwT[:, c, :], 128), ALU.mult, [identf, cwT], [DIAG], eng="pool")
                bvec(vec8[:, 0, :], I["gdn_dt_bias"][l:l + 1, :], vec8)
                bvec(vec8[:, 1, :], I["gdn_a_log"][l:l + 1, :], vec8)
                ACT(vec8[:, 1, :], vec8[:, 1, :], AF.Exp, [vec8], [vec8])
                TS(vec8[:, 1, :], vec8[:, 1, :], -1.0, None, ALU.mult, ALU.bypass, [vec8], [vec8])

            def issue_loads(idx):
                cfg, ti = tiles[idx]
                r0, T = rows(ti)
                xt = xt_slots[idx % 2]
                ap, dep = src_ap(layer_in.t if layer_in is not None else None, ti)
                LD(xt[:T, :], ap, xt, R=[(layer_in, ti)] if layer_in is not None else None)
                if ph > 0:
                    xa = xa_slots[idx % 2]
                    LD(xa[:T, :], acc_in.t[r0:r0 + T, :], xa, R=[(acc_in, ti)])

            if last_phase:
                fnw_bc = stage[0]
                fnw_v = stage[0][:, :, :].rearrange("p a b -> p (a b)")
                LD(fnw_v, I["final_norm_w"].partition_broadcast(128), stage[0])
            def run_part(idx, part):
                cfg, ti = tiles[idx]
                cur["hT"], cur["fT"] = hT_slots[idx % 2], fT_slots[idx % 2]
                xt_ = xt_slots[idx % 2]
                first_ = (ti == 0)
                last_ = (ti == NT - 1) or (ti == NT)
                if ph == 0:
                    phaseA_tile(cfg, l, ti, xt_, first_, last_, part)
                else:
                    phaseB_tile(cfg, l, ti, xt_, (ph - 1) * 4, first_, last_, part)

            issue_loads(0)
            if len(tiles) > 1:
                issue_loads(1)
            run_part(0, "front")
            for idx, (cfg, ti) in enumerate(tiles):
                if idx + 1 < len(tiles):
                    run_part(idx + 1, "front")
                r0, T = rows(ti)
                xt = xt_slots[idx % 2]
                xa = xa_slots[idx % 2] if ph > 0 else xt
                run_part(idx, "back")
                TT(xo[:T, :], big[3][:T, :], xa[:T, :], ALU.add, [big[3], xa], [xo], eng="pool")
                if not last_phase:
                    STO(acc_out.t[r0:r0 + T, :], xo[:T, :], xo, W=[(acc_out, ti)])
                else:
                    ACT(junk2[:T, :], xo[:T, :], AF.Square, [xo], [junk2, sm1], accum_out=sm1[:T, 4:5])
                    TS(sm1[:T, 4:5], sm1[:T, 4:5], 1.0 / D, EPS, ALU.mult, ALU.add, [sm1], [sm1])
                    ACT(sm1[:T, 4:5], sm1[:T, 4:5], AF.Sqrt, [sm1], [sm1])
                    RECIP(sm1[:T, 4:5], sm1[:T, 4:5], [sm1], [sm1])
                    STT(big[0][:T, :], xo[:T, :], sm1[:T, 4:5], fnw_v[:T, :], ALU.mult, ALU.mult, [xo, sm1, fnw_bc], [big[0]])
                    dsto = O["y_p"][r0:r0 + T, :] if ti < NT else O["y_s"][:, :]
                    STO(dsto, big[0][:T, :], big[0])
                if idx + 2 < len(tiles):
                    issue_loads(idx + 2)
            pes.close()
            cur_es[0] = es
            PA.clear()
            PB.clear()
        layer_in = scr[2]
    S.emit()
    es.close()
    return nc, dbg_out


_CACHE = {}


def make_in_maps(inputs):
    g = {k: np.ascontiguousarray(np.asarray(v, dtype=np.float32)) for k, v in inputs.items()}
    consts = host_consts()
    maps = []
    for c in range(8):
        m = {}
        m["xp"] = g["x_prompt"][c]
        m["xs"] = g["x_sample"][c * NSB:(c + 1) * NSB].reshape(NSB * SL, D)
        m["call"] = np.concatenate([g["c_prompt"][c:c + 1], g["c_sample"][c * NSB:(c + 1) * NSB]], axis=0)
        m["st_sc"] = g["state_ssd_conv"][:, c * NSB:(c + 1) * NSB]
        m["st_ssm"] = g["state_ssm"][:, c * NSB:(c + 1) * NSB]
        m["st_gc"] = g["state_gdn_conv"][:, c * NSB:(c + 1) * NSB]
        m["st_gdn"] = g["state_gdn"][:, c * NSB:(c + 1) * NSB]
        for k in ("norm_w", "w_ada", "b_ada", "w_in", "ssd_conv_w", "ssd_conv_b", "ssd_dt_bias", "ssd_a_log", "ssd_d",
                  "ssd_norm_w", "gdn_conv_w", "gdn_dt_bias", "gdn_a_log", "gdn_norm_w", "w_out"):
            m[k] = g[k]
        m["final_norm_w"] = g["final_norm_w"].reshape(1, D)
        for k, v in consts.items():
            m["c_" + k] = v
        maps.append({k: np.ascontiguousarray(v) for k, v in m.items()})
    return maps


def kernel(**inputs):
    if "nc" not in _CACHE:
        _CACHE["nc"] = build()[0]
    nc = _CACHE["nc"]
    maps = make_in_maps(inputs)
    res = run_bass_kernel_spmd(nc, maps, core_ids=list(range(8)))
    R = res.results
    cat = lambda k, ax: np.concatenate([np.asarray(r[k]) for r in R], axis=ax)
    y_p = np.stack([np.asarray(r["y_p"]) for r in R], 0)
    y_s = np.stack([np.asarray(r["y_s"]).reshape(NSB, SL, D) for r in R], 0).reshape(8 * NSB, SL, D)
    ncs_p = np.stack([np.asarray(r["ncs_p"]) for r in R], 1)
    nssm_p = np.stack([np.asarray(r["nssm_p"]).reshape(DEPTH, 16, 64, 128) for r in R], 1)
    ngc_p = np.stack([np.asarray(r["ngc_p"]) for r in R], 1)
    ngdn_p = np.stack([np.asarray(r["ngdn_p"]) for r in R], 1)
    ncs_s = np.concatenate([np.asarray(r["ncs_s"]).reshape(DEPTH, NSB, 3, 1536) for r in R], 1)
    nssm_s = np.concatenate([np.asarray(r["nssm_s"]).reshape(DEPTH, NSB, 16, 64, 128) for r in R], 1)
    ngc_s = np.concatenate([np.asarray(r["ngc_s"]).reshape(DEPTH, NSB, 3, 3072) for r in R], 1)
    ngdn_s = np.concatenate([np.asarray(r["ngdn_s"]) for r in R], 1)
    outs = (y_p, y_s, ncs_p, nssm_p, ngc_p, ngdn_p, ncs_s, nssm_s, ngc_s, ngdn_s)
    return tuple(np.ascontiguousarray(o, dtype=np.float32) for o in outs)
```

```python
import contextlib
import numpy as np
import concourse.bass as bass
import concourse.mybir as mybir
from concourse.bass_utils import run_bass_kernel_spmd

F32 = mybir.dt.float32
F32R = mybir.dt.float32r
BF16 = mybir.dt.bfloat16
AF = mybir.ActivationFunctionType
ALU = mybir.AluOpType
AX = mybir.AxisListType

ENGS = ("pe", "dve", "act", "pool", "sp")


class Buf:
    def __init__(self, t, name):
        self.t = t
        self.name = name
        self.st = {}
        self.dma_sem = None
        self.dma_cnt = 0

    def __getitem__(self, k):
        return self.t[k]


class Instr:
    __slots__ = ("eng", "fn", "deps", "signal", "count", "is_dma", "buf_sem", "dma_count")

    def __init__(self, eng, fn):
        self.eng = eng
        self.fn = fn
        self.deps = []
        self.signal = False
        self.count = None
        self.is_dma = False
        self.buf_sem = None
        self.dma_count = None


class Sched:
    def __init__(self, nc):
        self.nc = nc
        self.streams = {e: [] for e in ENGS}
        self.all = []

    def _states(self, buf, key):
        st = buf.st
        if key is None:
            if None not in st:
                st[None] = [None, {}]
            return list(st.values())
        if key not in st:
            if None in st:
                st[key] = [st[None][0], dict(st[None][1])]
            else:
                st[key] = [None, {}]
        res = [st[key]]
        if None in st:
            res.append(st[None])
        return res

    @staticmethod
    def _norm(lst):
        out = []
        for x in lst or []:
            out.append((x, None) if isinstance(x, Buf) else x)
        return out

    def add(self, eng, fn, reads=None, writes=None, dma_buf=None):
        ins = Instr(eng, fn)
        ins.is_dma = dma_buf is not None
        reads = self._norm(reads)
        writes = self._norm(writes)
        deps, raw = [], []
        for buf, key in reads:
            for s in self._states(buf, key):
                if s[0] is not None:
                    deps.append(s[0])
                    raw.append(s[0])
                if getattr(buf, "excl", False):
                    for r in s[1].values():
                        if r.eng != eng:
                            deps.append(r)
        for buf, key in writes:
            for s in self._states(buf, key):
                if s[0] is not None:
                    deps.append(s[0])
                deps.extend(s[1].values())
        fdeps = {}
        for d in deps:
            if d is ins:
                continue
            if (not ins.is_dma) and (not d.is_dma) and d.eng == eng:
                if eng == "pe" or not any(d is r for r in raw):
                    continue
            if d.is_dma:
                fdeps[id(d)] = (d, d.buf_sem.dma_cnt * 16)
            else:
                fdeps[id(d)] = (d, None)
        ins.deps = list(fdeps.values())
        if ins.is_dma:
            ins.buf_sem = dma_buf
            dma_buf.dma_cnt += 1
            ins.dma_count = dma_buf.dma_cnt * 16
        rkey = ("dma", id(ins)) if ins.is_dma else eng
        for buf, key in reads:
            for s in self._states(buf, key):
                s[1][rkey] = ins
        for buf, key in writes:
            for s in self._states(buf, key):
                s[0] = ins
                if key is None or s is not buf.st.get(None):
                    s[1] = {}
        self.streams[eng].append(ins)
        self.all.append(ins)
        return ins

    def barrier(self, dma_bufs_all):
        last = {}
        for e in ("pe", "dve", "act", "pool"):
            for ins in reversed(self.streams[e]):
                if not isinstance(ins, tuple) and not ins.is_dma:
                    last[e] = ins
                    ins.signal = True
                    break
        snap = [(b, b.dma_cnt * 16) for b in dma_bufs_all if b.dma_cnt > 0]
        mark = ("barrier", last, snap)
        for e in ENGS:
            self.streams[e].append(mark)

    def emit(self, final_wait_eng="sp"):
        nc = self.nc
        for ins in self.all:
            for d, _v in ins.deps:
                if not d.is_dma:
                    d.signal = True
        with contextlib.ExitStack() as es:
            eng_sem = {e: es.enter_context(nc.semaphore("s_" + e)) for e in ("pe", "dve", "act", "pool")}
            dma_bufs = []
            for ins in self.all:
                if isinstance(ins, tuple):
                    continue
                if ins.is_dma and ins.buf_sem.dma_sem is None:
                    ins.buf_sem.dma_sem = es.enter_context(nc.semaphore("d_%d" % len(dma_bufs)))
                    dma_bufs.append(ins.buf_sem)
            cnt = {e: 0 for e in eng_sem}
            for ins in self.all:
                if not ins.is_dma and ins.signal:
                    cnt[ins.eng] += 1
                    ins.count = cnt[ins.eng]
            block = es.enter_context(nc.Block())
            engobj = {"pe": block.tensor, "dve": block.vector, "act": block.scalar, "pool": block.gpsimd,
                      "sp": block.sync}
            sched = self

            def make(ename):
                def body(eng):
                    water = {}
                    for ins in sched.streams[ename]:
                        if isinstance(ins, tuple):
                            _, last, snap = ins
                            for e2, li in last.items():
                                if e2 != ename and water.get(("e", e2), 0) < li.count:
                                    water[("e", e2)] = li.count
                                    eng.wait_ge(eng_sem[e2], li.count)
                            for b_, v_ in snap:
                                if water.get(("d", id(b_)), 0) < v_:
                                    water[("d", id(b_))] = v_
                                    eng.wait_ge(b_.dma_sem, v_)
                            continue
                        need = {}
                        for d, dv in ins.deps:
                            if d.is_dma:
                                key, sem, val = ("d", id(d.buf_sem)), d.buf_sem.dma_sem, dv
                            else:
                                key, sem, val = ("e", d.eng), eng_sem[d.eng], d.count
                            if need.get(key, (None, 0))[1] < val:
                                need[key] = (sem, val)
                        for key, (sem, val) in need.items():
                            if water.get(key, 0) >= val:
                                continue
                            water[key] = val
                            eng.wait_ge(sem, val)
                        bi = ins.fn(eng)
                        if ins.is_dma:
                            bi.then_inc(ins.buf_sem.dma_sem, 16)
                        elif ins.signal:
                            bi.then_inc(eng_sem[ename], 1)
                    if ename == final_wait_eng:
                        for b in dma_bufs:
                            eng.wait_ge(b.dma_sem, b.dma_cnt * 16)
                return body

            for ename in ENGS:
                engobj[ename](make(ename))


D = 1024
SEQ = 2048
NT = SEQ // 128
NSB = 16
SL = 4
DEPTH = 2
IN_DIM = 6688
EPS = 1e-6
NEG = -30000.0
OFF_ZS, OFF_XBC, OFF_DT, OFF_ZG, OFF_QKV, OFF_A, OFF_B = 0, 1024, 2560, 2576, 3600, 6672, 6680


class Cfg:
    def __init__(self, name, T, nseq, L, Q, chain):
        self.name, self.T, self.nseq, self.L, self.Q, self.chain = name, T, nseq, L, Q, chain
        self.nch = T // Q
        self.nlev = {64: 5, 4: 1}[Q]


CP = Cfg("p", 128, 1, 128, 64, True)
CS = Cfg("s", 64, NSB, SL, 4, False)


def host_consts():
    c = {"ident": np.eye(128, dtype=np.float32), "ones": np.ones((128, 128), np.float32)}
    for cfg in (CP, CS):
        T, Q, nch = cfg.T, cfg.Q, cfg.nch
        ch = np.arange(T) // Q
        same = ch[:, None] == ch[None, :]
        idx = np.arange(T)
        le = idx[:, None] <= idx[None, :]
        lt = idx[:, None] < idx[None, :]
        n = cfg.name
        c["tri_" + n] = (same & le).astype(np.float32)
        c["blk_" + n] = same.astype(np.float32)
        c["nmI_" + n] = np.where(same & le, 0.0, NEG).astype(np.float32)
        c["nmS_" + n] = np.where(same & lt, 0.0, NEG).astype(np.float32)
        c["nmSr_" + n] = np.ascontiguousarray(c["nmS_" + n].T)
        cm = (ch[None, :] == np.arange(nch)[:, None]).astype(np.float32)
        c["cmf_" + n] = np.ascontiguousarray(np.broadcast_to(cm[None], (128, nch, T))).astype(np.float32)
        c["cmT_" + n] = np.ascontiguousarray(cm.T)
    e = np.zeros((17, 128), np.float32)
    e[0, :] = 1.0
    c["E_p"] = e
    e = np.zeros((17, 64), np.float32)
    for t in range(64):
        e[1 + t // SL, t] = 1.0
    c["E_s"] = e
    return c


def bl(ap, n):
    return ap.unsqueeze(len(ap.shape)).to_broadcast(list(ap.shape) + [n])


def bm(ap, n):
    return ap.unsqueeze(1).to_broadcast([ap.shape[0], n] + list(ap.shape[1:]))


def build(debug=(), limit=None):
    limit = limit or {}
    nc = bass.Bass("TRN2", target_bir_lowering=False)
    S = Sched(nc)
    es = contextlib.ExitStack()
    consts_np = host_consts()

    def din(name, shape):
        return nc.dram_tensor(name, list(shape), F32, kind="ExternalInput").ap()

    def dout(name, shape):
        return nc.dram_tensor(name, list(shape), F32, kind="ExternalOutput").ap()

    I = {}
    I["xp"] = din("xp", [SEQ, D])
    I["xs"] = din("xs", [NSB * SL, D])
    I["call"] = din("call", [17, D])
    I["st_sc"] = din("st_sc", [DEPTH, NSB, 3, 1536])
    I["st_ssm"] = din("st_ssm", [DEPTH, NSB, 16, 64, 128])
    I["st_gc"] = din("st_gc", [DEPTH, NSB, 3, 3072])
    I["st_gdn"] = din("st_gdn", [DEPTH, NSB, 8, 128, 128])
    wshapes = {"norm_w": [DEPTH, D], "w_ada": [DEPTH, D, 3 * D], "b_ada": [DEPTH, 3 * D], "w_in": [DEPTH, D, IN_DIM],
               "ssd_conv_w": [DEPTH, 4, 1536], "ssd_conv_b": [DEPTH, 1536], "ssd_dt_bias": [DEPTH, 16],
               "ssd_a_log": [DEPTH, 16], "ssd_d": [DEPTH, 16], "ssd_norm_w": [DEPTH, 1024],
               "gdn_conv_w": [DEPTH, 4, 3072], "gdn_dt_bias": [DEPTH, 8], "gdn_a_log": [DEPTH, 8],
               "gdn_norm_w": [DEPTH, 128], "w_out": [DEPTH, 2048, D], "final_norm_w": [1, D]}
    for k, shp in wshapes.items():
        I[k] = din(k, shp)
    for k, v in consts_np.items():
        I["c_" + k] = din("c_" + k, v.shape)
    O = {}
    O["y_p"] = dout("y_p", [SEQ, D])
    O["y_s"] = dout("y_s", [NSB * SL, D])
    O["ncs_p"] = dout("ncs_p", [DEPTH, 3, 1536])
    O["nssm_p"] = dout("nssm_p", [DEPTH, 1024, 128])
    O["ngc_p"] = dout("ngc_p", [DEPTH, 3, 3072])
    O["ngdn_p"] = dout("ngdn_p", [DEPTH, 8, 128, 128])
    O["ncs_s"] = dout("ncs_s", [DEPTH, NSB * 3, 1536])
    O["nssm_s"] = dout("nssm_s", [DEPTH, NSB, 1024, 128])
    O["ngc_s"] = dout("ngc_s", [DEPTH, NSB * 3, 3072])
    O["ngdn_s"] = dout("ngdn_s", [DEPTH, NSB, 8, 128, 128])
    NROW = SEQ + NSB * SL
    scr = [Buf(nc.dram_tensor("scr%d" % i, [NROW, D], F32, kind="Internal").ap(), "scr%d" % i) for i in range(3)]
    xin_ext = None

    allb = []

    def sb(name, shape, dt=F32):
        b_ = Buf(es.enter_context(nc.sbuf_tensor(name, list(shape), dt)), name)
        allb.append(b_)
        return b_

    banks = [Buf(es.enter_context(nc.psum_tensor("bank%d" % i, [128, 512], F32)), "bank%d" % i) for i in range(8)]
    for b_ in banks:
        b_.excl = True
    pstate = {"i": 0, "pinned": set()}

    def pget():
        while True:
            i = pstate["i"] % 8
            pstate["i"] += 1
            if i not in pstate["pinned"]:
                return banks[i]

    def ppin(b):
        pstate["pinned"].add(banks.index(b))

    def punpin(b):
        pstate["pinned"].discard(banks.index(b))

    def pf(b, shape, p=128):
        n = int(np.prod(shape))
        ap = b[:p, 0:n]
        if len(shape) == 1:
            return ap
        names = " ".join("a%d" % i for i in range(len(shape)))
        kw = {"a%d" % i: shape[i] for i in range(1, len(shape))}
        return ap.rearrange("p (%s) -> p %s" % (names, names), **kw)

    def pb(b, shape, p=128):
        n = int(np.prod(shape))
        ap = b[:p, :].bitcast(BF16)[:, 0:n]
        if len(shape) == 1:
            return ap
        names = " ".join("a%d" % i for i in range(len(shape)))
        kw = {"a%d" % i: shape[i] for i in range(1, len(shape))}
        return ap.rearrange("p (%s) -> p %s" % (names, names), **kw)

    def MM(out, lhsT, rhs, R, W, start=True, stop=True, **kw):
        S.add("pe", lambda e: e.matmul(out, lhsT=lhsT, rhs=rhs, start=start, stop=stop, **kw), reads=R, writes=W)

    def TR(out, in_, ident, R, W):
        S.add("pe", lambda e: e.transpose(out=out, in_=in_, identity=ident), reads=R, writes=W)

    def ACT(out, in_, func, R, W, **kw):
        S.add("act", lambda e: e.activation(out=out, in_=in_, func=func, **kw), reads=R, writes=W)

    def TT(out, in0, in1, op, R, W, eng="dve"):
        S.add(eng, lambda e: e.tensor_tensor(out=out, in0=in0, in1=in1, op=op), reads=R, writes=W)

    def TS(out, in0, s1, s2, op0, op1, R, W, eng="dve"):
        S.add(eng, lambda e: e.tensor_scalar(out=out, in0=in0, scalar1=s1, scalar2=s2, op0=op0, op1=op1), reads=R, writes=W)

    def STT(out, in0, scalar, in1, op0, op1, R, W):
        S.add("dve", lambda e: e.scalar_tensor_tensor(out=out, in0=in0, scalar=scalar, in1=in1, op0=op0, op1=op1), reads=R, writes=W)

    def CP_(out, in_, R, W, eng="dve"):
        S.add(eng, lambda e: e.tensor_copy(out=out, in_=in_), reads=R, writes=W)

    def RECIP(out, in_, R, W):
        S.add("dve", lambda e: e.reciprocal(out=out, in_=in_), reads=R, writes=W)

    def RED(out, in_, R, W):
        S.add("dve", lambda e: e.tensor_reduce(out=out, in_=in_, axis=AX.X, op=ALU.add), reads=R, writes=W)

    def MSET(ap, val, W, eng="pool"):
        S.add(eng, lambda e: e.memset(ap, val), writes=W)

    def LD(out, in_, buf, R=None, nonc=False):
        if nonc:
            S.add("sp", lambda e: e.dma_start(out=out, in_=in_, allow_slow_non_contiguous=True), reads=R, writes=[buf], dma_buf=buf)
        else:
            S.add("sp", lambda e: e.dma_start(out=out, in_=in_), reads=R, writes=[buf], dma_buf=buf)

    sto_n = [0]
    sto_dummy = Buf(None, "sto_dummy")

    def STO(out, in_, buf, W=None, extra_reads=None):
        k_ = sto_n[0] % 3
        sto_n[0] += 1
        S.add("pool", lambda e: e.dma_start(out=out, in_=in_), reads=[buf] + (extra_reads or []), writes=(W or []) + [(sto_dummy, k_)], dma_buf=buf)

    dbg_out = {}
    mod_stacks = []
    cur_es = [es]
    all_bufs = []

    def DBG(name, buf, ap, shape):
        if name in debug and name not in dbg_out:
            o = dout("dbg_" + name, shape)
            dbg_out[name] = o
            if ap.dtype != F32:
                tmp = Buf(cur_es[0].enter_context(nc.sbuf_tensor("dbgt_" + name, list(shape), F32)), "dbgt_" + name)
                all_bufs.append(tmp)
                CP_(tmp[:], ap, [buf], [tmp])
                STO(o, tmp[:], tmp)
            else:
                STO(o, ap, buf)

    C = {}
    for k, v in consts_np.items():
        if k.startswith("cmf_"):
            continue
        C[k] = sb("k_" + k, v.shape)
        LD(C[k][:], I["c_" + k], C[k])
    identf = C["ident"]
    onesf = C["ones"]
    identb = sb("identb", [128, 128], BF16)
    CP_(identb[:], identf[:], [identf], [identb])
    onesb = sb("onesb", [128, 128], BF16)
    CP_(onesb[:], onesf[:], [onesf], [onesb])
    cfg_pending = []
    for cfg in (CP, CS):
        n = cfg.name
        cfg.tri, cfg.blk, cfg.nmI, cfg.nmS, cfg.nmSr, cfg.cmT = (C[k + n] for k in ("tri_", "blk_", "nmI_", "nmS_", "nmSr_", "cmT_"))
        cfg.cmf = sb("cmfb_" + n, [128, cfg.nch, cfg.T], BF16)
        cfg.ncmf = sb("ncmfb_" + n, [128, cfg.nch, cfg.T], BF16)
        cfg_pending.append(cfg)
        cfg.cmTb = sb("cmTb_" + n, [cfg.T, cfg.nch], BF16)
        CP_(cfg.cmTb[:], cfg.cmT[:], [cfg.cmT], [cfg.cmTb])

    SW = 128
    stage = [sb("stage%d" % i, [128, 8, SW]) for i in range(2)]
    stage_i = [0]
    WO = sb("wo", [128, 8, D], BF16)
    DIAG = sb("diag", [128, 12, 4, 128], BF16)
    cwT = sb("cwT", [128, 12, 4])
    cbrow = sb("cbrow", [1, 1536], BF16)
    nwT = sb("nwT", [128, 8])
    scT = sb("scT", [128, 8, 17])
    shT = sb("shT", [128, 8, 17])
    gate_bc = {"p": sb("gate_p", [128, D]), "s": sb("gate_s", [64, D])}
    vec16 = sb("vec16", [128, 4, 16])
    vec8 = sb("vec8", [128, 2, 8])
    scTb = sb("scTb", [128, 8, 17], BF16)
    badaT = sb("badaT", [128, 24])
    nw_in = sb("nw_in", [128, 8])
    xt_slots = [sb("xt%d" % i, [128, D]) for i in range(2)]
    xa_slots = [sb("xa%d" % i, [128, D]) for i in range(2)]
    junk = sb("junk", [128, D], BF16)
    xn = sb("xn", [128, D], BF16)
    hT = sb("hT", [128, 8, 128], BF16)
    sm1 = sb("sm1", [128, 8])
    raw = sb("raw", [128, 12, 132], BF16)
    tailf = sb("tailf", [128, 12, 48])
    tailo = sb("tailo", [48, 1536])
    cst_in = tailo
    fT = sb("fT", [128, 12, 128], BF16)
    big = [sb("big%d" % i, [128, D]) for i in range(4)]
    bigb = [sb("bigb%d" % i, [128, D], BF16) for i in range(3)]
    xo = big[3]
    call_sb, sil_c, bgate, gate17, cbrow_f = big[0], bigb[0], big[1], big[2], tailo
    hTf_t = big[0]
    for cfg in cfg_pending:
        n_ = cfg.nch * cfg.T
        tmpv = big[0][:, 0:n_].rearrange("p (c t) -> p c t", t=cfg.T)
        LD(tmpv, I["c_cmf_" + cfg.name], big[0])
        CP_(cfg.cmf[:], tmpv, [big[0]], [cfg.cmf])
        TS(cfg.ncmf[:], tmpv, -1.0, None, ALU.mult, ALU.bypass, [big[0]], [cfg.ncmf])

    def load_cols(dst, dcol, src, ncols):
        c0 = 0
        while c0 < ncols:
            n = min(SW, ncols - c0)
            st = stage[stage_i[0] % 2]
            stage_i[0] += 1
            LD(st[:, :, 0:n], src[:, c0:c0 + n].rearrange("(k p) n -> p k n", p=128), st)
            CP_(dst[:, :, dcol + c0:dcol + c0 + n], st[:, :, 0:n], [st], [dst], eng="pool")
            c0 += n

    def load_wo(src_rows, nk, scale_ap_fn):
        for half in range(D // SW):
            st = stage[stage_i[0] % 2]
            stage_i[0] += 1
            LD(st[:, 0:nk, :], src_rows[:, half * SW:(half + 1) * SW].rearrange("(k p) n -> p k n", p=128), st)
            for k in range(nk):
                TS(WO[:, k, half * SW:(half + 1) * SW], st[:, k, :], scale_ap_fn(k), 1.0, ALU.mult, ALU.mult,
                   [st, nwT], [WO], eng="pool")

    def build_diag(convw, c0, nchunks):
        for k_ in range(4):
            LD(cwT[:, 0:nchunks, k_], convw[k_, c0 * 128:(c0 + nchunks) * 128].rearrange("(c p) -> p c", p=128), cwT, nonc=True)
        for c in range(nchunks):
            TT(DIAG[:, c, :, :], bm(identf[:], 4), bl(cwT[:, c, :], 128), ALU.mult, [identf, cwT], [DIAG], eng="pool")

    def do_mod(l):
        LD(call_sb[:17, :], I["call"], call_sb)
        ACT(sil_c[:17, :], call_sb[:17, :], AF.Silu, [call_sb], [sil_c])
        b = pget()
        for c in range(8):
            TR(pb(b, [8, 32])[:, c, 0:17], sil_c[:17, c * 128:(c + 1) * 128], identb[:17, :17], [sil_c, identb], [b])
        CP_(scTb[:], pb(b, [8, 32])[:, :, 0:17], [b], [scTb])
        LD(badaT[:], I["b_ada"][l].rearrange("(c p) -> p c", p=128), badaT, nonc=True)
        LD(nw_in[:], I["norm_w"][l].rearrange("(c p) -> p c", p=128), nw_in, nonc=True)
        LD(bgate[0:1, :], I["b_ada"][l:l + 1, 2048:3072], bgate)
        nblk = 3 * D // SW
        per = D // SW
        modes = contextlib.ExitStack()
        wb = Buf(modes.enter_context(nc.sbuf_tensor("modwb%d" % l, [128, 8, SW], BF16)), "modwb")
        all_bufs.append(wb)
        for blk in range(nblk):
            st = stage[stage_i[0] % 2]
            stage_i[0] += 1
            LD(st[:], I["w_ada"][l][:, blk * SW:(blk + 1) * SW].rearrange("(k p) n -> p k n", p=128), st)
            CP_(wb[:, :, 0:SW], st[:], [st], [wb], eng="pool")
            if blk < 2 * per:
                b = pget()
                nj = SW // 128
                for j in range(nj):
                    for k in range(8):
                        MM(pf(b, [nj, 17])[:, j, :], wb[:, k, j * 128:(j + 1) * 128], scTb[:, k, :], [wb, scTb], [b],
                           start=(k == 0), stop=(k == 7))
                dst = shT if blk < per else scT
                cc = (blk % per) * nj
                TT(dst[:, cc:cc + nj, :], pf(b, [nj, 17]), bl(badaT[:, blk * nj:blk * nj + nj], 17), ALU.add, [b, badaT], [dst])
            else:
                g = pget()
                gc0 = (blk - 2 * per) * SW
                for k in range(8):
                    MM(g[:17, 0:SW], scTb[:, k, :], wb[:, k, 0:SW], [wb, scTb], [g], start=(k == 0), stop=False)
                MM(g[:17, 0:SW], onesf[0:1, 0:17], bgate[0:1, gc0:gc0 + SW], [onesf, bgate], [g], start=False, stop=True)
                CP_(gate17[:17, gc0:gc0 + SW], g[:17, 0:SW], [g], [gate17])
        TS(scT[:], scT[:], 1.0, None, ALU.add, ALU.bypass, [scT], [scT])
        TT(scT[:], scT[:], bl(nw_in[:], 17), ALU.mult, [scT, nw_in], [scT])
        for n, E, T in (("p", C["E_p"], 128), ("s", C["E_s"], 64)):
            for half in range(2):
                b = pget()
                MM(b[:T, :], E[:, :T], gate17[:17, half * 512:(half + 1) * 512], [E, gate17], [b])
                CP_(gate_bc[n][:T, half * 512:(half + 1) * 512], b[:T, :], [b], [gate_bc[n]])
        mod_stacks.append(modes)

    def rows(ti):
        return (ti * 128, 128) if ti < NT else (SEQ, 64)

    def src_ap(layer_in, ti):
        r0, T = rows(ti)
        if layer_in is None:
            return (I["xp"][r0:r0 + T, :] if ti < NT else I["xs"][:, :]), None
        return layer_in[r0:r0 + T, :], (layer_in, ti)

    def prologue(cfg, xt):
        T = cfg.T
        ACT(junk[:T, :], xt[:T, :], AF.Square, [xt], [junk, sm1], accum_out=sm1[:T, 0:1])
        TS(sm1[:T, 0:1], sm1[:T, 0:1], 1.0 / D, EPS, ALU.mult, ALU.add, [sm1], [sm1])
        ACT(sm1[:T, 0:1], sm1[:T, 0:1], AF.Ln, [sm1], [sm1])
        ACT(sm1[:T, 0:1], sm1[:T, 0:1], AF.Exp, [sm1], [sm1], scale=-0.5)
        ACT(xn[:T, :], xt[:T, :], AF.Copy, [xt, sm1], [xn], scale=sm1[:T, 0:1])
        b = pget()
        for c in range(8):
            TR(pb(b, [8, T])[:, c, :], xn[:T, c * 128:(c + 1) * 128], identb[:T, :T], [xn, identb], [b])
        if cfg.nseq == 1:
            sc, sh = bl(scT[:, :, 0], T), bl(shT[:, :, 0], T)
            o1, o2, i1 = hTf_t[:, 0:8 * T].rearrange("p (c t) -> p c t", t=T), hT[:, :, :T], pb(b, [8, T])
        else:
            sc, sh = bl(scT[:, :, 1:17], SL), bl(shT[:, :, 1:17], SL)
            o1 = hTf_t[:, 0:8 * T].rearrange("p (c b l) -> p c b l", c=8, l=SL)
            o2 = hT[:, :, :T].rearrange("p c (b l) -> p c b l", l=SL)
            i1 = pb(b, [8, NSB, SL])
        TT(o1, i1, sc, ALU.mult, [b, scT], [hTf_t])
        TT(o2, o1, sh, ALU.add, [hTf_t, shT], [hT])

    def inproj_conv(cfg, l, wcol0, nchunks, bias_row, st_conv_in, conv_out, conv_out_s, ch0, is_last, first):
        T, L, nseq = cfg.T, cfg.L, cfg.nseq
        rawv = raw[:, :, 0:nseq * (L + 3)].rearrange("p c (b l) -> p c b l", l=L + 3)
        if cfg.nseq == 1:
            if first:
                MSET(raw[:, :, 0:3], 0.0, [raw])
            else:
                CP_(raw[:, 0:nchunks, 0:3], raw[:, 0:nchunks, L:L + 3], [raw], [raw])
        else:
            LD(cst_in[:, 0:nchunks * 128], st_conv_in[:, :, ch0 * 128:(ch0 + nchunks) * 128].rearrange("b k c -> (b k) c"), cst_in)
            for g0 in range(0, nchunks, 8):
                b = pget()
                ng = min(8, nchunks - g0)
                for c in range(ng):
                    TR(pf(b, [8, 48])[:, c, :], cst_in[:, (g0 + c) * 128:(g0 + c + 1) * 128], identf[:48, :48], [cst_in, identf], [b])
                CP_(rawv[:, g0:g0 + ng, :, 0:3], pf(b, [8, 48])[:, 0:ng, :].rearrange("p c (b k) -> p c b k", k=3), [b], [raw])
        if limit.get("stopC", 99) <= 1:
            return
        for g0 in range(0, nchunks, 4):
            b = pget()
            for c in range(4):
                for k in range(8):
                    MM(pf(b, [4, T])[:, c, :], PA["WB"][:, k, wcol0 + (g0 + c) * 128: wcol0 + (g0 + c + 1) * 128], hT[:, k, :T],
                       [PA["WB"], hT], [b], start=(k == 0), stop=(k == 7))
            if nseq == 1:
                ACT(raw[:, g0:g0 + 4, 3:3 + L], pf(b, [4, T]), AF.Copy, [b], [raw])
            else:
                for c in range(4):
                    ACT(rawv[:, g0 + c, :, 3:3 + L], pf(b, [4, T])[:, c, :].rearrange("p (b l) -> p b l", l=L), AF.Copy, [b], [raw])
            if is_last and limit.get("tail", True):
                for c in range(4):
                    ACT(tailf[:, g0 + c, 0:nseq * 3].rearrange("p (b k) -> p b k", k=3),
                        pf(b, [4, T])[:, c, :].rearrange("p (b l) -> p b l", l=L)[:, :, L - 3:L], AF.Copy, [b], [tailf])
        if limit.get("stopC", 99) <= 2:
            return
        for g0 in range(0, nchunks, 4):
            b = pget()
            for c in range(4):
                o = pf(b, [4, T])[:, c, :]
                if nseq > 1:
                    o = o.rearrange("p (b l) -> p b l", l=L)
                for k in range(4):
                    MM(o, DIAG[:, g0 + c, k, :], rawv[:, g0 + c, :, k:k + L] if nseq > 1 else raw[:, g0 + c, k:k + L],
                       [DIAG, raw], [b], start=(k == 0), stop=(k == 3 and bias_row is None))
                if bias_row is not None:
                    MM(pf(b, [4, T])[:, c, :], bias_row[0:1, (g0 + c) * 128:(g0 + c + 1) * 128], onesb[0:1, :T], [bias_row, onesb], [b],
                       start=False, stop=True)
            ACT(fT[:, g0:g0 + 4, :T], pf(b, [4, T]), AF.Silu, [b], [fT])
        if limit.get("stopC", 99) <= 3:
            return
        if is_last:
            n3 = nseq * 3
            for g0 in range(0, nchunks, 4):
                b = pget()
                for c in range(4):
                    TR(pf(b, [4, 128], p=n3)[:, c, :], tailf[:, g0 + c, 0:n3], identf[:, :], [tailf, identf], [b])
                CP_(tailo[:n3, g0 * 128:(g0 + 4) * 128], b[:n3, :], [b], [tailo])
            dst = conv_out[l][:, ch0 * 128:(ch0 + nchunks) * 128] if nseq == 1 else conv_out_s[l][:, ch0 * 128:(ch0 + nchunks) * 128]
            STO(dst, tailo[:n3, 0:nchunks * 128], tailo)

    def cum_and_tot(cfg, a_ap, a_buf, nh, out_c, out_dbc, pw):
        T, nch = cfg.T, cfg.nch
        b = pget()
        MM(b[:T, 0:nh], cfg.tri[:, :], a_ap, [cfg.tri, a_buf], [b])
        MM(b[:T, nh:2 * nh], cfg.blk[:, :], a_ap, [cfg.blk, a_buf], [b])
        ACT(out_c[:T, 0:2 * nh], b[:T, 0:2 * nh], AF.Copy, [b], [out_c])
        b2 = pget()
        MM(b2[:nch, 0:nh], cfg.cmT[:, :], a_ap, [cfg.cmT, a_buf], [b2])
        CP_(tot[:nch, 0:nh], b2[:nch, 0:nh], [b2], [tot])
        TT(totx[:nch, 0:nch * nh].rearrange("p (c h) -> p c h", h=nh), bm(tot[:nch, 0:nh], nch), bl(identf[:nch, :nch], nh), ALU.mult,
           [tot, identf], [totx])
        b3 = pget()
        MM(b3[:pw, 0:nch * nh], onesf[:nch, :pw], totx[:nch, 0:nch * nh], [onesf, totx], [b3])
        ACT(out_dbc[:pw, 0:nch * nh], b3[:pw, 0:nch * nh], AF.Exp, [b3], [out_dbc])

    tot = sb("tot", [16, 16])
    totx = sb("totx", [16, 256])
    sm = {k: sb("sm_" + k, [128, 64]) for k in ("a", "b", "c", "d", "e")}
    dbc = sb("dbc", [128, 256])
    acT = sb("acT", [16, 128])

    PA = {}

    def alloc_A(pes, tag):
        def sbp(name, shape, dt=F32):
            return Buf(pes.enter_context(nc.sbuf_tensor(name + tag, list(shape), dt)), name)
        PA.update(stf=sbp("stf", [128, D]), stb=sbp("stb", [128, D], BF16), hn=sbp("hn", [128, 8, 128]),
                  stf_s=sbp("stf_s", [128, D]), stb_s=sbp("stb_s", [128, D], BF16), Gs=sbp("Gs", [128, 2, 128]),
                  earg=sbp("earg", [128, 4, 128]), Ee=sbp("Ee", [128, 4, 128]), WT=sbp("WT", [128, 16, 128], BF16),
                  Btok=sbp("Btok", [128, 256], BF16), Btm=sbp("Btm", [128, 2, 256], BF16), CTm=sbp("CTm", [128, 2, 2, 128], BF16),
                  WB=sbp("WB", [128, 8, 2576], BF16))
    mslot = [0]

    def ssd_state_update(cfg, c, st_in, stb_in, st_out, yo, xdec, first_c, last_c):
        T, nch = cfg.T, cfg.nch
        Btm, CTm, Btok = PA["Btm"], PA["CTm"], PA["Btok"]
        sl = mslot[0] % 2
        mslot[0] += 1
        TS(Btm[:T, sl, :], Btok[:T, :], cfg.cmT[:, c:c + 1], 1.0, ALU.mult, ALU.mult, [Btok, cfg.cmT], [(Btm, sl)], eng="pool")
        TT(CTm[:, sl, :, :T], fT[:, 10:12, :T], bm(cfg.cmf[:, c, :], 2), ALU.mult, [fT, cfg.cmf], [(CTm, sl)], eng="pool")
        for g in range(2):
            MM(yo[g][:T, :], CTm[:, sl, g, :T], stb_in[:, g * 512:(g + 1) * 512], [(CTm, sl), stb_in], [yo[g]], start=first_c, stop=last_c)
        sp_ = [pget(), pget()]
        for g in range(2):
            MM(sp_[g][:, :], Btm[:T, sl, g * 128:(g + 1) * 128], xdec[:T, g * 512:(g + 1) * 512], [(Btm, sl), xdec], [sp_[g]])
        TT(big[3][:, :].rearrange("p (h q) -> p h q", q=64), st_in[:, :].rearrange("p (h q) -> p h q", q=64),
           bl(dbc[:, c * 16:(c + 1) * 16], 64), ALU.mult, [st_in, dbc], [big[3]])
        for g in range(2):
            TT(st_out[:, g * 512:(g + 1) * 512], big[3][:, g * 512:(g + 1) * 512], sp_[g][:, :], ALU.add, [big[3], sp_[g]], [st_out])

    def phaseA_tile(cfg, l, ti, xt, first, is_last):
        T, L, nseq, nch = cfg.T, cfg.L, cfg.nseq, cfg.nch
        stf, stb, hn, stf_s, stb_s, Gs, earg, Ee, WT, Btok = (PA[k] for k in ("stf", "stb", "hn", "stf_s", "stb_s", "Gs", "earg", "Ee", "WT", "Btok"))
        prologue(cfg, xt)
        if limit.get("stopA", 99) <= 0:
            return
        inproj_conv(cfg, l, OFF_XBC, 12, cbrow, I["st_sc"][l], O["ncs_p"], O["ncs_s"], 0, is_last, first)
        if ti == 0:
            DBG("fT_A", fT, fT[:, :, :T], [128, 12, T])
        if limit.get("stopA", 99) <= 1:
            return
        b = pget()
        for k in range(8):
            MM(b[:T, 0:16], hT[:, k, :T], PA["WB"][:, k, OFF_DT:OFF_DT + 16], [hT, PA["WB"]], [b], start=(k == 0), stop=(k == 7))
        A_, B_, C_, D_, E_ = (sm[k] for k in "abcde")
        TT(A_[:T, 0:16], b[:T, 0:16], vec16[:T, 0, :], ALU.add, [b, vec16], [A_])
        ACT(A_[:T, 0:16], A_[:T, 0:16], AF.Exp, [A_], [A_])
        ACT(A_[:T, 16:32], A_[:T, 0:16], AF.Ln, [A_], [A_], bias=1.0)
        TT(A_[:T, 32:48], A_[:T, 16:32], vec16[:T, 1, :], ALU.mult, [A_, vec16], [A_])
        cum_and_tot(cfg, A_[:T, 32:48], A_, 16, B_, dbc, 128)
        TT(C_[:T, 0:16], B_[:T, 16:32], B_[:T, 0:16], ALU.subtract, [B_], [C_])
        ACT(C_[:T, 0:16], C_[:T, 0:16], AF.Exp, [C_], [C_])
        ACT(C_[:T, 16:32], B_[:T, 0:16], AF.Exp, [B_], [C_])
        b = pget()
        TR(b[:16, 0:T], B_[:T, 0:16], identf[:T, :T], [B_, identf], [b])
        ACT(acT[:, :T], b[:16, 0:T], AF.Copy, [b], [acT])
        if ti == 0:
            DBG("acum", B_, B_[:T, 0:32], [T, 32])
        if limit.get("stopA", 99) <= 2:
            return
        bx, bb = pget(), pget()
        for c in range(8):
            TR(pb(bx, [8, 128], p=T)[:, c, :], fT[:, c, :T], identb[:, :], [fT, identb], [bx])
        for c in range(2):
            TR(pb(bb, [2, 128], p=T)[:, c, :], fT[:, 8 + c, :T], identb[:, :], [fT, identb], [bb])
        xc, xdec, xsD = bigb[0], bigb[1], big[0]
        TT(xc[:T, :].rearrange("p (h q) -> p h q", q=64), pb(bx, [16, 64], p=T), bl(A_[:T, 16:32], 64), ALU.mult, [bx, A_], [xc])
        TT(xsD[:T, :].rearrange("p (h q) -> p h q", q=64), pb(bx, [16, 64], p=T), bl(vec16[:T, 2, :], 64), ALU.mult, [bx, vec16], [xsD])
        TT(xdec[:T, :].rearrange("p (h q) -> p h q", q=64), xc[:T, :].rearrange("p (h q) -> p h q", q=64), bl(C_[:T, 0:16], 64), ALU.mult,
           [xc, C_], [xdec], eng="pool")
        ACT(Btok[:T, :], pb(bb, [256], p=T), AF.Copy, [bb], [Btok])
        b = pget()
        for g in range(2):
            MM(pf(b, [2, T], p=T)[:, g, :], fT[:, 8 + g, :T], fT[:, 10 + g, :T], [fT], [b])
        ACT(Gs[:T, :, :T], pf(b, [2, T], p=T), AF.Copy, [b], [Gs])
        for q in range(4):
            b = pget()
            for j in range(4):
                h = q * 4 + j
                MM(pf(b, [4, T], p=T)[:, j, :], identf[0:16, h:h + 1].to_broadcast([16, T]), acT[:, :T], [identf, acT], [b])
            for j in range(4):
                h = q * 4 + j
                STT(earg[:T, j, :T], pf(b, [4, T], p=T)[:, j, :], B_[:T, h:h + 1], cfg.nmI[:, :], ALU.subtract, ALU.add,
                    [b, B_, cfg.nmI], [earg])
            ACT(Ee[:T, :, :T], earg[:T, :, :T], AF.Exp, [earg], [Ee])
            TT(WT[:T, q * 4:q * 4 + 4, :T], Ee[:T, :, :T], bm(Gs[:T, q // 2, :T], 4), ALU.mult, [Ee, Gs], [WT])
        if limit.get("stopA", 99) <= 3:
            return
        yp = [pget(), pget()]
        for h in range(16):
            MM(yp[h // 8][:T, (h % 8) * 64:(h % 8 + 1) * 64], WT[:T, h, :T], xc[:T, h * 64:(h + 1) * 64], [WT, xc], [yp[h // 8]])
        ppin(yp[0]); ppin(yp[1])
        yo = [pget(), pget()]
        ppin(yo[0]); ppin(yo[1])
        if cfg.chain:
            if first:
                MSET(stf[:, :], 0.0, [stf])
                MSET(stb[:, :], 0.0, [stb])
            for c in range(nch):
                ssd_state_update(cfg, c, stf, stb, stf, yo, xdec, c == 0, c == nch - 1)
                ACT(stb[:, :], stf[:, :], AF.Copy, [stf], [stb])
            if is_last:
                for half in range(2):
                    b = pget()
                    for k in range(4):
                        kk = half * 4 + k
                        TR(pf(b, [4, 128])[:, k, :], stf[:, :].rearrange("n (j k) -> n k j", k=8)[:, kk, :], identf[:, :], [stf, identf], [b])
                    CP_(hn[:, half * 4:half * 4 + 4, :], pf(b, [4, 128]), [b], [hn])
                STO(O["nssm_p"][l].rearrange("(p k) n -> p k n", k=8), hn[:], hn)
        else:
            for c in range(limit.get("nchs", nch)):
                LD(hn[:], I["st_ssm"][l, c].rearrange("h q n -> (h q) n").rearrange("(p k) n -> p k n", k=8), hn)
                if limit.get("sub", 9) <= 0:
                    continue
                for half in range(2):
                    b = pget()
                    for k in range(4):
                        kk = half * 4 + k
                        TR(pf(b, [4, 128])[:, k, :], hn[:, kk, :], identf[:, :], [hn, identf], [b])
                    if limit.get("sub", 9) <= 1:
                        continue
                    ACT(stf_s[:, :].rearrange("n (j k) -> n k j", k=8)[:, half * 4:half * 4 + 4, :], pf(b, [4, 128]), AF.Copy, [b], [stf_s])
                    if limit.get("sub", 9) <= 2:
                        continue
                    CP_(stb_s[:, :].rearrange("n (j k) -> n k j", k=8)[:, half * 4:half * 4 + 4, :], pf(b, [4, 128]), [b], [stb_s])
                if limit.get("noupd"):
                    continue
                ssd_state_update(cfg, c, stf_s, stb_s, stf_s, yo, xdec, c == 0, c == limit.get("nchs", nch) - 1)
                if limit.get("noback"):
                    continue
                for half in range(2):
                    b = pget()
                    for k in range(4):
                        kk = half * 4 + k
                        TR(pf(b, [4, 128])[:, k, :], stf_s[:, :].rearrange("n (j k) -> n k j", k=8)[:, kk, :], identf[:, :], [stf_s, identf], [b])
                    CP_(hn[:, half * 4:half * 4 + 4, :], pf(b, [4, 128]), [b], [hn])
                STO(O["nssm_s"][l, c].rearrange("(p k) n -> p k n", k=8), hn[:], hn)
        if limit.get("stopA", 99) <= 4:
            punpin(yp[0]); punpin(yp[1]); punpin(yo[0]); punpin(yo[1])
            return
        t1, t2 = big[1], big[2]
        for g in range(2):
            TT(t1[:T, g * 512:(g + 1) * 512].rearrange("p (h q) -> p h q", q=64), yo[g][:T, :].rearrange("p (h q) -> p h q", q=64),
               bl(C_[:T, 16 + g * 8:16 + g * 8 + 8], 64), ALU.mult, [yo[g], C_], [t1])
            TT(t2[:T, g * 512:(g + 1) * 512], yp[g][:T, :], t1[:T, g * 512:(g + 1) * 512], ALU.add, [yp[g], t1], [t2])
        for x_ in yp + yo:
            punpin(x_)
        TT(t2[:T, :], t2[:T, :], xsD[:T, :], ALU.add, [t2, xsD], [t2], eng="pool")
        if ti == 0:
            DBG("yssd", t2, t2[:T, :], [T, D])
        for g in range(2):
            b = pget()
            for k in range(8):
                MM(b[:T, :], hT[:, k, :T], PA["WB"][:, k, OFF_ZS + g * 512:OFF_ZS + (g + 1) * 512], [hT, PA["WB"]], [b], start=(k == 0), stop=(k == 7))
            ACT(t1[:T, g * 512:(g + 1) * 512], b[:T, :], AF.Silu, [b], [t1])
        TT(t2[:T, :], t2[:T, :], t1[:T, :], ALU.mult, [t2, t1], [t2])
        for g in range(2):
            ACT(junk[:T, g * 512:(g + 1) * 512], t2[:T, g * 512:(g + 1) * 512], AF.Square, [t2], [junk, sm1], accum_out=sm1[:T, 2 + g:3 + g])
        TS(sm1[:T, 2:4], sm1[:T, 2:4], 1.0 / 512, EPS, ALU.mult, ALU.add, [sm1], [sm1])
        ACT(sm1[:T, 2:4], sm1[:T, 2:4], AF.Ln, [sm1], [sm1])
        ACT(sm1[:T, 2:4], sm1[:T, 2:4], AF.Exp, [sm1], [sm1], scale=-0.5)
        ynb = bigb[2]
        for g in range(2):
            ACT(ynb[:T, g * 512:(g + 1) * 512], t2[:T, g * 512:(g + 1) * 512], AF.Copy, [t2, sm1], [ynb], scale=sm1[:T, 2 + g:3 + g])
        out_proj(cfg, ynb, 8)

    def out_proj(cfg, ynb, nk):
        T = cfg.T
        b = pget()
        for c in range(nk):
            TR(pb(b, [8, T])[:, c, :], ynb[:T, c * 128:(c + 1) * 128], identb[:T, :T], [ynb, identb], [b])
        CP_(hT[:, 0:nk, :T], pb(b, [8, T])[:, 0:nk, :], [b], [hT])
        for half in range(2):
            b = pget()
            for k in range(nk):
                MM(b[:T, :], hT[:, k, :T], WO[:, k, half * 512:(half + 1) * 512], [hT, WO], [b], start=(k == 0), stop=(k == nk - 1))
            TT(big[3][:T, half * 512:(half + 1) * 512], b[:T, :], gate_bc[cfg.name][:T, half * 512:(half + 1) * 512], ALU.mult,
               [b, gate_bc[cfg.name]], [big[3]])

    PB = {}

    def alloc_B(pes, tag):
        def sbp(name, shape, dt=F32):
            return Buf(pes.enter_context(nc.sbuf_tensor(name + tag, list(shape), dt)), name)
        PB.update(Sp_f=sbp("Sp_f", [128, 4, 128]), Sp_b=sbp("Sp_b", [128, 4, 128], BF16), Ss_in=sbp("Ss_in", [128, 16, 128]),
                  Ss_b=sbp("Ss_b", [128, 16, 128], BF16), kv=sbp("kv", [128, 8, 128], BF16), kbg=sbp("kbg", [128, 4, 128]),
                  vb_=sbp("vb", [128, 4, 128]), kdec=sbp("kdec", [128, 4, 128], BF16), kdm=sbp("kdm", [64, 2, 128], BF16),
                  LG=sbp("LG", [128, 12]), LGT=sbp("LGT", [12, 128]), ea=sbp("ea", [128, 4, 128]),
                  E2a=sbp("E2a", [128, 4, 128]), E2b=sbp("E2b", [128, 4, 128]),
                  X0=sbp("X0", [128, 4, 128]), X1=sbp("X1", [128, 4, 128]), Y0=sbp("Y0", [128, 4, 128]), Y1=sbp("Y1", [128, 4, 128]),
                  R0=sbp("R0", [128, 4, 128]), R1=sbp("R1", [128, 4, 128]), attnT=sbp("attnT", [128, 4, 128], BF16),
                  egb=sbp("egb", [128, 4, 128]), qsq=sbp("qsq", [128, 4, 128], BF16),
                  nwTm=sbp("nwTm", [128, 1024], BF16), qdm=sbp("qdm", [128, 1024], BF16), u_sb=sbp("u_sb", [128, 4, 128]),
                  vnew=sbp("vnew", [128, 4, 128], BF16), dbcS=sbp("dbcS", [128, 64]), WB=sbp("WB", [128, 8, 2056], BF16))
        PB["E2"] = None

    def r32(ap):
        return ap.bitcast(F32R)

    def phaseB_tile(cfg, l, ti, xt, h0, first, is_last):
        T, L, nseq, nch = cfg.T, cfg.L, cfg.nseq, cfg.nch
        (Sp_f, Sp_b, Ss_in, Ss_b, kv, kbg, vb_, kdec, kdm, LG, LGT, ea, attnT, egb, nwTm, qdm, u_sb, vnew, dbcS) = (
            PB[k] for k in ("Sp_f", "Sp_b", "Ss_in", "Ss_b", "kv", "kbg", "vb_", "kdec", "kdm", "LG", "LGT", "ea", "attnT",
                            "egb", "nwTm", "qdm", "u_sb", "vnew", "dbcS"))
        E2 = [PB["E2a"], PB["E2b"]]
        Ss_out = Ss_in
        Xs, Ys, Rs = [PB["X0"], PB["X1"]], [PB["Y0"], PB["Y1"]], [PB["R0"], PB["R1"]]
        WB = PB["WB"]
        prologue(cfg, xt)
        inproj_conv_B(cfg, l, h0, is_last, first)
        A_, B_, C_, D_, E_ = (sm[k] for k in "abcde")
        b = pget()
        for k in range(8):
            MM(b[:T, 0:8], hT[:, k, :T], WB[:, k, 2048:2056], [hT, WB], [b], start=(k == 0), stop=(k == 7))
        TT(A_[:T, 0:4], b[:T, 0:4], vec8[:T, 0, h0:h0 + 4], ALU.add, [b, vec8], [A_])
        ACT(A_[:T, 0:4], A_[:T, 0:4], AF.Exp, [A_], [A_])
        ACT(A_[:T, 0:4], A_[:T, 0:4], AF.Ln, [A_], [A_], bias=1.0)
        TT(A_[:T, 4:8], A_[:T, 0:4], vec8[:T, 1, h0:h0 + 4], ALU.mult, [A_, vec8], [A_])
        ACT(A_[:T, 8:12], b[:T, 4:8], AF.Exp, [b], [A_], scale=-1.0)
        ACT(A_[:T, 8:12], A_[:T, 8:12], AF.Ln, [A_], [A_], bias=1.0)
        ACT(A_[:T, 12:16], A_[:T, 8:12], AF.Exp, [A_], [A_], scale=-1.0)
        ACT(A_[:T, 16:20], A_[:T, 8:12], AF.Copy, [A_], [A_], scale=-1.0)
        cum_and_tot(cfg, A_[:T, 4:8], A_, 4, B_, dbcS, 128)
        ACT(C_[:T, 0:4], B_[:T, 0:4], AF.Exp, [B_], [C_])
        TT(C_[:T, 4:8], B_[:T, 4:8], B_[:T, 0:4], ALU.subtract, [B_], [C_])
        ACT(C_[:T, 4:8], C_[:T, 4:8], AF.Exp, [C_], [C_])
        b = pget()
        for c in range(8):
            TR(pb(b, [8, 128], p=T)[:, c, :], fT[:, 4 + c, :T], identb[:, :], [fT, identb], [b])
        ACT(kv[:T, :, :], pb(b, [8, 128], p=T), AF.Copy, [b], [kv])
        ksq = big[0]
        TT(ksq[:T, 0:512].rearrange("p (h d) -> p h d", d=128), kv[:T, 0:4, :], kv[:T, 0:4, :], ALU.mult, [kv], [ksq])
        RED(D_[:T, 0:4], ksq[:T, 0:512].rearrange("p (h d) -> p h d", d=128), [ksq], [D_])
        TS(D_[:T, 0:4], D_[:T, 0:4], EPS, None, ALU.add, ALU.bypass, [D_], [D_])
        ACT(D_[:T, 4:8], D_[:T, 0:4], AF.Ln, [D_], [D_])
        ACT(D_[:T, 8:12], D_[:T, 4:8], AF.Exp, [D_], [D_], scale=-0.5)
        qsq = PB["qsq"]
        ACT(qsq[:, :, :T], fT[:, 0:4, :T], AF.Square, [fT], [qsq])
        b = pget()
        for j in range(4):
            MM(b[:T, j:j + 1], qsq[:, j, :T], onesb[:, 0:1], [qsq, onesb], [b])
        TS(D_[:T, 12:16], b[:T, 0:4], EPS, None, ALU.add, ALU.bypass, [b], [D_])
        ACT(D_[:T, 12:16], D_[:T, 12:16], AF.Ln, [D_], [D_])
        ACT(D_[:T, 16:20], D_[:T, 12:16], AF.Exp, [D_], [D_], scale=-0.5)
        TS(D_[:T, 16:20], D_[:T, 16:20], float(128 ** -0.5), None, ALU.mult, ALU.bypass, [D_], [D_])
        TT(E_[:T, 0:4], D_[:T, 8:12], A_[:T, 12:16], ALU.mult, [D_, A_], [E_])
        TT(E_[:T, 4:8], E_[:T, 0:4], C_[:T, 0:4], ALU.mult, [E_, C_], [E_])
        TT(E_[:T, 8:12], D_[:T, 8:12], C_[:T, 4:8], ALU.mult, [D_, C_], [E_])
        TS(E_[:T, 12:16], E_[:T, 0:4], -1.0, None, ALU.mult, ALU.bypass, [E_], [E_])
        TS(E_[:T, 16:20], D_[:T, 8:12], -1.0, None, ALU.mult, ALU.bypass, [D_], [E_])
        CP_(LG[:T, 0:4], B_[:T, 0:4], [B_], [LG])
        STT(LG[:T, 4:8], D_[:T, 4:8], -0.5, B_[:T, 0:4], ALU.mult, ALU.subtract, [D_, B_], [LG])
        STT(LG[:T, 8:12], D_[:T, 4:8], -0.5, B_[:T, 0:4], ALU.mult, ALU.add, [D_, B_], [LG])
        TT(LG[:T, 8:12], LG[:T, 8:12], A_[:T, 16:20], ALU.add, [LG, A_], [LG])
        b = pget()
        TR(b[:12, 0:T], LG[:T, 0:12], identf[:T, :T], [LG, identf], [b])
        ACT(LGT[:, :T], b[:12, 0:T], AF.Copy, [b], [LGT])
        TT(r32(kbg[:T, :, :]), kv[:T, 0:4, :], bl(E_[:T, 4:8], 128), ALU.mult, [kv, E_], [kbg])
        TT(kdec[:T, :, :], kv[:T, 0:4, :], bl(E_[:T, 8:12], 128), ALU.mult, [kv, E_], [kdec])
        TT(r32(vb_[:T, :, :]), kv[:T, 4:8, :], bl(A_[:T, 12:16], 128), ALU.mult, [kv, A_], [vb_])
        bG, bA = pget(), pget()
        for j in range(4):
            MM(pf(bG, [4, T], p=T)[:, j, :], fT[:, 4 + j, :T], fT[:, 4 + j, :T], [fT], [bG])
        for j in range(4):
            MM(pf(bA, [4, T], p=T)[:, j, :], fT[:, 4 + j, :T], fT[:, j, :T], [fT], [bA])
        bcs = [pget(), pget(), pget()]
        for r in range(3):
            for j in range(4):
                MM(pf(bcs[r], [4, T], p=T)[:, j, :], identf[0:12, r * 4 + j:r * 4 + j + 1].to_broadcast([12, T]), LGT[:, :T],
                   [identf, LGT], [bcs[r]])
        be = pget()
        for j in range(4):
            MM(pf(be, [4, T])[:, j, :], identf[0:12, j:j + 1].to_broadcast([12, 128]), LGT[:, :T], [identf, LGT], [be])
        X, Y, R = Xs[0], Ys[0], Rs[0]
        kinds = ((1, ALU.add, cfg.nmSr, 0), (2, ALU.subtract, cfg.nmS, 1), (0, ALU.subtract, cfg.nmI, 2))
        for r, op0, msk, ki in kinds:
            for j in range(4):
                STT(ea[:T, j, :T], pf(bcs[r], [4, T], p=T)[:, j, :], B_[:T, j:j + 1], msk[:, :], op0, ALU.add, [bcs[r], B_, msk], [ea])
            Ek = E2[ki % 2]
            ACT(Ek[:T, :, :T], ea[:T, :, :T], AF.Exp, [ea], [Ek])
            for j in range(4):
                if ki == 0:
                    STT(r32(X[:T, j, :T]), pf(bG, [4, T], p=T)[:, j, :], E_[:T, 12 + j:13 + j], Ek[:T, j, :T], ALU.mult, ALU.mult, [bG, E_, Ek], [X])
                elif ki == 1:
                    STT(r32(Y[:T, j, :T]), pf(bG, [4, T], p=T)[:, j, :], E_[:T, 16 + j:17 + j], Ek[:T, j, :T], ALU.mult, ALU.mult, [bG, E_, Ek], [Y])
                else:
                    STT(attnT[:T, j, :T], pf(bA, [4, T], p=T)[:, j, :], D_[:T, 8 + j:9 + j], Ek[:T, j, :T], ALU.mult, ALU.mult, [bA, D_, Ek], [attnT])
        ACT(egb[:, :, :T], pf(be, [4, T]), AF.Exp, [be], [egb])
        TT(r32(R[:T, :, :T]), Y[:T, :, :T], bm(identf[:T, :T], 4), ALU.add, [Y, identf], [R])
        for lev in range(cfg.nlev):
            X2, Y2, R2 = Xs[(lev + 1) % 2], Ys[(lev + 1) % 2], Rs[(lev + 1) % 2]
            lastlev = (lev == cfg.nlev - 1)
            bX, bY, bR = pget(), (None if lastlev else pget()), pget()
            for j in range(4):
                MM(pf(bX, [4, T], p=T)[:, j, :], r32(Y[:T, j, :T]), r32(X[:T, j, :T]), [X, Y], [bX])
            if not lastlev:
                for j in range(4):
                    MM(pf(bY, [4, T], p=T)[:, j, :], r32(X[:T, j, :T]), r32(Y[:T, j, :T]), [X, Y], [bY])
            ACT(r32(X2[:T, :, :T]), pf(bX, [4, T], p=T), AF.Copy, [bX], [X2])
            if not lastlev:
                CP_(r32(Y2[:T, :, :T]), pf(bY, [4, T], p=T), [bY], [Y2])
            for j in range(4):
                MM(pf(bR, [4, T], p=T)[:, j, :], r32(X2[:T, j, :T]), r32(R[:T, j, :T]), [X2, R], [bR])
            TT(r32(R2[:T, :, :T]), R[:T, :, :T], pf(bR, [4, T], p=T), ALU.add, [R, bR], [R2])
            X, Y, R = X2, Y2, R2
        bU, bW = pget(), pget()
        for j in range(4):
            MM(pf(bU, [4, 128], p=T)[:, j, :], r32(R[:T, j, :T]), r32(vb_[:T, j, :]), [R, vb_], [bU])
        for j in range(4):
            MM(pf(bW, [4, T])[:, j, :], r32(kbg[:T, j, :]), r32(R[:T, j, :T]), [kbg, R], [bW])
        ACT(u_sb[:T, :, :], pf(bU, [4, 128], p=T), AF.Copy, [bU], [u_sb])
        TT(egb[:, :, :T], fT[:, 0:4, :T], egb[:, :, :T], ALU.mult, [fT, egb], [egb])
        vnb, ob = pget(), pget()
        ppin(vnb)
        ppin(ob)
        vn4, o4 = pf(vnb, [4, 128], p=T), pf(ob, [4, 128], p=T)
        if cfg.chain:
            if first:
                MSET(Sp_f[:], 0.0, [Sp_f])
                MSET(Sp_b[:], 0.0, [Sp_b])
            nwv = nwTm[:, 0:nch * 4 * T].rearrange("p (c j t) -> p c j t", c=nch, j=4)
            qdv = qdm[:, 0:nch * 4 * T].rearrange("p (c j t) -> p c j t", c=nch, j=4)
            for c in range(nch):
                TT(nwv[:, c], pf(bW, [4, T]), bm(cfg.ncmf[:, c, :], 4), ALU.mult, [bW, cfg.ncmf], [(nwTm, c)])
                TT(qdv[:, c], egb[:, :, :T], bm(cfg.cmf[:, c, :], 4), ALU.mult, [egb, cfg.cmf], [(qdm, c)], eng="pool")
            for c in range(nch):
                for j in range(4):
                    MM(vn4[:, j, :], nwv[:, c, j, :], Sp_b[:, j, :], [(nwTm, c), Sp_b], [vnb], start=(j == 0 and c == 0), stop=False, skip_group_check=True)
                for j in range(4):
                    MM(o4[:, j, :], qdv[:, c, j, :], Sp_b[:, j, :], [(qdm, c), Sp_b], [ob], start=(j == 0 and c == 0), stop=False, skip_group_check=True)
                TT(vnew[:T, :, :], vn4, u_sb[:T, :, :], ALU.add, [vnb, u_sb], [vnew])
                bs = pget()
                for j in range(4):
                    MM(pf(bs, [4, 128])[:, j, :], kdec[c * 64:(c + 1) * 64, j, :], vnew[c * 64:(c + 1) * 64, j, :], [kdec, vnew], [bs])
                tmpS = big[0][:, 512:1024].rearrange("p (j v) -> p j v", v=128)
                TT(tmpS, Sp_f[:, :, :], bl(dbcS[:, c * 4:(c + 1) * 4], 128), ALU.mult, [Sp_f, dbcS], [big[0]])
                TT(Sp_f[:, :, :], tmpS, pf(bs, [4, 128]), ALU.add, [big[0], bs], [Sp_f])
                ACT(Sp_b[:, :, :], Sp_f[:, :, :], AF.Copy, [Sp_f], [Sp_b])
            if is_last:
                STO(O["ngdn_p"][l, h0:h0 + 4].rearrange("h k v -> k h v"), Sp_f[:], Sp_f)
        else:
            nwv = nwTm[:, 0:nch * T].rearrange("p (c t) -> p c t", t=T)
            qdv = qdm[:, 0:nch * T].rearrange("p (c t) -> p c t", t=T)
            ppin(bW)
            for j in range(4):
                h = h0 + j
                LD(Ss_in[:], I["st_gdn"][l, :, h].rearrange("b k v -> k b v"), Ss_in)
                CP_(Ss_b[:], Ss_in[:], [Ss_in], [Ss_b], eng="pool")
                TT(nwv, bm(pf(bW, [4, T])[:, j, :], nch), cfg.ncmf[:, :, :], ALU.mult, [bW, cfg.ncmf], [nwTm])
                TT(qdv, bm(egb[:, j, :T], nch), cfg.cmf[:, :, :], ALU.mult, [egb, cfg.cmf], [qdm], eng="pool")
                for c in range(nch):
                    MM(vn4[:, j, :], nwv[:, c, :], Ss_b[:, c, :], [nwTm, Ss_b], [vnb], start=(j == 0 and c == 0), stop=False, skip_group_check=True)
                for c in range(nch):
                    MM(o4[:, j, :], qdv[:, c, :], Ss_b[:, c, :], [qdm, Ss_b], [ob], start=(j == 0 and c == 0), stop=False, skip_group_check=True)
                TT(vnew[:T, j, :], vn4[:, j, :], u_sb[:T, j, :], ALU.add, [vnb, u_sb], [vnew])
                tmpS = big[2][:, :].rearrange("p (c v) -> p c v", v=128)
                for hf in range(2):
                    TT(tmpS, Ss_in[:, hf * 8:(hf + 1) * 8, :],
                       bl(dbcS[:, 0:nch * 4].rearrange("p (c j) -> p c j", j=4)[:, hf * 8:(hf + 1) * 8, j], 128), ALU.mult, [Ss_in, dbcS], [big[2]])
                    for c4 in range(hf * 8, hf * 8 + 8, 4):
                        bs = pget()
                        for cc in range(4):
                            c = c4 + cc
                            sl = mslot[0] % 2
                            mslot[0] += 1
                            TS(kdm[:T, sl, :], kdec[:T, j, :], cfg.cmT[:, c:c + 1], 1.0, ALU.mult, ALU.mult, [kdec, cfg.cmT], [(kdm, sl)], eng="pool")
                            MM(pf(bs, [4, 128])[:, cc, :], kdm[:T, sl, :], vnew[:T, j, :], [(kdm, sl), vnew], [bs])
                        TT(Ss_out[:, c4:c4 + 4, :], tmpS[:, c4 - hf * 8:c4 - hf * 8 + 4, :], pf(bs, [4, 128]), ALU.add, [big[2], bs], [Ss_out])
                STO(O["ngdn_s"][l, :, h].rearrange("b k v -> k b v"), Ss_out[:], Ss_out)
            punpin(bW)
        for j in range(4):
            MM(o4[:, j, :], attnT[:T, j, :T], vnew[:T, j, :], [attnT, vnew], [ob], start=False, stop=True, skip_group_check=True)
        punpin(vnb)
        of = big[1]
        ACT(of[:T, 0:512], ob[:T, :], AF.Copy, [ob], [of])
        punpin(ob)
        if ti == 0 and h0 == 0:
            DBG("ogdn", of, of[:T, 0:512], [T, 512])
        TT(ksq[:T, 0:512], of[:T, 0:512], of[:T, 0:512], ALU.mult, [of], [ksq])
        RED(C_[:T, 8:12], ksq[:T, 0:512].rearrange("p (h d) -> p h d", d=128), [ksq], [C_])
        TT(C_[:T, 12:16], D_[:T, 16:20], D_[:T, 16:20], ALU.mult, [D_], [C_])
        TT(C_[:T, 8:12], C_[:T, 8:12], C_[:T, 12:16], ALU.mult, [C_], [C_])
        TS(C_[:T, 8:12], C_[:T, 8:12], 1.0 / 128, EPS, ALU.mult, ALU.add, [C_], [C_])
        ACT(C_[:T, 8:12], C_[:T, 8:12], AF.Ln, [C_], [C_])
        ACT(C_[:T, 8:12], C_[:T, 8:12], AF.Exp, [C_], [C_], scale=-0.5)
        TT(C_[:T, 8:12], C_[:T, 8:12], D_[:T, 16:20], ALU.mult, [C_, D_], [C_])
        b = pget()
        for k in range(8):
            MM(b[:T, :], hT[:, k, :T], WB[:, k, 0:512], [hT, WB], [b], start=(k == 0), stop=(k == 7))
        sz = big[2]
        ACT(sz[:T, 0:512], b[:T, :], AF.Silu, [b], [sz])
        TT(of[:T, 0:512].rearrange("p (h d) -> p h d", d=128), of[:T, 0:512].rearrange("p (h d) -> p h d", d=128), bl(C_[:T, 8:12], 128),
           ALU.mult, [of, C_], [of])
        onb = bigb[2]
        TT(onb[:T, 0:512], of[:T, 0:512], sz[:T, 0:512], ALU.mult, [of, sz], [onb])
        out_proj(cfg, onb, 4)

    def inproj_conv_B(cfg, l, h0, is_last, first):
        T, L, nseq = cfg.T, cfg.L, cfg.nseq
        rawv = raw[:, :, 0:nseq * (L + 3)].rearrange("p c (b l) -> p c b l", l=L + 3)
        if nseq == 1:
            if first:
                MSET(raw[:, :, 0:3], 0.0, [raw])
            else:
                CP_(raw[:, :, 0:3], raw[:, :, L:L + 3], [raw], [raw])
        else:
            for part in range(3):
                cs0 = part * 1024 + h0 * 128
                LD(cst_in[:, part * 512:(part + 1) * 512], I["st_gc"][l][:, :, cs0:cs0 + 512].rearrange("b k c -> (b k) c"), cst_in)
            for g0 in (0, 8):
                b = pget()
                ng = min(8, 12 - g0)
                for c in range(ng):
                    TR(pf(b, [8, 48])[:, c, :], cst_in[:, (g0 + c) * 128:(g0 + c + 1) * 128], identf[:48, :48], [cst_in, identf], [b])
                CP_(rawv[:, g0:g0 + ng, :, 0:3], pf(b, [8, 48])[:, 0:ng, :].rearrange("p c (b k) -> p c b k", k=3), [b], [raw])
        for g0 in range(0, 12, 4):
            b = pget()
            for c in range(4):
                for k in range(8):
                    MM(pf(b, [4, T])[:, c, :], PB["WB"][:, k, 512 + (g0 + c) * 128: 512 + (g0 + c + 1) * 128], hT[:, k, :T],
                       [PB["WB"], hT], [b], start=(k == 0), stop=(k == 7))
            if nseq == 1:
                ACT(raw[:, g0:g0 + 4, 3:3 + L], pf(b, [4, T]), AF.Copy, [b], [raw])
            else:
                for c in range(4):
                    ACT(rawv[:, g0 + c, :, 3:3 + L], pf(b, [4, T])[:, c, :].rearrange("p (b l) -> p b l", l=L), AF.Copy, [b], [raw])
            if is_last and limit.get("tail", True):
                for c in range(4):
                    ACT(tailf[:, g0 + c, 0:nseq * 3].rearrange("p (b k) -> p b k", k=3),
                        pf(b, [4, T])[:, c, :].rearrange("p (b l) -> p b l", l=L)[:, :, L - 3:L], AF.Copy, [b], [tailf])
        for g0 in range(0, 12, 4):
            b = pget()
            for c in range(4):
                o = pf(b, [4, T])[:, c, :]
                if nseq > 1:
                    o = o.rearrange("p (b l) -> p b l", l=L)
                for k in range(4):
                    MM(o, DIAG[:, g0 + c, k, :], rawv[:, g0 + c, :, k:k + L] if nseq > 1 else raw[:, g0 + c, k:k + L],
                       [DIAG, raw], [b], start=(k == 0), stop=(k == 3))
            ACT(fT[:, g0:g0 + 4, :T], pf(b, [4, T]), AF.Silu, [b], [fT])
        if is_last:
            n3 = nseq * 3
            for g0 in range(0, 12, 4):
                b = pget()
                for c in range(4):
                    TR(pf(b, [4, 128], p=n3)[:, c, :], tailf[:, g0 + c, 0:n3], identf[:, :], [tailf, identf], [b])
                CP_(tailo[:n3, g0 * 128:(g0 + 4) * 128], b[:n3, :], [b], [tailo])
            dst = O["ngc_p"] if nseq == 1 else O["ngc_s"]
            for part in range(3):
                cs0 = part * 1024 + h0 * 128
                STO(dst[l][:, cs0:cs0 + 512], tailo[:n3, part * 512:(part + 1) * 512], tailo)

    def bvec(dst_ap, src_row_ap, buf):
        LD(dst_ap, src_row_ap.partition_broadcast(128), buf)

    tiles = [(CP, ti) for ti in range(limit.get("ntiles", NT))] + ([(CS, NT)] if limit.get("sample", True) else [])
    all_bufs.extend(allb)
    layer_in = None
    for l in range(limit.get("layers", DEPTH)):
        do_mod(l)
        S.barrier(all_bufs)
        mod_stacks.pop().close()
        for ph in limit.get("phases", (0, 1, 2)):
            acc_in = layer_in if ph == 0 else scr[(ph - 1)]
            last_phase = (l == DEPTH - 1 and ph == 2)
            acc_out = scr[ph] if ph < 2 else (scr[2] if not last_phase else None)
            S.barrier(all_bufs)
            pes = contextlib.ExitStack()
            cur_es[0] = pes
            if ph == 0:
                alloc_A(pes, "_%d_%d" % (l, ph))
            else:
                alloc_B(pes, "_%d_%d" % (l, ph))
            for v_ in list(PA.values()) + list(PB.values()):
                if v_ is not None and v_ not in all_bufs:
                    all_bufs.append(v_)
            if ph == 0:
                load_cols(PA["WB"], 0, I["w_in"][l][:, 0:2576], 2576)
                LD(nwT[:], I["ssd_norm_w"][l].rearrange("(k p) -> p k", p=128), nwT, nonc=True)
                load_wo(I["w_out"][l][0:1024, :], 8, lambda k: nwT[:, k:k + 1])
                build_diag(I["ssd_conv_w"][l], 0, 12)
                LD(cbrow_f[0:1, :], I["ssd_conv_b"][l:l + 1, :], cbrow_f)
                CP_(cbrow[:], cbrow_f[0:1, :], [cbrow_f], [cbrow])
                bvec(vec16[:, 0, :], I["ssd_dt_bias"][l:l + 1, :], vec16)
                bvec(vec16[:, 1, :], I["ssd_a_log"][l:l + 1, :], vec16)
                bvec(vec16[:, 2, :], I["ssd_d"][l:l + 1, :], vec16)
                ACT(vec16[:, 1, :], vec16[:, 1, :], AF.Exp, [vec16], [vec16])
                TS(vec16[:, 1, :], vec16[:, 1, :], -1.0, None, ALU.mult, ALU.bypass, [vec16], [vec16])
            else:
                h0 = (ph - 1) * 4
                load_cols(PB["WB"], 0, I["w_in"][l][:, OFF_ZG + h0 * 128:OFF_ZG + h0 * 128 + 512], 512)
                for part in range(3):
                    c0 = OFF_QKV + part * 1024 + h0 * 128
                    load_cols(PB["WB"], 512 + part * 512, I["w_in"][l][:, c0:c0 + 512], 512)
                load_cols(PB["WB"], 2048, I["w_in"][l][:, OFF_A + h0:OFF_A + h0 + 4], 4)
                load_cols(PB["WB"], 2052, I["w_in"][l][:, OFF_B + h0:OFF_B + h0 + 4], 4)
                LD(nwT[:, 0:1], I["gdn_norm_w"][l].rearrange("(p o) -> p o", o=1), nwT, nonc=True)
                load_wo(I["w_out"][l][1024 + h0 * 128:1024 + h0 * 128 + 512, :], 4, lambda k: nwT[:, 0:1])
                for part in range(3):
                    for k_ in range(4):
                        LD(cwT[:, part * 4:part * 4 + 4, k_],
                           I["gdn_conv_w"][l][k_, part * 1024 + h0 * 128:part * 1024 + h0 * 128 + 512].rearrange("(c p) -> p c", p=128), cwT, nonc=True)
                for c in range(12):
                    TT(DIAG[:, c, :, :], bm(identf[:], 4), bl(cwT[:, c, :], 128), ALU.mult, [identf, cwT], [DIAG], eng="pool")
                bvec(vec8[:, 0, :], I["gdn_dt_bias"][l:l + 1, :], vec8)
                bvec(vec8[:, 1, :], I["gdn_a_log"][l:l + 1, :], vec8)
                ACT(vec8[:, 1, :], vec8[:, 1, :], AF.Exp, [vec8], [vec8])
                TS(vec8[:, 1, :], vec8[:, 1, :], -1.0, None, ALU.mult, ALU.bypass, [vec8], [vec8])

            def issue_loads(idx):
                cfg, ti = tiles[idx]
                r0, T = rows(ti)
                xt = xt_slots[idx % 2]
                ap, dep = src_ap(layer_in.t if layer_in is not None else None, ti)
                LD(xt[:T, :], ap, xt, R=[(layer_in, ti)] if layer_in is not None else None)
                if ph > 0:
                    xa = xa_slots[idx % 2]
                    LD(xa[:T, :], acc_in.t[r0:r0 + T, :], xa, R=[(acc_in, ti)])

            if last_phase:
                fnw_bc = stage[0]
                fnw_v = stage[0][:, :, :].rearrange("p a b -> p (a b)")
                LD(fnw_v, I["final_norm_w"].partition_broadcast(128), stage[0])
            issue_loads(0)
            for idx, (cfg, ti) in enumerate(tiles):
                if idx + 1 < len(tiles):
                    issue_loads(idx + 1)
                r0, T = rows(ti)
                xt = xt_slots[idx % 2]
                xa = xa_slots[idx % 2] if ph > 0 else xt
                first = (ti == 0)
                is_last = (ti == NT - 1) or (ti == NT)
                if ph == 0:
                    phaseA_tile(cfg, l, ti, xt, first, is_last)
                else:
                    phaseB_tile(cfg, l, ti, xt, (ph - 1) * 4, first, is_last)
                TT(xo[:T, :], big[3][:T, :], xa[:T, :], ALU.add, [big[3], xa], [xo], eng="pool")
                if not last_phase:
                    STO(acc_out.t[r0:r0 + T, :], xo[:T, :], xo, W=[(acc_out, ti)])
                else:
                    ACT(junk[:T, :], xo[:T, :], AF.Square, [xo], [junk, sm1], accum_out=sm1[:T, 4:5])
                    TS(sm1[:T, 4:5], sm1[:T, 4:5], 1.0 / D, EPS, ALU.mult, ALU.add, [sm1], [sm1])
                    ACT(sm1[:T, 4:5], sm1[:T, 4:5], AF.Ln, [sm1], [sm1])
                    ACT(sm1[:T, 4:5], sm1[:T, 4:5], AF.Exp, [sm1], [sm1], scale=-0.5)
                    STT(big[0][:T, :], xo[:T, :], sm1[:T, 4:5], fnw_v[:T, :], ALU.mult, ALU.mult, [xo, sm1, fnw_bc], [big[0]])
                    dsto = O["y_p"][r0:r0 + T, :] if ti < NT else O["y_s"][:, :]
                    STO(dsto, big[0][:T, :], big[0])
            pes.close()
            cur_es[0] = es
            PA.clear()
            PB.clear()
        layer_in = scr[2]
    S.emit()
    es.close()
    return nc, dbg_out


_CACHE = {}


def make_in_maps(inputs):
    g = {k: np.ascontiguousarray(np.asarray(v, dtype=np.float32)) for k, v in inputs.items()}
    consts = host_consts()
    maps = []
    for c in range(8):
        m = {}
        m["xp"] = g["x_prompt"][c]
        m["xs"] = g["x_sample"][c * NSB:(c + 1) * NSB].reshape(NSB * SL, D)
        m["call"] = np.concatenate([g["c_prompt"][c:c + 1], g["c_sample"][c * NSB:(c + 1) * NSB]], axis=0)
        m["st_sc"] = g["state_ssd_conv"][:, c * NSB:(c + 1) * NSB]
        m["st_ssm"] = g["state_ssm"][:, c * NSB:(c + 1) * NSB]
        m["st_gc"] = g["state_gdn_conv"][:, c * NSB:(c + 1) * NSB]
        m["st_gdn"] = g["state_gdn"][:, c * NSB:(c + 1) * NSB]
        for k in ("norm_w", "w_ada", "b_ada", "w_in", "ssd_conv_w", "ssd_conv_b", "ssd_dt_bias", "ssd_a_log", "ssd_d",
                  "ssd_norm_w", "gdn_conv_w", "gdn_dt_bias", "gdn_a_log", "gdn_norm_w", "w_out"):
            m[k] = g[k]
        m["final_norm_w"] = g["final_norm_w"].reshape(1, D)
        for k, v in consts.items():
            m["c_" + k] = v
        maps.append({k: np.ascontiguousarray(v) for k, v in m.items()})
    return maps


def kernel(**inputs):
    if "nc" not in _CACHE:
        _CACHE["nc"] = build()[0]
    nc = _CACHE["nc"]
    maps = make_in_maps(inputs)
    res = run_bass_kernel_spmd(nc, maps, core_ids=list(range(8)))
    R = res.results
    cat = lambda k, ax: np.concatenate([np.asarray(r[k]) for r in R], axis=ax)
    y_p = np.stack([np.asarray(r["y_p"]) for r in R], 0)
    y_s = np.stack([np.asarray(r["y_s"]).reshape(NSB, SL, D) for r in R], 0).reshape(8 * NSB, SL, D)
    ncs_p = np.stack([np.asarray(r["ncs_p"]) for r in R], 1)
    nssm_p = np.stack([np.asarray(r["nssm_p"]).reshape(DEPTH, 16, 64, 128) for r in R], 1)
    ngc_p = np.stack([np.asarray(r["ngc_p"]) for r in R], 1)
    ngdn_p = np.stack([np.asarray(r["ngdn_p"]) for r in R], 1)
    ncs_s = np.concatenate([np.asarray(r["ncs_s"]).reshape(DEPTH, NSB, 3, 1536) for r in R], 1)
    nssm_s = np.concatenate([np.asarray(r["nssm_s"]).reshape(DEPTH, NSB, 16, 64, 128) for r in R], 1)
    ngc_s = np.concatenate([np.asarray(r["ngc_s"]).reshape(DEPTH, NSB, 3, 3072) for r in R], 1)
    ngdn_s = np.concatenate([np.asarray(r["ngdn_s"]) for r in R], 1)
    outs = (y_p, y_s, ncs_p, nssm_p, ngc_p, ngdn_p, ncs_s, nssm_s, ngc_s, ngdn_s)
    return tuple(np.ascontiguousarray(o, dtype=np.float32) for o in outs)
```

```python
import contextlib
import numpy as np
import concourse.bass as bass
import concourse.mybir as mybir
from concourse.bass_utils import run_bass_kernel_spmd

F32 = mybir.dt.float32
F32R = mybir.dt.float32r
BF16 = mybir.dt.bfloat16
AF = mybir.ActivationFunctionType
ALU = mybir.AluOpType
AX = mybir.AxisListType

ENGS = ("pe", "dve", "act", "pool", "sp")


class Buf:
    def __init__(self, t, name):
        self.t = t
        self.name = name
        self.st = {}
        self.dma_sem = None
        self.dma_cnt = 0

    def __getitem__(self, k):
        return self.t[k]


class Instr:
    __slots__ = ("eng", "fn", "deps", "signal", "count", "is_dma", "buf_sem", "dma_count")

    def __init__(self, eng, fn):
        self.eng = eng
        self.fn = fn
        self.deps = []
        self.signal = False
        self.count = None
        self.is_dma = False
        self.buf_sem = None
        self.dma_count = None


class Sched:
    def __init__(self, nc):
        self.nc = nc
        self.streams = {e: [] for e in ENGS}
        self.all = []

    def _states(self, buf, key):
        st = buf.st
        if key is None:
            if None not in st:
                st[None] = [None, {}]
            return list(st.values())
        if key not in st:
            if None in st:
                st[key] = [st[None][0], dict(st[None][1])]
            else:
                st[key] = [None, {}]
        res = [st[key]]
        if None in st:
            res.append(st[None])
        return res

    @staticmethod
    def _norm(lst):
        out = []
        for x in lst or []:
            out.append((x, None) if isinstance(x, Buf) else x)
        return out

    def add(self, eng, fn, reads=None, writes=None, dma_buf=None):
        ins = Instr(eng, fn)
        ins.is_dma = dma_buf is not None
        reads = self._norm(reads)
        writes = self._norm(writes)
        deps, raw = [], []
        for buf, key in reads:
            for s in self._states(buf, key):
                if s[0] is not None:
                    deps.append(s[0])
                    raw.append(s[0])
                if getattr(buf, "excl", False):
                    for r in s[1].values():
                        if r.eng != eng:
                            deps.append(r)
        for buf, key in writes:
            for s in self._states(buf, key):
                if s[0] is not None:
                    deps.append(s[0])
                deps.extend(s[1].values())
        fdeps = {}
        for d in deps:
            if d is ins:
                continue
            if (not ins.is_dma) and (not d.is_dma) and d.eng == eng:
                if eng == "pe" or not any(d is r for r in raw):
                    continue
            if d.is_dma:
                fdeps[id(d)] = (d, d.buf_sem.dma_cnt * 16)
            else:
                fdeps[id(d)] = (d, None)
        ins.deps = list(fdeps.values())
        if ins.is_dma:
            ins.buf_sem = dma_buf
            dma_buf.dma_cnt += 1
            ins.dma_count = dma_buf.dma_cnt * 16
        rkey = ("dma", id(ins)) if ins.is_dma else eng
        for buf, key in reads:
            for s in self._states(buf, key):
                s[1][rkey] = ins
        for buf, key in writes:
            for s in self._states(buf, key):
                s[0] = ins
                if key is None or s is not buf.st.get(None):
                    s[1] = {}
        self.streams[eng].append(ins)
        self.all.append(ins)
        return ins

    def barrier(self, dma_bufs_all):
        last = {}
        for e in ("pe", "dve", "act", "pool"):
            for ins in reversed(self.streams[e]):
                if not isinstance(ins, tuple) and not ins.is_dma:
                    last[e] = ins
                    ins.signal = True
                    break
        snap = [(b, b.dma_cnt * 16) for b in dma_bufs_all if b.dma_cnt > 0]
        mark = ("barrier", last, snap)
        for e in ENGS:
            self.streams[e].append(mark)

    def emit(self, final_wait_eng="sp"):
        nc = self.nc
        for ins in self.all:
            for d, _v in ins.deps:
                if not d.is_dma:
                    d.signal = True
        with contextlib.ExitStack() as es:
            eng_sem = {e: es.enter_context(nc.semaphore("s_" + e)) for e in ("pe", "dve", "act", "pool")}
            dma_bufs = []
            for ins in self.all:
                if isinstance(ins, tuple):
                    continue
                if ins.is_dma and ins.buf_sem.dma_sem is None:
                    ins.buf_sem.dma_sem = es.enter_context(nc.semaphore("d_%d" % len(dma_bufs)))
                    dma_bufs.append(ins.buf_sem)
            cnt = {e: 0 for e in eng_sem}
            for ins in self.all:
                if not ins.is_dma and ins.signal:
                    cnt[ins.eng] += 1
                    ins.count = cnt[ins.eng]
            block = es.enter_context(nc.Block())
            engobj = {"pe": block.tensor, "dve": block.vector, "act": block.scalar, "pool": block.gpsimd,
                      "sp": block.sync}
            sched = self

            def make(ename):
                def body(eng):
                    water = {}
                    for ins in sched.streams[ename]:
                        if isinstance(ins, tuple):
                            _, last, snap = ins
                            for e2, li in last.items():
                                if e2 != ename and water.get(("e", e2), 0) < li.count:
                                    water[("e", e2)] = li.count
                                    eng.wait_ge(eng_sem[e2], li.count)
                            for b_, v_ in snap:
                                if water.get(("d", id(b_)), 0) < v_:
                                    water[("d", id(b_))] = v_
                                    eng.wait_ge(b_.dma_sem, v_)
                            continue
                        need = {}
                        for d, dv in ins.deps:
                            if d.is_dma:
                                key, sem, val = ("d", id(d.buf_sem)), d.buf_sem.dma_sem, dv
                            else:
                                key, sem, val = ("e", d.eng), eng_sem[d.eng], d.count
                            if need.get(key, (None, 0))[1] < val:
                                need[key] = (sem, val)
                        for key, (sem, val) in need.items():
                            if water.get(key, 0) >= val:
                                continue
                            water[key] = val
                            eng.wait_ge(sem, val)
                        bi = ins.fn(eng)
                        if ins.is_dma:
                            bi.then_inc(ins.buf_sem.dma_sem, 16)
                        elif ins.signal:
                            bi.then_inc(eng_sem[ename], 1)
                    if ename == final_wait_eng:
                        for b in dma_bufs:
                            eng.wait_ge(b.dma_sem, b.dma_cnt * 16)
                return body

            for ename in ENGS:
                engobj[ename](make(ename))


D = 1024
SEQ = 2048
NT = SEQ // 128
NSB = 16
SL = 4
DEPTH = 2
IN_DIM = 6688
EPS = 1e-6
NEG = -30000.0
OFF_ZS, OFF_XBC, OFF_DT, OFF_ZG, OFF_QKV, OFF_A, OFF_B = 0, 1024, 2560, 2576, 3600, 6672, 6680


class Cfg:
    def __init__(self, name, T, nseq, L, Q, chain):
        self.name, self.T, self.nseq, self.L, self.Q, self.chain = name, T, nseq, L, Q, chain
        self.nch = T // Q
        self.nlev = {64: 5, 4: 1}[Q]


CP = Cfg("p", 128, 1, 128, 64, True)
CS = Cfg("s", 64, NSB, SL, 4, False)


def host_consts():
    c = {"ident": np.eye(128, dtype=np.float32), "ones": np.ones((128, 128), np.float32)}
    for cfg in (CP, CS):
        T, Q, nch = cfg.T, cfg.Q, cfg.nch
        ch = np.arange(T) // Q
        same = ch[:, None] == ch[None, :]
        idx = np.arange(T)
        le = idx[:, None] <= idx[None, :]
        lt = idx[:, None] < idx[None, :]
        n = cfg.name
        c["tri_" + n] = (same & le).astype(np.float32)
        c["blk_" + n] = same.astype(np.float32)
        c["nmI_" + n] = np.where(same & le, 0.0, NEG).astype(np.float32)
        c["nmS_" + n] = np.where(same & lt, 0.0, NEG).astype(np.float32)
        c["nmSr_" + n] = np.ascontiguousarray(c["nmS_" + n].T)
        cm = (ch[None, :] == np.arange(nch)[:, None]).astype(np.float32)
        c["cmf_" + n] = np.ascontiguousarray(np.broadcast_to(cm[None], (128, nch, T))).astype(np.float32)
        c["cmT_" + n] = np.ascontiguousarray(cm.T)
    e = np.zeros((17, 128), np.float32)
    e[0, :] = 1.0
    c["E_p"] = e
    e = np.zeros((17, 64), np.float32)
    for t in range(64):
        e[1 + t // SL, t] = 1.0
    c["E_s"] = e
    return c


def bl(ap, n):
    return ap.unsqueeze(len(ap.shape)).to_broadcast(list(ap.shape) + [n])


def bm(ap, n):
    return ap.unsqueeze(1).to_broadcast([ap.shape[0], n] + list(ap.shape[1:]))


def build(debug=(), limit=None):
    limit = limit or {}
    nc = bass.Bass("TRN2", target_bir_lowering=False)
    S = Sched(nc)
    es = contextlib.ExitStack()
    consts_np = host_consts()

    def din(name, shape):
        return nc.dram_tensor(name, list(shape), F32, kind="ExternalInput").ap()

    def dout(name, shape):
        return nc.dram_tensor(name, list(shape), F32, kind="ExternalOutput").ap()

    I = {}
    I["xp"] = din("xp", [SEQ, D])
    I["xs"] = din("xs", [NSB * SL, D])
    I["call"] = din("call", [17, D])
    I["st_sc"] = din("st_sc", [DEPTH, NSB, 3, 1536])
    I["st_ssm"] = din("st_ssm", [DEPTH, NSB, 16, 64, 128])
    I["st_gc"] = din("st_gc", [DEPTH, NSB, 3, 3072])
    I["st_gdn"] = din("st_gdn", [DEPTH, NSB, 8, 128, 128])
    wshapes = {"norm_w": [DEPTH, D], "w_ada": [DEPTH, D, 3 * D], "b_ada": [DEPTH, 3 * D], "w_in": [DEPTH, D, IN_DIM],
               "ssd_conv_w": [DEPTH, 4, 1536], "ssd_conv_b": [DEPTH, 1536], "ssd_dt_bias": [DEPTH, 16],
               "ssd_a_log": [DEPTH, 16], "ssd_d": [DEPTH, 16], "ssd_norm_w": [DEPTH, 1024],
               "gdn_conv_w": [DEPTH, 4, 3072], "gdn_dt_bias": [DEPTH, 8], "gdn_a_log": [DEPTH, 8],
               "gdn_norm_w": [DEPTH, 128], "w_out": [DEPTH, 2048, D], "final_norm_w": [1, D]}
    for k, shp in wshapes.items():
        I[k] = din(k, shp)
    for k, v in consts_np.items():
        I["c_" + k] = din("c_" + k, v.shape)
    O = {}
    O["y_p"] = dout("y_p", [SEQ, D])
    O["y_s"] = dout("y_s", [NSB * SL, D])
    O["ncs_p"] = dout("ncs_p", [DEPTH, 3, 1536])
    O["nssm_p"] = dout("nssm_p", [DEPTH, 1024, 128])
    O["ngc_p"] = dout("ngc_p", [DEPTH, 3, 3072])
    O["ngdn_p"] = dout("ngdn_p", [DEPTH, 8, 128, 128])
    O["ncs_s"] = dout("ncs_s", [DEPTH, NSB * 3, 1536])
    O["nssm_s"] = dout("nssm_s", [DEPTH, NSB, 1024, 128])
    O["ngc_s"] = dout("ngc_s", [DEPTH, NSB * 3, 3072])
    O["ngdn_s"] = dout("ngdn_s", [DEPTH, NSB, 8, 128, 128])
    NROW = SEQ + NSB * SL
    scr = [Buf(nc.dram_tensor("scr%d" % i, [NROW, D], F32, kind="Internal").ap(), "scr%d" % i) for i in range(3)]
    xin_ext = None

    allb = []

    def sb(name, shape, dt=F32):
        b_ = Buf(es.enter_context(nc.sbuf_tensor(name, list(shape), dt)), name)
        allb.append(b_)
        return b_

    banks = [Buf(es.enter_context(nc.psum_tensor("bank%d" % i, [128, 512], F32)), "bank%d" % i) for i in range(8)]
    for b_ in banks:
        b_.excl = True
    pstate = {"i": 0, "pinned": set()}

    def pget():
        while True:
            i = pstate["i"] % 8
            pstate["i"] += 1
            if i not in pstate["pinned"]:
                return banks[i]

    def ppin(b):
        pstate["pinned"].add(banks.index(b))

    def punpin(b):
        pstate["pinned"].discard(banks.index(b))

    def pf(b, shape, p=128):
        n = int(np.prod(shape))
        ap = b[:p, 0:n]
        if len(shape) == 1:
            return ap
        names = " ".join("a%d" % i for i in range(len(shape)))
        kw = {"a%d" % i: shape[i] for i in range(1, len(shape))}
        return ap.rearrange("p (%s) -> p %s" % (names, names), **kw)

    def pb(b, shape, p=128):
        n = int(np.prod(shape))
        ap = b[:p, :].bitcast(BF16)[:, 0:n]
        if len(shape) == 1:
            return ap
        names = " ".join("a%d" % i for i in range(len(shape)))
        kw = {"a%d" % i: shape[i] for i in range(1, len(shape))}
        return ap.rearrange("p (%s) -> p %s" % (names, names), **kw)

    def MM(out, lhsT, rhs, R, W, start=True, stop=True, **kw):
        S.add("pe", lambda e: e.matmul(out, lhsT=lhsT, rhs=rhs, start=start, stop=stop, **kw), reads=R, writes=W)

    def TR(out, in_, ident, R, W):
        S.add("pe", lambda e: e.transpose(out=out, in_=in_, identity=ident), reads=R, writes=W)

    def ACT(out, in_, func, R, W, **kw):
        S.add("act", lambda e: e.activation(out=out, in_=in_, func=func, **kw), reads=R, writes=W)

    def TT(out, in0, in1, op, R, W, eng="dve"):
        S.add(eng, lambda e: e.tensor_tensor(out=out, in0=in0, in1=in1, op=op), reads=R, writes=W)

    def TS(out, in0, s1, s2, op0, op1, R, W, eng="dve"):
        S.add(eng, lambda e: e.tensor_scalar(out=out, in0=in0, scalar1=s1, scalar2=s2, op0=op0, op1=op1), reads=R, writes=W)

    def STT(out, in0, scalar, in1, op0, op1, R, W):
        S.add("dve", lambda e: e.scalar_tensor_tensor(out=out, in0=in0, scalar=scalar, in1=in1, op0=op0, op1=op1), reads=R, writes=W)

    def CP_(out, in_, R, W, eng="dve"):
        S.add(eng, lambda e: e.tensor_copy(out=out, in_=in_), reads=R, writes=W)

    def RECIP(out, in_, R, W):
        S.add("dve", lambda e: e.reciprocal(out=out, in_=in_), reads=R, writes=W)

    def RED(out, in_, R, W):
        S.add("dve", lambda e: e.tensor_reduce(out=out, in_=in_, axis=AX.X, op=ALU.add), reads=R, writes=W)

    def MSET(ap, val, W, eng="pool"):
        S.add(eng, lambda e: e.memset(ap, val), writes=W)

    def LD(out, in_, buf, R=None, nonc=False):
        if nonc:
            S.add("sp", lambda e: e.dma_start(out=out, in_=in_, allow_slow_non_contiguous=True), reads=R, writes=[buf], dma_buf=buf)
        else:
            S.add("sp", lambda e: e.dma_start(out=out, in_=in_), reads=R, writes=[buf], dma_buf=buf)

    sto_n = [0]
    sto_dummy = Buf(None, "sto_dummy")

    def STO(out, in_, buf, W=None, extra_reads=None):
        S.add("sp", lambda e: e.dma_start(out=out, in_=in_), reads=[buf] + (extra_reads or []), writes=W, dma_buf=buf)

    dbg_out = {}
    mod_stacks = []
    cur_es = [es]
    all_bufs = []

    def DBG(name, buf, ap, shape):
        if name in debug and name not in dbg_out:
            o = dout("dbg_" + name, shape)
            dbg_out[name] = o
            if ap.dtype != F32:
                tmp = Buf(cur_es[0].enter_context(nc.sbuf_tensor("dbgt_" + name, list(shape), F32)), "dbgt_" + name)
                all_bufs.append(tmp)
                CP_(tmp[:], ap, [buf], [tmp])
                STO(o, tmp[:], tmp)
            else:
                STO(o, ap, buf)

    C = {}
    for k, v in consts_np.items():
        if k.startswith("cmf_"):
            continue
        C[k] = sb("k_" + k, v.shape)
        LD(C[k][:], I["c_" + k], C[k])
    identf = C["ident"]
    onesf = C["ones"]
    identb = sb("identb", [128, 128], BF16)
    CP_(identb[:], identf[:], [identf], [identb])
    onesb = sb("onesb", [128, 128], BF16)
    CP_(onesb[:], onesf[:], [onesf], [onesb])
    cfg_pending = []
    for cfg in (CP, CS):
        n = cfg.name
        cfg.tri, cfg.blk, cfg.nmI, cfg.nmS, cfg.nmSr, cfg.cmT = (C[k + n] for k in ("tri_", "blk_", "nmI_", "nmS_", "nmSr_", "cmT_"))
        cfg.cmf = sb("cmfb_" + n, [128, cfg.nch, cfg.T], BF16)
        cfg.ncmf = sb("ncmfb_" + n, [128, cfg.nch, cfg.T], BF16)
        cfg_pending.append(cfg)
        cfg.cmTb = sb("cmTb_" + n, [cfg.T, cfg.nch], BF16)
        CP_(cfg.cmTb[:], cfg.cmT[:], [cfg.cmT], [cfg.cmTb])

    SW = 128
    stage = [sb("stage%d" % i, [128, 8, SW]) for i in range(2)]
    stage_i = [0]
    WO = sb("wo", [128, 8, D], BF16)
    DIAG = sb("diag", [128, 12, 4, 128], BF16)
    cwT = sb("cwT", [128, 12, 4])
    cbrow = sb("cbrow", [1, 1536], BF16)
    nwT = sb("nwT", [128, 8])
    scT = sb("scT", [128, 8, 17])
    shT = sb("shT", [128, 8, 17])
    gate_bc = {"p": sb("gate_p", [128, D]), "s": sb("gate_s", [64, D])}
    vec16 = sb("vec16", [128, 4, 16])
    vec8 = sb("vec8", [128, 2, 8])
    scTb = sb("scTb", [128, 8, 17], BF16)
    badaT = sb("badaT", [128, 24])
    nw_in = sb("nw_in", [128, 8])
    xt_slots = [sb("xt%d" % i, [128, D]) for i in range(2)]
    xa_slots = [sb("xa%d" % i, [128, D]) for i in range(2)]
    junk = sb("junk", [128, D], BF16)
    xn = sb("xn", [128, D], BF16)
    hT = sb("hT", [128, 8, 128], BF16)
    sm1 = sb("sm1", [128, 8])
    raw = sb("raw", [128, 12, 132], BF16)
    tailf = sb("tailf", [128, 12, 48])
    tailo = sb("tailo", [48, 1536])
    cst_in = tailo
    fT = sb("fT", [128, 12, 128], BF16)
    big = [sb("big%d" % i, [128, D]) for i in range(4)]
    bigb = [sb("bigb%d" % i, [128, D], BF16) for i in range(3)]
    xo = big[3]
    call_sb, sil_c, bgate, gate17, cbrow_f = big[0], bigb[0], big[1], big[2], tailo
    hTf_t = big[0]
    for cfg in cfg_pending:
        n_ = cfg.nch * cfg.T
        tmpv = big[0][:, 0:n_].rearrange("p (c t) -> p c t", t=cfg.T)
        LD(tmpv, I["c_cmf_" + cfg.name], big[0])
        CP_(cfg.cmf[:], tmpv, [big[0]], [cfg.cmf])
        TS(cfg.ncmf[:], tmpv, -1.0, None, ALU.mult, ALU.bypass, [big[0]], [cfg.ncmf])

    def load_cols(dst, dcol, src, ncols):
        c0 = 0
        while c0 < ncols:
            n = min(SW, ncols - c0)
            st = stage[stage_i[0] % 2]
            stage_i[0] += 1
            LD(st[:, :, 0:n], src[:, c0:c0 + n].rearrange("(k p) n -> p k n", p=128), st)
            CP_(dst[:, :, dcol + c0:dcol + c0 + n], st[:, :, 0:n], [st], [dst], eng="pool")
            c0 += n

    def load_wo(src_rows, nk, scale_ap_fn):
        for half in range(D // SW):
            st = stage[stage_i[0] % 2]
            stage_i[0] += 1
            LD(st[:, 0:nk, :], src_rows[:, half * SW:(half + 1) * SW].rearrange("(k p) n -> p k n", p=128), st)
            for k in range(nk):
                TS(WO[:, k, half * SW:(half + 1) * SW], st[:, k, :], scale_ap_fn(k), 1.0, ALU.mult, ALU.mult,
                   [st, nwT], [WO], eng="pool")

    def build_diag(convw, c0, nchunks):
        for k_ in range(4):
            LD(cwT[:, 0:nchunks, k_], convw[k_, c0 * 128:(c0 + nchunks) * 128].rearrange("(c p) -> p c", p=128), cwT, nonc=True)
        for c in range(nchunks):
            TT(DIAG[:, c, :, :], bm(identf[:], 4), bl(cwT[:, c, :], 128), ALU.mult, [identf, cwT], [DIAG], eng="pool")

    def do_mod(l):
        LD(call_sb[:17, :], I["call"], call_sb)
        ACT(sil_c[:17, :], call_sb[:17, :], AF.Silu, [call_sb], [sil_c])
        b = pget()
        for c in range(8):
            TR(pb(b, [8, 32])[:, c, 0:17], sil_c[:17, c * 128:(c + 1) * 128], identb[:17, :17], [sil_c, identb], [b])
        CP_(scTb[:], pb(b, [8, 32])[:, :, 0:17], [b], [scTb])
        LD(badaT[:], I["b_ada"][l].rearrange("(c p) -> p c", p=128), badaT, nonc=True)
        LD(nw_in[:], I["norm_w"][l].rearrange("(c p) -> p c", p=128), nw_in, nonc=True)
        LD(bgate[0:1, :], I["b_ada"][l:l + 1, 2048:3072], bgate)
        nblk = 3 * D // SW
        per = D // SW
        modes = contextlib.ExitStack()
        wb = Buf(modes.enter_context(nc.sbuf_tensor("modwb%d" % l, [128, 8, SW], BF16)), "modwb")
        all_bufs.append(wb)
        for blk in range(nblk):
            st = stage[stage_i[0] % 2]
            stage_i[0] += 1
            LD(st[:], I["w_ada"][l][:, blk * SW:(blk + 1) * SW].rearrange("(k p) n -> p k n", p=128), st)
            CP_(wb[:, :, 0:SW], st[:], [st], [wb], eng="pool")
            if blk < 2 * per:
                b = pget()
                nj = SW // 128
                for j in range(nj):
                    for k in range(8):
                        MM(pf(b, [nj, 17])[:, j, :], wb[:, k, j * 128:(j + 1) * 128], scTb[:, k, :], [wb, scTb], [b],
                           start=(k == 0), stop=(k == 7))
                dst = shT if blk < per else scT
                cc = (blk % per) * nj
                TT(dst[:, cc:cc + nj, :], pf(b, [nj, 17]), bl(badaT[:, blk * nj:blk * nj + nj], 17), ALU.add, [b, badaT], [dst])
            else:
                g = pget()
                gc0 = (blk - 2 * per) * SW
                for k in range(8):
                    MM(g[:17, 0:SW], scTb[:, k, :], wb[:, k, 0:SW], [wb, scTb], [g], start=(k == 0), stop=False)
                MM(g[:17, 0:SW], onesf[0:1, 0:17], bgate[0:1, gc0:gc0 + SW], [onesf, bgate], [g], start=False, stop=True)
                CP_(gate17[:17, gc0:gc0 + SW], g[:17, 0:SW], [g], [gate17])
        TS(scT[:], scT[:], 1.0, None, ALU.add, ALU.bypass, [scT], [scT])
        TT(scT[:], scT[:], bl(nw_in[:], 17), ALU.mult, [scT, nw_in], [scT])
        for n, E, T in (("p", C["E_p"], 128), ("s", C["E_s"], 64)):
            for half in range(2):
                b = pget()
                MM(b[:T, :], E[:, :T], gate17[:17, half * 512:(half + 1) * 512], [E, gate17], [b])
                CP_(gate_bc[n][:T, half * 512:(half + 1) * 512], b[:T, :], [b], [gate_bc[n]])
        mod_stacks.append(modes)

    def rows(ti):
        return (ti * 128, 128) if ti < NT else (SEQ, 64)

    def src_ap(layer_in, ti):
        r0, T = rows(ti)
        if layer_in is None:
            return (I["xp"][r0:r0 + T, :] if ti < NT else I["xs"][:, :]), None
        return layer_in[r0:r0 + T, :], (layer_in, ti)

    def prologue(cfg, xt):
        T = cfg.T
        ACT(junk[:T, :], xt[:T, :], AF.Square, [xt], [junk, sm1], accum_out=sm1[:T, 0:1])
        TS(sm1[:T, 0:1], sm1[:T, 0:1], 1.0 / D, EPS, ALU.mult, ALU.add, [sm1], [sm1])
        ACT(sm1[:T, 0:1], sm1[:T, 0:1], AF.Ln, [sm1], [sm1])
        ACT(sm1[:T, 0:1], sm1[:T, 0:1], AF.Exp, [sm1], [sm1], scale=-0.5)
        ACT(xn[:T, :], xt[:T, :], AF.Copy, [xt, sm1], [xn], scale=sm1[:T, 0:1])
        b = pget()
        for c in range(8):
            TR(pb(b, [8, T])[:, c, :], xn[:T, c * 128:(c + 1) * 128], identb[:T, :T], [xn, identb], [b])
        if cfg.nseq == 1:
            sc, sh = bl(scT[:, :, 0], T), bl(shT[:, :, 0], T)
            o1, o2, i1 = hTf_t[:, 0:8 * T].rearrange("p (c t) -> p c t", t=T), hT[:, :, :T], pb(b, [8, T])
        else:
            sc, sh = bl(scT[:, :, 1:17], SL), bl(shT[:, :, 1:17], SL)
            o1 = hTf_t[:, 0:8 * T].rearrange("p (c b l) -> p c b l", c=8, l=SL)
            o2 = hT[:, :, :T].rearrange("p c (b l) -> p c b l", l=SL)
            i1 = pb(b, [8, NSB, SL])
        TT(o1, i1, sc, ALU.mult, [b, scT], [hTf_t])
        TT(o2, o1, sh, ALU.add, [hTf_t, shT], [hT])

    def inproj_conv(cfg, l, wcol0, nchunks, bias_row, st_conv_in, conv_out, conv_out_s, ch0, is_last, first):
        T, L, nseq = cfg.T, cfg.L, cfg.nseq
        rawv = raw[:, :, 0:nseq * (L + 3)].rearrange("p c (b l) -> p c b l", l=L + 3)
        if cfg.nseq == 1:
            if first:
                MSET(raw[:, :, 0:3], 0.0, [raw])
            else:
                CP_(raw[:, 0:nchunks, 0:3], raw[:, 0:nchunks, L:L + 3], [raw], [raw])
        else:
            LD(cst_in[:, 0:nchunks * 128], st_conv_in[:, :, ch0 * 128:(ch0 + nchunks) * 128].rearrange("b k c -> (b k) c"), cst_in)
            for g0 in range(0, nchunks, 8):
                b = pget()
                ng = min(8, nchunks - g0)
                for c in range(ng):
                    TR(pf(b, [8, 48])[:, c, :], cst_in[:, (g0 + c) * 128:(g0 + c + 1) * 128], identf[:48, :48], [cst_in, identf], [b])
                CP_(rawv[:, g0:g0 + ng, :, 0:3], pf(b, [8, 48])[:, 0:ng, :].rearrange("p c (b k) -> p c b k", k=3), [b], [raw])
        if limit.get("stopC", 99) <= 1:
            return
        for g0 in range(0, nchunks, 4):
            b = pget()
            for c in range(4):
                for k in range(8):
                    MM(pf(b, [4, T])[:, c, :], PA["WB"][:, k, wcol0 + (g0 + c) * 128: wcol0 + (g0 + c + 1) * 128], hT[:, k, :T],
                       [PA["WB"], hT], [b], start=(k == 0), stop=(k == 7))
            if nseq == 1:
                ACT(raw[:, g0:g0 + 4, 3:3 + L], pf(b, [4, T]), AF.Copy, [b], [raw])
            else:
                for c in range(4):
                    ACT(rawv[:, g0 + c, :, 3:3 + L], pf(b, [4, T])[:, c, :].rearrange("p (b l) -> p b l", l=L), AF.Copy, [b], [raw])
            if is_last and limit.get("tail", True):
                for c in range(4):
                    ACT(tailf[:, g0 + c, 0:nseq * 3].rearrange("p (b k) -> p b k", k=3),
                        pf(b, [4, T])[:, c, :].rearrange("p (b l) -> p b l", l=L)[:, :, L - 3:L], AF.Copy, [b], [tailf])
        if limit.get("stopC", 99) <= 2:
            return
        for g0 in range(0, nchunks, 4):
            b = pget()
            for c in range(4):
                o = pf(b, [4, T])[:, c, :]
                if nseq > 1:
                    o = o.rearrange("p (b l) -> p b l", l=L)
                for k in range(4):
                    MM(o, DIAG[:, g0 + c, k, :], rawv[:, g0 + c, :, k:k + L] if nseq > 1 else raw[:, g0 + c, k:k + L],
                       [DIAG, raw], [b], start=(k == 0), stop=(k == 3 and bias_row is None))
                if bias_row is not None:
                    MM(pf(b, [4, T])[:, c, :], bias_row[0:1, (g0 + c) * 128:(g0 + c + 1) * 128], onesb[0:1, :T], [bias_row, onesb], [b],
                       start=False, stop=True)
            ACT(fT[:, g0:g0 + 4, :T], pf(b, [4, T]), AF.Silu, [b], [fT])
        if limit.get("stopC", 99) <= 3:
            return
        if is_last:
            n3 = nseq * 3
            for g0 in range(0, nchunks, 4):
                b = pget()
                for c in range(4):
                    TR(pf(b, [4, 128], p=n3)[:, c, :], tailf[:, g0 + c, 0:n3], identf[:, :], [tailf, identf], [b])
                CP_(tailo[:n3, g0 * 128:(g0 + 4) * 128], b[:n3, :], [b], [tailo])
            dst = conv_out[l][:, ch0 * 128:(ch0 + nchunks) * 128] if nseq == 1 else conv_out_s[l][:, ch0 * 128:(ch0 + nchunks) * 128]
            STO(dst, tailo[:n3, 0:nchunks * 128], tailo)

    def cum_and_tot(cfg, a_ap, a_buf, nh, out_c, out_dbc, pw):
        T, nch = cfg.T, cfg.nch
        b = pget()
        MM(b[:T, 0:nh], cfg.tri[:, :], a_ap, [cfg.tri, a_buf], [b])
        MM(b[:T, nh:2 * nh], cfg.blk[:, :], a_ap, [cfg.blk, a_buf], [b])
        ACT(out_c[:T, 0:2 * nh], b[:T, 0:2 * nh], AF.Copy, [b], [out_c])
        b2 = pget()
        MM(b2[:nch, 0:nh], cfg.cmT[:, :], a_ap, [cfg.cmT, a_buf], [b2])
        CP_(tot[:nch, 0:nh], b2[:nch, 0:nh], [b2], [tot])
        TT(totx[:nch, 0:nch * nh].rearrange("p (c h) -> p c h", h=nh), bm(tot[:nch, 0:nh], nch), bl(identf[:nch, :nch], nh), ALU.mult,
           [tot, identf], [totx])
        b3 = pget()
        MM(b3[:pw, 0:nch * nh], onesf[:nch, :pw], totx[:nch, 0:nch * nh], [onesf, totx], [b3])
        ACT(out_dbc[:pw, 0:nch * nh], b3[:pw, 0:nch * nh], AF.Exp, [b3], [out_dbc])

    tot = sb("tot", [16, 16])
    totx = sb("totx", [16, 256])
    sm = {k: sb("sm_" + k, [128, 64]) for k in ("a", "b", "c", "d", "e")}
    dbc = sb("dbc", [128, 256])
    acT = sb("acT", [16, 128])

    PA = {}

    def alloc_A(pes, tag):
        def sbp(name, shape, dt=F32):
            return Buf(pes.enter_context(nc.sbuf_tensor(name + tag, list(shape), dt)), name)
        PA.update(stf=sbp("stf", [128, D]), stb=sbp("stb", [128, D], BF16), hn=sbp("hn", [128, 8, 128]),
                  stf_s=sbp("stf_s", [128, D]), stb_s=sbp("stb_s", [128, D], BF16), Gs=sbp("Gs", [128, 2, 128]),
                  earg=sbp("earg", [128, 4, 128]), Ee=sbp("Ee", [128, 4, 128]), WT=sbp("WT", [128, 16, 128], BF16),
                  Btok=sbp("Btok", [128, 256], BF16), Btm=sbp("Btm", [128, 2, 256], BF16), CTm=sbp("CTm", [128, 2, 2, 128], BF16),
                  WB=sbp("WB", [128, 8, 2576], BF16))
    mslot = [0]

    def ssd_state_update(cfg, c, st_in, stb_in, st_out, yo, xdec, first_c, last_c):
        T, nch = cfg.T, cfg.nch
        Btm, CTm, Btok = PA["Btm"], PA["CTm"], PA["Btok"]
        sl = mslot[0] % 2
        mslot[0] += 1
        TS(Btm[:T, sl, :], Btok[:T, :], cfg.cmT[:, c:c + 1], 1.0, ALU.mult, ALU.mult, [Btok, cfg.cmT], [(Btm, sl)], eng="pool")
        TT(CTm[:, sl, :, :T], fT[:, 10:12, :T], bm(cfg.cmf[:, c, :], 2), ALU.mult, [fT, cfg.cmf], [(CTm, sl)], eng="pool")
        for g in range(2):
            MM(yo[g][:T, :], CTm[:, sl, g, :T], stb_in[:, g * 512:(g + 1) * 512], [(CTm, sl), stb_in], [yo[g]], start=first_c, stop=last_c)
        sp_ = [pget(), pget()]
        for g in range(2):
            MM(sp_[g][:, :], Btm[:T, sl, g * 128:(g + 1) * 128], xdec[:T, g * 512:(g + 1) * 512], [(Btm, sl), xdec], [sp_[g]])
        TT(big[3][:, :].rearrange("p (h q) -> p h q", q=64), st_in[:, :].rearrange("p (h q) -> p h q", q=64),
           bl(dbc[:, c * 16:(c + 1) * 16], 64), ALU.mult, [st_in, dbc], [big[3]])
        for g in range(2):
            TT(st_out[:, g * 512:(g + 1) * 512], big[3][:, g * 512:(g + 1) * 512], sp_[g][:, :], ALU.add, [big[3], sp_[g]], [st_out])

    def phaseA_tile(cfg, l, ti, xt, first, is_last):
        T, L, nseq, nch = cfg.T, cfg.L, cfg.nseq, cfg.nch
        stf, stb, hn, stf_s, stb_s, Gs, earg, Ee, WT, Btok = (PA[k] for k in ("stf", "stb", "hn", "stf_s", "stb_s", "Gs", "earg", "Ee", "WT", "Btok"))
        prologue(cfg, xt)
        if limit.get("stopA", 99) <= 0:
            return
        inproj_conv(cfg, l, OFF_XBC, 12, cbrow, I["st_sc"][l], O["ncs_p"], O["ncs_s"], 0, is_last, first)
        if ti == 0:
            DBG("fT_A", fT, fT[:, :, :T], [128, 12, T])
        if limit.get("stopA", 99) <= 1:
            return
        b = pget()
        for k in range(8):
            MM(b[:T, 0:16], hT[:, k, :T], PA["WB"][:, k, OFF_DT:OFF_DT + 16], [hT, PA["WB"]], [b], start=(k == 0), stop=(k == 7))
        A_, B_, C_, D_, E_ = (sm[k] for k in "abcde")
        TT(A_[:T, 0:16], b[:T, 0:16], vec16[:T, 0, :], ALU.add, [b, vec16], [A_])
        ACT(A_[:T, 0:16], A_[:T, 0:16], AF.Exp, [A_], [A_])
        ACT(A_[:T, 16:32], A_[:T, 0:16], AF.Ln, [A_], [A_], bias=1.0)
        TT(A_[:T, 32:48], A_[:T, 16:32], vec16[:T, 1, :], ALU.mult, [A_, vec16], [A_])
        cum_and_tot(cfg, A_[:T, 32:48], A_, 16, B_, dbc, 128)
        TT(C_[:T, 0:16], B_[:T, 16:32], B_[:T, 0:16], ALU.subtract, [B_], [C_])
        ACT(C_[:T, 0:16], C_[:T, 0:16], AF.Exp, [C_], [C_])
        ACT(C_[:T, 16:32], B_[:T, 0:16], AF.Exp, [B_], [C_])
        b = pget()
        TR(b[:16, 0:T], B_[:T, 0:16], identf[:T, :T], [B_, identf], [b])
        ACT(acT[:, :T], b[:16, 0:T], AF.Copy, [b], [acT])
        if ti == 0:
            DBG("acum", B_, B_[:T, 0:32], [T, 32])
        if limit.get("stopA", 99) <= 2:
            return
        bx, bb = pget(), pget()
        for c in range(8):
            TR(pb(bx, [8, 128], p=T)[:, c, :], fT[:, c, :T], identb[:, :], [fT, identb], [bx])
        for c in range(2):
            TR(pb(bb, [2, 128], p=T)[:, c, :], fT[:, 8 + c, :T], identb[:, :], [fT, identb], [bb])
        xc, xdec, xsD = bigb[0], bigb[1], big[0]
        TT(xc[:T, :].rearrange("p (h q) -> p h q", q=64), pb(bx, [16, 64], p=T), bl(A_[:T, 16:32], 64), ALU.mult, [bx, A_], [xc])
        TT(xsD[:T, :].rearrange("p (h q) -> p h q", q=64), pb(bx, [16, 64], p=T), bl(vec16[:T, 2, :], 64), ALU.mult, [bx, vec16], [xsD])
        TT(xdec[:T, :].rearrange("p (h q) -> p h q", q=64), xc[:T, :].rearrange("p (h q) -> p h q", q=64), bl(C_[:T, 0:16], 64), ALU.mult,
           [xc, C_], [xdec], eng="pool")
        ACT(Btok[:T, :], pb(bb, [256], p=T), AF.Copy, [bb], [Btok])
        b = pget()
        for g in range(2):
            MM(pf(b, [2, T], p=T)[:, g, :], fT[:, 8 + g, :T], fT[:, 10 + g, :T], [fT], [b])
        ACT(Gs[:T, :, :T], pf(b, [2, T], p=T), AF.Copy, [b], [Gs])
        for q in range(4):
            b = pget()
            for j in range(4):
                h = q * 4 + j
                MM(pf(b, [4, T], p=T)[:, j, :], identf[0:16, h:h + 1].to_broadcast([16, T]), acT[:, :T], [identf, acT], [b])
            for j in range(4):
                h = q * 4 + j
                STT(earg[:T, j, :T], pf(b, [4, T], p=T)[:, j, :], B_[:T, h:h + 1], cfg.nmI[:, :], ALU.subtract, ALU.add,
                    [b, B_, cfg.nmI], [earg])
            ACT(Ee[:T, :, :T], earg[:T, :, :T], AF.Exp, [earg], [Ee])
            TT(WT[:T, q * 4:q * 4 + 4, :T], Ee[:T, :, :T], bm(Gs[:T, q // 2, :T], 4), ALU.mult, [Ee, Gs], [WT])
        if limit.get("stopA", 99) <= 3:
            return
        yp = [pget(), pget()]
        for h in range(16):
            MM(yp[h // 8][:T, (h % 8) * 64:(h % 8 + 1) * 64], WT[:T, h, :T], xc[:T, h * 64:(h + 1) * 64], [WT, xc], [yp[h // 8]])
        ppin(yp[0]); ppin(yp[1])
        yo = [pget(), pget()]
        ppin(yo[0]); ppin(yo[1])
        if cfg.chain:
            if first:
                MSET(stf[:, :], 0.0, [stf])
                MSET(stb[:, :], 0.0, [stb])
            for c in range(nch):
                ssd_state_update(cfg, c, stf, stb, stf, yo, xdec, c == 0, c == nch - 1)
                ACT(stb[:, :], stf[:, :], AF.Copy, [stf], [stb])
            if is_last:
                for half in range(2):
                    b = pget()
                    for k in range(4):
                        kk = half * 4 + k
                        TR(pf(b, [4, 128])[:, k, :], stf[:, :].rearrange("n (j k) -> n k j", k=8)[:, kk, :], identf[:, :], [stf, identf], [b])
                    CP_(hn[:, half * 4:half * 4 + 4, :], pf(b, [4, 128]), [b], [hn])
                STO(O["nssm_p"][l].rearrange("(p k) n -> p k n", k=8), hn[:], hn)
        else:
            for c in range(limit.get("nchs", nch)):
                LD(hn[:], I["st_ssm"][l, c].rearrange("h q n -> (h q) n").rearrange("(p k) n -> p k n", k=8), hn)
                if limit.get("sub", 9) <= 0:
                    continue
                for half in range(2):
                    b = pget()
                    for k in range(4):
                        kk = half * 4 + k
                        TR(pf(b, [4, 128])[:, k, :], hn[:, kk, :], identf[:, :], [hn, identf], [b])
                    if limit.get("sub", 9) <= 1:
                        continue
                    ACT(stf_s[:, :].rearrange("n (j k) -> n k j", k=8)[:, half * 4:half * 4 + 4, :], pf(b, [4, 128]), AF.Copy, [b], [stf_s])
                    if limit.get("sub", 9) <= 2:
                        continue
                    CP_(stb_s[:, :].rearrange("n (j k) -> n k j", k=8)[:, half * 4:half * 4 + 4, :], pf(b, [4, 128]), [b], [stb_s])
                if limit.get("noupd"):
                    continue
                ssd_state_update(cfg, c, stf_s, stb_s, stf_s, yo, xdec, c == 0, c == limit.get("nchs", nch) - 1)
                if limit.get("noback"):
                    continue
                for half in range(2):
                    b = pget()
                    for k in range(4):
                        kk = half * 4 + k
                        TR(pf(b, [4, 128])[:, k, :], stf_s[:, :].rearrange("n (j k) -> n k j", k=8)[:, kk, :], identf[:, :], [stf_s, identf], [b])
                    CP_(hn[:, half * 4:half * 4 + 4, :], pf(b, [4, 128]), [b], [hn])
                STO(O["nssm_s"][l, c].rearrange("(p k) n -> p k n", k=8), hn[:], hn)
        if limit.get("stopA", 99) <= 4:
            punpin(yp[0]); punpin(yp[1]); punpin(yo[0]); punpin(yo[1])
            return
        t1, t2 = big[1], big[2]
        for g in range(2):
            TT(t1[:T, g * 512:(g + 1) * 512].rearrange("p (h q) -> p h q", q=64), yo[g][:T, :].rearrange("p (h q) -> p h q", q=64),
               bl(C_[:T, 16 + g * 8:16 + g * 8 + 8], 64), ALU.mult, [yo[g], C_], [t1])
            TT(t2[:T, g * 512:(g + 1) * 512], yp[g][:T, :], t1[:T, g * 512:(g + 1) * 512], ALU.add, [yp[g], t1], [t2])
        for x_ in yp + yo:
            punpin(x_)
        TT(t2[:T, :], t2[:T, :], xsD[:T, :], ALU.add, [t2, xsD], [t2], eng="pool")
        if ti == 0:
            DBG("yssd", t2, t2[:T, :], [T, D])
        for g in range(2):
            b = pget()
            for k in range(8):
                MM(b[:T, :], hT[:, k, :T], PA["WB"][:, k, OFF_ZS + g * 512:OFF_ZS + (g + 1) * 512], [hT, PA["WB"]], [b], start=(k == 0), stop=(k == 7))
            ACT(t1[:T, g * 512:(g + 1) * 512], b[:T, :], AF.Silu, [b], [t1])
        TT(t2[:T, :], t2[:T, :], t1[:T, :], ALU.mult, [t2, t1], [t2])
        for g in range(2):
            ACT(junk[:T, g * 512:(g + 1) * 512], t2[:T, g * 512:(g + 1) * 512], AF.Square, [t2], [junk, sm1], accum_out=sm1[:T, 2 + g:3 + g])
        TS(sm1[:T, 2:4], sm1[:T, 2:4], 1.0 / 512, EPS, ALU.mult, ALU.add, [sm1], [sm1])
        ACT(sm1[:T, 2:4], sm1[:T, 2:4], AF.Ln, [sm1], [sm1])
        ACT(sm1[:T, 2:4], sm1[:T, 2:4], AF.Exp, [sm1], [sm1], scale=-0.5)
        ynb = bigb[2]
        for g in range(2):
            ACT(ynb[:T, g * 512:(g + 1) * 512], t2[:T, g * 512:(g + 1) * 512], AF.Copy, [t2, sm1], [ynb], scale=sm1[:T, 2 + g:3 + g])
        out_proj(cfg, ynb, 8)

    def out_proj(cfg, ynb, nk):
        T = cfg.T
        b = pget()
        for c in range(nk):
            TR(pb(b, [8, T])[:, c, :], ynb[:T, c * 128:(c + 1) * 128], identb[:T, :T], [ynb, identb], [b])
        CP_(hT[:, 0:nk, :T], pb(b, [8, T])[:, 0:nk, :], [b], [hT])
        for half in range(2):
            b = pget()
            for k in range(nk):
                MM(b[:T, :], hT[:, k, :T], WO[:, k, half * 512:(half + 1) * 512], [hT, WO], [b], start=(k == 0), stop=(k == nk - 1))
            TT(big[3][:T, half * 512:(half + 1) * 512], b[:T, :], gate_bc[cfg.name][:T, half * 512:(half + 1) * 512], ALU.mult,
               [b, gate_bc[cfg.name]], [big[3]])

    PB = {}

    def alloc_B(pes, tag):
        def sbp(name, shape, dt=F32):
            return Buf(pes.enter_context(nc.sbuf_tensor(name + tag, list(shape), dt)), name)
        PB.update(Sp_f=sbp("Sp_f", [128, 4, 128]), Sp_b=sbp("Sp_b", [128, 4, 128], BF16), Ss_in=sbp("Ss_in", [128, 16, 128]),
                  Ss_b=sbp("Ss_b", [128, 16, 128], BF16), kv=sbp("kv", [128, 8, 128], BF16), kbg=sbp("kbg", [128, 4, 128]),
                  vb_=sbp("vb", [128, 4, 128]), kdec=sbp("kdec", [128, 4, 128], BF16), kdm=sbp("kdm", [64, 2, 128], BF16),
                  LG=sbp("LG", [128, 12]), LGT=sbp("LGT", [12, 128]), ea=sbp("ea", [128, 4, 128]),
                  E2a=sbp("E2a", [128, 4, 128]), E2b=sbp("E2b", [128, 4, 128]),
                  X0=sbp("X0", [128, 4, 128]), X1=sbp("X1", [128, 4, 128]), Y0=sbp("Y0", [128, 4, 128]), Y1=sbp("Y1", [128, 4, 128]),
                  R0=sbp("R0", [128, 4, 128]), R1=sbp("R1", [128, 4, 128]), attnT=sbp("attnT", [128, 4, 128], BF16),
                  egb=sbp("egb", [128, 4, 128]), qsq=sbp("qsq", [128, 4, 128], BF16),
                  nwTm=sbp("nwTm", [128, 1024], BF16), qdm=sbp("qdm", [128, 1024], BF16), u_sb=sbp("u_sb", [128, 4, 128]),
                  vnew=sbp("vnew", [128, 4, 128], BF16), dbcS=sbp("dbcS", [128, 64]), WB=sbp("WB", [128, 8, 2056], BF16))
        PB["E2"] = None

    def r32(ap):
        return ap.bitcast(F32R)

    def phaseB_tile(cfg, l, ti, xt, h0, first, is_last):
        T, L, nseq, nch = cfg.T, cfg.L, cfg.nseq, cfg.nch
        (Sp_f, Sp_b, Ss_in, Ss_b, kv, kbg, vb_, kdec, kdm, LG, LGT, ea, attnT, egb, nwTm, qdm, u_sb, vnew, dbcS) = (
            PB[k] for k in ("Sp_f", "Sp_b", "Ss_in", "Ss_b", "kv", "kbg", "vb_", "kdec", "kdm", "LG", "LGT", "ea", "attnT",
                            "egb", "nwTm", "qdm", "u_sb", "vnew", "dbcS"))
        E2 = [PB["E2a"], PB["E2b"]]
        Ss_out = Ss_in
        Xs, Ys, Rs = [PB["X0"], PB["X1"]], [PB["Y0"], PB["Y1"]], [PB["R0"], PB["R1"]]
        WB = PB["WB"]
        prologue(cfg, xt)
        inproj_conv_B(cfg, l, h0, is_last, first)
        A_, B_, C_, D_, E_ = (sm[k] for k in "abcde")
        b = pget()
        for k in range(8):
            MM(b[:T, 0:8], hT[:, k, :T], WB[:, k, 2048:2056], [hT, WB], [b], start=(k == 0), stop=(k == 7))
        TT(A_[:T, 0:4], b[:T, 0:4], vec8[:T, 0, h0:h0 + 4], ALU.add, [b, vec8], [A_])
        ACT(A_[:T, 0:4], A_[:T, 0:4], AF.Exp, [A_], [A_])
        ACT(A_[:T, 0:4], A_[:T, 0:4], AF.Ln, [A_], [A_], bias=1.0)
        TT(A_[:T, 4:8], A_[:T, 0:4], vec8[:T, 1, h0:h0 + 4], ALU.mult, [A_, vec8], [A_])
        ACT(A_[:T, 8:12], b[:T, 4:8], AF.Exp, [b], [A_], scale=-1.0)
        ACT(A_[:T, 8:12], A_[:T, 8:12], AF.Ln, [A_], [A_], bias=1.0)
        ACT(A_[:T, 12:16], A_[:T, 8:12], AF.Exp, [A_], [A_], scale=-1.0)
        ACT(A_[:T, 16:20], A_[:T, 8:12], AF.Copy, [A_], [A_], scale=-1.0)
        cum_and_tot(cfg, A_[:T, 4:8], A_, 4, B_, dbcS, 128)
        ACT(C_[:T, 0:4], B_[:T, 0:4], AF.Exp, [B_], [C_])
        TT(C_[:T, 4:8], B_[:T, 4:8], B_[:T, 0:4], ALU.subtract, [B_], [C_])
        ACT(C_[:T, 4:8], C_[:T, 4:8], AF.Exp, [C_], [C_])
        b = pget()
        for c in range(8):
            TR(pb(b, [8, 128], p=T)[:, c, :], fT[:, 4 + c, :T], identb[:, :], [fT, identb], [b])
        ACT(kv[:T, :, :], pb(b, [8, 128], p=T), AF.Copy, [b], [kv])
        ksq = big[0]
        TT(ksq[:T, 0:512].rearrange("p (h d) -> p h d", d=128), kv[:T, 0:4, :], kv[:T, 0:4, :], ALU.mult, [kv], [ksq])
        RED(D_[:T, 0:4], ksq[:T, 0:512].rearrange("p (h d) -> p h d", d=128), [ksq], [D_])
        TS(D_[:T, 0:4], D_[:T, 0:4], EPS, None, ALU.add, ALU.bypass, [D_], [D_])
        ACT(D_[:T, 4:8], D_[:T, 0:4], AF.Ln, [D_], [D_])
        ACT(D_[:T, 8:12], D_[:T, 4:8], AF.Exp, [D_], [D_], scale=-0.5)
        qsq = PB["qsq"]
        ACT(qsq[:, :, :T], fT[:, 0:4, :T], AF.Square, [fT], [qsq])
        b = pget()
        for j in range(4):
            MM(b[:T, j:j + 1], qsq[:, j, :T], onesb[:, 0:1], [qsq, onesb], [b])
        TS(D_[:T, 12:16], b[:T, 0:4], EPS, None, ALU.add, ALU.bypass, [b], [D_])
        ACT(D_[:T, 12:16], D_[:T, 12:16], AF.Ln, [D_], [D_])
        ACT(D_[:T, 16:20], D_[:T, 12:16], AF.Exp, [D_], [D_], scale=-0.5)
        TS(D_[:T, 16:20], D_[:T, 16:20], float(128 ** -0.5), None, ALU.mult, ALU.bypass, [D_], [D_])
        TT(E_[:T, 0:4], D_[:T, 8:12], A_[:T, 12:16], ALU.mult, [D_, A_], [E_])
        TT(E_[:T, 4:8], E_[:T, 0:4], C_[:T, 0:4], ALU.mult, [E_, C_], [E_])
        TT(E_[:T, 8:12], D_[:T, 8:12], C_[:T, 4:8], ALU.mult, [D_, C_], [E_])
        TS(E_[:T, 12:16], E_[:T, 0:4], -1.0, None, ALU.mult, ALU.bypass, [E_], [E_])
        TS(E_[:T, 16:20], D_[:T, 8:12], -1.0, None, ALU.mult, ALU.bypass, [D_], [E_])
        CP_(LG[:T, 0:4], B_[:T, 0:4], [B_], [LG])
        STT(LG[:T, 4:8], D_[:T, 4:8], -0.5, B_[:T, 0:4], ALU.mult, ALU.subtract, [D_, B_], [LG])
        STT(LG[:T, 8:12], D_[:T, 4:8], -0.5, B_[:T, 0:4], ALU.mult, ALU.add, [D_, B_], [LG])
        TT(LG[:T, 8:12], LG[:T, 8:12], A_[:T, 16:20], ALU.add, [LG, A_], [LG])
        b = pget()
        TR(b[:12, 0:T], LG[:T, 0:12], identf[:T, :T], [LG, identf], [b])
        ACT(LGT[:, :T], b[:12, 0:T], AF.Copy, [b], [LGT])
        TT(r32(kbg[:T, :, :]), kv[:T, 0:4, :], bl(E_[:T, 4:8], 128), ALU.mult, [kv, E_], [kbg])
        TT(kdec[:T, :, :], kv[:T, 0:4, :], bl(E_[:T, 8:12], 128), ALU.mult, [kv, E_], [kdec])
        TT(r32(vb_[:T, :, :]), kv[:T, 4:8, :], bl(A_[:T, 12:16], 128), ALU.mult, [kv, A_], [vb_])
        bG, bA = pget(), pget()
        for j in range(4):
            MM(pf(bG, [4, T], p=T)[:, j, :], fT[:, 4 + j, :T], fT[:, 4 + j, :T], [fT], [bG])
        for j in range(4):
            MM(pf(bA, [4, T], p=T)[:, j, :], fT[:, 4 + j, :T], fT[:, j, :T], [fT], [bA])
        bcs = [pget(), pget(), pget()]
        for r in range(3):
            for j in range(4):
                MM(pf(bcs[r], [4, T], p=T)[:, j, :], identf[0:12, r * 4 + j:r * 4 + j + 1].to_broadcast([12, T]), LGT[:, :T],
                   [identf, LGT], [bcs[r]])
        be = pget()
        for j in range(4):
            MM(pf(be, [4, T])[:, j, :], identf[0:12, j:j + 1].to_broadcast([12, 128]), LGT[:, :T], [identf, LGT], [be])
        X, Y, R = Xs[0], Ys[0], Rs[0]
        kinds = ((1, ALU.add, cfg.nmSr, 0), (2, ALU.subtract, cfg.nmS, 1), (0, ALU.subtract, cfg.nmI, 2))
        for r, op0, msk, ki in kinds:
            for j in range(4):
                STT(ea[:T, j, :T], pf(bcs[r], [4, T], p=T)[:, j, :], B_[:T, j:j + 1], msk[:, :], op0, ALU.add, [bcs[r], B_, msk], [ea])
            Ek = E2[ki % 2]
            ACT(Ek[:T, :, :T], ea[:T, :, :T], AF.Exp, [ea], [Ek])
            for j in range(4):
                if ki == 0:
                    STT(r32(X[:T, j, :T]), pf(bG, [4, T], p=T)[:, j, :], E_[:T, 12 + j:13 + j], Ek[:T, j, :T], ALU.mult, ALU.mult, [bG, E_, Ek], [X])
                elif ki == 1:
                    STT(r32(Y[:T, j, :T]), pf(bG, [4, T], p=T)[:, j, :], E_[:T, 16 + j:17 + j], Ek[:T, j, :T], ALU.mult, ALU.mult, [bG, E_, Ek], [Y])
                else:
                    STT(attnT[:T, j, :T], pf(bA, [4, T], p=T)[:, j, :], D_[:T, 8 + j:9 + j], Ek[:T, j, :T], ALU.mult, ALU.mult, [bA, D_, Ek], [attnT])
        ACT(egb[:, :, :T], pf(be, [4, T]), AF.Exp, [be], [egb])
        TT(r32(R[:T, :, :T]), Y[:T, :, :T], bm(identf[:T, :T], 4), ALU.add, [Y, identf], [R])
        for lev in range(cfg.nlev):
            X2, Y2, R2 = Xs[(lev + 1) % 2], Ys[(lev + 1) % 2], Rs[(lev + 1) % 2]
            lastlev = (lev == cfg.nlev - 1)
            bX, bY, bR = pget(), (None if lastlev else pget()), pget()
            for j in range(4):
                MM(pf(bX, [4, T], p=T)[:, j, :], r32(Y[:T, j, :T]), r32(X[:T, j, :T]), [X, Y], [bX])
            if not lastlev:
                for j in range(4):
                    MM(pf(bY, [4, T], p=T)[:, j, :], r32(X[:T, j, :T]), r32(Y[:T, j, :T]), [X, Y], [bY])
            ACT(r32(X2[:T, :, :T]), pf(bX, [4, T], p=T), AF.Copy, [bX], [X2])
            if not lastlev:
                CP_(r32(Y2[:T, :, :T]), pf(bY, [4, T], p=T), [bY], [Y2])
            for j in range(4):
                MM(pf(bR, [4, T], p=T)[:, j, :], r32(X2[:T, j, :T]), r32(R[:T, j, :T]), [X2, R], [bR])
            TT(r32(R2[:T, :, :T]), R[:T, :, :T], pf(bR, [4, T], p=T), ALU.add, [R, bR], [R2])
            X, Y, R = X2, Y2, R2
        bU, bW = pget(), pget()
        for j in range(4):
            MM(pf(bU, [4, 128], p=T)[:, j, :], r32(R[:T, j, :T]), r32(vb_[:T, j, :]), [R, vb_], [bU])
        for j in range(4):
            MM(pf(bW, [4, T])[:, j, :], r32(kbg[:T, j, :]), r32(R[:T, j, :T]), [kbg, R], [bW])
        ACT(u_sb[:T, :, :], pf(bU, [4, 128], p=T), AF.Copy, [bU], [u_sb])
        TT(egb[:, :, :T], fT[:, 0:4, :T], egb[:, :, :T], ALU.mult, [fT, egb], [egb])
        vnb, ob = pget(), pget()
        ppin(vnb)
        ppin(ob)
        vn4, o4 = pf(vnb, [4, 128], p=T), pf(ob, [4, 128], p=T)
        if cfg.chain:
            if first:
                MSET(Sp_f[:], 0.0, [Sp_f])
                MSET(Sp_b[:], 0.0, [Sp_b])
            nwv = nwTm[:, 0:nch * 4 * T].rearrange("p (c j t) -> p c j t", c=nch, j=4)
            qdv = qdm[:, 0:nch * 4 * T].rearrange("p (c j t) -> p c j t", c=nch, j=4)
            for c in range(nch):
                TT(nwv[:, c], pf(bW, [4, T]), bm(cfg.ncmf[:, c, :], 4), ALU.mult, [bW, cfg.ncmf], [(nwTm, c)])
                TT(qdv[:, c], egb[:, :, :T], bm(cfg.cmf[:, c, :], 4), ALU.mult, [egb, cfg.cmf], [(qdm, c)], eng="pool")
            for c in range(nch):
                for j in range(4):
                    MM(vn4[:, j, :], nwv[:, c, j, :], Sp_b[:, j, :], [(nwTm, c), Sp_b], [vnb], start=(j == 0 and c == 0), stop=False, skip_group_check=True)
                for j in range(4):
                    MM(o4[:, j, :], qdv[:, c, j, :], Sp_b[:, j, :], [(qdm, c), Sp_b], [ob], start=(j == 0 and c == 0), stop=False, skip_group_check=True)
                TT(vnew[:T, :, :], vn4, u_sb[:T, :, :], ALU.add, [vnb, u_sb], [vnew])
                bs = pget()
                for j in range(4):
                    MM(pf(bs, [4, 128])[:, j, :], kdec[c * 64:(c + 1) * 64, j, :], vnew[c * 64:(c + 1) * 64, j, :], [kdec, vnew], [bs])
                tmpS = big[0][:, 512:1024].rearrange("p (j v) -> p j v", v=128)
                TT(tmpS, Sp_f[:, :, :], bl(dbcS[:, c * 4:(c + 1) * 4], 128), ALU.mult, [Sp_f, dbcS], [big[0]])
                TT(Sp_f[:, :, :], tmpS, pf(bs, [4, 128]), ALU.add, [big[0], bs], [Sp_f])
                ACT(Sp_b[:, :, :], Sp_f[:, :, :], AF.Copy, [Sp_f], [Sp_b])
            if is_last:
                STO(O["ngdn_p"][l, h0:h0 + 4].rearrange("h k v -> k h v"), Sp_f[:], Sp_f)
        else:
            nwv = nwTm[:, 0:nch * T].rearrange("p (c t) -> p c t", t=T)
            qdv = qdm[:, 0:nch * T].rearrange("p (c t) -> p c t", t=T)
            ppin(bW)
            for j in range(4):
                h = h0 + j
                LD(Ss_in[:], I["st_gdn"][l, :, h].rearrange("b k v -> k b v"), Ss_in)
                CP_(Ss_b[:], Ss_in[:], [Ss_in], [Ss_b], eng="pool")
                TT(nwv, bm(pf(bW, [4, T])[:, j, :], nch), cfg.ncmf[:, :, :], ALU.mult, [bW, cfg.ncmf], [nwTm])
                TT(qdv, bm(egb[:, j, :T], nch), cfg.cmf[:, :, :], ALU.mult, [egb, cfg.cmf], [qdm], eng="pool")
                for c in range(nch):
                    MM(vn4[:, j, :], nwv[:, c, :], Ss_b[:, c, :], [nwTm, Ss_b], [vnb], start=(j == 0 and c == 0), stop=False, skip_group_check=True)
                for c in range(nch):
                    MM(o4[:, j, :], qdv[:, c, :], Ss_b[:, c, :], [qdm, Ss_b], [ob], start=(j == 0 and c == 0), stop=False, skip_group_check=True)
                TT(vnew[:T, j, :], vn4[:, j, :], u_sb[:T, j, :], ALU.add, [vnb, u_sb], [vnew])
                tmpS = big[2][:, :].rearrange("p (c v) -> p c v", v=128)
                for hf in range(2):
                    TT(tmpS, Ss_in[:, hf * 8:(hf + 1) * 8, :],
                       bl(dbcS[:, 0:nch * 4].rearrange("p (c j) -> p c j", j=4)[:, hf * 8:(hf + 1) * 8, j], 128), ALU.mult, [Ss_in, dbcS], [big[2]])
                    for c4 in range(hf * 8, hf * 8 + 8, 4):
                        bs = pget()
                        for cc in range(4):
                            c = c4 + cc
                            sl = mslot[0] % 2
                            mslot[0] += 1
                            TS(kdm[:T, sl, :], kdec[:T, j, :], cfg.cmT[:, c:c + 1], 1.0, ALU.mult, ALU.mult, [kdec, cfg.cmT], [(kdm, sl)], eng="pool")
                            MM(pf(bs, [4, 128])[:, cc, :], kdm[:T, sl, :], vnew[:T, j, :], [(kdm, sl), vnew], [bs])
                        TT(Ss_out[:, c4:c4 + 4, :], tmpS[:, c4 - hf * 8:c4 - hf * 8 + 4, :], pf(bs, [4, 128]), ALU.add, [big[2], bs], [Ss_out])
                STO(O["ngdn_s"][l, :, h].rearrange("b k v -> k b v"), Ss_out[:], Ss_out)
            punpin(bW)
        for j in range(4):
            MM(o4[:, j, :], attnT[:T, j, :T], vnew[:T, j, :], [attnT, vnew], [ob], start=False, stop=True, skip_group_check=True)
        punpin(vnb)
        of = big[1]
        ACT(of[:T, 0:512], ob[:T, :], AF.Copy, [ob], [of])
        punpin(ob)
        if ti == 0 and h0 == 0:
            DBG("ogdn", of, of[:T, 0:512], [T, 512])
        TT(ksq[:T, 0:512], of[:T, 0:512], of[:T, 0:512], ALU.mult, [of], [ksq])
        RED(C_[:T, 8:12], ksq[:T, 0:512].rearrange("p (h d) -> p h d", d=128), [ksq], [C_])
        TT(C_[:T, 12:16], D_[:T, 16:20], D_[:T, 16:20], ALU.mult, [D_], [C_])
        TT(C_[:T, 8:12], C_[:T, 8:12], C_[:T, 12:16], ALU.mult, [C_], [C_])
        TS(C_[:T, 8:12], C_[:T, 8:12], 1.0 / 128, EPS, ALU.mult, ALU.add, [C_], [C_])
        ACT(C_[:T, 8:12], C_[:T, 8:12], AF.Ln, [C_], [C_])
        ACT(C_[:T, 8:12], C_[:T, 8:12], AF.Exp, [C_], [C_], scale=-0.5)
        TT(C_[:T, 8:12], C_[:T, 8:12], D_[:T, 16:20], ALU.mult, [C_, D_], [C_])
        b = pget()
        for k in range(8):
            MM(b[:T, :], hT[:, k, :T], WB[:, k, 0:512], [hT, WB], [b], start=(k == 0), stop=(k == 7))
        sz = big[2]
        ACT(sz[:T, 0:512], b[:T, :], AF.Silu, [b], [sz])
        TT(of[:T, 0:512].rearrange("p (h d) -> p h d", d=128), of[:T, 0:512].rearrange("p (h d) -> p h d", d=128), bl(C_[:T, 8:12], 128),
           ALU.mult, [of, C_], [of])
        onb = bigb[2]
        TT(onb[:T, 0:512], of[:T, 0:512], sz[:T, 0:512], ALU.mult, [of, sz], [onb])
        out_proj(cfg, onb, 4)

    def inproj_conv_B(cfg, l, h0, is_last, first):
        T, L, nseq = cfg.T, cfg.L, cfg.nseq
        rawv = raw[:, :, 0:nseq * (L + 3)].rearrange("p c (b l) -> p c b l", l=L + 3)
        if nseq == 1:
            if first:
                MSET(raw[:, :, 0:3], 0.0, [raw])
            else:
                CP_(raw[:, :, 0:3], raw[:, :, L:L + 3], [raw], [raw])
        else:
            for part in range(3):
                cs0 = part * 1024 + h0 * 128
                LD(cst_in[:, part * 512:(part + 1) * 512], I["st_gc"][l][:, :, cs0:cs0 + 512].rearrange("b k c -> (b k) c"), cst_in)
            for g0 in (0, 8):
                b = pget()
                ng = min(8, 12 - g0)
                for c in range(ng):
                    TR(pf(b, [8, 48])[:, c, :], cst_in[:, (g0 + c) * 128:(g0 + c + 1) * 128], identf[:48, :48], [cst_in, identf], [b])
                CP_(rawv[:, g0:g0 + ng, :, 0:3], pf(b, [8, 48])[:, 0:ng, :].rearrange("p c (b k) -> p c b k", k=3), [b], [raw])
        for g0 in range(0, 12, 4):
            b = pget()
            for c in range(4):
                for k in range(8):
                    MM(pf(b, [4, T])[:, c, :], PB["WB"][:, k, 512 + (g0 + c) * 128: 512 + (g0 + c + 1) * 128], hT[:, k, :T],
                       [PB["WB"], hT], [b], start=(k == 0), stop=(k == 7))
            if nseq == 1:
                ACT(raw[:, g0:g0 + 4, 3:3 + L], pf(b, [4, T]), AF.Copy, [b], [raw])
            else:
                for c in range(4):
                    ACT(rawv[:, g0 + c, :, 3:3 + L], pf(b, [4, T])[:, c, :].rearrange("p (b l) -> p b l", l=L), AF.Copy, [b], [raw])
            if is_last and limit.get("tail", True):
                for c in range(4):
                    ACT(tailf[:, g0 + c, 0:nseq * 3].rearrange("p (b k) -> p b k", k=3),
                        pf(b, [4, T])[:, c, :].rearrange("p (b l) -> p b l", l=L)[:, :, L - 3:L], AF.Copy, [b], [tailf])
        for g0 in range(0, 12, 4):
            b = pget()
            for c in range(4):
                o = pf(b, [4, T])[:, c, :]
                if nseq > 1:
                    o = o.rearrange("p (b l) -> p b l", l=L)
                for k in range(4):
                    MM(o, DIAG[:, g0 + c, k, :], rawv[:, g0 + c, :, k:k + L] if nseq > 1 else raw[:, g0 + c, k:k + L],
                       [DIAG, raw], [b], start=(k == 0), stop=(k == 3))
            ACT(fT[:, g0:g0 + 4, :T], pf(b, [4, T]), AF.Silu, [b], [fT])
        if is_last:
            n3 = nseq * 3
            for g0 in range(0, 12, 4):
                b = pget()
                for c in range(4):
                    TR(pf(b, [4, 128], p=n3)[:, c, :], tailf[:, g0 + c, 0:n3], identf[:, :], [tailf, identf], [b])
                CP_(tailo[:n3, g0 * 128:(g0 + 4) * 128], b[:n3, :], [b], [tailo])
            dst = O["ngc_p"] if nseq == 1 else O["ngc_s"]
            for part in range(3):
                cs0 = part * 1024 + h0 * 128
                STO(dst[l][:, cs0:cs0 + 512], tailo[:n3, part * 512:(part + 1) * 512], tailo)

    def bvec(dst_ap, src_row_ap, buf):
        LD(dst_ap, src_row_ap.partition_broadcast(128), buf)

    tiles = [(CP, ti) for ti in range(limit.get("ntiles", NT))] + ([(CS, NT)] if limit.get("sample", True) else [])
    all_bufs.extend(allb)
    layer_in = None
    for l in range(limit.get("layers", DEPTH)):
        do_mod(l)
        S.barrier(all_bufs)
        mod_stacks.pop().close()
        for ph in limit.get("phases", (0, 1, 2)):
            acc_in = layer_in if ph == 0 else scr[(ph - 1)]
            last_phase = (l == DEPTH - 1 and ph == 2)
            acc_out = scr[ph] if ph < 2 else (scr[2] if not last_phase else None)
            S.barrier(all_bufs)
            pes = contextlib.ExitStack()
            cur_es[0] = pes
            if ph == 0:
                alloc_A(pes, "_%d_%d" % (l, ph))
            else:
                alloc_B(pes, "_%d_%d" % (l, ph))
            for v_ in list(PA.values()) + list(PB.values()):
                if v_ is not None and v_ not in all_bufs:
                    all_bufs.append(v_)
            if ph == 0:
                load_cols(PA["WB"], 0, I["w_in"][l][:, 0:2576], 2576)
                LD(nwT[:], I["ssd_norm_w"][l].rearrange("(k p) -> p k", p=128), nwT, nonc=True)
                load_wo(I["w_out"][l][0:1024, :], 8, lambda k: nwT[:, k:k + 1])
                build_diag(I["ssd_conv_w"][l], 0, 12)
                LD(cbrow_f[0:1, :], I["ssd_conv_b"][l:l + 1, :], cbrow_f)
                CP_(cbrow[:], cbrow_f[0:1, :], [cbrow_f], [cbrow])
                bvec(vec16[:, 0, :], I["ssd_dt_bias"][l:l + 1, :], vec16)
                bvec(vec16[:, 1, :], I["ssd_a_log"][l:l + 1, :], vec16)
                bvec(vec16[:, 2, :], I["ssd_d"][l:l + 1, :], vec16)
                ACT(vec16[:, 1, :], vec16[:, 1, :], AF.Exp, [vec16], [vec16])
                TS(vec16[:, 1, :], vec16[:, 1, :], -1.0, None, ALU.mult, ALU.bypass, [vec16], [vec16])
            else:
                h0 = (ph - 1) * 4
                load_cols(PB["WB"], 0, I["w_in"][l][:, OFF_ZG + h0 * 128:OFF_ZG + h0 * 128 + 512], 512)
                for part in range(3):
                    c0 = OFF_QKV + part * 1024 + h0 * 128
                    load_cols(PB["WB"], 512 + part * 512, I["w_in"][l][:, c0:c0 + 512], 512)
                load_cols(PB["WB"], 2048, I["w_in"][l][:, OFF_A + h0:OFF_A + h0 + 4], 4)
                load_cols(PB["WB"], 2052, I["w_in"][l][:, OFF_B + h0:OFF_B + h0 + 4], 4)
                LD(nwT[:, 0:1], I["gdn_norm_w"][l].rearrange("(p o) -> p o", o=1), nwT, nonc=True)
                load_wo(I["w_out"][l][1024 + h0 * 128:1024 + h0 * 128 + 512, :], 4, lambda k: nwT[:, 0:1])
                for part in range(3):
                    for k_ in range(4):
                        LD(cwT[:, part * 4:part * 4 + 4, k_],
                           I["gdn_conv_w"][l][k_, part * 1024 + h0 * 128:part * 1024 + h0 * 128 + 512].rearrange("(c p) -> p c", p=128), cwT, nonc=True)
                for c in range(12):
                    TT(DIAG[:, c, :, :], bm(identf[:], 4), bl(cwT[:, c, :], 128), ALU.mult, [identf, cwT], [DIAG], eng="pool")
                bvec(vec8[:, 0, :], I["gdn_dt_bias"][l:l + 1, :], vec8)
                bvec(vec8[:, 1, :], I["gdn_a_log"][l:l + 1, :], vec8)
                ACT(vec8[:, 1, :], vec8[:, 1, :], AF.Exp, [vec8], [vec8])
                TS(vec8[:, 1, :], vec8[:, 1, :], -1.0, None, ALU.mult, ALU.bypass, [vec8], [vec8])

            def issue_loads(idx):
                cfg, ti = tiles[idx]
                r0, T = rows(ti)
                xt = xt_slots[idx % 2]
                ap, dep = src_ap(layer_in.t if layer_in is not None else None, ti)
                LD(xt[:T, :], ap, xt, R=[(layer_in, ti)] if layer_in is not None else None)
                if ph > 0:
                    xa = xa_slots[idx % 2]
                    LD(xa[:T, :], acc_in.t[r0:r0 + T, :], xa, R=[(acc_in, ti)])

            if last_phase:
                fnw_bc = stage[0]
                fnw_v = stage[0][:, :, :].rearrange("p a b -> p (a b)")
                LD(fnw_v, I["final_norm_w"].partition_broadcast(128), stage[0])
            issue_loads(0)
            for idx, (cfg, ti) in enumerate(tiles):
                if idx + 1 < len(tiles):
                    issue_loads(idx + 1)
                r0, T = rows(ti)
                xt = xt_slots[idx % 2]
                xa = xa_slots[idx % 2] if ph > 0 else xt
                first = (ti == 0)
                is_last = (ti == NT - 1) or (ti == NT)
                if ph == 0:
                    phaseA_tile(cfg, l, ti, xt, first, is_last)
                else:
                    phaseB_tile(cfg, l, ti, xt, (ph - 1) * 4, first, is_last)
                TT(xo[:T, :], big[3][:T, :], xa[:T, :], ALU.add, [big[3], xa], [xo], eng="pool")
                if not last_phase:
                    STO(acc_out.t[r0:r0 + T, :], xo[:T, :], xo, W=[(acc_out, ti)])
                else:
                    ACT(junk[:T, :], xo[:T, :], AF.Square, [xo], [junk, sm1], accum_out=sm1[:T, 4:5])
                    TS(sm1[:T, 4:5], sm1[:T, 4:5], 1.0 / D, EPS, ALU.mult, ALU.add, [sm1], [sm1])
                    ACT(sm1[:T, 4:5], sm1[:T, 4:5], AF.Ln, [sm1], [sm1])
                    ACT(sm1[:T, 4:5], sm1[:T, 4:5], AF.Exp, [sm1], [sm1], scale=-0.5)
                    STT(big[0][:T, :], xo[:T, :], sm1[:T, 4:5], fnw_v[:T, :], ALU.mult, ALU.mult, [xo, sm1, fnw_bc], [big[0]])
                    dsto = O["y_p"][r0:r0 + T, :] if ti < NT else O["y_s"][:, :]
                    STO(dsto, big[0][:T, :], big[0])
            pes.close()
            cur_es[0] = es
            PA.clear()
            PB.clear()
        layer_in = scr[2]
    S.emit()
    es.close()
    return nc, dbg_out


_CACHE = {}


def make_in_maps(inputs):
    g = {k: np.ascontiguousarray(np.asarray(v, dtype=np.float32)) for k, v in inputs.items()}
    consts = host_consts()
    maps = []
    for c in range(8):
        m = {}
        m["xp"] = g["x_prompt"][c]
        m["xs"] = g["x_sample"][c * NSB:(c + 1) * NSB].reshape(NSB * SL, D)
        m["call"] = np.concatenate([g["c_prompt"][c:c + 1], g["c_sample"][c * NSB:(c + 1) * NSB]], axis=0)
        m["st_sc"] = g["state_ssd_conv"][:, c * NSB:(c + 1) * NSB]
        m["st_ssm"] = g["state_ssm"][:, c * NSB:(c + 1) * NSB]
        m["st_gc"] = g["state_gdn_conv"][:, c * NSB:(c + 1) * NSB]
        m["st_gdn"] = g["state_gdn"][:, c * NSB:(c + 1) * NSB]
        for k in ("norm_w", "w_ada", "b_ada", "w_in", "ssd_conv_w", "ssd_conv_b", "ssd_dt_bias", "ssd_a_log", "ssd_d",
                  "ssd_norm_w", "gdn_conv_w", "gdn_dt_bias", "gdn_a_log", "gdn_norm_w", "w_out"):
            m[k] = g[k]
        m["final_norm_w"] = g["final_norm_w"].reshape(1, D)
        for k, v in consts.items():
            m["c_" + k] = v
        maps.append({k: np.ascontiguousarray(v) for k, v in m.items()})
    return maps


def kernel(**inputs):
    if "nc" not in _CACHE:
        _CACHE["nc"] = build()[0]
    nc = _CACHE["nc"]
    maps = make_in_maps(inputs)
    res = run_bass_kernel_spmd(nc, maps, core_ids=list(range(8)))
    R = res.results
    cat = lambda k, ax: np.concatenate([np.asarray(r[k]) for r in R], axis=ax)
    y_p = np.stack([np.asarray(r["y_p"]) for r in R], 0)
    y_s = np.stack([np.asarray(r["y_s"]).reshape(NSB, SL, D) for r in R], 0).reshape(8 * NSB, SL, D)
    ncs_p = np.stack([np.asarray(r["ncs_p"]) for r in R], 1)
    nssm_p = np.stack([np.asarray(r["nssm_p"]).reshape(DEPTH, 16, 64, 128) for r in R], 1)
    ngc_p = np.stack([np.asarray(r["ngc_p"]) for r in R], 1)
    ngdn_p = np.stack([np.asarray(r["ngdn_p"]) for r in R], 1)
    ncs_s = np.concatenate([np.asarray(r["ncs_s"]).reshape(DEPTH, NSB, 3, 1536) for r in R], 1)
    nssm_s = np.concatenate([np.asarray(r["nssm_s"]).reshape(DEPTH, NSB, 16, 64, 128) for r in R], 1)
    ngc_s = np.concatenate([np.asarray(r["ngc_s"]).reshape(DEPTH, NSB, 3, 3072) for r in R], 1)
    ngdn_s = np.concatenate([np.asarray(r["ngdn_s"]) for r in R], 1)
    outs = (y_p, y_s, ncs_p, nssm_p, ngc_p, ngdn_p, ncs_s, nssm_s, ngc_s, ngdn_s)
    return tuple(np.ascontiguousarray(o, dtype=np.float32) for o in outs)
```

```python
import contextlib
import numpy as np
import concourse.bass as bass
import concourse.mybir as mybir
from concourse.bass_utils import run_bass_kernel_spmd

F32 = mybir.dt.float32
F32R = mybir.dt.float32r
BF16 = mybir.dt.bfloat16
AF = mybir.ActivationFunctionType
ALU = mybir.AluOpType
AX = mybir.AxisListType

ENGS = ("pe", "dve", "act", "pool", "sp")


class Buf:
    def __init__(self, t, name):
        self.t = t
        self.name = name
        self.st = {}
        self.dma_sem = None
        self.dma_cnt = 0

    def __getitem__(self, k):
        return self.t[k]


class Instr:
    __slots__ = ("eng", "fn", "deps", "signal", "count", "is_dma", "buf_sem", "dma_count")

    def __init__(self, eng, fn):
        self.eng = eng
        self.fn = fn
        self.deps = []
        self.signal = False
        self.count = None
        self.is_dma = False
        self.buf_sem = None
        self.dma_count = None


class Sched:
    def __init__(self, nc):
        self.nc = nc
        self.streams = {e: [] for e in ENGS}
        self.all = []

    def _states(self, buf, key):
        st = buf.st
        if key is None:
            if None not in st:
                st[None] = [None, {}]
            return list(st.values())
        if key not in st:
            if None in st:
                st[key] = [st[None][0], dict(st[None][1])]
            else:
                st[key] = [None, {}]
        res = [st[key]]
        if None in st:
            res.append(st[None])
        return res

    @staticmethod
    def _norm(lst):
        out = []
        for x in lst or []:
            out.append((x, None) if isinstance(x, Buf) else x)
        return out

    def add(self, eng, fn, reads=None, writes=None, dma_buf=None):
        ins = Instr(eng, fn)
        ins.is_dma = dma_buf is not None
        reads = self._norm(reads)
        writes = self._norm(writes)
        deps, raw = [], []
        for buf, key in reads:
            for s in self._states(buf, key):
                if s[0] is not None:
                    deps.append(s[0])
                    raw.append(s[0])
                if getattr(buf, "excl", False):
                    for r in s[1].values():
                        if r.eng != eng:
                            deps.append(r)
        for buf, key in writes:
            for s in self._states(buf, key):
                if s[0] is not None:
                    deps.append(s[0])
                deps.extend(s[1].values())
        fdeps = {}
        for d in deps:
            if d is ins:
                continue
            if (not ins.is_dma) and (not d.is_dma) and d.eng == eng:
                if eng == "pe" or not any(d is r for r in raw):
                    continue
            if d.is_dma:
                fdeps[id(d)] = (d, d.buf_sem.dma_cnt * 16)
            else:
                fdeps[id(d)] = (d, None)
        ins.deps = list(fdeps.values())
        if ins.is_dma:
            ins.buf_sem = dma_buf
            dma_buf.dma_cnt += 1
            ins.dma_count = dma_buf.dma_cnt * 16
        rkey = ("dma", id(ins)) if ins.is_dma else eng
        for buf, key in reads:
            for s in self._states(buf, key):
                s[1][rkey] = ins
        for buf, key in writes:
            for s in self._states(buf, key):
                s[0] = ins
                if key is None or s is not buf.st.get(None):
                    s[1] = {}
        self.streams[eng].append(ins)
        self.all.append(ins)
        return ins

    def barrier(self, dma_bufs_all):
        last = {}
        for e in ("pe", "dve", "act", "pool"):
            for ins in reversed(self.streams[e]):
                if not isinstance(ins, tuple) and not ins.is_dma:
                    last[e] = ins
                    ins.signal = True
                    break
        snap = [(b, b.dma_cnt * 16) for b in dma_bufs_all if b.dma_cnt > 0]
        mark = ("barrier", last, snap)
        for e in ENGS:
            self.streams[e].append(mark)

    def emit(self, final_wait_eng="sp"):
        nc = self.nc
        for ins in self.all:
            for d, _v in ins.deps:
                if not d.is_dma:
                    d.signal = True
        with contextlib.ExitStack() as es:
            eng_sem = {e: es.enter_context(nc.semaphore("s_" + e)) for e in ("pe", "dve", "act", "pool")}
            dma_bufs = []
            for ins in self.all:
                if isinstance(ins, tuple):
                    continue
                if ins.is_dma and ins.buf_sem.dma_sem is None:
                    ins.buf_sem.dma_sem = es.enter_context(nc.semaphore("d_%d" % len(dma_bufs)))
                    dma_bufs.append(ins.buf_sem)
            cnt = {e: 0 for e in eng_sem}
            for ins in self.all:
                if not ins.is_dma and ins.signal:
                    cnt[ins.eng] += 1
                    ins.count = cnt[ins.eng]
            block = es.enter_context(nc.Block())
            engobj = {"pe": block.tensor, "dve": block.vector, "act": block.scalar, "pool": block.gpsimd,
                      "sp": block.sync}
            sched = self

            def make(ename):
                def body(eng):
                    water = {}
                    for ins in sched.streams[ename]:
                        if isinstance(ins, tuple):
                            _, last, snap = ins
                            for e2, li in last.items():
                                if e2 != ename and water.get(("e", e2), 0) < li.count:
                                    water[("e", e2)] = li.count
                                    eng.wait_ge(eng_sem[e2], li.count)
                            for b_, v_ in snap:
                                if water.get(("d", id(b_)), 0) < v_:
                                    water[("d", id(b_))] = v_
                                    eng.wait_ge(b_.dma_sem, v_)
                            continue
                        need = {}
                        for d, dv in ins.deps:
                            if d.is_dma:
                                key, sem, val = ("d", id(d.buf_sem)), d.buf_sem.dma_sem, dv
                            else:
                                key, sem, val = ("e", d.eng), eng_sem[d.eng], d.count
                            if need.get(key, (None, 0))[1] < val:
                                need[key] = (sem, val)
                        for key, (sem, val) in need.items():
                            if water.get(key, 0) >= val:
                                continue
                            water[key] = val
                            eng.wait_ge(sem, val)
                        bi = ins.fn(eng)
                        if ins.is_dma:
                            bi.then_inc(ins.buf_sem.dma_sem, 16)
                        elif ins.signal:
                            bi.then_inc(eng_sem[ename], 1)
                    if ename == final_wait_eng:
                        for b in dma_bufs:
                            eng.wait_ge(b.dma_sem, b.dma_cnt * 16)
                return body

            for ename in ENGS:
                engobj[ename](make(ename))


D = 1024
SEQ = 2048
NT = SEQ // 128
NSB = 16
SL = 4
DEPTH = 2
IN_DIM = 6688
EPS = 1e-6
NEG = -30000.0
OFF_ZS, OFF_XBC, OFF_DT, OFF_ZG, OFF_QKV, OFF_A, OFF_B = 0, 1024, 2560, 2576, 3600, 6672, 6680


class Cfg:
    def __init__(self, name, T, nseq, L, Q, chain):
        self.name, self.T, self.nseq, self.L, self.Q, self.chain = name, T, nseq, L, Q, chain
        self.nch = T // Q
        self.nlev = {64: 5, 4: 1}[Q]


CP = Cfg("p", 128, 1, 128, 64, True)
CS = Cfg("s", 64, NSB, SL, 4, False)


def host_consts():
    c = {"ident": np.eye(128, dtype=np.float32), "ones": np.ones((128, 128), np.float32)}
    for cfg in (CP, CS):
        T, Q, nch = cfg.T, cfg.Q, cfg.nch
        ch = np.arange(T) // Q
        same = ch[:, None] == ch[None, :]
        idx = np.arange(T)
        le = idx[:, None] <= idx[None, :]
        lt = idx[:, None] < idx[None, :]
        n = cfg.name
        c["tri_" + n] = (same & le).astype(np.float32)
        c["blk_" + n] = same.astype(np.float32)
        c["nmI_" + n] = np.where(same & le, 0.0, NEG).astype(np.float32)
        c["nmS_" + n] = np.where(same & lt, 0.0, NEG).astype(np.float32)
        c["nmSr_" + n] = np.ascontiguousarray(c["nmS_" + n].T)
        cm = (ch[None, :] == np.arange(nch)[:, None]).astype(np.float32)
        c["cmf_" + n] = np.ascontiguousarray(np.broadcast_to(cm[None], (128, nch, T))).astype(np.float32)
        c["cmT_" + n] = np.ascontiguousarray(cm.T)
    e = np.zeros((17, 128), np.float32)
    e[0, :] = 1.0
    c["E_p"] = e
    e = np.zeros((17, 64), np.float32)
    for t in range(64):
        e[1 + t // SL, t] = 1.0
    c["E_s"] = e
    return c


def bl(ap, n):
    return ap.unsqueeze(len(ap.shape)).to_broadcast(list(ap.shape) + [n])


def bm(ap, n):
    return ap.unsqueeze(1).to_broadcast([ap.shape[0], n] + list(ap.shape[1:]))


def build(debug=(), limit=None):
    limit = limit or {}
    nc = bass.Bass("TRN2", target_bir_lowering=False)
    S = Sched(nc)
    es = contextlib.ExitStack()
    consts_np = host_consts()

    def din(name, shape):
        return nc.dram_tensor(name, list(shape), F32, kind="ExternalInput").ap()

    def dout(name, shape):
        return nc.dram_tensor(name, list(shape), F32, kind="ExternalOutput").ap()

    I = {}
    I["xp"] = din("xp", [SEQ, D])
    I["xs"] = din("xs", [NSB * SL, D])
    I["call"] = din("call", [17, D])
    I["st_sc"] = din("st_sc", [DEPTH, NSB, 3, 1536])
    I["st_ssm"] = din("st_ssm", [DEPTH, NSB, 16, 64, 128])
    I["st_gc"] = din("st_gc", [DEPTH, NSB, 3, 3072])
    I["st_gdn"] = din("st_gdn", [DEPTH, NSB, 8, 128, 128])
    wshapes = {"norm_w": [DEPTH, D], "w_ada": [DEPTH, D, 3 * D], "b_ada": [DEPTH, 3 * D], "w_in": [DEPTH, D, IN_DIM],
               "ssd_conv_w": [DEPTH, 4, 1536], "ssd_conv_b": [DEPTH, 1536], "ssd_dt_bias": [DEPTH, 16],
               "ssd_a_log": [DEPTH, 16], "ssd_d": [DEPTH, 16], "ssd_norm_w": [DEPTH, 1024],
               "gdn_conv_w": [DEPTH, 4, 3072], "gdn_dt_bias": [DEPTH, 8], "gdn_a_log": [DEPTH, 8],
               "gdn_norm_w": [DEPTH, 128], "w_out": [DEPTH, 2048, D], "final_norm_w": [1, D]}
    for k, shp in wshapes.items():
        I[k] = din(k, shp)
    for k, v in consts_np.items():
        I["c_" + k] = din("c_" + k, v.shape)
    O = {}
    O["y_p"] = dout("y_p", [SEQ, D])
    O["y_s"] = dout("y_s", [NSB * SL, D])
    O["ncs_p"] = dout("ncs_p", [DEPTH, 3, 1536])
    O["nssm_p"] = dout("nssm_p", [DEPTH, 1024, 128])
    O["ngc_p"] = dout("ngc_p", [DEPTH, 3, 3072])
    O["ngdn_p"] = dout("ngdn_p", [DEPTH, 8, 128, 128])
    O["ncs_s"] = dout("ncs_s", [DEPTH, NSB * 3, 1536])
    O["nssm_s"] = dout("nssm_s", [DEPTH, NSB, 1024, 128])
    O["ngc_s"] = dout("ngc_s", [DEPTH, NSB * 3, 3072])
    O["ngdn_s"] = dout("ngdn_s", [DEPTH, NSB, 8, 128, 128])
    NROW = SEQ + NSB * SL
    scr = [Buf(nc.dram_tensor("scr%d" % i, [NROW, D], F32, kind="Internal").ap(), "scr%d" % i) for i in range(3)]
    xin_ext = None

    allb = []

    def sb(name, shape, dt=F32):
        b_ = Buf(es.enter_context(nc.sbuf_tensor(name, list(shape), dt)), name)
        allb.append(b_)
        return b_

    banks = [Buf(es.enter_context(nc.psum_tensor("bank%d" % i, [128, 512], F32)), "bank%d" % i) for i in range(8)]
    for b_ in banks:
        b_.excl = True
    pstate = {"i": 0, "pinned": set()}

    def pget():
        while True:
            i = pstate["i"] % 8
            pstate["i"] += 1
            if i not in pstate["pinned"]:
                return banks[i]

    def ppin(b):
        pstate["pinned"].add(banks.index(b))

    def punpin(b):
        pstate["pinned"].discard(banks.index(b))

    def pf(b, shape, p=128):
        n = int(np.prod(shape))
        ap = b[:p, 0:n]
        if len(shape) == 1:
            return ap
        names = " ".join("a%d" % i for i in range(len(shape)))
        kw = {"a%d" % i: shape[i] for i in range(1, len(shape))}
        return ap.rearrange("p (%s) -> p %s" % (names, names), **kw)

    def pb(b, shape, p=128):
        n = int(np.prod(shape))
        ap = b[:p, :].bitcast(BF16)[:, 0:n]
        if len(shape) == 1:
            return ap
        names = " ".join("a%d" % i for i in range(len(shape)))
        kw = {"a%d" % i: shape[i] for i in range(1, len(shape))}
        return ap.rearrange("p (%s) -> p %s" % (names, names), **kw)

    def MM(out, lhsT, rhs, R, W, start=True, stop=True, **kw):
        S.add("pe", lambda e: e.matmul(out, lhsT=lhsT, rhs=rhs, start=start, stop=stop, **kw), reads=R, writes=W)

    def TR(out, in_, ident, R, W):
        S.add("pe", lambda e: e.transpose(out=out, in_=in_, identity=ident), reads=R, writes=W)

    def ACT(out, in_, func, R, W, **kw):
        S.add("act", lambda e: e.activation(out=out, in_=in_, func=func, **kw), reads=R, writes=W)

    def TT(out, in0, in1, op, R, W, eng="dve"):
        S.add(eng, lambda e: e.tensor_tensor(out=out, in0=in0, in1=in1, op=op), reads=R, writes=W)

    def TS(out, in0, s1, s2, op0, op1, R, W, eng="dve"):
        S.add(eng, lambda e: e.tensor_scalar(out=out, in0=in0, scalar1=s1, scalar2=s2, op0=op0, op1=op1), reads=R, writes=W)

    def STT(out, in0, scalar, in1, op0, op1, R, W):
        S.add("dve", lambda e: e.scalar_tensor_tensor(out=out, in0=in0, scalar=scalar, in1=in1, op0=op0, op1=op1), reads=R, writes=W)

    def CP_(out, in_, R, W, eng="dve"):
        S.add(eng, lambda e: e.tensor_copy(out=out, in_=in_), reads=R, writes=W)

    def RECIP(out, in_, R, W):
        S.add("dve", lambda e: e.reciprocal(out=out, in_=in_), reads=R, writes=W)

    def RED(out, in_, R, W):
        S.add("dve", lambda e: e.tensor_reduce(out=out, in_=in_, axis=AX.X, op=ALU.add), reads=R, writes=W)

    def MSET(ap, val, W, eng="pool"):
        S.add(eng, lambda e: e.memset(ap, val), writes=W)

    def LD(out, in_, buf, R=None, nonc=False):
        if nonc:
            S.add("sp", lambda e: e.dma_start(out=out, in_=in_, allow_slow_non_contiguous=True), reads=R, writes=[buf], dma_buf=buf)
        else:
            S.add("sp", lambda e: e.dma_start(out=out, in_=in_), reads=R, writes=[buf], dma_buf=buf)

    sto_n = [0]
    sto_dummy = Buf(None, "sto_dummy")

    def STO(out, in_, buf, W=None, extra_reads=None):
        S.add("sp", lambda e: e.dma_start(out=out, in_=in_), reads=[buf] + (extra_reads or []), writes=W, dma_buf=buf)

    dbg_out = {}
    mod_stacks = []
    cur_es = [es]
    all_bufs = []

    def DBG(name, buf, ap, shape):
        if name in debug and name not in dbg_out:
            o = dout("dbg_" + name, shape)
            dbg_out[name] = o
            if ap.dtype != F32:
                tmp = Buf(cur_es[0].enter_context(nc.sbuf_tensor("dbgt_" + name, list(shape), F32)), "dbgt_" + name)
                all_bufs.append(tmp)
                CP_(tmp[:], ap, [buf], [tmp])
                STO(o, tmp[:], tmp)
            else:
                STO(o, ap, buf)

    C = {}
    for k, v in consts_np.items():
        if k.startswith("cmf_"):
            continue
        C[k] = sb("k_" + k, v.shape)
        LD(C[k][:], I["c_" + k], C[k])
    identf = C["ident"]
    onesf = C["ones"]
    identb = sb("identb", [128, 128], BF16)
    CP_(identb[:], identf[:], [identf], [identb])
    onesb = sb("onesb", [128, 128], BF16)
    CP_(onesb[:], onesf[:], [onesf], [onesb])
    cfg_pending = []
    for cfg in (CP, CS):
        n = cfg.name
        cfg.tri, cfg.blk, cfg.nmI, cfg.nmS, cfg.nmSr, cfg.cmT = (C[k + n] for k in ("tri_", "blk_", "nmI_", "nmS_", "nmSr_", "cmT_"))
        cfg.cmf = sb("cmfb_" + n, [128, cfg.nch, cfg.T], BF16)
        cfg.ncmf = sb("ncmfb_" + n, [128, cfg.nch, cfg.T], BF16)
        cfg_pending.append(cfg)
        cfg.cmTb = sb("cmTb_" + n, [cfg.T, cfg.nch], BF16)
        CP_(cfg.cmTb[:], cfg.cmT[:], [cfg.cmT], [cfg.cmTb])

    SW = 128
    stage = [sb("stage%d" % i, [128, 8, SW]) for i in range(2)]
    stage_i = [0]
    WO = sb("wo", [128, 8, D], BF16)
    DIAG = sb("diag", [128, 12, 4, 128], BF16)
    cwT = sb("cwT", [128, 12, 4])
    cbrow = sb("cbrow", [1, 1536], BF16)
    nwT = sb("nwT", [128, 8])
    scT = sb("scT", [128, 8, 17])
    shT = sb("shT", [128, 8, 17])
    gate_bc = {"p": sb("gate_p", [128, D]), "s": sb("gate_s", [64, D])}
    vec16 = sb("vec16", [128, 4, 16])
    vec8 = sb("vec8", [128, 2, 8])
    scTb = sb("scTb", [128, 8, 17], BF16)
    badaT = sb("badaT", [128, 24])
    nw_in = sb("nw_in", [128, 8])
    xt_slots = [sb("xt%d" % i, [128, D]) for i in range(2)]
    xa_slots = [sb("xa%d" % i, [128, D]) for i in range(2)]
    stage_all = stage + xt_slots + xa_slots
    junk = sb("junk", [128, D], BF16)
    xn = sb("xn", [128, D], BF16)
    hT = sb("hT", [128, 8, 128], BF16)
    sm1 = sb("sm1", [128, 8])
    raw = sb("raw", [128, 12, 132], BF16)
    tailf = sb("tailf", [128, 12, 48])
    tailo = sb("tailo", [48, 1536])
    cst_in = tailo
    fT = sb("fT", [128, 12, 128], BF16)
    big = [sb("big%d" % i, [128, D]) for i in range(4)]
    bigb = [sb("bigb%d" % i, [128, D], BF16) for i in range(3)]
    xo = big[3]
    call_sb, sil_c, bgate, gate17, cbrow_f = big[0], bigb[0], big[1], big[2], tailo
    hTf_t = big[0]
    for cfg in cfg_pending:
        n_ = cfg.nch * cfg.T
        tmpv = big[0][:, 0:n_].rearrange("p (c t) -> p c t", t=cfg.T)
        LD(tmpv, I["c_cmf_" + cfg.name], big[0])
        CP_(cfg.cmf[:], tmpv, [big[0]], [cfg.cmf])
        TS(cfg.ncmf[:], tmpv, -1.0, None, ALU.mult, ALU.bypass, [big[0]], [cfg.ncmf])

    def stg():
        i = stage_i[0] % len(stage_all)
        stage_i[0] += 1
        b_ = stage_all[i]
        v = b_[:, :, :] if b_ in stage else b_[:, :].rearrange("p (k n) -> p k n", k=8)
        return b_, v, ("pool", "act", "dve")[i % 3]

    def cast(eng, out, in_, R, W, scale=None):
        if eng == "act":
            if scale is None:
                ACT(out, in_, AF.Copy, R, W)
            else:
                ACT(out, in_, AF.Copy, R, W, scale=scale)
        elif scale is None:
            CP_(out, in_, R, W, eng=eng)
        else:
            TS(out, in_, scale, 1.0, ALU.mult, ALU.mult, R, W, eng=eng)

    def load_cols(dst, dcol, src, ncols):
        c0 = 0
        while c0 < ncols:
            n = min(SW, ncols - c0)
            sbuf_, st, eng = stg()
            LD(st[:, :, 0:n], src[:, c0:c0 + n].rearrange("(k p) n -> p k n", p=128), sbuf_)
            cast(eng, dst[:, :, dcol + c0:dcol + c0 + n], st[:, :, 0:n], [sbuf_], [dst])
            c0 += n

    def load_wo(src_rows, nk, scale_ap_fn):
        for half in range(D // SW):
            sbuf_, st, eng = stg()
            LD(st[:, 0:nk, :], src_rows[:, half * SW:(half + 1) * SW].rearrange("(k p) n -> p k n", p=128), sbuf_)
            for k in range(nk):
                cast(eng, WO[:, k, half * SW:(half + 1) * SW], st[:, k, :], [sbuf_, nwT], [WO], scale=scale_ap_fn(k))

    def build_diag(convw, c0, nchunks):
        for k_ in range(4):
            LD(cwT[:, 0:nchunks, k_], convw[k_, c0 * 128:(c0 + nchunks) * 128].rearrange("(c p) -> p c", p=128), cwT, nonc=True)
        for c in range(nchunks):
            TT(DIAG[:, c, :, :], bm(identf[:], 4), bl(cwT[:, c, :], 128), ALU.mult, [identf, cwT], [DIAG], eng="pool")

    def do_mod(l):
        LD(call_sb[:17, :], I["call"], call_sb)
        ACT(sil_c[:17, :], call_sb[:17, :], AF.Silu, [call_sb], [sil_c])
        b = pget()
        for c in range(8):
            TR(pb(b, [8, 32])[:, c, 0:17], sil_c[:17, c * 128:(c + 1) * 128], identb[:17, :17], [sil_c, identb], [b])
        CP_(scTb[:], pb(b, [8, 32])[:, :, 0:17], [b], [scTb])
        LD(badaT[:], I["b_ada"][l].rearrange("(c p) -> p c", p=128), badaT, nonc=True)
        LD(nw_in[:], I["norm_w"][l].rearrange("(c p) -> p c", p=128), nw_in, nonc=True)
        LD(bgate[0:1, :], I["b_ada"][l:l + 1, 2048:3072], bgate)
        nblk = 3 * D // SW
        per = D // SW
        modes = contextlib.ExitStack()
        wb = Buf(modes.enter_context(nc.sbuf_tensor("modwb%d" % l, [128, 8, SW], BF16)), "modwb")
        all_bufs.append(wb)
        for blk in range(nblk):
            sbuf_, st, eng = stg()
            LD(st, I["w_ada"][l][:, blk * SW:(blk + 1) * SW].rearrange("(k p) n -> p k n", p=128), sbuf_)
            cast(eng, wb[:, :, 0:SW], st, [sbuf_], [wb])
            if blk < 2 * per:
                b = pget()
                nj = SW // 128
                for j in range(nj):
                    for k in range(8):
                        MM(pf(b, [nj, 17])[:, j, :], wb[:, k, j * 128:(j + 1) * 128], scTb[:, k, :], [wb, scTb], [b],
                           start=(k == 0), stop=(k == 7))
                dst = shT if blk < per else scT
                cc = (blk % per) * nj
                TT(dst[:, cc:cc + nj, :], pf(b, [nj, 17]), bl(badaT[:, blk * nj:blk * nj + nj], 17), ALU.add, [b, badaT], [dst])
            else:
                g = pget()
                gc0 = (blk - 2 * per) * SW
                for k in range(8):
                    MM(g[:17, 0:SW], scTb[:, k, :], wb[:, k, 0:SW], [wb, scTb], [g], start=(k == 0), stop=False)
                MM(g[:17, 0:SW], onesf[0:1, 0:17], bgate[0:1, gc0:gc0 + SW], [onesf, bgate], [g], start=False, stop=True)
                CP_(gate17[:17, gc0:gc0 + SW], g[:17, 0:SW], [g], [gate17])
        TS(scT[:], scT[:], 1.0, None, ALU.add, ALU.bypass, [scT], [scT])
        TT(scT[:], scT[:], bl(nw_in[:], 17), ALU.mult, [scT, nw_in], [scT])
        for n, E, T in (("p", C["E_p"], 128), ("s", C["E_s"], 64)):
            for half in range(2):
                b = pget()
                MM(b[:T, :], E[:, :T], gate17[:17, half * 512:(half + 1) * 512], [E, gate17], [b])
                CP_(gate_bc[n][:T, half * 512:(half + 1) * 512], b[:T, :], [b], [gate_bc[n]])
        mod_stacks.append(modes)

    def rows(ti):
        return (ti * 128, 128) if ti < NT else (SEQ, 64)

    def src_ap(layer_in, ti):
        r0, T = rows(ti)
        if layer_in is None:
            return (I["xp"][r0:r0 + T, :] if ti < NT else I["xs"][:, :]), None
        return layer_in[r0:r0 + T, :], (layer_in, ti)

    def prologue(cfg, xt):
        T = cfg.T
        ACT(junk[:T, :], xt[:T, :], AF.Square, [xt], [junk, sm1], accum_out=sm1[:T, 0:1])
        TS(sm1[:T, 0:1], sm1[:T, 0:1], 1.0 / D, EPS, ALU.mult, ALU.add, [sm1], [sm1])
        ACT(sm1[:T, 0:1], sm1[:T, 0:1], AF.Ln, [sm1], [sm1])
        ACT(sm1[:T, 0:1], sm1[:T, 0:1], AF.Exp, [sm1], [sm1], scale=-0.5)
        ACT(xn[:T, :], xt[:T, :], AF.Copy, [xt, sm1], [xn], scale=sm1[:T, 0:1])
        b = pget()
        for c in range(8):
            TR(pb(b, [8, T])[:, c, :], xn[:T, c * 128:(c + 1) * 128], identb[:T, :T], [xn, identb], [b])
        if cfg.nseq == 1:
            sc, sh = bl(scT[:, :, 0], T), bl(shT[:, :, 0], T)
            o1, o2, i1 = hTf_t[:, 0:8 * T].rearrange("p (c t) -> p c t", t=T), hT[:, :, :T], pb(b, [8, T])
        else:
            sc, sh = bl(scT[:, :, 1:17], SL), bl(shT[:, :, 1:17], SL)
            o1 = hTf_t[:, 0:8 * T].rearrange("p (c b l) -> p c b l", c=8, l=SL)
            o2 = hT[:, :, :T].rearrange("p c (b l) -> p c b l", l=SL)
            i1 = pb(b, [8, NSB, SL])
        TT(o1, i1, sc, ALU.mult, [b, scT], [hTf_t])
        TT(o2, o1, sh, ALU.add, [hTf_t, shT], [hT])

    def inproj_conv(cfg, l, wcol0, nchunks, bias_row, st_conv_in, conv_out, conv_out_s, ch0, is_last, first):
        T, L, nseq = cfg.T, cfg.L, cfg.nseq
        rawv = raw[:, :, 0:nseq * (L + 3)].rearrange("p c (b l) -> p c b l", l=L + 3)
        if cfg.nseq == 1:
            if first:
                MSET(raw[:, :, 0:3], 0.0, [raw])
            else:
                CP_(raw[:, 0:nchunks, 0:3], raw[:, 0:nchunks, L:L + 3], [raw], [raw])
        else:
            LD(cst_in[:, 0:nchunks * 128], st_conv_in[:, :, ch0 * 128:(ch0 + nchunks) * 128].rearrange("b k c -> (b k) c"), cst_in)
            for g0 in range(0, nchunks, 8):
                b = pget()
                ng = min(8, nchunks - g0)
                for c in range(ng):
                    TR(pf(b, [8, 48])[:, c, :], cst_in[:, (g0 + c) * 128:(g0 + c + 1) * 128], identf[:48, :48], [cst_in, identf], [b])
                CP_(rawv[:, g0:g0 + ng, :, 0:3], pf(b, [8, 48])[:, 0:ng, :].rearrange("p c (b k) -> p c b k", k=3), [b], [raw])
        if limit.get("stopC", 99) <= 1:
            return
        for g0 in range(0, nchunks, 4):
            b = pget()
            for c in range(4):
                for k in range(8):
                    MM(pf(b, [4, T])[:, c, :], PA["WB"][:, k, wcol0 + (g0 + c) * 128: wcol0 + (g0 + c + 1) * 128], hT[:, k, :T],
                       [PA["WB"], hT], [b], start=(k == 0), stop=(k == 7))
            if nseq == 1:
                ACT(raw[:, g0:g0 + 4, 3:3 + L], pf(b, [4, T]), AF.Copy, [b], [raw])
            else:
                for c in range(4):
                    ACT(rawv[:, g0 + c, :, 3:3 + L], pf(b, [4, T])[:, c, :].rearrange("p (b l) -> p b l", l=L), AF.Copy, [b], [raw])
            if is_last and limit.get("tail", True):
                for c in range(4):
                    ACT(tailf[:, g0 + c, 0:nseq * 3].rearrange("p (b k) -> p b k", k=3),
                        pf(b, [4, T])[:, c, :].rearrange("p (b l) -> p b l", l=L)[:, :, L - 3:L], AF.Copy, [b], [tailf])
        if limit.get("stopC", 99) <= 2:
            return
        for g0 in range(0, nchunks, 4):
            b = pget()
            for c in range(4):
                o = pf(b, [4, T])[:, c, :]
                if nseq > 1:
                    o = o.rearrange("p (b l) -> p b l", l=L)
                for k in range(4):
                    MM(o, DIAG[:, g0 + c, k, :], rawv[:, g0 + c, :, k:k + L] if nseq > 1 else raw[:, g0 + c, k:k + L],
                       [DIAG, raw], [b], start=(k == 0), stop=(k == 3 and bias_row is None))
                if bias_row is not None:
                    MM(pf(b, [4, T])[:, c, :], bias_row[0:1, (g0 + c) * 128:(g0 + c + 1) * 128], onesb[0:1, :T], [bias_row, onesb], [b],
                       start=False, stop=True)
            ACT(fT[:, g0:g0 + 4, :T], pf(b, [4, T]), AF.Silu, [b], [fT])
        if limit.get("stopC", 99) <= 3:
            return
        if is_last:
            n3 = nseq * 3
            for g0 in range(0, nchunks, 4):
                b = pget()
                for c in range(4):
                    TR(pf(b, [4, 128], p=n3)[:, c, :], tailf[:, g0 + c, 0:n3], identf[:, :], [tailf, identf], [b])
                CP_(tailo[:n3, g0 * 128:(g0 + 4) * 128], b[:n3, :], [b], [tailo])
            dst = conv_out[l][:, ch0 * 128:(ch0 + nchunks) * 128] if nseq == 1 else conv_out_s[l][:, ch0 * 128:(ch0 + nchunks) * 128]
            STO(dst, tailo[:n3, 0:nchunks * 128], tailo)

    def cum_and_tot(cfg, a_ap, a_buf, nh, out_c, out_dbc, pw):
        T, nch = cfg.T, cfg.nch
        b = pget()
        MM(b[:T, 0:nh], cfg.tri[:, :], a_ap, [cfg.tri, a_buf], [b])
        MM(b[:T, nh:2 * nh], cfg.blk[:, :], a_ap, [cfg.blk, a_buf], [b])
        ACT(out_c[:T, 0:2 * nh], b[:T, 0:2 * nh], AF.Copy, [b], [out_c])
        b2 = pget()
        MM(b2[:nch, 0:nh], cfg.cmT[:, :], a_ap, [cfg.cmT, a_buf], [b2])
        CP_(tot[:nch, 0:nh], b2[:nch, 0:nh], [b2], [tot])
        TT(totx[:nch, 0:nch * nh].rearrange("p (c h) -> p c h", h=nh), bm(tot[:nch, 0:nh], nch), bl(identf[:nch, :nch], nh), ALU.mult,
           [tot, identf], [totx])
        b3 = pget()
        MM(b3[:pw, 0:nch * nh], onesf[:nch, :pw], totx[:nch, 0:nch * nh], [onesf, totx], [b3])
        ACT(out_dbc[:pw, 0:nch * nh], b3[:pw, 0:nch * nh], AF.Exp, [b3], [out_dbc])

    tot = sb("tot", [16, 16])
    totx = sb("totx", [16, 256])
    sm = {k: sb("sm_" + k, [128, 64]) for k in ("a", "b", "c", "d", "e")}
    dbc = sb("dbc", [128, 256])
    acT = sb("acT", [16, 128])

    PA = {}

    def alloc_A(pes, tag):
        def sbp(name, shape, dt=F32):
            return Buf(pes.enter_context(nc.sbuf_tensor(name + tag, list(shape), dt)), name)
        PA.update(stf=sbp("stf", [128, D]), stb=sbp("stb", [128, D], BF16), hn=sbp("hn", [128, 8, 128]),
                  stf_s=sbp("stf_s", [128, D]), stb_s=sbp("stb_s", [128, D], BF16), Gs=sbp("Gs", [128, 2, 128]),
                  earg=sbp("earg", [128, 4, 128]), Ee=sbp("Ee", [128, 4, 128]), WT=sbp("WT", [128, 16, 128], BF16),
                  Btok=sbp("Btok", [128, 256], BF16), Btm=sbp("Btm", [128, 2, 256], BF16), CTm=sbp("CTm", [128, 2, 2, 128], BF16),
                  WB=sbp("WB", [128, 8, 2576], BF16))
    mslot = [0]

    def ssd_state_update(cfg, c, st_in, stb_in, st_out, yo, xdec, first_c, last_c):
        T, nch = cfg.T, cfg.nch
        Btm, CTm, Btok = PA["Btm"], PA["CTm"], PA["Btok"]
        sl = mslot[0] % 2
        mslot[0] += 1
        TS(Btm[:T, sl, :], Btok[:T, :], cfg.cmT[:, c:c + 1], 1.0, ALU.mult, ALU.mult, [Btok, cfg.cmT], [(Btm, sl)], eng="pool")
        TT(CTm[:, sl, :, :T], fT[:, 10:12, :T], bm(cfg.cmf[:, c, :], 2), ALU.mult, [fT, cfg.cmf], [(CTm, sl)], eng="pool")
        for g in range(2):
            MM(yo[g][:T, :], CTm[:, sl, g, :T], stb_in[:, g * 512:(g + 1) * 512], [(CTm, sl), stb_in], [yo[g]], start=first_c, stop=last_c)
        sp_ = [pget(), pget()]
        for g in range(2):
            MM(sp_[g][:, :], Btm[:T, sl, g * 128:(g + 1) * 128], xdec[:T, g * 512:(g + 1) * 512], [(Btm, sl), xdec], [sp_[g]])
        TT(big[3][:, :].rearrange("p (h q) -> p h q", q=64), st_in[:, :].rearrange("p (h q) -> p h q", q=64),
           bl(dbc[:, c * 16:(c + 1) * 16], 64), ALU.mult, [st_in, dbc], [big[3]])
        for g in range(2):
            TT(st_out[:, g * 512:(g + 1) * 512], big[3][:, g * 512:(g + 1) * 512], sp_[g][:, :], ALU.add, [big[3], sp_[g]], [st_out])

    def phaseA_tile(cfg, l, ti, xt, first, is_last):
        T, L, nseq, nch = cfg.T, cfg.L, cfg.nseq, cfg.nch
        stf, stb, hn, stf_s, stb_s, Gs, earg, Ee, WT, Btok = (PA[k] for k in ("stf", "stb", "hn", "stf_s", "stb_s", "Gs", "earg", "Ee", "WT", "Btok"))
        prologue(cfg, xt)
        if limit.get("stopA", 99) <= 0:
            return
        inproj_conv(cfg, l, OFF_XBC, 12, cbrow, I["st_sc"][l], O["ncs_p"], O["ncs_s"], 0, is_last, first)
        if ti == 0:
            DBG("fT_A", fT, fT[:, :, :T], [128, 12, T])
        if limit.get("stopA", 99) <= 1:
            return
        b = pget()
        for k in range(8):
            MM(b[:T, 0:16], hT[:, k, :T], PA["WB"][:, k, OFF_DT:OFF_DT + 16], [hT, PA["WB"]], [b], start=(k == 0), stop=(k == 7))
        A_, B_, C_, D_, E_ = (sm[k] for k in "abcde")
        TT(A_[:T, 0:16], b[:T, 0:16], vec16[:T, 0, :], ALU.add, [b, vec16], [A_])
        ACT(A_[:T, 0:16], A_[:T, 0:16], AF.Exp, [A_], [A_])
        ACT(A_[:T, 16:32], A_[:T, 0:16], AF.Ln, [A_], [A_], bias=1.0)
        TT(A_[:T, 32:48], A_[:T, 16:32], vec16[:T, 1, :], ALU.mult, [A_, vec16], [A_])
        cum_and_tot(cfg, A_[:T, 32:48], A_, 16, B_, dbc, 128)
        TT(C_[:T, 0:16], B_[:T, 16:32], B_[:T, 0:16], ALU.subtract, [B_], [C_])
        ACT(C_[:T, 0:16], C_[:T, 0:16], AF.Exp, [C_], [C_])
        ACT(C_[:T, 16:32], B_[:T, 0:16], AF.Exp, [B_], [C_])
        b = pget()
        TR(b[:16, 0:T], B_[:T, 0:16], identf[:T, :T], [B_, identf], [b])
        ACT(acT[:, :T], b[:16, 0:T], AF.Copy, [b], [acT])
        if ti == 0:
            DBG("acum", B_, B_[:T, 0:32], [T, 32])
        if limit.get("stopA", 99) <= 2:
            return
        bx, bb = pget(), pget()
        for c in range(8):
            TR(pb(bx, [8, 128], p=T)[:, c, :], fT[:, c, :T], identb[:, :], [fT, identb], [bx])
        for c in range(2):
            TR(pb(bb, [2, 128], p=T)[:, c, :], fT[:, 8 + c, :T], identb[:, :], [fT, identb], [bb])
        xc, xdec, xsD = bigb[0], bigb[1], big[0]
        TT(xc[:T, :].rearrange("p (h q) -> p h q", q=64), pb(bx, [16, 64], p=T), bl(A_[:T, 16:32], 64), ALU.mult, [bx, A_], [xc])
        TT(xsD[:T, :].rearrange("p (h q) -> p h q", q=64), pb(bx, [16, 64], p=T), bl(vec16[:T, 2, :], 64), ALU.mult, [bx, vec16], [xsD])
        TT(xdec[:T, :].rearrange("p (h q) -> p h q", q=64), xc[:T, :].rearrange("p (h q) -> p h q", q=64), bl(C_[:T, 0:16], 64), ALU.mult,
           [xc, C_], [xdec], eng="pool")
        ACT(Btok[:T, :], pb(bb, [256], p=T), AF.Copy, [bb], [Btok])
        b = pget()
        for g in range(2):
            MM(pf(b, [2, T], p=T)[:, g, :], fT[:, 8 + g, :T], fT[:, 10 + g, :T], [fT], [b])
        ACT(Gs[:T, :, :T], pf(b, [2, T], p=T), AF.Copy, [b], [Gs])
        for q in range(4):
            b = pget()
            for j in range(4):
                h = q * 4 + j
                MM(pf(b, [4, T], p=T)[:, j, :], identf[0:16, h:h + 1].to_broadcast([16, T]), acT[:, :T], [identf, acT], [b])
            for j in range(4):
                h = q * 4 + j
                STT(earg[:T, j, :T], pf(b, [4, T], p=T)[:, j, :], B_[:T, h:h + 1], cfg.nmI[:, :], ALU.subtract, ALU.add,
                    [b, B_, cfg.nmI], [earg])
            ACT(Ee[:T, :, :T], earg[:T, :, :T], AF.Exp, [earg], [Ee])
            TT(WT[:T, q * 4:q * 4 + 4, :T], Ee[:T, :, :T], bm(Gs[:T, q // 2, :T], 4), ALU.mult, [Ee, Gs], [WT])
        if limit.get("stopA", 99) <= 3:
            return
        yp = [pget(), pget()]
        for h in range(16):
            MM(yp[h // 8][:T, (h % 8) * 64:(h % 8 + 1) * 64], WT[:T, h, :T], xc[:T, h * 64:(h + 1) * 64], [WT, xc], [yp[h // 8]])
        ppin(yp[0]); ppin(yp[1])
        yo = [pget(), pget()]
        ppin(yo[0]); ppin(yo[1])
        if cfg.chain:
            if first:
                MSET(stf[:, :], 0.0, [stf])
                MSET(stb[:, :], 0.0, [stb])
            for c in range(nch):
                ssd_state_update(cfg, c, stf, stb, stf, yo, xdec, c == 0, c == nch - 1)
                ACT(stb[:, :], stf[:, :], AF.Copy, [stf], [stb])
            if is_last:
                for half in range(2):
                    b = pget()
                    for k in range(4):
                        kk = half * 4 + k
                        TR(pf(b, [4, 128])[:, k, :], stf[:, :].rearrange("n (j k) -> n k j", k=8)[:, kk, :], identf[:, :], [stf, identf], [b])
                    CP_(hn[:, half * 4:half * 4 + 4, :], pf(b, [4, 128]), [b], [hn])
                STO(O["nssm_p"][l].rearrange("(p k) n -> p k n", k=8), hn[:], hn)
        else:
            for c in range(limit.get("nchs", nch)):
                LD(hn[:], I["st_ssm"][l, c].rearrange("h q n -> (h q) n").rearrange("(p k) n -> p k n", k=8), hn)
                if limit.get("sub", 9) <= 0:
                    continue
                for half in range(2):
                    b = pget()
                    for k in range(4):
                        kk = half * 4 + k
                        TR(pf(b, [4, 128])[:, k, :], hn[:, kk, :], identf[:, :], [hn, identf], [b])
                    if limit.get("sub", 9) <= 1:
                        continue
                    ACT(stf_s[:, :].rearrange("n (j k) -> n k j", k=8)[:, half * 4:half * 4 + 4, :], pf(b, [4, 128]), AF.Copy, [b], [stf_s])
                    if limit.get("sub", 9) <= 2:
                        continue
                    CP_(stb_s[:, :].rearrange("n (j k) -> n k j", k=8)[:, half * 4:half * 4 + 4, :], pf(b, [4, 128]), [b], [stb_s])
                if limit.get("noupd"):
                    continue
                ssd_state_update(cfg, c, stf_s, stb_s, stf_s, yo, xdec, c == 0, c == limit.get("nchs", nch) - 1)
                if limit.get("noback"):
                    continue
                for half in range(2):
                    b = pget()
                    for k in range(4):
                        kk = half * 4 + k
                        TR(pf(b, [4, 128])[:, k, :], stf_s[:, :].rearrange("n (j k) -> n k j", k=8)[:, kk, :], identf[:, :], [stf_s, identf], [b])
                    CP_(hn[:, half * 4:half * 4 + 4, :], pf(b, [4, 128]), [b], [hn])
                STO(O["nssm_s"][l, c].rearrange("(p k) n -> p k n", k=8), hn[:], hn)
        if limit.get("stopA", 99) <= 4:
            punpin(yp[0]); punpin(yp[1]); punpin(yo[0]); punpin(yo[1])
            return
        t1, t2 = big[1], big[2]
        for g in range(2):
            TT(t1[:T, g * 512:(g + 1) * 512].rearrange("p (h q) -> p h q", q=64), yo[g][:T, :].rearrange("p (h q) -> p h q", q=64),
               bl(C_[:T, 16 + g * 8:16 + g * 8 + 8], 64), ALU.mult, [yo[g], C_], [t1])
            TT(t2[:T, g * 512:(g + 1) * 512], yp[g][:T, :], t1[:T, g * 512:(g + 1) * 512], ALU.add, [yp[g], t1], [t2])
        for x_ in yp + yo:
            punpin(x_)
        TT(t2[:T, :], t2[:T, :], xsD[:T, :], ALU.add, [t2, xsD], [t2], eng="pool")
        if ti == 0:
            DBG("yssd", t2, t2[:T, :], [T, D])
        for g in range(2):
            b = pget()
            for k in range(8):
                MM(b[:T, :], hT[:, k, :T], PA["WB"][:, k, OFF_ZS + g * 512:OFF_ZS + (g + 1) * 512], [hT, PA["WB"]], [b], start=(k == 0), stop=(k == 7))
            ACT(t1[:T, g * 512:(g + 1) * 512], b[:T, :], AF.Silu, [b], [t1])
        TT(t2[:T, :], t2[:T, :], t1[:T, :], ALU.mult, [t2, t1], [t2])
        for g in range(2):
            ACT(junk[:T, g * 512:(g + 1) * 512], t2[:T, g * 512:(g + 1) * 512], AF.Square, [t2], [junk, sm1], accum_out=sm1[:T, 2 + g:3 + g])
        TS(sm1[:T, 2:4], sm1[:T, 2:4], 1.0 / 512, EPS, ALU.mult, ALU.add, [sm1], [sm1])
        ACT(sm1[:T, 2:4], sm1[:T, 2:4], AF.Ln, [sm1], [sm1])
        ACT(sm1[:T, 2:4], sm1[:T, 2:4], AF.Exp, [sm1], [sm1], scale=-0.5)
        ynb = bigb[2]
        for g in range(2):
            ACT(ynb[:T, g * 512:(g + 1) * 512], t2[:T, g * 512:(g + 1) * 512], AF.Copy, [t2, sm1], [ynb], scale=sm1[:T, 2 + g:3 + g])
        out_proj(cfg, ynb, 8)

    def out_proj(cfg, ynb, nk):
        T = cfg.T
        b = pget()
        for c in range(nk):
            TR(pb(b, [8, T])[:, c, :], ynb[:T, c * 128:(c + 1) * 128], identb[:T, :T], [ynb, identb], [b])
        CP_(hT[:, 0:nk, :T], pb(b, [8, T])[:, 0:nk, :], [b], [hT])
        for half in range(2):
            b = pget()
            for k in range(nk):
                MM(b[:T, :], hT[:, k, :T], WO[:, k, half * 512:(half + 1) * 512], [hT, WO], [b], start=(k == 0), stop=(k == nk - 1))
            TT(big[3][:T, half * 512:(half + 1) * 512], b[:T, :], gate_bc[cfg.name][:T, half * 512:(half + 1) * 512], ALU.mult,
               [b, gate_bc[cfg.name]], [big[3]])

    PB = {}

    def alloc_B(pes, tag):
        def sbp(name, shape, dt=F32):
            return Buf(pes.enter_context(nc.sbuf_tensor(name + tag, list(shape), dt)), name)
        PB.update(Sp_f=sbp("Sp_f", [128, 4, 128]), Sp_b=sbp("Sp_b", [128, 4, 128], BF16), Ss_in=sbp("Ss_in", [128, 16, 128]),
                  Ss_b=sbp("Ss_b", [128, 16, 128], BF16), kv=sbp("kv", [128, 8, 128], BF16), kbg=sbp("kbg", [128, 4, 128]),
                  vb_=sbp("vb", [128, 4, 128]), kdec=sbp("kdec", [128, 4, 128], BF16), kdm=sbp("kdm", [64, 2, 128], BF16),
                  LG=sbp("LG", [128, 12]), LGT=sbp("LGT", [12, 128]), ea=sbp("ea", [128, 4, 128]),
                  E2a=sbp("E2a", [128, 4, 128]), E2b=sbp("E2b", [128, 4, 128]),
                  X0=sbp("X0", [128, 4, 128]), X1=sbp("X1", [128, 4, 128]), Y0=sbp("Y0", [128, 4, 128]), Y1=sbp("Y1", [128, 4, 128]),
                  R0=sbp("R0", [128, 4, 128]), R1=sbp("R1", [128, 4, 128]), attnT=sbp("attnT", [128, 4, 128], BF16),
                  egb=sbp("egb", [128, 4, 128]), qsq=sbp("qsq", [128, 4, 128], BF16),
                  nwTm=sbp("nwTm", [128, 1024], BF16), qdm=sbp("qdm", [128, 1024], BF16), u_sb=sbp("u_sb", [128, 4, 128]),
                  vnew=sbp("vnew", [128, 4, 128], BF16), dbcS=sbp("dbcS", [128, 64]), WB=sbp("WB", [128, 8, 2056], BF16))
        PB["E2"] = None

    def r32(ap):
        return ap.bitcast(F32R)

    def phaseB_tile(cfg, l, ti, xt, h0, first, is_last):
        T, L, nseq, nch = cfg.T, cfg.L, cfg.nseq, cfg.nch
        (Sp_f, Sp_b, Ss_in, Ss_b, kv, kbg, vb_, kdec, kdm, LG, LGT, ea, attnT, egb, nwTm, qdm, u_sb, vnew, dbcS) = (
            PB[k] for k in ("Sp_f", "Sp_b", "Ss_in", "Ss_b", "kv", "kbg", "vb_", "kdec", "kdm", "LG", "LGT", "ea", "attnT",
                            "egb", "nwTm", "qdm", "u_sb", "vnew", "dbcS"))
        E2 = [PB["E2a"], PB["E2b"]]
        Ss_out = Ss_in
        Xs, Ys, Rs = [PB["X0"], PB["X1"]], [PB["Y0"], PB["Y1"]], [PB["R0"], PB["R1"]]
        WB = PB["WB"]
        prologue(cfg, xt)
        inproj_conv_B(cfg, l, h0, is_last, first)
        A_, B_, C_, D_, E_ = (sm[k] for k in "abcde")
        b = pget()
        for k in range(8):
            MM(b[:T, 0:8], hT[:, k, :T], WB[:, k, 2048:2056], [hT, WB], [b], start=(k == 0), stop=(k == 7))
        TT(A_[:T, 0:4], b[:T, 0:4], vec8[:T, 0, h0:h0 + 4], ALU.add, [b, vec8], [A_])
        ACT(A_[:T, 0:4], A_[:T, 0:4], AF.Exp, [A_], [A_])
        ACT(A_[:T, 0:4], A_[:T, 0:4], AF.Ln, [A_], [A_], bias=1.0)
        TT(A_[:T, 4:8], A_[:T, 0:4], vec8[:T, 1, h0:h0 + 4], ALU.mult, [A_, vec8], [A_])
        ACT(A_[:T, 8:12], b[:T, 4:8], AF.Exp, [b], [A_], scale=-1.0)
        ACT(A_[:T, 8:12], A_[:T, 8:12], AF.Ln, [A_], [A_], bias=1.0)
        ACT(A_[:T, 12:16], A_[:T, 8:12], AF.Exp, [A_], [A_], scale=-1.0)
        ACT(A_[:T, 16:20], A_[:T, 8:12], AF.Copy, [A_], [A_], scale=-1.0)
        cum_and_tot(cfg, A_[:T, 4:8], A_, 4, B_, dbcS, 128)
        ACT(C_[:T, 0:4], B_[:T, 0:4], AF.Exp, [B_], [C_])
        TT(C_[:T, 4:8], B_[:T, 4:8], B_[:T, 0:4], ALU.subtract, [B_], [C_])
        ACT(C_[:T, 4:8], C_[:T, 4:8], AF.Exp, [C_], [C_])
        b = pget()
        for c in range(8):
            TR(pb(b, [8, 128], p=T)[:, c, :], fT[:, 4 + c, :T], identb[:, :], [fT, identb], [b])
        ACT(kv[:T, :, :], pb(b, [8, 128], p=T), AF.Copy, [b], [kv])
        ksq = big[0]
        TT(ksq[:T, 0:512].rearrange("p (h d) -> p h d", d=128), kv[:T, 0:4, :], kv[:T, 0:4, :], ALU.mult, [kv], [ksq])
        RED(D_[:T, 0:4], ksq[:T, 0:512].rearrange("p (h d) -> p h d", d=128), [ksq], [D_])
        TS(D_[:T, 0:4], D_[:T, 0:4], EPS, None, ALU.add, ALU.bypass, [D_], [D_])
        ACT(D_[:T, 4:8], D_[:T, 0:4], AF.Ln, [D_], [D_])
        ACT(D_[:T, 8:12], D_[:T, 4:8], AF.Exp, [D_], [D_], scale=-0.5)
        qsq = PB["qsq"]
        ACT(qsq[:, :, :T], fT[:, 0:4, :T], AF.Square, [fT], [qsq])
        b = pget()
        for j in range(4):
            MM(b[:T, j:j + 1], qsq[:, j, :T], onesb[:, 0:1], [qsq, onesb], [b])
        TS(D_[:T, 12:16], b[:T, 0:4], EPS, None, ALU.add, ALU.bypass, [b], [D_])
        ACT(D_[:T, 12:16], D_[:T, 12:16], AF.Ln, [D_], [D_])
        ACT(D_[:T, 16:20], D_[:T, 12:16], AF.Exp, [D_], [D_], scale=-0.5)
        TS(D_[:T, 16:20], D_[:T, 16:20], float(128 ** -0.5), None, ALU.mult, ALU.bypass, [D_], [D_])
        TT(E_[:T, 0:4], D_[:T, 8:12], A_[:T, 12:16], ALU.mult, [D_, A_], [E_])
        TT(E_[:T, 4:8], E_[:T, 0:4], C_[:T, 0:4], ALU.mult, [E_, C_], [E_])
        TT(E_[:T, 8:12], D_[:T, 8:12], C_[:T, 4:8], ALU.mult, [D_, C_], [E_])
        TS(E_[:T, 12:16], E_[:T, 0:4], -1.0, None, ALU.mult, ALU.bypass, [E_], [E_])
        TS(E_[:T, 16:20], D_[:T, 8:12], -1.0, None, ALU.mult, ALU.bypass, [D_], [E_])
        CP_(LG[:T, 0:4], B_[:T, 0:4], [B_], [LG])
        STT(LG[:T, 4:8], D_[:T, 4:8], -0.5, B_[:T, 0:4], ALU.mult, ALU.subtract, [D_, B_], [LG])
        STT(LG[:T, 8:12], D_[:T, 4:8], -0.5, B_[:T, 0:4], ALU.mult, ALU.add, [D_, B_], [LG])
        TT(LG[:T, 8:12], LG[:T, 8:12], A_[:T, 16:20], ALU.add, [LG, A_], [LG])
        b = pget()
        TR(b[:12, 0:T], LG[:T, 0:12], identf[:T, :T], [LG, identf], [b])
        ACT(LGT[:, :T], b[:12, 0:T], AF.Copy, [b], [LGT])
        TT(r32(kbg[:T, :, :]), kv[:T, 0:4, :], bl(E_[:T, 4:8], 128), ALU.mult, [kv, E_], [kbg])
        TT(kdec[:T, :, :], kv[:T, 0:4, :], bl(E_[:T, 8:12], 128), ALU.mult, [kv, E_], [kdec])
        TT(r32(vb_[:T, :, :]), kv[:T, 4:8, :], bl(A_[:T, 12:16], 128), ALU.mult, [kv, A_], [vb_])
        bG, bA = pget(), pget()
        for j in range(4):
            MM(pf(bG, [4, T], p=T)[:, j, :], fT[:, 4 + j, :T], fT[:, 4 + j, :T], [fT], [bG])
        for j in range(4):
            MM(pf(bA, [4, T], p=T)[:, j, :], fT[:, 4 + j, :T], fT[:, j, :T], [fT], [bA])
        bcs = [pget(), pget(), pget()]
        for r in range(3):
            for j in range(4):
                MM(pf(bcs[r], [4, T], p=T)[:, j, :], identf[0:12, r * 4 + j:r * 4 + j + 1].to_broadcast([12, T]), LGT[:, :T],
                   [identf, LGT], [bcs[r]])
        be = pget()
        for j in range(4):
            MM(pf(be, [4, T])[:, j, :], identf[0:12, j:j + 1].to_broadcast([12, 128]), LGT[:, :T], [identf, LGT], [be])
        X, Y, R = Xs[0], Ys[0], Rs[0]
        kinds = ((1, ALU.add, cfg.nmSr, 0), (2, ALU.subtract, cfg.nmS, 1), (0, ALU.subtract, cfg.nmI, 2))
        for r, op0, msk, ki in kinds:
            for j in range(4):
                STT(ea[:T, j, :T], pf(bcs[r], [4, T], p=T)[:, j, :], B_[:T, j:j + 1], msk[:, :], op0, ALU.add, [bcs[r], B_, msk], [ea])
            Ek = E2[ki % 2]
            ACT(Ek[:T, :, :T], ea[:T, :, :T], AF.Exp, [ea], [Ek])
            for j in range(4):
                if ki == 0:
                    STT(r32(X[:T, j, :T]), pf(bG, [4, T], p=T)[:, j, :], E_[:T, 12 + j:13 + j], Ek[:T, j, :T], ALU.mult, ALU.mult, [bG, E_, Ek], [X])
                elif ki == 1:
                    STT(r32(Y[:T, j, :T]), pf(bG, [4, T], p=T)[:, j, :], E_[:T, 16 + j:17 + j], Ek[:T, j, :T], ALU.mult, ALU.mult, [bG, E_, Ek], [Y])
                else:
                    STT(attnT[:T, j, :T], pf(bA, [4, T], p=T)[:, j, :], D_[:T, 8 + j:9 + j], Ek[:T, j, :T], ALU.mult, ALU.mult, [bA, D_, Ek], [attnT])
        ACT(egb[:, :, :T], pf(be, [4, T]), AF.Exp, [be], [egb])
        TT(r32(R[:T, :, :T]), Y[:T, :, :T], bm(identf[:T, :T], 4), ALU.add, [Y, identf], [R])
        for lev in range(cfg.nlev):
            X2, Y2, R2 = Xs[(lev + 1) % 2], Ys[(lev + 1) % 2], Rs[(lev + 1) % 2]
            lastlev = (lev == cfg.nlev - 1)
            bX, bY, bR = pget(), (None if lastlev else pget()), pget()
            for j in range(4):
                MM(pf(bX, [4, T], p=T)[:, j, :], r32(Y[:T, j, :T]), r32(X[:T, j, :T]), [X, Y], [bX])
            if not lastlev:
                for j in range(4):
                    MM(pf(bY, [4, T], p=T)[:, j, :], r32(X[:T, j, :T]), r32(Y[:T, j, :T]), [X, Y], [bY])
            ACT(r32(X2[:T, :, :T]), pf(bX, [4, T], p=T), AF.Copy, [bX], [X2])
            if not lastlev:
                CP_(r32(Y2[:T, :, :T]), pf(bY, [4, T], p=T), [bY], [Y2])
            for j in range(4):
                MM(pf(bR, [4, T], p=T)[:, j, :], r32(X2[:T, j, :T]), r32(R[:T, j, :T]), [X2, R], [bR])
            TT(r32(R2[:T, :, :T]), R[:T, :, :T], pf(bR, [4, T], p=T), ALU.add, [R, bR], [R2])
            X, Y, R = X2, Y2, R2
        bU, bW = pget(), pget()
        for j in range(4):
            MM(pf(bU, [4, 128], p=T)[:, j, :], r32(R[:T, j, :T]), r32(vb_[:T, j, :]), [R, vb_], [bU])
        for j in range(4):
            MM(pf(bW, [4, T])[:, j, :], r32(kbg[:T, j, :]), r32(R[:T, j, :T]), [kbg, R], [bW])
        ACT(u_sb[:T, :, :], pf(bU, [4, 128], p=T), AF.Copy, [bU], [u_sb])
        TT(egb[:, :, :T], fT[:, 0:4, :T], egb[:, :, :T], ALU.mult, [fT, egb], [egb])
        vnb, ob = pget(), pget()
        ppin(vnb)
        ppin(ob)
        vn4, o4 = pf(vnb, [4, 128], p=T), pf(ob, [4, 128], p=T)
        if cfg.chain:
            if first:
                MSET(Sp_f[:], 0.0, [Sp_f])
                MSET(Sp_b[:], 0.0, [Sp_b])
            nwv = nwTm[:, 0:nch * 4 * T].rearrange("p (c j t) -> p c j t", c=nch, j=4)
            qdv = qdm[:, 0:nch * 4 * T].rearrange("p (c j t) -> p c j t", c=nch, j=4)
            for c in range(nch):
                TT(nwv[:, c], pf(bW, [4, T]), bm(cfg.ncmf[:, c, :], 4), ALU.mult, [bW, cfg.ncmf], [(nwTm, c)])
                TT(qdv[:, c], egb[:, :, :T], bm(cfg.cmf[:, c, :], 4), ALU.mult, [egb, cfg.cmf], [(qdm, c)], eng="pool")
            for c in range(nch):
                for j in range(4):
                    MM(vn4[:, j, :], nwv[:, c, j, :], Sp_b[:, j, :], [(nwTm, c), Sp_b], [vnb], start=(j == 0 and c == 0), stop=False, skip_group_check=True)
                for j in range(4):
                    MM(o4[:, j, :], qdv[:, c, j, :], Sp_b[:, j, :], [(qdm, c), Sp_b], [ob], start=(j == 0 and c == 0), stop=False, skip_group_check=True)
                TT(vnew[:T, :, :], vn4, u_sb[:T, :, :], ALU.add, [vnb, u_sb], [vnew])
                bs = pget()
                for j in range(4):
                    MM(pf(bs, [4, 128])[:, j, :], kdec[c * 64:(c + 1) * 64, j, :], vnew[c * 64:(c + 1) * 64, j, :], [kdec, vnew], [bs])
                tmpS = big[0][:, 512:1024].rearrange("p (j v) -> p j v", v=128)
                TT(tmpS, Sp_f[:, :, :], bl(dbcS[:, c * 4:(c + 1) * 4], 128), ALU.mult, [Sp_f, dbcS], [big[0]])
                TT(Sp_f[:, :, :], tmpS, pf(bs, [4, 128]), ALU.add, [big[0], bs], [Sp_f])
                ACT(Sp_b[:, :, :], Sp_f[:, :, :], AF.Copy, [Sp_f], [Sp_b])
            if is_last:
                STO(O["ngdn_p"][l, h0:h0 + 4].rearrange("h k v -> k h v"), Sp_f[:], Sp_f)
        else:
            nwv = nwTm[:, 0:nch * T].rearrange("p (c t) -> p c t", t=T)
            qdv = qdm[:, 0:nch * T].rearrange("p (c t) -> p c t", t=T)
            ppin(bW)
            for j in range(4):
                h = h0 + j
                LD(Ss_in[:], I["st_gdn"][l, :, h].rearrange("b k v -> k b v"), Ss_in)
                CP_(Ss_b[:], Ss_in[:], [Ss_in], [Ss_b], eng="pool")
                TT(nwv, bm(pf(bW, [4, T])[:, j, :], nch), cfg.ncmf[:, :, :], ALU.mult, [bW, cfg.ncmf], [nwTm])
                TT(qdv, bm(egb[:, j, :T], nch), cfg.cmf[:, :, :], ALU.mult, [egb, cfg.cmf], [qdm], eng="pool")
                for c in range(nch):
                    MM(vn4[:, j, :], nwv[:, c, :], Ss_b[:, c, :], [nwTm, Ss_b], [vnb], start=(j == 0 and c == 0), stop=False, skip_group_check=True)
                for c in range(nch):
                    MM(o4[:, j, :], qdv[:, c, :], Ss_b[:, c, :], [qdm, Ss_b], [ob], start=(j == 0 and c == 0), stop=False, skip_group_check=True)
                TT(vnew[:T, j, :], vn4[:, j, :], u_sb[:T, j, :], ALU.add, [vnb, u_sb], [vnew])
                tmpS = big[2][:, :].rearrange("p (c v) -> p c v", v=128)
                for hf in range(2):
                    TT(tmpS, Ss_in[:, hf * 8:(hf + 1) * 8, :],
                       bl(dbcS[:, 0:nch * 4].rearrange("p (c j) -> p c j", j=4)[:, hf * 8:(hf + 1) * 8, j], 128), ALU.mult, [Ss_in, dbcS], [big[2]])
                    for c4 in range(hf * 8, hf * 8 + 8, 4):
                        bs = pget()
                        for cc in range(4):
                            c = c4 + cc
                            sl = mslot[0] % 2
                            mslot[0] += 1
                            TS(kdm[:T, sl, :], kdec[:T, j, :], cfg.cmT[:, c:c + 1], 1.0, ALU.mult, ALU.mult, [kdec, cfg.cmT], [(kdm, sl)], eng="pool")
                            MM(pf(bs, [4, 128])[:, cc, :], kdm[:T, sl, :], vnew[:T, j, :], [(kdm, sl), vnew], [bs])
                        TT(Ss_out[:, c4:c4 + 4, :], tmpS[:, c4 - hf * 8:c4 - hf * 8 + 4, :], pf(bs, [4, 128]), ALU.add, [big[2], bs], [Ss_out])
                STO(O["ngdn_s"][l, :, h].rearrange("b k v -> k b v"), Ss_out[:], Ss_out)
            punpin(bW)
        for j in range(4):
            MM(o4[:, j, :], attnT[:T, j, :T], vnew[:T, j, :], [attnT, vnew], [ob], start=False, stop=True, skip_group_check=True)
        punpin(vnb)
        of = big[1]
        ACT(of[:T, 0:512], ob[:T, :], AF.Copy, [ob], [of])
        punpin(ob)
        if ti == 0 and h0 == 0:
            DBG("ogdn", of, of[:T, 0:512], [T, 512])
        TT(ksq[:T, 0:512], of[:T, 0:512], of[:T, 0:512], ALU.mult, [of], [ksq])
        RED(C_[:T, 8:12], ksq[:T, 0:512].rearrange("p (h d) -> p h d", d=128), [ksq], [C_])
        TT(C_[:T, 12:16], D_[:T, 16:20], D_[:T, 16:20], ALU.mult, [D_], [C_])
        TT(C_[:T, 8:12], C_[:T, 8:12], C_[:T, 12:16], ALU.mult, [C_], [C_])
        TS(C_[:T, 8:12], C_[:T, 8:12], 1.0 / 128, EPS, ALU.mult, ALU.add, [C_], [C_])
        ACT(C_[:T, 8:12], C_[:T, 8:12], AF.Ln, [C_], [C_])
        ACT(C_[:T, 8:12], C_[:T, 8:12], AF.Exp, [C_], [C_], scale=-0.5)
        TT(C_[:T, 8:12], C_[:T, 8:12], D_[:T, 16:20], ALU.mult, [C_, D_], [C_])
        b = pget()
        for k in range(8):
            MM(b[:T, :], hT[:, k, :T], WB[:, k, 0:512], [hT, WB], [b], start=(k == 0), stop=(k == 7))
        sz = big[2]
        ACT(sz[:T, 0:512], b[:T, :], AF.Silu, [b], [sz])
        TT(of[:T, 0:512].rearrange("p (h d) -> p h d", d=128), of[:T, 0:512].rearrange("p (h d) -> p h d", d=128), bl(C_[:T, 8:12], 128),
           ALU.mult, [of, C_], [of])
        onb = bigb[2]
        TT(onb[:T, 0:512], of[:T, 0:512], sz[:T, 0:512], ALU.mult, [of, sz], [onb])
        out_proj(cfg, onb, 4)

    def inproj_conv_B(cfg, l, h0, is_last, first):
        T, L, nseq = cfg.T, cfg.L, cfg.nseq
        rawv = raw[:, :, 0:nseq * (L + 3)].rearrange("p c (b l) -> p c b l", l=L + 3)
        if nseq == 1:
            if first:
                MSET(raw[:, :, 0:3], 0.0, [raw])
            else:
                CP_(raw[:, :, 0:3], raw[:, :, L:L + 3], [raw], [raw])
        else:
            for part in range(3):
                cs0 = part * 1024 + h0 * 128
                LD(cst_in[:, part * 512:(part + 1) * 512], I["st_gc"][l][:, :, cs0:cs0 + 512].rearrange("b k c -> (b k) c"), cst_in)
            for g0 in (0, 8):
                b = pget()
                ng = min(8, 12 - g0)
                for c in range(ng):
                    TR(pf(b, [8, 48])[:, c, :], cst_in[:, (g0 + c) * 128:(g0 + c + 1) * 128], identf[:48, :48], [cst_in, identf], [b])
                CP_(rawv[:, g0:g0 + ng, :, 0:3], pf(b, [8, 48])[:, 0:ng, :].rearrange("p c (b k) -> p c b k", k=3), [b], [raw])
        for g0 in range(0, 12, 4):
            b = pget()
            for c in range(4):
                for k in range(8):
                    MM(pf(b, [4, T])[:, c, :], PB["WB"][:, k, 512 + (g0 + c) * 128: 512 + (g0 + c + 1) * 128], hT[:, k, :T],
                       [PB["WB"], hT], [b], start=(k == 0), stop=(k == 7))
            if nseq == 1:
                ACT(raw[:, g0:g0 + 4, 3:3 + L], pf(b, [4, T]), AF.Copy, [b], [raw])
            else:
                for c in range(4):
                    ACT(rawv[:, g0 + c, :, 3:3 + L], pf(b, [4, T])[:, c, :].rearrange("p (b l) -> p b l", l=L), AF.Copy, [b], [raw])
            if is_last and limit.get("tail", True):
                for c in range(4):
                    ACT(tailf[:, g0 + c, 0:nseq * 3].rearrange("p (b k) -> p b k", k=3),
                        pf(b, [4, T])[:, c, :].rearrange("p (b l) -> p b l", l=L)[:, :, L - 3:L], AF.Copy, [b], [tailf])
        for g0 in range(0, 12, 4):
            b = pget()
            for c in range(4):
                o = pf(b, [4, T])[:, c, :]
                if nseq > 1:
                    o = o.rearrange("p (b l) -> p b l", l=L)
                for k in range(4):
                    MM(o, DIAG[:, g0 + c, k, :], rawv[:, g0 + c, :, k:k + L] if nseq > 1 else raw[:, g0 + c, k:k + L],
                       [DIAG, raw], [b], start=(k == 0), stop=(k == 3))
            ACT(fT[:, g0:g0 + 4, :T], pf(b, [4, T]), AF.Silu, [b], [fT])
        if is_last:
            n3 = nseq * 3
            for g0 in range(0, 12, 4):
                b = pget()
                for c in range(4):
                    TR(pf(b, [4, 128], p=n3)[:, c, :], tailf[:, g0 + c, 0:n3], identf[:, :], [tailf, identf], [b])
                CP_(tailo[:n3, g0 * 128:(g0 + 4) * 128], b[:n3, :], [b], [tailo])
            dst = O["ngc_p"] if nseq == 1 else O["ngc_s"]
            for part in range(3):
                cs0 = part * 1024 + h0 * 128
                STO(dst[l][:, cs0:cs0 + 512], tailo[:n3, part * 512:(part + 1) * 512], tailo)

    def bvec(dst_ap, src_row_ap, buf):
        LD(dst_ap, src_row_ap.partition_broadcast(128), buf)

    tiles = [(CP, ti) for ti in range(limit.get("ntiles", NT))] + ([(CS, NT)] if limit.get("sample", True) else [])
    all_bufs.extend(allb)
    layer_in = None
    for l in range(limit.get("layers", DEPTH)):
        do_mod(l)
        S.barrier(all_bufs)
        mod_stacks.pop().close()
        for ph in limit.get("phases", (0, 1, 2)):
            acc_in = layer_in if ph == 0 else scr[(ph - 1)]
            last_phase = (l == DEPTH - 1 and ph == 2)
            acc_out = scr[ph] if ph < 2 else (scr[2] if not last_phase else None)
            S.barrier(all_bufs)
            pes = contextlib.ExitStack()
            cur_es[0] = pes
            if ph == 0:
                alloc_A(pes, "_%d_%d" % (l, ph))
            else:
                alloc_B(pes, "_%d_%d" % (l, ph))
            for v_ in list(PA.values()) + list(PB.values()):
                if v_ is not None and v_ not in all_bufs:
                    all_bufs.append(v_)
            if ph == 0:
                load_cols(PA["WB"], 0, I["w_in"][l][:, 0:2576], 2576)
                LD(nwT[:], I["ssd_norm_w"][l].rearrange("(k p) -> p k", p=128), nwT, nonc=True)
                load_wo(I["w_out"][l][0:1024, :], 8, lambda k: nwT[:, k:k + 1])
                build_diag(I["ssd_conv_w"][l], 0, 12)
                LD(cbrow_f[0:1, :], I["ssd_conv_b"][l:l + 1, :], cbrow_f)
                CP_(cbrow[:], cbrow_f[0:1, :], [cbrow_f], [cbrow])
                bvec(vec16[:, 0, :], I["ssd_dt_bias"][l:l + 1, :], vec16)
                bvec(vec16[:, 1, :], I["ssd_a_log"][l:l + 1, :], vec16)
                bvec(vec16[:, 2, :], I["ssd_d"][l:l + 1, :], vec16)
                ACT(vec16[:, 1, :], vec16[:, 1, :], AF.Exp, [vec16], [vec16])
                TS(vec16[:, 1, :], vec16[:, 1, :], -1.0, None, ALU.mult, ALU.bypass, [vec16], [vec16])
            else:
                h0 = (ph - 1) * 4
                load_cols(PB["WB"], 0, I["w_in"][l][:, OFF_ZG + h0 * 128:OFF_ZG + h0 * 128 + 512], 512)
                for part in range(3):
                    c0 = OFF_QKV + part * 1024 + h0 * 128
                    load_cols(PB["WB"], 512 + part * 512, I["w_in"][l][:, c0:c0 + 512], 512)
                load_cols(PB["WB"], 2048, I["w_in"][l][:, OFF_A + h0:OFF_A + h0 + 4], 4)
                load_cols(PB["WB"], 2052, I["w_in"][l][:, OFF_B + h0:OFF_B + h0 + 4], 4)
                LD(nwT[:, 0:1], I["gdn_norm_w"][l].rearrange("(p o) -> p o", o=1), nwT, nonc=True)
                load_wo(I["w_out"][l][1024 + h0 * 128:1024 + h0 * 128 + 512, :], 4, lambda k: nwT[:, 0:1])
                for part in range(3):
                    for k_ in range(4):
                        LD(cwT[:, part * 4:part * 4 + 4, k_],
                           I["gdn_conv_w"][l][k_, part * 1024 + h0 * 128:part * 1024 + h0 * 128 + 512].rearrange("(c p) -> p c", p=128), cwT, nonc=True)
                for c in range(12):
                    TT(DIAG[:, c, :, :], bm(identf[:], 4), bl(cwT[:, c, :], 128), ALU.mult, [identf, cwT], [DIAG], eng="pool")
                bvec(vec8[:, 0, :], I["gdn_dt_bias"][l:l + 1, :], vec8)
                bvec(vec8[:, 1, :], I["gdn_a_log"][l:l + 1, :], vec8)
                ACT(vec8[:, 1, :], vec8[:, 1, :], AF.Exp, [vec8], [vec8])
                TS(vec8[:, 1, :], vec8[:, 1, :], -1.0, None, ALU.mult, ALU.bypass, [vec8], [vec8])

            def issue_loads(idx):
                cfg, ti = tiles[idx]
                r0, T = rows(ti)
                xt = xt_slots[idx % 2]
                ap, dep = src_ap(layer_in.t if layer_in is not None else None, ti)
                LD(xt[:T, :], ap, xt, R=[(layer_in, ti)] if layer_in is not None else None)
                if ph > 0:
                    xa = xa_slots[idx % 2]
                    LD(xa[:T, :], acc_in.t[r0:r0 + T, :], xa, R=[(acc_in, ti)])

            if last_phase:
                fnw_bc = stage[0]
                fnw_v = stage[0][:, :, :].rearrange("p a b -> p (a b)")
                LD(fnw_v, I["final_norm_w"].partition_broadcast(128), stage[0])
            issue_loads(0)
            for idx, (cfg, ti) in enumerate(tiles):
                if idx + 1 < len(tiles):
                    issue_loads(idx + 1)
                r0, T = rows(ti)
                xt = xt_slots[idx % 2]
                xa = xa_slots[idx % 2] if ph > 0 else xt
                first = (ti == 0)
                is_last = (ti == NT - 1) or (ti == NT)
                if ph == 0:
                    phaseA_tile(cfg, l, ti, xt, first, is_last)
                else:
                    phaseB_tile(cfg, l, ti, xt, (ph - 1) * 4, first, is_last)
                TT(xo[:T, :], big[3][:T, :], xa[:T, :], ALU.add, [big[3], xa], [xo], eng="pool")
                if not last_phase:
                    STO(acc_out.t[r0:r0 + T, :], xo[:T, :], xo, W=[(acc_out, ti)])
                else:
                    ACT(junk[:T, :], xo[:T, :], AF.Square, [xo], [junk, sm1], accum_out=sm1[:T, 4:5])
                    TS(sm1[:T, 4:5], sm1[:T, 4:5], 1.0 / D, EPS, ALU.mult, ALU.add, [sm1], [sm1])
                    ACT(sm1[:T, 4:5], sm1[:T, 4:5], AF.Ln, [sm1], [sm1])
                    ACT(sm1[:T, 4:5], sm1[:T, 4:5], AF.Exp, [sm1], [sm1], scale=-0.5)
                    STT(big[0][:T, :], xo[:T, :], sm1[:T, 4:5], fnw_v[:T, :], ALU.mult, ALU.mult, [xo, sm1, fnw_bc], [big[0]])
                    dsto = O["y_p"][r0:r0 + T, :] if ti < NT else O["y_s"][:, :]
                    STO(dsto, big[0][:T, :], big[0])
            pes.close()
            cur_es[0] = es
            PA.clear()
            PB.clear()
        layer_in = scr[2]
    S.emit()
    es.close()
    return nc, dbg_out


_CACHE = {}


def make_in_maps(inputs):
    g = {k: np.ascontiguousarray(np.asarray(v, dtype=np.float32)) for k, v in inputs.items()}
    consts = host_consts()
    maps = []
    for c in range(8):
        m = {}
        m["xp"] = g["x_prompt"][c]
        m["xs"] = g["x_sample"][c * NSB:(c + 1) * NSB].reshape(NSB * SL, D)
        m["call"] = np.concatenate([g["c_prompt"][c:c + 1], g["c_sample"][c * NSB:(c + 1) * NSB]], axis=0)
        m["st_sc"] = g["state_ssd_conv"][:, c * NSB:(c + 1) * NSB]
        m["st_ssm"] = g["state_ssm"][:, c * NSB:(c + 1) * NSB]
        m["st_gc"] = g["state_gdn_conv"][:, c * NSB:(c + 1) * NSB]
        m["st_gdn"] = g["state_gdn"][:, c * NSB:(c + 1) * NSB]
        for k in ("norm_w", "w_ada", "b_ada", "w_in", "ssd_conv_w", "ssd_conv_b", "ssd_dt_bias", "ssd_a_log", "ssd_d",
                  "ssd_norm_w", "gdn_conv_w", "gdn_dt_bias", "gdn_a_log", "gdn_norm_w", "w_out"):
            m[k] = g[k]
        m["final_norm_w"] = g["final_norm_w"].reshape(1, D)
        for k, v in consts.items():
            m["c_" + k] = v
        maps.append({k: np.ascontiguousarray(v) for k, v in m.items()})
    return maps


def kernel(**inputs):
    if "nc" not in _CACHE:
        _CACHE["nc"] = build()[0]
    nc = _CACHE["nc"]
    maps = make_in_maps(inputs)
    res = run_bass_kernel_spmd(nc, maps, core_ids=list(range(8)))
    R = res.results
    cat = lambda k, ax: np.concatenate([np.asarray(r[k]) for r in R], axis=ax)
    y_p = np.stack([np.asarray(r["y_p"]) for r in R], 0)
    y_s = np.stack([np.asarray(r["y_s"]).reshape(NSB, SL, D) for r in R], 0).reshape(8 * NSB, SL, D)
    ncs_p = np.stack([np.asarray(r["ncs_p"]) for r in R], 1)
    nssm_p = np.stack([np.asarray(r["nssm_p"]).reshape(DEPTH, 16, 64, 128) for r in R], 1)
    ngc_p = np.stack([np.asarray(r["ngc_p"]) for r in R], 1)
    ngdn_p = np.stack([np.asarray(r["ngdn_p"]) for r in R], 1)
    ncs_s = np.concatenate([np.asarray(r["ncs_s"]).reshape(DEPTH, NSB, 3, 1536) for r in R], 1)
    nssm_s = np.concatenate([np.asarray(r["nssm_s"]).reshape(DEPTH, NSB, 16, 64, 128) for r in R], 1)
    ngc_s = np.concatenate([np.asarray(r["ngc_s"]).reshape(DEPTH, NSB, 3, 3072) for r in R], 1)
    ngdn_s = np.concatenate([np.asarray(r["ngdn_s"]) for r in R], 1)
    outs = (y_p, y_s, ncs_p, nssm_p, ngc_p, ngdn_p, ncs_s, nssm_s, ngc_s, ngdn_s)
    return tuple(np.ascontiguousarray(o, dtype=np.float32) for o in outs)
```

```python
import contextlib
import numpy as np
import concourse.bass as bass
import concourse.mybir as mybir
from concourse.bass_utils import run_bass_kernel_spmd

F32 = mybir.dt.float32
F32R = mybir.dt.float32r
BF16 = mybir.dt.bfloat16
AF = mybir.ActivationFunctionType
ALU = mybir.AluOpType
AX = mybir.AxisListType

ENGS = ("pe", "dve", "act", "pool", "sp")


class Buf:
    def __init__(self, t, name):
        self.t = t
        self.name = name
        self.st = {}
        self.dma_sem = None
        self.dma_cnt = 0

    def __getitem__(self, k):
        return self.t[k]


class Instr:
    __slots__ = ("eng", "fn", "deps", "signal", "count", "is_dma", "buf_sem", "dma_count")

    def __init__(self, eng, fn):
        self.eng = eng
        self.fn = fn
        self.deps = []
        self.signal = False
        self.count = None
        self.is_dma = False
        self.buf_sem = None
        self.dma_count = None


class Sched:
    def __init__(self, nc):
        self.nc = nc
        self.streams = {e: [] for e in ENGS}
        self.all = []
        self.filler = None
        self.nfill = 0

    def _states(self, buf, key):
        st = buf.st
        if key is None:
            if None not in st:
                st[None] = [None, {}]
            return list(st.values())
        if key not in st:
            if None in st:
                st[key] = [st[None][0], dict(st[None][1])]
            else:
                st[key] = [None, {}]
        res = [st[key]]
        if None in st:
            res.append(st[None])
        return res

    @staticmethod
    def _norm(lst):
        out = []
        for x in lst or []:
            out.append((x, None) if isinstance(x, Buf) else x)
        return out

    def add(self, eng, fn, reads=None, writes=None, dma_buf=None):
        ins = Instr(eng, fn)
        ins.is_dma = dma_buf is not None
        reads = self._norm(reads)
        writes = self._norm(writes)
        deps, raw = [], []
        for buf, key in reads:
            for s in self._states(buf, key):
                if s[0] is not None:
                    deps.append(s[0])
                    raw.append(s[0])
                if getattr(buf, "excl", False):
                    for r in s[1].values():
                        if r.eng != eng:
                            deps.append(r)
        for buf, key in writes:
            for s in self._states(buf, key):
                if s[0] is not None:
                    deps.append(s[0])
                deps.extend(s[1].values())
        fdeps = {}
        for d in deps:
            if d is ins:
                continue
            if (not ins.is_dma) and (not d.is_dma) and d.eng == eng:
                if eng == "pe" or not any(d is r for r in raw):
                    continue
            if d.is_dma:
                fdeps[id(d)] = (d, d.buf_sem.dma_cnt * 16)
            else:
                fdeps[id(d)] = (d, None)
        ins.deps = list(fdeps.values())
        if ins.is_dma:
            ins.buf_sem = dma_buf
            dma_buf.dma_cnt += 1
            ins.dma_count = dma_buf.dma_cnt * 16
        rkey = ("dma", id(ins)) if ins.is_dma else eng
        for buf, key in reads:
            for s in self._states(buf, key):
                s[1][rkey] = ins
        for buf, key in writes:
            for s in self._states(buf, key):
                s[0] = ins
                if key is None or s is not buf.st.get(None):
                    s[1] = {}
        self.streams[eng].append(ins)
        self.all.append(ins)
        return ins

    def barrier(self, dma_bufs_all):
        last = {}
        for e in ("pe", "dve", "act", "pool"):
            for ins in reversed(self.streams[e]):
                if not isinstance(ins, tuple) and not ins.is_dma:
                    last[e] = ins
                    ins.signal = True
                    break
        snap = [(b, b.dma_cnt * 16) for b in dma_bufs_all if b.dma_cnt > 0]
        mark = ("barrier", last, snap)
        for e in ENGS:
            self.streams[e].append(mark)

    def emit(self, final_wait_eng="sp"):
        nc = self.nc
        for ins in self.all:
            for d, _v in ins.deps:
                if not d.is_dma:
                    d.signal = True
        with contextlib.ExitStack() as es:
            eng_sem = {e: es.enter_context(nc.semaphore("s_" + e)) for e in ("pe", "dve", "act", "pool")}
            dma_bufs = []
            for ins in self.all:
                if isinstance(ins, tuple):
                    continue
                if ins.is_dma and ins.buf_sem.dma_sem is None:
                    ins.buf_sem.dma_sem = es.enter_context(nc.semaphore("d_%d" % len(dma_bufs)))
                    dma_bufs.append(ins.buf_sem)
            cnt = {e: 0 for e in eng_sem}
            for ins in self.all:
                if not ins.is_dma and ins.signal:
                    cnt[ins.eng] += 1
                    ins.count = cnt[ins.eng]
            block = es.enter_context(nc.Block())
            engobj = {"pe": block.tensor, "dve": block.vector, "act": block.scalar, "pool": block.gpsimd,
                      "sp": block.sync}
            sched = self

            def make(ename):
                def body(eng):
                    water = {}
                    for ins in sched.streams[ename]:
                        if isinstance(ins, tuple):
                            _, last, snap = ins
                            for e2, li in last.items():
                                if e2 != ename and water.get(("e", e2), 0) < li.count:
                                    water[("e", e2)] = li.count
                                    eng.wait_ge(eng_sem[e2], li.count)
                            for b_, v_ in snap:
                                if water.get(("d", id(b_)), 0) < v_:
                                    water[("d", id(b_))] = v_
                                    eng.wait_ge(b_.dma_sem, v_)
                            continue
                        need = {}
                        for d, dv in ins.deps:
                            if d.is_dma:
                                key, sem, val = ("d", id(d.buf_sem)), d.buf_sem.dma_sem, dv
                            else:
                                key, sem, val = ("e", d.eng), eng_sem[d.eng], d.count
                            if need.get(key, (None, 0))[1] < val:
                                need[key] = (sem, val)
                        todo = [(key, sem, val) for key, (sem, val) in need.items() if water.get(key, 0) < val]
                        if todo and ename == "pe" and sched.filler is not None and not ins.is_dma:
                            for _ in range(sched.nfill):
                                sched.filler(eng)
                        for key, sem, val in todo:
                            water[key] = val
                            eng.wait_ge(sem, val)
                        bi = ins.fn(eng)
                        if ins.is_dma:
                            bi.then_inc(ins.buf_sem.dma_sem, 16)
                        elif ins.signal:
                            bi.then_inc(eng_sem[ename], 1)
                    if ename == final_wait_eng:
                        for b in dma_bufs:
                            eng.wait_ge(b.dma_sem, b.dma_cnt * 16)
                return body

            for ename in ENGS:
                engobj[ename](make(ename))


NFILL = 2
D = 1024
SEQ = 2048
NT = SEQ // 128
NSB = 16
SL = 4
DEPTH = 2
IN_DIM = 6688
EPS = 1e-6
NEG = -30000.0
OFF_ZS, OFF_XBC, OFF_DT, OFF_ZG, OFF_QKV, OFF_A, OFF_B = 0, 1024, 2560, 2576, 3600, 6672, 6680


class Cfg:
    def __init__(self, name, T, nseq, L, Q, chain):
        self.name, self.T, self.nseq, self.L, self.Q, self.chain = name, T, nseq, L, Q, chain
        self.nch = T // Q
        self.nlev = {64: 5, 4: 1}[Q]


CP = Cfg("p", 128, 1, 128, 64, True)
CS = Cfg("s", 64, NSB, SL, 4, False)


def host_consts():
    c = {"ident": np.eye(128, dtype=np.float32), "ones": np.ones((128, 128), np.float32)}
    for cfg in (CP, CS):
        T, Q, nch = cfg.T, cfg.Q, cfg.nch
        ch = np.arange(T) // Q
        same = ch[:, None] == ch[None, :]
        idx = np.arange(T)
        le = idx[:, None] <= idx[None, :]
        lt = idx[:, None] < idx[None, :]
        n = cfg.name
        c["tri_" + n] = (same & le).astype(np.float32)
        c["blk_" + n] = same.astype(np.float32)
        c["nmI_" + n] = np.where(same & le, 0.0, NEG).astype(np.float32)
        c["nmS_" + n] = np.where(same & lt, 0.0, NEG).astype(np.float32)
        c["nmSr_" + n] = np.ascontiguousarray(c["nmS_" + n].T)
        cm = (ch[None, :] == np.arange(nch)[:, None]).astype(np.float32)
        c["cmf_" + n] = np.ascontiguousarray(np.broadcast_to(cm[None], (128, nch, T))).astype(np.float32)
        c["cmT_" + n] = np.ascontiguousarray(cm.T)
    e = np.zeros((17, 128), np.float32)
    e[0, :] = 1.0
    c["E_p"] = e
    e = np.zeros((17, 64), np.float32)
    for t in range(64):
        e[1 + t // SL, t] = 1.0
    c["E_s"] = e
    return c


def bl(ap, n):
    return ap.unsqueeze(len(ap.shape)).to_broadcast(list(ap.shape) + [n])


def bm(ap, n):
    return ap.unsqueeze(1).to_broadcast([ap.shape[0], n] + list(ap.shape[1:]))


def build(debug=(), limit=None):
    limit = limit or {}
    nc = bass.Bass("TRN2", target_bir_lowering=False)
    S = Sched(nc)
    es = contextlib.ExitStack()
    consts_np = host_consts()

    def din(name, shape):
        return nc.dram_tensor(name, list(shape), F32, kind="ExternalInput").ap()

    def dout(name, shape):
        return nc.dram_tensor(name, list(shape), F32, kind="ExternalOutput").ap()

    I = {}
    I["xp"] = din("xp", [SEQ, D])
    I["xs"] = din("xs", [NSB * SL, D])
    I["call"] = din("call", [17, D])
    I["st_sc"] = din("st_sc", [DEPTH, NSB, 3, 1536])
    I["st_ssm"] = din("st_ssm", [DEPTH, NSB, 16, 64, 128])
    I["st_gc"] = din("st_gc", [DEPTH, NSB, 3, 3072])
    I["st_gdn"] = din("st_gdn", [DEPTH, NSB, 8, 128, 128])
    wshapes = {"norm_w": [DEPTH, D], "w_ada": [DEPTH, D, 3 * D], "b_ada": [DEPTH, 3 * D], "w_in": [DEPTH, D, IN_DIM],
               "ssd_conv_w": [DEPTH, 4, 1536], "ssd_conv_b": [DEPTH, 1536], "ssd_dt_bias": [DEPTH, 16],
               "ssd_a_log": [DEPTH, 16], "ssd_d": [DEPTH, 16], "ssd_norm_w": [DEPTH, 1024],
               "gdn_conv_w": [DEPTH, 4, 3072], "gdn_dt_bias": [DEPTH, 8], "gdn_a_log": [DEPTH, 8],
               "gdn_norm_w": [DEPTH, 128], "w_out": [DEPTH, 2048, D], "final_norm_w": [1, D]}
    for k, shp in wshapes.items():
        I[k] = din(k, shp)
    for k, v in consts_np.items():
        I["c_" + k] = din("c_" + k, v.shape)
    O = {}
    O["y_p"] = dout("y_p", [SEQ, D])
    O["y_s"] = dout("y_s", [NSB * SL, D])
    O["ncs_p"] = dout("ncs_p", [DEPTH, 3, 1536])
    O["nssm_p"] = dout("nssm_p", [DEPTH, 1024, 128])
    O["ngc_p"] = dout("ngc_p", [DEPTH, 3, 3072])
    O["ngdn_p"] = dout("ngdn_p", [DEPTH, 8, 128, 128])
    O["ncs_s"] = dout("ncs_s", [DEPTH, NSB * 3, 1536])
    O["nssm_s"] = dout("nssm_s", [DEPTH, NSB, 1024, 128])
    O["ngc_s"] = dout("ngc_s", [DEPTH, NSB * 3, 3072])
    O["ngdn_s"] = dout("ngdn_s", [DEPTH, NSB, 8, 128, 128])
    NROW = SEQ + NSB * SL
    scr = [Buf(nc.dram_tensor("scr%d" % i, [NROW, D], F32, kind="Internal").ap(), "scr%d" % i) for i in range(3)]
    xin_ext = None

    allb = []

    def sb(name, shape, dt=F32):
        b_ = Buf(es.enter_context(nc.sbuf_tensor(name, list(shape), dt)), name)
        allb.append(b_)
        return b_

    banks = [Buf(es.enter_context(nc.psum_tensor("bank%d" % i, [128, 512], F32)), "bank%d" % i) for i in range(8)]
    for b_ in banks:
        b_.excl = True
    pstate = {"i": 0, "pinned": set()}
    NB = 7 if limit.get("nfill", NFILL) > 0 else 8

    def pget():
        while True:
            i = pstate["i"] % NB
            pstate["i"] += 1
            if i not in pstate["pinned"]:
                return banks[i]

    def ppin(b):
        pstate["pinned"].add(banks.index(b))

    def punpin(b):
        pstate["pinned"].discard(banks.index(b))

    def pf(b, shape, p=128):
        n = int(np.prod(shape))
        ap = b[:p, 0:n]
        if len(shape) == 1:
            return ap
        names = " ".join("a%d" % i for i in range(len(shape)))
        kw = {"a%d" % i: shape[i] for i in range(1, len(shape))}
        return ap.rearrange("p (%s) -> p %s" % (names, names), **kw)

    def pb(b, shape, p=128):
        n = int(np.prod(shape))
        ap = b[:p, :].bitcast(BF16)[:, 0:n]
        if len(shape) == 1:
            return ap
        names = " ".join("a%d" % i for i in range(len(shape)))
        kw = {"a%d" % i: shape[i] for i in range(1, len(shape))}
        return ap.rearrange("p (%s) -> p %s" % (names, names), **kw)

    def MM(out, lhsT, rhs, R, W, start=True, stop=True, **kw):
        S.add("pe", lambda e: e.matmul(out, lhsT=lhsT, rhs=rhs, start=start, stop=stop, **kw), reads=R, writes=W)

    def TR(out, in_, ident, R, W):
        S.add("pe", lambda e: e.transpose(out=out, in_=in_, identity=ident), reads=R, writes=W)

    def ACT(out, in_, func, R, W, **kw):
        S.add("act", lambda e: e.activation(out=out, in_=in_, func=func, **kw), reads=R, writes=W)

    def TT(out, in0, in1, op, R, W, eng="dve"):
        S.add(eng, lambda e: e.tensor_tensor(out=out, in0=in0, in1=in1, op=op), reads=R, writes=W)

    def TS(out, in0, s1, s2, op0, op1, R, W, eng="dve"):
        S.add(eng, lambda e: e.tensor_scalar(out=out, in0=in0, scalar1=s1, scalar2=s2, op0=op0, op1=op1), reads=R, writes=W)

    def STT(out, in0, scalar, in1, op0, op1, R, W):
        S.add("dve", lambda e: e.scalar_tensor_tensor(out=out, in0=in0, scalar=scalar, in1=in1, op0=op0, op1=op1), reads=R, writes=W)

    def CP_(out, in_, R, W, eng="dve"):
        S.add(eng, lambda e: e.tensor_copy(out=out, in_=in_), reads=R, writes=W)

    def RECIP(out, in_, R, W):
        S.add("dve", lambda e: e.reciprocal(out=out, in_=in_), reads=R, writes=W)

    def RED(out, in_, R, W):
        S.add("dve", lambda e: e.tensor_reduce(out=out, in_=in_, axis=AX.X, op=ALU.add), reads=R, writes=W)

    def MSET(ap, val, W, eng="pool"):
        S.add(eng, lambda e: e.memset(ap, val), writes=W)

    def LD(out, in_, buf, R=None, nonc=False):
        if nonc:
            S.add("sp", lambda e: e.dma_start(out=out, in_=in_, allow_slow_non_contiguous=True), reads=R, writes=[buf], dma_buf=buf)
        else:
            S.add("sp", lambda e: e.dma_start(out=out, in_=in_), reads=R, writes=[buf], dma_buf=buf)

    sto_n = [0]
    sto_dummy = Buf(None, "sto_dummy")

    def STO(out, in_, buf, W=None, extra_reads=None):
        S.add("sp", lambda e: e.dma_start(out=out, in_=in_), reads=[buf] + (extra_reads or []), writes=W, dma_buf=buf)

    dbg_out = {}
    mod_stacks = []
    cur_es = [es]
    all_bufs = []

    def DBG(name, buf, ap, shape):
        if name in debug and name not in dbg_out:
            o = dout("dbg_" + name, shape)
            dbg_out[name] = o
            if ap.dtype != F32:
                tmp = Buf(cur_es[0].enter_context(nc.sbuf_tensor("dbgt_" + name, list(shape), F32)), "dbgt_" + name)
                all_bufs.append(tmp)
                CP_(tmp[:], ap, [buf], [tmp])
                STO(o, tmp[:], tmp)
            else:
                STO(o, ap, buf)

    C = {}
    for k, v in consts_np.items():
        if k.startswith("cmf_"):
            continue
        C[k] = sb("k_" + k, v.shape)
        LD(C[k][:], I["c_" + k], C[k])
    identf = C["ident"]
    onesf = C["ones"]
    identb = sb("identb", [128, 128], BF16)
    CP_(identb[:], identf[:], [identf], [identb])
    onesb = sb("onesb", [128, 128], BF16)
    CP_(onesb[:], onesf[:], [onesf], [onesb])
    cfg_pending = []
    for cfg in (CP, CS):
        n = cfg.name
        cfg.tri, cfg.blk, cfg.nmI, cfg.nmS, cfg.nmSr, cfg.cmT = (C[k + n] for k in ("tri_", "blk_", "nmI_", "nmS_", "nmSr_", "cmT_"))
        cfg.cmf = sb("cmfb_" + n, [128, cfg.nch, cfg.T], BF16)
        cfg.ncmf = sb("ncmfb_" + n, [128, cfg.nch, cfg.T], BF16)
        cfg_pending.append(cfg)
        cfg.cmTb = sb("cmTb_" + n, [cfg.T, cfg.nch], BF16)
        CP_(cfg.cmTb[:], cfg.cmT[:], [cfg.cmT], [cfg.cmTb])

    SW = 128
    stage = [sb("stage%d" % i, [128, 8, SW]) for i in range(2)]
    stage_i = [0]
    WO = sb("wo", [128, 8, D], BF16)
    DIAG = sb("diag", [128, 12, 4, 128], BF16)
    cwT = sb("cwT", [128, 12, 4])
    cbrow = sb("cbrow", [1, 1536], BF16)
    nwT = sb("nwT", [128, 8])
    scT = sb("scT", [128, 8, 17])
    shT = sb("shT", [128, 8, 17])
    gate_bc = {"p": sb("gate_p", [128, D]), "s": sb("gate_s", [64, D])}
    vec16 = sb("vec16", [128, 4, 16])
    vec8 = sb("vec8", [128, 2, 8])
    scTb = sb("scTb", [128, 8, 17], BF16)
    badaT = sb("badaT", [128, 24])
    nw_in = sb("nw_in", [128, 8])
    xt_slots = [sb("xt%d" % i, [128, D]) for i in range(2)]
    xa_slots = [sb("xa%d" % i, [128, D]) for i in range(2)]
    stage_all = stage + xt_slots + xa_slots
    junk = sb("junk", [128, D], BF16)
    xn = sb("xn", [128, D], BF16)
    hT = sb("hT", [128, 8, 128], BF16)
    sm1 = sb("sm1", [128, 8])
    raw = sb("raw", [128, 12, 132], BF16)
    tailf = sb("tailf", [128, 12, 48])
    tailo = sb("tailo", [48, 1536])
    cst_in = tailo
    fT = sb("fT", [128, 12, 128], BF16)
    big = [sb("big%d" % i, [128, D]) for i in range(4)]
    bigb = [sb("bigb%d" % i, [128, D], BF16) for i in range(3)]
    xo = big[3]
    call_sb, sil_c, bgate, gate17, cbrow_f = big[0], bigb[0], big[1], big[2], tailo
    hTf_t = big[0]
    for cfg in cfg_pending:
        n_ = cfg.nch * cfg.T
        tmpv = big[0][:, 0:n_].rearrange("p (c t) -> p c t", t=cfg.T)
        LD(tmpv, I["c_cmf_" + cfg.name], big[0])
        CP_(cfg.cmf[:], tmpv, [big[0]], [cfg.cmf])
        TS(cfg.ncmf[:], tmpv, -1.0, None, ALU.mult, ALU.bypass, [big[0]], [cfg.ncmf])

    def stg():
        i = stage_i[0] % len(stage_all)
        stage_i[0] += 1
        b_ = stage_all[i]
        v = b_[:, :, :] if b_ in stage else b_[:, :].rearrange("p (k n) -> p k n", k=8)
        return b_, v, ("pool", "act", "dve")[i % 3]

    def cast(eng, out, in_, R, W, scale=None):
        if eng == "act":
            if scale is None:
                ACT(out, in_, AF.Copy, R, W)
            else:
                ACT(out, in_, AF.Copy, R, W, scale=scale)
        elif scale is None:
            CP_(out, in_, R, W, eng=eng)
        else:
            TS(out, in_, scale, 1.0, ALU.mult, ALU.mult, R, W, eng=eng)

    def load_cols(dst, dcol, src, ncols):
        c0 = 0
        while c0 < ncols:
            n = min(SW, ncols - c0)
            sbuf_, st, eng = stg()
            LD(st[:, :, 0:n], src[:, c0:c0 + n].rearrange("(k p) n -> p k n", p=128), sbuf_)
            cast(eng, dst[:, :, dcol + c0:dcol + c0 + n], st[:, :, 0:n], [sbuf_], [dst])
            c0 += n

    def load_wo(src_rows, nk, scale_ap_fn):
        for half in range(D // SW):
            sbuf_, st, eng = stg()
            LD(st[:, 0:nk, :], src_rows[:, half * SW:(half + 1) * SW].rearrange("(k p) n -> p k n", p=128), sbuf_)
            for k in range(nk):
                cast(eng, WO[:, k, half * SW:(half + 1) * SW], st[:, k, :], [sbuf_, nwT], [WO], scale=scale_ap_fn(k))

    def build_diag(convw, c0, nchunks):
        for k_ in range(4):
            LD(cwT[:, 0:nchunks, k_], convw[k_, c0 * 128:(c0 + nchunks) * 128].rearrange("(c p) -> p c", p=128), cwT, nonc=True)
        for c in range(nchunks):
            TT(DIAG[:, c, :, :], bm(identf[:], 4), bl(cwT[:, c, :], 128), ALU.mult, [identf, cwT], [DIAG], eng="pool")

    def do_mod(l):
        LD(call_sb[:17, :], I["call"], call_sb)
        ACT(sil_c[:17, :], call_sb[:17, :], AF.Silu, [call_sb], [sil_c])
        b = pget()
        for c in range(8):
            TR(pb(b, [8, 32])[:, c, 0:17], sil_c[:17, c * 128:(c + 1) * 128], identb[:17, :17], [sil_c, identb], [b])
        CP_(scTb[:], pb(b, [8, 32])[:, :, 0:17], [b], [scTb])
        LD(badaT[:], I["b_ada"][l].rearrange("(c p) -> p c", p=128), badaT, nonc=True)
        LD(nw_in[:], I["norm_w"][l].rearrange("(c p) -> p c", p=128), nw_in, nonc=True)
        LD(bgate[0:1, :], I["b_ada"][l:l + 1, 2048:3072], bgate)
        nblk = 3 * D // SW
        per = D // SW
        modes = contextlib.ExitStack()
        wb = Buf(modes.enter_context(nc.sbuf_tensor("modwb%d" % l, [128, 8, SW], BF16)), "modwb")
        all_bufs.append(wb)
        for blk in range(nblk):
            sbuf_, st, eng = stg()
            LD(st, I["w_ada"][l][:, blk * SW:(blk + 1) * SW].rearrange("(k p) n -> p k n", p=128), sbuf_)
            cast(eng, wb[:, :, 0:SW], st, [sbuf_], [wb])
            if blk < 2 * per:
                b = pget()
                nj = SW // 128
                for j in range(nj):
                    for k in range(8):
                        MM(pf(b, [nj, 17])[:, j, :], wb[:, k, j * 128:(j + 1) * 128], scTb[:, k, :], [wb, scTb], [b],
                           start=(k == 0), stop=(k == 7))
                dst = shT if blk < per else scT
                cc = (blk % per) * nj
                TT(dst[:, cc:cc + nj, :], pf(b, [nj, 17]), bl(badaT[:, blk * nj:blk * nj + nj], 17), ALU.add, [b, badaT], [dst])
            else:
                g = pget()
                gc0 = (blk - 2 * per) * SW
                for k in range(8):
                    MM(g[:17, 0:SW], scTb[:, k, :], wb[:, k, 0:SW], [wb, scTb], [g], start=(k == 0), stop=False)
                MM(g[:17, 0:SW], onesf[0:1, 0:17], bgate[0:1, gc0:gc0 + SW], [onesf, bgate], [g], start=False, stop=True)
                CP_(gate17[:17, gc0:gc0 + SW], g[:17, 0:SW], [g], [gate17])
        TS(scT[:], scT[:], 1.0, None, ALU.add, ALU.bypass, [scT], [scT])
        TT(scT[:], scT[:], bl(nw_in[:], 17), ALU.mult, [scT, nw_in], [scT])
        for n, E, T in (("p", C["E_p"], 128), ("s", C["E_s"], 64)):
            for half in range(2):
                b = pget()
                MM(b[:T, :], E[:, :T], gate17[:17, half * 512:(half + 1) * 512], [E, gate17], [b])
                CP_(gate_bc[n][:T, half * 512:(half + 1) * 512], b[:T, :], [b], [gate_bc[n]])
        mod_stacks.append(modes)

    def rows(ti):
        return (ti * 128, 128) if ti < NT else (SEQ, 64)

    def src_ap(layer_in, ti):
        r0, T = rows(ti)
        if layer_in is None:
            return (I["xp"][r0:r0 + T, :] if ti < NT else I["xs"][:, :]), None
        return layer_in[r0:r0 + T, :], (layer_in, ti)

    def prologue(cfg, xt):
        T = cfg.T
        ACT(junk[:T, :], xt[:T, :], AF.Square, [xt], [junk, sm1], accum_out=sm1[:T, 0:1])
        TS(sm1[:T, 0:1], sm1[:T, 0:1], 1.0 / D, EPS, ALU.mult, ALU.add, [sm1], [sm1])
        ACT(sm1[:T, 0:1], sm1[:T, 0:1], AF.Ln, [sm1], [sm1])
        ACT(sm1[:T, 0:1], sm1[:T, 0:1], AF.Exp, [sm1], [sm1], scale=-0.5)
        ACT(xn[:T, :], xt[:T, :], AF.Copy, [xt, sm1], [xn], scale=sm1[:T, 0:1])
        b = pget()
        for c in range(8):
            TR(pb(b, [8, T])[:, c, :], xn[:T, c * 128:(c + 1) * 128], identb[:T, :T], [xn, identb], [b])
        if cfg.nseq == 1:
            sc, sh = bl(scT[:, :, 0], T), bl(shT[:, :, 0], T)
            o1, o2, i1 = hTf_t[:, 0:8 * T].rearrange("p (c t) -> p c t", t=T), hT[:, :, :T], pb(b, [8, T])
        else:
            sc, sh = bl(scT[:, :, 1:17], SL), bl(shT[:, :, 1:17], SL)
            o1 = hTf_t[:, 0:8 * T].rearrange("p (c b l) -> p c b l", c=8, l=SL)
            o2 = hT[:, :, :T].rearrange("p c (b l) -> p c b l", l=SL)
            i1 = pb(b, [8, NSB, SL])
        TT(o1, i1, sc, ALU.mult, [b, scT], [hTf_t])
        TT(o2, o1, sh, ALU.add, [hTf_t, shT], [hT])

    def inproj_conv(cfg, l, wcol0, nchunks, bias_row, st_conv_in, conv_out, conv_out_s, ch0, is_last, first):
        T, L, nseq = cfg.T, cfg.L, cfg.nseq
        rawv = raw[:, :, 0:nseq * (L + 3)].rearrange("p c (b l) -> p c b l", l=L + 3)
        if cfg.nseq == 1:
            if first:
                MSET(raw[:, :, 0:3], 0.0, [raw])
            else:
                CP_(raw[:, 0:nchunks, 0:3], raw[:, 0:nchunks, L:L + 3], [raw], [raw])
        else:
            LD(cst_in[:, 0:nchunks * 128], st_conv_in[:, :, ch0 * 128:(ch0 + nchunks) * 128].rearrange("b k c -> (b k) c"), cst_in)
            for g0 in range(0, nchunks, 8):
                b = pget()
                ng = min(8, nchunks - g0)
                for c in range(ng):
                    TR(pf(b, [8, 48])[:, c, :], cst_in[:, (g0 + c) * 128:(g0 + c + 1) * 128], identf[:48, :48], [cst_in, identf], [b])
                CP_(rawv[:, g0:g0 + ng, :, 0:3], pf(b, [8, 48])[:, 0:ng, :].rearrange("p c (b k) -> p c b k", k=3), [b], [raw])
        if limit.get("stopC", 99) <= 1:
            return
        for g0 in range(0, nchunks, 4):
            b = pget()
            for c in range(4):
                for k in range(8):
                    MM(pf(b, [4, T])[:, c, :], PA["WB"][:, k, wcol0 + (g0 + c) * 128: wcol0 + (g0 + c + 1) * 128], hT[:, k, :T],
                       [PA["WB"], hT], [b], start=(k == 0), stop=(k == 7))
            if nseq == 1:
                ACT(raw[:, g0:g0 + 4, 3:3 + L], pf(b, [4, T]), AF.Copy, [b], [raw])
            else:
                for c in range(4):
                    ACT(rawv[:, g0 + c, :, 3:3 + L], pf(b, [4, T])[:, c, :].rearrange("p (b l) -> p b l", l=L), AF.Copy, [b], [raw])
            if is_last and limit.get("tail", True):
                for c in range(4):
                    ACT(tailf[:, g0 + c, 0:nseq * 3].rearrange("p (b k) -> p b k", k=3),
                        pf(b, [4, T])[:, c, :].rearrange("p (b l) -> p b l", l=L)[:, :, L - 3:L], AF.Copy, [b], [tailf])
        if limit.get("stopC", 99) <= 2:
            return
        for g0 in range(0, nchunks, 4):
            b = pget()
            for c in range(4):
                o = pf(b, [4, T])[:, c, :]
                if nseq > 1:
                    o = o.rearrange("p (b l) -> p b l", l=L)
                for k in range(4):
                    MM(o, DIAG[:, g0 + c, k, :], rawv[:, g0 + c, :, k:k + L] if nseq > 1 else raw[:, g0 + c, k:k + L],
                       [DIAG, raw], [b], start=(k == 0), stop=(k == 3 and bias_row is None))
                if bias_row is not None:
                    MM(pf(b, [4, T])[:, c, :], bias_row[0:1, (g0 + c) * 128:(g0 + c + 1) * 128], onesb[0:1, :T], [bias_row, onesb], [b],
                       start=False, stop=True)
            ACT(fT[:, g0:g0 + 4, :T], pf(b, [4, T]), AF.Silu, [b], [fT])
        if limit.get("stopC", 99) <= 3:
            return
        if is_last:
            n3 = nseq * 3
            for g0 in range(0, nchunks, 4):
                b = pget()
                for c in range(4):
                    TR(pf(b, [4, 128], p=n3)[:, c, :], tailf[:, g0 + c, 0:n3], identf[:, :], [tailf, identf], [b])
                CP_(tailo[:n3, g0 * 128:(g0 + 4) * 128], b[:n3, :], [b], [tailo])
            dst = conv_out[l][:, ch0 * 128:(ch0 + nchunks) * 128] if nseq == 1 else conv_out_s[l][:, ch0 * 128:(ch0 + nchunks) * 128]
            STO(dst, tailo[:n3, 0:nchunks * 128], tailo)

    def cum_and_tot(cfg, a_ap, a_buf, nh, out_c, out_dbc, pw):
        T, nch = cfg.T, cfg.nch
        b = pget()
        MM(b[:T, 0:nh], cfg.tri[:, :], a_ap, [cfg.tri, a_buf], [b])
        MM(b[:T, nh:2 * nh], cfg.blk[:, :], a_ap, [cfg.blk, a_buf], [b])
        ACT(out_c[:T, 0:2 * nh], b[:T, 0:2 * nh], AF.Copy, [b], [out_c])
        b2 = pget()
        MM(b2[:nch, 0:nh], cfg.cmT[:, :], a_ap, [cfg.cmT, a_buf], [b2])
        CP_(tot[:nch, 0:nh], b2[:nch, 0:nh], [b2], [tot])
        TT(totx[:nch, 0:nch * nh].rearrange("p (c h) -> p c h", h=nh), bm(tot[:nch, 0:nh], nch), bl(identf[:nch, :nch], nh), ALU.mult,
           [tot, identf], [totx])
        b3 = pget()
        MM(b3[:pw, 0:nch * nh], onesf[:nch, :pw], totx[:nch, 0:nch * nh], [onesf, totx], [b3])
        ACT(out_dbc[:pw, 0:nch * nh], b3[:pw, 0:nch * nh], AF.Exp, [b3], [out_dbc])

    tot = sb("tot", [16, 16])
    totx = sb("totx", [16, 256])
    sm = {k: sb("sm_" + k, [128, 64]) for k in ("a", "b", "c", "d", "e")}
    dbc = sb("dbc", [128, 256])
    acT = sb("acT", [16, 128])

    PA = {}

    def alloc_A(pes, tag):
        def sbp(name, shape, dt=F32):
            return Buf(pes.enter_context(nc.sbuf_tensor(name + tag, list(shape), dt)), name)
        PA.update(stf=sbp("stf", [128, D]), stb=sbp("stb", [128, D], BF16), hn=sbp("hn", [128, 8, 128]),
                  stf_s=sbp("stf_s", [128, D]), stb_s=sbp("stb_s", [128, D], BF16), Gs=sbp("Gs", [128, 2, 128]),
                  earg=sbp("earg", [128, 4, 128]), Ee=sbp("Ee", [128, 4, 128]), WT=sbp("WT", [128, 16, 128], BF16),
                  Btok=sbp("Btok", [128, 256], BF16), Btm=sbp("Btm", [128, 2, 256], BF16), CTm=sbp("CTm", [128, 2, 2, 128], BF16),
                  WB=sbp("WB", [128, 8, 2576], BF16))
    mslot = [0]

    def ssd_state_update(cfg, c, st_in, stb_in, st_out, yo, xdec, first_c, last_c):
        T, nch = cfg.T, cfg.nch
        Btm, CTm, Btok = PA["Btm"], PA["CTm"], PA["Btok"]
        sl = mslot[0] % 2
        mslot[0] += 1
        TS(Btm[:T, sl, :], Btok[:T, :], cfg.cmT[:, c:c + 1], 1.0, ALU.mult, ALU.mult, [Btok, cfg.cmT], [(Btm, sl)], eng="pool")
        TT(CTm[:, sl, :, :T], fT[:, 10:12, :T], bm(cfg.cmf[:, c, :], 2), ALU.mult, [fT, cfg.cmf], [(CTm, sl)], eng="pool")
        for g in range(2):
            MM(yo[g][:T, :], CTm[:, sl, g, :T], stb_in[:, g * 512:(g + 1) * 512], [(CTm, sl), stb_in], [yo[g]], start=first_c, stop=last_c)
        sp_ = [pget(), pget()]
        for g in range(2):
            MM(sp_[g][:, :], Btm[:T, sl, g * 128:(g + 1) * 128], xdec[:T, g * 512:(g + 1) * 512], [(Btm, sl), xdec], [sp_[g]])
        TT(big[3][:, :].rearrange("p (h q) -> p h q", q=64), st_in[:, :].rearrange("p (h q) -> p h q", q=64),
           bl(dbc[:, c * 16:(c + 1) * 16], 64), ALU.mult, [st_in, dbc], [big[3]])
        for g in range(2):
            TT(st_out[:, g * 512:(g + 1) * 512], big[3][:, g * 512:(g + 1) * 512], sp_[g][:, :], ALU.add, [big[3], sp_[g]], [st_out])

    def phaseA_tile(cfg, l, ti, xt, first, is_last):
        T, L, nseq, nch = cfg.T, cfg.L, cfg.nseq, cfg.nch
        stf, stb, hn, stf_s, stb_s, Gs, earg, Ee, WT, Btok = (PA[k] for k in ("stf", "stb", "hn", "stf_s", "stb_s", "Gs", "earg", "Ee", "WT", "Btok"))
        prologue(cfg, xt)
        if limit.get("stopA", 99) <= 0:
            return
        inproj_conv(cfg, l, OFF_XBC, 12, cbrow, I["st_sc"][l], O["ncs_p"], O["ncs_s"], 0, is_last, first)
        if ti == 0:
            DBG("fT_A", fT, fT[:, :, :T], [128, 12, T])
        if limit.get("stopA", 99) <= 1:
            return
        b = pget()
        for k in range(8):
            MM(b[:T, 0:16], hT[:, k, :T], PA["WB"][:, k, OFF_DT:OFF_DT + 16], [hT, PA["WB"]], [b], start=(k == 0), stop=(k == 7))
        A_, B_, C_, D_, E_ = (sm[k] for k in "abcde")
        TT(A_[:T, 0:16], b[:T, 0:16], vec16[:T, 0, :], ALU.add, [b, vec16], [A_])
        ACT(A_[:T, 0:16], A_[:T, 0:16], AF.Exp, [A_], [A_])
        ACT(A_[:T, 16:32], A_[:T, 0:16], AF.Ln, [A_], [A_], bias=1.0)
        TT(A_[:T, 32:48], A_[:T, 16:32], vec16[:T, 1, :], ALU.mult, [A_, vec16], [A_])
        cum_and_tot(cfg, A_[:T, 32:48], A_, 16, B_, dbc, 128)
        TT(C_[:T, 0:16], B_[:T, 16:32], B_[:T, 0:16], ALU.subtract, [B_], [C_])
        ACT(C_[:T, 0:16], C_[:T, 0:16], AF.Exp, [C_], [C_])
        ACT(C_[:T, 16:32], B_[:T, 0:16], AF.Exp, [B_], [C_])
        b = pget()
        TR(b[:16, 0:T], B_[:T, 0:16], identf[:T, :T], [B_, identf], [b])
        ACT(acT[:, :T], b[:16, 0:T], AF.Copy, [b], [acT])
        if ti == 0:
            DBG("acum", B_, B_[:T, 0:32], [T, 32])
        if limit.get("stopA", 99) <= 2:
            return
        bx, bb = pget(), pget()
        for c in range(8):
            TR(pb(bx, [8, 128], p=T)[:, c, :], fT[:, c, :T], identb[:, :], [fT, identb], [bx])
        for c in range(2):
            TR(pb(bb, [2, 128], p=T)[:, c, :], fT[:, 8 + c, :T], identb[:, :], [fT, identb], [bb])
        xc, xdec, xsD = bigb[0], bigb[1], big[0]
        TT(xc[:T, :].rearrange("p (h q) -> p h q", q=64), pb(bx, [16, 64], p=T), bl(A_[:T, 16:32], 64), ALU.mult, [bx, A_], [xc])
        TT(xsD[:T, :].rearrange("p (h q) -> p h q", q=64), pb(bx, [16, 64], p=T), bl(vec16[:T, 2, :], 64), ALU.mult, [bx, vec16], [xsD])
        TT(xdec[:T, :].rearrange("p (h q) -> p h q", q=64), xc[:T, :].rearrange("p (h q) -> p h q", q=64), bl(C_[:T, 0:16], 64), ALU.mult,
           [xc, C_], [xdec], eng="pool")
        ACT(Btok[:T, :], pb(bb, [256], p=T), AF.Copy, [bb], [Btok])
        b = pget()
        for g in range(2):
            MM(pf(b, [2, T], p=T)[:, g, :], fT[:, 8 + g, :T], fT[:, 10 + g, :T], [fT], [b])
        ACT(Gs[:T, :, :T], pf(b, [2, T], p=T), AF.Copy, [b], [Gs])
        for q in range(4):
            b = pget()
            for j in range(4):
                h = q * 4 + j
                MM(pf(b, [4, T], p=T)[:, j, :], identf[0:16, h:h + 1].to_broadcast([16, T]), acT[:, :T], [identf, acT], [b])
            for j in range(4):
                h = q * 4 + j
                STT(earg[:T, j, :T], pf(b, [4, T], p=T)[:, j, :], B_[:T, h:h + 1], cfg.nmI[:, :], ALU.subtract, ALU.add,
                    [b, B_, cfg.nmI], [earg])
            ACT(Ee[:T, :, :T], earg[:T, :, :T], AF.Exp, [earg], [Ee])
            TT(WT[:T, q * 4:q * 4 + 4, :T], Ee[:T, :, :T], bm(Gs[:T, q // 2, :T], 4), ALU.mult, [Ee, Gs], [WT])
        if limit.get("stopA", 99) <= 3:
            return
        yp = [pget(), pget()]
        for h in range(16):
            MM(yp[h // 8][:T, (h % 8) * 64:(h % 8 + 1) * 64], WT[:T, h, :T], xc[:T, h * 64:(h + 1) * 64], [WT, xc], [yp[h // 8]])
        ppin(yp[0]); ppin(yp[1])
        yo = [pget(), pget()]
        ppin(yo[0]); ppin(yo[1])
        if cfg.chain:
            if first:
                MSET(stf[:, :], 0.0, [stf])
                MSET(stb[:, :], 0.0, [stb])
            for c in range(nch):
                ssd_state_update(cfg, c, stf, stb, stf, yo, xdec, c == 0, c == nch - 1)
                ACT(stb[:, :], stf[:, :], AF.Copy, [stf], [stb])
            if is_last:
                for half in range(2):
                    b = pget()
                    for k in range(4):
                        kk = half * 4 + k
                        TR(pf(b, [4, 128])[:, k, :], stf[:, :].rearrange("n (j k) -> n k j", k=8)[:, kk, :], identf[:, :], [stf, identf], [b])
                    CP_(hn[:, half * 4:half * 4 + 4, :], pf(b, [4, 128]), [b], [hn])
                STO(O["nssm_p"][l].rearrange("(p k) n -> p k n", k=8), hn[:], hn)
        else:
            for c in range(limit.get("nchs", nch)):
                LD(hn[:], I["st_ssm"][l, c].rearrange("h q n -> (h q) n").rearrange("(p k) n -> p k n", k=8), hn)
                if limit.get("sub", 9) <= 0:
                    continue
                for half in range(2):
                    b = pget()
                    for k in range(4):
                        kk = half * 4 + k
                        TR(pf(b, [4, 128])[:, k, :], hn[:, kk, :], identf[:, :], [hn, identf], [b])
                    if limit.get("sub", 9) <= 1:
                        continue
                    ACT(stf_s[:, :].rearrange("n (j k) -> n k j", k=8)[:, half * 4:half * 4 + 4, :], pf(b, [4, 128]), AF.Copy, [b], [stf_s])
                    if limit.get("sub", 9) <= 2:
                        continue
                    CP_(stb_s[:, :].rearrange("n (j k) -> n k j", k=8)[:, half * 4:half * 4 + 4, :], pf(b, [4, 128]), [b], [stb_s])
                if limit.get("noupd"):
                    continue
                ssd_state_update(cfg, c, stf_s, stb_s, stf_s, yo, xdec, c == 0, c == limit.get("nchs", nch) - 1)
                if limit.get("noback"):
                    continue
                for half in range(2):
                    b = pget()
                    for k in range(4):
                        kk = half * 4 + k
                        TR(pf(b, [4, 128])[:, k, :], stf_s[:, :].rearrange("n (j k) -> n k j", k=8)[:, kk, :], identf[:, :], [stf_s, identf], [b])
                    CP_(hn[:, half * 4:half * 4 + 4, :], pf(b, [4, 128]), [b], [hn])
                STO(O["nssm_s"][l, c].rearrange("(p k) n -> p k n", k=8), hn[:], hn)
        if limit.get("stopA", 99) <= 4:
            punpin(yp[0]); punpin(yp[1]); punpin(yo[0]); punpin(yo[1])
            return
        t1, t2 = big[1], big[2]
        for g in range(2):
            TT(t1[:T, g * 512:(g + 1) * 512].rearrange("p (h q) -> p h q", q=64), yo[g][:T, :].rearrange("p (h q) -> p h q", q=64),
               bl(C_[:T, 16 + g * 8:16 + g * 8 + 8], 64), ALU.mult, [yo[g], C_], [t1])
            TT(t2[:T, g * 512:(g + 1) * 512], yp[g][:T, :], t1[:T, g * 512:(g + 1) * 512], ALU.add, [yp[g], t1], [t2])
        for x_ in yp + yo:
            punpin(x_)
        TT(t2[:T, :], t2[:T, :], xsD[:T, :], ALU.add, [t2, xsD], [t2], eng="pool")
        if ti == 0:
            DBG("yssd", t2, t2[:T, :], [T, D])
        for g in range(2):
            b = pget()
            for k in range(8):
                MM(b[:T, :], hT[:, k, :T], PA["WB"][:, k, OFF_ZS + g * 512:OFF_ZS + (g + 1) * 512], [hT, PA["WB"]], [b], start=(k == 0), stop=(k == 7))
            ACT(t1[:T, g * 512:(g + 1) * 512], b[:T, :], AF.Silu, [b], [t1])
        TT(t2[:T, :], t2[:T, :], t1[:T, :], ALU.mult, [t2, t1], [t2])
        for g in range(2):
            ACT(junk[:T, g * 512:(g + 1) * 512], t2[:T, g * 512:(g + 1) * 512], AF.Square, [t2], [junk, sm1], accum_out=sm1[:T, 2 + g:3 + g])
        TS(sm1[:T, 2:4], sm1[:T, 2:4], 1.0 / 512, EPS, ALU.mult, ALU.add, [sm1], [sm1])
        ACT(sm1[:T, 2:4], sm1[:T, 2:4], AF.Ln, [sm1], [sm1])
        ACT(sm1[:T, 2:4], sm1[:T, 2:4], AF.Exp, [sm1], [sm1], scale=-0.5)
        ynb = bigb[2]
        for g in range(2):
            ACT(ynb[:T, g * 512:(g + 1) * 512], t2[:T, g * 512:(g + 1) * 512], AF.Copy, [t2, sm1], [ynb], scale=sm1[:T, 2 + g:3 + g])
        out_proj(cfg, ynb, 8)

    def out_proj(cfg, ynb, nk):
        T = cfg.T
        b = pget()
        for c in range(nk):
            TR(pb(b, [8, T])[:, c, :], ynb[:T, c * 128:(c + 1) * 128], identb[:T, :T], [ynb, identb], [b])
        CP_(hT[:, 0:nk, :T], pb(b, [8, T])[:, 0:nk, :], [b], [hT])
        for half in range(2):
            b = pget()
            for k in range(nk):
                MM(b[:T, :], hT[:, k, :T], WO[:, k, half * 512:(half + 1) * 512], [hT, WO], [b], start=(k == 0), stop=(k == nk - 1))
            TT(big[3][:T, half * 512:(half + 1) * 512], b[:T, :], gate_bc[cfg.name][:T, half * 512:(half + 1) * 512], ALU.mult,
               [b, gate_bc[cfg.name]], [big[3]])

    PB = {}

    def alloc_B(pes, tag):
        def sbp(name, shape, dt=F32):
            return Buf(pes.enter_context(nc.sbuf_tensor(name + tag, list(shape), dt)), name)
        PB.update(Sp_f=sbp("Sp_f", [128, 4, 128]), Sp_b=sbp("Sp_b", [128, 4, 128], BF16), Ss_in=sbp("Ss_in", [128, 16, 128]),
                  Ss_b=sbp("Ss_b", [128, 16, 128], BF16), kv=sbp("kv", [128, 8, 128], BF16), kbg=sbp("kbg", [128, 4, 128]),
                  vb_=sbp("vb", [128, 4, 128]), kdec=sbp("kdec", [128, 4, 128], BF16), kdm=sbp("kdm", [64, 2, 128], BF16),
                  LG=sbp("LG", [128, 12]), LGT=sbp("LGT", [12, 128]), ea=sbp("ea", [128, 4, 128]),
                  E2a=sbp("E2a", [128, 4, 128]), E2b=sbp("E2b", [128, 4, 128]),
                  X0=sbp("X0", [128, 4, 128]), X1=sbp("X1", [128, 4, 128]), Y0=sbp("Y0", [128, 4, 128]), Y1=sbp("Y1", [128, 4, 128]),
                  R0=sbp("R0", [128, 4, 128]), R1=sbp("R1", [128, 4, 128]), attnT=sbp("attnT", [128, 4, 128], BF16),
                  egb=sbp("egb", [128, 4, 128]), qsq=sbp("qsq", [128, 4, 128], BF16),
                  nwTm=sbp("nwTm", [128, 1024], BF16), qdm=sbp("qdm", [128, 1024], BF16), u_sb=sbp("u_sb", [128, 4, 128]),
                  vnew=sbp("vnew", [128, 4, 128], BF16), dbcS=sbp("dbcS", [128, 64]), WB=sbp("WB", [128, 8, 2056], BF16))
        PB["E2"] = None

    def r32(ap):
        return ap.bitcast(F32R)

    def phaseB_tile(cfg, l, ti, xt, h0, first, is_last):
        T, L, nseq, nch = cfg.T, cfg.L, cfg.nseq, cfg.nch
        (Sp_f, Sp_b, Ss_in, Ss_b, kv, kbg, vb_, kdec, kdm, LG, LGT, ea, attnT, egb, nwTm, qdm, u_sb, vnew, dbcS) = (
            PB[k] for k in ("Sp_f", "Sp_b", "Ss_in", "Ss_b", "kv", "kbg", "vb_", "kdec", "kdm", "LG", "LGT", "ea", "attnT",
                            "egb", "nwTm", "qdm", "u_sb", "vnew", "dbcS"))
        E2 = [PB["E2a"], PB["E2b"]]
        Ss_out = Ss_in
        Xs, Ys, Rs = [PB["X0"], PB["X1"]], [PB["Y0"], PB["Y1"]], [PB["R0"], PB["R1"]]
        WB = PB["WB"]
        prologue(cfg, xt)
        inproj_conv_B(cfg, l, h0, is_last, first)
        A_, B_, C_, D_, E_ = (sm[k] for k in "abcde")
        b = pget()
        for k in range(8):
            MM(b[:T, 0:8], hT[:, k, :T], WB[:, k, 2048:2056], [hT, WB], [b], start=(k == 0), stop=(k == 7))
        TT(A_[:T, 0:4], b[:T, 0:4], vec8[:T, 0, h0:h0 + 4], ALU.add, [b, vec8], [A_])
        ACT(A_[:T, 0:4], A_[:T, 0:4], AF.Exp, [A_], [A_])
        ACT(A_[:T, 0:4], A_[:T, 0:4], AF.Ln, [A_], [A_], bias=1.0)
        TT(A_[:T, 4:8], A_[:T, 0:4], vec8[:T, 1, h0:h0 + 4], ALU.mult, [A_, vec8], [A_])
        ACT(A_[:T, 8:12], b[:T, 4:8], AF.Exp, [b], [A_], scale=-1.0)
        ACT(A_[:T, 8:12], A_[:T, 8:12], AF.Ln, [A_], [A_], bias=1.0)
        ACT(A_[:T, 12:16], A_[:T, 8:12], AF.Exp, [A_], [A_], scale=-1.0)
        ACT(A_[:T, 16:20], A_[:T, 8:12], AF.Copy, [A_], [A_], scale=-1.0)
        cum_and_tot(cfg, A_[:T, 4:8], A_, 4, B_, dbcS, 128)
        ACT(C_[:T, 0:4], B_[:T, 0:4], AF.Exp, [B_], [C_])
        TT(C_[:T, 4:8], B_[:T, 4:8], B_[:T, 0:4], ALU.subtract, [B_], [C_])
        ACT(C_[:T, 4:8], C_[:T, 4:8], AF.Exp, [C_], [C_])
        b = pget()
        for c in range(8):
            TR(pb(b, [8, 128], p=T)[:, c, :], fT[:, 4 + c, :T], identb[:, :], [fT, identb], [b])
        ACT(kv[:T, :, :], pb(b, [8, 128], p=T), AF.Copy, [b], [kv])
        ksq = big[0]
        TT(ksq[:T, 0:512].rearrange("p (h d) -> p h d", d=128), kv[:T, 0:4, :], kv[:T, 0:4, :], ALU.mult, [kv], [ksq])
        RED(D_[:T, 0:4], ksq[:T, 0:512].rearrange("p (h d) -> p h d", d=128), [ksq], [D_])
        TS(D_[:T, 0:4], D_[:T, 0:4], EPS, None, ALU.add, ALU.bypass, [D_], [D_])
        ACT(D_[:T, 4:8], D_[:T, 0:4], AF.Ln, [D_], [D_])
        ACT(D_[:T, 8:12], D_[:T, 4:8], AF.Exp, [D_], [D_], scale=-0.5)
        qsq = PB["qsq"]
        ACT(qsq[:, :, :T], fT[:, 0:4, :T], AF.Square, [fT], [qsq])
        b = pget()
        for j in range(4):
            MM(b[:T, j:j + 1], qsq[:, j, :T], onesb[:, 0:1], [qsq, onesb], [b])
        TS(D_[:T, 12:16], b[:T, 0:4], EPS, None, ALU.add, ALU.bypass, [b], [D_])
        ACT(D_[:T, 12:16], D_[:T, 12:16], AF.Ln, [D_], [D_])
        ACT(D_[:T, 16:20], D_[:T, 12:16], AF.Exp, [D_], [D_], scale=-0.5)
        TS(D_[:T, 16:20], D_[:T, 16:20], float(128 ** -0.5), None, ALU.mult, ALU.bypass, [D_], [D_])
        TT(E_[:T, 0:4], D_[:T, 8:12], A_[:T, 12:16], ALU.mult, [D_, A_], [E_])
        TT(E_[:T, 4:8], E_[:T, 0:4], C_[:T, 0:4], ALU.mult, [E_, C_], [E_])
        TT(E_[:T, 8:12], D_[:T, 8:12], C_[:T, 4:8], ALU.mult, [D_, C_], [E_])
        TS(E_[:T, 12:16], E_[:T, 0:4], -1.0, None, ALU.mult, ALU.bypass, [E_], [E_])
        TS(E_[:T, 16:20], D_[:T, 8:12], -1.0, None, ALU.mult, ALU.bypass, [D_], [E_])
        CP_(LG[:T, 0:4], B_[:T, 0:4], [B_], [LG])
        STT(LG[:T, 4:8], D_[:T, 4:8], -0.5, B_[:T, 0:4], ALU.mult, ALU.subtract, [D_, B_], [LG])
        STT(LG[:T, 8:12], D_[:T, 4:8], -0.5, B_[:T, 0:4], ALU.mult, ALU.add, [D_, B_], [LG])
        TT(LG[:T, 8:12], LG[:T, 8:12], A_[:T, 16:20], ALU.add, [LG, A_], [LG])
        b = pget()
        TR(b[:12, 0:T], LG[:T, 0:12], identf[:T, :T], [LG, identf], [b])
        ACT(LGT[:, :T], b[:12, 0:T], AF.Copy, [b], [LGT])
        TT(r32(kbg[:T, :, :]), kv[:T, 0:4, :], bl(E_[:T, 4:8], 128), ALU.mult, [kv, E_], [kbg])
        TT(kdec[:T, :, :], kv[:T, 0:4, :], bl(E_[:T, 8:12], 128), ALU.mult, [kv, E_], [kdec])
        TT(r32(vb_[:T, :, :]), kv[:T, 4:8, :], bl(A_[:T, 12:16], 128), ALU.mult, [kv, A_], [vb_])
        bG, bA = pget(), pget()
        for j in range(4):
            MM(pf(bG, [4, T], p=T)[:, j, :], fT[:, 4 + j, :T], fT[:, 4 + j, :T], [fT], [bG])
        for j in range(4):
            MM(pf(bA, [4, T], p=T)[:, j, :], fT[:, 4 + j, :T], fT[:, j, :T], [fT], [bA])
        bcs = [pget(), pget(), pget()]
        for r in range(3):
            for j in range(4):
                MM(pf(bcs[r], [4, T], p=T)[:, j, :], identf[0:12, r * 4 + j:r * 4 + j + 1].to_broadcast([12, T]), LGT[:, :T],
                   [identf, LGT], [bcs[r]])
        be = pget()
        for j in range(4):
            MM(pf(be, [4, T])[:, j, :], identf[0:12, j:j + 1].to_broadcast([12, 128]), LGT[:, :T], [identf, LGT], [be])
        X, Y, R = Xs[0], Ys[0], Rs[0]
        kinds = ((1, ALU.add, cfg.nmSr, 0), (2, ALU.subtract, cfg.nmS, 1), (0, ALU.subtract, cfg.nmI, 2))
        for r, op0, msk, ki in kinds:
            for j in range(4):
                STT(ea[:T, j, :T], pf(bcs[r], [4, T], p=T)[:, j, :], B_[:T, j:j + 1], msk[:, :], op0, ALU.add, [bcs[r], B_, msk], [ea])
            Ek = E2[ki % 2]
            ACT(Ek[:T, :, :T], ea[:T, :, :T], AF.Exp, [ea], [Ek])
            for j in range(4):
                if ki == 0:
                    STT(r32(X[:T, j, :T]), pf(bG, [4, T], p=T)[:, j, :], E_[:T, 12 + j:13 + j], Ek[:T, j, :T], ALU.mult, ALU.mult, [bG, E_, Ek], [X])
                elif ki == 1:
                    STT(r32(Y[:T, j, :T]), pf(bG, [4, T], p=T)[:, j, :], E_[:T, 16 + j:17 + j], Ek[:T, j, :T], ALU.mult, ALU.mult, [bG, E_, Ek], [Y])
                else:
                    STT(attnT[:T, j, :T], pf(bA, [4, T], p=T)[:, j, :], D_[:T, 8 + j:9 + j], Ek[:T, j, :T], ALU.mult, ALU.mult, [bA, D_, Ek], [attnT])
        ACT(egb[:, :, :T], pf(be, [4, T]), AF.Exp, [be], [egb])
        TT(r32(R[:T, :, :T]), Y[:T, :, :T], bm(identf[:T, :T], 4), ALU.add, [Y, identf], [R])
        for lev in range(cfg.nlev):
            X2, Y2, R2 = Xs[(lev + 1) % 2], Ys[(lev + 1) % 2], Rs[(lev + 1) % 2]
            lastlev = (lev == cfg.nlev - 1)
            bX, bY, bR = pget(), (None if lastlev else pget()), pget()
            for j in range(4):
                MM(pf(bX, [4, T], p=T)[:, j, :], r32(Y[:T, j, :T]), r32(X[:T, j, :T]), [X, Y], [bX])
            if not lastlev:
                for j in range(4):
                    MM(pf(bY, [4, T], p=T)[:, j, :], r32(X[:T, j, :T]), r32(Y[:T, j, :T]), [X, Y], [bY])
            ACT(r32(X2[:T, :, :T]), pf(bX, [4, T], p=T), AF.Copy, [bX], [X2])
            if not lastlev:
                CP_(r32(Y2[:T, :, :T]), pf(bY, [4, T], p=T), [bY], [Y2])
            for j in range(4):
                MM(pf(bR, [4, T], p=T)[:, j, :], r32(X2[:T, j, :T]), r32(R[:T, j, :T]), [X2, R], [bR])
            TT(r32(R2[:T, :, :T]), R[:T, :, :T], pf(bR, [4, T], p=T), ALU.add, [R, bR], [R2])
            X, Y, R = X2, Y2, R2
        bU, bW = pget(), pget()
        for j in range(4):
            MM(pf(bU, [4, 128], p=T)[:, j, :], r32(R[:T, j, :T]), r32(vb_[:T, j, :]), [R, vb_], [bU])
        for j in range(4):
            MM(pf(bW, [4, T])[:, j, :], r32(kbg[:T, j, :]), r32(R[:T, j, :T]), [kbg, R], [bW])
        ACT(u_sb[:T, :, :], pf(bU, [4, 128], p=T), AF.Copy, [bU], [u_sb])
        TT(egb[:, :, :T], fT[:, 0:4, :T], egb[:, :, :T], ALU.mult, [fT, egb], [egb])
        vnb, ob = pget(), pget()
        ppin(vnb)
        ppin(ob)
        vn4, o4 = pf(vnb, [4, 128], p=T), pf(ob, [4, 128], p=T)
        if cfg.chain:
            if first:
                MSET(Sp_f[:], 0.0, [Sp_f])
                MSET(Sp_b[:], 0.0, [Sp_b])
            nwv = nwTm[:, 0:nch * 4 * T].rearrange("p (c j t) -> p c j t", c=nch, j=4)
            qdv = qdm[:, 0:nch * 4 * T].rearrange("p (c j t) -> p c j t", c=nch, j=4)
            for c in range(nch):
                TT(nwv[:, c], pf(bW, [4, T]), bm(cfg.ncmf[:, c, :], 4), ALU.mult, [bW, cfg.ncmf], [(nwTm, c)])
                TT(qdv[:, c], egb[:, :, :T], bm(cfg.cmf[:, c, :], 4), ALU.mult, [egb, cfg.cmf], [(qdm, c)], eng="pool")
            for c in range(nch):
                for j in range(4):
                    MM(vn4[:, j, :], nwv[:, c, j, :], Sp_b[:, j, :], [(nwTm, c), Sp_b], [vnb], start=(j == 0 and c == 0), stop=False, skip_group_check=True)
                for j in range(4):
                    MM(o4[:, j, :], qdv[:, c, j, :], Sp_b[:, j, :], [(qdm, c), Sp_b], [ob], start=(j == 0 and c == 0), stop=False, skip_group_check=True)
                TT(vnew[:T, :, :], vn4, u_sb[:T, :, :], ALU.add, [vnb, u_sb], [vnew])
                bs = pget()
                for j in range(4):
                    MM(pf(bs, [4, 128])[:, j, :], kdec[c * 64:(c + 1) * 64, j, :], vnew[c * 64:(c + 1) * 64, j, :], [kdec, vnew], [bs])
                tmpS = big[0][:, 512:1024].rearrange("p (j v) -> p j v", v=128)
                TT(tmpS, Sp_f[:, :, :], bl(dbcS[:, c * 4:(c + 1) * 4], 128), ALU.mult, [Sp_f, dbcS], [big[0]])
                TT(Sp_f[:, :, :], tmpS, pf(bs, [4, 128]), ALU.add, [big[0], bs], [Sp_f])
                ACT(Sp_b[:, :, :], Sp_f[:, :, :], AF.Copy, [Sp_f], [Sp_b])
            if is_last:
                STO(O["ngdn_p"][l, h0:h0 + 4].rearrange("h k v -> k h v"), Sp_f[:], Sp_f)
        else:
            nwv = nwTm[:, 0:nch * T].rearrange("p (c t) -> p c t", t=T)
            qdv = qdm[:, 0:nch * T].rearrange("p (c t) -> p c t", t=T)
            ppin(bW)
            for j in range(4):
                h = h0 + j
                LD(Ss_in[:], I["st_gdn"][l, :, h].rearrange("b k v -> k b v"), Ss_in)
                CP_(Ss_b[:], Ss_in[:], [Ss_in], [Ss_b], eng="pool")
                TT(nwv, bm(pf(bW, [4, T])[:, j, :], nch), cfg.ncmf[:, :, :], ALU.mult, [bW, cfg.ncmf], [nwTm])
                TT(qdv, bm(egb[:, j, :T], nch), cfg.cmf[:, :, :], ALU.mult, [egb, cfg.cmf], [qdm], eng="pool")
                for c in range(nch):
                    MM(vn4[:, j, :], nwv[:, c, :], Ss_b[:, c, :], [nwTm, Ss_b], [vnb], start=(j == 0 and c == 0), stop=False, skip_group_check=True)
                for c in range(nch):
                    MM(o4[:, j, :], qdv[:, c, :], Ss_b[:, c, :], [qdm, Ss_b], [ob], start=(j == 0 and c == 0), stop=False, skip_group_check=True)
                TT(vnew[:T, j, :], vn4[:, j, :], u_sb[:T, j, :], ALU.add, [vnb, u_sb], [vnew])
                tmpS = big[2][:, :].rearrange("p (c v) -> p c v", v=128)
                for hf in range(2):
                    TT(tmpS, Ss_in[:, hf * 8:(hf + 1) * 8, :],
                       bl(dbcS[:, 0:nch * 4].rearrange("p (c j) -> p c j", j=4)[:, hf * 8:(hf + 1) * 8, j], 128), ALU.mult, [Ss_in, dbcS], [big[2]])
                    for c4 in range(hf * 8, hf * 8 + 8, 4):
                        bs = pget()
                        for cc in range(4):
                            c = c4 + cc
                            sl = mslot[0] % 2
                            mslot[0] += 1
                            TS(kdm[:T, sl, :], kdec[:T, j, :], cfg.cmT[:, c:c + 1], 1.0, ALU.mult, ALU.mult, [kdec, cfg.cmT], [(kdm, sl)], eng="pool")
                            MM(pf(bs, [4, 128])[:, cc, :], kdm[:T, sl, :], vnew[:T, j, :], [(kdm, sl), vnew], [bs])
                        TT(Ss_out[:, c4:c4 + 4, :], tmpS[:, c4 - hf * 8:c4 - hf * 8 + 4, :], pf(bs, [4, 128]), ALU.add, [big[2], bs], [Ss_out])
                STO(O["ngdn_s"][l, :, h].rearrange("b k v -> k b v"), Ss_out[:], Ss_out)
            punpin(bW)
        for j in range(4):
            MM(o4[:, j, :], attnT[:T, j, :T], vnew[:T, j, :], [attnT, vnew], [ob], start=False, stop=True, skip_group_check=True)
        punpin(vnb)
        of = big[1]
        ACT(of[:T, 0:512], ob[:T, :], AF.Copy, [ob], [of])
        punpin(ob)
        if ti == 0 and h0 == 0:
            DBG("ogdn", of, of[:T, 0:512], [T, 512])
        TT(ksq[:T, 0:512], of[:T, 0:512], of[:T, 0:512], ALU.mult, [of], [ksq])
        RED(C_[:T, 8:12], ksq[:T, 0:512].rearrange("p (h d) -> p h d", d=128), [ksq], [C_])
        TT(C_[:T, 12:16], D_[:T, 16:20], D_[:T, 16:20], ALU.mult, [D_], [C_])
        TT(C_[:T, 8:12], C_[:T, 8:12], C_[:T, 12:16], ALU.mult, [C_], [C_])
        TS(C_[:T, 8:12], C_[:T, 8:12], 1.0 / 128, EPS, ALU.mult, ALU.add, [C_], [C_])
        ACT(C_[:T, 8:12], C_[:T, 8:12], AF.Ln, [C_], [C_])
        ACT(C_[:T, 8:12], C_[:T, 8:12], AF.Exp, [C_], [C_], scale=-0.5)
        TT(C_[:T, 8:12], C_[:T, 8:12], D_[:T, 16:20], ALU.mult, [C_, D_], [C_])
        b = pget()
        for k in range(8):
            MM(b[:T, :], hT[:, k, :T], WB[:, k, 0:512], [hT, WB], [b], start=(k == 0), stop=(k == 7))
        sz = big[2]
        ACT(sz[:T, 0:512], b[:T, :], AF.Silu, [b], [sz])
        TT(of[:T, 0:512].rearrange("p (h d) -> p h d", d=128), of[:T, 0:512].rearrange("p (h d) -> p h d", d=128), bl(C_[:T, 8:12], 128),
           ALU.mult, [of, C_], [of])
        onb = bigb[2]
        TT(onb[:T, 0:512], of[:T, 0:512], sz[:T, 0:512], ALU.mult, [of, sz], [onb])
        out_proj(cfg, onb, 4)

    def inproj_conv_B(cfg, l, h0, is_last, first):
        T, L, nseq = cfg.T, cfg.L, cfg.nseq
        rawv = raw[:, :, 0:nseq * (L + 3)].rearrange("p c (b l) -> p c b l", l=L + 3)
        if nseq == 1:
            if first:
                MSET(raw[:, :, 0:3], 0.0, [raw])
            else:
                CP_(raw[:, :, 0:3], raw[:, :, L:L + 3], [raw], [raw])
        else:
            for part in range(3):
                cs0 = part * 1024 + h0 * 128
                LD(cst_in[:, part * 512:(part + 1) * 512], I["st_gc"][l][:, :, cs0:cs0 + 512].rearrange("b k c -> (b k) c"), cst_in)
            for g0 in (0, 8):
                b = pget()
                ng = min(8, 12 - g0)
                for c in range(ng):
                    TR(pf(b, [8, 48])[:, c, :], cst_in[:, (g0 + c) * 128:(g0 + c + 1) * 128], identf[:48, :48], [cst_in, identf], [b])
                CP_(rawv[:, g0:g0 + ng, :, 0:3], pf(b, [8, 48])[:, 0:ng, :].rearrange("p c (b k) -> p c b k", k=3), [b], [raw])
        for g0 in range(0, 12, 4):
            b = pget()
            for c in range(4):
                for k in range(8):
                    MM(pf(b, [4, T])[:, c, :], PB["WB"][:, k, 512 + (g0 + c) * 128: 512 + (g0 + c + 1) * 128], hT[:, k, :T],
                       [PB["WB"], hT], [b], start=(k == 0), stop=(k == 7))
            if nseq == 1:
                ACT(raw[:, g0:g0 + 4, 3:3 + L], pf(b, [4, T]), AF.Copy, [b], [raw])
            else:
                for c in range(4):
                    ACT(rawv[:, g0 + c, :, 3:3 + L], pf(b, [4, T])[:, c, :].rearrange("p (b l) -> p b l", l=L), AF.Copy, [b], [raw])
            if is_last and limit.get("tail", True):
                for c in range(4):
                    ACT(tailf[:, g0 + c, 0:nseq * 3].rearrange("p (b k) -> p b k", k=3),
                        pf(b, [4, T])[:, c, :].rearrange("p (b l) -> p b l", l=L)[:, :, L - 3:L], AF.Copy, [b], [tailf])
        for g0 in range(0, 12, 4):
            b = pget()
            for c in range(4):
                o = pf(b, [4, T])[:, c, :]
                if nseq > 1:
                    o = o.rearrange("p (b l) -> p b l", l=L)
                for k in range(4):
                    MM(o, DIAG[:, g0 + c, k, :], rawv[:, g0 + c, :, k:k + L] if nseq > 1 else raw[:, g0 + c, k:k + L],
                       [DIAG, raw], [b], start=(k == 0), stop=(k == 3))
            ACT(fT[:, g0:g0 + 4, :T], pf(b, [4, T]), AF.Silu, [b], [fT])
        if is_last:
            n3 = nseq * 3
            for g0 in range(0, 12, 4):
                b = pget()
                for c in range(4):
                    TR(pf(b, [4, 128], p=n3)[:, c, :], tailf[:, g0 + c, 0:n3], identf[:, :], [tailf, identf], [b])
                CP_(tailo[:n3, g0 * 128:(g0 + 4) * 128], b[:n3, :], [b], [tailo])
            dst = O["ngc_p"] if nseq == 1 else O["ngc_s"]
            for part in range(3):
                cs0 = part * 1024 + h0 * 128
                STO(dst[l][:, cs0:cs0 + 512], tailo[:n3, part * 512:(part + 1) * 512], tailo)

    def bvec(dst_ap, src_row_ap, buf):
        LD(dst_ap, src_row_ap.partition_broadcast(128), buf)

    tiles = [(CP, ti) for ti in range(limit.get("ntiles", NT))] + ([(CS, NT)] if limit.get("sample", True) else [])
    all_bufs.extend(allb)
    layer_in = None
    for l in range(limit.get("layers", DEPTH)):
        do_mod(l)
        S.barrier(all_bufs)
        mod_stacks.pop().close()
        for ph in limit.get("phases", (0, 1, 2)):
            acc_in = layer_in if ph == 0 else scr[(ph - 1)]
            last_phase = (l == DEPTH - 1 and ph == 2)
            acc_out = scr[ph] if ph < 2 else (scr[2] if not last_phase else None)
            S.barrier(all_bufs)
            pes = contextlib.ExitStack()
            cur_es[0] = pes
            if ph == 0:
                alloc_A(pes, "_%d_%d" % (l, ph))
            else:
                alloc_B(pes, "_%d_%d" % (l, ph))
            for v_ in list(PA.values()) + list(PB.values()):
                if v_ is not None and v_ not in all_bufs:
                    all_bufs.append(v_)
            if ph == 0:
                load_cols(PA["WB"], 0, I["w_in"][l][:, 0:2576], 2576)
                LD(nwT[:], I["ssd_norm_w"][l].rearrange("(k p) -> p k", p=128), nwT, nonc=True)
                load_wo(I["w_out"][l][0:1024, :], 8, lambda k: nwT[:, k:k + 1])
                build_diag(I["ssd_conv_w"][l], 0, 12)
                LD(cbrow_f[0:1, :], I["ssd_conv_b"][l:l + 1, :], cbrow_f)
                CP_(cbrow[:], cbrow_f[0:1, :], [cbrow_f], [cbrow])
                bvec(vec16[:, 0, :], I["ssd_dt_bias"][l:l + 1, :], vec16)
                bvec(vec16[:, 1, :], I["ssd_a_log"][l:l + 1, :], vec16)
                bvec(vec16[:, 2, :], I["ssd_d"][l:l + 1, :], vec16)
                ACT(vec16[:, 1, :], vec16[:, 1, :], AF.Exp, [vec16], [vec16])
                TS(vec16[:, 1, :], vec16[:, 1, :], -1.0, None, ALU.mult, ALU.bypass, [vec16], [vec16])
            else:
                h0 = (ph - 1) * 4
                load_cols(PB["WB"], 0, I["w_in"][l][:, OFF_ZG + h0 * 128:OFF_ZG + h0 * 128 + 512], 512)
                for part in range(3):
                    c0 = OFF_QKV + part * 1024 + h0 * 128
                    load_cols(PB["WB"], 512 + part * 512, I["w_in"][l][:, c0:c0 + 512], 512)
                load_cols(PB["WB"], 2048, I["w_in"][l][:, OFF_A + h0:OFF_A + h0 + 4], 4)
                load_cols(PB["WB"], 2052, I["w_in"][l][:, OFF_B + h0:OFF_B + h0 + 4], 4)
                LD(nwT[:, 0:1], I["gdn_norm_w"][l].rearrange("(p o) -> p o", o=1), nwT, nonc=True)
                load_wo(I["w_out"][l][1024 + h0 * 128:1024 + h0 * 128 + 512, :], 4, lambda k: nwT[:, 0:1])
                for part in range(3):
                    for k_ in range(4):
                        LD(cwT[:, part * 4:part * 4 + 4, k_],
                           I["gdn_conv_w"][l][k_, part * 1024 + h0 * 128:part * 1024 + h0 * 128 + 512].rearrange("(c p) -> p c", p=128), cwT, nonc=True)
                for c in range(12):
                    TT(DIAG[:, c, :, :], bm(identf[:], 4), bl(cwT[:, c, :], 128), ALU.mult, [identf, cwT], [DIAG], eng="pool")
                bvec(vec8[:, 0, :], I["gdn_dt_bias"][l:l + 1, :], vec8)
                bvec(vec8[:, 1, :], I["gdn_a_log"][l:l + 1, :], vec8)
                ACT(vec8[:, 1, :], vec8[:, 1, :], AF.Exp, [vec8], [vec8])
                TS(vec8[:, 1, :], vec8[:, 1, :], -1.0, None, ALU.mult, ALU.bypass, [vec8], [vec8])

            def issue_loads(idx):
                cfg, ti = tiles[idx]
                r0, T = rows(ti)
                xt = xt_slots[idx % 2]
                ap, dep = src_ap(layer_in.t if layer_in is not None else None, ti)
                LD(xt[:T, :], ap, xt, R=[(layer_in, ti)] if layer_in is not None else None)
                if ph > 0:
                    xa = xa_slots[idx % 2]
                    LD(xa[:T, :], acc_in.t[r0:r0 + T, :], xa, R=[(acc_in, ti)])

            if last_phase:
                fnw_bc = stage[0]
                fnw_v = stage[0][:, :, :].rearrange("p a b -> p (a b)")
                LD(fnw_v, I["final_norm_w"].partition_broadcast(128), stage[0])
            issue_loads(0)
            for idx, (cfg, ti) in enumerate(tiles):
                if idx + 1 < len(tiles):
                    issue_loads(idx + 1)
                r0, T = rows(ti)
                xt = xt_slots[idx % 2]
                xa = xa_slots[idx % 2] if ph > 0 else xt
                first = (ti == 0)
                is_last = (ti == NT - 1) or (ti == NT)
                if ph == 0:
                    phaseA_tile(cfg, l, ti, xt, first, is_last)
                else:
                    phaseB_tile(cfg, l, ti, xt, (ph - 1) * 4, first, is_last)
                TT(xo[:T, :], big[3][:T, :], xa[:T, :], ALU.add, [big[3], xa], [xo], eng="pool")
                if not last_phase:
                    STO(acc_out.t[r0:r0 + T, :], xo[:T, :], xo, W=[(acc_out, ti)])
                else:
                    ACT(junk[:T, :], xo[:T, :], AF.Square, [xo], [junk, sm1], accum_out=sm1[:T, 4:5])
                    TS(sm1[:T, 4:5], sm1[:T, 4:5], 1.0 / D, EPS, ALU.mult, ALU.add, [sm1], [sm1])
                    ACT(sm1[:T, 4:5], sm1[:T, 4:5], AF.Ln, [sm1], [sm1])
                    ACT(sm1[:T, 4:5], sm1[:T, 4:5], AF.Exp, [sm1], [sm1], scale=-0.5)
                    STT(big[0][:T, :], xo[:T, :], sm1[:T, 4:5], fnw_v[:T, :], ALU.mult, ALU.mult, [xo, sm1, fnw_bc], [big[0]])
                    dsto = O["y_p"][r0:r0 + T, :] if ti < NT else O["y_s"][:, :]
                    STO(dsto, big[0][:T, :], big[0])
            pes.close()
            cur_es[0] = es
            PA.clear()
            PB.clear()
        layer_in = scr[2]
    if limit.get("nfill", NFILL) > 0:
        fill_rhs = CS.cmf[:, :, :].rearrange("p c t -> p (c t)")[:, 0:512]
        S.filler = lambda e: e.matmul(banks[7][:, :], lhsT=identb[:, :], rhs=fill_rhs, start=True, stop=True)
        S.nfill = limit.get("nfill", NFILL)
    S.emit()
    es.close()
    return nc, dbg_out


_CACHE = {}


def make_in_maps(inputs):
    g = {k: np.ascontiguousarray(np.asarray(v, dtype=np.float32)) for k, v in inputs.items()}
    consts = host_consts()
    maps = []
    for c in range(8):
        m = {}
        m["xp"] = g["x_prompt"][c]
        m["xs"] = g["x_sample"][c * NSB:(c + 1) * NSB].reshape(NSB * SL, D)
        m["call"] = np.concatenate([g["c_prompt"][c:c + 1], g["c_sample"][c * NSB:(c + 1) * NSB]], axis=0)
        m["st_sc"] = g["state_ssd_conv"][:, c * NSB:(c + 1) * NSB]
        m["st_ssm"] = g["state_ssm"][:, c * NSB:(c + 1) * NSB]
        m["st_gc"] = g["state_gdn_conv"][:, c * NSB:(c + 1) * NSB]
        m["st_gdn"] = g["state_gdn"][:, c * NSB:(c + 1) * NSB]
        for k in ("norm_w", "w_ada", "b_ada", "w_in", "ssd_conv_w", "ssd_conv_b", "ssd_dt_bias", "ssd_a_log", "ssd_d",
                  "ssd_norm_w", "gdn_conv_w", "gdn_dt_bias", "gdn_a_log", "gdn_norm_w", "w_out"):
            m[k] = g[k]
        m["final_norm_w"] = g["final_norm_w"].reshape(1, D)
        for k, v in consts.items():
            m["c_" + k] = v
        maps.append({k: np.ascontiguousarray(v) for k, v in m.items()})
    return maps


def kernel(**inputs):
    if "nc" not in _CACHE:
        _CACHE["nc"] = build()[0]
    nc = _CACHE["nc"]
    maps = make_in_maps(inputs)
    res = run_bass_kernel_spmd(nc, maps, core_ids=list(range(8)))
    R = res.results
    cat = lambda k, ax: np.concatenate([np.asarray(r[k]) for r in R], axis=ax)
    y_p = np.stack([np.asarray(r["y_p"]) for r in R], 0)
    y_s = np.stack([np.asarray(r["y_s"]).reshape(NSB, SL, D) for r in R], 0).reshape(8 * NSB, SL, D)
    ncs_p = np.stack([np.asarray(r["ncs_p"]) for r in R], 1)
    nssm_p = np.stack([np.asarray(r["nssm_p"]).reshape(DEPTH, 16, 64, 128) for r in R], 1)
    ngc_p = np.stack([np.asarray(r["ngc_p"]) for r in R], 1)
    ngdn_p = np.stack([np.asarray(r["ngdn_p"]) for r in R], 1)
    ncs_s = np.concatenate([np.asarray(r["ncs_s"]).reshape(DEPTH, NSB, 3, 1536) for r in R], 1)
    nssm_s = np.concatenate([np.asarray(r["nssm_s"]).reshape(DEPTH, NSB, 16, 64, 128) for r in R], 1)
    ngc_s = np.concatenate([np.asarray(r["ngc_s"]).reshape(DEPTH, NSB, 3, 3072) for r in R], 1)
    ngdn_s = np.concatenate([np.asarray(r["ngdn_s"]) for r in R], 1)
    outs = (y_p, y_s, ncs_p, nssm_p, ngc_p, ngdn_p, ncs_s, nssm_s, ngc_s, ngdn_s)
    return tuple(np.ascontiguousarray(o, dtype=np.float32) for o in outs)
```

```python
import contextlib
import numpy as np
import concourse.bass as bass
import concourse.mybir as mybir
from concourse.bass_utils import run_bass_kernel_spmd

F32 = mybir.dt.float32
F32R = mybir.dt.float32r
BF16 = mybir.dt.bfloat16
AF = mybir.ActivationFunctionType
ALU = mybir.AluOpType
AX = mybir.AxisListType

ENGS = ("pe", "dve", "act", "pool", "sp")


class Buf:
    def __init__(self, t, name):
        self.t = t
        self.name = name
        self.st = {}
        self.dma_sem = None
        self.dma_cnt = 0

    def __getitem__(self, k):
        return self.t[k]


class Instr:
    __slots__ = ("eng", "fn", "deps", "signal", "count", "is_dma", "buf_sem", "dma_count")

    def __init__(self, eng, fn):
        self.eng = eng
        self.fn = fn
        self.deps = []
        self.signal = False
        self.count = None
        self.is_dma = False
        self.buf_sem = None
        self.dma_count = None


class Sched:
    def __init__(self, nc):
        self.nc = nc
        self.streams = {e: [] for e in ENGS}
        self.all = []
        self.filler = None
        self.nfill = 0
        self.fill_table = {}
        self.pe_groups = 0

    def _states(self, buf, key):
        st = buf.st
        if key is None:
            if None not in st:
                st[None] = [None, {}]
            return list(st.values())
        if key not in st:
            if None in st:
                st[key] = [st[None][0], dict(st[None][1])]
            else:
                st[key] = [None, {}]
        res = [st[key]]
        if None in st:
            res.append(st[None])
        return res

    @staticmethod
    def _norm(lst):
        out = []
        for x in lst or []:
            out.append((x, None) if isinstance(x, Buf) else x)
        return out

    def add(self, eng, fn, reads=None, writes=None, dma_buf=None):
        ins = Instr(eng, fn)
        ins.is_dma = dma_buf is not None
        reads = self._norm(reads)
        writes = self._norm(writes)
        deps, raw = [], []
        for buf, key in reads:
            for s in self._states(buf, key):
                if s[0] is not None:
                    deps.append(s[0])
                    raw.append(s[0])
                if getattr(buf, "excl", False):
                    for r in s[1].values():
                        if r.eng != eng:
                            deps.append(r)
        for buf, key in writes:
            for s in self._states(buf, key):
                if s[0] is not None:
                    deps.append(s[0])
                deps.extend(s[1].values())
        fdeps = {}
        for d in deps:
            if d is ins:
                continue
            if (not ins.is_dma) and (not d.is_dma) and d.eng == eng:
                if eng == "pe" or not any(d is r for r in raw):
                    continue
            if d.is_dma:
                fdeps[id(d)] = (d, d.buf_sem.dma_cnt * 16)
            else:
                fdeps[id(d)] = (d, None)
        ins.deps = list(fdeps.values())
        if ins.is_dma:
            ins.buf_sem = dma_buf
            dma_buf.dma_cnt += 1
            ins.dma_count = dma_buf.dma_cnt * 16
        rkey = ("dma", id(ins)) if ins.is_dma else eng
        for buf, key in reads:
            for s in self._states(buf, key):
                s[1][rkey] = ins
        for buf, key in writes:
            for s in self._states(buf, key):
                s[0] = ins
                if key is None or s is not buf.st.get(None):
                    s[1] = {}
        self.streams[eng].append(ins)
        self.all.append(ins)
        return ins

    def barrier(self, dma_bufs_all):
        last = {}
        for e in ("pe", "dve", "act", "pool"):
            for ins in reversed(self.streams[e]):
                if not isinstance(ins, tuple) and not ins.is_dma:
                    last[e] = ins
                    ins.signal = True
                    break
        snap = [(b, b.dma_cnt * 16) for b in dma_bufs_all if b.dma_cnt > 0]
        mark = ("barrier", last, snap)
        for e in ENGS:
            self.streams[e].append(mark)

    def emit(self, final_wait_eng="sp"):
        nc = self.nc
        for ins in self.all:
            for d, _v in ins.deps:
                if not d.is_dma:
                    d.signal = True
        with contextlib.ExitStack() as es:
            eng_sem = {e: es.enter_context(nc.semaphore("s_" + e)) for e in ("pe", "dve", "act", "pool")}
            dma_bufs = []
            for ins in self.all:
                if isinstance(ins, tuple):
                    continue
                if ins.is_dma and ins.buf_sem.dma_sem is None:
                    ins.buf_sem.dma_sem = es.enter_context(nc.semaphore("d_%d" % len(dma_bufs)))
                    dma_bufs.append(ins.buf_sem)
            cnt = {e: 0 for e in eng_sem}
            for ins in self.all:
                if not ins.is_dma and ins.signal:
                    cnt[ins.eng] += 1
                    ins.count = cnt[ins.eng]
            block = es.enter_context(nc.Block())
            engobj = {"pe": block.tensor, "dve": block.vector, "act": block.scalar, "pool": block.gpsimd,
                      "sp": block.sync}
            sched = self

            def make(ename):
                def body(eng):
                    water = {}
                    for ins in sched.streams[ename]:
                        if isinstance(ins, tuple):
                            _, last, snap = ins
                            nw_ = 0
                            for e2, li in last.items():
                                if e2 != ename and water.get(("e", e2), 0) < li.count:
                                    water[("e", e2)] = li.count
                                    eng.wait_ge(eng_sem[e2], li.count)
                                    nw_ += 1
                            for b_, v_ in snap:
                                if water.get(("d", id(b_)), 0) < v_:
                                    water[("d", id(b_))] = v_
                                    eng.wait_ge(b_.dma_sem, v_)
                                    nw_ += 1
                            continue
                        need = {}
                        for d, dv in ins.deps:
                            if d.is_dma:
                                key, sem, val = ("d", id(d.buf_sem)), d.buf_sem.dma_sem, dv
                            else:
                                key, sem, val = ("e", d.eng), eng_sem[d.eng], d.count
                            if need.get(key, (None, 0))[1] < val:
                                need[key] = (sem, val)
                        todo = [(key, sem, val) for key, (sem, val) in need.items() if water.get(key, 0) < val]
                        if ename == "pe" and not ins.is_dma:
                            k_ = sched.pe_groups
                            sched.pe_groups += 1
                            if todo and sched.filler is not None:
                                for _ in range(sched.fill_table.get(k_, 0)):
                                    sched.filler(eng)
                        for key, sem, val in todo:
                            water[key] = val
                            eng.wait_ge(sem, val)
                        bi = ins.fn(eng)
                        if ins.is_dma:
                            bi.then_inc(ins.buf_sem.dma_sem, 16)
                        elif ins.signal:
                            bi.then_inc(eng_sem[ename], 1)
                    if ename == final_wait_eng:
                        for b in dma_bufs:
                            eng.wait_ge(b.dma_sem, b.dma_cnt * 16)
                return body

            for ename in ENGS:
                engobj[ename](make(ename))


FILL_TABLE = (
    "8:12,16:3,24:2,32:8,40:3,48:2,56:8,64:3,72:2,80:8,88:3,96:2,104:8,112:3,120:3,128:8,136:3,145:2,154:7,163:1,17"
    "2:2,181:7,190:2,199:2,208:1,212:60,220:6,316:2,384:7,387:1,417:12,437:5,457:26,465:2,481:8,489:7,585:2,653:7,6"
    "56:1,686:12,706:5,726:26,734:2,750:8,758:7,854:2,922:7,925:1,955:12,975:5,995:26,1003:2,1019:8,1027:7,1123:2,1"
    "191:7,1194:1,1224:12,1244:6,1264:26,1272:2,1288:8,1296:7,1392:2,1460:7,1463:1,1493:12,1513:6,1533:26,1541:2,15"
    "57:8,1565:7,1661:2,1729:7,1732:1,1762:12,1782:6,1802:26,1810:2,1826:8,1834:7,1930:2,1998:7,2001:1,2031:12,2051"
    ":6,2071:26,2079:2,2095:8,2103:7,2199:2,2267:7,2270:1,2300:12,2320:6,2340:26,2348:2,2364:8,2372:7,2468:2,2536:7"
    ",2539:1,2569:12,2589:6,2609:26,2617:2,2633:8,2641:7,2737:2,2805:7,2808:1,2838:12,2858:6,2878:26,2886:2,2902:8,"
    "2910:7,3006:2,3074:7,3077:1,3107:12,3127:6,3147:27,3155:2,3171:8,3179:7,3275:2,3343:7,3346:1,3376:12,3396:6,34"
    "16:26,3424:2,3440:8,3448:7,3544:2,3612:7,3615:1,3645:12,3665:6,3685:26,3693:2,3709:8,3717:7,3813:2,3881:7,3884"
    ":1,3914:12,3934:6,3954:26,3962:2,3978:8,3986:7,4082:2,4150:7,4153:1,4183:12,4203:6,4223:26,4231:2,4247:8,4255:"
    "7,4351:2,4431:8,4434:1,4464:12,4484:6,4488:3,4496:7,4512:20,4520:2,4536:8,4544:12,4652:4,4732:8,4735:2,4765:13"
    ",4789:12,4793:2,4801:19,4809:12,4813:3,4821:19,4829:12,4833:2,4841:19,4849:12,4853:2,4861:17,4869:12,4873:3,48"
    "81:16,4889:13,4893:4,4901:18,4909:12,4913:2,4921:17,4929:15,4933:2,4941:19,4949:14,4953:3,4961:18,4969:12,4973"
    ":2,4981:17,4989:12,4993:3,5001:19,5009:12,5013:3,5021:16,5029:13,5033:2,5041:18,5049:12,5053:2,5061:17,5069:11"
    ",5073:3,5081:16,5089:13,5093:2,5101:5,5117:20,5125:1,5141:60,5149:7,5245:2,5301:11,5304:1,5313:9,5317:6,5342:6"
    ",5350:3,5394:2,5398:2,5406:2,5414:2,5418:4,5442:10,5446:1,5454:11,5462:7,5558:2,5614:11,5617:1,5626:9,5630:7,5"
    "655:6,5663:3,5675:1,5707:2,5711:2,5719:2,5727:1,5731:4,5739:1,5755:14,5759:1,5767:10,5775:6,5871:2,5927:11,593"
    "0:1,5939:9,5943:7,5968:6,5976:3,6020:2,6024:2,6032:2,6040:1,6044:4,6052:1,6068:14,6072:1,6080:11,6088:7,6184:2"
    ",6240:11,6243:1,6252:9,6256:6,6281:6,6289:3,6333:2,6337:2,6345:2,6353:1,6357:4,6381:12,6385:1,6393:11,6401:7,6"
    "497:2,6553:11,6556:1,6565:10,6569:6,6594:7,6602:3,6646:2,6650:2,6658:2,6666:1,6670:4,6694:12,6698:1,6706:10,67"
    "14:6,6810:2,6866:11,6869:1,6878:10,6882:7,6907:6,6915:3,6959:2,6963:2,6971:2,6979:1,6983:4,7007:11,7011:1,7019"
    ":11,7027:7,7123:2,7179:11,7182:1,7191:11,7195:7,7220:6,7228:3,7272:2,7276:2,7284:2,7292:1,7296:4,7320:11,7324:"
    "1,7332:11,7340:7,7436:2,7492:12,7495:1,7504:10,7508:7,7533:6,7541:3,7585:2,7589:2,7597:2,7605:1,7609:4,7633:11"
    ",7637:1,7645:11,7653:7,7749:2,7805:11,7808:1,7817:10,7821:7,7846:6,7854:3,7898:2,7902:2,7910:2,7918:1,7922:4,7"
    "946:11,7950:1,7958:11,7966:7,8062:2,8118:11,8121:1,8130:10,8134:7,8159:6,8167:3,8211:2,8215:2,8223:2,8231:1,82"
    "35:4,8259:11,8263:1,8271:11,8279:7,8375:2,8431:11,8434:1,8443:9,8447:6,8472:6,8480:3,8524:2,8528:2,8536:2,8544"
    ":1,8548:4,8572:11,8576:1,8584:11,8592:7,8688:2,8744:11,8747:1,8756:10,8760:7,8785:6,8793:3,8837:2,8841:2,8849:"
    "2,8857:1,8861:4,8885:11,8889:1,8897:11,8905:7,9001:2,9057:11,9060:1,9069:10,9073:7,9098:6,9106:3,9150:2,9154:2"
    ",9162:2,9170:1,9174:4,9198:11,9202:1,9210:11,9218:7,9314:2,9370:11,9373:1,9382:10,9386:7,9411:6,9419:3,9463:2,"
    "9467:2,9475:2,9483:1,9487:4,9511:11,9515:1,9523:11,9531:7,9627:2,9683:11,9686:1,9695:10,9699:7,9724:6,9732:3,9"
    "776:2,9780:2,9788:2,9796:1,9800:4,9824:11,9828:1,9836:11,9844:7,9940:2,10008:12,10011:1,10020:10,10024:7,10049"
    ":6,10057:3,10101:2,10105:2,10113:2,10121:1,10125:4,10149:11,10153:1,10161:11,10169:18,10277:4,10345:11,10348:1"
    ",10357:25,10361:6,10386:4,10390:4,10394:1,10402:3,10436:1,10445:2,10450:50,10484:1,10493:2,10498:50,10532:1,10"
    "541:2,10546:49,10589:2,10606:7,10610:1,10618:60,10626:6,10722:2,10778:11,10781:1,10790:9,10794:7,10819:6,10827"
    ":3,10871:2,10875:2,10883:2,10891:1,10895:4,10903:1,10919:14,10923:1,10931:10,10939:6,11035:2,11091:11,11094:1,"
    "11103:9,11107:7,11132:6,11140:3,11184:2,11188:2,11196:2,11204:1,11208:4,11216:1,11232:14,11236:1,11244:11,1125"
    "2:7,11348:2,11404:11,11407:1,11416:9,11420:6,11445:6,11453:3,11497:2,11501:2,11509:2,11517:1,11521:4,11545:11,"
    "11549:1,11557:11,11565:7,11661:2,11717:11,11720:1,11729:10,11733:7,11758:6,11766:3,11810:2,11814:2,11822:2,118"
    "30:1,11834:4,11858:11,11862:1,11870:11,11878:7,11974:2,12030:11,12033:1,12042:10,12046:7,12071:6,12079:3,12123"
    ":2,12127:2,12135:2,12143:1,12147:4,12171:11,12175:1,12183:11,12191:7,12287:2,12343:11,12346:1,12355:10,12359:7"
    ",12384:6,12392:3,12436:2,12440:2,12448:2,12456:1,12460:4,12484:11,12488:1,12496:11,12504:7,12600:2,12656:11,12"
    "659:1,12668:10,12672:7,12697:6,12705:3,12749:2,12753:2,12761:2,12769:1,12773:4,12797:11,12801:1,12809:11,12817"
    ":7,12913:2,12969:11,12972:1,12981:10,12985:7,13010:6,13018:3,13062:2,13066:2,13074:2,13082:1,13086:4,13110:11,"
    "13114:1,13122:11,13130:7,13226:2,13282:12,13285:1,13294:9,13298:6,13323:6,13331:3,13375:2,13379:2,13387:2,1339"
    "5:1,13399:4,13423:11,13427:1,13435:10,13443:6,13539:2,13595:11,13598:1,13607:9,13611:7,13636:6,13644:3,13688:2"
    ",13692:2,13700:2,13708:1,13712:4,13720:1,13736:14,13740:1,13748:11,13756:7,13852:2,13908:11,13911:1,13920:10,1"
    "3924:7,13949:6,13957:3,14001:2,14005:2,14013:2,14021:1,14025:4,14049:11,14053:1,14061:11,14069:7,14165:2,14221"
    ":11,14224:1,14233:10,14237:7,14262:6,14270:3,14314:2,14318:2,14326:2,14334:1,14338:4,14362:11,14366:1,14374:11"
    ",14382:7,14478:2,14534:11,14537:1,14546:9,14550:6,14575:6,14583:3,14627:2,14631:2,14639:2,14647:1,14651:4,1467"
    "5:11,14679:1,14687:11,14695:7,14791:2,14847:11,14850:1,14859:9,14863:6,14888:6,14896:3,14940:2,14944:2,14952:2"
    ",14960:1,14964:4,14988:12,14992:1,15000:11,15008:6,15104:2,15160:11,15163:1,15172:10,15176:7,15201:6,15209:3,1"
    "5253:2,15257:2,15265:2,15273:1,15277:4,15301:10,15305:1,15313:11,15321:7,15417:2,15485:12,15488:1,15497:9,1550"
    "1:7,15526:6,15534:3,15578:2,15582:2,15590:2,15598:1,15602:4,15626:10,15630:1,15638:11,15646:18,15754:3,15822:1"
    "1,15825:1,15834:25,15838:6,15863:4,15867:4,15871:1,15879:3,15913:2,15922:2,15927:50,15961:1,15970:2,15975:51,1"
    "6009:1,16018:2,16023:48,16057:1,16066:2,16083:7,16087:1,16095:30,16103:18,16111:3,16119:3,16127:8,16135:3,1614"
    "3:2,16151:8,16159:3,16167:2,16175:8,16183:3,16191:2,16199:8,16207:3,16215:2,16223:8,16231:3,16240:2,16249:7,16"
    "258:1,16267:2,16276:7,16285:1,16294:2,16303:1,16307:60,16315:6,16411:2,16479:5,16482:1,16512:12,16532:7,16552:"
    "26,16560:2,16576:8,16584:7,16680:2,16748:7,16751:1,16781:12,16801:6,16821:27,16829:2,16845:8,16853:7,16949:2,1"
    "7017:7,17020:1,17050:12,17070:6,17090:26,17098:2,17114:8,17122:7,17218:2,17286:7,17289:1,17319:12,17339:6,1735"
    "9:26,17367:2,17383:8,17391:7,17487:2,17555:7,17558:1,17588:12,17608:6,17628:26,17636:2,17652:8,17660:7,17756:2"
    ",17824:7,17827:1,17857:12,17877:6,17897:26,17905:2,17921:8,17929:7,18025:2,18093:7,18096:1,18126:12,18146:6,18"
    "166:26,18174:2,18190:8,18198:7,18294:2,18362:7,18365:1,18395:12,18415:6,18435:26,18443:2,18459:8,18467:7,18563"
    ":2,18631:7,18634:1,18664:12,18684:6,18704:26,18712:2,18728:8,18736:7,18832:2,18900:7,18903:1,18933:12,18953:6,"
    "18973:26,18981:2,18997:8,19005:7,19101:2,19169:7,19172:1,19202:12,19222:6,19242:26,19250:2,19266:8,19274:7,193"
    "70:2,19438:5,19441:1,19471:12,19491:5,19511:26,19519:2,19535:8,19543:7,19639:2,19707:7,19710:1,19740:12,19760:"
    "5,19780:26,19788:2,19804:8,19812:7,19908:2,19976:7,19979:1,20009:12,20029:6,20049:27,20057:2,20073:8,20081:7,2"
    "0177:2,20245:7,20248:1,20278:12,20298:6,20318:26,20326:2,20342:8,20350:7,20446:2,20526:8,20529:1,20559:12,2057"
    "9:6,20583:3,20591:7,20607:20,20615:2,20631:8,20639:12,20747:3,20827:8,20830:2,20860:13,20884:12,20888:3,20896:"
    "17,20904:12,20908:3,20916:16,20924:13,20928:2,20936:20,20944:13,20948:2,20956:18,20964:12,20968:2,20976:17,209"
    "84:12,20988:4,20996:16,21004:13,21008:2,21016:19,21024:12,21028:2,21036:18,21044:15,21048:3,21056:16,21064:13,"
    "21068:3,21076:18,21084:13,21088:2,21096:17,21104:12,21108:3,21116:16,21124:13,21128:2,21136:18,21144:12,21148:"
    "2,21156:17,21164:15,21168:2,21176:23,21184:12,21188:2,21196:5,21212:20,21220:2,21236:60,21244:7,21340:2,21396:"
    "11,21399:1,21408:10,21412:7,21437:6,21445:3,21489:2,21493:2,21501:2,21509:1,21513:4,21537:11,21541:1,21549:11,"
    "21557:7,21653:2,21709:11,21712:1,21721:10,21725:7,21750:6,21758:3,21802:2,21806:2,21814:2,21822:1,21826:4,2185"
    "0:11,21854:1,21862:10,21870:6,21966:2,22022:11,22025:1,22034:9,22038:7,22063:6,22071:3,22115:2,22119:2,22127:2"
    ",22135:1,22139:4,22147:1,22163:14,22167:1,22175:11,22183:7,22279:2,22335:11,22338:1,22347:10,22351:6,22376:6,2"
    "2384:3,22428:2,22432:2,22440:2,22448:1,22452:4,22476:12,22480:1,22488:10,22496:6,22592:2,22648:11,22651:1,2266"
    "0:9,22664:7,22689:6,22697:3,22741:2,22745:2,22753:2,22761:1,22765:4,22789:11,22793:1,22801:10,22809:6,22905:2,"
    "22961:11,22964:1,22973:9,22977:7,23002:6,23010:3,23054:2,23058:2,23066:2,23074:1,23078:4,23086:1,23102:14,2310"
    "6:1,23114:11,23122:7,23218:2,23274:11,23277:1,23286:9,23290:6,23315:6,23323:3,23367:2,23371:2,23379:2,23387:1,"
    "23391:4,23415:11,23419:1,23427:11,23435:7,23531:2,23587:11,23590:1,23599:10,23603:7,23628:6,23636:3,23680:2,23"
    "684:2,23692:2,23700:1,23704:4,23728:11,23732:1,23740:11,23748:7,23844:2,23900:11,23903:1,23912:10,23916:7,2394"
    "1:6,23949:3,23993:2,23997:2,24005:2,24013:1,24017:4,24041:11,24045:1,24053:11,24061:7,24157:2,24213:11,24216:1"
    ",24225:10,24229:7,24254:6,24262:3,24306:2,24310:2,24318:2,24326:1,24330:4,24354:12,24358:1,24366:11,24374:7,24"
    "470:2,24526:11,24529:1,24538:10,24542:7,24567:7,24575:3,24619:2,24623:2,24631:2,24639:1,24643:4,24667:12,24671"
    ":1,24679:10,24687:6,24783:2,24839:11,24842:1,24851:10,24855:7,24880:6,24888:3,24932:2,24936:2,24944:2,24952:1,"
    "24956:4,24964:1,24980:14,24984:1,24992:10,25000:6,25096:2,25152:11,25155:1,25164:11,25168:8,25193:6,25201:3,25"
    "245:2,25249:2,25257:2,25265:1,25269:4,25293:11,25297:1,25305:11,25313:7,25409:2,25465:11,25468:1,25477:9,25481"
    ":6,25506:6,25514:3,25558:2,25562:2,25570:2,25578:1,25582:4,25606:12,25610:1,25618:10,25626:6,25722:2,25778:11,"
    "25781:1,25790:9,25794:7,25819:6,25827:3,25871:2,25875:2,25883:2,25891:1,25895:4,25919:10,25923:1,25931:11,2593"
    "9:7,26035:2,26103:12,26106:1,26115:9,26119:7,26144:6,26152:3,26196:2,26200:2,26208:2,26216:1,26220:4,26244:10,"
    "26248:1,26256:11,26264:17,26372:3,26440:11,26443:1,26452:24,26456:6,26481:4,26485:4,26489:1,26497:3,26531:2,26"
    "540:2,26545:49,26588:2,26593:50,26627:1,26636:2,26641:49,26675:1,26684:2,26701:7,26705:1,26713:60,26721:6,2681"
    "7:2,26873:11,26876:1,26885:9,26889:7,26914:6,26922:3,26966:2,26970:2,26978:2,26986:1,26990:4,27014:10,27018:1,"
    "27026:32,27034:6,27130:2,27186:11,27189:1,27198:10,27202:7,27227:6,27235:3,27279:2,27283:2,27291:2,27299:1,273"
    "03:4,27327:12,27331:1,27339:32,27347:6,27443:2,27499:11,27502:1,27511:9,27515:7,27540:6,27548:3,27592:2,27596:"
    "2,27604:2,27612:1,27616:4,27624:1,27640:14,27644:1,27652:32,27660:6,27756:2,27812:11,27815:1,27824:10,27828:7,"
    "27853:7,27861:3,27905:2,27909:2,27917:2,27925:1,27929:4,27953:11,27957:1,27965:32,27973:6,28069:2,28125:11,281"
    "28:1,28137:9,28141:7,28166:6,28174:3,28218:2,28222:2,28230:2,28238:1,28242:4,28250:1,28266:14,28270:1,28278:32"
    ",28286:6,28382:2,28438:11,28441:2,28450:10,28454:7,28479:6,28487:3,28531:2,28535:2,28543:2,28551:1,28555:4,285"
    "79:11,28583:1,28591:31,28599:6,28695:2,28751:11,28754:1,28763:10,28767:7,28792:6,28800:3,28844:2,28848:2,28856"
    ":2,28864:1,28868:4,28892:11,28896:1,28904:32,28912:6,29008:2,29064:11,29067:1,29076:9,29080:7,29105:6,29113:3,"
    "29157:2,29161:2,29169:2,29177:1,29181:4,29189:1,29205:14,29209:1,29217:31,29225:6,29321:2,29377:11,29380:1,293"
    "89:10,29393:7,29418:6,29426:3,29470:2,29474:2,29482:2,29490:1,29494:4,29518:10,29522:1,29530:32,29538:6,29634:"
    "2,29690:11,29693:1,29702:9,29706:7,29731:6,29739:3,29783:2,29787:2,29795:2,29803:1,29807:4,29831:11,29835:1,29"
    "843:31,29851:6,29947:2,30003:11,30006:1,30015:10,30019:7,30044:6,30052:3,30096:2,30100:2,30108:2,30116:1,30120"
    ":4,30144:10,30148:1,30156:32,30164:6,30260:2,30316:11,30319:1,30328:9,30332:7,30357:6,30365:3,30409:2,30413:2,"
    "30421:2,30429:1,30433:4,30457:11,30461:1,30469:31,30477:6,30573:2,30629:11,30632:1,30641:10,30645:7,30670:6,30"
    "678:3,30722:2,30726:2,30734:2,30742:1,30746:4,30770:11,30774:1,30782:32,30790:6,30886:2,30942:11,30945:1,30954"
    ":9,30958:7,30983:6,30991:3,31035:2,31039:2,31047:2,31055:1,31059:4,31083:11,31087:1,31095:32,31103:6,31199:2,3"
    "1255:11,31258:1,31267:9,31271:7,31296:6,31304:3,31348:2,31352:2,31360:2,31368:1,31372:4,31396:10,31400:1,31408"
    ":32,31416:6,31512:2,31580:12,31583:1,31592:9,31596:7,31621:6,31629:3,31673:2,31677:2,31685:2,31693:1,31697:4,3"
    "1721:9,31725:1,31733:32,31741:11,31849:3,31917:11,31920:1,31929:19,31933:12,31958:7,31962:4,31966:1,31974:3,32"
    "008:2,32017:2,32022:49,32056:1,32065:2,32070:50,32104:1,32113:2,32118:51,32152:1,32161:2,32178:6,32182:1"
)
D = 1024
SEQ = 2048
NT = SEQ // 128
NSB = 16
SL = 4
DEPTH = 2
IN_DIM = 6688
EPS = 1e-6
NEG = -30000.0
OFF_ZS, OFF_XBC, OFF_DT, OFF_ZG, OFF_QKV, OFF_A, OFF_B = 0, 1024, 2560, 2576, 3600, 6672, 6680


class Cfg:
    def __init__(self, name, T, nseq, L, Q, chain):
        self.name, self.T, self.nseq, self.L, self.Q, self.chain = name, T, nseq, L, Q, chain
        self.nch = T // Q
        self.nlev = {64: 5, 4: 1}[Q]


CP = Cfg("p", 128, 1, 128, 64, True)
CS = Cfg("s", 64, NSB, SL, 4, False)


def host_consts():
    c = {"ident": np.eye(128, dtype=np.float32), "ones": np.ones((128, 128), np.float32)}
    for cfg in (CP, CS):
        T, Q, nch = cfg.T, cfg.Q, cfg.nch
        ch = np.arange(T) // Q
        same = ch[:, None] == ch[None, :]
        idx = np.arange(T)
        le = idx[:, None] <= idx[None, :]
        lt = idx[:, None] < idx[None, :]
        n = cfg.name
        c["tri_" + n] = (same & le).astype(np.float32)
        c["blk_" + n] = same.astype(np.float32)
        c["nmI_" + n] = np.where(same & le, 0.0, NEG).astype(np.float32)
        c["nmS_" + n] = np.where(same & lt, 0.0, NEG).astype(np.float32)
        c["nmSr_" + n] = np.ascontiguousarray(c["nmS_" + n].T)
        cm = (ch[None, :] == np.arange(nch)[:, None]).astype(np.float32)
        c["cmf_" + n] = np.ascontiguousarray(np.broadcast_to(cm[None], (128, nch, T))).astype(np.float32)
        c["cmT_" + n] = np.ascontiguousarray(cm.T)
    e = np.zeros((17, 128), np.float32)
    e[0, :] = 1.0
    c["E_p"] = e
    e = np.zeros((17, 64), np.float32)
    for t in range(64):
        e[1 + t // SL, t] = 1.0
    c["E_s"] = e
    return c


def bl(ap, n):
    return ap.unsqueeze(len(ap.shape)).to_broadcast(list(ap.shape) + [n])


def bm(ap, n):
    return ap.unsqueeze(1).to_broadcast([ap.shape[0], n] + list(ap.shape[1:]))


def build(debug=(), limit=None):
    limit = limit or {}
    nc = bass.Bass("TRN2", target_bir_lowering=False)
    S = Sched(nc)
    es = contextlib.ExitStack()
    consts_np = host_consts()

    def din(name, shape):
        return nc.dram_tensor(name, list(shape), F32, kind="ExternalInput").ap()

    def dout(name, shape):
        return nc.dram_tensor(name, list(shape), F32, kind="ExternalOutput").ap()

    I = {}
    I["xp"] = din("xp", [SEQ, D])
    I["xs"] = din("xs", [NSB * SL, D])
    I["call"] = din("call", [17, D])
    I["st_sc"] = din("st_sc", [DEPTH, NSB, 3, 1536])
    I["st_ssm"] = din("st_ssm", [DEPTH, NSB, 16, 64, 128])
    I["st_gc"] = din("st_gc", [DEPTH, NSB, 3, 3072])
    I["st_gdn"] = din("st_gdn", [DEPTH, NSB, 8, 128, 128])
    wshapes = {"norm_w": [DEPTH, D], "w_ada": [DEPTH, D, 3 * D], "b_ada": [DEPTH, 3 * D], "w_in": [DEPTH, D, IN_DIM],
               "ssd_conv_w": [DEPTH, 4, 1536], "ssd_conv_b": [DEPTH, 1536], "ssd_dt_bias": [DEPTH, 16],
               "ssd_a_log": [DEPTH, 16], "ssd_d": [DEPTH, 16], "ssd_norm_w": [DEPTH, 1024],
               "gdn_conv_w": [DEPTH, 4, 3072], "gdn_dt_bias": [DEPTH, 8], "gdn_a_log": [DEPTH, 8],
               "gdn_norm_w": [DEPTH, 128], "w_out": [DEPTH, 2048, D], "final_norm_w": [1, D]}
    for k, shp in wshapes.items():
        I[k] = din(k, shp)
    for k, v in consts_np.items():
        I["c_" + k] = din("c_" + k, v.shape)
    O = {}
    O["y_p"] = dout("y_p", [SEQ, D])
    O["y_s"] = dout("y_s", [NSB * SL, D])
    O["ncs_p"] = dout("ncs_p", [DEPTH, 3, 1536])
    O["nssm_p"] = dout("nssm_p", [DEPTH, 1024, 128])
    O["ngc_p"] = dout("ngc_p", [DEPTH, 3, 3072])
    O["ngdn_p"] = dout("ngdn_p", [DEPTH, 8, 128, 128])
    O["ncs_s"] = dout("ncs_s", [DEPTH, NSB * 3, 1536])
    O["nssm_s"] = dout("nssm_s", [DEPTH, NSB, 1024, 128])
    O["ngc_s"] = dout("ngc_s", [DEPTH, NSB * 3, 3072])
    O["ngdn_s"] = dout("ngdn_s", [DEPTH, NSB, 8, 128, 128])
    NROW = SEQ + NSB * SL
    scr = [Buf(nc.dram_tensor("scr%d" % i, [NROW, D], F32, kind="Internal").ap(), "scr%d" % i) for i in range(3)]
    xin_ext = None

    allb = []

    def sb(name, shape, dt=F32):
        b_ = Buf(es.enter_context(nc.sbuf_tensor(name, list(shape), dt)), name)
        allb.append(b_)
        return b_

    banks = [Buf(es.enter_context(nc.psum_tensor("bank%d" % i, [128, 512], F32)), "bank%d" % i) for i in range(8)]
    for b_ in banks:
        b_.excl = True
    pstate = {"i": 0, "pinned": set()}
    NB = 7

    def pget():
        while True:
            i = pstate["i"] % NB
            pstate["i"] += 1
            if i not in pstate["pinned"]:
                return banks[i]

    def ppin(b):
        pstate["pinned"].add(banks.index(b))

    def punpin(b):
        pstate["pinned"].discard(banks.index(b))

    def pf(b, shape, p=128):
        n = int(np.prod(shape))
        ap = b[:p, 0:n]
        if len(shape) == 1:
            return ap
        names = " ".join("a%d" % i for i in range(len(shape)))
        kw = {"a%d" % i: shape[i] for i in range(1, len(shape))}
        return ap.rearrange("p (%s) -> p %s" % (names, names), **kw)

    def pb(b, shape, p=128):
        n = int(np.prod(shape))
        ap = b[:p, :].bitcast(BF16)[:, 0:n]
        if len(shape) == 1:
            return ap
        names = " ".join("a%d" % i for i in range(len(shape)))
        kw = {"a%d" % i: shape[i] for i in range(1, len(shape))}
        return ap.rearrange("p (%s) -> p %s" % (names, names), **kw)

    def MM(out, lhsT, rhs, R, W, start=True, stop=True, **kw):
        S.add("pe", lambda e: e.matmul(out, lhsT=lhsT, rhs=rhs, start=start, stop=stop, **kw), reads=R, writes=W)

    def TR(out, in_, ident, R, W):
        S.add("pe", lambda e: e.transpose(out=out, in_=in_, identity=ident), reads=R, writes=W)

    def ACT(out, in_, func, R, W, **kw):
        S.add("act", lambda e: e.activation(out=out, in_=in_, func=func, **kw), reads=R, writes=W)

    def TT(out, in0, in1, op, R, W, eng="dve"):
        S.add(eng, lambda e: e.tensor_tensor(out=out, in0=in0, in1=in1, op=op), reads=R, writes=W)

    def TS(out, in0, s1, s2, op0, op1, R, W, eng="dve"):
        S.add(eng, lambda e: e.tensor_scalar(out=out, in0=in0, scalar1=s1, scalar2=s2, op0=op0, op1=op1), reads=R, writes=W)

    def STT(out, in0, scalar, in1, op0, op1, R, W):
        S.add("dve", lambda e: e.scalar_tensor_tensor(out=out, in0=in0, scalar=scalar, in1=in1, op0=op0, op1=op1), reads=R, writes=W)

    def CP_(out, in_, R, W, eng="dve"):
        S.add(eng, lambda e: e.tensor_copy(out=out, in_=in_), reads=R, writes=W)

    def RECIP(out, in_, R, W):
        S.add("dve", lambda e: e.reciprocal(out=out, in_=in_), reads=R, writes=W)

    def RED(out, in_, R, W):
        S.add("dve", lambda e: e.tensor_reduce(out=out, in_=in_, axis=AX.X, op=ALU.add), reads=R, writes=W)

    def MSET(ap, val, W, eng="pool"):
        S.add(eng, lambda e: e.memset(ap, val), writes=W)

    def LD(out, in_, buf, R=None, nonc=False):
        if nonc:
            S.add("sp", lambda e: e.dma_start(out=out, in_=in_, allow_slow_non_contiguous=True), reads=R, writes=[buf], dma_buf=buf)
        else:
            S.add("sp", lambda e: e.dma_start(out=out, in_=in_), reads=R, writes=[buf], dma_buf=buf)

    sto_n = [0]
    sto_dummy = Buf(None, "sto_dummy")

    def STO(out, in_, buf, W=None, extra_reads=None):
        S.add("sp", lambda e: e.dma_start(out=out, in_=in_), reads=[buf] + (extra_reads or []), writes=W, dma_buf=buf)

    dbg_out = {}
    mod_stacks = []
    cur_es = [es]
    all_bufs = []

    def DBG(name, buf, ap, shape):
        if name in debug and name not in dbg_out:
            o = dout("dbg_" + name, shape)
            dbg_out[name] = o
            if ap.dtype != F32:
                tmp = Buf(cur_es[0].enter_context(nc.sbuf_tensor("dbgt_" + name, list(shape), F32)), "dbgt_" + name)
                all_bufs.append(tmp)
                CP_(tmp[:], ap, [buf], [tmp])
                STO(o, tmp[:], tmp)
            else:
                STO(o, ap, buf)

    C = {}
    for k, v in consts_np.items():
        if k.startswith("cmf_"):
            continue
        C[k] = sb("k_" + k, v.shape)
        LD(C[k][:], I["c_" + k], C[k])
    identf = C["ident"]
    onesf = C["ones"]
    identb = sb("identb", [128, 128], BF16)
    CP_(identb[:], identf[:], [identf], [identb])
    onesb = sb("onesb", [128, 128], BF16)
    CP_(onesb[:], onesf[:], [onesf], [onesb])
    cfg_pending = []
    for cfg in (CP, CS):
        n = cfg.name
        cfg.tri, cfg.blk, cfg.nmI, cfg.nmS, cfg.nmSr, cfg.cmT = (C[k + n] for k in ("tri_", "blk_", "nmI_", "nmS_", "nmSr_", "cmT_"))
        cfg.cmf = sb("cmfb_" + n, [128, cfg.nch, cfg.T], BF16)
        cfg.ncmf = sb("ncmfb_" + n, [128, cfg.nch, cfg.T], BF16)
        cfg_pending.append(cfg)
        cfg.cmTb = sb("cmTb_" + n, [cfg.T, cfg.nch], BF16)
        CP_(cfg.cmTb[:], cfg.cmT[:], [cfg.cmT], [cfg.cmTb])

    SW = 128
    stage = [sb("stage%d" % i, [128, 8, SW]) for i in range(2)]
    stage_i = [0]
    WO = sb("wo", [128, 8, D], BF16)
    DIAG = sb("diag", [128, 12, 4, 128], BF16)
    cwT = sb("cwT", [128, 12, 4])
    cbrow = sb("cbrow", [1, 1536], BF16)
    nwT = sb("nwT", [128, 8])
    scT = sb("scT", [128, 8, 17])
    shT = sb("shT", [128, 8, 17])
    gate_bc = {"p": sb("gate_p", [128, D]), "s": sb("gate_s", [64, D])}
    vec16 = sb("vec16", [128, 4, 16])
    vec8 = sb("vec8", [128, 2, 8])
    scTb = sb("scTb", [128, 8, 17], BF16)
    badaT = sb("badaT", [128, 24])
    nw_in = sb("nw_in", [128, 8])
    xt_slots = [sb("xt%d" % i, [128, D]) for i in range(2)]
    xa_slots = [sb("xa%d" % i, [128, D]) for i in range(2)]
    stage_all = stage + xt_slots + xa_slots
    junk = sb("junk", [128, D], BF16)
    xn = sb("xn", [128, D], BF16)
    hT = sb("hT", [128, 8, 128], BF16)
    sm1 = sb("sm1", [128, 8])
    raw = sb("raw", [128, 12, 132], BF16)
    tailf = sb("tailf", [128, 12, 48])
    tailo = sb("tailo", [48, 1536])
    cst_in = tailo
    fT = sb("fT", [128, 12, 128], BF16)
    big = [sb("big%d" % i, [128, D]) for i in range(4)]
    bigb = [sb("bigb%d" % i, [128, D], BF16) for i in range(3)]
    xo = big[3]
    call_sb, sil_c, bgate, gate17, cbrow_f = big[0], bigb[0], big[1], big[2], tailo
    hTf_t = big[0]
    for cfg in cfg_pending:
        n_ = cfg.nch * cfg.T
        tmpv = big[0][:, 0:n_].rearrange("p (c t) -> p c t", t=cfg.T)
        LD(tmpv, I["c_cmf_" + cfg.name], big[0])
        CP_(cfg.cmf[:], tmpv, [big[0]], [cfg.cmf])
        TS(cfg.ncmf[:], tmpv, -1.0, None, ALU.mult, ALU.bypass, [big[0]], [cfg.ncmf])

    def stg():
        i = stage_i[0] % len(stage_all)
        stage_i[0] += 1
        b_ = stage_all[i]
        v = b_[:, :, :] if b_ in stage else b_[:, :].rearrange("p (k n) -> p k n", k=8)
        return b_, v, ("pool", "act", "dve")[i % 3]

    def cast(eng, out, in_, R, W, scale=None):
        if eng == "act":
            if scale is None:
                ACT(out, in_, AF.Copy, R, W)
            else:
                ACT(out, in_, AF.Copy, R, W, scale=scale)
        elif scale is None:
            CP_(out, in_, R, W, eng=eng)
        else:
            TS(out, in_, scale, 1.0, ALU.mult, ALU.mult, R, W, eng=eng)

    def load_cols(dst, dcol, src, ncols):
        c0 = 0
        while c0 < ncols:
            n = min(SW, ncols - c0)
            sbuf_, st, eng = stg()
            LD(st[:, :, 0:n], src[:, c0:c0 + n].rearrange("(k p) n -> p k n", p=128), sbuf_)
            cast(eng, dst[:, :, dcol + c0:dcol + c0 + n], st[:, :, 0:n], [sbuf_], [dst])
            c0 += n

    def load_wo(src_rows, nk, scale_ap_fn):
        for half in range(D // SW):
            sbuf_, st, eng = stg()
            LD(st[:, 0:nk, :], src_rows[:, half * SW:(half + 1) * SW].rearrange("(k p) n -> p k n", p=128), sbuf_)
            for k in range(nk):
                cast(eng, WO[:, k, half * SW:(half + 1) * SW], st[:, k, :], [sbuf_, nwT], [WO], scale=scale_ap_fn(k))

    def build_diag(convw, c0, nchunks):
        for k_ in range(4):
            LD(cwT[:, 0:nchunks, k_], convw[k_, c0 * 128:(c0 + nchunks) * 128].rearrange("(c p) -> p c", p=128), cwT, nonc=True)
        for c in range(nchunks):
            TT(DIAG[:, c, :, :], bm(identf[:], 4), bl(cwT[:, c, :], 128), ALU.mult, [identf, cwT], [DIAG], eng="pool")

    def do_mod(l):
        LD(call_sb[:17, :], I["call"], call_sb)
        ACT(sil_c[:17, :], call_sb[:17, :], AF.Silu, [call_sb], [sil_c])
        b = pget()
        for c in range(8):
            TR(pb(b, [8, 32])[:, c, 0:17], sil_c[:17, c * 128:(c + 1) * 128], identb[:17, :17], [sil_c, identb], [b])
        CP_(scTb[:], pb(b, [8, 32])[:, :, 0:17], [b], [scTb])
        LD(badaT[:], I["b_ada"][l].rearrange("(c p) -> p c", p=128), badaT, nonc=True)
        LD(nw_in[:], I["norm_w"][l].rearrange("(c p) -> p c", p=128), nw_in, nonc=True)
        LD(bgate[0:1, :], I["b_ada"][l:l + 1, 2048:3072], bgate)
        nblk = 3 * D // SW
        per = D // SW
        modes = contextlib.ExitStack()
        wb = Buf(modes.enter_context(nc.sbuf_tensor("modwb%d" % l, [128, 8, SW], BF16)), "modwb")
        all_bufs.append(wb)
        for blk in range(nblk):
            sbuf_, st, eng = stg()
            LD(st, I["w_ada"][l][:, blk * SW:(blk + 1) * SW].rearrange("(k p) n -> p k n", p=128), sbuf_)
            cast(eng, wb[:, :, 0:SW], st, [sbuf_], [wb])
            if blk < 2 * per:
                b = pget()
                nj = SW // 128
                for j in range(nj):
                    for k in range(8):
                        MM(pf(b, [nj, 17])[:, j, :], wb[:, k, j * 128:(j + 1) * 128], scTb[:, k, :], [wb, scTb], [b],
                           start=(k == 0), stop=(k == 7))
                dst = shT if blk < per else scT
                cc = (blk % per) * nj
                TT(dst[:, cc:cc + nj, :], pf(b, [nj, 17]), bl(badaT[:, blk * nj:blk * nj + nj], 17), ALU.add, [b, badaT], [dst])
            else:
                g = pget()
                gc0 = (blk - 2 * per) * SW
                for k in range(8):
                    MM(g[:17, 0:SW], scTb[:, k, :], wb[:, k, 0:SW], [wb, scTb], [g], start=(k == 0), stop=False)
                MM(g[:17, 0:SW], onesf[0:1, 0:17], bgate[0:1, gc0:gc0 + SW], [onesf, bgate], [g], start=False, stop=True)
                CP_(gate17[:17, gc0:gc0 + SW], g[:17, 0:SW], [g], [gate17])
        TS(scT[:], scT[:], 1.0, None, ALU.add, ALU.bypass, [scT], [scT])
        TT(scT[:], scT[:], bl(nw_in[:], 17), ALU.mult, [scT, nw_in], [scT])
        for n, E, T in (("p", C["E_p"], 128), ("s", C["E_s"], 64)):
            for half in range(2):
                b = pget()
                MM(b[:T, :], E[:, :T], gate17[:17, half * 512:(half + 1) * 512], [E, gate17], [b])
                CP_(gate_bc[n][:T, half * 512:(half + 1) * 512], b[:T, :], [b], [gate_bc[n]])
        mod_stacks.append(modes)

    def rows(ti):
        return (ti * 128, 128) if ti < NT else (SEQ, 64)

    def src_ap(layer_in, ti):
        r0, T = rows(ti)
        if layer_in is None:
            return (I["xp"][r0:r0 + T, :] if ti < NT else I["xs"][:, :]), None
        return layer_in[r0:r0 + T, :], (layer_in, ti)

    def prologue(cfg, xt):
        T = cfg.T
        ACT(junk[:T, :], xt[:T, :], AF.Square, [xt], [junk, sm1], accum_out=sm1[:T, 0:1])
        TS(sm1[:T, 0:1], sm1[:T, 0:1], 1.0 / D, EPS, ALU.mult, ALU.add, [sm1], [sm1])
        ACT(sm1[:T, 0:1], sm1[:T, 0:1], AF.Ln, [sm1], [sm1])
        ACT(sm1[:T, 0:1], sm1[:T, 0:1], AF.Exp, [sm1], [sm1], scale=-0.5)
        ACT(xn[:T, :], xt[:T, :], AF.Copy, [xt, sm1], [xn], scale=sm1[:T, 0:1])
        b = pget()
        for c in range(8):
            TR(pb(b, [8, T])[:, c, :], xn[:T, c * 128:(c + 1) * 128], identb[:T, :T], [xn, identb], [b])
        if cfg.nseq == 1:
            sc, sh = bl(scT[:, :, 0], T), bl(shT[:, :, 0], T)
            o1, o2, i1 = hTf_t[:, 0:8 * T].rearrange("p (c t) -> p c t", t=T), hT[:, :, :T], pb(b, [8, T])
        else:
            sc, sh = bl(scT[:, :, 1:17], SL), bl(shT[:, :, 1:17], SL)
            o1 = hTf_t[:, 0:8 * T].rearrange("p (c b l) -> p c b l", c=8, l=SL)
            o2 = hT[:, :, :T].rearrange("p c (b l) -> p c b l", l=SL)
            i1 = pb(b, [8, NSB, SL])
        TT(o1, i1, sc, ALU.mult, [b, scT], [hTf_t])
        TT(o2, o1, sh, ALU.add, [hTf_t, shT], [hT])

    def inproj_conv(cfg, l, wcol0, nchunks, bias_row, st_conv_in, conv_out, conv_out_s, ch0, is_last, first):
        T, L, nseq = cfg.T, cfg.L, cfg.nseq
        rawv = raw[:, :, 0:nseq * (L + 3)].rearrange("p c (b l) -> p c b l", l=L + 3)
        if cfg.nseq == 1:
            if first:
                MSET(raw[:, :, 0:3], 0.0, [raw])
            else:
                CP_(raw[:, 0:nchunks, 0:3], raw[:, 0:nchunks, L:L + 3], [raw], [raw])
        else:
            LD(cst_in[:, 0:nchunks * 128], st_conv_in[:, :, ch0 * 128:(ch0 + nchunks) * 128].rearrange("b k c -> (b k) c"), cst_in)
            for g0 in range(0, nchunks, 8):
                b = pget()
                ng = min(8, nchunks - g0)
                for c in range(ng):
                    TR(pf(b, [8, 48])[:, c, :], cst_in[:, (g0 + c) * 128:(g0 + c + 1) * 128], identf[:48, :48], [cst_in, identf], [b])
                CP_(rawv[:, g0:g0 + ng, :, 0:3], pf(b, [8, 48])[:, 0:ng, :].rearrange("p c (b k) -> p c b k", k=3), [b], [raw])
        if limit.get("stopC", 99) <= 1:
            return
        for g0 in range(0, nchunks, 4):
            b = pget()
            for c in range(4):
                for k in range(8):
                    MM(pf(b, [4, T])[:, c, :], PA["WB"][:, k, wcol0 + (g0 + c) * 128: wcol0 + (g0 + c + 1) * 128], hT[:, k, :T],
                       [PA["WB"], hT], [b], start=(k == 0), stop=(k == 7))
            if nseq == 1:
                ACT(raw[:, g0:g0 + 4, 3:3 + L], pf(b, [4, T]), AF.Copy, [b], [raw])
            else:
                for c in range(4):
                    ACT(rawv[:, g0 + c, :, 3:3 + L], pf(b, [4, T])[:, c, :].rearrange("p (b l) -> p b l", l=L), AF.Copy, [b], [raw])
            if is_last and limit.get("tail", True):
                for c in range(4):
                    ACT(tailf[:, g0 + c, 0:nseq * 3].rearrange("p (b k) -> p b k", k=3),
                        pf(b, [4, T])[:, c, :].rearrange("p (b l) -> p b l", l=L)[:, :, L - 3:L], AF.Copy, [b], [tailf])
        if limit.get("stopC", 99) <= 2:
            return
        for g0 in range(0, nchunks, 4):
            b = pget()
            for c in range(4):
                o = pf(b, [4, T])[:, c, :]
                if nseq > 1:
                    o = o.rearrange("p (b l) -> p b l", l=L)
                for k in range(4):
                    MM(o, DIAG[:, g0 + c, k, :], rawv[:, g0 + c, :, k:k + L] if nseq > 1 else raw[:, g0 + c, k:k + L],
                       [DIAG, raw], [b], start=(k == 0), stop=(k == 3 and bias_row is None))
                if bias_row is not None:
                    MM(pf(b, [4, T])[:, c, :], bias_row[0:1, (g0 + c) * 128:(g0 + c + 1) * 128], onesb[0:1, :T], [bias_row, onesb], [b],
                       start=False, stop=True)
            ACT(fT[:, g0:g0 + 4, :T], pf(b, [4, T]), AF.Silu, [b], [fT])
        if limit.get("stopC", 99) <= 3:
            return
        if is_last:
            n3 = nseq * 3
            for g0 in range(0, nchunks, 4):
                b = pget()
                for c in range(4):
                    TR(pf(b, [4, 128], p=n3)[:, c, :], tailf[:, g0 + c, 0:n3], identf[:, :], [tailf, identf], [b])
                CP_(tailo[:n3, g0 * 128:(g0 + 4) * 128], b[:n3, :], [b], [tailo])
            dst = conv_out[l][:, ch0 * 128:(ch0 + nchunks) * 128] if nseq == 1 else conv_out_s[l][:, ch0 * 128:(ch0 + nchunks) * 128]
            STO(dst, tailo[:n3, 0:nchunks * 128], tailo)

    def cum_and_tot(cfg, a_ap, a_buf, nh, out_c, out_dbc, pw):
        T, nch = cfg.T, cfg.nch
        b = pget()
        MM(b[:T, 0:nh], cfg.tri[:, :], a_ap, [cfg.tri, a_buf], [b])
        MM(b[:T, nh:2 * nh], cfg.blk[:, :], a_ap, [cfg.blk, a_buf], [b])
        ACT(out_c[:T, 0:2 * nh], b[:T, 0:2 * nh], AF.Copy, [b], [out_c])
        b2 = pget()
        MM(b2[:nch, 0:nh], cfg.cmT[:, :], a_ap, [cfg.cmT, a_buf], [b2])
        CP_(tot[:nch, 0:nh], b2[:nch, 0:nh], [b2], [tot])
        TT(totx[:nch, 0:nch * nh].rearrange("p (c h) -> p c h", h=nh), bm(tot[:nch, 0:nh], nch), bl(identf[:nch, :nch], nh), ALU.mult,
           [tot, identf], [totx])
        b3 = pget()
        MM(b3[:pw, 0:nch * nh], onesf[:nch, :pw], totx[:nch, 0:nch * nh], [onesf, totx], [b3])
        ACT(out_dbc[:pw, 0:nch * nh], b3[:pw, 0:nch * nh], AF.Exp, [b3], [out_dbc])

    tot = sb("tot", [16, 16])
    totx = sb("totx", [16, 256])
    sm = {k: sb("sm_" + k, [128, 64]) for k in ("a", "b", "c", "d", "e")}
    dbc = sb("dbc", [128, 256])
    acT = sb("acT", [16, 128])

    PA = {}

    def alloc_A(pes, tag):
        def sbp(name, shape, dt=F32):
            return Buf(pes.enter_context(nc.sbuf_tensor(name + tag, list(shape), dt)), name)
        PA.update(stf=sbp("stf", [128, D]), stb=sbp("stb", [128, D], BF16), hn=sbp("hn", [128, 8, 128]),
                  stf_s=sbp("stf_s", [128, D]), stb_s=sbp("stb_s", [128, D], BF16), Gs=sbp("Gs", [128, 2, 128]),
                  earg=sbp("earg", [128, 4, 128]), Ee=sbp("Ee", [128, 4, 128]), WT=sbp("WT", [128, 16, 128], BF16),
                  Btok=sbp("Btok", [128, 256], BF16), Btm=sbp("Btm", [128, 2, 256], BF16), CTm=sbp("CTm", [128, 2, 2, 128], BF16),
                  WB=sbp("WB", [128, 8, 2576], BF16))
    mslot = [0]

    def ssd_state_update(cfg, c, st_in, stb_in, st_out, yo, xdec, first_c, last_c):
        T, nch = cfg.T, cfg.nch
        Btm, CTm, Btok = PA["Btm"], PA["CTm"], PA["Btok"]
        sl = mslot[0] % 2
        mslot[0] += 1
        TS(Btm[:T, sl, :], Btok[:T, :], cfg.cmT[:, c:c + 1], 1.0, ALU.mult, ALU.mult, [Btok, cfg.cmT], [(Btm, sl)], eng="pool")
        TT(CTm[:, sl, :, :T], fT[:, 10:12, :T], bm(cfg.cmf[:, c, :], 2), ALU.mult, [fT, cfg.cmf], [(CTm, sl)], eng="pool")
        for g in range(2):
            MM(yo[g][:T, :], CTm[:, sl, g, :T], stb_in[:, g * 512:(g + 1) * 512], [(CTm, sl), stb_in], [yo[g]], start=first_c, stop=last_c)
        sp_ = [pget(), pget()]
        for g in range(2):
            MM(sp_[g][:, :], Btm[:T, sl, g * 128:(g + 1) * 128], xdec[:T, g * 512:(g + 1) * 512], [(Btm, sl), xdec], [sp_[g]])
        TT(big[3][:, :].rearrange("p (h q) -> p h q", q=64), st_in[:, :].rearrange("p (h q) -> p h q", q=64),
           bl(dbc[:, c * 16:(c + 1) * 16], 64), ALU.mult, [st_in, dbc], [big[3]])
        for g in range(2):
            TT(st_out[:, g * 512:(g + 1) * 512], big[3][:, g * 512:(g + 1) * 512], sp_[g][:, :], ALU.add, [big[3], sp_[g]], [st_out])

    def phaseA_tile(cfg, l, ti, xt, first, is_last):
        T, L, nseq, nch = cfg.T, cfg.L, cfg.nseq, cfg.nch
        stf, stb, hn, stf_s, stb_s, Gs, earg, Ee, WT, Btok = (PA[k] for k in ("stf", "stb", "hn", "stf_s", "stb_s", "Gs", "earg", "Ee", "WT", "Btok"))
        prologue(cfg, xt)
        if limit.get("stopA", 99) <= 0:
            return
        inproj_conv(cfg, l, OFF_XBC, 12, cbrow, I["st_sc"][l], O["ncs_p"], O["ncs_s"], 0, is_last, first)
        if ti == 0:
            DBG("fT_A", fT, fT[:, :, :T], [128, 12, T])
        if limit.get("stopA", 99) <= 1:
            return
        b = pget()
        for k in range(8):
            MM(b[:T, 0:16], hT[:, k, :T], PA["WB"][:, k, OFF_DT:OFF_DT + 16], [hT, PA["WB"]], [b], start=(k == 0), stop=(k == 7))
        A_, B_, C_, D_, E_ = (sm[k] for k in "abcde")
        TT(A_[:T, 0:16], b[:T, 0:16], vec16[:T, 0, :], ALU.add, [b, vec16], [A_])
        ACT(A_[:T, 0:16], A_[:T, 0:16], AF.Exp, [A_], [A_])
        ACT(A_[:T, 16:32], A_[:T, 0:16], AF.Ln, [A_], [A_], bias=1.0)
        TT(A_[:T, 32:48], A_[:T, 16:32], vec16[:T, 1, :], ALU.mult, [A_, vec16], [A_])
        cum_and_tot(cfg, A_[:T, 32:48], A_, 16, B_, dbc, 128)
        TT(C_[:T, 0:16], B_[:T, 16:32], B_[:T, 0:16], ALU.subtract, [B_], [C_])
        ACT(C_[:T, 0:16], C_[:T, 0:16], AF.Exp, [C_], [C_])
        ACT(C_[:T, 16:32], B_[:T, 0:16], AF.Exp, [B_], [C_])
        b = pget()
        TR(b[:16, 0:T], B_[:T, 0:16], identf[:T, :T], [B_, identf], [b])
        ACT(acT[:, :T], b[:16, 0:T], AF.Copy, [b], [acT])
        if ti == 0:
            DBG("acum", B_, B_[:T, 0:32], [T, 32])
        if limit.get("stopA", 99) <= 2:
            return
        bx, bb = pget(), pget()
        for c in range(8):
            TR(pb(bx, [8, 128], p=T)[:, c, :], fT[:, c, :T], identb[:, :], [fT, identb], [bx])
        for c in range(2):
            TR(pb(bb, [2, 128], p=T)[:, c, :], fT[:, 8 + c, :T], identb[:, :], [fT, identb], [bb])
        xc, xdec, xsD = bigb[0], bigb[1], big[0]
        TT(xc[:T, :].rearrange("p (h q) -> p h q", q=64), pb(bx, [16, 64], p=T), bl(A_[:T, 16:32], 64), ALU.mult, [bx, A_], [xc])
        TT(xsD[:T, :].rearrange("p (h q) -> p h q", q=64), pb(bx, [16, 64], p=T), bl(vec16[:T, 2, :], 64), ALU.mult, [bx, vec16], [xsD])
        TT(xdec[:T, :].rearrange("p (h q) -> p h q", q=64), xc[:T, :].rearrange("p (h q) -> p h q", q=64), bl(C_[:T, 0:16], 64), ALU.mult,
           [xc, C_], [xdec], eng="pool")
        ACT(Btok[:T, :], pb(bb, [256], p=T), AF.Copy, [bb], [Btok])
        b = pget()
        for g in range(2):
            MM(pf(b, [2, T], p=T)[:, g, :], fT[:, 8 + g, :T], fT[:, 10 + g, :T], [fT], [b])
        ACT(Gs[:T, :, :T], pf(b, [2, T], p=T), AF.Copy, [b], [Gs])
        for q in range(4):
            b = pget()
            for j in range(4):
                h = q * 4 + j
                MM(pf(b, [4, T], p=T)[:, j, :], identf[0:16, h:h + 1].to_broadcast([16, T]), acT[:, :T], [identf, acT], [b])
            for j in range(4):
                h = q * 4 + j
                STT(earg[:T, j, :T], pf(b, [4, T], p=T)[:, j, :], B_[:T, h:h + 1], cfg.nmI[:, :], ALU.subtract, ALU.add,
                    [b, B_, cfg.nmI], [earg])
            ACT(Ee[:T, :, :T], earg[:T, :, :T], AF.Exp, [earg], [Ee])
            TT(WT[:T, q * 4:q * 4 + 4, :T], Ee[:T, :, :T], bm(Gs[:T, q // 2, :T], 4), ALU.mult, [Ee, Gs], [WT])
        if limit.get("stopA", 99) <= 3:
            return
        yp = [pget(), pget()]
        for h in range(16):
            MM(yp[h // 8][:T, (h % 8) * 64:(h % 8 + 1) * 64], WT[:T, h, :T], xc[:T, h * 64:(h + 1) * 64], [WT, xc], [yp[h // 8]])
        ppin(yp[0]); ppin(yp[1])
        yo = [pget(), pget()]
        ppin(yo[0]); ppin(yo[1])
        if cfg.chain:
            if first:
                MSET(stf[:, :], 0.0, [stf])
                MSET(stb[:, :], 0.0, [stb])
            for c in range(nch):
                ssd_state_update(cfg, c, stf, stb, stf, yo, xdec, c == 0, c == nch - 1)
                ACT(stb[:, :], stf[:, :], AF.Copy, [stf], [stb])
            if is_last:
                for half in range(2):
                    b = pget()
                    for k in range(4):
                        kk = half * 4 + k
                        TR(pf(b, [4, 128])[:, k, :], stf[:, :].rearrange("n (j k) -> n k j", k=8)[:, kk, :], identf[:, :], [stf, identf], [b])
                    CP_(hn[:, half * 4:half * 4 + 4, :], pf(b, [4, 128]), [b], [hn])
                STO(O["nssm_p"][l].rearrange("(p k) n -> p k n", k=8), hn[:], hn)
        else:
            for c in range(limit.get("nchs", nch)):
                LD(hn[:], I["st_ssm"][l, c].rearrange("h q n -> (h q) n").rearrange("(p k) n -> p k n", k=8), hn)
                if limit.get("sub", 9) <= 0:
                    continue
                for half in range(2):
                    b = pget()
                    for k in range(4):
                        kk = half * 4 + k
                        TR(pf(b, [4, 128])[:, k, :], hn[:, kk, :], identf[:, :], [hn, identf], [b])
                    if limit.get("sub", 9) <= 1:
                        continue
                    ACT(stf_s[:, :].rearrange("n (j k) -> n k j", k=8)[:, half * 4:half * 4 + 4, :], pf(b, [4, 128]), AF.Copy, [b], [stf_s])
                    if limit.get("sub", 9) <= 2:
                        continue
                    CP_(stb_s[:, :].rearrange("n (j k) -> n k j", k=8)[:, half * 4:half * 4 + 4, :], pf(b, [4, 128]), [b], [stb_s])
                if limit.get("noupd"):
                    continue
                ssd_state_update(cfg, c, stf_s, stb_s, stf_s, yo, xdec, c == 0, c == limit.get("nchs", nch) - 1)
                if limit.get("noback"):
                    continue
                for half in range(2):
                    b = pget()
                    for k in range(4):
                        kk = half * 4 + k
                        TR(pf(b, [4, 128])[:, k, :], stf_s[:, :].rearrange("n (j k) -> n k j", k=8)[:, kk, :], identf[:, :], [stf_s, identf], [b])
                    CP_(hn[:, half * 4:half * 4 + 4, :], pf(b, [4, 128]), [b], [hn])
                STO(O["nssm_s"][l, c].rearrange("(p k) n -> p k n", k=8), hn[:], hn)
        if limit.get("stopA", 99) <= 4:
            punpin(yp[0]); punpin(yp[1]); punpin(yo[0]); punpin(yo[1])
            return
        t1, t2 = big[1], big[2]
        for g in range(2):
            TT(t1[:T, g * 512:(g + 1) * 512].rearrange("p (h q) -> p h q", q=64), yo[g][:T, :].rearrange("p (h q) -> p h q", q=64),
               bl(C_[:T, 16 + g * 8:16 + g * 8 + 8], 64), ALU.mult, [yo[g], C_], [t1])
            TT(t2[:T, g * 512:(g + 1) * 512], yp[g][:T, :], t1[:T, g * 512:(g + 1) * 512], ALU.add, [yp[g], t1], [t2])
        for x_ in yp + yo:
            punpin(x_)
        TT(t2[:T, :], t2[:T, :], xsD[:T, :], ALU.add, [t2, xsD], [t2], eng="pool")
        if ti == 0:
            DBG("yssd", t2, t2[:T, :], [T, D])
        for g in range(2):
            b = pget()
            for k in range(8):
                MM(b[:T, :], hT[:, k, :T], PA["WB"][:, k, OFF_ZS + g * 512:OFF_ZS + (g + 1) * 512], [hT, PA["WB"]], [b], start=(k == 0), stop=(k == 7))
            ACT(t1[:T, g * 512:(g + 1) * 512], b[:T, :], AF.Silu, [b], [t1])
        TT(t2[:T, :], t2[:T, :], t1[:T, :], ALU.mult, [t2, t1], [t2])
        for g in range(2):
            ACT(junk[:T, g * 512:(g + 1) * 512], t2[:T, g * 512:(g + 1) * 512], AF.Square, [t2], [junk, sm1], accum_out=sm1[:T, 2 + g:3 + g])
        TS(sm1[:T, 2:4], sm1[:T, 2:4], 1.0 / 512, EPS, ALU.mult, ALU.add, [sm1], [sm1])
        ACT(sm1[:T, 2:4], sm1[:T, 2:4], AF.Ln, [sm1], [sm1])
        ACT(sm1[:T, 2:4], sm1[:T, 2:4], AF.Exp, [sm1], [sm1], scale=-0.5)
        ynb = bigb[2]
        for g in range(2):
            ACT(ynb[:T, g * 512:(g + 1) * 512], t2[:T, g * 512:(g + 1) * 512], AF.Copy, [t2, sm1], [ynb], scale=sm1[:T, 2 + g:3 + g])
        out_proj(cfg, ynb, 8)

    def out_proj(cfg, ynb, nk):
        T = cfg.T
        b = pget()
        for c in range(nk):
            TR(pb(b, [8, T])[:, c, :], ynb[:T, c * 128:(c + 1) * 128], identb[:T, :T], [ynb, identb], [b])
        CP_(hT[:, 0:nk, :T], pb(b, [8, T])[:, 0:nk, :], [b], [hT])
        for half in range(2):
            b = pget()
            for k in range(nk):
                MM(b[:T, :], hT[:, k, :T], WO[:, k, half * 512:(half + 1) * 512], [hT, WO], [b], start=(k == 0), stop=(k == nk - 1))
            TT(big[3][:T, half * 512:(half + 1) * 512], b[:T, :], gate_bc[cfg.name][:T, half * 512:(half + 1) * 512], ALU.mult,
               [b, gate_bc[cfg.name]], [big[3]])

    PB = {}

    def alloc_B(pes, tag):
        def sbp(name, shape, dt=F32):
            return Buf(pes.enter_context(nc.sbuf_tensor(name + tag, list(shape), dt)), name)
        PB.update(Sp_f=sbp("Sp_f", [128, 4, 128]), Sp_b=sbp("Sp_b", [128, 4, 128], BF16), Ss_in=sbp("Ss_in", [128, 16, 128]),
                  Ss_b=sbp("Ss_b", [128, 16, 128], BF16), kv=sbp("kv", [128, 8, 128], BF16), kbg=sbp("kbg", [128, 4, 128]),
                  vb_=sbp("vb", [128, 4, 128]), kdec=sbp("kdec", [128, 4, 128], BF16), kdm=sbp("kdm", [64, 2, 128], BF16),
                  LG=sbp("LG", [128, 12]), LGT=sbp("LGT", [12, 128]), ea=sbp("ea", [128, 4, 128]),
                  E2a=sbp("E2a", [128, 4, 128]), E2b=sbp("E2b", [128, 4, 128]),
                  X0=sbp("X0", [128, 4, 128]), X1=sbp("X1", [128, 4, 128]), Y0=sbp("Y0", [128, 4, 128]), Y1=sbp("Y1", [128, 4, 128]),
                  R0=sbp("R0", [128, 4, 128]), R1=sbp("R1", [128, 4, 128]), attnT=sbp("attnT", [128, 4, 128], BF16),
                  egb=sbp("egb", [128, 4, 128]), qsq=sbp("qsq", [128, 4, 128], BF16),
                  nwTm=sbp("nwTm", [128, 1024], BF16), qdm=sbp("qdm", [128, 1024], BF16), u_sb=sbp("u_sb", [128, 4, 128]),
                  vnew=sbp("vnew", [128, 4, 128], BF16), dbcS=sbp("dbcS", [128, 64]), WB=sbp("WB", [128, 8, 2056], BF16))
        PB["E2"] = None

    def r32(ap):
        return ap.bitcast(F32R)

    def phaseB_tile(cfg, l, ti, xt, h0, first, is_last):
        T, L, nseq, nch = cfg.T, cfg.L, cfg.nseq, cfg.nch
        (Sp_f, Sp_b, Ss_in, Ss_b, kv, kbg, vb_, kdec, kdm, LG, LGT, ea, attnT, egb, nwTm, qdm, u_sb, vnew, dbcS) = (
            PB[k] for k in ("Sp_f", "Sp_b", "Ss_in", "Ss_b", "kv", "kbg", "vb_", "kdec", "kdm", "LG", "LGT", "ea", "attnT",
                            "egb", "nwTm", "qdm", "u_sb", "vnew", "dbcS"))
        E2 = [PB["E2a"], PB["E2b"]]
        Ss_out = Ss_in
        Xs, Ys, Rs = [PB["X0"], PB["X1"]], [PB["Y0"], PB["Y1"]], [PB["R0"], PB["R1"]]
        WB = PB["WB"]
        prologue(cfg, xt)
        inproj_conv_B(cfg, l, h0, is_last, first)
        A_, B_, C_, D_, E_ = (sm[k] for k in "abcde")
        b = pget()
        for k in range(8):
            MM(b[:T, 0:8], hT[:, k, :T], WB[:, k, 2048:2056], [hT, WB], [b], start=(k == 0), stop=(k == 7))
        TT(A_[:T, 0:4], b[:T, 0:4], vec8[:T, 0, h0:h0 + 4], ALU.add, [b, vec8], [A_])
        ACT(A_[:T, 0:4], A_[:T, 0:4], AF.Exp, [A_], [A_])
        ACT(A_[:T, 0:4], A_[:T, 0:4], AF.Ln, [A_], [A_], bias=1.0)
        TT(A_[:T, 4:8], A_[:T, 0:4], vec8[:T, 1, h0:h0 + 4], ALU.mult, [A_, vec8], [A_])
        ACT(A_[:T, 8:12], b[:T, 4:8], AF.Exp, [b], [A_], scale=-1.0)
        ACT(A_[:T, 8:12], A_[:T, 8:12], AF.Ln, [A_], [A_], bias=1.0)
        ACT(A_[:T, 12:16], A_[:T, 8:12], AF.Exp, [A_], [A_], scale=-1.0)
        ACT(A_[:T, 16:20], A_[:T, 8:12], AF.Copy, [A_], [A_], scale=-1.0)
        cum_and_tot(cfg, A_[:T, 4:8], A_, 4, B_, dbcS, 128)
        ACT(C_[:T, 0:4], B_[:T, 0:4], AF.Exp, [B_], [C_])
        TT(C_[:T, 4:8], B_[:T, 4:8], B_[:T, 0:4], ALU.subtract, [B_], [C_])
        ACT(C_[:T, 4:8], C_[:T, 4:8], AF.Exp, [C_], [C_])
        b = pget()
        for c in range(8):
            TR(pb(b, [8, 128], p=T)[:, c, :], fT[:, 4 + c, :T], identb[:, :], [fT, identb], [b])
        ACT(kv[:T, :, :], pb(b, [8, 128], p=T), AF.Copy, [b], [kv])
        ksq = big[0]
        TT(ksq[:T, 0:512].rearrange("p (h d) -> p h d", d=128), kv[:T, 0:4, :], kv[:T, 0:4, :], ALU.mult, [kv], [ksq])
        RED(D_[:T, 0:4], ksq[:T, 0:512].rearrange("p (h d) -> p h d", d=128), [ksq], [D_])
        TS(D_[:T, 0:4], D_[:T, 0:4], EPS, None, ALU.add, ALU.bypass, [D_], [D_])
        ACT(D_[:T, 4:8], D_[:T, 0:4], AF.Ln, [D_], [D_])
        ACT(D_[:T, 8:12], D_[:T, 4:8], AF.Exp, [D_], [D_], scale=-0.5)
        qsq = PB["qsq"]
        ACT(qsq[:, :, :T], fT[:, 0:4, :T], AF.Square, [fT], [qsq])
        b = pget()
        for j in range(4):
            MM(b[:T, j:j + 1], qsq[:, j, :T], onesb[:, 0:1], [qsq, onesb], [b])
        TS(D_[:T, 12:16], b[:T, 0:4], EPS, None, ALU.add, ALU.bypass, [b], [D_])
        ACT(D_[:T, 12:16], D_[:T, 12:16], AF.Ln, [D_], [D_])
        ACT(D_[:T, 16:20], D_[:T, 12:16], AF.Exp, [D_], [D_], scale=-0.5)
        TS(D_[:T, 16:20], D_[:T, 16:20], float(128 ** -0.5), None, ALU.mult, ALU.bypass, [D_], [D_])
        TT(E_[:T, 0:4], D_[:T, 8:12], A_[:T, 12:16], ALU.mult, [D_, A_], [E_])
        TT(E_[:T, 4:8], E_[:T, 0:4], C_[:T, 0:4], ALU.mult, [E_, C_], [E_])
        TT(E_[:T, 8:12], D_[:T, 8:12], C_[:T, 4:8], ALU.mult, [D_, C_], [E_])
        TS(E_[:T, 12:16], E_[:T, 0:4], -1.0, None, ALU.mult, ALU.bypass, [E_], [E_])
        TS(E_[:T, 16:20], D_[:T, 8:12], -1.0, None, ALU.mult, ALU.bypass, [D_], [E_])
        CP_(LG[:T, 0:4], B_[:T, 0:4], [B_], [LG])
        STT(LG[:T, 4:8], D_[:T, 4:8], -0.5, B_[:T, 0:4], ALU.mult, ALU.subtract, [D_, B_], [LG])
        STT(LG[:T, 8:12], D_[:T, 4:8], -0.5, B_[:T, 0:4], ALU.mult, ALU.add, [D_, B_], [LG])
        TT(LG[:T, 8:12], LG[:T, 8:12], A_[:T, 16:20], ALU.add, [LG, A_], [LG])
        b = pget()
        TR(b[:12, 0:T], LG[:T, 0:12], identf[:T, :T], [LG, identf], [b])
        ACT(LGT[:, :T], b[:12, 0:T], AF.Copy, [b], [LGT])
        TT(r32(kbg[:T, :, :]), kv[:T, 0:4, :], bl(E_[:T, 4:8], 128), ALU.mult, [kv, E_], [kbg])
        TT(kdec[:T, :, :], kv[:T, 0:4, :], bl(E_[:T, 8:12], 128), ALU.mult, [kv, E_], [kdec])
        TT(r32(vb_[:T, :, :]), kv[:T, 4:8, :], bl(A_[:T, 12:16], 128), ALU.mult, [kv, A_], [vb_])
        bG, bA = pget(), pget()
        for j in range(4):
            MM(pf(bG, [4, T], p=T)[:, j, :], fT[:, 4 + j, :T], fT[:, 4 + j, :T], [fT], [bG])
        for j in range(4):
            MM(pf(bA, [4, T], p=T)[:, j, :], fT[:, 4 + j, :T], fT[:, j, :T], [fT], [bA])
        bcs = [pget(), pget(), pget()]
        for r in range(3):
            for j in range(4):
                MM(pf(bcs[r], [4, T], p=T)[:, j, :], identf[0:12, r * 4 + j:r * 4 + j + 1].to_broadcast([12, T]), LGT[:, :T],
                   [identf, LGT], [bcs[r]])
        be = pget()
        for j in range(4):
            MM(pf(be, [4, T])[:, j, :], identf[0:12, j:j + 1].to_broadcast([12, 128]), LGT[:, :T], [identf, LGT], [be])
        X, Y, R = Xs[0], Ys[0], Rs[0]
        kinds = ((1, ALU.add, cfg.nmSr, 0), (2, ALU.subtract, cfg.nmS, 1), (0, ALU.subtract, cfg.nmI, 2))
        for r, op0, msk, ki in kinds:
            for j in range(4):
                STT(ea[:T, j, :T], pf(bcs[r], [4, T], p=T)[:, j, :], B_[:T, j:j + 1], msk[:, :], op0, ALU.add, [bcs[r], B_, msk], [ea])
            Ek = E2[ki % 2]
            ACT(Ek[:T, :, :T], ea[:T, :, :T], AF.Exp, [ea], [Ek])
            for j in range(4):
                if ki == 0:
                    STT(r32(X[:T, j, :T]), pf(bG, [4, T], p=T)[:, j, :], E_[:T, 12 + j:13 + j], Ek[:T, j, :T], ALU.mult, ALU.mult, [bG, E_, Ek], [X])
                elif ki == 1:
                    STT(r32(Y[:T, j, :T]), pf(bG, [4, T], p=T)[:, j, :], E_[:T, 16 + j:17 + j], Ek[:T, j, :T], ALU.mult, ALU.mult, [bG, E_, Ek], [Y])
                else:
                    STT(attnT[:T, j, :T], pf(bA, [4, T], p=T)[:, j, :], D_[:T, 8 + j:9 + j], Ek[:T, j, :T], ALU.mult, ALU.mult, [bA, D_, Ek], [attnT])
        ACT(egb[:, :, :T], pf(be, [4, T]), AF.Exp, [be], [egb])
        TT(r32(R[:T, :, :T]), Y[:T, :, :T], bm(identf[:T, :T], 4), ALU.add, [Y, identf], [R])
        for lev in range(cfg.nlev):
            X2, Y2, R2 = Xs[(lev + 1) % 2], Ys[(lev + 1) % 2], Rs[(lev + 1) % 2]
            lastlev = (lev == cfg.nlev - 1)
            bX, bY, bR = pget(), (None if lastlev else pget()), pget()
            for j in range(4):
                MM(pf(bX, [4, T], p=T)[:, j, :], r32(Y[:T, j, :T]), r32(X[:T, j, :T]), [X, Y], [bX])
            if not lastlev:
                for j in range(4):
                    MM(pf(bY, [4, T], p=T)[:, j, :], r32(X[:T, j, :T]), r32(Y[:T, j, :T]), [X, Y], [bY])
            ACT(r32(X2[:T, :, :T]), pf(bX, [4, T], p=T), AF.Copy, [bX], [X2])
            if not lastlev:
                CP_(r32(Y2[:T, :, :T]), pf(bY, [4, T], p=T), [bY], [Y2])
            for j in range(4):
                MM(pf(bR, [4, T], p=T)[:, j, :], r32(X2[:T, j, :T]), r32(R[:T, j, :T]), [X2, R], [bR])
            TT(r32(R2[:T, :, :T]), R[:T, :, :T], pf(bR, [4, T], p=T), ALU.add, [R, bR], [R2])
            X, Y, R = X2, Y2, R2
        bU, bW = pget(), pget()
        for j in range(4):
            MM(pf(bU, [4, 128], p=T)[:, j, :], r32(R[:T, j, :T]), r32(vb_[:T, j, :]), [R, vb_], [bU])
        for j in range(4):
            MM(pf(bW, [4, T])[:, j, :], r32(kbg[:T, j, :]), r32(R[:T, j, :T]), [kbg, R], [bW])
        ACT(u_sb[:T, :, :], pf(bU, [4, 128], p=T), AF.Copy, [bU], [u_sb])
        TT(egb[:, :, :T], fT[:, 0:4, :T], egb[:, :, :T], ALU.mult, [fT, egb], [egb])
        vnb, ob = pget(), pget()
        ppin(vnb)
        ppin(ob)
        vn4, o4 = pf(vnb, [4, 128], p=T), pf(ob, [4, 128], p=T)
        if cfg.chain:
            if first:
                MSET(Sp_f[:], 0.0, [Sp_f])
                MSET(Sp_b[:], 0.0, [Sp_b])
            nwv = nwTm[:, 0:nch * 4 * T].rearrange("p (c j t) -> p c j t", c=nch, j=4)
            qdv = qdm[:, 0:nch * 4 * T].rearrange("p (c j t) -> p c j t", c=nch, j=4)
            for c in range(nch):
                TT(nwv[:, c], pf(bW, [4, T]), bm(cfg.ncmf[:, c, :], 4), ALU.mult, [bW, cfg.ncmf], [(nwTm, c)])
                TT(qdv[:, c], egb[:, :, :T], bm(cfg.cmf[:, c, :], 4), ALU.mult, [egb, cfg.cmf], [(qdm, c)], eng="pool")
            for c in range(nch):
                for j in range(4):
                    MM(vn4[:, j, :], nwv[:, c, j, :], Sp_b[:, j, :], [(nwTm, c), Sp_b], [vnb], start=(j == 0 and c == 0), stop=False, skip_group_check=True)
                for j in range(4):
                    MM(o4[:, j, :], qdv[:, c, j, :], Sp_b[:, j, :], [(qdm, c), Sp_b], [ob], start=(j == 0 and c == 0), stop=False, skip_group_check=True)
                TT(vnew[:T, :, :], vn4, u_sb[:T, :, :], ALU.add, [vnb, u_sb], [vnew])
                bs = pget()
                for j in range(4):
                    MM(pf(bs, [4, 128])[:, j, :], kdec[c * 64:(c + 1) * 64, j, :], vnew[c * 64:(c + 1) * 64, j, :], [kdec, vnew], [bs])
                tmpS = big[0][:, 512:1024].rearrange("p (j v) -> p j v", v=128)
                TT(tmpS, Sp_f[:, :, :], bl(dbcS[:, c * 4:(c + 1) * 4], 128), ALU.mult, [Sp_f, dbcS], [big[0]])
                TT(Sp_f[:, :, :], tmpS, pf(bs, [4, 128]), ALU.add, [big[0], bs], [Sp_f])
                ACT(Sp_b[:, :, :], Sp_f[:, :, :], AF.Copy, [Sp_f], [Sp_b])
            if is_last:
                STO(O["ngdn_p"][l, h0:h0 + 4].rearrange("h k v -> k h v"), Sp_f[:], Sp_f)
        else:
            nwv = nwTm[:, 0:nch * T].rearrange("p (c t) -> p c t", t=T)
            qdv = qdm[:, 0:nch * T].rearrange("p (c t) -> p c t", t=T)
            ppin(bW)
            for j in range(4):
                h = h0 + j
                LD(Ss_in[:], I["st_gdn"][l, :, h].rearrange("b k v -> k b v"), Ss_in)
                CP_(Ss_b[:], Ss_in[:], [Ss_in], [Ss_b], eng="pool")
                TT(nwv, bm(pf(bW, [4, T])[:, j, :], nch), cfg.ncmf[:, :, :], ALU.mult, [bW, cfg.ncmf], [nwTm])
                TT(qdv, bm(egb[:, j, :T], nch), cfg.cmf[:, :, :], ALU.mult, [egb, cfg.cmf], [qdm], eng="pool")
                for c in range(nch):
                    MM(vn4[:, j, :], nwv[:, c, :], Ss_b[:, c, :], [nwTm, Ss_b], [vnb], start=(j == 0 and c == 0), stop=False, skip_group_check=True)
                for c in range(nch):
                    MM(o4[:, j, :], qdv[:, c, :], Ss_b[:, c, :], [qdm, Ss_b], [ob], start=(j == 0 and c == 0), stop=False, skip_group_check=True)
                TT(vnew[:T, j, :], vn4[:, j, :], u_sb[:T, j, :], ALU.add, [vnb, u_sb], [vnew])
                tmpS = big[2][:, :].rearrange("p (c v) -> p c v", v=128)
                for hf in range(2):
                    TT(tmpS, Ss_in[:, hf * 8:(hf + 1) * 8, :],
                       bl(dbcS[:, 0:nch * 4].rearrange("p (c j) -> p c j", j=4)[:, hf * 8:(hf + 1) * 8, j], 128), ALU.mult, [Ss_in, dbcS], [big[2]])
                    for c4 in range(hf * 8, hf * 8 + 8, 4):
                        bs = pget()
                        for cc in range(4):
                            c = c4 + cc
                            sl = mslot[0] % 2
                            mslot[0] += 1
                            TS(kdm[:T, sl, :], kdec[:T, j, :], cfg.cmT[:, c:c + 1], 1.0, ALU.mult, ALU.mult, [kdec, cfg.cmT], [(kdm, sl)], eng="pool")
                            MM(pf(bs, [4, 128])[:, cc, :], kdm[:T, sl, :], vnew[:T, j, :], [(kdm, sl), vnew], [bs])
                        TT(Ss_out[:, c4:c4 + 4, :], tmpS[:, c4 - hf * 8:c4 - hf * 8 + 4, :], pf(bs, [4, 128]), ALU.add, [big[2], bs], [Ss_out])
                STO(O["ngdn_s"][l, :, h].rearrange("b k v -> k b v"), Ss_out[:], Ss_out)
            punpin(bW)
        for j in range(4):
            MM(o4[:, j, :], attnT[:T, j, :T], vnew[:T, j, :], [attnT, vnew], [ob], start=False, stop=True, skip_group_check=True)
        punpin(vnb)
        of = big[1]
        ACT(of[:T, 0:512], ob[:T, :], AF.Copy, [ob], [of])
        punpin(ob)
        if ti == 0 and h0 == 0:
            DBG("ogdn", of, of[:T, 0:512], [T, 512])
        TT(ksq[:T, 0:512], of[:T, 0:512], of[:T, 0:512], ALU.mult, [of], [ksq])
        RED(C_[:T, 8:12], ksq[:T, 0:512].rearrange("p (h d) -> p h d", d=128), [ksq], [C_])
        TT(C_[:T, 12:16], D_[:T, 16:20], D_[:T, 16:20], ALU.mult, [D_], [C_])
        TT(C_[:T, 8:12], C_[:T, 8:12], C_[:T, 12:16], ALU.mult, [C_], [C_])
        TS(C_[:T, 8:12], C_[:T, 8:12], 1.0 / 128, EPS, ALU.mult, ALU.add, [C_], [C_])
        ACT(C_[:T, 8:12], C_[:T, 8:12], AF.Ln, [C_], [C_])
        ACT(C_[:T, 8:12], C_[:T, 8:12], AF.Exp, [C_], [C_], scale=-0.5)
        TT(C_[:T, 8:12], C_[:T, 8:12], D_[:T, 16:20], ALU.mult, [C_, D_], [C_])
        b = pget()
        for k in range(8):
            MM(b[:T, :], hT[:, k, :T], WB[:, k, 0:512], [hT, WB], [b], start=(k == 0), stop=(k == 7))
        sz = big[2]
        ACT(sz[:T, 0:512], b[:T, :], AF.Silu, [b], [sz])
        TT(of[:T, 0:512].rearrange("p (h d) -> p h d", d=128), of[:T, 0:512].rearrange("p (h d) -> p h d", d=128), bl(C_[:T, 8:12], 128),
           ALU.mult, [of, C_], [of])
        onb = bigb[2]
        TT(onb[:T, 0:512], of[:T, 0:512], sz[:T, 0:512], ALU.mult, [of, sz], [onb])
        out_proj(cfg, onb, 4)

    def inproj_conv_B(cfg, l, h0, is_last, first):
        T, L, nseq = cfg.T, cfg.L, cfg.nseq
        rawv = raw[:, :, 0:nseq * (L + 3)].rearrange("p c (b l) -> p c b l", l=L + 3)
        if nseq == 1:
            if first:
                MSET(raw[:, :, 0:3], 0.0, [raw])
            else:
                CP_(raw[:, :, 0:3], raw[:, :, L:L + 3], [raw], [raw])
        else:
            for part in range(3):
                cs0 = part * 1024 + h0 * 128
                LD(cst_in[:, part * 512:(part + 1) * 512], I["st_gc"][l][:, :, cs0:cs0 + 512].rearrange("b k c -> (b k) c"), cst_in)
            for g0 in (0, 8):
                b = pget()
                ng = min(8, 12 - g0)
                for c in range(ng):
                    TR(pf(b, [8, 48])[:, c, :], cst_in[:, (g0 + c) * 128:(g0 + c + 1) * 128], identf[:48, :48], [cst_in, identf], [b])
                CP_(rawv[:, g0:g0 + ng, :, 0:3], pf(b, [8, 48])[:, 0:ng, :].rearrange("p c (b k) -> p c b k", k=3), [b], [raw])
        for g0 in range(0, 12, 4):
            b = pget()
            for c in range(4):
                for k in range(8):
                    MM(pf(b, [4, T])[:, c, :], PB["WB"][:, k, 512 + (g0 + c) * 128: 512 + (g0 + c + 1) * 128], hT[:, k, :T],
                       [PB["WB"], hT], [b], start=(k == 0), stop=(k == 7))
            if nseq == 1:
                ACT(raw[:, g0:g0 + 4, 3:3 + L], pf(b, [4, T]), AF.Copy, [b], [raw])
            else:
                for c in range(4):
                    ACT(rawv[:, g0 + c, :, 3:3 + L], pf(b, [4, T])[:, c, :].rearrange("p (b l) -> p b l", l=L), AF.Copy, [b], [raw])
            if is_last and limit.get("tail", True):
                for c in range(4):
                    ACT(tailf[:, g0 + c, 0:nseq * 3].rearrange("p (b k) -> p b k", k=3),
                        pf(b, [4, T])[:, c, :].rearrange("p (b l) -> p b l", l=L)[:, :, L - 3:L], AF.Copy, [b], [tailf])
        for g0 in range(0, 12, 4):
            b = pget()
            for c in range(4):
                o = pf(b, [4, T])[:, c, :]
                if nseq > 1:
                    o = o.rearrange("p (b l) -> p b l", l=L)
                for k in range(4):
                    MM(o, DIAG[:, g0 + c, k, :], rawv[:, g0 + c, :, k:k + L] if nseq > 1 else raw[:, g0 + c, k:k + L],
                       [DIAG, raw], [b], start=(k == 0), stop=(k == 3))
            ACT(fT[:, g0:g0 + 4, :T], pf(b, [4, T]), AF.Silu, [b], [fT])
        if is_last:
            n3 = nseq * 3
            for g0 in range(0, 12, 4):
                b = pget()
                for c in range(4):
                    TR(pf(b, [4, 128], p=n3)[:, c, :], tailf[:, g0 + c, 0:n3], identf[:, :], [tailf, identf], [b])
                CP_(tailo[:n3, g0 * 128:(g0 + 4) * 128], b[:n3, :], [b], [tailo])
            dst = O["ngc_p"] if nseq == 1 else O["ngc_s"]
            for part in range(3):
                cs0 = part * 1024 + h0 * 128
                STO(dst[l][:, cs0:cs0 + 512], tailo[:n3, part * 512:(part + 1) * 512], tailo)

    def bvec(dst_ap, src_row_ap, buf):
        LD(dst_ap, src_row_ap.partition_broadcast(128), buf)

    tiles = [(CP, ti) for ti in range(limit.get("ntiles", NT))] + ([(CS, NT)] if limit.get("sample", True) else [])
    all_bufs.extend(allb)
    layer_in = None
    for l in range(limit.get("layers", DEPTH)):
        do_mod(l)
        S.barrier(all_bufs)
        mod_stacks.pop().close()
        for ph in limit.get("phases", (0, 1, 2)):
            acc_in = layer_in if ph == 0 else scr[(ph - 1)]
            last_phase = (l == DEPTH - 1 and ph == 2)
            acc_out = scr[ph] if ph < 2 else (scr[2] if not last_phase else None)
            S.barrier(all_bufs)
            pes = contextlib.ExitStack()
            cur_es[0] = pes
            if ph == 0:
                alloc_A(pes, "_%d_%d" % (l, ph))
            else:
                alloc_B(pes, "_%d_%d" % (l, ph))
            for v_ in list(PA.values()) + list(PB.values()):
                if v_ is not None and v_ not in all_bufs:
                    all_bufs.append(v_)
            if ph == 0:
                load_cols(PA["WB"], 0, I["w_in"][l][:, 0:2576], 2576)
                LD(nwT[:], I["ssd_norm_w"][l].rearrange("(k p) -> p k", p=128), nwT, nonc=True)
                load_wo(I["w_out"][l][0:1024, :], 8, lambda k: nwT[:, k:k + 1])
                build_diag(I["ssd_conv_w"][l], 0, 12)
                LD(cbrow_f[0:1, :], I["ssd_conv_b"][l:l + 1, :], cbrow_f)
                CP_(cbrow[:], cbrow_f[0:1, :], [cbrow_f], [cbrow])
                bvec(vec16[:, 0, :], I["ssd_dt_bias"][l:l + 1, :], vec16)
                bvec(vec16[:, 1, :], I["ssd_a_log"][l:l + 1, :], vec16)
                bvec(vec16[:, 2, :], I["ssd_d"][l:l + 1, :], vec16)
                ACT(vec16[:, 1, :], vec16[:, 1, :], AF.Exp, [vec16], [vec16])
                TS(vec16[:, 1, :], vec16[:, 1, :], -1.0, None, ALU.mult, ALU.bypass, [vec16], [vec16])
            else:
                h0 = (ph - 1) * 4
                load_cols(PB["WB"], 0, I["w_in"][l][:, OFF_ZG + h0 * 128:OFF_ZG + h0 * 128 + 512], 512)
                for part in range(3):
                    c0 = OFF_QKV + part * 1024 + h0 * 128
                    load_cols(PB["WB"], 512 + part * 512, I["w_in"][l][:, c0:c0 + 512], 512)
                load_cols(PB["WB"], 2048, I["w_in"][l][:, OFF_A + h0:OFF_A + h0 + 4], 4)
                load_cols(PB["WB"], 2052, I["w_in"][l][:, OFF_B + h0:OFF_B + h0 + 4], 4)
                LD(nwT[:, 0:1], I["gdn_norm_w"][l].rearrange("(p o) -> p o", o=1), nwT, nonc=True)
                load_wo(I["w_out"][l][1024 + h0 * 128:1024 + h0 * 128 + 512, :], 4, lambda k: nwT[:, 0:1])
                for part in range(3):
                    for k_ in range(4):
                        LD(cwT[:, part * 4:part * 4 + 4, k_],
                           I["gdn_conv_w"][l][k_, part * 1024 + h0 * 128:part * 1024 + h0 * 128 + 512].rearrange("(c p) -> p c", p=128), cwT, nonc=True)
                for c in range(12):
                    TT(DIAG[:, c, :, :], bm(identf[:], 4), bl(cwT[:, c, :], 128), ALU.mult, [identf, cwT], [DIAG], eng="pool")
                bvec(vec8[:, 0, :], I["gdn_dt_bias"][l:l + 1, :], vec8)
                bvec(vec8[:, 1, :], I["gdn_a_log"][l:l + 1, :], vec8)
                ACT(vec8[:, 1, :], vec8[:, 1, :], AF.Exp, [vec8], [vec8])
                TS(vec8[:, 1, :], vec8[:, 1, :], -1.0, None, ALU.mult, ALU.bypass, [vec8], [vec8])

            def issue_loads(idx):
                cfg, ti = tiles[idx]
                r0, T = rows(ti)
                xt = xt_slots[idx % 2]
                ap, dep = src_ap(layer_in.t if layer_in is not None else None, ti)
                LD(xt[:T, :], ap, xt, R=[(layer_in, ti)] if layer_in is not None else None)
                if ph > 0:
                    xa = xa_slots[idx % 2]
                    LD(xa[:T, :], acc_in.t[r0:r0 + T, :], xa, R=[(acc_in, ti)])

            if last_phase:
                fnw_bc = stage[0]
                fnw_v = stage[0][:, :, :].rearrange("p a b -> p (a b)")
                LD(fnw_v, I["final_norm_w"].partition_broadcast(128), stage[0])
            issue_loads(0)
            for idx, (cfg, ti) in enumerate(tiles):
                if idx + 1 < len(tiles):
                    issue_loads(idx + 1)
                r0, T = rows(ti)
                xt = xt_slots[idx % 2]
                xa = xa_slots[idx % 2] if ph > 0 else xt
                first = (ti == 0)
                is_last = (ti == NT - 1) or (ti == NT)
                if ph == 0:
                    phaseA_tile(cfg, l, ti, xt, first, is_last)
                else:
                    phaseB_tile(cfg, l, ti, xt, (ph - 1) * 4, first, is_last)
                TT(xo[:T, :], big[3][:T, :], xa[:T, :], ALU.add, [big[3], xa], [xo], eng="pool")
                if not last_phase:
                    STO(acc_out.t[r0:r0 + T, :], xo[:T, :], xo, W=[(acc_out, ti)])
                else:
                    ACT(junk[:T, :], xo[:T, :], AF.Square, [xo], [junk, sm1], accum_out=sm1[:T, 4:5])
                    TS(sm1[:T, 4:5], sm1[:T, 4:5], 1.0 / D, EPS, ALU.mult, ALU.add, [sm1], [sm1])
                    ACT(sm1[:T, 4:5], sm1[:T, 4:5], AF.Ln, [sm1], [sm1])
                    ACT(sm1[:T, 4:5], sm1[:T, 4:5], AF.Exp, [sm1], [sm1], scale=-0.5)
                    STT(big[0][:T, :], xo[:T, :], sm1[:T, 4:5], fnw_v[:T, :], ALU.mult, ALU.mult, [xo, sm1, fnw_bc], [big[0]])
                    dsto = O["y_p"][r0:r0 + T, :] if ti < NT else O["y_s"][:, :]
                    STO(dsto, big[0][:T, :], big[0])
            pes.close()
            cur_es[0] = es
            PA.clear()
            PB.clear()
        layer_in = scr[2]
    if FILL_TABLE and not limit.get("nofill"):
        fill_rhs = CS.cmf[:, :, :].rearrange("p c t -> p (c t)")[:, 0:512]
        S.filler = lambda e: e.matmul(banks[7][:, :], lhsT=identb[:, :], rhs=fill_rhs, start=True, stop=True)
        S.fill_table = {int(kv.split(":")[0]): int(kv.split(":")[1]) for kv in FILL_TABLE.split(",") if kv}
    S.emit()
    build.pe_groups = S.pe_groups
    es.close()
    return nc, dbg_out


_CACHE = {}


def make_in_maps(inputs):
    g = {k: np.ascontiguousarray(np.asarray(v, dtype=np.float32)) for k, v in inputs.items()}
    consts = host_consts()
    maps = []
    for c in range(8):
        m = {}
        m["xp"] = g["x_prompt"][c]
        m["xs"] = g["x_sample"][c * NSB:(c + 1) * NSB].reshape(NSB * SL, D)
        m["call"] = np.concatenate([g["c_prompt"][c:c + 1], g["c_sample"][c * NSB:(c + 1) * NSB]], axis=0)
        m["st_sc"] = g["state_ssd_conv"][:, c * NSB:(c + 1) * NSB]
        m["st_ssm"] = g["state_ssm"][:, c * NSB:(c + 1) * NSB]
        m["st_gc"] = g["state_gdn_conv"][:, c * NSB:(c + 1) * NSB]
        m["st_gdn"] = g["state_gdn"][:, c * NSB:(c + 1) * NSB]
        for k in ("norm_w", "w_ada", "b_ada", "w_in", "ssd_conv_w", "ssd_conv_b", "ssd_dt_bias", "ssd_a_log", "ssd_d",
                  "ssd_norm_w", "gdn_conv_w", "gdn_dt_bias", "gdn_a_log", "gdn_norm_w", "w_out"):
            m[k] = g[k]
        m["final_norm_w"] = g["final_norm_w"].reshape(1, D)
        for k, v in consts.items():
            m["c_" + k] = v
        maps.append({k: np.ascontiguousarray(v) for k, v in m.items()})
    return maps


def kernel(**inputs):
    if "nc" not in _CACHE:
        _CACHE["nc"] = build()[0]
    nc = _CACHE["nc"]
    maps = make_in_maps(inputs)
    res = run_bass_kernel_spmd(nc, maps, core_ids=list(range(8)))
    R = res.results
    cat = lambda k, ax: np.concatenate([np.asarray(r[k]) for r in R], axis=ax)
    y_p = np.stack([np.asarray(r["y_p"]) for r in R], 0)
    y_s = np.stack([np.asarray(r["y_s"]).reshape(NSB, SL, D) for r in R], 0).reshape(8 * NSB, SL, D)
    ncs_p = np.stack([np.asarray(r["ncs_p"]) for r in R], 1)
    nssm_p = np.stack([np.asarray(r["nssm_p"]).reshape(DEPTH, 16, 64, 128) for r in R], 1)
    ngc_p = np.stack([np.asarray(r["ngc_p"]) for r in R], 1)
    ngdn_p = np.stack([np.asarray(r["ngdn_p"]) for r in R], 1)
    ncs_s = np.concatenate([np.asarray(r["ncs_s"]).reshape(DEPTH, NSB, 3, 1536) for r in R], 1)
    nssm_s = np.concatenate([np.asarray(r["nssm_s"]).reshape(DEPTH, NSB, 16, 64, 128) for r in R], 1)
    ngc_s = np.concatenate([np.asarray(r["ngc_s"]).reshape(DEPTH, NSB, 3, 3072) for r in R], 1)
    ngdn_s = np.concatenate([np.asarray(r["ngdn_s"]) for r in R], 1)
    outs = (y_p, y_s, ncs_p, nssm_p, ngc_p, ngdn_p, ncs_s, nssm_s, ngc_s, ngdn_s)
    return tuple(np.ascontiguousarray(o, dtype=np.float32) for o in outs)
```

```python
import contextlib
import numpy as np
import concourse.bass as bass
import concourse.mybir as mybir
from concourse.bass_utils import run_bass_kernel_spmd

F32 = mybir.dt.float32
F32R = mybir.dt.float32r
BF16 = mybir.dt.bfloat16
AF = mybir.ActivationFunctionType
ALU = mybir.AluOpType
AX = mybir.AxisListType

ENGS = ("pe", "dve", "act", "pool", "sp")


class Buf:
    def __init__(self, t, name):
        self.t = t
        self.name = name
        self.st = {}
        self.dma_sem = None
        self.dma_cnt = 0

    def __getitem__(self, k):
        return self.t[k]


class Instr:
    __slots__ = ("eng", "fn", "deps", "signal", "count", "is_dma", "buf_sem", "dma_count")

    def __init__(self, eng, fn):
        self.eng = eng
        self.fn = fn
        self.deps = []
        self.signal = False
        self.count = None
        self.is_dma = False
        self.buf_sem = None
        self.dma_count = None


class Sched:
    def __init__(self, nc):
        self.nc = nc
        self.streams = {e: [] for e in ENGS}
        self.all = []
        self.filler = None
        self.nfill = 0
        self.fill_table = {}
        self.pe_groups = 0

    def _states(self, buf, key):
        st = buf.st
        if key is None:
            if None not in st:
                st[None] = [None, {}]
            return list(st.values())
        if key not in st:
            if None in st:
                st[key] = [st[None][0], dict(st[None][1])]
            else:
                st[key] = [None, {}]
        res = [st[key]]
        if None in st:
            res.append(st[None])
        return res

    @staticmethod
    def _norm(lst):
        out = []
        for x in lst or []:
            out.append((x, None) if isinstance(x, Buf) else x)
        return out

    def add(self, eng, fn, reads=None, writes=None, dma_buf=None):
        ins = Instr(eng, fn)
        ins.is_dma = dma_buf is not None
        reads = self._norm(reads)
        writes = self._norm(writes)
        deps, raw = [], []
        for buf, key in reads:
            for s in self._states(buf, key):
                if s[0] is not None:
                    deps.append(s[0])
                    raw.append(s[0])
                if getattr(buf, "excl", False):
                    for r in s[1].values():
                        if r.eng != eng:
                            deps.append(r)
        for buf, key in writes:
            for s in self._states(buf, key):
                if s[0] is not None:
                    deps.append(s[0])
                deps.extend(s[1].values())
        fdeps = {}
        for d in deps:
            if d is ins:
                continue
            if (not ins.is_dma) and (not d.is_dma) and d.eng == eng:
                if eng == "pe" or not any(d is r for r in raw):
                    continue
            if d.is_dma:
                fdeps[id(d)] = (d, d.buf_sem.dma_cnt * 16)
            else:
                fdeps[id(d)] = (d, None)
        ins.deps = list(fdeps.values())
        if ins.is_dma:
            ins.buf_sem = dma_buf
            dma_buf.dma_cnt += 1
            ins.dma_count = dma_buf.dma_cnt * 16
        rkey = ("dma", id(ins)) if ins.is_dma else eng
        for buf, key in reads:
            for s in self._states(buf, key):
                s[1][rkey] = ins
        for buf, key in writes:
            for s in self._states(buf, key):
                s[0] = ins
                if key is None or s is not buf.st.get(None):
                    s[1] = {}
        self.streams[eng].append(ins)
        self.all.append(ins)
        return ins

    def barrier(self, dma_bufs_all):
        last = {}
        for e in ("pe", "dve", "act", "pool"):
            for ins in reversed(self.streams[e]):
                if not isinstance(ins, tuple) and not ins.is_dma:
                    last[e] = ins
                    ins.signal = True
                    break
        snap = [(b, b.dma_cnt * 16) for b in dma_bufs_all if b.dma_cnt > 0]
        mark = ("barrier", last, snap)
        for e in ENGS:
            self.streams[e].append(mark)

    def emit(self, final_wait_eng="sp"):
        nc = self.nc
        for ins in self.all:
            for d, _v in ins.deps:
                if not d.is_dma:
                    d.signal = True
        with contextlib.ExitStack() as es:
            eng_sem = {e: es.enter_context(nc.semaphore("s_" + e)) for e in ("pe", "dve", "act", "pool")}
            dma_bufs = []
            for ins in self.all:
                if isinstance(ins, tuple):
                    continue
                if ins.is_dma and ins.buf_sem.dma_sem is None:
                    ins.buf_sem.dma_sem = es.enter_context(nc.semaphore("d_%d" % len(dma_bufs)))
                    dma_bufs.append(ins.buf_sem)
            cnt = {e: 0 for e in eng_sem}
            for ins in self.all:
                if not ins.is_dma and ins.signal:
                    cnt[ins.eng] += 1
                    ins.count = cnt[ins.eng]
            block = es.enter_context(nc.Block())
            engobj = {"pe": block.tensor, "dve": block.vector, "act": block.scalar, "pool": block.gpsimd,
                      "sp": block.sync}
            sched = self

            def make(ename):
                def body(eng):
                    water = {}
                    for ins in sched.streams[ename]:
                        if isinstance(ins, tuple):
                            _, last, snap = ins
                            nw_ = 0
                            for e2, li in last.items():
                                if e2 != ename and water.get(("e", e2), 0) < li.count:
                                    water[("e", e2)] = li.count
                                    eng.wait_ge(eng_sem[e2], li.count)
                                    nw_ += 1
                            for b_, v_ in snap:
                                if water.get(("d", id(b_)), 0) < v_:
                                    water[("d", id(b_))] = v_
                                    eng.wait_ge(b_.dma_sem, v_)
                                    nw_ += 1
                            continue
                        need = {}
                        for d, dv in ins.deps:
                            if d.is_dma:
                                key, sem, val = ("d", id(d.buf_sem)), d.buf_sem.dma_sem, dv
                            else:
                                key, sem, val = ("e", d.eng), eng_sem[d.eng], d.count
                            if need.get(key, (None, 0))[1] < val:
                                need[key] = (sem, val)
                        todo = [(key, sem, val) for key, (sem, val) in need.items() if water.get(key, 0) < val]
                        if ename == "pe" and not ins.is_dma:
                            k_ = sched.pe_groups
                            sched.pe_groups += 1
                            if todo and sched.filler is not None:
                                for _ in range(sched.fill_table.get(k_, 0)):
                                    sched.filler(eng)
                        for key, sem, val in todo:
                            water[key] = val
                            eng.wait_ge(sem, val)
                        bi = ins.fn(eng)
                        if ins.is_dma:
                            bi.then_inc(ins.buf_sem.dma_sem, 16)
                        elif ins.signal:
                            bi.then_inc(eng_sem[ename], 1)
                    if ename == final_wait_eng:
                        for b in dma_bufs:
                            eng.wait_ge(b.dma_sem, b.dma_cnt * 16)
                return body

            for ename in ENGS:
                engobj[ename](make(ename))


FILL_TABLE = (
    "8:13,16:4,24:3,32:9,40:4,48:3,56:12,64:4,72:2,80:10,88:4,96:3,104:12,112:3,120:3,128:11,136:4,145:3,154:10,163"
    ":2,172:2,181:8,190:2,199:3,208:1,212:80,220:7,316:2,384:9,387:1,417:13,437:8,457:35,465:2,481:10,489:9,585:2,6"
    "53:9,656:1,686:14,706:7,726:34,734:2,750:10,758:9,854:2,922:9,925:1,955:14,975:8,995:34,1003:2,1019:10,1027:9,"
    "1123:2,1191:9,1194:1,1224:14,1244:8,1264:34,1272:2,1288:10,1296:9,1392:2,1460:9,1463:1,1493:14,1513:8,1533:34,"
    "1541:2,1557:10,1565:9,1661:2,1729:9,1732:1,1762:14,1782:8,1802:34,1810:2,1826:10,1834:9,1930:2,1998:9,2001:1,2"
    "031:14,2051:8,2071:34,2079:2,2095:10,2103:9,2199:2,2267:9,2270:1,2300:14,2320:8,2340:34,2348:2,2364:10,2372:9,"
    "2468:2,2536:9,2539:1,2569:14,2589:8,2609:34,2617:2,2633:10,2641:9,2737:2,2805:9,2808:1,2838:14,2858:8,2878:34,"
    "2886:2,2902:10,2910:9,3006:2,3074:9,3077:1,3107:14,3127:8,3147:35,3155:2,3171:10,3179:9,3275:2,3343:9,3346:1,3"
    "376:14,3396:8,3416:34,3424:2,3440:10,3448:9,3544:2,3612:9,3615:1,3645:14,3665:8,3685:34,3693:2,3709:10,3717:9,"
    "3813:2,3881:9,3884:1,3914:14,3934:8,3954:34,3962:2,3978:10,3986:9,4082:2,4150:9,4153:1,4183:14,4203:8,4223:34,"
    "4231:2,4247:10,4255:9,4351:2,4431:11,4434:1,4464:14,4484:8,4488:4,4496:8,4512:26,4520:2,4536:10,4544:16,4652:7"
    ",4732:8,4735:3,4765:18,4789:14,4793:3,4801:25,4809:16,4813:4,4821:25,4829:18,4833:3,4841:21,4849:18,4853:2,486"
    "1:22,4869:15,4873:4,4881:23,4889:16,4893:5,4901:25,4909:15,4913:3,4921:23,4929:18,4933:3,4941:23,4949:17,4953:"
    "4,4961:24,4969:16,4973:3,4981:22,4989:15,4993:3,5001:27,5009:14,5013:4,5021:23,5029:16,5033:3,5041:22,5049:15,"
    "5053:3,5061:24,5069:15,5073:4,5081:23,5089:15,5093:3,5101:7,5117:25,5125:1,5141:80,5149:7,5245:2,5301:15,5304:"
    "1,5313:14,5317:7,5342:7,5350:4,5394:2,5398:2,5406:2,5414:2,5418:4,5426:1,5442:17,5446:1,5454:13,5462:9,5558:2,"
    "5614:15,5617:1,5626:10,5630:9,5655:7,5663:6,5675:1,5707:2,5711:2,5719:2,5727:1,5731:4,5739:1,5755:20,5759:1,57"
    "67:12,5775:8,5871:2,5927:15,5930:1,5939:10,5943:9,5968:7,5976:4,6020:2,6024:2,6032:2,6040:1,6044:4,6052:1,6068"
    ":20,6072:1,6080:14,6088:9,6184:2,6240:15,6243:1,6252:12,6256:9,6281:9,6289:6,6333:2,6337:2,6345:2,6353:1,6357:"
    "4,6381:19,6385:1,6393:14,6401:9,6497:2,6553:15,6556:1,6565:12,6569:9,6594:7,6602:4,6646:2,6650:2,6658:2,6666:1"
    ",6670:4,6694:14,6698:1,6706:13,6714:8,6810:2,6866:15,6869:1,6878:11,6882:9,6907:6,6915:6,6959:2,6963:2,6971:2,"
    "6979:1,6983:4,7007:14,7011:1,7019:14,7027:9,7123:2,7179:15,7182:1,7191:13,7195:9,7220:8,7228:4,7272:2,7276:2,7"
    "284:2,7292:1,7296:5,7304:1,7320:18,7324:1,7332:14,7340:9,7436:2,7492:15,7495:1,7504:11,7508:9,7533:6,7541:4,75"
    "85:2,7589:2,7597:2,7605:1,7609:5,7617:1,7633:18,7637:1,7645:14,7653:9,7749:2,7805:15,7808:1,7817:11,7821:9,784"
    "6:6,7854:4,7898:2,7902:2,7910:2,7918:1,7922:5,7930:1,7946:18,7950:1,7958:14,7966:9,8062:2,8118:15,8121:1,8130:"
    "11,8134:9,8159:6,8167:4,8211:3,8215:2,8223:2,8231:1,8235:6,8243:1,8259:13,8263:1,8271:14,8279:9,8375:2,8431:15"
    ",8434:1,8443:14,8447:7,8472:6,8480:4,8524:2,8528:2,8536:2,8544:1,8548:5,8556:1,8572:18,8576:1,8584:14,8592:9,8"
    "688:2,8744:15,8747:1,8756:11,8760:9,8785:6,8793:4,8837:2,8841:2,8849:2,8857:1,8861:4,8885:18,8889:1,8897:14,89"
    "05:9,9001:2,9057:15,9060:1,9069:11,9073:9,9098:6,9106:4,9150:2,9154:2,9162:2,9170:1,9174:4,9198:18,9202:1,9210"
    ":14,9218:9,9314:2,9370:15,9373:1,9382:11,9386:9,9411:6,9419:4,9463:2,9467:2,9475:2,9483:1,9487:4,9511:18,9515:"
    "1,9523:14,9531:9,9627:2,9683:15,9686:1,9695:11,9699:9,9724:6,9732:4,9776:2,9780:2,9788:2,9796:1,9800:4,9824:18"
    ",9828:1,9836:14,9844:9,9940:2,10008:15,10011:1,10020:11,10024:9,10049:7,10057:4,10101:2,10105:2,10113:2,10121:"
    "1,10125:4,10133:1,10149:18,10153:1,10161:14,10169:23,10277:4,10345:14,10348:1,10357:29,10361:8,10386:4,10390:4"
    ",10394:1,10402:3,10436:2,10445:2,10450:61,10484:1,10493:2,10498:62,10532:1,10541:2,10545:1,10546:63,10580:1,10"
    "589:2,10606:8,10610:1,10618:80,10626:7,10722:2,10778:15,10781:1,10790:11,10794:9,10819:6,10827:4,10871:2,10875"
    ":2,10883:2,10891:1,10895:4,10903:1,10919:20,10923:1,10931:13,10939:8,11035:2,11091:15,11094:1,11103:14,11107:7"
    ",11132:7,11140:4,11184:2,11188:2,11196:2,11204:1,11208:4,11216:1,11232:20,11236:1,11244:14,11252:9,11348:2,114"
    "04:15,11407:1,11416:12,11420:9,11445:9,11453:6,11497:2,11501:2,11509:2,11517:1,11521:5,11529:1,11545:18,11549:"
    "1,11557:14,11565:9,11661:2,11717:15,11720:1,11729:11,11733:9,11758:6,11766:4,11810:2,11814:2,11822:2,11830:1,1"
    "1834:4,11858:18,11862:1,11870:14,11878:9,11974:2,12030:15,12033:1,12042:11,12046:9,12071:6,12079:4,12123:2,121"
    "27:2,12135:2,12143:1,12147:4,12171:18,12175:1,12183:14,12191:9,12287:2,12343:15,12346:1,12355:11,12359:9,12384"
    ":6,12392:4,12436:2,12440:2,12448:2,12456:1,12460:4,12484:18,12488:1,12496:13,12504:9,12600:2,12656:15,12659:1,"
    "12668:11,12672:9,12697:6,12705:6,12749:2,12753:2,12761:2,12769:1,12773:4,12797:14,12801:1,12809:14,12817:9,129"
    "13:2,12969:15,12972:1,12981:12,12985:9,13010:9,13018:6,13062:2,13066:2,13074:2,13082:1,13086:4,13110:18,13114:"
    "1,13122:14,13130:9,13226:2,13282:16,13285:1,13294:11,13298:9,13323:6,13331:4,13375:2,13379:2,13387:2,13395:1,1"
    "3399:4,13423:13,13427:1,13435:13,13443:8,13539:2,13595:15,13598:1,13607:11,13611:9,13636:6,13644:6,13688:2,136"
    "92:2,13700:2,13708:1,13712:4,13720:1,13736:19,13740:1,13748:14,13756:9,13852:2,13908:15,13911:1,13920:12,13924"
    ":9,13949:9,13957:6,14001:2,14005:2,14013:2,14021:1,14025:4,14049:18,14053:1,14061:14,14069:9,14165:2,14221:15,"
    "14224:1,14233:11,14237:9,14262:6,14270:4,14314:2,14318:2,14326:2,14334:1,14338:4,14362:18,14366:1,14374:14,143"
    "82:9,14478:2,14534:15,14537:1,14546:11,14550:9,14575:6,14583:4,14627:2,14631:2,14639:2,14647:1,14651:5,14659:1"
    ",14675:18,14679:1,14687:14,14695:9,14791:2,14847:15,14850:1,14859:11,14863:9,14888:6,14896:4,14940:2,14944:3,1"
    "4952:2,14960:1,14964:5,14972:1,14988:19,14992:1,15000:14,15008:8,15104:2,15160:15,15163:1,15172:11,15176:9,152"
    "01:6,15209:4,15253:2,15257:2,15265:2,15273:1,15277:4,15301:17,15305:1,15313:14,15321:9,15417:2,15485:15,15488:"
    "1,15497:10,15501:9,15526:8,15534:4,15578:2,15582:2,15590:2,15598:1,15602:4,15626:12,15630:1,15638:14,15646:22,"
    "15754:3,15822:14,15825:1,15834:29,15838:8,15863:5,15867:4,15871:1,15879:3,15913:2,15922:3,15927:63,15961:1,159"
    "70:2,15975:65,16009:2,16018:2,16023:60,16057:1,16066:2,16083:9,16087:1,16095:39,16103:20,16111:4,16119:3,16127"
    ":12,16135:4,16143:4,16151:10,16159:4,16167:3,16175:12,16183:4,16191:2,16199:9,16207:4,16215:3,16223:9,16231:4,"
    "16240:4,16249:10,16258:2,16267:2,16276:7,16285:1,16294:2,16303:1,16305:1,16307:80,16315:7,16411:2,16479:8,1648"
    "2:1,16512:14,16532:9,16552:34,16560:2,16576:10,16584:9,16680:2,16748:9,16751:1,16781:15,16801:8,16821:35,16829"
    ":2,16845:10,16853:9,16949:2,17017:9,17020:1,17050:14,17070:8,17090:34,17098:2,17114:10,17122:9,17218:2,17286:9"
    ",17289:1,17319:14,17339:8,17359:34,17367:2,17383:10,17391:9,17487:2,17555:9,17558:1,17588:14,17608:8,17628:34,"
    "17636:2,17652:10,17660:9,17756:2,17824:9,17827:1,17857:14,17877:8,17897:34,17905:2,17921:10,17929:9,18025:2,18"
    "093:9,18096:1,18126:14,18146:8,18166:34,18174:2,18190:10,18198:9,18294:2,18362:9,18365:1,18395:14,18415:8,1843"
    "5:34,18443:2,18459:10,18467:9,18563:2,18631:9,18634:1,18664:14,18684:8,18704:34,18712:2,18728:10,18736:9,18832"
    ":2,18900:9,18903:1,18933:14,18953:8,18973:34,18981:2,18997:10,19005:9,19101:2,19169:9,19172:1,19202:14,19222:8"
    ",19242:34,19250:2,19266:10,19274:9,19370:2,19438:8,19441:1,19471:14,19491:8,19511:34,19519:2,19535:10,19543:9,"
    "19639:2,19707:9,19710:1,19740:14,19760:7,19780:34,19788:2,19804:10,19812:9,19908:2,19976:9,19979:1,20009:14,20"
    "029:8,20049:35,20057:2,20073:10,20081:9,20177:2,20245:9,20248:1,20278:14,20298:8,20318:34,20326:2,20342:10,203"
    "50:9,20446:2,20526:11,20529:1,20559:14,20579:8,20583:4,20591:8,20607:24,20615:2,20631:10,20639:16,20747:4,2082"
    "7:11,20830:3,20860:14,20884:18,20888:4,20896:24,20904:16,20908:4,20916:23,20924:15,20928:3,20936:27,20944:18,2"
    "0948:3,20956:23,20964:15,20968:3,20976:24,20984:16,20988:5,20996:23,21004:15,21008:3,21016:25,21024:16,21028:3"
    ",21036:22,21044:16,21048:4,21056:23,21064:16,21068:4,21076:26,21084:15,21088:3,21096:24,21104:16,21108:4,21116"
    ":23,21124:15,21128:3,21136:24,21144:16,21148:3,21156:21,21164:16,21168:3,21176:27,21184:15,21188:3,21196:7,212"
    "12:25,21220:2,21236:80,21244:8,21340:2,21396:15,21399:1,21408:14,21412:9,21437:7,21445:4,21489:2,21493:2,21501"
    ":2,21509:1,21513:4,21521:1,21537:18,21541:1,21549:12,21557:9,21653:2,21709:15,21712:1,21721:11,21725:9,21750:6"
    ",21758:6,21802:3,21806:2,21814:2,21822:1,21826:4,21850:14,21854:1,21862:13,21870:8,21966:2,22022:15,22025:1,22"
    "034:10,22038:7,22063:9,22071:4,22115:2,22119:2,22127:2,22135:1,22139:5,22147:1,22163:20,22167:1,22175:14,22183"
    ":9,22279:2,22335:15,22338:1,22347:12,22351:9,22376:6,22384:4,22428:2,22432:2,22440:2,22448:1,22452:4,22460:1,2"
    "2476:19,22480:1,22488:13,22496:8,22592:2,22648:15,22651:1,22660:11,22664:9,22689:9,22697:6,22741:2,22745:2,227"
    "53:2,22761:1,22765:4,22789:18,22793:1,22801:13,22809:8,22905:2,22961:15,22964:1,22973:11,22977:9,23002:6,23010"
    ":4,23054:2,23058:2,23066:2,23074:1,23078:4,23086:1,23102:20,23106:1,23114:14,23122:9,23218:2,23274:15,23277:1,"
    "23286:11,23290:9,23315:6,23323:4,23367:2,23371:2,23379:2,23387:1,23391:5,23399:1,23415:18,23419:1,23427:14,234"
    "35:9,23531:2,23587:15,23590:1,23599:11,23603:9,23628:6,23636:4,23680:2,23684:2,23692:2,23700:1,23704:4,23728:1"
    "8,23732:1,23740:14,23748:9,23844:2,23900:15,23903:1,23912:11,23916:9,23941:6,23949:4,23993:2,23997:2,24005:2,2"
    "4013:1,24017:4,24041:18,24045:1,24053:14,24061:9,24157:2,24213:15,24216:1,24225:11,24229:9,24254:6,24262:4,243"
    "06:2,24310:2,24318:2,24326:1,24330:4,24354:19,24358:1,24366:14,24374:9,24470:2,24526:15,24529:1,24538:11,24542"
    ":9,24567:7,24575:4,24619:2,24623:2,24631:2,24639:1,24643:4,24667:13,24671:1,24679:13,24687:8,24783:2,24839:15,"
    "24842:1,24851:12,24855:9,24880:6,24888:6,24932:2,24936:2,24944:2,24952:1,24956:4,24964:1,24980:20,24984:1,2499"
    "2:12,25000:8,25096:2,25152:15,25155:1,25164:12,25168:11,25193:6,25201:4,25245:2,25249:2,25257:2,25265:1,25269:"
    "4,25293:18,25297:1,25305:14,25313:9,25409:2,25465:15,25468:1,25477:11,25481:9,25506:6,25514:6,25558:2,25562:2,"
    "25570:2,25578:1,25582:4,25606:15,25610:1,25618:13,25626:8,25722:2,25778:15,25781:1,25790:11,25794:9,25819:9,25"
    "827:6,25871:2,25875:2,25883:2,25891:1,25895:4,25919:17,25923:1,25931:14,25939:9,26035:2,26103:15,26106:1,26115"
    ":10,26119:9,26144:8,26152:4,26196:2,26200:2,26208:2,26216:1,26220:4,26244:13,26248:1,26256:14,26264:22,26372:3"
    ",26440:14,26443:1,26452:28,26456:8,26481:4,26485:4,26489:1,26497:3,26531:2,26540:2,26545:64,26579:1,26588:2,26"
    "593:62,26627:1,26636:2,26641:63,26675:1,26684:2,26701:8,26705:1,26713:80,26721:6,26817:2,26873:15,26876:1,2688"
    "5:14,26889:7,26914:6,26922:4,26966:2,26970:2,26978:2,26986:1,26990:4,26998:1,27014:17,27018:1,27026:41,27034:7"
    ",27130:2,27186:15,27189:1,27198:10,27202:9,27227:6,27235:6,27279:2,27283:2,27291:2,27299:1,27303:4,27327:15,27"
    "331:1,27339:41,27347:7,27443:2,27499:15,27502:1,27511:14,27515:7,27540:7,27548:4,27592:2,27596:2,27604:2,27612"
    ":1,27616:4,27624:1,27640:20,27644:1,27652:41,27660:7,27756:2,27812:15,27815:1,27824:14,27828:7,27853:8,27861:4"
    ",27905:2,27909:2,27917:2,27925:1,27929:4,27937:1,27953:18,27957:1,27965:39,27973:7,28069:2,28125:15,28128:1,28"
    "137:11,28141:9,28166:6,28174:4,28218:2,28222:2,28230:2,28238:1,28242:4,28250:1,28266:19,28270:1,28278:40,28286"
    ":7,28382:2,28438:15,28441:2,28450:11,28454:9,28479:6,28487:4,28531:2,28535:2,28543:2,28551:1,28555:4,28579:18,"
    "28583:1,28591:40,28599:7,28695:2,28751:15,28754:1,28763:10,28767:9,28792:6,28800:6,28844:2,28848:2,28856:2,288"
    "64:1,28868:4,28876:1,28892:19,28896:1,28904:40,28912:7,29008:2,29064:15,29067:1,29076:11,29080:9,29105:6,29113"
    ":4,29157:2,29161:2,29169:2,29177:1,29181:4,29189:1,29205:20,29209:1,29217:40,29225:7,29321:2,29377:15,29380:1,"
    "29389:10,29393:9,29418:8,29426:4,29470:2,29474:2,29482:2,29490:1,29494:4,29518:12,29522:1,29530:41,29538:7,296"
    "34:2,29690:15,29693:1,29702:10,29706:9,29731:8,29739:4,29783:2,29787:2,29795:2,29803:1,29807:4,29831:13,29835:"
    "1,29843:40,29851:7,29947:2,30003:15,30006:1,30015:10,30019:9,30044:8,30052:4,30096:2,30100:2,30108:2,30116:1,3"
    "0120:4,30144:12,30148:1,30156:41,30164:7,30260:2,30316:15,30319:1,30328:10,30332:9,30357:8,30365:4,30409:2,304"
    "13:2,30421:2,30429:1,30433:4,30457:13,30461:1,30469:40,30477:7,30573:2,30629:15,30632:1,30641:11,30645:9,30670"
    ":7,30678:4,30722:2,30726:2,30734:2,30742:1,30746:4,30770:13,30774:1,30782:41,30790:7,30886:2,30942:15,30945:1,"
    "30954:10,30958:9,30983:7,30991:4,31035:2,31039:2,31047:2,31055:1,31059:4,31067:1,31083:19,31087:1,31095:41,311"
    "03:7,31199:2,31255:15,31258:1,31267:14,31271:7,31296:7,31304:4,31348:2,31352:2,31360:2,31368:1,31372:4,31380:1"
    ",31396:17,31400:1,31408:41,31416:7,31512:2,31580:15,31583:1,31592:11,31596:9,31621:6,31629:4,31673:2,31677:2,3"
    "1685:2,31693:1,31697:5,31705:1,31721:17,31725:1,31733:41,31741:11,31849:6,31917:14,31920:1,31929:25,31933:12,3"
    "1958:10,31962:6,31966:1,31974:3,32008:2,32017:3,32022:63,32056:1,32065:2,32070:64,32104:2,32113:2,32118:62,321"
    "52:1,32161:2,32178:8,32182:1"
)
D = 1024
SEQ = 2048
NT = SEQ // 128
NSB = 16
SL = 4
DEPTH = 2
IN_DIM = 6688
EPS = 1e-6
NEG = -30000.0
OFF_ZS, OFF_XBC, OFF_DT, OFF_ZG, OFF_QKV, OFF_A, OFF_B = 0, 1024, 2560, 2576, 3600, 6672, 6680


class Cfg:
    def __init__(self, name, T, nseq, L, Q, chain):
        self.name, self.T, self.nseq, self.L, self.Q, self.chain = name, T, nseq, L, Q, chain
        self.nch = T // Q
        self.nlev = {64: 5, 4: 1}[Q]


CP = Cfg("p", 128, 1, 128, 64, True)
CS = Cfg("s", 64, NSB, SL, 4, False)


def host_consts():
    c = {"ident": np.eye(128, dtype=np.float32), "ones": np.ones((128, 128), np.float32)}
    for cfg in (CP, CS):
        T, Q, nch = cfg.T, cfg.Q, cfg.nch
        ch = np.arange(T) // Q
        same = ch[:, None] == ch[None, :]
        idx = np.arange(T)
        le = idx[:, None] <= idx[None, :]
        lt = idx[:, None] < idx[None, :]
        n = cfg.name
        c["tri_" + n] = (same & le).astype(np.float32)
        c["blk_" + n] = same.astype(np.float32)
        c["nmI_" + n] = np.where(same & le, 0.0, NEG).astype(np.float32)
        c["nmS_" + n] = np.where(same & lt, 0.0, NEG).astype(np.float32)
        c["nmSr_" + n] = np.ascontiguousarray(c["nmS_" + n].T)
        cm = (ch[None, :] == np.arange(nch)[:, None]).astype(np.float32)
        c["cmf_" + n] = np.ascontiguousarray(np.broadcast_to(cm[None], (128, nch, T))).astype(np.float32)
        c["cmT_" + n] = np.ascontiguousarray(cm.T)
    e = np.zeros((17, 128), np.float32)
    e[0, :] = 1.0
    c["E_p"] = e
    e = np.zeros((17, 64), np.float32)
    for t in range(64):
        e[1 + t // SL, t] = 1.0
    c["E_s"] = e
    return c


def bl(ap, n):
    return ap.unsqueeze(len(ap.shape)).to_broadcast(list(ap.shape) + [n])


def bm(ap, n):
    return ap.unsqueeze(1).to_broadcast([ap.shape[0], n] + list(ap.shape[1:]))


def build(debug=(), limit=None):
    limit = limit or {}
    nc = bass.Bass("TRN2", target_bir_lowering=False)
    S = Sched(nc)
    es = contextlib.ExitStack()
    consts_np = host_consts()

    def din(name, shape):
        return nc.dram_tensor(name, list(shape), F32, kind="ExternalInput").ap()

    def dout(name, shape):
        return nc.dram_tensor(name, list(shape), F32, kind="ExternalOutput").ap()

    I = {}
    I["xp"] = din("xp", [SEQ, D])
    I["xs"] = din("xs", [NSB * SL, D])
    I["call"] = din("call", [17, D])
    I["st_sc"] = din("st_sc", [DEPTH, NSB, 3, 1536])
    I["st_ssm"] = din("st_ssm", [DEPTH, NSB, 16, 64, 128])
    I["st_gc"] = din("st_gc", [DEPTH, NSB, 3, 3072])
    I["st_gdn"] = din("st_gdn", [DEPTH, NSB, 8, 128, 128])
    wshapes = {"norm_w": [DEPTH, D], "w_ada": [DEPTH, D, 3 * D], "b_ada": [DEPTH, 3 * D], "w_in": [DEPTH, D, IN_DIM],
               "ssd_conv_w": [DEPTH, 4, 1536], "ssd_conv_b": [DEPTH, 1536], "ssd_dt_bias": [DEPTH, 16],
               "ssd_a_log": [DEPTH, 16], "ssd_d": [DEPTH, 16], "ssd_norm_w": [DEPTH, 1024],
               "gdn_conv_w": [DEPTH, 4, 3072], "gdn_dt_bias": [DEPTH, 8], "gdn_a_log": [DEPTH, 8],
               "gdn_norm_w": [DEPTH, 128], "w_out": [DEPTH, 2048, D], "final_norm_w": [1, D]}
    for k, shp in wshapes.items():
        I[k] = din(k, shp)
    for k, v in consts_np.items():
        I["c_" + k] = din("c_" + k, v.shape)
    O = {}
    O["y_p"] = dout("y_p", [SEQ, D])
    O["y_s"] = dout("y_s", [NSB * SL, D])
    O["ncs_p"] = dout("ncs_p", [DEPTH, 3, 1536])
    O["nssm_p"] = dout("nssm_p", [DEPTH, 1024, 128])
    O["ngc_p"] = dout("ngc_p", [DEPTH, 3, 3072])
    O["ngdn_p"] = dout("ngdn_p", [DEPTH, 8, 128, 128])
    O["ncs_s"] = dout("ncs_s", [DEPTH, NSB * 3, 1536])
    O["nssm_s"] = dout("nssm_s", [DEPTH, NSB, 1024, 128])
    O["ngc_s"] = dout("ngc_s", [DEPTH, NSB * 3, 3072])
    O["ngdn_s"] = dout("ngdn_s", [DEPTH, NSB, 8, 128, 128])
    NROW = SEQ + NSB * SL
    scr = [Buf(nc.dram_tensor("scr%d" % i, [NROW, D], F32, kind="Internal").ap(), "scr%d" % i) for i in range(3)]
    xin_ext = None

    allb = []

    def sb(name, shape, dt=F32):
        b_ = Buf(es.enter_context(nc.sbuf_tensor(name, list(shape), dt)), name)
        allb.append(b_)
        return b_

    banks = [Buf(es.enter_context(nc.psum_tensor("bank%d" % i, [128, 512], F32)), "bank%d" % i) for i in range(8)]
    for b_ in banks:
        b_.excl = True
    pstate = {"i": 0, "pinned": set()}
    NB = 7

    def pget():
        while True:
            i = pstate["i"] % NB
            pstate["i"] += 1
            if i not in pstate["pinned"]:
                return banks[i]

    def ppin(b):
        pstate["pinned"].add(banks.index(b))

    def punpin(b):
        pstate["pinned"].discard(banks.index(b))

    def pf(b, shape, p=128):
        n = int(np.prod(shape))
        ap = b[:p, 0:n]
        if len(shape) == 1:
            return ap
        names = " ".join("a%d" % i for i in range(len(shape)))
        kw = {"a%d" % i: shape[i] for i in range(1, len(shape))}
        return ap.rearrange("p (%s) -> p %s" % (names, names), **kw)

    def pb(b, shape, p=128):
        n = int(np.prod(shape))
        ap = b[:p, :].bitcast(BF16)[:, 0:n]
        if len(shape) == 1:
            return ap
        names = " ".join("a%d" % i for i in range(len(shape)))
        kw = {"a%d" % i: shape[i] for i in range(1, len(shape))}
        return ap.rearrange("p (%s) -> p %s" % (names, names), **kw)

    def MM(out, lhsT, rhs, R, W, start=True, stop=True, **kw):
        S.add("pe", lambda e: e.matmul(out, lhsT=lhsT, rhs=rhs, start=start, stop=stop, **kw), reads=R, writes=W)

    def TR(out, in_, ident, R, W):
        S.add("pe", lambda e: e.transpose(out=out, in_=in_, identity=ident), reads=R, writes=W)

    def ACT(out, in_, func, R, W, **kw):
        S.add("act", lambda e: e.activation(out=out, in_=in_, func=func, **kw), reads=R, writes=W)

    def TT(out, in0, in1, op, R, W, eng="dve"):
        S.add(eng, lambda e: e.tensor_tensor(out=out, in0=in0, in1=in1, op=op), reads=R, writes=W)

    def TS(out, in0, s1, s2, op0, op1, R, W, eng="dve"):
        S.add(eng, lambda e: e.tensor_scalar(out=out, in0=in0, scalar1=s1, scalar2=s2, op0=op0, op1=op1), reads=R, writes=W)

    def STT(out, in0, scalar, in1, op0, op1, R, W):
        S.add("dve", lambda e: e.scalar_tensor_tensor(out=out, in0=in0, scalar=scalar, in1=in1, op0=op0, op1=op1), reads=R, writes=W)

    def CP_(out, in_, R, W, eng="dve"):
        S.add(eng, lambda e: e.tensor_copy(out=out, in_=in_), reads=R, writes=W)

    def RECIP(out, in_, R, W):
        S.add("dve", lambda e: e.reciprocal(out=out, in_=in_), reads=R, writes=W)

    def RED(out, in_, R, W):
        S.add("dve", lambda e: e.tensor_reduce(out=out, in_=in_, axis=AX.X, op=ALU.add), reads=R, writes=W)

    def MSET(ap, val, W, eng="pool"):
        S.add(eng, lambda e: e.memset(ap, val), writes=W)

    def LD(out, in_, buf, R=None, nonc=False):
        if nonc:
            S.add("sp", lambda e: e.dma_start(out=out, in_=in_, allow_slow_non_contiguous=True), reads=R, writes=[buf], dma_buf=buf)
        else:
            S.add("sp", lambda e: e.dma_start(out=out, in_=in_), reads=R, writes=[buf], dma_buf=buf)

    sto_n = [0]
    sto_dummy = Buf(None, "sto_dummy")

    def STO(out, in_, buf, W=None, extra_reads=None):
        S.add("sp", lambda e: e.dma_start(out=out, in_=in_), reads=[buf] + (extra_reads or []), writes=W, dma_buf=buf)

    dbg_out = {}
    mod_stacks = []
    cur_es = [es]
    all_bufs = []

    def DBG(name, buf, ap, shape):
        if name in debug and name not in dbg_out:
            o = dout("dbg_" + name, shape)
            dbg_out[name] = o
            if ap.dtype != F32:
                tmp = Buf(cur_es[0].enter_context(nc.sbuf_tensor("dbgt_" + name, list(shape), F32)), "dbgt_" + name)
                all_bufs.append(tmp)
                CP_(tmp[:], ap, [buf], [tmp])
                STO(o, tmp[:], tmp)
            else:
                STO(o, ap, buf)

    C = {}
    for k, v in consts_np.items():
        if k.startswith("cmf_"):
            continue
        C[k] = sb("k_" + k, v.shape)
        LD(C[k][:], I["c_" + k], C[k])
    identf = C["ident"]
    onesf = C["ones"]
    identb = sb("identb", [128, 128], BF16)
    CP_(identb[:], identf[:], [identf], [identb])
    onesb = sb("onesb", [128, 128], BF16)
    CP_(onesb[:], onesf[:], [onesf], [onesb])
    cfg_pending = []
    for cfg in (CP, CS):
        n = cfg.name
        cfg.tri, cfg.blk, cfg.nmI, cfg.nmS, cfg.nmSr, cfg.cmT = (C[k + n] for k in ("tri_", "blk_", "nmI_", "nmS_", "nmSr_", "cmT_"))
        cfg.cmf = sb("cmfb_" + n, [128, cfg.nch, cfg.T], BF16)
        cfg.ncmf = sb("ncmfb_" + n, [128, cfg.nch, cfg.T], BF16)
        cfg_pending.append(cfg)
        cfg.cmTb = sb("cmTb_" + n, [cfg.T, cfg.nch], BF16)
        CP_(cfg.cmTb[:], cfg.cmT[:], [cfg.cmT], [cfg.cmTb])

    SW = 128
    stage = [sb("stage%d" % i, [128, 8, SW]) for i in range(2)]
    stage_i = [0]
    WO = sb("wo", [128, 8, D], BF16)
    DIAG = sb("diag", [128, 12, 4, 128], BF16)
    cwT = sb("cwT", [128, 12, 4])
    cbrow = sb("cbrow", [1, 1536], BF16)
    nwT = sb("nwT", [128, 8])
    scT = sb("scT", [128, 8, 17])
    shT = sb("shT", [128, 8, 17])
    gate_bc = {"p": sb("gate_p", [128, D]), "s": sb("gate_s", [64, D])}
    vec16 = sb("vec16", [128, 4, 16])
    vec8 = sb("vec8", [128, 2, 8])
    scTb = sb("scTb", [128, 8, 17], BF16)
    badaT = sb("badaT", [128, 24])
    nw_in = sb("nw_in", [128, 8])
    xt_slots = [sb("xt%d" % i, [128, D]) for i in range(2)]
    xa_slots = [sb("xa%d" % i, [128, D]) for i in range(2)]
    stage_all = stage + xt_slots + xa_slots
    junk = sb("junk", [128, D], BF16)
    xn = sb("xn", [128, D], BF16)
    hT = sb("hT", [128, 8, 128], BF16)
    sm1 = sb("sm1", [128, 8])
    raw = sb("raw", [128, 12, 132], BF16)
    tailf = sb("tailf", [128, 12, 48])
    tailo = sb("tailo", [48, 1536])
    cst_in = tailo
    fT = sb("fT", [128, 12, 128], BF16)
    big = [sb("big%d" % i, [128, D]) for i in range(4)]
    bigb = [sb("bigb%d" % i, [128, D], BF16) for i in range(3)]
    xo = big[3]
    call_sb, sil_c, bgate, gate17, cbrow_f = big[0], bigb[0], big[1], big[2], tailo
    hTf_t = big[0]
    for cfg in cfg_pending:
        n_ = cfg.nch * cfg.T
        tmpv = big[0][:, 0:n_].rearrange("p (c t) -> p c t", t=cfg.T)
        LD(tmpv, I["c_cmf_" + cfg.name], big[0])
        CP_(cfg.cmf[:], tmpv, [big[0]], [cfg.cmf])
        TS(cfg.ncmf[:], tmpv, -1.0, None, ALU.mult, ALU.bypass, [big[0]], [cfg.ncmf])

    def stg():
        i = stage_i[0] % len(stage_all)
        stage_i[0] += 1
        b_ = stage_all[i]
        v = b_[:, :, :] if b_ in stage else b_[:, :].rearrange("p (k n) -> p k n", k=8)
        return b_, v, ("pool", "act", "dve")[i % 3]

    def cast(eng, out, in_, R, W, scale=None):
        if eng == "act":
            if scale is None:
                ACT(out, in_, AF.Copy, R, W)
            else:
                ACT(out, in_, AF.Copy, R, W, scale=scale)
        elif scale is None:
            CP_(out, in_, R, W, eng=eng)
        else:
            TS(out, in_, scale, 1.0, ALU.mult, ALU.mult, R, W, eng=eng)

    def load_cols(dst, dcol, src, ncols):
        c0 = 0
        while c0 < ncols:
            n = min(SW, ncols - c0)
            sbuf_, st, eng = stg()
            LD(st[:, :, 0:n], src[:, c0:c0 + n].rearrange("(k p) n -> p k n", p=128), sbuf_)
            cast(eng, dst[:, :, dcol + c0:dcol + c0 + n], st[:, :, 0:n], [sbuf_], [dst])
            c0 += n

    def load_wo(src_rows, nk, scale_ap_fn):
        for half in range(D // SW):
            sbuf_, st, eng = stg()
            LD(st[:, 0:nk, :], src_rows[:, half * SW:(half + 1) * SW].rearrange("(k p) n -> p k n", p=128), sbuf_)
            for k in range(nk):
                cast(eng, WO[:, k, half * SW:(half + 1) * SW], st[:, k, :], [sbuf_, nwT], [WO], scale=scale_ap_fn(k))

    def build_diag(convw, c0, nchunks):
        for k_ in range(4):
            LD(cwT[:, 0:nchunks, k_], convw[k_, c0 * 128:(c0 + nchunks) * 128].rearrange("(c p) -> p c", p=128), cwT, nonc=True)
        for c in range(nchunks):
            TT(DIAG[:, c, :, :], bm(identf[:], 4), bl(cwT[:, c, :], 128), ALU.mult, [identf, cwT], [DIAG], eng="pool")

    def do_mod(l):
        LD(call_sb[:17, :], I["call"], call_sb)
        ACT(sil_c[:17, :], call_sb[:17, :], AF.Silu, [call_sb], [sil_c])
        b = pget()
        for c in range(8):
            TR(pb(b, [8, 32])[:, c, 0:17], sil_c[:17, c * 128:(c + 1) * 128], identb[:17, :17], [sil_c, identb], [b])
        CP_(scTb[:], pb(b, [8, 32])[:, :, 0:17], [b], [scTb])
        LD(badaT[:], I["b_ada"][l].rearrange("(c p) -> p c", p=128), badaT, nonc=True)
        LD(nw_in[:], I["norm_w"][l].rearrange("(c p) -> p c", p=128), nw_in, nonc=True)
        LD(bgate[0:1, :], I["b_ada"][l:l + 1, 2048:3072], bgate)
        nblk = 3 * D // SW
        per = D // SW
        modes = contextlib.ExitStack()
        wb = Buf(modes.enter_context(nc.sbuf_tensor("modwb%d" % l, [128, 8, SW], BF16)), "modwb")
        all_bufs.append(wb)
        for blk in range(nblk):
            sbuf_, st, eng = stg()
            LD(st, I["w_ada"][l][:, blk * SW:(blk + 1) * SW].rearrange("(k p) n -> p k n", p=128), sbuf_)
            cast(eng, wb[:, :, 0:SW], st, [sbuf_], [wb])
            if blk < 2 * per:
                b = pget()
                nj = SW // 128
                for j in range(nj):
                    for k in range(8):
                        MM(pf(b, [nj, 17])[:, j, :], wb[:, k, j * 128:(j + 1) * 128], scTb[:, k, :], [wb, scTb], [b],
                           start=(k == 0), stop=(k == 7))
                dst = shT if blk < per else scT
                cc = (blk % per) * nj
                TT(dst[:, cc:cc + nj, :], pf(b, [nj, 17]), bl(badaT[:, blk * nj:blk * nj + nj], 17), ALU.add, [b, badaT], [dst])
            else:
                g = pget()
                gc0 = (blk - 2 * per) * SW
                for k in range(8):
                    MM(g[:17, 0:SW], scTb[:, k, :], wb[:, k, 0:SW], [wb, scTb], [g], start=(k == 0), stop=False)
                MM(g[:17, 0:SW], onesf[0:1, 0:17], bgate[0:1, gc0:gc0 + SW], [onesf, bgate], [g], start=False, stop=True)
                CP_(gate17[:17, gc0:gc0 + SW], g[:17, 0:SW], [g], [gate17])
        TS(scT[:], scT[:], 1.0, None, ALU.add, ALU.bypass, [scT], [scT])
        TT(scT[:], scT[:], bl(nw_in[:], 17), ALU.mult, [scT, nw_in], [scT])
        for n, E, T in (("p", C["E_p"], 128), ("s", C["E_s"], 64)):
            for half in range(2):
                b = pget()
                MM(b[:T, :], E[:, :T], gate17[:17, half * 512:(half + 1) * 512], [E, gate17], [b])
                CP_(gate_bc[n][:T, half * 512:(half + 1) * 512], b[:T, :], [b], [gate_bc[n]])
        mod_stacks.append(modes)

    def rows(ti):
        return (ti * 128, 128) if ti < NT else (SEQ, 64)

    def src_ap(layer_in, ti):
        r0, T = rows(ti)
        if layer_in is None:
            return (I["xp"][r0:r0 + T, :] if ti < NT else I["xs"][:, :]), None
        return layer_in[r0:r0 + T, :], (layer_in, ti)

    def prologue(cfg, xt):
        T = cfg.T
        ACT(junk[:T, :], xt[:T, :], AF.Square, [xt], [junk, sm1], accum_out=sm1[:T, 0:1])
        TS(sm1[:T, 0:1], sm1[:T, 0:1], 1.0 / D, EPS, ALU.mult, ALU.add, [sm1], [sm1])
        ACT(sm1[:T, 0:1], sm1[:T, 0:1], AF.Ln, [sm1], [sm1])
        ACT(sm1[:T, 0:1], sm1[:T, 0:1], AF.Exp, [sm1], [sm1], scale=-0.5)
        ACT(xn[:T, :], xt[:T, :], AF.Copy, [xt, sm1], [xn], scale=sm1[:T, 0:1])
        b = pget()
        for c in range(8):
            TR(pb(b, [8, T])[:, c, :], xn[:T, c * 128:(c + 1) * 128], identb[:T, :T], [xn, identb], [b])
        if cfg.nseq == 1:
            sc, sh = bl(scT[:, :, 0], T), bl(shT[:, :, 0], T)
            o1, o2, i1 = hTf_t[:, 0:8 * T].rearrange("p (c t) -> p c t", t=T), hT[:, :, :T], pb(b, [8, T])
        else:
            sc, sh = bl(scT[:, :, 1:17], SL), bl(shT[:, :, 1:17], SL)
            o1 = hTf_t[:, 0:8 * T].rearrange("p (c b l) -> p c b l", c=8, l=SL)
            o2 = hT[:, :, :T].rearrange("p c (b l) -> p c b l", l=SL)
            i1 = pb(b, [8, NSB, SL])
        TT(o1, i1, sc, ALU.mult, [b, scT], [hTf_t])
        TT(o2, o1, sh, ALU.add, [hTf_t, shT], [hT])

    def inproj_conv(cfg, l, wcol0, nchunks, bias_row, st_conv_in, conv_out, conv_out_s, ch0, is_last, first):
        T, L, nseq = cfg.T, cfg.L, cfg.nseq
        rawv = raw[:, :, 0:nseq * (L + 3)].rearrange("p c (b l) -> p c b l", l=L + 3)
        if cfg.nseq == 1:
            if first:
                MSET(raw[:, :, 0:3], 0.0, [raw])
            else:
                CP_(raw[:, 0:nchunks, 0:3], raw[:, 0:nchunks, L:L + 3], [raw], [raw])
        else:
            LD(cst_in[:, 0:nchunks * 128], st_conv_in[:, :, ch0 * 128:(ch0 + nchunks) * 128].rearrange("b k c -> (b k) c"), cst_in)
            for g0 in range(0, nchunks, 8):
                b = pget()
                ng = min(8, nchunks - g0)
                for c in range(ng):
                    TR(pf(b, [8, 48])[:, c, :], cst_in[:, (g0 + c) * 128:(g0 + c + 1) * 128], identf[:48, :48], [cst_in, identf], [b])
                CP_(rawv[:, g0:g0 + ng, :, 0:3], pf(b, [8, 48])[:, 0:ng, :].rearrange("p c (b k) -> p c b k", k=3), [b], [raw])
        if limit.get("stopC", 99) <= 1:
            return
        for g0 in range(0, nchunks, 4):
            b = pget()
            for c in range(4):
                for k in range(8):
                    MM(pf(b, [4, T])[:, c, :], PA["WB"][:, k, wcol0 + (g0 + c) * 128: wcol0 + (g0 + c + 1) * 128], hT[:, k, :T],
                       [PA["WB"], hT], [b], start=(k == 0), stop=(k == 7))
            if nseq == 1:
                ACT(raw[:, g0:g0 + 4, 3:3 + L], pf(b, [4, T]), AF.Copy, [b], [raw])
            else:
                for c in range(4):
                    ACT(rawv[:, g0 + c, :, 3:3 + L], pf(b, [4, T])[:, c, :].rearrange("p (b l) -> p b l", l=L), AF.Copy, [b], [raw])
            if is_last and limit.get("tail", True):
                for c in range(4):
                    ACT(tailf[:, g0 + c, 0:nseq * 3].rearrange("p (b k) -> p b k", k=3),
                        pf(b, [4, T])[:, c, :].rearrange("p (b l) -> p b l", l=L)[:, :, L - 3:L], AF.Copy, [b], [tailf])
        if limit.get("stopC", 99) <= 2:
            return
        for g0 in range(0, nchunks, 4):
            b = pget()
            for c in range(4):
                o = pf(b, [4, T])[:, c, :]
                if nseq > 1:
                    o = o.rearrange("p (b l) -> p b l", l=L)
                for k in range(4):
                    MM(o, DIAG[:, g0 + c, k, :], rawv[:, g0 + c, :, k:k + L] if nseq > 1 else raw[:, g0 + c, k:k + L],
                       [DIAG, raw], [b], start=(k == 0), stop=(k == 3 and bias_row is None))
                if bias_row is not None:
                    MM(pf(b, [4, T])[:, c, :], bias_row[0:1, (g0 + c) * 128:(g0 + c + 1) * 128], onesb[0:1, :T], [bias_row, onesb], [b],
                       start=False, stop=True)
            ACT(fT[:, g0:g0 + 4, :T], pf(b, [4, T]), AF.Silu, [b], [fT])
        if limit.get("stopC", 99) <= 3:
            return
        if is_last:
            n3 = nseq * 3
            for g0 in range(0, nchunks, 4):
                b = pget()
                for c in range(4):
                    TR(pf(b, [4, 128], p=n3)[:, c, :], tailf[:, g0 + c, 0:n3], identf[:, :], [tailf, identf], [b])
                CP_(tailo[:n3, g0 * 128:(g0 + 4) * 128], b[:n3, :], [b], [tailo])
            dst = conv_out[l][:, ch0 * 128:(ch0 + nchunks) * 128] if nseq == 1 else conv_out_s[l][:, ch0 * 128:(ch0 + nchunks) * 128]
            STO(dst, tailo[:n3, 0:nchunks * 128], tailo)

    def cum_and_tot(cfg, a_ap, a_buf, nh, out_c, out_dbc, pw):
        T, nch = cfg.T, cfg.nch
        b = pget()
        MM(b[:T, 0:nh], cfg.tri[:, :], a_ap, [cfg.tri, a_buf], [b])
        MM(b[:T, nh:2 * nh], cfg.blk[:, :], a_ap, [cfg.blk, a_buf], [b])
        ACT(out_c[:T, 0:2 * nh], b[:T, 0:2 * nh], AF.Copy, [b], [out_c])
        b2 = pget()
        MM(b2[:nch, 0:nh], cfg.cmT[:, :], a_ap, [cfg.cmT, a_buf], [b2])
        CP_(tot[:nch, 0:nh], b2[:nch, 0:nh], [b2], [tot])
        TT(totx[:nch, 0:nch * nh].rearrange("p (c h) -> p c h", h=nh), bm(tot[:nch, 0:nh], nch), bl(identf[:nch, :nch], nh), ALU.mult,
           [tot, identf], [totx])
        b3 = pget()
        MM(b3[:pw, 0:nch * nh], onesf[:nch, :pw], totx[:nch, 0:nch * nh], [onesf, totx], [b3])
        ACT(out_dbc[:pw, 0:nch * nh], b3[:pw, 0:nch * nh], AF.Exp, [b3], [out_dbc])

    tot = sb("tot", [16, 16])
    totx = sb("totx", [16, 256])
    sm = {k: sb("sm_" + k, [128, 64]) for k in ("a", "b", "c", "d", "e")}
    dbc = sb("dbc", [128, 256])
    acT = sb("acT", [16, 128])

    PA = {}

    def alloc_A(pes, tag):
        def sbp(name, shape, dt=F32):
            return Buf(pes.enter_context(nc.sbuf_tensor(name + tag, list(shape), dt)), name)
        PA.update(stf=sbp("stf", [128, D]), stb=sbp("stb", [128, D], BF16), hn=sbp("hn", [128, 8, 128]),
                  stf_s=sbp("stf_s", [128, D]), stb_s=sbp("stb_s", [128, D], BF16), Gs=sbp("Gs", [128, 2, 128]),
                  earg=sbp("earg", [128, 4, 128]), Ee=sbp("Ee", [128, 4, 128]), WT=sbp("WT", [128, 16, 128], BF16),
                  Btok=sbp("Btok", [128, 256], BF16), Btm=sbp("Btm", [128, 2, 256], BF16), CTm=sbp("CTm", [128, 2, 2, 128], BF16),
                  WB=sbp("WB", [128, 8, 2576], BF16))
    mslot = [0]

    def ssd_state_update(cfg, c, st_in, stb_in, st_out, yo, xdec, first_c, last_c):
        T, nch = cfg.T, cfg.nch
        Btm, CTm, Btok = PA["Btm"], PA["CTm"], PA["Btok"]
        sl = mslot[0] % 2
        mslot[0] += 1
        TS(Btm[:T, sl, :], Btok[:T, :], cfg.cmT[:, c:c + 1], 1.0, ALU.mult, ALU.mult, [Btok, cfg.cmT], [(Btm, sl)], eng="pool")
        TT(CTm[:, sl, :, :T], fT[:, 10:12, :T], bm(cfg.cmf[:, c, :], 2), ALU.mult, [fT, cfg.cmf], [(CTm, sl)], eng="pool")
        for g in range(2):
            MM(yo[g][:T, :], CTm[:, sl, g, :T], stb_in[:, g * 512:(g + 1) * 512], [(CTm, sl), stb_in], [yo[g]], start=first_c, stop=last_c)
        sp_ = [pget(), pget()]
        for g in range(2):
            MM(sp_[g][:, :], Btm[:T, sl, g * 128:(g + 1) * 128], xdec[:T, g * 512:(g + 1) * 512], [(Btm, sl), xdec], [sp_[g]])
        TT(big[3][:, :].rearrange("p (h q) -> p h q", q=64), st_in[:, :].rearrange("p (h q) -> p h q", q=64),
           bl(dbc[:, c * 16:(c + 1) * 16], 64), ALU.mult, [st_in, dbc], [big[3]])
        for g in range(2):
            TT(st_out[:, g * 512:(g + 1) * 512], big[3][:, g * 512:(g + 1) * 512], sp_[g][:, :], ALU.add, [big[3], sp_[g]], [st_out])

    def phaseA_tile(cfg, l, ti, xt, first, is_last):
        T, L, nseq, nch = cfg.T, cfg.L, cfg.nseq, cfg.nch
        stf, stb, hn, stf_s, stb_s, Gs, earg, Ee, WT, Btok = (PA[k] for k in ("stf", "stb", "hn", "stf_s", "stb_s", "Gs", "earg", "Ee", "WT", "Btok"))
        prologue(cfg, xt)
        if limit.get("stopA", 99) <= 0:
            return
        inproj_conv(cfg, l, OFF_XBC, 12, cbrow, I["st_sc"][l], O["ncs_p"], O["ncs_s"], 0, is_last, first)
        if ti == 0:
            DBG("fT_A", fT, fT[:, :, :T], [128, 12, T])
        if limit.get("stopA", 99) <= 1:
            return
        b = pget()
        for k in range(8):
            MM(b[:T, 0:16], hT[:, k, :T], PA["WB"][:, k, OFF_DT:OFF_DT + 16], [hT, PA["WB"]], [b], start=(k == 0), stop=(k == 7))
        A_, B_, C_, D_, E_ = (sm[k] for k in "abcde")
        TT(A_[:T, 0:16], b[:T, 0:16], vec16[:T, 0, :], ALU.add, [b, vec16], [A_])
        ACT(A_[:T, 0:16], A_[:T, 0:16], AF.Exp, [A_], [A_])
        ACT(A_[:T, 16:32], A_[:T, 0:16], AF.Ln, [A_], [A_], bias=1.0)
        TT(A_[:T, 32:48], A_[:T, 16:32], vec16[:T, 1, :], ALU.mult, [A_, vec16], [A_])
        cum_and_tot(cfg, A_[:T, 32:48], A_, 16, B_, dbc, 128)
        TT(C_[:T, 0:16], B_[:T, 16:32], B_[:T, 0:16], ALU.subtract, [B_], [C_])
        ACT(C_[:T, 0:16], C_[:T, 0:16], AF.Exp, [C_], [C_])
        ACT(C_[:T, 16:32], B_[:T, 0:16], AF.Exp, [B_], [C_])
        b = pget()
        TR(b[:16, 0:T], B_[:T, 0:16], identf[:T, :T], [B_, identf], [b])
        ACT(acT[:, :T], b[:16, 0:T], AF.Copy, [b], [acT])
        if ti == 0:
            DBG("acum", B_, B_[:T, 0:32], [T, 32])
        if limit.get("stopA", 99) <= 2:
            return
        bx, bb = pget(), pget()
        for c in range(8):
            TR(pb(bx, [8, 128], p=T)[:, c, :], fT[:, c, :T], identb[:, :], [fT, identb], [bx])
        for c in range(2):
            TR(pb(bb, [2, 128], p=T)[:, c, :], fT[:, 8 + c, :T], identb[:, :], [fT, identb], [bb])
        xc, xdec, xsD = bigb[0], bigb[1], big[0]
        TT(xc[:T, :].rearrange("p (h q) -> p h q", q=64), pb(bx, [16, 64], p=T), bl(A_[:T, 16:32], 64), ALU.mult, [bx, A_], [xc])
        TT(xsD[:T, :].rearrange("p (h q) -> p h q", q=64), pb(bx, [16, 64], p=T), bl(vec16[:T, 2, :], 64), ALU.mult, [bx, vec16], [xsD])
        TT(xdec[:T, :].rearrange("p (h q) -> p h q", q=64), xc[:T, :].rearrange("p (h q) -> p h q", q=64), bl(C_[:T, 0:16], 64), ALU.mult,
           [xc, C_], [xdec], eng="pool")
        ACT(Btok[:T, :], pb(bb, [256], p=T), AF.Copy, [bb], [Btok])
        b = pget()
        for g in range(2):
            MM(pf(b, [2, T], p=T)[:, g, :], fT[:, 8 + g, :T], fT[:, 10 + g, :T], [fT], [b])
        ACT(Gs[:T, :, :T], pf(b, [2, T], p=T), AF.Copy, [b], [Gs])
        for q in range(4):
            b = pget()
            for j in range(4):
                h = q * 4 + j
                MM(pf(b, [4, T], p=T)[:, j, :], identf[0:16, h:h + 1].to_broadcast([16, T]), acT[:, :T], [identf, acT], [b])
            for j in range(4):
                h = q * 4 + j
                STT(earg[:T, j, :T], pf(b, [4, T], p=T)[:, j, :], B_[:T, h:h + 1], cfg.nmI[:, :], ALU.subtract, ALU.add,
                    [b, B_, cfg.nmI], [earg])
            ACT(Ee[:T, :, :T], earg[:T, :, :T], AF.Exp, [earg], [Ee])
            TT(WT[:T, q * 4:q * 4 + 4, :T], Ee[:T, :, :T], bm(Gs[:T, q // 2, :T], 4), ALU.mult, [Ee, Gs], [WT])
        if limit.get("stopA", 99) <= 3:
            return
        yp = [pget(), pget()]
        for h in range(16):
            MM(yp[h // 8][:T, (h % 8) * 64:(h % 8 + 1) * 64], WT[:T, h, :T], xc[:T, h * 64:(h + 1) * 64], [WT, xc], [yp[h // 8]])
        ppin(yp[0]); ppin(yp[1])
        yo = [pget(), pget()]
        ppin(yo[0]); ppin(yo[1])
        if cfg.chain:
            if first:
                MSET(stf[:, :], 0.0, [stf])
                MSET(stb[:, :], 0.0, [stb])
            for c in range(nch):
                ssd_state_update(cfg, c, stf, stb, stf, yo, xdec, c == 0, c == nch - 1)
                ACT(stb[:, :], stf[:, :], AF.Copy, [stf], [stb])
            if is_last:
                for half in range(2):
                    b = pget()
                    for k in range(4):
                        kk = half * 4 + k
                        TR(pf(b, [4, 128])[:, k, :], stf[:, :].rearrange("n (j k) -> n k j", k=8)[:, kk, :], identf[:, :], [stf, identf], [b])
                    CP_(hn[:, half * 4:half * 4 + 4, :], pf(b, [4, 128]), [b], [hn])
                STO(O["nssm_p"][l].rearrange("(p k) n -> p k n", k=8), hn[:], hn)
        else:
            for c in range(limit.get("nchs", nch)):
                LD(hn[:], I["st_ssm"][l, c].rearrange("h q n -> (h q) n").rearrange("(p k) n -> p k n", k=8), hn)
                if limit.get("sub", 9) <= 0:
                    continue
                for half in range(2):
                    b = pget()
                    for k in range(4):
                        kk = half * 4 + k
                        TR(pf(b, [4, 128])[:, k, :], hn[:, kk, :], identf[:, :], [hn, identf], [b])
                    if limit.get("sub", 9) <= 1:
                        continue
                    ACT(stf_s[:, :].rearrange("n (j k) -> n k j", k=8)[:, half * 4:half * 4 + 4, :], pf(b, [4, 128]), AF.Copy, [b], [stf_s])
                    if limit.get("sub", 9) <= 2:
                        continue
                    CP_(stb_s[:, :].rearrange("n (j k) -> n k j", k=8)[:, half * 4:half * 4 + 4, :], pf(b, [4, 128]), [b], [stb_s])
                if limit.get("noupd"):
                    continue
                ssd_state_update(cfg, c, stf_s, stb_s, stf_s, yo, xdec, c == 0, c == limit.get("nchs", nch) - 1)
                if limit.get("noback"):
                    continue
                for half in range(2):
                    b = pget()
                    for k in range(4):
                        kk = half * 4 + k
                        TR(pf(b, [4, 128])[:, k, :], stf_s[:, :].rearrange("n (j k) -> n k j", k=8)[:, kk, :], identf[:, :], [stf_s, identf], [b])
                    CP_(hn[:, half * 4:half * 4 + 4, :], pf(b, [4, 128]), [b], [hn])
                STO(O["nssm_s"][l, c].rearrange("(p k) n -> p k n", k=8), hn[:], hn)
        if limit.get("stopA", 99) <= 4:
            punpin(yp[0]); punpin(yp[1]); punpin(yo[0]); punpin(yo[1])
            return
        t1, t2 = big[1], big[2]
        for g in range(2):
            TT(t1[:T, g * 512:(g + 1) * 512].rearrange("p (h q) -> p h q", q=64), yo[g][:T, :].rearrange("p (h q) -> p h q", q=64),
               bl(C_[:T, 16 + g * 8:16 + g * 8 + 8], 64), ALU.mult, [yo[g], C_], [t1])
            TT(t2[:T, g * 512:(g + 1) * 512], yp[g][:T, :], t1[:T, g * 512:(g + 1) * 512], ALU.add, [yp[g], t1], [t2])
        for x_ in yp + yo:
            punpin(x_)
        TT(t2[:T, :], t2[:T, :], xsD[:T, :], ALU.add, [t2, xsD], [t2], eng="pool")
        if ti == 0:
            DBG("yssd", t2, t2[:T, :], [T, D])
        for g in range(2):
            b = pget()
            for k in range(8):
                MM(b[:T, :], hT[:, k, :T], PA["WB"][:, k, OFF_ZS + g * 512:OFF_ZS + (g + 1) * 512], [hT, PA["WB"]], [b], start=(k == 0), stop=(k == 7))
            ACT(t1[:T, g * 512:(g + 1) * 512], b[:T, :], AF.Silu, [b], [t1])
        TT(t2[:T, :], t2[:T, :], t1[:T, :], ALU.mult, [t2, t1], [t2])
        for g in range(2):
            ACT(junk[:T, g * 512:(g + 1) * 512], t2[:T, g * 512:(g + 1) * 512], AF.Square, [t2], [junk, sm1], accum_out=sm1[:T, 2 + g:3 + g])
        TS(sm1[:T, 2:4], sm1[:T, 2:4], 1.0 / 512, EPS, ALU.mult, ALU.add, [sm1], [sm1])
        ACT(sm1[:T, 2:4], sm1[:T, 2:4], AF.Ln, [sm1], [sm1])
        ACT(sm1[:T, 2:4], sm1[:T, 2:4], AF.Exp, [sm1], [sm1], scale=-0.5)
        ynb = bigb[2]
        for g in range(2):
            ACT(ynb[:T, g * 512:(g + 1) * 512], t2[:T, g * 512:(g + 1) * 512], AF.Copy, [t2, sm1], [ynb], scale=sm1[:T, 2 + g:3 + g])
        out_proj(cfg, ynb, 8)

    def out_proj(cfg, ynb, nk):
        T = cfg.T
        b = pget()
        for c in range(nk):
            TR(pb(b, [8, T])[:, c, :], ynb[:T, c * 128:(c + 1) * 128], identb[:T, :T], [ynb, identb], [b])
        CP_(hT[:, 0:nk, :T], pb(b, [8, T])[:, 0:nk, :], [b], [hT])
        for half in range(2):
            b = pget()
            for k in range(nk):
                MM(b[:T, :], hT[:, k, :T], WO[:, k, half * 512:(half + 1) * 512], [hT, WO], [b], start=(k == 0), stop=(k == nk - 1))
            TT(big[3][:T, half * 512:(half + 1) * 512], b[:T, :], gate_bc[cfg.name][:T, half * 512:(half + 1) * 512], ALU.mult,
               [b, gate_bc[cfg.name]], [big[3]])

    PB = {}

    def alloc_B(pes, tag):
        def sbp(name, shape, dt=F32):
            return Buf(pes.enter_context(nc.sbuf_tensor(name + tag, list(shape), dt)), name)
        PB.update(Sp_f=sbp("Sp_f", [128, 4, 128]), Sp_b=sbp("Sp_b", [128, 4, 128], BF16), Ss_in=sbp("Ss_in", [128, 16, 128]),
                  Ss_b=sbp("Ss_b", [128, 16, 128], BF16), kv=sbp("kv", [128, 8, 128], BF16), kbg=sbp("kbg", [128, 4, 128]),
                  vb_=sbp("vb", [128, 4, 128]), kdec=sbp("kdec", [128, 4, 128], BF16), kdm=sbp("kdm", [64, 2, 128], BF16),
                  LG=sbp("LG", [128, 12]), LGT=sbp("LGT", [12, 128]), ea=sbp("ea", [128, 4, 128]),
                  E2a=sbp("E2a", [128, 4, 128]), E2b=sbp("E2b", [128, 4, 128]),
                  X0=sbp("X0", [128, 4, 128]), X1=sbp("X1", [128, 4, 128]), Y0=sbp("Y0", [128, 4, 128]), Y1=sbp("Y1", [128, 4, 128]),
                  R0=sbp("R0", [128, 4, 128]), R1=sbp("R1", [128, 4, 128]), attnT=sbp("attnT", [128, 4, 128], BF16),
                  egb=sbp("egb", [128, 4, 128]), qsq=sbp("qsq", [128, 4, 128], BF16),
                  nwTm=sbp("nwTm", [128, 1024], BF16), qdm=sbp("qdm", [128, 1024], BF16), u_sb=sbp("u_sb", [128, 4, 128]),
                  vnew=sbp("vnew", [128, 4, 128], BF16), dbcS=sbp("dbcS", [128, 64]), WB=sbp("WB", [128, 8, 2056], BF16))
        PB["E2"] = None

    def r32(ap):
        return ap.bitcast(F32R)

    def phaseB_tile(cfg, l, ti, xt, h0, first, is_last):
        T, L, nseq, nch = cfg.T, cfg.L, cfg.nseq, cfg.nch
        (Sp_f, Sp_b, Ss_in, Ss_b, kv, kbg, vb_, kdec, kdm, LG, LGT, ea, attnT, egb, nwTm, qdm, u_sb, vnew, dbcS) = (
            PB[k] for k in ("Sp_f", "Sp_b", "Ss_in", "Ss_b", "kv", "kbg", "vb_", "kdec", "kdm", "LG", "LGT", "ea", "attnT",
                            "egb", "nwTm", "qdm", "u_sb", "vnew", "dbcS"))
        E2 = [PB["E2a"], PB["E2b"]]
        Ss_out = Ss_in
        Xs, Ys, Rs = [PB["X0"], PB["X1"]], [PB["Y0"], PB["Y1"]], [PB["R0"], PB["R1"]]
        WB = PB["WB"]
        prologue(cfg, xt)
        inproj_conv_B(cfg, l, h0, is_last, first)
        A_, B_, C_, D_, E_ = (sm[k] for k in "abcde")
        b = pget()
        for k in range(8):
            MM(b[:T, 0:8], hT[:, k, :T], WB[:, k, 2048:2056], [hT, WB], [b], start=(k == 0), stop=(k == 7))
        TT(A_[:T, 0:4], b[:T, 0:4], vec8[:T, 0, h0:h0 + 4], ALU.add, [b, vec8], [A_])
        ACT(A_[:T, 0:4], A_[:T, 0:4], AF.Exp, [A_], [A_])
        ACT(A_[:T, 0:4], A_[:T, 0:4], AF.Ln, [A_], [A_], bias=1.0)
        TT(A_[:T, 4:8], A_[:T, 0:4], vec8[:T, 1, h0:h0 + 4], ALU.mult, [A_, vec8], [A_])
        ACT(A_[:T, 8:12], b[:T, 4:8], AF.Exp, [b], [A_], scale=-1.0)
        ACT(A_[:T, 8:12], A_[:T, 8:12], AF.Ln, [A_], [A_], bias=1.0)
        ACT(A_[:T, 12:16], A_[:T, 8:12], AF.Exp, [A_], [A_], scale=-1.0)
        ACT(A_[:T, 16:20], A_[:T, 8:12], AF.Copy, [A_], [A_], scale=-1.0)
        cum_and_tot(cfg, A_[:T, 4:8], A_, 4, B_, dbcS, 128)
        ACT(C_[:T, 0:4], B_[:T, 0:4], AF.Exp, [B_], [C_])
        TT(C_[:T, 4:8], B_[:T, 4:8], B_[:T, 0:4], ALU.subtract, [B_], [C_])
        ACT(C_[:T, 4:8], C_[:T, 4:8], AF.Exp, [C_], [C_])
        b = pget()
        for c in range(8):
            TR(pb(b, [8, 128], p=T)[:, c, :], fT[:, 4 + c, :T], identb[:, :], [fT, identb], [b])
        ACT(kv[:T, :, :], pb(b, [8, 128], p=T), AF.Copy, [b], [kv])
        ksq = big[0]
        TT(ksq[:T, 0:512].rearrange("p (h d) -> p h d", d=128), kv[:T, 0:4, :], kv[:T, 0:4, :], ALU.mult, [kv], [ksq])
        RED(D_[:T, 0:4], ksq[:T, 0:512].rearrange("p (h d) -> p h d", d=128), [ksq], [D_])
        TS(D_[:T, 0:4], D_[:T, 0:4], EPS, None, ALU.add, ALU.bypass, [D_], [D_])
        ACT(D_[:T, 4:8], D_[:T, 0:4], AF.Ln, [D_], [D_])
        ACT(D_[:T, 8:12], D_[:T, 4:8], AF.Exp, [D_], [D_], scale=-0.5)
        qsq = PB["qsq"]
        ACT(qsq[:, :, :T], fT[:, 0:4, :T], AF.Square, [fT], [qsq])
        b = pget()
        for j in range(4):
            MM(b[:T, j:j + 1], qsq[:, j, :T], onesb[:, 0:1], [qsq, onesb], [b])
        TS(D_[:T, 12:16], b[:T, 0:4], EPS, None, ALU.add, ALU.bypass, [b], [D_])
        ACT(D_[:T, 12:16], D_[:T, 12:16], AF.Ln, [D_], [D_])
        ACT(D_[:T, 16:20], D_[:T, 12:16], AF.Exp, [D_], [D_], scale=-0.5)
        TS(D_[:T, 16:20], D_[:T, 16:20], float(128 ** -0.5), None, ALU.mult, ALU.bypass, [D_], [D_])
        TT(E_[:T, 0:4], D_[:T, 8:12], A_[:T, 12:16], ALU.mult, [D_, A_], [E_])
        TT(E_[:T, 4:8], E_[:T, 0:4], C_[:T, 0:4], ALU.mult, [E_, C_], [E_])
        TT(E_[:T, 8:12], D_[:T, 8:12], C_[:T, 4:8], ALU.mult, [D_, C_], [E_])
        TS(E_[:T, 12:16], E_[:T, 0:4], -1.0, None, ALU.mult, ALU.bypass, [E_], [E_])
        TS(E_[:T, 16:20], D_[:T, 8:12], -1.0, None, ALU.mult, ALU.bypass, [D_], [E_])
        CP_(LG[:T, 0:4], B_[:T, 0:4], [B_], [LG])
        STT(LG[:T, 4:8], D_[:T, 4:8], -0.5, B_[:T, 0:4], ALU.mult, ALU.subtract, [D_, B_], [LG])
        STT(LG[:T, 8:12], D_[:T, 4:8], -0.5, B_[:T, 0:4], ALU.mult, ALU.add, [D_, B_], [LG])
        TT(LG[:T, 8:12], LG[:T, 8:12], A_[:T, 16:20], ALU.add, [LG, A_], [LG])
        b = pget()
        TR(b[:12, 0:T], LG[:T, 0:12], identf[:T, :T], [LG, identf], [b])
        ACT(LGT[:, :T], b[:12, 0:T], AF.Copy, [b], [LGT])
        TT(r32(kbg[:T, :, :]), kv[:T, 0:4, :], bl(E_[:T, 4:8], 128), ALU.mult, [kv, E_], [kbg])
        TT(kdec[:T, :, :], kv[:T, 0:4, :], bl(E_[:T, 8:12], 128), ALU.mult, [kv, E_], [kdec])
        TT(r32(vb_[:T, :, :]), kv[:T, 4:8, :], bl(A_[:T, 12:16], 128), ALU.mult, [kv, A_], [vb_])
        bG, bA = pget(), pget()
        for j in range(4):
            MM(pf(bG, [4, T], p=T)[:, j, :], fT[:, 4 + j, :T], fT[:, 4 + j, :T], [fT], [bG])
        for j in range(4):
            MM(pf(bA, [4, T], p=T)[:, j, :], fT[:, 4 + j, :T], fT[:, j, :T], [fT], [bA])
        bcs = [pget(), pget(), pget()]
        for r in range(3):
            for j in range(4):
                MM(pf(bcs[r], [4, T], p=T)[:, j, :], identf[0:12, r * 4 + j:r * 4 + j + 1].to_broadcast([12, T]), LGT[:, :T],
                   [identf, LGT], [bcs[r]])
        be = pget()
        for j in range(4):
            MM(pf(be, [4, T])[:, j, :], identf[0:12, j:j + 1].to_broadcast([12, 128]), LGT[:, :T], [identf, LGT], [be])
        X, Y, R = Xs[0], Ys[0], Rs[0]
        kinds = ((1, ALU.add, cfg.nmSr, 0), (2, ALU.subtract, cfg.nmS, 1), (0, ALU.subtract, cfg.nmI, 2))
        for r, op0, msk, ki in kinds:
            for j in range(4):
                STT(ea[:T, j, :T], pf(bcs[r], [4, T], p=T)[:, j, :], B_[:T, j:j + 1], msk[:, :], op0, ALU.add, [bcs[r], B_, msk], [ea])
            Ek = E2[ki % 2]
            ACT(Ek[:T, :, :T], ea[:T, :, :T], AF.Exp, [ea], [Ek])
            for j in range(4):
                if ki == 0:
                    STT(r32(X[:T, j, :T]), pf(bG, [4, T], p=T)[:, j, :], E_[:T, 12 + j:13 + j], Ek[:T, j, :T], ALU.mult, ALU.mult, [bG, E_, Ek], [X])
                elif ki == 1:
                    STT(r32(Y[:T, j, :T]), pf(bG, [4, T], p=T)[:, j, :], E_[:T, 16 + j:17 + j], Ek[:T, j, :T], ALU.mult, ALU.mult, [bG, E_, Ek], [Y])
                else:
                    STT(attnT[:T, j, :T], pf(bA, [4, T], p=T)[:, j, :], D_[:T, 8 + j:9 + j], Ek[:T, j, :T], ALU.mult, ALU.mult, [bA, D_, Ek], [attnT])
        ACT(egb[:, :, :T], pf(be, [4, T]), AF.Exp, [be], [egb])
        TT(r32(R[:T, :, :T]), Y[:T, :, :T], bm(identf[:T, :T], 4), ALU.add, [Y, identf], [R])
        for lev in range(cfg.nlev):
            X2, Y2, R2 = Xs[(lev + 1) % 2], Ys[(lev + 1) % 2], Rs[(lev + 1) % 2]
            lastlev = (lev == cfg.nlev - 1)
            bX, bY, bR = pget(), (None if lastlev else pget()), pget()
            for j in range(4):
                MM(pf(bX, [4, T], p=T)[:, j, :], r32(Y[:T, j, :T]), r32(X[:T, j, :T]), [X, Y], [bX])
            if not lastlev:
                for j in range(4):
                    MM(pf(bY, [4, T], p=T)[:, j, :], r32(X[:T, j, :T]), r32(Y[:T, j, :T]), [X, Y], [bY])
            ACT(r32(X2[:T, :, :T]), pf(bX, [4, T], p=T), AF.Copy, [bX], [X2])
            if not lastlev:
                CP_(r32(Y2[:T, :, :T]), pf(bY, [4, T], p=T), [bY], [Y2])
            for j in range(4):
                MM(pf(bR, [4, T], p=T)[:, j, :], r32(X2[:T, j, :T]), r32(R[:T, j, :T]), [X2, R], [bR])
            TT(r32(R2[:T, :, :T]), R[:T, :, :T], pf(bR, [4, T], p=T), ALU.add, [R, bR], [R2])
            X, Y, R = X2, Y2, R2
        bU, bW = pget(), pget()
        for j in range(4):
            MM(pf(bU, [4, 128], p=T)[:, j, :], r32(R[:T, j, :T]), r32(vb_[:T, j, :]), [R, vb_], [bU])
        for j in range(4):
            MM(pf(bW, [4, T])[:, j, :], r32(kbg[:T, j, :]), r32(R[:T, j, :T]), [kbg, R], [bW])
        ACT(u_sb[:T, :, :], pf(bU, [4, 128], p=T), AF.Copy, [bU], [u_sb])
        TT(egb[:, :, :T], fT[:, 0:4, :T], egb[:, :, :T], ALU.mult, [fT, egb], [egb])
        vnb, ob = pget(), pget()
        ppin(vnb)
        ppin(ob)
        vn4, o4 = pf(vnb, [4, 128], p=T), pf(ob, [4, 128], p=T)
        if cfg.chain:
            if first:
                MSET(Sp_f[:], 0.0, [Sp_f])
                MSET(Sp_b[:], 0.0, [Sp_b])
            nwv = nwTm[:, 0:nch * 4 * T].rearrange("p (c j t) -> p c j t", c=nch, j=4)
            qdv = qdm[:, 0:nch * 4 * T].rearrange("p (c j t) -> p c j t", c=nch, j=4)
            for c in range(nch):
                TT(nwv[:, c], pf(bW, [4, T]), bm(cfg.ncmf[:, c, :], 4), ALU.mult, [bW, cfg.ncmf], [(nwTm, c)])
                TT(qdv[:, c], egb[:, :, :T], bm(cfg.cmf[:, c, :], 4), ALU.mult, [egb, cfg.cmf], [(qdm, c)], eng="pool")
            for c in range(nch):
                for j in range(4):
                    MM(vn4[:, j, :], nwv[:, c, j, :], Sp_b[:, j, :], [(nwTm, c), Sp_b], [vnb], start=(j == 0 and c == 0), stop=False, skip_group_check=True)
                for j in range(4):
                    MM(o4[:, j, :], qdv[:, c, j, :], Sp_b[:, j, :], [(qdm, c), Sp_b], [ob], start=(j == 0 and c == 0), stop=False, skip_group_check=True)
                TT(vnew[:T, :, :], vn4, u_sb[:T, :, :], ALU.add, [vnb, u_sb], [vnew])
                bs = pget()
                for j in range(4):
                    MM(pf(bs, [4, 128])[:, j, :], kdec[c * 64:(c + 1) * 64, j, :], vnew[c * 64:(c + 1) * 64, j, :], [kdec, vnew], [bs])
                tmpS = big[0][:, 512:1024].rearrange("p (j v) -> p j v", v=128)
                TT(tmpS, Sp_f[:, :, :], bl(dbcS[:, c * 4:(c + 1) * 4], 128), ALU.mult, [Sp_f, dbcS], [big[0]])
                TT(Sp_f[:, :, :], tmpS, pf(bs, [4, 128]), ALU.add, [big[0], bs], [Sp_f])
                ACT(Sp_b[:, :, :], Sp_f[:, :, :], AF.Copy, [Sp_f], [Sp_b])
            if is_last:
                STO(O["ngdn_p"][l, h0:h0 + 4].rearrange("h k v -> k h v"), Sp_f[:], Sp_f)
        else:
            nwv = nwTm[:, 0:nch * T].rearrange("p (c t) -> p c t", t=T)
            qdv = qdm[:, 0:nch * T].rearrange("p (c t) -> p c t", t=T)
            ppin(bW)
            for j in range(4):
                h = h0 + j
                LD(Ss_in[:], I["st_gdn"][l, :, h].rearrange("b k v -> k b v"), Ss_in)
                CP_(Ss_b[:], Ss_in[:], [Ss_in], [Ss_b], eng="pool")
                TT(nwv, bm(pf(bW, [4, T])[:, j, :], nch), cfg.ncmf[:, :, :], ALU.mult, [bW, cfg.ncmf], [nwTm])
                TT(qdv, bm(egb[:, j, :T], nch), cfg.cmf[:, :, :], ALU.mult, [egb, cfg.cmf], [qdm], eng="pool")
                for c in range(nch):
                    MM(vn4[:, j, :], nwv[:, c, :], Ss_b[:, c, :], [nwTm, Ss_b], [vnb], start=(j == 0 and c == 0), stop=False, skip_group_check=True)
                for c in range(nch):
                    MM(o4[:, j, :], qdv[:, c, :], Ss_b[:, c, :], [qdm, Ss_b], [ob], start=(j == 0 and c == 0), stop=False, skip_group_check=True)
                TT(vnew[:T, j, :], vn4[:, j, :], u_sb[:T, j, :], ALU.add, [vnb, u_sb], [vnew])
                tmpS = big[2][:, :].rearrange("p (c v) -> p c v", v=128)
                for hf in range(2):
                    TT(tmpS, Ss_in[:, hf * 8:(hf + 1) * 8, :],
                       bl(dbcS[:, 0:nch * 4].rearrange("p (c j) -> p c j", j=4)[:, hf * 8:(hf + 1) * 8, j], 128), ALU.mult, [Ss_in, dbcS], [big[2]])
                    for c4 in range(hf * 8, hf * 8 + 8, 4):
                        bs = pget()
                        for cc in range(4):
                            c = c4 + cc
                            sl = mslot[0] % 2
                            mslot[0] += 1
                            TS(kdm[:T, sl, :], kdec[:T, j, :], cfg.cmT[:, c:c + 1], 1.0, ALU.mult, ALU.mult, [kdec, cfg.cmT], [(kdm, sl)], eng="pool")
                            MM(pf(bs, [4, 128])[:, cc, :], kdm[:T, sl, :], vnew[:T, j, :], [(kdm, sl), vnew], [bs])
                        TT(Ss_out[:, c4:c4 + 4, :], tmpS[:, c4 - hf * 8:c4 - hf * 8 + 4, :], pf(bs, [4, 128]), ALU.add, [big[2], bs], [Ss_out])
                STO(O["ngdn_s"][l, :, h].rearrange("b k v -> k b v"), Ss_out[:], Ss_out)
            punpin(bW)
        for j in range(4):
            MM(o4[:, j, :], attnT[:T, j, :T], vnew[:T, j, :], [attnT, vnew], [ob], start=False, stop=True, skip_group_check=True)
        punpin(vnb)
        of = big[1]
        ACT(of[:T, 0:512], ob[:T, :], AF.Copy, [ob], [of])
        punpin(ob)
        if ti == 0 and h0 == 0:
            DBG("ogdn", of, of[:T, 0:512], [T, 512])
        TT(ksq[:T, 0:512], of[:T, 0:512], of[:T, 0:512], ALU.mult, [of], [ksq])
        RED(C_[:T, 8:12], ksq[:T, 0:512].rearrange("p (h d) -> p h d", d=128), [ksq], [C_])
        TT(C_[:T, 12:16], D_[:T, 16:20], D_[:T, 16:20], ALU.mult, [D_], [C_])
        TT(C_[:T, 8:12], C_[:T, 8:12], C_[:T, 12:16], ALU.mult, [C_], [C_])
        TS(C_[:T, 8:12], C_[:T, 8:12], 1.0 / 128, EPS, ALU.mult, ALU.add, [C_], [C_])
        ACT(C_[:T, 8:12], C_[:T, 8:12], AF.Ln, [C_], [C_])
        ACT(C_[:T, 8:12], C_[:T, 8:12], AF.Exp, [C_], [C_], scale=-0.5)
        TT(C_[:T, 8:12], C_[:T, 8:12], D_[:T, 16:20], ALU.mult, [C_, D_], [C_])
        b = pget()
        for k in range(8):
            MM(b[:T, :], hT[:, k, :T], WB[:, k, 0:512], [hT, WB], [b], start=(k == 0), stop=(k == 7))
        sz = big[2]
        ACT(sz[:T, 0:512], b[:T, :], AF.Silu, [b], [sz])
        TT(of[:T, 0:512].rearrange("p (h d) -> p h d", d=128), of[:T, 0:512].rearrange("p (h d) -> p h d", d=128), bl(C_[:T, 8:12], 128),
           ALU.mult, [of, C_], [of])
        onb = bigb[2]
        TT(onb[:T, 0:512], of[:T, 0:512], sz[:T, 0:512], ALU.mult, [of, sz], [onb])
        out_proj(cfg, onb, 4)

    def inproj_conv_B(cfg, l, h0, is_last, first):
        T, L, nseq = cfg.T, cfg.L, cfg.nseq
        rawv = raw[:, :, 0:nseq * (L + 3)].rearrange("p c (b l) -> p c b l", l=L + 3)
        if nseq == 1:
            if first:
                MSET(raw[:, :, 0:3], 0.0, [raw])
            else:
                CP_(raw[:, :, 0:3], raw[:, :, L:L + 3], [raw], [raw])
        else:
            for part in range(3):
                cs0 = part * 1024 + h0 * 128
                LD(cst_in[:, part * 512:(part + 1) * 512], I["st_gc"][l][:, :, cs0:cs0 + 512].rearrange("b k c -> (b k) c"), cst_in)
            for g0 in (0, 8):
                b = pget()
                ng = min(8, 12 - g0)
                for c in range(ng):
                    TR(pf(b, [8, 48])[:, c, :], cst_in[:, (g0 + c) * 128:(g0 + c + 1) * 128], identf[:48, :48], [cst_in, identf], [b])
                CP_(rawv[:, g0:g0 + ng, :, 0:3], pf(b, [8, 48])[:, 0:ng, :].rearrange("p c (b k) -> p c b k", k=3), [b], [raw])
        for g0 in range(0, 12, 4):
            b = pget()
            for c in range(4):
                for k in range(8):
                    MM(pf(b, [4, T])[:, c, :], PB["WB"][:, k, 512 + (g0 + c) * 128: 512 + (g0 + c + 1) * 128], hT[:, k, :T],
                       [PB["WB"], hT], [b], start=(k == 0), stop=(k == 7))
            if nseq == 1:
                ACT(raw[:, g0:g0 + 4, 3:3 + L], pf(b, [4, T]), AF.Copy, [b], [raw])
            else:
                for c in range(4):
                    ACT(rawv[:, g0 + c, :, 3:3 + L], pf(b, [4, T])[:, c, :].rearrange("p (b l) -> p b l", l=L), AF.Copy, [b], [raw])
            if is_last and limit.get("tail", True):
                for c in range(4):
                    ACT(tailf[:, g0 + c, 0:nseq * 3].rearrange("p (b k) -> p b k", k=3),
                        pf(b, [4, T])[:, c, :].rearrange("p (b l) -> p b l", l=L)[:, :, L - 3:L], AF.Copy, [b], [tailf])
        for g0 in range(0, 12, 4):
            b = pget()
            for c in range(4):
                o = pf(b, [4, T])[:, c, :]
                if nseq > 1:
                    o = o.rearrange("p (b l) -> p b l", l=L)
                for k in range(4):
                    MM(o, DIAG[:, g0 + c, k, :], rawv[:, g0 + c, :, k:k + L] if nseq > 1 else raw[:, g0 + c, k:k + L],
                       [DIAG, raw], [b], start=(k == 0), stop=(k == 3))
            ACT(fT[:, g0:g0 + 4, :T], pf(b, [4, T]), AF.Silu, [b], [fT])
        if is_last:
            n3 = nseq * 3
            for g0 in range(0, 12, 4):
                b = pget()
                for c in range(4):
                    TR(pf(b, [4, 128], p=n3)[:, c, :], tailf[:, g0 + c, 0:n3], identf[:, :], [tailf, identf], [b])
                CP_(tailo[:n3, g0 * 128:(g0 + 4) * 128], b[:n3, :], [b], [tailo])
            dst = O["ngc_p"] if nseq == 1 else O["ngc_s"]
            for part in range(3):
                cs0 = part * 1024 + h0 * 128
                STO(dst[l][:, cs0:cs0 + 512], tailo[:n3, part * 512:(part + 1) * 512], tailo)

    def bvec(dst_ap, src_row_ap, buf):
        LD(dst_ap, src_row_ap.partition_broadcast(128), buf)

    tiles = [(CP, ti) for ti in range(limit.get("ntiles", NT))] + ([(CS, NT)] if limit.get("sample", True) else [])
    all_bufs.extend(allb)
    layer_in = None
    for l in range(limit.get("layers", DEPTH)):
        do_mod(l)
        S.barrier(all_bufs)
        mod_stacks.pop().close()
        for ph in limit.get("phases", (0, 1, 2)):
            acc_in = layer_in if ph == 0 else scr[(ph - 1)]
            last_phase = (l == DEPTH - 1 and ph == 2)
            acc_out = scr[ph] if ph < 2 else (scr[2] if not last_phase else None)
            S.barrier(all_bufs)
            pes = contextlib.ExitStack()
            cur_es[0] = pes
            if ph == 0:
                alloc_A(pes, "_%d_%d" % (l, ph))
            else:
                alloc_B(pes, "_%d_%d" % (l, ph))
            for v_ in list(PA.values()) + list(PB.values()):
                if v_ is not None and v_ not in all_bufs:
                    all_bufs.append(v_)
            if ph == 0:
                load_cols(PA["WB"], 0, I["w_in"][l][:, 0:2576], 2576)
                LD(nwT[:], I["ssd_norm_w"][l].rearrange("(k p) -> p k", p=128), nwT, nonc=True)
                load_wo(I["w_out"][l][0:1024, :], 8, lambda k: nwT[:, k:k + 1])
                build_diag(I["ssd_conv_w"][l], 0, 12)
                LD(cbrow_f[0:1, :], I["ssd_conv_b"][l:l + 1, :], cbrow_f)
                CP_(cbrow[:], cbrow_f[0:1, :], [cbrow_f], [cbrow])
                bvec(vec16[:, 0, :], I["ssd_dt_bias"][l:l + 1, :], vec16)
                bvec(vec16[:, 1, :], I["ssd_a_log"][l:l + 1, :], vec16)
                bvec(vec16[:, 2, :], I["ssd_d"][l:l + 1, :], vec16)
                ACT(vec16[:, 1, :], vec16[:, 1, :], AF.Exp, [vec16], [vec16])
                TS(vec16[:, 1, :], vec16[:, 1, :], -1.0, None, ALU.mult, ALU.bypass, [vec16], [vec16])
            else:
                h0 = (ph - 1) * 4
                load_cols(PB["WB"], 0, I["w_in"][l][:, OFF_ZG + h0 * 128:OFF_ZG + h0 * 128 + 512], 512)
                for part in range(3):
                    c0 = OFF_QKV + part * 1024 + h0 * 128
                    load_cols(PB["WB"], 512 + part * 512, I["w_in"][l][:, c0:c0 + 512], 512)
                load_cols(PB["WB"], 2048, I["w_in"][l][:, OFF_A + h0:OFF_A + h0 + 4], 4)
                load_cols(PB["WB"], 2052, I["w_in"][l][:, OFF_B + h0:OFF_B + h0 + 4], 4)
                LD(nwT[:, 0:1], I["gdn_norm_w"][l].rearrange("(p o) -> p o", o=1), nwT, nonc=True)
                load_wo(I["w_out"][l][1024 + h0 * 128:1024 + h0 * 128 + 512, :], 4, lambda k: nwT[:, 0:1])
                for part in range(3):
                    for k_ in range(4):
                        LD(cwT[:, part * 4:part * 4 + 4, k_],
                           I["gdn_conv_w"][l][k_, part * 1024 + h0 * 128:part * 1024 + h0 * 128 + 512].rearrange("(c p) -> p c", p=128), cwT, nonc=True)
                for c in range(12):
                    TT(DIAG[:, c, :, :], bm(identf[:], 4), bl(cwT[:, c, :], 128), ALU.mult, [identf, cwT], [DIAG], eng="pool")
                bvec(vec8[:, 0, :], I["gdn_dt_bias"][l:l + 1, :], vec8)
                bvec(vec8[:, 1, :], I["gdn_a_log"][l:l + 1, :], vec8)
                ACT(vec8[:, 1, :], vec8[:, 1, :], AF.Exp, [vec8], [vec8])
                TS(vec8[:, 1, :], vec8[:, 1, :], -1.0, None, ALU.mult, ALU.bypass, [vec8], [vec8])

            def issue_loads(idx):
                cfg, ti = tiles[idx]
                r0, T = rows(ti)
                xt = xt_slots[idx % 2]
                ap, dep = src_ap(layer_in.t if layer_in is not None else None, ti)
                LD(xt[:T, :], ap, xt, R=[(layer_in, ti)] if layer_in is not None else None)
                if ph > 0:
                    xa = xa_slots[idx % 2]
                    LD(xa[:T, :], acc_in.t[r0:r0 + T, :], xa, R=[(acc_in, ti)])

            if last_phase:
                fnw_bc = stage[0]
                fnw_v = stage[0][:, :, :].rearrange("p a b -> p (a b)")
                LD(fnw_v, I["final_norm_w"].partition_broadcast(128), stage[0])
            issue_loads(0)
            for idx, (cfg, ti) in enumerate(tiles):
                if idx + 1 < len(tiles):
                    issue_loads(idx + 1)
                r0, T = rows(ti)
                xt = xt_slots[idx % 2]
                xa = xa_slots[idx % 2] if ph > 0 else xt
                first = (ti == 0)
                is_last = (ti == NT - 1) or (ti == NT)
                if ph == 0:
                    phaseA_tile(cfg, l, ti, xt, first, is_last)
                else:
                    phaseB_tile(cfg, l, ti, xt, (ph - 1) * 4, first, is_last)
                TT(xo[:T, :], big[3][:T, :], xa[:T, :], ALU.add, [big[3], xa], [xo], eng="pool")
                if not last_phase:
                    STO(acc_out.t[r0:r0 + T, :], xo[:T, :], xo, W=[(acc_out, ti)])
                else:
                    ACT(junk[:T, :], xo[:T, :], AF.Square, [xo], [junk, sm1], accum_out=sm1[:T, 4:5])
                    TS(sm1[:T, 4:5], sm1[:T, 4:5], 1.0 / D, EPS, ALU.mult, ALU.add, [sm1], [sm1])
                    ACT(sm1[:T, 4:5], sm1[:T, 4:5], AF.Ln, [sm1], [sm1])
                    ACT(sm1[:T, 4:5], sm1[:T, 4:5], AF.Exp, [sm1], [sm1], scale=-0.5)
                    STT(big[0][:T, :], xo[:T, :], sm1[:T, 4:5], fnw_v[:T, :], ALU.mult, ALU.mult, [xo, sm1, fnw_bc], [big[0]])
                    dsto = O["y_p"][r0:r0 + T, :] if ti < NT else O["y_s"][:, :]
                    STO(dsto, big[0][:T, :], big[0])
            pes.close()
            cur_es[0] = es
            PA.clear()
            PB.clear()
        layer_in = scr[2]
    if FILL_TABLE and not limit.get("nofill"):
        fill_rhs = CS.cmf[:, :, :].rearrange("p c t -> p (c t)")[:, 0:512]
        S.filler = lambda e: e.matmul(banks[7][:, :], lhsT=identb[:, :], rhs=fill_rhs, start=True, stop=True)
        S.fill_table = {int(kv.split(":")[0]): int(kv.split(":")[1]) for kv in FILL_TABLE.split(",") if kv}
    S.emit()
    build.pe_groups = S.pe_groups
    es.close()
    return nc, dbg_out


_CACHE = {}


def make_in_maps(inputs):
    g = {k: np.ascontiguousarray(np.asarray(v, dtype=np.float32)) for k, v in inputs.items()}
    consts = host_consts()
    maps = []
    for c in range(8):
        m = {}
        m["xp"] = g["x_prompt"][c]
        m["xs"] = g["x_sample"][c * NSB:(c + 1) * NSB].reshape(NSB * SL, D)
        m["call"] = np.concatenate([g["c_prompt"][c:c + 1], g["c_sample"][c * NSB:(c + 1) * NSB]], axis=0)
        m["st_sc"] = g["state_ssd_conv"][:, c * NSB:(c + 1) * NSB]
        m["st_ssm"] = g["state_ssm"][:, c * NSB:(c + 1) * NSB]
        m["st_gc"] = g["state_gdn_conv"][:, c * NSB:(c + 1) * NSB]
        m["st_gdn"] = g["state_gdn"][:, c * NSB:(c + 1) * NSB]
        for k in ("norm_w", "w_ada", "b_ada", "w_in", "ssd_conv_w", "ssd_conv_b", "ssd_dt_bias", "ssd_a_log", "ssd_d",
                  "ssd_norm_w", "gdn_conv_w", "gdn_dt_bias", "gdn_a_log", "gdn_norm_w", "w_out"):
            m[k] = g[k]
        m["final_norm_w"] = g["final_norm_w"].reshape(1, D)
        for k, v in consts.items():
            m["c_" + k] = v
        maps.append({k: np.ascontiguousarray(v) for k, v in m.items()})
    return maps


def kernel(**inputs):
    if "nc" not in _CACHE:
        _CACHE["nc"] = build()[0]
    nc = _CACHE["nc"]
    maps = make_in_maps(inputs)
    res = run_bass_kernel_spmd(nc, maps, core_ids=list(range(8)))
    R = res.results
    cat = lambda k, ax: np.concatenate([np.asarray(r[k]) for r in R], axis=ax)
    y_p = np.stack([np.asarray(r["y_p"]) for r in R], 0)
    y_s = np.stack([np.asarray(r["y_s"]).reshape(NSB, SL, D) for r in R], 0).reshape(8 * NSB, SL, D)
    ncs_p = np.stack([np.asarray(r["ncs_p"]) for r in R], 1)
    nssm_p = np.stack([np.asarray(r["nssm_p"]).reshape(DEPTH, 16, 64, 128) for r in R], 1)
    ngc_p = np.stack([np.asarray(r["ngc_p"]) for r in R], 1)
    ngdn_p = np.stack([np.asarray(r["ngdn_p"]) for r in R], 1)
    ncs_s = np.concatenate([np.asarray(r["ncs_s"]).reshape(DEPTH, NSB, 3, 1536) for r in R], 1)
    nssm_s = np.concatenate([np.asarray(r["nssm_s"]).reshape(DEPTH, NSB, 16, 64, 128) for r in R], 1)
    ngc_s = np.concatenate([np.asarray(r["ngc_s"]).reshape(DEPTH, NSB, 3, 3072) for r in R], 1)
    ngdn_s = np.concatenate([np.asarray(r["ngdn_s"]) for r in R], 1)
    outs = (y_p, y_s, ncs_p, nssm_p, ngc_p, ngdn_p, ncs_s, nssm_s, ngc_s, ngdn_s)
    return tuple(np.ascontiguousarray(o, dtype=np.float32) for o in outs)
```

```python
import contextlib
import numpy as np
import concourse.bass as bass
import concourse.mybir as mybir
from concourse.bass_utils import run_bass_kernel_spmd

F32 = mybir.dt.float32
F32R = mybir.dt.float32r
BF16 = mybir.dt.bfloat16
AF = mybir.ActivationFunctionType
ALU = mybir.AluOpType
AX = mybir.AxisListType

ENGS = ("pe", "dve", "act", "pool", "sp")


class Buf:
    def __init__(self, t, name):
        self.t = t
        self.name = name
        self.st = {}
        self.dma_sem = None
        self.dma_cnt = 0

    def __getitem__(self, k):
        return self.t[k]


class Instr:
    __slots__ = ("eng", "fn", "deps", "signal", "count", "is_dma", "buf_sem", "dma_count")

    def __init__(self, eng, fn):
        self.eng = eng
        self.fn = fn
        self.deps = []
        self.signal = False
        self.count = None
        self.is_dma = False
        self.buf_sem = None
        self.dma_count = None


class Sched:
    def __init__(self, nc):
        self.nc = nc
        self.streams = {e: [] for e in ENGS}
        self.all = []
        self.filler = None
        self.nfill = 0
        self.fill_table = {}
        self.pe_groups = 0

    def _states(self, buf, key):
        st = buf.st
        if key is None:
            if None not in st:
                st[None] = [None, {}]
            return list(st.values())
        if key not in st:
            if None in st:
                st[key] = [st[None][0], dict(st[None][1])]
            else:
                st[key] = [None, {}]
        res = [st[key]]
        if None in st:
            res.append(st[None])
        return res

    @staticmethod
    def _norm(lst):
        out = []
        for x in lst or []:
            out.append((x, None) if isinstance(x, Buf) else x)
        return out

    def add(self, eng, fn, reads=None, writes=None, dma_buf=None):
        ins = Instr(eng, fn)
        ins.is_dma = dma_buf is not None
        reads = self._norm(reads)
        writes = self._norm(writes)
        deps, raw = [], []
        for buf, key in reads:
            for s in self._states(buf, key):
                if s[0] is not None:
                    deps.append(s[0])
                    raw.append(s[0])
                if getattr(buf, "excl", False):
                    for r in s[1].values():
                        if r.eng != eng:
                            deps.append(r)
        for buf, key in writes:
            for s in self._states(buf, key):
                if s[0] is not None:
                    deps.append(s[0])
                deps.extend(s[1].values())
        fdeps = {}
        for d in deps:
            if d is ins:
                continue
            if (not ins.is_dma) and (not d.is_dma) and d.eng == eng:
                if eng == "pe" or not any(d is r for r in raw):
                    continue
            if d.is_dma:
                fdeps[id(d)] = (d, d.buf_sem.dma_cnt * 16)
            else:
                fdeps[id(d)] = (d, None)
        ins.deps = list(fdeps.values())
        if ins.is_dma:
            ins.buf_sem = dma_buf
            dma_buf.dma_cnt += 1
            ins.dma_count = dma_buf.dma_cnt * 16
        rkey = ("dma", id(ins)) if ins.is_dma else eng
        for buf, key in reads:
            for s in self._states(buf, key):
                s[1][rkey] = ins
        for buf, key in writes:
            for s in self._states(buf, key):
                s[0] = ins
                if key is None or s is not buf.st.get(None):
                    s[1] = {}
        self.streams[eng].append(ins)
        self.all.append(ins)
        return ins

    def barrier(self, dma_bufs_all):
        last = {}
        for e in ("pe", "dve", "act", "pool"):
            for ins in reversed(self.streams[e]):
                if not isinstance(ins, tuple) and not ins.is_dma:
                    last[e] = ins
                    ins.signal = True
                    break
        snap = [(b, b.dma_cnt * 16) for b in dma_bufs_all if b.dma_cnt > 0]
        mark = ("barrier", last, snap)
        for e in ENGS:
            self.streams[e].append(mark)

    def emit(self, final_wait_eng="sp"):
        nc = self.nc
        for ins in self.all:
            for d, _v in ins.deps:
                if not d.is_dma:
                    d.signal = True
        with contextlib.ExitStack() as es:
            eng_sem = {e: es.enter_context(nc.semaphore("s_" + e)) for e in ("pe", "dve", "act", "pool")}
            dma_bufs = []
            for ins in self.all:
                if isinstance(ins, tuple):
                    continue
                if ins.is_dma and ins.buf_sem.dma_sem is None:
                    ins.buf_sem.dma_sem = es.enter_context(nc.semaphore("d_%d" % len(dma_bufs)))
                    dma_bufs.append(ins.buf_sem)
            cnt = {e: 0 for e in eng_sem}
            for ins in self.all:
                if not ins.is_dma and ins.signal:
                    cnt[ins.eng] += 1
                    ins.count = cnt[ins.eng]
            block = es.enter_context(nc.Block())
            engobj = {"pe": block.tensor, "dve": block.vector, "act": block.scalar, "pool": block.gpsimd,
                      "sp": block.sync}
            sched = self

            def make(ename):
                def body(eng):
                    water = {}
                    for ins in sched.streams[ename]:
                        if isinstance(ins, tuple):
                            _, last, snap = ins
                            nw_ = 0
                            for e2, li in last.items():
                                if e2 != ename and water.get(("e", e2), 0) < li.count:
                                    water[("e", e2)] = li.count
                                    eng.wait_ge(eng_sem[e2], li.count)
                                    nw_ += 1
                            for b_, v_ in snap:
                                if water.get(("d", id(b_)), 0) < v_:
                                    water[("d", id(b_))] = v_
                                    eng.wait_ge(b_.dma_sem, v_)
                                    nw_ += 1
                            continue
                        need = {}
                        for d, dv in ins.deps:
                            if d.is_dma:
                                key, sem, val = ("d", id(d.buf_sem)), d.buf_sem.dma_sem, dv
                            else:
                                key, sem, val = ("e", d.eng), eng_sem[d.eng], d.count
                            if need.get(key, (None, 0))[1] < val:
                                need[key] = (sem, val)
                        todo = [(key, sem, val) for key, (sem, val) in need.items() if water.get(key, 0) < val]
                        if ename == "pe" and not ins.is_dma:
                            k_ = sched.pe_groups
                            sched.pe_groups += 1
                            if todo and sched.filler is not None:
                                for _ in range(sched.fill_table.get(k_, 0)):
                                    sched.filler(eng)
                        for key, sem, val in todo:
                            water[key] = val
                            eng.wait_ge(sem, val)
                        bi = ins.fn(eng)
                        if ins.is_dma:
                            bi.then_inc(ins.buf_sem.dma_sem, 16)
                        elif ins.signal:
                            bi.then_inc(eng_sem[ename], 1)
                    if ename == final_wait_eng:
                        for b in dma_bufs:
                            eng.wait_ge(b.dma_sem, b.dma_cnt * 16)
                return body

            for ename in ENGS:
                engobj[ename](make(ename))


FILL_TABLE = (
    "8:16,16:4,24:3,32:9,40:4,48:3,56:14,64:4,72:3,80:13,88:4,96:3,104:14,112:4,120:3,128:12,136:4,145:3,154:12,163"
    ":3,172:3,181:11,190:2,199:3,208:1,210:1,212:80,220:7,316:2,384:10,387:1,417:18,437:8,457:40,465:2,481:11,489:1"
    "0,585:2,653:10,656:1,686:16,706:9,726:39,734:2,750:11,758:10,854:2,922:10,925:1,955:15,975:9,995:39,1003:2,101"
    "9:11,1027:10,1123:2,1191:10,1194:1,1224:16,1244:9,1264:39,1272:2,1288:11,1296:10,1392:2,1460:10,1463:1,1493:16"
    ",1513:9,1533:39,1541:2,1557:11,1565:10,1661:2,1729:10,1732:1,1762:16,1782:9,1802:39,1810:2,1826:11,1834:10,193"
    "0:2,1998:10,2001:1,2031:16,2051:9,2071:39,2079:2,2095:11,2103:10,2199:2,2267:10,2270:1,2300:16,2320:9,2340:39,"
    "2348:2,2364:11,2372:10,2468:2,2536:10,2539:1,2569:16,2589:9,2609:39,2617:2,2633:11,2641:10,2737:2,2805:10,2808"
    ":1,2838:16,2858:9,2878:39,2886:2,2902:11,2910:10,3006:2,3074:10,3077:1,3107:15,3127:9,3147:39,3155:2,3171:11,3"
    "179:10,3275:2,3343:10,3346:1,3376:15,3396:9,3416:39,3424:2,3440:11,3448:10,3544:2,3612:10,3615:1,3645:15,3665:"
    "9,3685:39,3693:2,3709:11,3717:10,3813:2,3881:10,3884:1,3914:16,3934:9,3954:39,3962:2,3978:11,3986:10,4082:2,41"
    "50:10,4153:1,4183:16,4203:9,4223:39,4231:2,4247:11,4255:10,4351:2,4431:13,4434:1,4464:15,4484:9,4488:5,4496:9,"
    "4512:29,4520:2,4536:11,4544:19,4652:8,4732:8,4735:3,4765:24,4789:18,4793:4,4801:27,4809:20,4813:4,4821:29,4829"
    ":21,4833:4,4841:25,4849:19,4853:4,4861:25,4869:20,4873:4,4881:27,4889:20,4893:5,4901:28,4909:16,4913:4,4921:27"
    ",4929:19,4933:4,4941:28,4949:21,4953:4,4961:27,4969:20,4973:4,4981:27,4989:20,4993:4,5001:31,5009:17,5013:5,50"
    "21:27,5029:20,5033:4,5041:26,5049:17,5053:4,5061:28,5069:20,5073:4,5081:26,5089:20,5093:3,5101:7,5117:28,5125:"
    "1,5141:80,5149:8,5245:2,5301:17,5304:1,5313:17,5317:9,5342:8,5350:4,5394:2,5398:2,5406:2,5414:2,5418:4,5426:1,"
    "5442:21,5446:1,5454:16,5462:10,5558:2,5614:17,5617:1,5626:12,5630:10,5655:9,5663:8,5675:1,5707:2,5711:2,5719:2"
    ",5727:1,5731:4,5739:1,5755:23,5759:1,5767:14,5775:10,5871:2,5927:17,5930:1,5939:11,5943:10,5968:7,5976:4,6020:"
    "2,6024:2,6032:2,6040:1,6044:4,6052:1,6068:23,6072:1,6080:15,6088:10,6184:2,6240:17,6243:1,6252:16,6256:10,6281"
    ":9,6289:7,6301:1,6313:1,6333:2,6337:2,6345:2,6353:1,6357:4,6381:23,6385:1,6393:16,6401:10,6497:2,6553:17,6556:"
    "1,6565:13,6569:10,6594:7,6602:4,6646:2,6650:2,6658:2,6666:1,6670:5,6678:1,6694:20,6698:1,6706:15,6714:10,6810:"
    "2,6866:17,6869:1,6878:12,6882:10,6907:6,6915:6,6927:1,6939:1,6959:2,6963:2,6971:2,6979:1,6983:4,7007:20,7011:1"
    ",7019:15,7027:10,7123:2,7179:17,7182:1,7191:14,7195:10,7220:8,7228:4,7240:1,7252:1,7272:2,7276:2,7284:2,7292:1"
    ",7296:6,7304:1,7320:22,7324:1,7332:16,7340:10,7436:2,7492:17,7495:1,7504:12,7508:10,7533:10,7541:4,7585:2,7589"
    ":2,7597:2,7605:1,7609:5,7617:1,7633:22,7637:1,7645:16,7653:10,7749:2,7805:17,7808:1,7817:12,7821:10,7846:6,785"
    "4:4,7898:2,7902:2,7910:2,7918:1,7922:5,7930:1,7946:22,7950:1,7958:15,7966:10,8062:2,8118:17,8121:1,8130:12,813"
    "4:10,8159:6,8167:6,8211:3,8215:2,8223:2,8231:2,8235:7,8243:1,8259:19,8263:1,8271:16,8279:10,8375:2,8431:17,843"
    "4:1,8443:14,8447:9,8472:6,8480:4,8524:2,8528:2,8536:2,8544:1,8548:5,8556:1,8572:22,8576:1,8584:16,8592:10,8688"
    ":2,8744:17,8747:1,8756:13,8760:10,8785:9,8793:7,8805:1,8837:2,8841:2,8849:2,8857:1,8861:4,8885:19,8889:1,8897:"
    "16,8905:10,9001:2,9057:17,9060:1,9069:15,9073:9,9098:8,9106:4,9150:2,9154:2,9162:2,9170:1,9174:5,9182:1,9198:2"
    "2,9202:1,9210:16,9218:10,9314:2,9370:17,9373:1,9382:12,9386:10,9411:10,9419:4,9463:2,9467:2,9475:2,9483:1,9487"
    ":4,9511:19,9515:1,9523:16,9531:10,9627:2,9683:17,9686:1,9695:15,9699:9,9724:8,9732:4,9776:2,9780:2,9788:2,9796"
    ":1,9800:4,9824:18,9828:1,9836:16,9844:10,9940:2,10008:17,10011:1,10020:13,10024:10,10049:9,10057:4,10101:2,101"
    "05:2,10113:2,10121:1,10125:5,10133:1,10149:22,10153:1,10161:16,10169:26,10277:4,10345:16,10348:1,10357:31,1036"
    "1:10,10386:5,10390:4,10394:1,10402:3,10436:2,10445:3,10450:69,10484:1,10493:2,10498:71,10532:2,10541:2,10545:1"
    ",10546:69,10580:1,10589:2,10606:10,10610:1,10618:80,10626:7,10722:2,10778:17,10781:1,10790:15,10794:9,10819:7,"
    "10827:4,10871:2,10875:2,10883:2,10891:1,10895:4,10903:1,10919:23,10923:1,10931:15,10939:10,11035:2,11091:17,11"
    "094:1,11103:14,11107:9,11132:7,11140:4,11184:2,11188:2,11196:2,11204:1,11208:4,11216:1,11232:23,11236:1,11244:"
    "15,11252:10,11348:2,11404:17,11407:1,11416:12,11420:10,11445:9,11453:7,11465:1,11497:2,11501:2,11509:2,11517:1"
    ",11521:6,11529:1,11545:22,11549:1,11557:16,11565:10,11661:2,11717:17,11720:1,11729:12,11733:10,11758:6,11766:4"
    ",11810:2,11814:2,11822:2,11830:1,11834:5,11842:1,11858:22,11862:1,11870:16,11878:10,11974:2,12030:17,12033:1,1"
    "2042:12,12046:10,12071:10,12079:4,12123:2,12127:2,12135:2,12143:1,12147:4,12171:19,12175:1,12183:16,12191:10,1"
    "2287:2,12343:17,12346:1,12355:15,12359:9,12384:8,12392:4,12436:2,12440:2,12448:2,12456:1,12460:4,12484:18,1248"
    "8:1,12496:15,12504:10,12600:2,12656:17,12659:1,12668:12,12672:10,12697:6,12705:8,12717:1,12749:2,12753:2,12761"
    ":2,12769:1,12773:4,12797:15,12801:1,12809:16,12817:10,12913:2,12969:17,12972:1,12981:12,12985:10,13010:9,13018"
    ":7,13030:1,13062:2,13066:2,13074:2,13082:1,13086:5,13094:1,13110:22,13114:1,13122:16,13130:10,13226:2,13282:18"
    ",13285:1,13294:12,13298:11,13323:10,13331:4,13375:2,13379:2,13387:2,13395:1,13399:4,13423:19,13427:1,13435:15,"
    "13443:10,13539:2,13595:17,13598:1,13607:12,13611:10,13636:6,13644:8,13656:1,13688:2,13692:2,13700:2,13708:1,13"
    "712:4,13720:1,13736:21,13740:1,13748:16,13756:10,13852:2,13908:17,13911:1,13920:12,13924:10,13949:9,13957:7,13"
    "969:1,14001:2,14005:2,14013:2,14021:1,14025:5,14033:1,14049:22,14053:1,14061:16,14069:10,14165:2,14221:17,1422"
    "4:1,14233:12,14237:10,14262:10,14270:4,14314:2,14318:2,14326:2,14334:1,14338:4,14362:19,14366:1,14374:16,14382"
    ":10,14478:2,14534:17,14537:1,14546:15,14550:9,14575:7,14583:4,14627:2,14631:2,14639:2,14647:1,14651:5,14659:1,"
    "14675:22,14679:1,14687:16,14695:10,14791:2,14847:17,14850:1,14859:13,14863:10,14888:9,14896:7,14940:2,14944:3,"
    "14952:2,14960:1,14964:6,14972:1,14988:23,14992:1,15000:16,15008:10,15104:2,15160:17,15163:1,15172:12,15176:10,"
    "15201:6,15209:4,15253:2,15257:2,15265:2,15273:1,15277:5,15285:1,15301:21,15305:1,15313:16,15321:10,15417:2,154"
    "85:17,15488:1,15497:14,15501:9,15526:8,15534:6,15546:1,15558:1,15578:2,15582:2,15590:2,15598:1,15602:4,15626:1"
    "9,15630:1,15638:15,15646:25,15754:6,15822:16,15825:1,15834:33,15838:8,15863:7,15867:4,15871:1,15879:3,15913:2,"
    "15922:3,15926:1,15927:70,15961:1,15970:2,15975:73,16009:2,16018:2,16023:69,16057:1,16066:2,16083:11,16087:1,16"
    "095:44,16103:24,16111:4,16119:3,16127:12,16135:4,16143:4,16151:13,16159:4,16167:3,16175:14,16183:4,16191:3,161"
    "99:12,16207:4,16215:3,16223:12,16231:4,16240:4,16249:12,16258:2,16267:2,16276:9,16285:2,16294:3,16303:1,16305:"
    "2,16307:80,16315:7,16411:2,16479:9,16482:1,16512:16,16532:10,16552:39,16560:2,16576:11,16584:10,16680:2,16748:"
    "10,16751:1,16781:16,16801:9,16821:39,16829:2,16845:11,16853:10,16949:2,17017:10,17020:1,17050:16,17070:9,17090"
    ":39,17098:2,17114:11,17122:10,17218:2,17286:10,17289:1,17319:16,17339:9,17359:39,17367:2,17383:11,17391:10,174"
    "87:2,17555:10,17558:1,17588:16,17608:9,17628:39,17636:2,17652:11,17660:10,17756:2,17824:10,17827:1,17857:16,17"
    "877:9,17897:39,17905:2,17921:11,17929:10,18025:2,18093:10,18096:1,18126:15,18146:9,18166:39,18174:2,18190:11,1"
    "8198:10,18294:2,18362:10,18365:1,18395:15,18415:9,18435:39,18443:2,18459:11,18467:10,18563:2,18631:10,18634:1,"
    "18664:15,18684:9,18704:39,18712:2,18728:11,18736:10,18832:2,18900:10,18903:1,18933:15,18953:9,18973:39,18981:2"
    ",18997:11,19005:10,19101:2,19169:10,19172:1,19202:16,19222:9,19242:39,19250:2,19266:11,19274:10,19370:2,19438:"
    "9,19441:1,19471:16,19491:9,19511:39,19519:2,19535:11,19543:10,19639:2,19707:10,19710:1,19740:16,19760:9,19780:"
    "39,19788:2,19804:11,19812:10,19908:2,19976:10,19979:1,20009:16,20029:9,20049:40,20057:2,20073:11,20081:10,2017"
    "7:2,20245:10,20248:1,20278:15,20298:9,20318:39,20326:2,20342:11,20350:10,20446:2,20526:13,20529:1,20559:15,205"
    "79:9,20583:5,20591:9,20607:28,20615:2,20631:11,20639:18,20747:7,20827:11,20830:3,20860:22,20884:19,20888:4,208"
    "96:29,20904:17,20908:5,20916:27,20924:18,20928:4,20936:31,20944:18,20948:4,20956:27,20964:20,20968:4,20976:28,"
    "20984:17,20988:5,20996:27,21004:17,21008:4,21016:28,21024:20,21028:4,21036:27,21044:17,21048:5,21056:27,21064:"
    "18,21068:5,21076:30,21084:16,21088:4,21096:28,21104:18,21108:5,21116:27,21124:20,21128:4,21136:27,21144:20,211"
    "48:3,21156:24,21164:17,21168:4,21176:30,21184:20,21188:3,21196:7,21212:28,21220:2,21236:80,21244:8,21340:2,213"
    "96:17,21399:1,21408:17,21412:10,21437:8,21445:4,21489:2,21493:2,21501:2,21509:1,21513:4,21521:1,21537:20,21541"
    ":1,21549:16,21557:10,21653:2,21709:17,21712:1,21721:12,21725:10,21750:6,21758:8,21770:1,21802:3,21806:2,21814:"
    "2,21822:1,21826:4,21850:15,21854:1,21862:15,21870:10,21966:2,22022:17,22025:1,22034:12,22038:9,22063:9,22071:6"
    ",22083:1,22115:2,22119:2,22127:2,22135:1,22139:6,22147:1,22163:23,22167:1,22175:16,22183:10,22279:2,22335:17,2"
    "2338:1,22347:13,22351:10,22376:10,22384:7,22396:1,22428:2,22432:2,22440:2,22448:1,22452:4,22460:1,22476:22,224"
    "80:1,22488:14,22496:10,22592:2,22648:17,22651:1,22660:15,22664:9,22689:9,22697:7,22709:1,22721:1,22741:2,22745"
    ":2,22753:2,22761:1,22765:4,22789:22,22793:1,22801:15,22809:10,22905:2,22961:17,22964:1,22973:12,22977:10,23002"
    ":6,23010:4,23054:2,23058:2,23066:2,23074:1,23078:4,23086:1,23102:23,23106:1,23114:15,23122:10,23218:2,23274:17"
    ",23277:1,23286:12,23290:10,23315:6,23323:7,23335:1,23367:2,23371:2,23379:2,23387:1,23391:5,23399:1,23415:22,23"
    "419:1,23427:16,23435:10,23531:2,23587:17,23590:1,23599:13,23603:10,23628:9,23636:7,23648:1,23680:2,23684:2,236"
    "92:2,23700:1,23704:4,23728:19,23732:1,23740:16,23748:10,23844:2,23900:17,23903:1,23912:15,23916:9,23941:8,2394"
    "9:4,23993:2,23997:2,24005:2,24013:1,24017:4,24041:18,24045:1,24053:16,24061:10,24157:2,24213:17,24216:1,24225:"
    "12,24229:10,24254:6,24262:7,24274:1,24306:2,24310:2,24318:2,24326:1,24330:5,24338:1,24354:23,24358:1,24366:16,"
    "24374:10,24470:2,24526:17,24529:1,24538:12,24542:10,24567:11,24575:7,24587:1,24619:2,24623:2,24631:2,24639:1,2"
    "4643:4,24667:16,24671:1,24679:14,24687:10,24783:2,24839:17,24842:1,24851:16,24855:10,24880:7,24888:6,24900:1,2"
    "4912:1,24924:1,24932:2,24936:2,24944:2,24952:1,24956:4,24964:1,24980:23,24984:1,24992:15,25000:10,25096:2,2515"
    "2:17,25155:1,25164:14,25168:12,25193:8,25201:4,25245:2,25249:2,25257:2,25265:1,25269:4,25293:18,25297:1,25305:"
    "16,25313:10,25409:2,25465:17,25468:1,25477:12,25481:10,25506:6,25514:8,25526:1,25558:2,25562:2,25570:2,25578:1"
    ",25582:5,25590:1,25606:20,25610:1,25618:15,25626:10,25722:2,25778:17,25781:1,25790:12,25794:10,25819:9,25827:6"
    ",25839:1,25851:1,25871:2,25875:2,25883:2,25891:1,25895:4,25919:18,25923:1,25931:16,25939:10,26035:2,26103:17,2"
    "6106:1,26115:14,26119:9,26144:8,26152:6,26164:1,26176:1,26196:3,26200:2,26208:2,26216:1,26220:4,26244:19,26248"
    ":1,26256:16,26264:25,26372:3,26440:16,26443:1,26452:33,26456:10,26481:5,26485:4,26489:1,26497:3,26531:2,26540:"
    "2,26545:71,26579:1,26588:2,26593:71,26627:2,26636:2,26641:69,26675:1,26684:2,26701:10,26705:1,26713:80,26721:6"
    ",26817:2,26873:17,26876:1,26885:14,26889:9,26914:6,26922:7,26934:1,26966:2,26970:2,26978:2,26986:1,26990:4,269"
    "98:1,27014:20,27018:1,27026:46,27034:7,27130:2,27186:17,27189:1,27198:14,27202:9,27227:8,27235:6,27247:1,27259"
    ":1,27279:2,27283:2,27291:2,27299:1,27303:4,27327:20,27331:1,27339:46,27347:9,27443:2,27499:15,27502:1,27511:15"
    ",27515:9,27540:8,27548:4,27592:2,27596:2,27604:2,27612:1,27616:5,27624:1,27640:21,27644:1,27652:43,27660:9,277"
    "56:2,27812:15,27815:1,27824:15,27828:9,27853:8,27861:4,27905:2,27909:2,27917:2,27925:1,27929:4,27937:1,27953:2"
    "0,27957:1,27965:44,27973:9,28069:2,28125:17,28128:1,28137:13,28141:10,28166:9,28174:7,28218:2,28222:2,28230:2,"
    "28238:1,28242:4,28250:1,28266:23,28270:1,28278:46,28286:8,28382:2,28438:15,28441:2,28450:11,28454:10,28479:6,2"
    "8487:4,28531:2,28535:2,28543:2,28551:1,28555:4,28579:20,28583:1,28591:42,28599:9,28695:2,28751:15,28754:1,2876"
    "3:13,28767:9,28792:6,28800:7,28844:2,28848:2,28856:2,28864:1,28868:4,28876:1,28892:21,28896:1,28904:46,28912:9"
    ",29008:2,29064:17,29067:1,29076:12,29080:10,29105:10,29113:7,29125:1,29157:2,29161:2,29169:2,29177:1,29181:4,2"
    "9189:1,29205:23,29209:1,29217:44,29225:7,29321:2,29377:17,29380:1,29389:11,29393:10,29418:8,29426:4,29438:1,29"
    "470:2,29474:2,29482:2,29490:1,29494:4,29502:1,29518:19,29522:1,29530:45,29538:7,29634:2,29690:17,29693:1,29702"
    ":11,29706:10,29731:8,29739:4,29751:1,29783:2,29787:2,29795:2,29803:1,29807:4,29815:1,29831:19,29835:1,29843:44"
    ",29851:7,29947:2,30003:15,30006:1,30015:13,30019:9,30044:8,30052:5,30096:2,30100:2,30108:2,30116:1,30120:4,301"
    "44:14,30148:1,30156:42,30164:9,30260:2,30316:15,30319:1,30328:13,30332:9,30357:8,30365:6,30409:2,30413:2,30421"
    ":2,30429:1,30433:5,30441:1,30457:18,30461:1,30469:46,30477:9,30573:2,30629:17,30632:1,30641:11,30645:10,30670:"
    "8,30678:4,30722:2,30726:2,30734:2,30742:1,30746:4,30770:14,30774:1,30782:46,30790:9,30886:2,30942:17,30945:1,3"
    "0954:12,30958:10,30983:10,30991:7,31003:1,31035:2,31039:2,31047:2,31055:1,31059:4,31067:1,31083:23,31087:1,310"
    "95:45,31103:9,31199:2,31255:17,31258:1,31267:14,31271:9,31296:7,31304:4,31348:2,31352:2,31360:2,31368:1,31372:"
    "4,31380:1,31396:21,31400:1,31408:45,31416:9,31512:2,31580:17,31583:1,31592:11,31596:10,31621:9,31629:4,31673:2"
    ",31677:2,31685:2,31693:1,31697:6,31705:1,31721:21,31725:1,31733:46,31741:11,31849:8,31917:16,31920:1,31929:34,"
    "31933:12,31958:10,31962:6,31966:1,31974:4,32008:2,32017:3,32022:71,32056:1,32065:2,32070:74,32104:2,32106:1,32"
    "113:2,32118:70,32152:1,32161:2,32178:8,32182:1"
)
D = 1024
SEQ = 2048
NT = SEQ // 128
NSB = 16
SL = 4
DEPTH = 2
IN_DIM = 6688
EPS = 1e-6
NEG = -30000.0
OFF_ZS, OFF_XBC, OFF_DT, OFF_ZG, OFF_QKV, OFF_A, OFF_B = 0, 1024, 2560, 2576, 3600, 6672, 6680


class Cfg:
    def __init__(self, name, T, nseq, L, Q, chain):
        self.name, self.T, self.nseq, self.L, self.Q, self.chain = name, T, nseq, L, Q, chain
        self.nch = T // Q
        self.nlev = {64: 5, 4: 1}[Q]


CP = Cfg("p", 128, 1, 128, 64, True)
CS = Cfg("s", 64, NSB, SL, 4, False)


def host_consts():
    c = {"ident": np.eye(128, dtype=np.float32), "ones": np.ones((128, 128), np.float32)}
    for cfg in (CP, CS):
        T, Q, nch = cfg.T, cfg.Q, cfg.nch
        ch = np.arange(T) // Q
        same = ch[:, None] == ch[None, :]
        idx = np.arange(T)
        le = idx[:, None] <= idx[None, :]
        lt = idx[:, None] < idx[None, :]
        n = cfg.name
        c["tri_" + n] = (same & le).astype(np.float32)
        c["blk_" + n] = same.astype(np.float32)
        c["nmI_" + n] = np.where(same & le, 0.0, NEG).astype(np.float32)
        c["nmS_" + n] = np.where(same & lt, 0.0, NEG).astype(np.float32)
        c["nmSr_" + n] = np.ascontiguousarray(c["nmS_" + n].T)
        cm = (ch[None, :] == np.arange(nch)[:, None]).astype(np.float32)
        c["cmf_" + n] = np.ascontiguousarray(np.broadcast_to(cm[None], (128, nch, T))).astype(np.float32)
        c["cmT_" + n] = np.ascontiguousarray(cm.T)
    e = np.zeros((17, 128), np.float32)
    e[0, :] = 1.0
    c["E_p"] = e
    e = np.zeros((17, 64), np.float32)
    for t in range(64):
        e[1 + t // SL, t] = 1.0
    c["E_s"] = e
    return c


def bl(ap, n):
    return ap.unsqueeze(len(ap.shape)).to_broadcast(list(ap.shape) + [n])


def bm(ap, n):
    return ap.unsqueeze(1).to_broadcast([ap.shape[0], n] + list(ap.shape[1:]))


def build(debug=(), limit=None):
    limit = limit or {}
    nc = bass.Bass("TRN2", target_bir_lowering=False)
    S = Sched(nc)
    es = contextlib.ExitStack()
    consts_np = host_consts()

    def din(name, shape):
        return nc.dram_tensor(name, list(shape), F32, kind="ExternalInput").ap()

    def dout(name, shape):
        return nc.dram_tensor(name, list(shape), F32, kind="ExternalOutput").ap()

    I = {}
    I["xp"] = din("xp", [SEQ, D])
    I["xs"] = din("xs", [NSB * SL, D])
    I["call"] = din("call", [17, D])
    I["st_sc"] = din("st_sc", [DEPTH, NSB, 3, 1536])
    I["st_ssm"] = din("st_ssm", [DEPTH, NSB, 16, 64, 128])
    I["st_gc"] = din("st_gc", [DEPTH, NSB, 3, 3072])
    I["st_gdn"] = din("st_gdn", [DEPTH, NSB, 8, 128, 128])
    wshapes = {"norm_w": [DEPTH, D], "w_ada": [DEPTH, D, 3 * D], "b_ada": [DEPTH, 3 * D], "w_in": [DEPTH, D, IN_DIM],
               "ssd_conv_w": [DEPTH, 4, 1536], "ssd_conv_b": [DEPTH, 1536], "ssd_dt_bias": [DEPTH, 16],
               "ssd_a_log": [DEPTH, 16], "ssd_d": [DEPTH, 16], "ssd_norm_w": [DEPTH, 1024],
               "gdn_conv_w": [DEPTH, 4, 3072], "gdn_dt_bias": [DEPTH, 8], "gdn_a_log": [DEPTH, 8],
               "gdn_norm_w": [DEPTH, 128], "w_out": [DEPTH, 2048, D], "final_norm_w": [1, D]}
    for k, shp in wshapes.items():
        I[k] = din(k, shp)
    for k, v in consts_np.items():
        I["c_" + k] = din("c_" + k, v.shape)
    O = {}
    O["y_p"] = dout("y_p", [SEQ, D])
    O["y_s"] = dout("y_s", [NSB * SL, D])
    O["ncs_p"] = dout("ncs_p", [DEPTH, 3, 1536])
    O["nssm_p"] = dout("nssm_p", [DEPTH, 1024, 128])
    O["ngc_p"] = dout("ngc_p", [DEPTH, 3, 3072])
    O["ngdn_p"] = dout("ngdn_p", [DEPTH, 8, 128, 128])
    O["ncs_s"] = dout("ncs_s", [DEPTH, NSB * 3, 1536])
    O["nssm_s"] = dout("nssm_s", [DEPTH, NSB, 1024, 128])
    O["ngc_s"] = dout("ngc_s", [DEPTH, NSB * 3, 3072])
    O["ngdn_s"] = dout("ngdn_s", [DEPTH, NSB, 8, 128, 128])
    NROW = SEQ + NSB * SL
    scr = [Buf(nc.dram_tensor("scr%d" % i, [NROW, D], F32, kind="Internal").ap(), "scr%d" % i) for i in range(3)]
    xin_ext = None

    allb = []

    def sb(name, shape, dt=F32):
        b_ = Buf(es.enter_context(nc.sbuf_tensor(name, list(shape), dt)), name)
        allb.append(b_)
        return b_

    banks = [Buf(es.enter_context(nc.psum_tensor("bank%d" % i, [128, 512], F32)), "bank%d" % i) for i in range(8)]
    for b_ in banks:
        b_.excl = True
    pstate = {"i": 0, "pinned": set()}
    NB = 7

    def pget():
        while True:
            i = pstate["i"] % NB
            pstate["i"] += 1
            if i not in pstate["pinned"]:
                return banks[i]

    def ppin(b):
        pstate["pinned"].add(banks.index(b))

    def punpin(b):
        pstate["pinned"].discard(banks.index(b))

    def pf(b, shape, p=128):
        n = int(np.prod(shape))
        ap = b[:p, 0:n]
        if len(shape) == 1:
            return ap
        names = " ".join("a%d" % i for i in range(len(shape)))
        kw = {"a%d" % i: shape[i] for i in range(1, len(shape))}
        return ap.rearrange("p (%s) -> p %s" % (names, names), **kw)

    def pb(b, shape, p=128):
        n = int(np.prod(shape))
        ap = b[:p, :].bitcast(BF16)[:, 0:n]
        if len(shape) == 1:
            return ap
        names = " ".join("a%d" % i for i in range(len(shape)))
        kw = {"a%d" % i: shape[i] for i in range(1, len(shape))}
        return ap.rearrange("p (%s) -> p %s" % (names, names), **kw)

    def MM(out, lhsT, rhs, R, W, start=True, stop=True, **kw):
        S.add("pe", lambda e: e.matmul(out, lhsT=lhsT, rhs=rhs, start=start, stop=stop, **kw), reads=R, writes=W)

    def TR(out, in_, ident, R, W):
        S.add("pe", lambda e: e.transpose(out=out, in_=in_, identity=ident), reads=R, writes=W)

    def ACT(out, in_, func, R, W, **kw):
        S.add("act", lambda e: e.activation(out=out, in_=in_, func=func, **kw), reads=R, writes=W)

    def TT(out, in0, in1, op, R, W, eng="dve"):
        S.add(eng, lambda e: e.tensor_tensor(out=out, in0=in0, in1=in1, op=op), reads=R, writes=W)

    def TS(out, in0, s1, s2, op0, op1, R, W, eng="dve"):
        S.add(eng, lambda e: e.tensor_scalar(out=out, in0=in0, scalar1=s1, scalar2=s2, op0=op0, op1=op1), reads=R, writes=W)

    def STT(out, in0, scalar, in1, op0, op1, R, W):
        S.add("dve", lambda e: e.scalar_tensor_tensor(out=out, in0=in0, scalar=scalar, in1=in1, op0=op0, op1=op1), reads=R, writes=W)

    def CP_(out, in_, R, W, eng="dve"):
        S.add(eng, lambda e: e.tensor_copy(out=out, in_=in_), reads=R, writes=W)

    def RECIP(out, in_, R, W):
        S.add("dve", lambda e: e.reciprocal(out=out, in_=in_), reads=R, writes=W)

    def RED(out, in_, R, W):
        S.add("dve", lambda e: e.tensor_reduce(out=out, in_=in_, axis=AX.X, op=ALU.add), reads=R, writes=W)

    def MSET(ap, val, W, eng="pool"):
        S.add(eng, lambda e: e.memset(ap, val), writes=W)

    def LD(out, in_, buf, R=None, nonc=False):
        if nonc:
            S.add("sp", lambda e: e.dma_start(out=out, in_=in_, allow_slow_non_contiguous=True), reads=R, writes=[buf], dma_buf=buf)
        else:
            S.add("sp", lambda e: e.dma_start(out=out, in_=in_), reads=R, writes=[buf], dma_buf=buf)

    sto_n = [0]
    sto_dummy = Buf(None, "sto_dummy")

    def STO(out, in_, buf, W=None, extra_reads=None):
        S.add("sp", lambda e: e.dma_start(out=out, in_=in_), reads=[buf] + (extra_reads or []), writes=W, dma_buf=buf)

    dbg_out = {}
    mod_stacks = []
    cur_es = [es]
    all_bufs = []

    def DBG(name, buf, ap, shape):
        if name in debug and name not in dbg_out:
            o = dout("dbg_" + name, shape)
            dbg_out[name] = o
            if ap.dtype != F32:
                tmp = Buf(cur_es[0].enter_context(nc.sbuf_tensor("dbgt_" + name, list(shape), F32)), "dbgt_" + name)
                all_bufs.append(tmp)
                CP_(tmp[:], ap, [buf], [tmp])
                STO(o, tmp[:], tmp)
            else:
                STO(o, ap, buf)

    C = {}
    for k, v in consts_np.items():
        if k.startswith("cmf_"):
            continue
        C[k] = sb("k_" + k, v.shape)
        LD(C[k][:], I["c_" + k], C[k])
    identf = C["ident"]
    onesf = C["ones"]
    identb = sb("identb", [128, 128], BF16)
    CP_(identb[:], identf[:], [identf], [identb])
    onesb = sb("onesb", [128, 128], BF16)
    CP_(onesb[:], onesf[:], [onesf], [onesb])
    cfg_pending = []
    for cfg in (CP, CS):
        n = cfg.name
        cfg.tri, cfg.blk, cfg.nmI, cfg.nmS, cfg.nmSr, cfg.cmT = (C[k + n] for k in ("tri_", "blk_", "nmI_", "nmS_", "nmSr_", "cmT_"))
        cfg.cmf = sb("cmfb_" + n, [128, cfg.nch, cfg.T], BF16)
        cfg.ncmf = sb("ncmfb_" + n, [128, cfg.nch, cfg.T], BF16)
        cfg_pending.append(cfg)
        cfg.cmTb = sb("cmTb_" + n, [cfg.T, cfg.nch], BF16)
        CP_(cfg.cmTb[:], cfg.cmT[:], [cfg.cmT], [cfg.cmTb])

    SW = 128
    stage = [sb("stage%d" % i, [128, 8, SW]) for i in range(2)]
    stage_i = [0]
    WO = sb("wo", [128, 8, D], BF16)
    DIAG = sb("diag", [128, 12, 4, 128], BF16)
    cwT = sb("cwT", [128, 12, 4])
    cbrow = sb("cbrow", [1, 1536], BF16)
    nwT = sb("nwT", [128, 8])
    scT = sb("scT", [128, 8, 17])
    shT = sb("shT", [128, 8, 17])
    gate_bc = {"p": sb("gate_p", [128, D]), "s": sb("gate_s", [64, D])}
    vec16 = sb("vec16", [128, 4, 16])
    vec8 = sb("vec8", [128, 2, 8])
    scTb = sb("scTb", [128, 8, 17], BF16)
    badaT = sb("badaT", [128, 24])
    nw_in = sb("nw_in", [128, 8])
    xt_slots = [sb("xt%d" % i, [128, D]) for i in range(2)]
    xa_slots = [sb("xa%d" % i, [128, D]) for i in range(2)]
    stage_all = stage + xt_slots + xa_slots
    junk = sb("junk", [128, D], BF16)
    xn = sb("xn", [128, D], BF16)
    hT = sb("hT", [128, 8, 128], BF16)
    sm1 = sb("sm1", [128, 8])
    raw = sb("raw", [128, 12, 132], BF16)
    tailf = sb("tailf", [128, 12, 48])
    tailo = sb("tailo", [48, 1536])
    cst_in = tailo
    fT = sb("fT", [128, 12, 128], BF16)
    big = [sb("big%d" % i, [128, D]) for i in range(4)]
    bigb = [sb("bigb%d" % i, [128, D], BF16) for i in range(3)]
    xo = big[3]
    call_sb, sil_c, bgate, gate17, cbrow_f = big[0], bigb[0], big[1], big[2], tailo
    hTf_t = big[0]
    for cfg in cfg_pending:
        n_ = cfg.nch * cfg.T
        tmpv = big[0][:, 0:n_].rearrange("p (c t) -> p c t", t=cfg.T)
        LD(tmpv, I["c_cmf_" + cfg.name], big[0])
        CP_(cfg.cmf[:], tmpv, [big[0]], [cfg.cmf])
        TS(cfg.ncmf[:], tmpv, -1.0, None, ALU.mult, ALU.bypass, [big[0]], [cfg.ncmf])

    def stg():
        i = stage_i[0] % len(stage_all)
        stage_i[0] += 1
        b_ = stage_all[i]
        v = b_[:, :, :] if b_ in stage else b_[:, :].rearrange("p (k n) -> p k n", k=8)
        return b_, v, ("pool", "act", "dve")[i % 3]

    def cast(eng, out, in_, R, W, scale=None):
        if eng == "act":
            if scale is None:
                ACT(out, in_, AF.Copy, R, W)
            else:
                ACT(out, in_, AF.Copy, R, W, scale=scale)
        elif scale is None:
            CP_(out, in_, R, W, eng=eng)
        else:
            TS(out, in_, scale, 1.0, ALU.mult, ALU.mult, R, W, eng=eng)

    def load_cols(dst, dcol, src, ncols):
        c0 = 0
        while c0 < ncols:
            n = min(SW, ncols - c0)
            sbuf_, st, eng = stg()
            LD(st[:, :, 0:n], src[:, c0:c0 + n].rearrange("(k p) n -> p k n", p=128), sbuf_)
            cast(eng, dst[:, :, dcol + c0:dcol + c0 + n], st[:, :, 0:n], [sbuf_], [dst])
            c0 += n

    def load_wo(src_rows, nk, scale_ap_fn):
        for half in range(D // SW):
            sbuf_, st, eng = stg()
            LD(st[:, 0:nk, :], src_rows[:, half * SW:(half + 1) * SW].rearrange("(k p) n -> p k n", p=128), sbuf_)
            for k in range(nk):
                cast(eng, WO[:, k, half * SW:(half + 1) * SW], st[:, k, :], [sbuf_, nwT], [WO], scale=scale_ap_fn(k))

    def build_diag(convw, c0, nchunks):
        for k_ in range(4):
            LD(cwT[:, 0:nchunks, k_], convw[k_, c0 * 128:(c0 + nchunks) * 128].rearrange("(c p) -> p c", p=128), cwT, nonc=True)
        for c in range(nchunks):
            TT(DIAG[:, c, :, :], bm(identf[:], 4), bl(cwT[:, c, :], 128), ALU.mult, [identf, cwT], [DIAG], eng="pool")

    def do_mod(l):
        LD(call_sb[:17, :], I["call"], call_sb)
        ACT(sil_c[:17, :], call_sb[:17, :], AF.Silu, [call_sb], [sil_c])
        b = pget()
        for c in range(8):
            TR(pb(b, [8, 32])[:, c, 0:17], sil_c[:17, c * 128:(c + 1) * 128], identb[:17, :17], [sil_c, identb], [b])
        CP_(scTb[:], pb(b, [8, 32])[:, :, 0:17], [b], [scTb])
        LD(badaT[:], I["b_ada"][l].rearrange("(c p) -> p c", p=128), badaT, nonc=True)
        LD(nw_in[:], I["norm_w"][l].rearrange("(c p) -> p c", p=128), nw_in, nonc=True)
        LD(bgate[0:1, :], I["b_ada"][l:l + 1, 2048:3072], bgate)
        nblk = 3 * D // SW
        per = D // SW
        modes = contextlib.ExitStack()
        wb = Buf(modes.enter_context(nc.sbuf_tensor("modwb%d" % l, [128, 8, SW], BF16)), "modwb")
        all_bufs.append(wb)
        for blk in range(nblk):
            sbuf_, st, eng = stg()
            LD(st, I["w_ada"][l][:, blk * SW:(blk + 1) * SW].rearrange("(k p) n -> p k n", p=128), sbuf_)
            cast(eng, wb[:, :, 0:SW], st, [sbuf_], [wb])
            if blk < 2 * per:
                b = pget()
                nj = SW // 128
                for j in range(nj):
                    for k in range(8):
                        MM(pf(b, [nj, 17])[:, j, :], wb[:, k, j * 128:(j + 1) * 128], scTb[:, k, :], [wb, scTb], [b],
                           start=(k == 0), stop=(k == 7))
                dst = shT if blk < per else scT
                cc = (blk % per) * nj
                TT(dst[:, cc:cc + nj, :], pf(b, [nj, 17]), bl(badaT[:, blk * nj:blk * nj + nj], 17), ALU.add, [b, badaT], [dst])
            else:
                g = pget()
                gc0 = (blk - 2 * per) * SW
                for k in range(8):
                    MM(g[:17, 0:SW], scTb[:, k, :], wb[:, k, 0:SW], [wb, scTb], [g], start=(k == 0), stop=False)
                MM(g[:17, 0:SW], onesf[0:1, 0:17], bgate[0:1, gc0:gc0 + SW], [onesf, bgate], [g], start=False, stop=True)
                CP_(gate17[:17, gc0:gc0 + SW], g[:17, 0:SW], [g], [gate17])
        TS(scT[:], scT[:], 1.0, None, ALU.add, ALU.bypass, [scT], [scT])
        TT(scT[:], scT[:], bl(nw_in[:], 17), ALU.mult, [scT, nw_in], [scT])
        for n, E, T in (("p", C["E_p"], 128), ("s", C["E_s"], 64)):
            for half in range(2):
                b = pget()
                MM(b[:T, :], E[:, :T], gate17[:17, half * 512:(half + 1) * 512], [E, gate17], [b])
                CP_(gate_bc[n][:T, half * 512:(half + 1) * 512], b[:T, :], [b], [gate_bc[n]])
        mod_stacks.append(modes)

    def rows(ti):
        return (ti * 128, 128) if ti < NT else (SEQ, 64)

    def src_ap(layer_in, ti):
        r0, T = rows(ti)
        if layer_in is None:
            return (I["xp"][r0:r0 + T, :] if ti < NT else I["xs"][:, :]), None
        return layer_in[r0:r0 + T, :], (layer_in, ti)

    def prologue(cfg, xt):
        T = cfg.T
        ACT(junk[:T, :], xt[:T, :], AF.Square, [xt], [junk, sm1], accum_out=sm1[:T, 0:1])
        TS(sm1[:T, 0:1], sm1[:T, 0:1], 1.0 / D, EPS, ALU.mult, ALU.add, [sm1], [sm1])
        ACT(sm1[:T, 0:1], sm1[:T, 0:1], AF.Ln, [sm1], [sm1])
        ACT(sm1[:T, 0:1], sm1[:T, 0:1], AF.Exp, [sm1], [sm1], scale=-0.5)
        ACT(xn[:T, :], xt[:T, :], AF.Copy, [xt, sm1], [xn], scale=sm1[:T, 0:1])
        b = pget()
        for c in range(8):
            TR(pb(b, [8, T])[:, c, :], xn[:T, c * 128:(c + 1) * 128], identb[:T, :T], [xn, identb], [b])
        if cfg.nseq == 1:
            sc, sh = bl(scT[:, :, 0], T), bl(shT[:, :, 0], T)
            o1, o2, i1 = hTf_t[:, 0:8 * T].rearrange("p (c t) -> p c t", t=T), hT[:, :, :T], pb(b, [8, T])
        else:
            sc, sh = bl(scT[:, :, 1:17], SL), bl(shT[:, :, 1:17], SL)
            o1 = hTf_t[:, 0:8 * T].rearrange("p (c b l) -> p c b l", c=8, l=SL)
            o2 = hT[:, :, :T].rearrange("p c (b l) -> p c b l", l=SL)
            i1 = pb(b, [8, NSB, SL])
        TT(o1, i1, sc, ALU.mult, [b, scT], [hTf_t])
        TT(o2, o1, sh, ALU.add, [hTf_t, shT], [hT])

    def inproj_conv(cfg, l, wcol0, nchunks, bias_row, st_conv_in, conv_out, conv_out_s, ch0, is_last, first):
        T, L, nseq = cfg.T, cfg.L, cfg.nseq
        rawv = raw[:, :, 0:nseq * (L + 3)].rearrange("p c (b l) -> p c b l", l=L + 3)
        if cfg.nseq == 1:
            if first:
                MSET(raw[:, :, 0:3], 0.0, [raw])
            else:
                CP_(raw[:, 0:nchunks, 0:3], raw[:, 0:nchunks, L:L + 3], [raw], [raw])
        else:
            LD(cst_in[:, 0:nchunks * 128], st_conv_in[:, :, ch0 * 128:(ch0 + nchunks) * 128].rearrange("b k c -> (b k) c"), cst_in)
            for g0 in range(0, nchunks, 8):
                b = pget()
                ng = min(8, nchunks - g0)
                for c in range(ng):
                    TR(pf(b, [8, 48])[:, c, :], cst_in[:, (g0 + c) * 128:(g0 + c + 1) * 128], identf[:48, :48], [cst_in, identf], [b])
                CP_(rawv[:, g0:g0 + ng, :, 0:3], pf(b, [8, 48])[:, 0:ng, :].rearrange("p c (b k) -> p c b k", k=3), [b], [raw])
        if limit.get("stopC", 99) <= 1:
            return
        for g0 in range(0, nchunks, 4):
            b = pget()
            for c in range(4):
                for k in range(8):
                    MM(pf(b, [4, T])[:, c, :], PA["WB"][:, k, wcol0 + (g0 + c) * 128: wcol0 + (g0 + c + 1) * 128], hT[:, k, :T],
                       [PA["WB"], hT], [b], start=(k == 0), stop=(k == 7))
            if nseq == 1:
                ACT(raw[:, g0:g0 + 4, 3:3 + L], pf(b, [4, T]), AF.Copy, [b], [raw])
            else:
                for c in range(4):
                    ACT(rawv[:, g0 + c, :, 3:3 + L], pf(b, [4, T])[:, c, :].rearrange("p (b l) -> p b l", l=L), AF.Copy, [b], [raw])
            if is_last and limit.get("tail", True):
                for c in range(4):
                    ACT(tailf[:, g0 + c, 0:nseq * 3].rearrange("p (b k) -> p b k", k=3),
                        pf(b, [4, T])[:, c, :].rearrange("p (b l) -> p b l", l=L)[:, :, L - 3:L], AF.Copy, [b], [tailf])
        if limit.get("stopC", 99) <= 2:
            return
        for g0 in range(0, nchunks, 4):
            b = pget()
            for c in range(4):
                o = pf(b, [4, T])[:, c, :]
                if nseq > 1:
                    o = o.rearrange("p (b l) -> p b l", l=L)
                for k in range(4):
                    MM(o, DIAG[:, g0 + c, k, :], rawv[:, g0 + c, :, k:k + L] if nseq > 1 else raw[:, g0 + c, k:k + L],
                       [DIAG, raw], [b], start=(k == 0), stop=(k == 3 and bias_row is None))
                if bias_row is not None:
                    MM(pf(b, [4, T])[:, c, :], bias_row[0:1, (g0 + c) * 128:(g0 + c + 1) * 128], onesb[0:1, :T], [bias_row, onesb], [b],
                       start=False, stop=True)
            ACT(fT[:, g0:g0 + 4, :T], pf(b, [4, T]), AF.Silu, [b], [fT])
        if limit.get("stopC", 99) <= 3:
            return
        if is_last:
            n3 = nseq * 3
            for g0 in range(0, nchunks, 4):
                b = pget()
                for c in range(4):
                    TR(pf(b, [4, 128], p=n3)[:, c, :], tailf[:, g0 + c, 0:n3], identf[:, :], [tailf, identf], [b])
                CP_(tailo[:n3, g0 * 128:(g0 + 4) * 128], b[:n3, :], [b], [tailo])
            dst = conv_out[l][:, ch0 * 128:(ch0 + nchunks) * 128] if nseq == 1 else conv_out_s[l][:, ch0 * 128:(ch0 + nchunks) * 128]
            STO(dst, tailo[:n3, 0:nchunks * 128], tailo)

    def cum_and_tot(cfg, a_ap, a_buf, nh, out_c, out_dbc, pw):
        T, nch = cfg.T, cfg.nch
        b = pget()
        MM(b[:T, 0:nh], cfg.tri[:, :], a_ap, [cfg.tri, a_buf], [b])
        MM(b[:T, nh:2 * nh], cfg.blk[:, :], a_ap, [cfg.blk, a_buf], [b])
        ACT(out_c[:T, 0:2 * nh], b[:T, 0:2 * nh], AF.Copy, [b], [out_c])
        b2 = pget()
        MM(b2[:nch, 0:nh], cfg.cmT[:, :], a_ap, [cfg.cmT, a_buf], [b2])
        CP_(tot[:nch, 0:nh], b2[:nch, 0:nh], [b2], [tot])
        TT(totx[:nch, 0:nch * nh].rearrange("p (c h) -> p c h", h=nh), bm(tot[:nch, 0:nh], nch), bl(identf[:nch, :nch], nh), ALU.mult,
           [tot, identf], [totx])
        b3 = pget()
        MM(b3[:pw, 0:nch * nh], onesf[:nch, :pw], totx[:nch, 0:nch * nh], [onesf, totx], [b3])
        ACT(out_dbc[:pw, 0:nch * nh], b3[:pw, 0:nch * nh], AF.Exp, [b3], [out_dbc])

    tot = sb("tot", [16, 16])
    totx = sb("totx", [16, 256])
    sm = {k: sb("sm_" + k, [128, 64]) for k in ("a", "b", "c", "d", "e")}
    dbc = sb("dbc", [128, 256])
    acT = sb("acT", [16, 128])

    PA = {}

    def alloc_A(pes, tag):
        def sbp(name, shape, dt=F32):
            return Buf(pes.enter_context(nc.sbuf_tensor(name + tag, list(shape), dt)), name)
        PA.update(stf=sbp("stf", [128, D]), stb=sbp("stb", [128, D], BF16), hn=sbp("hn", [128, 8, 128]),
                  stf_s=sbp("stf_s", [128, D]), stb_s=sbp("stb_s", [128, D], BF16), Gs=sbp("Gs", [128, 2, 128]),
                  earg=sbp("earg", [128, 4, 128]), Ee=sbp("Ee", [128, 4, 128]), WT=sbp("WT", [128, 16, 128], BF16),
                  Btok=sbp("Btok", [128, 256], BF16), Btm=sbp("Btm", [128, 2, 256], BF16), CTm=sbp("CTm", [128, 2, 2, 128], BF16),
                  WB=sbp("WB", [128, 8, 2576], BF16))
    mslot = [0]

    def ssd_state_update(cfg, c, st_in, stb_in, st_out, yo, xdec, first_c, last_c):
        T, nch = cfg.T, cfg.nch
        Btm, CTm, Btok = PA["Btm"], PA["CTm"], PA["Btok"]
        sl = mslot[0] % 2
        mslot[0] += 1
        TS(Btm[:T, sl, :], Btok[:T, :], cfg.cmT[:, c:c + 1], 1.0, ALU.mult, ALU.mult, [Btok, cfg.cmT], [(Btm, sl)], eng="pool")
        TT(CTm[:, sl, :, :T], fT[:, 10:12, :T], bm(cfg.cmf[:, c, :], 2), ALU.mult, [fT, cfg.cmf], [(CTm, sl)], eng="pool")
        for g in range(2):
            MM(yo[g][:T, :], CTm[:, sl, g, :T], stb_in[:, g * 512:(g + 1) * 512], [(CTm, sl), stb_in], [yo[g]], start=first_c, stop=last_c)
        sp_ = [pget(), pget()]
        for g in range(2):
            MM(sp_[g][:, :], Btm[:T, sl, g * 128:(g + 1) * 128], xdec[:T, g * 512:(g + 1) * 512], [(Btm, sl), xdec], [sp_[g]])
        TT(big[3][:, :].rearrange("p (h q) -> p h q", q=64), st_in[:, :].rearrange("p (h q) -> p h q", q=64),
           bl(dbc[:, c * 16:(c + 1) * 16], 64), ALU.mult, [st_in, dbc], [big[3]])
        for g in range(2):
            TT(st_out[:, g * 512:(g + 1) * 512], big[3][:, g * 512:(g + 1) * 512], sp_[g][:, :], ALU.add, [big[3], sp_[g]], [st_out])

    def phaseA_tile(cfg, l, ti, xt, first, is_last):
        T, L, nseq, nch = cfg.T, cfg.L, cfg.nseq, cfg.nch
        stf, stb, hn, stf_s, stb_s, Gs, earg, Ee, WT, Btok = (PA[k] for k in ("stf", "stb", "hn", "stf_s", "stb_s", "Gs", "earg", "Ee", "WT", "Btok"))
        prologue(cfg, xt)
        if limit.get("stopA", 99) <= 0:
            return
        inproj_conv(cfg, l, OFF_XBC, 12, cbrow, I["st_sc"][l], O["ncs_p"], O["ncs_s"], 0, is_last, first)
        if ti == 0:
            DBG("fT_A", fT, fT[:, :, :T], [128, 12, T])
        if limit.get("stopA", 99) <= 1:
            return
        b = pget()
        for k in range(8):
            MM(b[:T, 0:16], hT[:, k, :T], PA["WB"][:, k, OFF_DT:OFF_DT + 16], [hT, PA["WB"]], [b], start=(k == 0), stop=(k == 7))
        A_, B_, C_, D_, E_ = (sm[k] for k in "abcde")
        TT(A_[:T, 0:16], b[:T, 0:16], vec16[:T, 0, :], ALU.add, [b, vec16], [A_])
        ACT(A_[:T, 0:16], A_[:T, 0:16], AF.Exp, [A_], [A_])
        ACT(A_[:T, 16:32], A_[:T, 0:16], AF.Ln, [A_], [A_], bias=1.0)
        TT(A_[:T, 32:48], A_[:T, 16:32], vec16[:T, 1, :], ALU.mult, [A_, vec16], [A_])
        cum_and_tot(cfg, A_[:T, 32:48], A_, 16, B_, dbc, 128)
        TT(C_[:T, 0:16], B_[:T, 16:32], B_[:T, 0:16], ALU.subtract, [B_], [C_])
        ACT(C_[:T, 0:16], C_[:T, 0:16], AF.Exp, [C_], [C_])
        ACT(C_[:T, 16:32], B_[:T, 0:16], AF.Exp, [B_], [C_])
        b = pget()
        TR(b[:16, 0:T], B_[:T, 0:16], identf[:T, :T], [B_, identf], [b])
        ACT(acT[:, :T], b[:16, 0:T], AF.Copy, [b], [acT])
        if ti == 0:
            DBG("acum", B_, B_[:T, 0:32], [T, 32])
        if limit.get("stopA", 99) <= 2:
            return
        bx, bb = pget(), pget()
        for c in range(8):
            TR(pb(bx, [8, 128], p=T)[:, c, :], fT[:, c, :T], identb[:, :], [fT, identb], [bx])
        for c in range(2):
            TR(pb(bb, [2, 128], p=T)[:, c, :], fT[:, 8 + c, :T], identb[:, :], [fT, identb], [bb])
        xc, xdec, xsD = bigb[0], bigb[1], big[0]
        TT(xc[:T, :].rearrange("p (h q) -> p h q", q=64), pb(bx, [16, 64], p=T), bl(A_[:T, 16:32], 64), ALU.mult, [bx, A_], [xc])
        TT(xsD[:T, :].rearrange("p (h q) -> p h q", q=64), pb(bx, [16, 64], p=T), bl(vec16[:T, 2, :], 64), ALU.mult, [bx, vec16], [xsD])
        TT(xdec[:T, :].rearrange("p (h q) -> p h q", q=64), xc[:T, :].rearrange("p (h q) -> p h q", q=64), bl(C_[:T, 0:16], 64), ALU.mult,
           [xc, C_], [xdec], eng="pool")
        ACT(Btok[:T, :], pb(bb, [256], p=T), AF.Copy, [bb], [Btok])
        b = pget()
        for g in range(2):
            MM(pf(b, [2, T], p=T)[:, g, :], fT[:, 8 + g, :T], fT[:, 10 + g, :T], [fT], [b])
        ACT(Gs[:T, :, :T], pf(b, [2, T], p=T), AF.Copy, [b], [Gs])
        for q in range(4):
            b = pget()
            for j in range(4):
                h = q * 4 + j
                MM(pf(b, [4, T], p=T)[:, j, :], identf[0:16, h:h + 1].to_broadcast([16, T]), acT[:, :T], [identf, acT], [b])
            for j in range(4):
                h = q * 4 + j
                STT(earg[:T, j, :T], pf(b, [4, T], p=T)[:, j, :], B_[:T, h:h + 1], cfg.nmI[:, :], ALU.subtract, ALU.add,
                    [b, B_, cfg.nmI], [earg])
            ACT(Ee[:T, :, :T], earg[:T, :, :T], AF.Exp, [earg], [Ee])
            TT(WT[:T, q * 4:q * 4 + 4, :T], Ee[:T, :, :T], bm(Gs[:T, q // 2, :T], 4), ALU.mult, [Ee, Gs], [WT])
        if limit.get("stopA", 99) <= 3:
            return
        yp = [pget(), pget()]
        for h in range(16):
            MM(yp[h // 8][:T, (h % 8) * 64:(h % 8 + 1) * 64], WT[:T, h, :T], xc[:T, h * 64:(h + 1) * 64], [WT, xc], [yp[h // 8]])
        ppin(yp[0]); ppin(yp[1])
        yo = [pget(), pget()]
        ppin(yo[0]); ppin(yo[1])
        if cfg.chain:
            if first:
                MSET(stf[:, :], 0.0, [stf])
                MSET(stb[:, :], 0.0, [stb])
            for c in range(nch):
                ssd_state_update(cfg, c, stf, stb, stf, yo, xdec, c == 0, c == nch - 1)
                ACT(stb[:, :], stf[:, :], AF.Copy, [stf], [stb])
            if is_last:
                for half in range(2):
                    b = pget()
                    for k in range(4):
                        kk = half * 4 + k
                        TR(pf(b, [4, 128])[:, k, :], stf[:, :].rearrange("n (j k) -> n k j", k=8)[:, kk, :], identf[:, :], [stf, identf], [b])
                    CP_(hn[:, half * 4:half * 4 + 4, :], pf(b, [4, 128]), [b], [hn])
                STO(O["nssm_p"][l].rearrange("(p k) n -> p k n", k=8), hn[:], hn)
        else:
            for c in range(limit.get("nchs", nch)):
                LD(hn[:], I["st_ssm"][l, c].rearrange("h q n -> (h q) n").rearrange("(p k) n -> p k n", k=8), hn)
                if limit.get("sub", 9) <= 0:
                    continue
                for half in range(2):
                    b = pget()
                    for k in range(4):
                        kk = half * 4 + k
                        TR(pf(b, [4, 128])[:, k, :], hn[:, kk, :], identf[:, :], [hn, identf], [b])
                    if limit.get("sub", 9) <= 1:
                        continue
                    ACT(stf_s[:, :].rearrange("n (j k) -> n k j", k=8)[:, half * 4:half * 4 + 4, :], pf(b, [4, 128]), AF.Copy, [b], [stf_s])
                    if limit.get("sub", 9) <= 2:
                        continue
                    CP_(stb_s[:, :].rearrange("n (j k) -> n k j", k=8)[:, half * 4:half * 4 + 4, :], pf(b, [4, 128]), [b], [stb_s])
                if limit.get("noupd"):
                    continue
                ssd_state_update(cfg, c, stf_s, stb_s, stf_s, yo, xdec, c == 0, c == limit.get("nchs", nch) - 1)
                if limit.get("noback"):
                    continue
                for half in range(2):
                    b = pget()
                    for k in range(4):
                        kk = half * 4 + k
                        TR(pf(b, [4, 128])[:, k, :], stf_s[:, :].rearrange("n (j k) -> n k j", k=8)[:, kk, :], identf[:, :], [stf_s, identf], [b])
                    CP_(hn[:, half * 4:half * 4 + 4, :], pf(b, [4, 128]), [b], [hn])
                STO(O["nssm_s"][l, c].rearrange("(p k) n -> p k n", k=8), hn[:], hn)
        if limit.get("stopA", 99) <= 4:
            punpin(yp[0]); punpin(yp[1]); punpin(yo[0]); punpin(yo[1])
            return
        t1, t2 = big[1], big[2]
        for g in range(2):
            TT(t1[:T, g * 512:(g + 1) * 512].rearrange("p (h q) -> p h q", q=64), yo[g][:T, :].rearrange("p (h q) -> p h q", q=64),
               bl(C_[:T, 16 + g * 8:16 + g * 8 + 8], 64), ALU.mult, [yo[g], C_], [t1])
            TT(t2[:T, g * 512:(g + 1) * 512], yp[g][:T, :], t1[:T, g * 512:(g + 1) * 512], ALU.add, [yp[g], t1], [t2])
        for x_ in yp + yo:
            punpin(x_)
        TT(t2[:T, :], t2[:T, :], xsD[:T, :], ALU.add, [t2, xsD], [t2], eng="pool")
        if ti == 0:
            DBG("yssd", t2, t2[:T, :], [T, D])
        for g in range(2):
            b = pget()
            for k in range(8):
                MM(b[:T, :], hT[:, k, :T], PA["WB"][:, k, OFF_ZS + g * 512:OFF_ZS + (g + 1) * 512], [hT, PA["WB"]], [b], start=(k == 0), stop=(k == 7))
            ACT(t1[:T, g * 512:(g + 1) * 512], b[:T, :], AF.Silu, [b], [t1])
        TT(t2[:T, :], t2[:T, :], t1[:T, :], ALU.mult, [t2, t1], [t2])
        for g in range(2):
            ACT(junk[:T, g * 512:(g + 1) * 512], t2[:T, g * 512:(g + 1) * 512], AF.Square, [t2], [junk, sm1], accum_out=sm1[:T, 2 + g:3 + g])
        TS(sm1[:T, 2:4], sm1[:T, 2:4], 1.0 / 512, EPS, ALU.mult, ALU.add, [sm1], [sm1])
        ACT(sm1[:T, 2:4], sm1[:T, 2:4], AF.Ln, [sm1], [sm1])
        ACT(sm1[:T, 2:4], sm1[:T, 2:4], AF.Exp, [sm1], [sm1], scale=-0.5)
        ynb = bigb[2]
        for g in range(2):
            ACT(ynb[:T, g * 512:(g + 1) * 512], t2[:T, g * 512:(g + 1) * 512], AF.Copy, [t2, sm1], [ynb], scale=sm1[:T, 2 + g:3 + g])
        out_proj(cfg, ynb, 8)

    def out_proj(cfg, ynb, nk):
        T = cfg.T
        b = pget()
        for c in range(nk):
            TR(pb(b, [8, T])[:, c, :], ynb[:T, c * 128:(c + 1) * 128], identb[:T, :T], [ynb, identb], [b])
        CP_(hT[:, 0:nk, :T], pb(b, [8, T])[:, 0:nk, :], [b], [hT])
        for half in range(2):
            b = pget()
            for k in range(nk):
                MM(b[:T, :], hT[:, k, :T], WO[:, k, half * 512:(half + 1) * 512], [hT, WO], [b], start=(k == 0), stop=(k == nk - 1))
            TT(big[3][:T, half * 512:(half + 1) * 512], b[:T, :], gate_bc[cfg.name][:T, half * 512:(half + 1) * 512], ALU.mult,
               [b, gate_bc[cfg.name]], [big[3]])

    PB = {}

    def alloc_B(pes, tag):
        def sbp(name, shape, dt=F32):
            return Buf(pes.enter_context(nc.sbuf_tensor(name + tag, list(shape), dt)), name)
        PB.update(Sp_f=sbp("Sp_f", [128, 4, 128]), Sp_b=sbp("Sp_b", [128, 4, 128], BF16), Ss_in=sbp("Ss_in", [128, 16, 128]),
                  Ss_b=sbp("Ss_b", [128, 16, 128], BF16), kv=sbp("kv", [128, 8, 128], BF16), kbg=sbp("kbg", [128, 4, 128]),
                  vb_=sbp("vb", [128, 4, 128]), kdec=sbp("kdec", [128, 4, 128], BF16), kdm=sbp("kdm", [64, 2, 128], BF16),
                  LG=sbp("LG", [128, 12]), LGT=sbp("LGT", [12, 128]), ea=sbp("ea", [128, 4, 128]),
                  E2a=sbp("E2a", [128, 4, 128]), E2b=sbp("E2b", [128, 4, 128]),
                  X0=sbp("X0", [128, 4, 128]), X1=sbp("X1", [128, 4, 128]), Y0=sbp("Y0", [128, 4, 128]), Y1=sbp("Y1", [128, 4, 128]),
                  R0=sbp("R0", [128, 4, 128]), R1=sbp("R1", [128, 4, 128]), attnT=sbp("attnT", [128, 4, 128], BF16),
                  egb=sbp("egb", [128, 4, 128]), qsq=sbp("qsq", [128, 4, 128], BF16),
                  nwTm=sbp("nwTm", [128, 1024], BF16), qdm=sbp("qdm", [128, 1024], BF16), u_sb=sbp("u_sb", [128, 4, 128]),
                  vnew=sbp("vnew", [128, 4, 128], BF16), dbcS=sbp("dbcS", [128, 64]), WB=sbp("WB", [128, 8, 2056], BF16))
        PB["E2"] = None

    def r32(ap):
        return ap.bitcast(F32R)

    def phaseB_tile(cfg, l, ti, xt, h0, first, is_last):
        T, L, nseq, nch = cfg.T, cfg.L, cfg.nseq, cfg.nch
        (Sp_f, Sp_b, Ss_in, Ss_b, kv, kbg, vb_, kdec, kdm, LG, LGT, ea, attnT, egb, nwTm, qdm, u_sb, vnew, dbcS) = (
            PB[k] for k in ("Sp_f", "Sp_b", "Ss_in", "Ss_b", "kv", "kbg", "vb_", "kdec", "kdm", "LG", "LGT", "ea", "attnT",
                            "egb", "nwTm", "qdm", "u_sb", "vnew", "dbcS"))
        E2 = [PB["E2a"], PB["E2b"]]
        Ss_out = Ss_in
        Xs, Ys, Rs = [PB["X0"], PB["X1"]], [PB["Y0"], PB["Y1"]], [PB["R0"], PB["R1"]]
        WB = PB["WB"]
        prologue(cfg, xt)
        inproj_conv_B(cfg, l, h0, is_last, first)
        A_, B_, C_, D_, E_ = (sm[k] for k in "abcde")
        b = pget()
        for k in range(8):
            MM(b[:T, 0:8], hT[:, k, :T], WB[:, k, 2048:2056], [hT, WB], [b], start=(k == 0), stop=(k == 7))
        TT(A_[:T, 0:4], b[:T, 0:4], vec8[:T, 0, h0:h0 + 4], ALU.add, [b, vec8], [A_])
        ACT(A_[:T, 0:4], A_[:T, 0:4], AF.Exp, [A_], [A_])
        ACT(A_[:T, 0:4], A_[:T, 0:4], AF.Ln, [A_], [A_], bias=1.0)
        TT(A_[:T, 4:8], A_[:T, 0:4], vec8[:T, 1, h0:h0 + 4], ALU.mult, [A_, vec8], [A_])
        ACT(A_[:T, 8:12], b[:T, 4:8], AF.Exp, [b], [A_], scale=-1.0)
        ACT(A_[:T, 8:12], A_[:T, 8:12], AF.Ln, [A_], [A_], bias=1.0)
        ACT(A_[:T, 12:16], A_[:T, 8:12], AF.Exp, [A_], [A_], scale=-1.0)
        ACT(A_[:T, 16:20], A_[:T, 8:12], AF.Copy, [A_], [A_], scale=-1.0)
        cum_and_tot(cfg, A_[:T, 4:8], A_, 4, B_, dbcS, 128)
        ACT(C_[:T, 0:4], B_[:T, 0:4], AF.Exp, [B_], [C_])
        TT(C_[:T, 4:8], B_[:T, 4:8], B_[:T, 0:4], ALU.subtract, [B_], [C_])
        ACT(C_[:T, 4:8], C_[:T, 4:8], AF.Exp, [C_], [C_])
        b = pget()
        for c in range(8):
            TR(pb(b, [8, 128], p=T)[:, c, :], fT[:, 4 + c, :T], identb[:, :], [fT, identb], [b])
        ACT(kv[:T, :, :], pb(b, [8, 128], p=T), AF.Copy, [b], [kv])
        ksq = big[0]
        TT(ksq[:T, 0:512].rearrange("p (h d) -> p h d", d=128), kv[:T, 0:4, :], kv[:T, 0:4, :], ALU.mult, [kv], [ksq])
        RED(D_[:T, 0:4], ksq[:T, 0:512].rearrange("p (h d) -> p h d", d=128), [ksq], [D_])
        TS(D_[:T, 0:4], D_[:T, 0:4], EPS, None, ALU.add, ALU.bypass, [D_], [D_])
        ACT(D_[:T, 4:8], D_[:T, 0:4], AF.Ln, [D_], [D_])
        ACT(D_[:T, 8:12], D_[:T, 4:8], AF.Exp, [D_], [D_], scale=-0.5)
        qsq = PB["qsq"]
        ACT(qsq[:, :, :T], fT[:, 0:4, :T], AF.Square, [fT], [qsq])
        b = pget()
        for j in range(4):
            MM(b[:T, j:j + 1], qsq[:, j, :T], onesb[:, 0:1], [qsq, onesb], [b])
        TS(D_[:T, 12:16], b[:T, 0:4], EPS, None, ALU.add, ALU.bypass, [b], [D_])
        ACT(D_[:T, 12:16], D_[:T, 12:16], AF.Ln, [D_], [D_])
        ACT(D_[:T, 16:20], D_[:T, 12:16], AF.Exp, [D_], [D_], scale=-0.5)
        TS(D_[:T, 16:20], D_[:T, 16:20], float(128 ** -0.5), None, ALU.mult, ALU.bypass, [D_], [D_])
        TT(E_[:T, 0:4], D_[:T, 8:12], A_[:T, 12:16], ALU.mult, [D_, A_], [E_])
        TT(E_[:T, 4:8], E_[:T, 0:4], C_[:T, 0:4], ALU.mult, [E_, C_], [E_])
        TT(E_[:T, 8:12], D_[:T, 8:12], C_[:T, 4:8], ALU.mult, [D_, C_], [E_])
        TS(E_[:T, 12:16], E_[:T, 0:4], -1.0, None, ALU.mult, ALU.bypass, [E_], [E_])
        TS(E_[:T, 16:20], D_[:T, 8:12], -1.0, None, ALU.mult, ALU.bypass, [D_], [E_])
        CP_(LG[:T, 0:4], B_[:T, 0:4], [B_], [LG])
        STT(LG[:T, 4:8], D_[:T, 4:8], -0.5, B_[:T, 0:4], ALU.mult, ALU.subtract, [D_, B_], [LG])
        STT(LG[:T, 8:12], D_[:T, 4:8], -0.5, B_[:T, 0:4], ALU.mult, ALU.add, [D_, B_], [LG])
        TT(LG[:T, 8:12], LG[:T, 8:12], A_[:T, 16:20], ALU.add, [LG, A_], [LG])
        b = pget()
        TR(b[:12, 0:T], LG[:T, 0:12], identf[:T, :T], [LG, identf], [b])
        ACT(LGT[:, :T], b[:12, 0:T], AF.Copy, [b], [LGT])
        TT(r32(kbg[:T, :, :]), kv[:T, 0:4, :], bl(E_[:T, 4:8], 128), ALU.mult, [kv, E_], [kbg])
        TT(kdec[:T, :, :], kv[:T, 0:4, :], bl(E_[:T, 8:12], 128), ALU.mult, [kv, E_], [kdec])
        TT(r32(vb_[:T, :, :]), kv[:T, 4:8, :], bl(A_[:T, 12:16], 128), ALU.mult, [kv, A_], [vb_])
        bG, bA = pget(), pget()
        for j in range(4):
            MM(pf(bG, [4, T], p=T)[:, j, :], fT[:, 4 + j, :T], fT[:, 4 + j, :T], [fT], [bG])
        for j in range(4):
            MM(pf(bA, [4, T], p=T)[:, j, :], fT[:, 4 + j, :T], fT[:, j, :T], [fT], [bA])
        bcs = [pget(), pget(), pget()]
        for r in range(3):
            for j in range(4):
                MM(pf(bcs[r], [4, T], p=T)[:, j, :], identf[0:12, r * 4 + j:r * 4 + j + 1].to_broadcast([12, T]), LGT[:, :T],
                   [identf, LGT], [bcs[r]])
        be = pget()
        for j in range(4):
            MM(pf(be, [4, T])[:, j, :], identf[0:12, j:j + 1].to_broadcast([12, 128]), LGT[:, :T], [identf, LGT], [be])
        X, Y, R = Xs[0], Ys[0], Rs[0]
        kinds = ((1, ALU.add, cfg.nmSr, 0), (2, ALU.subtract, cfg.nmS, 1), (0, ALU.subtract, cfg.nmI, 2))
        for r, op0, msk, ki in kinds:
            for j in range(4):
                STT(ea[:T, j, :T], pf(bcs[r], [4, T], p=T)[:, j, :], B_[:T, j:j + 1], msk[:, :], op0, ALU.add, [bcs[r], B_, msk], [ea])
            Ek = E2[ki % 2]
            ACT(Ek[:T, :, :T], ea[:T, :, :T], AF.Exp, [ea], [Ek])
            for j in range(4):
                if ki == 0:
                    STT(r32(X[:T, j, :T]), pf(bG, [4, T], p=T)[:, j, :], E_[:T, 12 + j:13 + j], Ek[:T, j, :T], ALU.mult, ALU.mult, [bG, E_, Ek], [X])
                elif ki == 1:
                    STT(r32(Y[:T, j, :T]), pf(bG, [4, T], p=T)[:, j, :], E_[:T, 16 + j:17 + j], Ek[:T, j, :T], ALU.mult, ALU.mult, [bG, E_, Ek], [Y])
                else:
                    STT(attnT[:T, j, :T], pf(bA, [4, T], p=T)[:, j, :], D_[:T, 8 + j:9 + j], Ek[:T, j, :T], ALU.mult, ALU.mult, [bA, D_, Ek], [attnT])
        ACT(egb[:, :, :T], pf(be, [4, T]), AF.Exp, [be], [egb])
        TT(r32(R[:T, :, :T]), Y[:T, :, :T], bm(identf[:T, :T], 4), ALU.add, [Y, identf], [R])
        for lev in range(cfg.nlev):
            X2, Y2, R2 = Xs[(lev + 1) % 2], Ys[(lev + 1) % 2], Rs[(lev + 1) % 2]
            lastlev = (lev == cfg.nlev - 1)
            bX, bY, bR = pget(), (None if lastlev else pget()), pget()
            for j in range(4):
                MM(pf(bX, [4, T], p=T)[:, j, :], r32(Y[:T, j, :T]), r32(X[:T, j, :T]), [X, Y], [bX])
            if not lastlev:
                for j in range(4):
                    MM(pf(bY, [4, T], p=T)[:, j, :], r32(X[:T, j, :T]), r32(Y[:T, j, :T]), [X, Y], [bY])
            ACT(r32(X2[:T, :, :T]), pf(bX, [4, T], p=T), AF.Copy, [bX], [X2])
            if not lastlev:
                CP_(r32(Y2[:T, :, :T]), pf(bY, [4, T], p=T), [bY], [Y2])
            for j in range(4):
                MM(pf(bR, [4, T], p=T)[:, j, :], r32(X2[:T, j, :T]), r32(R[:T, j, :T]), [X2, R], [bR])
            TT(r32(R2[:T, :, :T]), R[:T, :, :T], pf(bR, [4, T], p=T), ALU.add, [R, bR], [R2])
            X, Y, R = X2, Y2, R2
        bU, bW = pget(), pget()
        for j in range(4):
            MM(pf(bU, [4, 128], p=T)[:, j, :], r32(R[:T, j, :T]), r32(vb_[:T, j, :]), [R, vb_], [bU])
        for j in range(4):
            MM(pf(bW, [4, T])[:, j, :], r32(kbg[:T, j, :]), r32(R[:T, j, :T]), [kbg, R], [bW])
        ACT(u_sb[:T, :, :], pf(bU, [4, 128], p=T), AF.Copy, [bU], [u_sb])
        TT(egb[:, :, :T], fT[:, 0:4, :T], egb[:, :, :T], ALU.mult, [fT, egb], [egb])
        vnb, ob = pget(), pget()
        ppin(vnb)
        ppin(ob)
        vn4, o4 = pf(vnb, [4, 128], p=T), pf(ob, [4, 128], p=T)
        if cfg.chain:
            if first:
                MSET(Sp_f[:], 0.0, [Sp_f])
                MSET(Sp_b[:], 0.0, [Sp_b])
            nwv = nwTm[:, 0:nch * 4 * T].rearrange("p (c j t) -> p c j t", c=nch, j=4)
            qdv = qdm[:, 0:nch * 4 * T].rearrange("p (c j t) -> p c j t", c=nch, j=4)
            for c in range(nch):
                TT(nwv[:, c], pf(bW, [4, T]), bm(cfg.ncmf[:, c, :], 4), ALU.mult, [bW, cfg.ncmf], [(nwTm, c)])
                TT(qdv[:, c], egb[:, :, :T], bm(cfg.cmf[:, c, :], 4), ALU.mult, [egb, cfg.cmf], [(qdm, c)], eng="pool")
            for c in range(nch):
                for j in range(4):
                    MM(vn4[:, j, :], nwv[:, c, j, :], Sp_b[:, j, :], [(nwTm, c), Sp_b], [vnb], start=(j == 0 and c == 0), stop=False, skip_group_check=True)
                for j in range(4):
                    MM(o4[:, j, :], qdv[:, c, j, :], Sp_b[:, j, :], [(qdm, c), Sp_b], [ob], start=(j == 0 and c == 0), stop=False, skip_group_check=True)
                TT(vnew[:T, :, :], vn4, u_sb[:T, :, :], ALU.add, [vnb, u_sb], [vnew])
                bs = pget()
                for j in range(4):
                    MM(pf(bs, [4, 128])[:, j, :], kdec[c * 64:(c + 1) * 64, j, :], vnew[c * 64:(c + 1) * 64, j, :], [kdec, vnew], [bs])
                tmpS = big[0][:, 512:1024].rearrange("p (j v) -> p j v", v=128)
                TT(tmpS, Sp_f[:, :, :], bl(dbcS[:, c * 4:(c + 1) * 4], 128), ALU.mult, [Sp_f, dbcS], [big[0]])
                TT(Sp_f[:, :, :], tmpS, pf(bs, [4, 128]), ALU.add, [big[0], bs], [Sp_f])
                ACT(Sp_b[:, :, :], Sp_f[:, :, :], AF.Copy, [Sp_f], [Sp_b])
            if is_last:
                STO(O["ngdn_p"][l, h0:h0 + 4].rearrange("h k v -> k h v"), Sp_f[:], Sp_f)
        else:
            nwv = nwTm[:, 0:nch * T].rearrange("p (c t) -> p c t", t=T)
            qdv = qdm[:, 0:nch * T].rearrange("p (c t) -> p c t", t=T)
            ppin(bW)
            for j in range(4):
                h = h0 + j
                LD(Ss_in[:], I["st_gdn"][l, :, h].rearrange("b k v -> k b v"), Ss_in)
                CP_(Ss_b[:], Ss_in[:], [Ss_in], [Ss_b], eng="pool")
                TT(nwv, bm(pf(bW, [4, T])[:, j, :], nch), cfg.ncmf[:, :, :], ALU.mult, [bW, cfg.ncmf], [nwTm])
                TT(qdv, bm(egb[:, j, :T], nch), cfg.cmf[:, :, :], ALU.mult, [egb, cfg.cmf], [qdm], eng="pool")
                for c in range(nch):
                    MM(vn4[:, j, :], nwv[:, c, :], Ss_b[:, c, :], [nwTm, Ss_b], [vnb], start=(j == 0 and c == 0), stop=False, skip_group_check=True)
                for c in range(nch):
                    MM(o4[:, j, :], qdv[:, c, :], Ss_b[:, c, :], [qdm, Ss_b], [ob], start=(j == 0 and c == 0), stop=False, skip_group_check=True)
                TT(vnew[:T, j, :], vn4[:, j, :], u_sb[:T, j, :], ALU.add, [vnb, u_sb], [vnew])
                tmpS = big[2][:, :].rearrange("p (c v) -> p c v", v=128)
                for hf in range(2):
                    TT(tmpS, Ss_in[:, hf * 8:(hf + 1) * 8, :],
                       bl(dbcS[:, 0:nch * 4].rearrange("p (c j) -> p c j", j=4)[:, hf * 8:(hf + 1) * 8, j], 128), ALU.mult, [Ss_in, dbcS], [big[2]])
                    for c4 in range(hf * 8, hf * 8 + 8, 4):
                        bs = pget()
                        for cc in range(4):
                            c = c4 + cc
                            sl = mslot[0] % 2
                            mslot[0] += 1
                            TS(kdm[:T, sl, :], kdec[:T, j, :], cfg.cmT[:, c:c + 1], 1.0, ALU.mult, ALU.mult, [kdec, cfg.cmT], [(kdm, sl)], eng="pool")
                            MM(pf(bs, [4, 128])[:, cc, :], kdm[:T, sl, :], vnew[:T, j, :], [(kdm, sl), vnew], [bs])
                        TT(Ss_out[:, c4:c4 + 4, :], tmpS[:, c4 - hf * 8:c4 - hf * 8 + 4, :], pf(bs, [4, 128]), ALU.add, [big[2], bs], [Ss_out])
                STO(O["ngdn_s"][l, :, h].rearrange("b k v -> k b v"), Ss_out[:], Ss_out)
            punpin(bW)
        for j in range(4):
            MM(o4[:, j, :], attnT[:T, j, :T], vnew[:T, j, :], [attnT, vnew], [ob], start=False, stop=True, skip_group_check=True)
        punpin(vnb)
        of = big[1]
        ACT(of[:T, 0:512], ob[:T, :], AF.Copy, [ob], [of])
        punpin(ob)
        if ti == 0 and h0 == 0:
            DBG("ogdn", of, of[:T, 0:512], [T, 512])
        TT(ksq[:T, 0:512], of[:T, 0:512], of[:T, 0:512], ALU.mult, [of], [ksq])
        RED(C_[:T, 8:12], ksq[:T, 0:512].rearrange("p (h d) -> p h d", d=128), [ksq], [C_])
        TT(C_[:T, 12:16], D_[:T, 16:20], D_[:T, 16:20], ALU.mult, [D_], [C_])
        TT(C_[:T, 8:12], C_[:T, 8:12], C_[:T, 12:16], ALU.mult, [C_], [C_])
        TS(C_[:T, 8:12], C_[:T, 8:12], 1.0 / 128, EPS, ALU.mult, ALU.add, [C_], [C_])
        ACT(C_[:T, 8:12], C_[:T, 8:12], AF.Ln, [C_], [C_])
        ACT(C_[:T, 8:12], C_[:T, 8:12], AF.Exp, [C_], [C_], scale=-0.5)
        TT(C_[:T, 8:12], C_[:T, 8:12], D_[:T, 16:20], ALU.mult, [C_, D_], [C_])
        b = pget()
        for k in range(8):
            MM(b[:T, :], hT[:, k, :T], WB[:, k, 0:512], [hT, WB], [b], start=(k == 0), stop=(k == 7))
        sz = big[2]
        ACT(sz[:T, 0:512], b[:T, :], AF.Silu, [b], [sz])
        TT(of[:T, 0:512].rearrange("p (h d) -> p h d", d=128), of[:T, 0:512].rearrange("p (h d) -> p h d", d=128), bl(C_[:T, 8:12], 128),
           ALU.mult, [of, C_], [of])
        onb = bigb[2]
        TT(onb[:T, 0:512], of[:T, 0:512], sz[:T, 0:512], ALU.mult, [of, sz], [onb])
        out_proj(cfg, onb, 4)

    def inproj_conv_B(cfg, l, h0, is_last, first):
        T, L, nseq = cfg.T, cfg.L, cfg.nseq
        rawv = raw[:, :, 0:nseq * (L + 3)].rearrange("p c (b l) -> p c b l", l=L + 3)
        if nseq == 1:
            if first:
                MSET(raw[:, :, 0:3], 0.0, [raw])
            else:
                CP_(raw[:, :, 0:3], raw[:, :, L:L + 3], [raw], [raw])
        else:
            for part in range(3):
                cs0 = part * 1024 + h0 * 128
                LD(cst_in[:, part * 512:(part + 1) * 512], I["st_gc"][l][:, :, cs0:cs0 + 512].rearrange("b k c -> (b k) c"), cst_in)
            for g0 in (0, 8):
                b = pget()
                ng = min(8, 12 - g0)
                for c in range(ng):
                    TR(pf(b, [8, 48])[:, c, :], cst_in[:, (g0 + c) * 128:(g0 + c + 1) * 128], identf[:48, :48], [cst_in, identf], [b])
                CP_(rawv[:, g0:g0 + ng, :, 0:3], pf(b, [8, 48])[:, 0:ng, :].rearrange("p c (b k) -> p c b k", k=3), [b], [raw])
        for g0 in range(0, 12, 4):
            b = pget()
            for c in range(4):
                for k in range(8):
                    MM(pf(b, [4, T])[:, c, :], PB["WB"][:, k, 512 + (g0 + c) * 128: 512 + (g0 + c + 1) * 128], hT[:, k, :T],
                       [PB["WB"], hT], [b], start=(k == 0), stop=(k == 7))
            if nseq == 1:
                ACT(raw[:, g0:g0 + 4, 3:3 + L], pf(b, [4, T]), AF.Copy, [b], [raw])
            else:
                for c in range(4):
                    ACT(rawv[:, g0 + c, :, 3:3 + L], pf(b, [4, T])[:, c, :].rearrange("p (b l) -> p b l", l=L), AF.Copy, [b], [raw])
            if is_last and limit.get("tail", True):
                for c in range(4):
                    ACT(tailf[:, g0 + c, 0:nseq * 3].rearrange("p (b k) -> p b k", k=3),
                        pf(b, [4, T])[:, c, :].rearrange("p (b l) -> p b l", l=L)[:, :, L - 3:L], AF.Copy, [b], [tailf])
        for g0 in range(0, 12, 4):
            b = pget()
            for c in range(4):
                o = pf(b, [4, T])[:, c, :]
                if nseq > 1:
                    o = o.rearrange("p (b l) -> p b l", l=L)
                for k in range(4):
                    MM(o, DIAG[:, g0 + c, k, :], rawv[:, g0 + c, :, k:k + L] if nseq > 1 else raw[:, g0 + c, k:k + L],
                       [DIAG, raw], [b], start=(k == 0), stop=(k == 3))
            ACT(fT[:, g0:g0 + 4, :T], pf(b, [4, T]), AF.Silu, [b], [fT])
        if is_last:
            n3 = nseq * 3
            for g0 in range(0, 12, 4):
                b = pget()
                for c in range(4):
                    TR(pf(b, [4, 128], p=n3)[:, c, :], tailf[:, g0 + c, 0:n3], identf[:, :], [tailf, identf], [b])
                CP_(tailo[:n3, g0 * 128:(g0 + 4) * 128], b[:n3, :], [b], [tailo])
            dst = O["ngc_p"] if nseq == 1 else O["ngc_s"]
            for part in range(3):
                cs0 = part * 1024 + h0 * 128
                STO(dst[l][:, cs0:cs0 + 512], tailo[:n3, part * 512:(part + 1) * 512], tailo)

    def bvec(dst_ap, src_row_ap, buf):
        LD(dst_ap, src_row_ap.partition_broadcast(128), buf)

    tiles = [(CP, ti) for ti in range(limit.get("ntiles", NT))] + ([(CS, NT)] if limit.get("sample", True) else [])
    all_bufs.extend(allb)
    layer_in = None
    for l in range(limit.get("layers", DEPTH)):
        do_mod(l)
        S.barrier(all_bufs)
        mod_stacks.pop().close()
        for ph in limit.get("phases", (0, 1, 2)):
            acc_in = layer_in if ph == 0 else scr[(ph - 1)]
            last_phase = (l == DEPTH - 1 and ph == 2)
            acc_out = scr[ph] if ph < 2 else (scr[2] if not last_phase else None)
            S.barrier(all_bufs)
            pes = contextlib.ExitStack()
            cur_es[0] = pes
            if ph == 0:
                alloc_A(pes, "_%d_%d" % (l, ph))
            else:
                alloc_B(pes, "_%d_%d" % (l, ph))
            for v_ in list(PA.values()) + list(PB.values()):
                if v_ is not None and v_ not in all_bufs:
                    all_bufs.append(v_)
            if ph == 0:
                load_cols(PA["WB"], 0, I["w_in"][l][:, 0:2576], 2576)
                LD(nwT[:], I["ssd_norm_w"][l].rearrange("(k p) -> p k", p=128), nwT, nonc=True)
                load_wo(I["w_out"][l][0:1024, :], 8, lambda k: nwT[:, k:k + 1])
                build_diag(I["ssd_conv_w"][l], 0, 12)
                LD(cbrow_f[0:1, :], I["ssd_conv_b"][l:l + 1, :], cbrow_f)
                CP_(cbrow[:], cbrow_f[0:1, :], [cbrow_f], [cbrow])
                bvec(vec16[:, 0, :], I["ssd_dt_bias"][l:l + 1, :], vec16)
                bvec(vec16[:, 1, :], I["ssd_a_log"][l:l + 1, :], vec16)
                bvec(vec16[:, 2, :], I["ssd_d"][l:l + 1, :], vec16)
                ACT(vec16[:, 1, :], vec16[:, 1, :], AF.Exp, [vec16], [vec16])
                TS(vec16[:, 1, :], vec16[:, 1, :], -1.0, None, ALU.mult, ALU.bypass, [vec16], [vec16])
            else:
                h0 = (ph - 1) * 4
                load_cols(PB["WB"], 0, I["w_in"][l][:, OFF_ZG + h0 * 128:OFF_ZG + h0 * 128 + 512], 512)
                for part in range(3):
                    c0 = OFF_QKV + part * 1024 + h0 * 128
                    load_cols(PB["WB"], 512 + part * 512, I["w_in"][l][:, c0:c0 + 512], 512)
                load_cols(PB["WB"], 2048, I["w_in"][l][:, OFF_A + h0:OFF_A + h0 + 4], 4)
                load_cols(PB["WB"], 2052, I["w_in"][l][:, OFF_B + h0:OFF_B + h0 + 4], 4)
                LD(nwT[:, 0:1], I["gdn_norm_w"][l].rearrange("(p o) -> p o", o=1), nwT, nonc=True)
                load_wo(I["w_out"][l][1024 + h0 * 128:1024 + h0 * 128 + 512, :], 4, lambda k: nwT[:, 0:1])
                for part in range(3):
                    for k_ in range(4):
                        LD(cwT[:, part * 4:part * 4 + 4, k_],
                           I["gdn_conv_w"][l][k_, part * 1024 + h0 * 128:part * 1024 + h0 * 128 + 512].rearrange("(c p) -> p c", p=128), cwT, nonc=True)
                for c in range(12):
                    TT(DIAG[:, c, :, :], bm(identf[:], 4), bl(cwT[:, c, :], 128), ALU.mult, [identf, cwT], [DIAG], eng="pool")
                bvec(vec8[:, 0, :], I["gdn_dt_bias"][l:l + 1, :], vec8)
                bvec(vec8[:, 1, :], I["gdn_a_log"][l:l + 1, :], vec8)
                ACT(vec8[:, 1, :], vec8[:, 1, :], AF.Exp, [vec8], [vec8])
                TS(vec8[:, 1, :], vec8[:, 1, :], -1.0, None, ALU.mult, ALU.bypass, [vec8], [vec8])

            def issue_loads(idx):
                cfg, ti = tiles[idx]
                r0, T = rows(ti)
                xt = xt_slots[idx % 2]
                ap, dep = src_ap(layer_in.t if layer_in is not None else None, ti)
                LD(xt[:T, :], ap, xt, R=[(layer_in, ti)] if layer_in is not None else None)
                if ph > 0:
                    xa = xa_slots[idx % 2]
                    LD(xa[:T, :], acc_in.t[r0:r0 + T, :], xa, R=[(acc_in, ti)])

            if last_phase:
                fnw_bc = stage[0]
                fnw_v = stage[0][:, :, :].rearrange("p a b -> p (a b)")
                LD(fnw_v, I["final_norm_w"].partition_broadcast(128), stage[0])
            issue_loads(0)
            for idx, (cfg, ti) in enumerate(tiles):
                if idx + 1 < len(tiles):
                    issue_loads(idx + 1)
                r0, T = rows(ti)
                xt = xt_slots[idx % 2]
                xa = xa_slots[idx % 2] if ph > 0 else xt
                first = (ti == 0)
                is_last = (ti == NT - 1) or (ti == NT)
                if ph == 0:
                    phaseA_tile(cfg, l, ti, xt, first, is_last)
                else:
                    phaseB_tile(cfg, l, ti, xt, (ph - 1) * 4, first, is_last)
                TT(xo[:T, :], big[3][:T, :], xa[:T, :], ALU.add, [big[3], xa], [xo], eng="pool")
                if not last_phase:
                    STO(acc_out.t[r0:r0 + T, :], xo[:T, :], xo, W=[(acc_out, ti)])
                else:
                    ACT(junk[:T, :], xo[:T, :], AF.Square, [xo], [junk, sm1], accum_out=sm1[:T, 4:5])
                    TS(sm1[:T, 4:5], sm1[:T, 4:5], 1.0 / D, EPS, ALU.mult, ALU.add, [sm1], [sm1])
                    ACT(sm1[:T, 4:5], sm1[:T, 4:5], AF.Ln, [sm1], [sm1])
                    ACT(sm1[:T, 4:5], sm1[:T, 4:5], AF.Exp, [sm1], [sm1], scale=-0.5)
                    STT(big[0][:T, :], xo[:T, :], sm1[:T, 4:5], fnw_v[:T, :], ALU.mult, ALU.mult, [xo, sm1, fnw_bc], [big[0]])
                    dsto = O["y_p"][r0:r0 + T, :] if ti < NT else O["y_s"][:, :]
                    STO(dsto, big[0][:T, :], big[0])
            pes.close()
            cur_es[0] = es
            PA.clear()
            PB.clear()
        layer_in = scr[2]
    if FILL_TABLE and not limit.get("nofill"):
        fill_rhs = CS.cmf[:, :, :].rearrange("p c t -> p (c t)")[:, 0:512]
        S.filler = lambda e: e.matmul(banks[7][:, :], lhsT=identb[:, :], rhs=fill_rhs, start=True, stop=True)
        S.fill_table = {int(kv.split(":")[0]): int(kv.split(":")[1]) for kv in FILL_TABLE.split(",") if kv}
    S.emit()
    build.pe_groups = S.pe_groups
    es.close()
    return nc, dbg_out


_CACHE = {}


def make_in_maps(inputs):
    g = {k: np.ascontiguousarray(np.asarray(v, dtype=np.float32)) for k, v in inputs.items()}
    consts = host_consts()
    maps = []
    for c in range(8):
        m = {}
        m["xp"] = g["x_prompt"][c]
        m["xs"] = g["x_sample"][c * NSB:(c + 1) * NSB].reshape(NSB * SL, D)
        m["call"] = np.concatenate([g["c_prompt"][c:c + 1], g["c_sample"][c * NSB:(c + 1) * NSB]], axis=0)
        m["st_sc"] = g["state_ssd_conv"][:, c * NSB:(c + 1) * NSB]
        m["st_ssm"] = g["state_ssm"][:, c * NSB:(c + 1) * NSB]
        m["st_gc"] = g["state_gdn_conv"][:, c * NSB:(c + 1) * NSB]
        m["st_gdn"] = g["state_gdn"][:, c * NSB:(c + 1) * NSB]
        for k in ("norm_w", "w_ada", "b_ada", "w_in", "ssd_conv_w", "ssd_conv_b", "ssd_dt_bias", "ssd_a_log", "ssd_d",
                  "ssd_norm_w", "gdn_conv_w", "gdn_dt_bias", "gdn_a_log", "gdn_norm_w", "w_out"):
            m[k] = g[k]
        m["final_norm_w"] = g["final_norm_w"].reshape(1, D)
        for k, v in consts.items():
            m["c_" + k] = v
        maps.append({k: np.ascontiguousarray(v) for k, v in m.items()})
    return maps


def kernel(**inputs):
    if "nc" not in _CACHE:
        _CACHE["nc"] = build()[0]
    nc = _CACHE["nc"]
    maps = make_in_maps(inputs)
    res = run_bass_kernel_spmd(nc, maps, core_ids=list(range(8)))
    R = res.results
    cat = lambda k, ax: np.concatenate([np.asarray(r[k]) for r in R], axis=ax)
    y_p = np.stack([np.asarray(r["y_p"]) for r in R], 0)
    y_s = np.stack([np.asarray(r["y_s"]).reshape(NSB, SL, D) for r in R], 0).reshape(8 * NSB, SL, D)
    ncs_p = np.stack([np.asarray(r["ncs_p"]) for r in R], 1)
    nssm_p = np.stack([np.asarray(r["nssm_p"]).reshape(DEPTH, 16, 64, 128) for r in R], 1)
    ngc_p = np.stack([np.asarray(r["ngc_p"]) for r in R], 1)
    ngdn_p = np.stack([np.asarray(r["ngdn_p"]) for r in R], 1)
    ncs_s = np.concatenate([np.asarray(r["ncs_s"]).reshape(DEPTH, NSB, 3, 1536) for r in R], 1)
    nssm_s = np.concatenate([np.asarray(r["nssm_s"]).reshape(DEPTH, NSB, 16, 64, 128) for r in R], 1)
    ngc_s = np.concatenate([np.asarray(r["ngc_s"]).reshape(DEPTH, NSB, 3, 3072) for r in R], 1)
    ngdn_s = np.concatenate([np.asarray(r["ngdn_s"]) for r in R], 1)
    outs = (y_p, y_s, ncs_p, nssm_p, ngc_p, ngdn_p, ncs_s, nssm_s, ngc_s, ngdn_s)
    return tuple(np.ascontiguousarray(o, dtype=np.float32) for o in outs)
```

```python
import contextlib
import numpy as np
import concourse.bass as bass
import concourse.mybir as mybir
from concourse.bass_utils import run_bass_kernel_spmd

F32 = mybir.dt.float32
F32R = mybir.dt.float32r
BF16 = mybir.dt.bfloat16
AF = mybir.ActivationFunctionType
ALU = mybir.AluOpType
AX = mybir.AxisListType

ENGS = ("pe", "dve", "act", "pool", "sp")


class Buf:
    def __init__(self, t, name):
        self.t = t
        self.name = name
        self.st = {}
        self.dma_sem = None
        self.dma_cnt = 0

    def __getitem__(self, k):
        return self.t[k]


class Instr:
    __slots__ = ("eng", "fn", "deps", "signal", "count", "is_dma", "buf_sem", "dma_count")

    def __init__(self, eng, fn):
        self.eng = eng
        self.fn = fn
        self.deps = []
        self.signal = False
        self.count = None
        self.is_dma = False
        self.buf_sem = None
        self.dma_count = None


class Sched:
    def __init__(self, nc):
        self.nc = nc
        self.streams = {e: [] for e in ENGS}
        self.all = []
        self.filler = None
        self.nfill = 0
        self.fill_table = {}
        self.pe_groups = 0

    def _states(self, buf, key):
        st = buf.st
        if key is None:
            if None not in st:
                st[None] = [None, {}]
            return list(st.values())
        if key not in st:
            if None in st:
                st[key] = [st[None][0], dict(st[None][1])]
            else:
                st[key] = [None, {}]
        res = [st[key]]
        if None in st:
            res.append(st[None])
        return res

    @staticmethod
    def _norm(lst):
        out = []
        for x in lst or []:
            out.append((x, None) if isinstance(x, Buf) else x)
        return out

    def add(self, eng, fn, reads=None, writes=None, dma_buf=None):
        ins = Instr(eng, fn)
        ins.is_dma = dma_buf is not None
        reads = self._norm(reads)
        writes = self._norm(writes)
        deps, raw = [], []
        for buf, key in reads:
            for s in self._states(buf, key):
                if s[0] is not None:
                    deps.append(s[0])
                    raw.append(s[0])
                if getattr(buf, "excl", False):
                    for r in s[1].values():
                        if r.eng != eng:
                            deps.append(r)
        for buf, key in writes:
            for s in self._states(buf, key):
                if s[0] is not None:
                    deps.append(s[0])
                deps.extend(s[1].values())
        fdeps = {}
        for d in deps:
            if d is ins:
                continue
            if (not ins.is_dma) and (not d.is_dma) and d.eng == eng:
                if eng == "pe" or not any(d is r for r in raw):
                    continue
            if d.is_dma:
                fdeps[id(d)] = (d, d.buf_sem.dma_cnt * 16)
            else:
                fdeps[id(d)] = (d, None)
        ins.deps = list(fdeps.values())
        if ins.is_dma:
            ins.buf_sem = dma_buf
            dma_buf.dma_cnt += 1
            ins.dma_count = dma_buf.dma_cnt * 16
        rkey = ("dma", id(ins)) if ins.is_dma else eng
        for buf, key in reads:
            for s in self._states(buf, key):
                s[1][rkey] = ins
        for buf, key in writes:
            for s in self._states(buf, key):
                s[0] = ins
                if key is None or s is not buf.st.get(None):
                    s[1] = {}
        self.streams[eng].append(ins)
        self.all.append(ins)
        return ins

    def barrier(self, dma_bufs_all):
        last = {}
        for e in ("pe", "dve", "act", "pool"):
            for ins in reversed(self.streams[e]):
                if not isinstance(ins, tuple) and not ins.is_dma:
                    last[e] = ins
                    ins.signal = True
                    break
        snap = [(b, b.dma_cnt * 16) for b in dma_bufs_all if b.dma_cnt > 0]
        mark = ("barrier", last, snap)
        for e in ENGS:
            self.streams[e].append(mark)

    def emit(self, final_wait_eng="sp"):
        nc = self.nc
        for ins in self.all:
            for d, _v in ins.deps:
                if not d.is_dma:
                    d.signal = True
        with contextlib.ExitStack() as es:
            eng_sem = {e: es.enter_context(nc.semaphore("s_" + e)) for e in ("pe", "dve", "act", "pool")}
            dma_bufs = []
            for ins in self.all:
                if isinstance(ins, tuple):
                    continue
                if ins.is_dma and ins.buf_sem.dma_sem is None:
                    ins.buf_sem.dma_sem = es.enter_context(nc.semaphore("d_%d" % len(dma_bufs)))
                    dma_bufs.append(ins.buf_sem)
            cnt = {e: 0 for e in eng_sem}
            for ins in self.all:
                if not ins.is_dma and ins.signal:
                    cnt[ins.eng] += 1
                    ins.count = cnt[ins.eng]
            block = es.enter_context(nc.Block())
            engobj = {"pe": block.tensor, "dve": block.vector, "act": block.scalar, "pool": block.gpsimd,
                      "sp": block.sync}
            sched = self

            def make(ename):
                def body(eng):
                    water = {}
                    for ins in sched.streams[ename]:
                        if isinstance(ins, tuple):
                            _, last, snap = ins
                            nw_ = 0
                            for e2, li in last.items():
                                if e2 != ename and water.get(("e", e2), 0) < li.count:
                                    water[("e", e2)] = li.count
                                    eng.wait_ge(eng_sem[e2], li.count)
                                    nw_ += 1
                            for b_, v_ in snap:
                                if water.get(("d", id(b_)), 0) < v_:
                                    water[("d", id(b_))] = v_
                                    eng.wait_ge(b_.dma_sem, v_)
                                    nw_ += 1
                            continue
                        need = {}
                        for d, dv in ins.deps:
                            if d.is_dma:
                                key, sem, val = ("d", id(d.buf_sem)), d.buf_sem.dma_sem, dv
                            else:
                                key, sem, val = ("e", d.eng), eng_sem[d.eng], d.count
                            if need.get(key, (None, 0))[1] < val:
                                need[key] = (sem, val)
                        todo = [(key, sem, val) for key, (sem, val) in need.items() if water.get(key, 0) < val]
                        if ename == "pe" and not ins.is_dma:
                            k_ = sched.pe_groups
                            sched.pe_groups += 1
                            if todo and sched.filler is not None:
                                for _ in range(sched.fill_table.get(k_, 0)):
                                    sched.filler(eng)
                        for key, sem, val in todo:
                            water[key] = val
                            eng.wait_ge(sem, val)
                        bi = ins.fn(eng)
                        if ins.is_dma:
                            bi.then_inc(ins.buf_sem.dma_sem, 16)
                        elif ins.signal:
                            bi.then_inc(eng_sem[ename], 1)
                    if ename == final_wait_eng:
                        for b in dma_bufs:
                            eng.wait_ge(b.dma_sem, b.dma_cnt * 16)
                return body

            for ename in ENGS:
                engobj[ename](make(ename))


FILL_TABLE = (
    "8:16,16:5,24:4,32:13,40:5,48:3,56:14,64:5,72:4,80:15,88:5,96:4,104:15,112:5,120:4,128:14,136:5,145:3,154:13,16"
    "3:3,172:3,181:13,190:3,199:3,208:1,210:1,212:80,220:7,316:3,384:11,387:1,417:18,437:10,457:44,465:3,481:12,489"
    ":11,585:3,653:11,656:1,686:17,706:10,726:42,734:3,750:12,758:11,854:3,922:11,925:1,955:16,975:10,995:42,1003:3"
    ",1019:12,1027:11,1123:3,1191:11,1194:1,1224:17,1244:10,1264:42,1272:3,1288:12,1296:11,1392:3,1460:11,1463:1,14"
    "93:17,1513:10,1533:42,1541:3,1557:12,1565:11,1661:3,1729:11,1732:1,1762:17,1782:10,1802:42,1810:3,1826:12,1834"
    ":11,1930:3,1998:11,2001:1,2031:17,2051:10,2071:42,2079:3,2095:12,2103:11,2199:3,2267:11,2270:1,2300:17,2320:10"
    ",2340:42,2348:3,2364:12,2372:11,2468:3,2536:11,2539:1,2569:17,2589:10,2609:42,2617:3,2633:12,2641:11,2737:3,28"
    "05:11,2808:1,2838:17,2858:10,2878:42,2886:3,2902:12,2910:11,3006:3,3074:11,3077:1,3107:16,3127:10,3147:42,3155"
    ":3,3171:12,3179:11,3275:3,3343:11,3346:1,3376:16,3396:10,3416:42,3424:3,3440:12,3448:11,3544:3,3612:11,3615:1,"
    "3645:16,3665:10,3685:42,3693:3,3709:12,3717:11,3813:3,3881:11,3884:1,3914:17,3934:10,3954:42,3962:3,3978:12,39"
    "86:11,4082:3,4150:11,4153:1,4183:17,4203:10,4223:42,4231:3,4247:12,4255:11,4351:3,4431:14,4434:2,4464:20,4484:"
    "10,4488:5,4496:10,4512:31,4520:3,4536:12,4544:21,4652:9,4732:11,4735:3,4765:24,4789:22,4793:5,4801:32,4809:23,"
    "4813:4,4821:35,4829:21,4833:5,4841:29,4849:23,4853:5,4861:30,4869:23,4873:5,4881:34,4889:20,4893:5,4901:35,490"
    "9:21,4913:4,4921:29,4929:23,4933:5,4941:33,4949:21,4953:5,4961:32,4969:23,4973:5,4981:29,4989:23,4993:4,5001:3"
    "4,5009:22,5013:5,5021:31,5029:23,5033:5,5041:30,5049:22,5053:4,5061:34,5069:23,5073:4,5081:30,5089:23,5093:4,5"
    "101:9,5117:31,5125:1,5141:80,5149:8,5245:3,5301:18,5304:1,5313:17,5317:11,5342:8,5350:5,5362:1,5374:1,5394:2,5"
    "398:2,5406:2,5414:2,5418:5,5426:2,5442:24,5446:2,5454:17,5462:11,5558:3,5614:18,5617:1,5626:13,5630:11,5655:11"
    ",5663:9,5675:2,5707:2,5711:2,5719:2,5727:1,5731:4,5739:1,5755:25,5759:2,5767:16,5775:11,5871:3,5927:18,5930:1,"
    "5939:16,5943:10,5968:9,5976:4,6020:2,6024:2,6032:2,6040:1,6044:6,6052:2,6068:25,6072:2,6080:17,6088:11,6184:3,"
    "6240:18,6243:1,6252:18,6256:11,6281:9,6289:8,6301:1,6313:1,6325:1,6333:2,6337:2,6345:2,6353:1,6357:4,6365:1,63"
    "81:25,6385:2,6393:17,6401:11,6497:3,6553:18,6556:1,6565:14,6569:11,6594:12,6602:4,6646:2,6650:2,6658:2,6666:1,"
    "6670:5,6678:1,6694:24,6698:2,6706:16,6714:11,6810:3,6866:18,6869:1,6878:12,6882:11,6907:7,6915:6,6927:1,6939:1"
    ",6959:2,6963:2,6971:2,6979:1,6983:4,6991:1,7007:24,7011:2,7019:16,7027:11,7123:3,7179:18,7182:1,7191:14,7195:1"
    "1,7220:8,7228:4,7240:1,7252:1,7264:1,7272:3,7276:2,7284:2,7292:1,7296:6,7304:1,7320:25,7324:2,7332:16,7340:11,"
    "7436:3,7492:18,7495:1,7504:12,7508:11,7533:10,7541:6,7553:1,7585:2,7589:2,7597:2,7605:1,7609:6,7617:2,7633:25,"
    "7637:2,7645:17,7653:11,7749:3,7805:18,7808:1,7817:16,7821:11,7846:8,7854:4,7898:2,7902:2,7910:2,7918:1,7922:5,"
    "7930:2,7946:25,7950:2,7958:17,7966:11,8062:3,8118:18,8121:1,8130:12,8134:11,8159:7,8167:8,8179:1,8211:3,8215:2"
    ",8223:2,8231:2,8235:7,8243:2,8259:23,8263:2,8271:17,8279:11,8375:3,8431:18,8434:1,8443:14,8447:11,8472:7,8480:"
    "4,8524:2,8528:2,8536:2,8544:1,8548:5,8556:1,8572:25,8576:2,8584:17,8592:11,8688:3,8744:18,8747:1,8756:13,8760:"
    "11,8785:9,8793:7,8805:1,8817:1,8829:1,8837:2,8841:2,8849:2,8857:1,8861:6,8869:1,8885:23,8889:2,8897:17,8905:11"
    ",9001:3,9057:18,9060:1,9069:15,9073:11,9098:8,9106:5,9118:1,9130:1,9150:2,9154:2,9162:2,9170:1,9174:5,9182:2,9"
    "198:25,9202:2,9210:17,9218:11,9314:3,9370:18,9373:1,9382:12,9386:11,9411:10,9419:4,9431:1,9443:1,9455:1,9463:2"
    ",9467:2,9475:2,9483:1,9487:6,9495:1,9511:23,9515:2,9523:17,9531:11,9627:3,9683:18,9686:1,9695:15,9699:11,9724:"
    "8,9732:5,9744:1,9756:1,9776:2,9780:2,9788:2,9796:1,9800:4,9808:1,9824:23,9828:1,9836:17,9844:11,9940:3,10008:1"
    "9,10011:1,10020:18,10024:11,10049:9,10057:6,10069:1,10101:2,10105:2,10113:2,10121:1,10125:6,10133:2,10149:25,1"
    "0153:2,10161:17,10169:32,10277:4,10345:18,10348:1,10357:36,10361:11,10386:6,10390:4,10394:1,10402:3,10436:2,10"
    "445:3,10447:1,10450:75,10466:2,10484:2,10493:2,10498:80,10532:2,10534:1,10541:2,10545:1,10546:79,10580:2,10589"
    ":2,10606:11,10610:1,10618:80,10626:8,10722:3,10778:18,10781:1,10790:15,10794:11,10819:7,10827:4,10871:2,10875:"
    "2,10883:2,10891:1,10895:4,10903:1,10919:25,10923:2,10931:17,10939:11,11035:3,11091:18,11094:1,11103:17,11107:1"
    "1,11132:8,11140:4,11184:2,11188:2,11196:2,11204:1,11208:4,11216:2,11232:25,11236:2,11244:17,11252:11,11348:3,1"
    "1404:18,11407:1,11416:12,11420:11,11445:9,11453:8,11465:1,11497:2,11501:2,11509:2,11517:1,11521:7,11529:2,1154"
    "5:25,11549:2,11557:17,11565:11,11661:3,11717:18,11720:1,11729:16,11733:11,11758:7,11766:4,11810:2,11814:2,1182"
    "2:2,11830:1,11834:5,11842:2,11858:25,11862:2,11870:17,11878:11,11974:3,12030:18,12033:1,12042:12,12046:11,1207"
    "1:10,12079:6,12091:1,12123:2,12127:2,12135:2,12143:1,12147:4,12155:1,12171:19,12175:2,12183:17,12191:11,12287:"
    "3,12343:18,12346:1,12355:15,12359:11,12384:8,12392:4,12404:1,12436:2,12440:2,12448:2,12456:1,12460:4,12468:1,1"
    "2484:23,12488:2,12496:17,12504:11,12600:3,12656:18,12659:1,12668:13,12672:11,12697:7,12705:8,12717:1,12729:1,1"
    "2741:1,12749:2,12753:2,12761:2,12769:1,12773:4,12781:1,12797:17,12801:2,12809:17,12817:11,12913:3,12969:18,129"
    "72:1,12981:16,12985:10,13010:11,13018:9,13030:1,13042:1,13062:2,13066:2,13074:2,13082:1,13086:5,13094:2,13110:"
    "25,13114:2,13122:17,13130:11,13226:3,13282:20,13285:1,13294:12,13298:12,13323:12,13331:7,13343:1,13375:2,13379"
    ":2,13387:2,13395:1,13399:4,13407:1,13423:23,13427:2,13435:17,13443:11,13539:3,13595:18,13598:1,13607:13,13611:"
    "11,13636:7,13644:8,13656:1,13668:1,13680:1,13688:2,13692:2,13700:2,13708:1,13712:4,13720:2,13736:24,13740:2,13"
    "748:17,13756:11,13852:3,13908:18,13911:1,13920:12,13924:11,13949:9,13957:7,13969:1,13981:1,14001:2,14005:2,140"
    "13:2,14021:1,14025:6,14033:2,14049:25,14053:2,14061:17,14069:11,14165:3,14221:18,14224:1,14233:16,14237:11,142"
    "62:10,14270:6,14282:1,14294:1,14314:2,14318:2,14326:2,14334:1,14338:4,14346:1,14362:24,14366:2,14374:17,14382:"
    "11,14478:3,14534:18,14537:1,14546:15,14550:11,14575:7,14583:4,14627:2,14631:2,14639:2,14647:1,14651:5,14659:1,"
    "14675:25,14679:2,14687:17,14695:11,14791:3,14847:18,14850:1,14859:13,14863:11,14888:9,14896:8,14908:1,14940:2,"
    "14944:3,14952:2,14960:1,14964:6,14972:2,14988:25,14992:2,15000:17,15008:11,15104:3,15160:18,15163:1,15172:12,1"
    "5176:11,15201:7,15209:7,15221:1,15253:2,15257:2,15265:2,15273:1,15277:5,15285:2,15301:24,15305:2,15313:17,1532"
    "1:11,15417:3,15485:19,15488:1,15497:17,15501:11,15526:10,15534:8,15546:1,15558:1,15578:2,15582:2,15590:2,15598"
    ":1,15602:4,15610:1,15626:20,15630:2,15638:17,15646:30,15754:6,15822:18,15825:1,15834:40,15838:10,15863:7,15867"
    ":4,15871:1,15879:4,15913:2,15922:3,15924:1,15926:1,15927:75,15943:2,15961:2,15970:2,15975:78,15991:1,16009:2,1"
    "6011:1,16018:2,16023:75,16039:2,16057:2,16066:2,16083:12,16087:1,16095:48,16103:30,16111:4,16119:3,16127:13,16"
    "135:5,16143:4,16151:15,16159:5,16167:4,16175:15,16183:5,16191:4,16199:14,16207:5,16215:4,16223:14,16231:5,1624"
    "0:4,16249:13,16258:3,16267:3,16276:12,16285:3,16294:3,16303:1,16305:2,16307:80,16315:7,16411:3,16479:10,16482:"
    "1,16512:21,16532:11,16552:42,16560:3,16576:12,16584:11,16680:3,16748:11,16751:1,16781:21,16801:10,16821:42,168"
    "29:3,16845:12,16853:11,16949:3,17017:11,17020:1,17050:21,17070:10,17090:43,17098:3,17114:12,17122:11,17218:3,1"
    "7286:11,17289:1,17319:21,17339:10,17359:42,17367:3,17383:12,17391:11,17487:3,17555:11,17558:1,17588:21,17608:1"
    "0,17628:43,17636:3,17652:12,17660:11,17756:3,17824:11,17827:1,17857:21,17877:10,17897:42,17905:3,17921:12,1792"
    "9:11,18025:3,18093:11,18096:1,18126:20,18146:10,18166:42,18174:3,18190:12,18198:11,18294:3,18362:11,18365:1,18"
    "395:16,18415:10,18435:42,18443:3,18459:12,18467:11,18563:3,18631:11,18634:1,18664:16,18684:10,18704:42,18712:3"
    ",18728:12,18736:11,18832:3,18900:11,18903:1,18933:16,18953:10,18973:42,18981:3,18997:12,19005:11,19101:3,19169"
    ":11,19172:1,19202:17,19222:10,19242:42,19250:3,19266:12,19274:11,19370:3,19438:10,19441:1,19471:17,19491:10,19"
    "511:42,19519:3,19535:12,19543:11,19639:3,19707:11,19710:1,19740:17,19760:10,19780:42,19788:3,19804:12,19812:11"
    ",19908:3,19976:11,19979:1,20009:17,20029:10,20049:44,20057:3,20073:12,20081:11,20177:3,20245:11,20248:1,20278:"
    "16,20298:10,20318:42,20326:3,20342:12,20350:11,20446:3,20526:14,20529:2,20559:20,20579:10,20583:5,20591:10,206"
    "07:31,20615:3,20631:13,20639:22,20747:9,20827:11,20830:3,20860:24,20884:23,20888:4,20896:32,20904:22,20908:5,2"
    "0916:34,20924:20,20928:5,20936:33,20944:22,20948:4,20956:29,20964:23,20968:5,20976:33,20984:18,20988:5,20996:3"
    "4,21004:22,21008:4,21016:29,21024:23,21028:5,21036:31,21044:22,21048:5,21056:29,21064:22,21068:5,21076:32,2108"
    "4:21,21088:5,21096:31,21104:23,21108:5,21116:30,21124:23,21128:5,21136:29,21144:23,21148:4,21156:31,21164:22,2"
    "1168:4,21176:30,21184:23,21188:4,21196:9,21212:31,21220:2,21236:80,21244:9,21340:3,21396:18,21399:1,21408:17,2"
    "1412:11,21437:10,21445:7,21457:1,21489:2,21493:2,21501:2,21509:1,21513:4,21521:1,21537:24,21541:2,21549:16,215"
    "57:11,21653:3,21709:18,21712:1,21721:16,21725:11,21750:7,21758:8,21770:1,21782:1,21794:1,21802:3,21806:2,21814"
    ":2,21822:1,21826:4,21834:1,21850:21,21854:2,21862:18,21870:11,21966:3,22022:18,22025:1,22034:13,22038:11,22063"
    ":9,22071:6,22083:1,22095:1,22115:2,22119:2,22127:2,22135:1,22139:6,22147:2,22163:25,22167:2,22175:17,22183:11,"
    "22279:3,22335:18,22338:1,22347:14,22351:11,22376:10,22384:7,22396:1,22408:1,22428:2,22432:2,22440:2,22448:1,22"
    "452:4,22460:1,22476:23,22480:2,22488:16,22496:11,22592:3,22648:18,22651:1,22660:15,22664:11,22689:9,22697:9,22"
    "709:1,22721:1,22741:2,22745:2,22753:2,22761:1,22765:4,22773:1,22789:25,22793:2,22801:17,22809:11,22905:3,22961"
    ":18,22964:1,22973:12,22977:11,23002:7,23010:7,23022:1,23054:2,23058:2,23066:2,23074:1,23078:6,23086:2,23102:25"
    ",23106:2,23114:17,23122:11,23218:3,23274:18,23277:1,23286:16,23290:11,23315:7,23323:7,23335:1,23347:1,23359:1,"
    "23367:2,23371:2,23379:2,23387:1,23391:5,23399:1,23415:25,23419:2,23427:17,23435:11,23531:3,23587:18,23590:1,23"
    "599:13,23603:11,23628:9,23636:7,23648:1,23660:1,23680:2,23684:2,23692:2,23700:1,23704:6,23712:1,23728:23,23732"
    ":2,23740:17,23748:11,23844:3,23900:18,23903:1,23912:15,23916:11,23941:8,23949:5,23961:1,23973:1,23993:2,23997:"
    "2,24005:2,24013:1,24017:4,24025:1,24041:18,24045:2,24053:17,24061:11,24157:3,24213:18,24216:1,24225:13,24229:1"
    "1,24254:10,24262:9,24274:1,24286:1,24306:2,24310:2,24318:2,24326:1,24330:5,24338:2,24354:26,24358:2,24366:17,2"
    "4374:11,24470:3,24526:18,24529:1,24538:12,24542:11,24567:11,24575:7,24587:1,24599:1,24619:2,24623:2,24631:2,24"
    "639:1,24643:4,24651:1,24667:16,24671:2,24679:17,24687:11,24783:3,24839:18,24842:1,24851:16,24855:11,24880:7,24"
    "888:6,24900:1,24912:1,24924:1,24932:2,24936:2,24944:2,24952:1,24956:4,24964:2,24980:25,24984:2,24992:16,25000:"
    "11,25096:3,25152:18,25155:1,25164:14,25168:13,25193:11,25201:7,25213:1,25245:3,25249:2,25257:2,25265:1,25269:4"
    ",25277:1,25293:18,25297:2,25305:17,25313:11,25409:3,25465:18,25468:1,25477:13,25481:11,25506:10,25514:9,25526:"
    "1,25538:1,25558:2,25562:2,25570:2,25578:1,25582:5,25590:2,25606:24,25610:2,25618:17,25626:11,25722:3,25778:18,"
    "25781:1,25790:12,25794:11,25819:9,25827:8,25839:1,25851:1,25871:2,25875:2,25883:2,25891:1,25895:4,25903:1,2591"
    "9:18,25923:2,25931:17,25939:11,26035:3,26103:19,26106:1,26115:17,26119:11,26144:10,26152:8,26164:1,26176:1,261"
    "96:3,26200:2,26208:2,26216:1,26220:4,26228:1,26244:19,26248:2,26256:17,26264:28,26372:5,26440:18,26443:1,26452"
    ":41,26456:10,26481:6,26485:4,26489:1,26497:3,26531:2,26540:2,26545:75,26561:2,26579:2,26588:2,26593:77,26609:2"
    ",26627:2,26629:1,26636:2,26641:73,26657:2,26675:1,26684:2,26701:10,26705:1,26713:80,26721:7,26817:3,26873:18,2"
    "6876:1,26885:17,26889:11,26914:9,26922:7,26934:1,26946:1,26958:1,26966:2,26970:2,26978:2,26986:1,26990:4,26998"
    ":1,27014:24,27018:2,27026:48,27034:10,27130:3,27186:18,27189:1,27198:14,27202:11,27227:8,27235:8,27247:1,27259"
    ":1,27279:3,27283:2,27291:2,27299:1,27303:4,27311:1,27327:24,27331:1,27339:50,27347:9,27443:3,27499:17,27502:1,"
    "27511:18,27515:11,27540:8,27548:6,27560:1,27572:1,27592:2,27596:2,27604:2,27612:1,27616:5,27624:1,27640:24,276"
    "44:2,27652:48,27660:11,27756:3,27812:17,27815:1,27824:15,27828:11,27853:11,27861:7,27873:1,27905:2,27909:2,279"
    "17:2,27925:1,27929:4,27937:1,27953:24,27957:2,27965:48,27973:11,28069:3,28125:18,28128:1,28137:13,28141:11,281"
    "66:9,28174:7,28186:1,28198:1,28218:2,28222:2,28230:2,28238:1,28242:6,28250:2,28266:25,28270:2,28278:50,28286:1"
    "0,28382:3,28438:17,28441:3,28450:12,28454:11,28479:7,28487:4,28531:2,28535:2,28543:2,28551:1,28555:4,28563:1,2"
    "8579:20,28583:2,28591:48,28599:9,28695:3,28751:17,28754:1,28763:13,28767:11,28792:7,28800:9,28812:1,28844:2,28"
    "848:2,28856:2,28864:1,28868:4,28876:1,28892:23,28896:2,28904:50,28912:11,29008:3,29064:18,29067:1,29076:13,290"
    "80:11,29105:11,29113:9,29125:1,29157:2,29161:2,29169:2,29177:1,29181:4,29189:1,29205:25,29209:2,29217:49,29225"
    ":7,29321:3,29377:18,29380:1,29389:16,29393:10,29418:14,29426:4,29438:1,29470:2,29474:2,29482:2,29490:1,29494:4"
    ",29502:2,29518:23,29522:2,29530:49,29538:10,29634:3,29690:18,29693:1,29702:16,29706:10,29731:10,29739:7,29751:"
    "1,29783:2,29787:2,29795:2,29803:1,29807:4,29815:1,29831:23,29835:2,29843:48,29851:10,29947:3,30003:17,30006:1,"
    "30015:13,30019:11,30044:8,30052:5,30064:1,30076:1,30096:2,30100:2,30108:2,30116:1,30120:4,30128:1,30144:21,301"
    "48:1,30156:48,30164:9,30260:3,30316:17,30319:1,30328:17,30332:11,30357:8,30365:7,30377:1,30389:1,30409:2,30413"
    ":2,30421:2,30429:1,30433:5,30441:1,30457:23,30461:2,30469:50,30477:11,30573:3,30629:18,30632:1,30641:13,30645:"
    "11,30670:10,30678:7,30690:1,30722:2,30726:2,30734:2,30742:1,30746:5,30754:1,30770:17,30774:2,30782:49,30790:11"
    ",30886:3,30942:18,30945:1,30954:13,30958:11,30983:11,30991:9,31003:1,31035:2,31039:2,31047:2,31055:1,31059:4,3"
    "1067:1,31083:26,31087:2,31095:48,31103:11,31199:3,31255:18,31258:1,31267:14,31271:11,31296:7,31304:7,31316:1,3"
    "1348:2,31352:2,31360:2,31368:1,31372:6,31380:2,31396:24,31400:2,31408:49,31416:11,31512:3,31580:19,31583:1,315"
    "92:16,31596:11,31621:9,31629:6,31641:1,31653:1,31673:2,31677:2,31685:2,31693:1,31697:6,31705:2,31721:24,31725:"
    "2,31733:50,31741:14,31849:8,31917:18,31920:1,31929:40,31933:12,31958:12,31962:7,31966:1,31974:5,32008:2,32017:"
    "3,32019:1,32022:76,32038:2,32056:2,32065:2,32070:78,32086:1,32104:2,32106:1,32113:2,32118:76,32134:2,32152:2,3"
    "2161:2,32178:11,32182:1"
)
D = 1024
SEQ = 2048
NT = SEQ // 128
NSB = 16
SL = 4
DEPTH = 2
IN_DIM = 6688
EPS = 1e-6
NEG = -30000.0
OFF_ZS, OFF_XBC, OFF_DT, OFF_ZG, OFF_QKV, OFF_A, OFF_B = 0, 1024, 2560, 2576, 3600, 6672, 6680


class Cfg:
    def __init__(self, name, T, nseq, L, Q, chain):
        self.name, self.T, self.nseq, self.L, self.Q, self.chain = name, T, nseq, L, Q, chain
        self.nch = T // Q
        self.nlev = {64: 5, 4: 1}[Q]


CP = Cfg("p", 128, 1, 128, 64, True)
CS = Cfg("s", 64, NSB, SL, 4, False)


def host_consts():
    c = {"ident": np.eye(128, dtype=np.float32), "ones": np.ones((128, 128), np.float32)}
    for cfg in (CP, CS):
        T, Q, nch = cfg.T, cfg.Q, cfg.nch
        ch = np.arange(T) // Q
        same = ch[:, None] == ch[None, :]
        idx = np.arange(T)
        le = idx[:, None] <= idx[None, :]
        lt = idx[:, None] < idx[None, :]
        n = cfg.name
        c["tri_" + n] = (same & le).astype(np.float32)
        c["blk_" + n] = same.astype(np.float32)
        c["nmI_" + n] = np.where(same & le, 0.0, NEG).astype(np.float32)
        c["nmS_" + n] = np.where(same & lt, 0.0, NEG).astype(np.float32)
        c["nmSr_" + n] = np.ascontiguousarray(c["nmS_" + n].T)
        cm = (ch[None, :] == np.arange(nch)[:, None]).astype(np.float32)
        c["cmf_" + n] = np.ascontiguousarray(np.broadcast_to(cm[None], (128, nch, T))).astype(np.float32)
        c["cmT_" + n] = np.ascontiguousarray(cm.T)
    e = np.zeros((17, 128), np.float32)
    e[0, :] = 1.0
    c["E_p"] = e
    e = np.zeros((17, 64), np.float32)
    for t in range(64):
        e[1 + t // SL, t] = 1.0
    c["E_s"] = e
    return c


def bl(ap, n):
    return ap.unsqueeze(len(ap.shape)).to_broadcast(list(ap.shape) + [n])


def bm(ap, n):
    return ap.unsqueeze(1).to_broadcast([ap.shape[0], n] + list(ap.shape[1:]))


def build(debug=(), limit=None):
    limit = limit or {}
    nc = bass.Bass("TRN2", target_bir_lowering=False)
    S = Sched(nc)
    es = contextlib.ExitStack()
    consts_np = host_consts()

    def din(name, shape):
        return nc.dram_tensor(name, list(shape), F32, kind="ExternalInput").ap()

    def dout(name, shape):
        return nc.dram_tensor(name, list(shape), F32, kind="ExternalOutput").ap()

    I = {}
    I["xp"] = din("xp", [SEQ, D])
    I["xs"] = din("xs", [NSB * SL, D])
    I["call"] = din("call", [17, D])
    I["st_sc"] = din("st_sc", [DEPTH, NSB, 3, 1536])
    I["st_ssm"] = din("st_ssm", [DEPTH, NSB, 16, 64, 128])
    I["st_gc"] = din("st_gc", [DEPTH, NSB, 3, 3072])
    I["st_gdn"] = din("st_gdn", [DEPTH, NSB, 8, 128, 128])
    wshapes = {"norm_w": [DEPTH, D], "w_ada": [DEPTH, D, 3 * D], "b_ada": [DEPTH, 3 * D], "w_in": [DEPTH, D, IN_DIM],
               "ssd_conv_w": [DEPTH, 4, 1536], "ssd_conv_b": [DEPTH, 1536], "ssd_dt_bias": [DEPTH, 16],
               "ssd_a_log": [DEPTH, 16], "ssd_d": [DEPTH, 16], "ssd_norm_w": [DEPTH, 1024],
               "gdn_conv_w": [DEPTH, 4, 3072], "gdn_dt_bias": [DEPTH, 8], "gdn_a_log": [DEPTH, 8],
               "gdn_norm_w": [DEPTH, 128], "w_out": [DEPTH, 2048, D], "final_norm_w": [1, D]}
    for k, shp in wshapes.items():
        I[k] = din(k, shp)
    for k, v in consts_np.items():
        I["c_" + k] = din("c_" + k, v.shape)
    O = {}
    O["y_p"] = dout("y_p", [SEQ, D])
    O["y_s"] = dout("y_s", [NSB * SL, D])
    O["ncs_p"] = dout("ncs_p", [DEPTH, 3, 1536])
    O["nssm_p"] = dout("nssm_p", [DEPTH, 1024, 128])
    O["ngc_p"] = dout("ngc_p", [DEPTH, 3, 3072])
    O["ngdn_p"] = dout("ngdn_p", [DEPTH, 8, 128, 128])
    O["ncs_s"] = dout("ncs_s", [DEPTH, NSB * 3, 1536])
    O["nssm_s"] = dout("nssm_s", [DEPTH, NSB, 1024, 128])
    O["ngc_s"] = dout("ngc_s", [DEPTH, NSB * 3, 3072])
    O["ngdn_s"] = dout("ngdn_s", [DEPTH, NSB, 8, 128, 128])
    NROW = SEQ + NSB * SL
    scr = [Buf(nc.dram_tensor("scr%d" % i, [NROW, D], F32, kind="Internal").ap(), "scr%d" % i) for i in range(3)]
    xin_ext = None

    allb = []

    def sb(name, shape, dt=F32):
        b_ = Buf(es.enter_context(nc.sbuf_tensor(name, list(shape), dt)), name)
        allb.append(b_)
        return b_

    banks = [Buf(es.enter_context(nc.psum_tensor("bank%d" % i, [128, 512], F32)), "bank%d" % i) for i in range(8)]
    for b_ in banks:
        b_.excl = True
    pstate = {"i": 0, "pinned": set()}
    NB = 7

    def pget():
        while True:
            i = pstate["i"] % NB
            pstate["i"] += 1
            if i not in pstate["pinned"]:
                return banks[i]

    def ppin(b):
        pstate["pinned"].add(banks.index(b))

    def punpin(b):
        pstate["pinned"].discard(banks.index(b))

    def pf(b, shape, p=128):
        n = int(np.prod(shape))
        ap = b[:p, 0:n]
        if len(shape) == 1:
            return ap
        names = " ".join("a%d" % i for i in range(len(shape)))
        kw = {"a%d" % i: shape[i] for i in range(1, len(shape))}
        return ap.rearrange("p (%s) -> p %s" % (names, names), **kw)

    def pb(b, shape, p=128):
        n = int(np.prod(shape))
        ap = b[:p, :].bitcast(BF16)[:, 0:n]
        if len(shape) == 1:
            return ap
        names = " ".join("a%d" % i for i in range(len(shape)))
        kw = {"a%d" % i: shape[i] for i in range(1, len(shape))}
        return ap.rearrange("p (%s) -> p %s" % (names, names), **kw)

    def MM(out, lhsT, rhs, R, W, start=True, stop=True, **kw):
        S.add("pe", lambda e: e.matmul(out, lhsT=lhsT, rhs=rhs, start=start, stop=stop, **kw), reads=R, writes=W)

    def TR(out, in_, ident, R, W):
        S.add("pe", lambda e: e.transpose(out=out, in_=in_, identity=ident), reads=R, writes=W)

    def ACT(out, in_, func, R, W, **kw):
        S.add("act", lambda e: e.activation(out=out, in_=in_, func=func, **kw), reads=R, writes=W)

    def TT(out, in0, in1, op, R, W, eng="dve"):
        S.add(eng, lambda e: e.tensor_tensor(out=out, in0=in0, in1=in1, op=op), reads=R, writes=W)

    def TS(out, in0, s1, s2, op0, op1, R, W, eng="dve"):
        S.add(eng, lambda e: e.tensor_scalar(out=out, in0=in0, scalar1=s1, scalar2=s2, op0=op0, op1=op1), reads=R, writes=W)

    def STT(out, in0, scalar, in1, op0, op1, R, W):
        S.add("dve", lambda e: e.scalar_tensor_tensor(out=out, in0=in0, scalar=scalar, in1=in1, op0=op0, op1=op1), reads=R, writes=W)

    def CP_(out, in_, R, W, eng="dve"):
        S.add(eng, lambda e: e.tensor_copy(out=out, in_=in_), reads=R, writes=W)

    def RECIP(out, in_, R, W):
        S.add("dve", lambda e: e.reciprocal(out=out, in_=in_), reads=R, writes=W)

    def RED(out, in_, R, W):
        S.add("dve", lambda e: e.tensor_reduce(out=out, in_=in_, axis=AX.X, op=ALU.add), reads=R, writes=W)

    def MSET(ap, val, W, eng="pool"):
        S.add(eng, lambda e: e.memset(ap, val), writes=W)

    def LD(out, in_, buf, R=None, nonc=False):
        if nonc:
            S.add("sp", lambda e: e.dma_start(out=out, in_=in_, allow_slow_non_contiguous=True), reads=R, writes=[buf], dma_buf=buf)
        else:
            S.add("sp", lambda e: e.dma_start(out=out, in_=in_), reads=R, writes=[buf], dma_buf=buf)

    sto_n = [0]
    sto_dummy = Buf(None, "sto_dummy")

    def STO(out, in_, buf, W=None, extra_reads=None):
        S.add("sp", lambda e: e.dma_start(out=out, in_=in_), reads=[buf] + (extra_reads or []), writes=W, dma_buf=buf)

    dbg_out = {}
    mod_stacks = []
    cur_es = [es]
    all_bufs = []

    def DBG(name, buf, ap, shape):
        if name in debug and name not in dbg_out:
            o = dout("dbg_" + name, shape)
            dbg_out[name] = o
            if ap.dtype != F32:
                tmp = Buf(cur_es[0].enter_context(nc.sbuf_tensor("dbgt_" + name, list(shape), F32)), "dbgt_" + name)
                all_bufs.append(tmp)
                CP_(tmp[:], ap, [buf], [tmp])
                STO(o, tmp[:], tmp)
            else:
                STO(o, ap, buf)

    C = {}
    for k, v in consts_np.items():
        if k.startswith("cmf_"):
            continue
        C[k] = sb("k_" + k, v.shape)
        LD(C[k][:], I["c_" + k], C[k])
    identf = C["ident"]
    onesf = C["ones"]
    identb = sb("identb", [128, 128], BF16)
    CP_(identb[:], identf[:], [identf], [identb])
    onesb = sb("onesb", [128, 128], BF16)
    CP_(onesb[:], onesf[:], [onesf], [onesb])
    cfg_pending = []
    for cfg in (CP, CS):
        n = cfg.name
        cfg.tri, cfg.blk, cfg.nmI, cfg.nmS, cfg.nmSr, cfg.cmT = (C[k + n] for k in ("tri_", "blk_", "nmI_", "nmS_", "nmSr_", "cmT_"))
        cfg.cmf = sb("cmfb_" + n, [128, cfg.nch, cfg.T], BF16)
        cfg.ncmf = sb("ncmfb_" + n, [128, cfg.nch, cfg.T], BF16)
        cfg_pending.append(cfg)
        cfg.cmTb = sb("cmTb_" + n, [cfg.T, cfg.nch], BF16)
        CP_(cfg.cmTb[:], cfg.cmT[:], [cfg.cmT], [cfg.cmTb])

    SW = 128
    stage = [sb("stage%d" % i, [128, 8, SW]) for i in range(2)]
    stage_i = [0]
    WO = sb("wo", [128, 8, D], BF16)
    DIAG = sb("diag", [128, 12, 4, 128], BF16)
    cwT = sb("cwT", [128, 12, 4])
    cbrow = sb("cbrow", [1, 1536], BF16)
    nwT = sb("nwT", [128, 8])
    scT = sb("scT", [128, 8, 17])
    shT = sb("shT", [128, 8, 17])
    gate_bc = {"p": sb("gate_p", [128, D]), "s": sb("gate_s", [64, D])}
    vec16 = sb("vec16", [128, 4, 16])
    vec8 = sb("vec8", [128, 2, 8])
    scTb = sb("scTb", [128, 8, 17], BF16)
    badaT = sb("badaT", [128, 24])
    nw_in = sb("nw_in", [128, 8])
    xt_slots = [sb("xt%d" % i, [128, D]) for i in range(2)]
    xa_slots = [sb("xa%d" % i, [128, D]) for i in range(2)]
    stage_all = stage + xt_slots + xa_slots
    junk = sb("junk", [128, D], BF16)
    xn = sb("xn", [128, D], BF16)
    hT = sb("hT", [128, 8, 128], BF16)
    sm1 = sb("sm1", [128, 8])
    raw = sb("raw", [128, 12, 132], BF16)
    tailf = sb("tailf", [128, 12, 48])
    tailo = sb("tailo", [48, 1536])
    cst_in = tailo
    fT = sb("fT", [128, 12, 128], BF16)
    big = [sb("big%d" % i, [128, D]) for i in range(4)]
    bigb = [sb("bigb%d" % i, [128, D], BF16) for i in range(3)]
    xo = big[3]
    call_sb, sil_c, bgate, gate17, cbrow_f = big[0], bigb[0], big[1], big[2], tailo
    hTf_t = big[0]
    for cfg in cfg_pending:
        n_ = cfg.nch * cfg.T
        tmpv = big[0][:, 0:n_].rearrange("p (c t) -> p c t", t=cfg.T)
        LD(tmpv, I["c_cmf_" + cfg.name], big[0])
        CP_(cfg.cmf[:], tmpv, [big[0]], [cfg.cmf])
        TS(cfg.ncmf[:], tmpv, -1.0, None, ALU.mult, ALU.bypass, [big[0]], [cfg.ncmf])

    def stg():
        i = stage_i[0] % len(stage_all)
        stage_i[0] += 1
        b_ = stage_all[i]
        v = b_[:, :, :] if b_ in stage else b_[:, :].rearrange("p (k n) -> p k n", k=8)
        return b_, v, ("pool", "act", "dve")[i % 3]

    def cast(eng, out, in_, R, W, scale=None):
        if eng == "act":
            if scale is None:
                ACT(out, in_, AF.Copy, R, W)
            else:
                ACT(out, in_, AF.Copy, R, W, scale=scale)
        elif scale is None:
            CP_(out, in_, R, W, eng=eng)
        else:
            TS(out, in_, scale, 1.0, ALU.mult, ALU.mult, R, W, eng=eng)

    def load_cols(dst, dcol, src, ncols):
        c0 = 0
        while c0 < ncols:
            n = min(SW, ncols - c0)
            sbuf_, st, eng = stg()
            LD(st[:, :, 0:n], src[:, c0:c0 + n].rearrange("(k p) n -> p k n", p=128), sbuf_)
            cast(eng, dst[:, :, dcol + c0:dcol + c0 + n], st[:, :, 0:n], [sbuf_], [dst])
            c0 += n

    def load_wo(src_rows, nk, scale_ap_fn):
        for half in range(D // SW):
            sbuf_, st, eng = stg()
            LD(st[:, 0:nk, :], src_rows[:, half * SW:(half + 1) * SW].rearrange("(k p) n -> p k n", p=128), sbuf_)
            for k in range(nk):
                cast(eng, WO[:, k, half * SW:(half + 1) * SW], st[:, k, :], [sbuf_, nwT], [WO], scale=scale_ap_fn(k))

    def build_diag(convw, c0, nchunks):
        for k_ in range(4):
            LD(cwT[:, 0:nchunks, k_], convw[k_, c0 * 128:(c0 + nchunks) * 128].rearrange("(c p) -> p c", p=128), cwT, nonc=True)
        for c in range(nchunks):
            TT(DIAG[:, c, :, :], bm(identf[:], 4), bl(cwT[:, c, :], 128), ALU.mult, [identf, cwT], [DIAG], eng="pool")

    def do_mod(l):
        LD(call_sb[:17, :], I["call"], call_sb)
        ACT(sil_c[:17, :], call_sb[:17, :], AF.Silu, [call_sb], [sil_c])
        b = pget()
        for c in range(8):
            TR(pb(b, [8, 32])[:, c, 0:17], sil_c[:17, c * 128:(c + 1) * 128], identb[:17, :17], [sil_c, identb], [b])
        CP_(scTb[:], pb(b, [8, 32])[:, :, 0:17], [b], [scTb])
        LD(badaT[:], I["b_ada"][l].rearrange("(c p) -> p c", p=128), badaT, nonc=True)
        LD(nw_in[:], I["norm_w"][l].rearrange("(c p) -> p c", p=128), nw_in, nonc=True)
        LD(bgate[0:1, :], I["b_ada"][l:l + 1, 2048:3072], bgate)
        nblk = 3 * D // SW
        per = D // SW
        modes = contextlib.ExitStack()
        wb = Buf(modes.enter_context(nc.sbuf_tensor("modwb%d" % l, [128, 8, SW], BF16)), "modwb")
        all_bufs.append(wb)
        for blk in range(nblk):
            sbuf_, st, eng = stg()
            LD(st, I["w_ada"][l][:, blk * SW:(blk + 1) * SW].rearrange("(k p) n -> p k n", p=128), sbuf_)
            cast(eng, wb[:, :, 0:SW], st, [sbuf_], [wb])
            if blk < 2 * per:
                b = pget()
                nj = SW // 128
                for j in range(nj):
                    for k in range(8):
                        MM(pf(b, [nj, 17])[:, j, :], wb[:, k, j * 128:(j + 1) * 128], scTb[:, k, :], [wb, scTb], [b],
                           start=(k == 0), stop=(k == 7))
                dst = shT if blk < per else scT
                cc = (blk % per) * nj
                TT(dst[:, cc:cc + nj, :], pf(b, [nj, 17]), bl(badaT[:, blk * nj:blk * nj + nj], 17), ALU.add, [b, badaT], [dst])
            else:
                g = pget()
                gc0 = (blk - 2 * per) * SW
                for k in range(8):
                    MM(g[:17, 0:SW], scTb[:, k, :], wb[:, k, 0:SW], [wb, scTb], [g], start=(k == 0), stop=False)
                MM(g[:17, 0:SW], onesf[0:1, 0:17], bgate[0:1, gc0:gc0 + SW], [onesf, bgate], [g], start=False, stop=True)
                CP_(gate17[:17, gc0:gc0 + SW], g[:17, 0:SW], [g], [gate17])
        TS(scT[:], scT[:], 1.0, None, ALU.add, ALU.bypass, [scT], [scT])
        TT(scT[:], scT[:], bl(nw_in[:], 17), ALU.mult, [scT, nw_in], [scT])
        for n, E, T in (("p", C["E_p"], 128), ("s", C["E_s"], 64)):
            for half in range(2):
                b = pget()
                MM(b[:T, :], E[:, :T], gate17[:17, half * 512:(half + 1) * 512], [E, gate17], [b])
                CP_(gate_bc[n][:T, half * 512:(half + 1) * 512], b[:T, :], [b], [gate_bc[n]])
        mod_stacks.append(modes)

    def rows(ti):
        return (ti * 128, 128) if ti < NT else (SEQ, 64)

    def src_ap(layer_in, ti):
        r0, T = rows(ti)
        if layer_in is None:
            return (I["xp"][r0:r0 + T, :] if ti < NT else I["xs"][:, :]), None
        return layer_in[r0:r0 + T, :], (layer_in, ti)

    def prologue(cfg, xt):
        T = cfg.T
        ACT(junk[:T, :], xt[:T, :], AF.Square, [xt], [junk, sm1], accum_out=sm1[:T, 0:1])
        TS(sm1[:T, 0:1], sm1[:T, 0:1], 1.0 / D, EPS, ALU.mult, ALU.add, [sm1], [sm1])
        ACT(sm1[:T, 0:1], sm1[:T, 0:1], AF.Ln, [sm1], [sm1])
        ACT(sm1[:T, 0:1], sm1[:T, 0:1], AF.Exp, [sm1], [sm1], scale=-0.5)
        ACT(xn[:T, :], xt[:T, :], AF.Copy, [xt, sm1], [xn], scale=sm1[:T, 0:1])
        b = pget()
        for c in range(8):
            TR(pb(b, [8, T])[:, c, :], xn[:T, c * 128:(c + 1) * 128], identb[:T, :T], [xn, identb], [b])
        if cfg.nseq == 1:
            sc, sh = bl(scT[:, :, 0], T), bl(shT[:, :, 0], T)
            o1, o2, i1 = hTf_t[:, 0:8 * T].rearrange("p (c t) -> p c t", t=T), hT[:, :, :T], pb(b, [8, T])
        else:
            sc, sh = bl(scT[:, :, 1:17], SL), bl(shT[:, :, 1:17], SL)
            o1 = hTf_t[:, 0:8 * T].rearrange("p (c b l) -> p c b l", c=8, l=SL)
            o2 = hT[:, :, :T].rearrange("p c (b l) -> p c b l", l=SL)
            i1 = pb(b, [8, NSB, SL])
        TT(o1, i1, sc, ALU.mult, [b, scT], [hTf_t])
        TT(o2, o1, sh, ALU.add, [hTf_t, shT], [hT])

    def inproj_conv(cfg, l, wcol0, nchunks, bias_row, st_conv_in, conv_out, conv_out_s, ch0, is_last, first):
        T, L, nseq = cfg.T, cfg.L, cfg.nseq
        rawv = raw[:, :, 0:nseq * (L + 3)].rearrange("p c (b l) -> p c b l", l=L + 3)
        if cfg.nseq == 1:
            if first:
                MSET(raw[:, :, 0:3], 0.0, [raw])
            else:
                CP_(raw[:, 0:nchunks, 0:3], raw[:, 0:nchunks, L:L + 3], [raw], [raw])
        else:
            LD(cst_in[:, 0:nchunks * 128], st_conv_in[:, :, ch0 * 128:(ch0 + nchunks) * 128].rearrange("b k c -> (b k) c"), cst_in)
            for g0 in range(0, nchunks, 8):
                b = pget()
                ng = min(8, nchunks - g0)
                for c in range(ng):
                    TR(pf(b, [8, 48])[:, c, :], cst_in[:, (g0 + c) * 128:(g0 + c + 1) * 128], identf[:48, :48], [cst_in, identf], [b])
                CP_(rawv[:, g0:g0 + ng, :, 0:3], pf(b, [8, 48])[:, 0:ng, :].rearrange("p c (b k) -> p c b k", k=3), [b], [raw])
        if limit.get("stopC", 99) <= 1:
            return
        for g0 in range(0, nchunks, 4):
            b = pget()
            for c in range(4):
                for k in range(8):
                    MM(pf(b, [4, T])[:, c, :], PA["WB"][:, k, wcol0 + (g0 + c) * 128: wcol0 + (g0 + c + 1) * 128], hT[:, k, :T],
                       [PA["WB"], hT], [b], start=(k == 0), stop=(k == 7))
            if nseq == 1:
                ACT(raw[:, g0:g0 + 4, 3:3 + L], pf(b, [4, T]), AF.Copy, [b], [raw])
            else:
                for c in range(4):
                    ACT(rawv[:, g0 + c, :, 3:3 + L], pf(b, [4, T])[:, c, :].rearrange("p (b l) -> p b l", l=L), AF.Copy, [b], [raw])
            if is_last and limit.get("tail", True):
                for c in range(4):
                    ACT(tailf[:, g0 + c, 0:nseq * 3].rearrange("p (b k) -> p b k", k=3),
                        pf(b, [4, T])[:, c, :].rearrange("p (b l) -> p b l", l=L)[:, :, L - 3:L], AF.Copy, [b], [tailf])
        if limit.get("stopC", 99) <= 2:
            return
        for g0 in range(0, nchunks, 4):
            b = pget()
            for c in range(4):
                o = pf(b, [4, T])[:, c, :]
                if nseq > 1:
                    o = o.rearrange("p (b l) -> p b l", l=L)
                for k in range(4):
                    MM(o, DIAG[:, g0 + c, k, :], rawv[:, g0 + c, :, k:k + L] if nseq > 1 else raw[:, g0 + c, k:k + L],
                       [DIAG, raw], [b], start=(k == 0), stop=(k == 3 and bias_row is None))
                if bias_row is not None:
                    MM(pf(b, [4, T])[:, c, :], bias_row[0:1, (g0 + c) * 128:(g0 + c + 1) * 128], onesb[0:1, :T], [bias_row, onesb], [b],
                       start=False, stop=True)
            ACT(fT[:, g0:g0 + 4, :T], pf(b, [4, T]), AF.Silu, [b], [fT])
        if limit.get("stopC", 99) <= 3:
            return
        if is_last:
            n3 = nseq * 3
            for g0 in range(0, nchunks, 4):
                b = pget()
                for c in range(4):
                    TR(pf(b, [4, 128], p=n3)[:, c, :], tailf[:, g0 + c, 0:n3], identf[:, :], [tailf, identf], [b])
                CP_(tailo[:n3, g0 * 128:(g0 + 4) * 128], b[:n3, :], [b], [tailo])
            dst = conv_out[l][:, ch0 * 128:(ch0 + nchunks) * 128] if nseq == 1 else conv_out_s[l][:, ch0 * 128:(ch0 + nchunks) * 128]
            STO(dst, tailo[:n3, 0:nchunks * 128], tailo)

    def cum_and_tot(cfg, a_ap, a_buf, nh, out_c, out_dbc, pw):
        T, nch = cfg.T, cfg.nch
        b = pget()
        MM(b[:T, 0:nh], cfg.tri[:, :], a_ap, [cfg.tri, a_buf], [b])
        MM(b[:T, nh:2 * nh], cfg.blk[:, :], a_ap, [cfg.blk, a_buf], [b])
        ACT(out_c[:T, 0:2 * nh], b[:T, 0:2 * nh], AF.Copy, [b], [out_c])
        b2 = pget()
        MM(b2[:nch, 0:nh], cfg.cmT[:, :], a_ap, [cfg.cmT, a_buf], [b2])
        CP_(tot[:nch, 0:nh], b2[:nch, 0:nh], [b2], [tot])
        TT(totx[:nch, 0:nch * nh].rearrange("p (c h) -> p c h", h=nh), bm(tot[:nch, 0:nh], nch), bl(identf[:nch, :nch], nh), ALU.mult,
           [tot, identf], [totx])
        b3 = pget()
        MM(b3[:pw, 0:nch * nh], onesf[:nch, :pw], totx[:nch, 0:nch * nh], [onesf, totx], [b3])
        ACT(out_dbc[:pw, 0:nch * nh], b3[:pw, 0:nch * nh], AF.Exp, [b3], [out_dbc])

    tot = sb("tot", [16, 16])
    totx = sb("totx", [16, 256])
    sm = {k: sb("sm_" + k, [128, 64]) for k in ("a", "b", "c", "d", "e")}
    dbc = sb("dbc", [128, 256])
    acT = sb("acT", [16, 128])

    PA = {}

    def alloc_A(pes, tag):
        def sbp(name, shape, dt=F32):
            return Buf(pes.enter_context(nc.sbuf_tensor(name + tag, list(shape), dt)), name)
        PA.update(stf=sbp("stf", [128, D]), stb=sbp("stb", [128, D], BF16), hn=sbp("hn", [128, 8, 128]),
                  stf_s=sbp("stf_s", [128, D]), stb_s=sbp("stb_s", [128, D], BF16), Gs=sbp("Gs", [128, 2, 128]),
                  earg=sbp("earg", [128, 4, 128]), Ee=sbp("Ee", [128, 4, 128]), WT=sbp("WT", [128, 16, 128], BF16),
                  Btok=sbp("Btok", [128, 256], BF16), Btm=sbp("Btm", [128, 2, 256], BF16), CTm=sbp("CTm", [128, 2, 2, 128], BF16),
                  WB=sbp("WB", [128, 8, 2576], BF16))
    mslot = [0]

    def ssd_state_update(cfg, c, st_in, stb_in, st_out, yo, xdec, first_c, last_c):
        T, nch = cfg.T, cfg.nch
        Btm, CTm, Btok = PA["Btm"], PA["CTm"], PA["Btok"]
        sl = mslot[0] % 2
        mslot[0] += 1
        TS(Btm[:T, sl, :], Btok[:T, :], cfg.cmT[:, c:c + 1], 1.0, ALU.mult, ALU.mult, [Btok, cfg.cmT], [(Btm, sl)], eng="pool")
        TT(CTm[:, sl, :, :T], fT[:, 10:12, :T], bm(cfg.cmf[:, c, :], 2), ALU.mult, [fT, cfg.cmf], [(CTm, sl)], eng="pool")
        for g in range(2):
            MM(yo[g][:T, :], CTm[:, sl, g, :T], stb_in[:, g * 512:(g + 1) * 512], [(CTm, sl), stb_in], [yo[g]], start=first_c, stop=last_c)
        sp_ = [pget(), pget()]
        for g in range(2):
            MM(sp_[g][:, :], Btm[:T, sl, g * 128:(g + 1) * 128], xdec[:T, g * 512:(g + 1) * 512], [(Btm, sl), xdec], [sp_[g]])
        TT(big[3][:, :].rearrange("p (h q) -> p h q", q=64), st_in[:, :].rearrange("p (h q) -> p h q", q=64),
           bl(dbc[:, c * 16:(c + 1) * 16], 64), ALU.mult, [st_in, dbc], [big[3]])
        for g in range(2):
            TT(st_out[:, g * 512:(g + 1) * 512], big[3][:, g * 512:(g + 1) * 512], sp_[g][:, :], ALU.add, [big[3], sp_[g]], [st_out])

    def phaseA_tile(cfg, l, ti, xt, first, is_last):
        T, L, nseq, nch = cfg.T, cfg.L, cfg.nseq, cfg.nch
        stf, stb, hn, stf_s, stb_s, Gs, earg, Ee, WT, Btok = (PA[k] for k in ("stf", "stb", "hn", "stf_s", "stb_s", "Gs", "earg", "Ee", "WT", "Btok"))
        prologue(cfg, xt)
        if limit.get("stopA", 99) <= 0:
            return
        inproj_conv(cfg, l, OFF_XBC, 12, cbrow, I["st_sc"][l], O["ncs_p"], O["ncs_s"], 0, is_last, first)
        if ti == 0:
            DBG("fT_A", fT, fT[:, :, :T], [128, 12, T])
        if limit.get("stopA", 99) <= 1:
            return
        b = pget()
        for k in range(8):
            MM(b[:T, 0:16], hT[:, k, :T], PA["WB"][:, k, OFF_DT:OFF_DT + 16], [hT, PA["WB"]], [b], start=(k == 0), stop=(k == 7))
        A_, B_, C_, D_, E_ = (sm[k] for k in "abcde")
        TT(A_[:T, 0:16], b[:T, 0:16], vec16[:T, 0, :], ALU.add, [b, vec16], [A_])
        ACT(A_[:T, 0:16], A_[:T, 0:16], AF.Exp, [A_], [A_])
        ACT(A_[:T, 16:32], A_[:T, 0:16], AF.Ln, [A_], [A_], bias=1.0)
        TT(A_[:T, 32:48], A_[:T, 16:32], vec16[:T, 1, :], ALU.mult, [A_, vec16], [A_])
        cum_and_tot(cfg, A_[:T, 32:48], A_, 16, B_, dbc, 128)
        TT(C_[:T, 0:16], B_[:T, 16:32], B_[:T, 0:16], ALU.subtract, [B_], [C_])
        ACT(C_[:T, 0:16], C_[:T, 0:16], AF.Exp, [C_], [C_])
        ACT(C_[:T, 16:32], B_[:T, 0:16], AF.Exp, [B_], [C_])
        b = pget()
        TR(b[:16, 0:T], B_[:T, 0:16], identf[:T, :T], [B_, identf], [b])
        ACT(acT[:, :T], b[:16, 0:T], AF.Copy, [b], [acT])
        if ti == 0:
            DBG("acum", B_, B_[:T, 0:32], [T, 32])
        if limit.get("stopA", 99) <= 2:
            return
        bx, bb = pget(), pget()
        for c in range(8):
            TR(pb(bx, [8, 128], p=T)[:, c, :], fT[:, c, :T], identb[:, :], [fT, identb], [bx])
        for c in range(2):
            TR(pb(bb, [2, 128], p=T)[:, c, :], fT[:, 8 + c, :T], identb[:, :], [fT, identb], [bb])
        xc, xdec, xsD = bigb[0], bigb[1], big[0]
        TT(xc[:T, :].rearrange("p (h q) -> p h q", q=64), pb(bx, [16, 64], p=T), bl(A_[:T, 16:32], 64), ALU.mult, [bx, A_], [xc])
        TT(xsD[:T, :].rearrange("p (h q) -> p h q", q=64), pb(bx, [16, 64], p=T), bl(vec16[:T, 2, :], 64), ALU.mult, [bx, vec16], [xsD])
        TT(xdec[:T, :].rearrange("p (h q) -> p h q", q=64), xc[:T, :].rearrange("p (h q) -> p h q", q=64), bl(C_[:T, 0:16], 64), ALU.mult,
           [xc, C_], [xdec], eng="pool")
        ACT(Btok[:T, :], pb(bb, [256], p=T), AF.Copy, [bb], [Btok])
        b = pget()
        for g in range(2):
            MM(pf(b, [2, T], p=T)[:, g, :], fT[:, 8 + g, :T], fT[:, 10 + g, :T], [fT], [b])
        ACT(Gs[:T, :, :T], pf(b, [2, T], p=T), AF.Copy, [b], [Gs])
        for q in range(4):
            b = pget()
            for j in range(4):
                h = q * 4 + j
                MM(pf(b, [4, T], p=T)[:, j, :], identf[0:16, h:h + 1].to_broadcast([16, T]), acT[:, :T], [identf, acT], [b])
            for j in range(4):
                h = q * 4 + j
                STT(earg[:T, j, :T], pf(b, [4, T], p=T)[:, j, :], B_[:T, h:h + 1], cfg.nmI[:, :], ALU.subtract, ALU.add,
                    [b, B_, cfg.nmI], [earg])
            ACT(Ee[:T, :, :T], earg[:T, :, :T], AF.Exp, [earg], [Ee])
            TT(WT[:T, q * 4:q * 4 + 4, :T], Ee[:T, :, :T], bm(Gs[:T, q // 2, :T], 4), ALU.mult, [Ee, Gs], [WT])
        if limit.get("stopA", 99) <= 3:
            return
        yp = [pget(), pget()]
        for h in range(16):
            MM(yp[h // 8][:T, (h % 8) * 64:(h % 8 + 1) * 64], WT[:T, h, :T], xc[:T, h * 64:(h + 1) * 64], [WT, xc], [yp[h // 8]])
        ppin(yp[0]); ppin(yp[1])
        yo = [pget(), pget()]
        ppin(yo[0]); ppin(yo[1])
        if cfg.chain:
            if first:
                MSET(stf[:, :], 0.0, [stf])
                MSET(stb[:, :], 0.0, [stb])
            for c in range(nch):
                ssd_state_update(cfg, c, stf, stb, stf, yo, xdec, c == 0, c == nch - 1)
                ACT(stb[:, :], stf[:, :], AF.Copy, [stf], [stb])
            if is_last:
                for half in range(2):
                    b = pget()
                    for k in range(4):
                        kk = half * 4 + k
                        TR(pf(b, [4, 128])[:, k, :], stf[:, :].rearrange("n (j k) -> n k j", k=8)[:, kk, :], identf[:, :], [stf, identf], [b])
                    CP_(hn[:, half * 4:half * 4 + 4, :], pf(b, [4, 128]), [b], [hn])
                STO(O["nssm_p"][l].rearrange("(p k) n -> p k n", k=8), hn[:], hn)
        else:
            for c in range(limit.get("nchs", nch)):
                LD(hn[:], I["st_ssm"][l, c].rearrange("h q n -> (h q) n").rearrange("(p k) n -> p k n", k=8), hn)
                if limit.get("sub", 9) <= 0:
                    continue
                for half in range(2):
                    b = pget()
                    for k in range(4):
                        kk = half * 4 + k
                        TR(pf(b, [4, 128])[:, k, :], hn[:, kk, :], identf[:, :], [hn, identf], [b])
                    if limit.get("sub", 9) <= 1:
                        continue
                    ACT(stf_s[:, :].rearrange("n (j k) -> n k j", k=8)[:, half * 4:half * 4 + 4, :], pf(b, [4, 128]), AF.Copy, [b], [stf_s])
                    if limit.get("sub", 9) <= 2:
                        continue
                    CP_(stb_s[:, :].rearrange("n (j k) -> n k j", k=8)[:, half * 4:half * 4 + 4, :], pf(b, [4, 128]), [b], [stb_s])
                if limit.get("noupd"):
                    continue
                ssd_state_update(cfg, c, stf_s, stb_s, stf_s, yo, xdec, c == 0, c == limit.get("nchs", nch) - 1)
                if limit.get("noback"):
                    continue
                for half in range(2):
                    b = pget()
                    for k in range(4):
                        kk = half * 4 + k
                        TR(pf(b, [4, 128])[:, k, :], stf_s[:, :].rearrange("n (j k) -> n k j", k=8)[:, kk, :], identf[:, :], [stf_s, identf], [b])
                    CP_(hn[:, half * 4:half * 4 + 4, :], pf(b, [4, 128]), [b], [hn])
                STO(O["nssm_s"][l, c].rearrange("(p k) n -> p k n", k=8), hn[:], hn)
        if limit.get("stopA", 99) <= 4:
            punpin(yp[0]); punpin(yp[1]); punpin(yo[0]); punpin(yo[1])
            return
        t1, t2 = big[1], big[2]
        for g in range(2):
            TT(t1[:T, g * 512:(g + 1) * 512].rearrange("p (h q) -> p h q", q=64), yo[g][:T, :].rearrange("p (h q) -> p h q", q=64),
               bl(C_[:T, 16 + g * 8:16 + g * 8 + 8], 64), ALU.mult, [yo[g], C_], [t1])
            TT(t2[:T, g * 512:(g + 1) * 512], yp[g][:T, :], t1[:T, g * 512:(g + 1) * 512], ALU.add, [yp[g], t1], [t2])
        for x_ in yp + yo:
            punpin(x_)
        TT(t2[:T, :], t2[:T, :], xsD[:T, :], ALU.add, [t2, xsD], [t2], eng="pool")
        if ti == 0:
            DBG("yssd", t2, t2[:T, :], [T, D])
        for g in range(2):
            b = pget()
            for k in range(8):
                MM(b[:T, :], hT[:, k, :T], PA["WB"][:, k, OFF_ZS + g * 512:OFF_ZS + (g + 1) * 512], [hT, PA["WB"]], [b], start=(k == 0), stop=(k == 7))
            ACT(t1[:T, g * 512:(g + 1) * 512], b[:T, :], AF.Silu, [b], [t1])
        TT(t2[:T, :], t2[:T, :], t1[:T, :], ALU.mult, [t2, t1], [t2])
        for g in range(2):
            ACT(junk[:T, g * 512:(g + 1) * 512], t2[:T, g * 512:(g + 1) * 512], AF.Square, [t2], [junk, sm1], accum_out=sm1[:T, 2 + g:3 + g])
        TS(sm1[:T, 2:4], sm1[:T, 2:4], 1.0 / 512, EPS, ALU.mult, ALU.add, [sm1], [sm1])
        ACT(sm1[:T, 2:4], sm1[:T, 2:4], AF.Ln, [sm1], [sm1])
        ACT(sm1[:T, 2:4], sm1[:T, 2:4], AF.Exp, [sm1], [sm1], scale=-0.5)
        ynb = bigb[2]
        for g in range(2):
            ACT(ynb[:T, g * 512:(g + 1) * 512], t2[:T, g * 512:(g + 1) * 512], AF.Copy, [t2, sm1], [ynb], scale=sm1[:T, 2 + g:3 + g])
        out_proj(cfg, ynb, 8)

    def out_proj(cfg, ynb, nk):
        T = cfg.T
        b = pget()
        for c in range(nk):
            TR(pb(b, [8, T])[:, c, :], ynb[:T, c * 128:(c + 1) * 128], identb[:T, :T], [ynb, identb], [b])
        CP_(hT[:, 0:nk, :T], pb(b, [8, T])[:, 0:nk, :], [b], [hT])
        for half in range(2):
            b = pget()
            for k in range(nk):
                MM(b[:T, :], hT[:, k, :T], WO[:, k, half * 512:(half + 1) * 512], [hT, WO], [b], start=(k == 0), stop=(k == nk - 1))
            TT(big[3][:T, half * 512:(half + 1) * 512], b[:T, :], gate_bc[cfg.name][:T, half * 512:(half + 1) * 512], ALU.mult,
               [b, gate_bc[cfg.name]], [big[3]])

    PB = {}

    def alloc_B(pes, tag):
        def sbp(name, shape, dt=F32):
            return Buf(pes.enter_context(nc.sbuf_tensor(name + tag, list(shape), dt)), name)
        PB.update(Sp_f=sbp("Sp_f", [128, 4, 128]), Sp_b=sbp("Sp_b", [128, 4, 128], BF16), Ss_in=sbp("Ss_in", [128, 16, 128]),
                  Ss_b=sbp("Ss_b", [128, 16, 128], BF16), kv=sbp("kv", [128, 8, 128], BF16), kbg=sbp("kbg", [128, 4, 128]),
                  vb_=sbp("vb", [128, 4, 128]), kdec=sbp("kdec", [128, 4, 128], BF16), kdm=sbp("kdm", [64, 2, 128], BF16),
                  LG=sbp("LG", [128, 12]), LGT=sbp("LGT", [12, 128]), ea=sbp("ea", [128, 4, 128]),
                  E2a=sbp("E2a", [128, 4, 128]), E2b=sbp("E2b", [128, 4, 128]),
                  X0=sbp("X0", [128, 4, 128]), X1=sbp("X1", [128, 4, 128]), Y0=sbp("Y0", [128, 4, 128]), Y1=sbp("Y1", [128, 4, 128]),
                  R0=sbp("R0", [128, 4, 128]), R1=sbp("R1", [128, 4, 128]), attnT=sbp("attnT", [128, 4, 128], BF16),
                  egb=sbp("egb", [128, 4, 128]), qsq=sbp("qsq", [128, 4, 128], BF16),
                  nwTm=sbp("nwTm", [128, 1024], BF16), qdm=sbp("qdm", [128, 1024], BF16), u_sb=sbp("u_sb", [128, 4, 128]),
                  vnew=sbp("vnew", [128, 4, 128], BF16), dbcS=sbp("dbcS", [128, 64]), WB=sbp("WB", [128, 8, 2056], BF16))
        PB["E2"] = None

    def r32(ap):
        return ap.bitcast(F32R)

    def phaseB_tile(cfg, l, ti, xt, h0, first, is_last):
        T, L, nseq, nch = cfg.T, cfg.L, cfg.nseq, cfg.nch
        (Sp_f, Sp_b, Ss_in, Ss_b, kv, kbg, vb_, kdec, kdm, LG, LGT, ea, attnT, egb, nwTm, qdm, u_sb, vnew, dbcS) = (
            PB[k] for k in ("Sp_f", "Sp_b", "Ss_in", "Ss_b", "kv", "kbg", "vb_", "kdec", "kdm", "LG", "LGT", "ea", "attnT",
                            "egb", "nwTm", "qdm", "u_sb", "vnew", "dbcS"))
        E2 = [PB["E2a"], PB["E2b"]]
        Ss_out = Ss_in
        Xs, Ys, Rs = [PB["X0"], PB["X1"]], [PB["Y0"], PB["Y1"]], [PB["R0"], PB["R1"]]
        WB = PB["WB"]
        prologue(cfg, xt)
        inproj_conv_B(cfg, l, h0, is_last, first)
        A_, B_, C_, D_, E_ = (sm[k] for k in "abcde")
        b = pget()
        for k in range(8):
            MM(b[:T, 0:8], hT[:, k, :T], WB[:, k, 2048:2056], [hT, WB], [b], start=(k == 0), stop=(k == 7))
        TT(A_[:T, 0:4], b[:T, 0:4], vec8[:T, 0, h0:h0 + 4], ALU.add, [b, vec8], [A_])
        ACT(A_[:T, 0:4], A_[:T, 0:4], AF.Exp, [A_], [A_])
        ACT(A_[:T, 0:4], A_[:T, 0:4], AF.Ln, [A_], [A_], bias=1.0)
        TT(A_[:T, 4:8], A_[:T, 0:4], vec8[:T, 1, h0:h0 + 4], ALU.mult, [A_, vec8], [A_])
        ACT(A_[:T, 8:12], b[:T, 4:8], AF.Exp, [b], [A_], scale=-1.0)
        ACT(A_[:T, 8:12], A_[:T, 8:12], AF.Ln, [A_], [A_], bias=1.0)
        ACT(A_[:T, 12:16], A_[:T, 8:12], AF.Exp, [A_], [A_], scale=-1.0)
        ACT(A_[:T, 16:20], A_[:T, 8:12], AF.Copy, [A_], [A_], scale=-1.0)
        cum_and_tot(cfg, A_[:T, 4:8], A_, 4, B_, dbcS, 128)
        ACT(C_[:T, 0:4], B_[:T, 0:4], AF.Exp, [B_], [C_])
        TT(C_[:T, 4:8], B_[:T, 4:8], B_[:T, 0:4], ALU.subtract, [B_], [C_])
        ACT(C_[:T, 4:8], C_[:T, 4:8], AF.Exp, [C_], [C_])
        b = pget()
        for c in range(8):
            TR(pb(b, [8, 128], p=T)[:, c, :], fT[:, 4 + c, :T], identb[:, :], [fT, identb], [b])
        ACT(kv[:T, :, :], pb(b, [8, 128], p=T), AF.Copy, [b], [kv])
        ksq = big[0]
        TT(ksq[:T, 0:512].rearrange("p (h d) -> p h d", d=128), kv[:T, 0:4, :], kv[:T, 0:4, :], ALU.mult, [kv], [ksq])
        RED(D_[:T, 0:4], ksq[:T, 0:512].rearrange("p (h d) -> p h d", d=128), [ksq], [D_])
        TS(D_[:T, 0:4], D_[:T, 0:4], EPS, None, ALU.add, ALU.bypass, [D_], [D_])
        ACT(D_[:T, 4:8], D_[:T, 0:4], AF.Ln, [D_], [D_])
        ACT(D_[:T, 8:12], D_[:T, 4:8], AF.Exp, [D_], [D_], scale=-0.5)
        qsq = PB["qsq"]
        ACT(qsq[:, :, :T], fT[:, 0:4, :T], AF.Square, [fT], [qsq])
        b = pget()
        for j in range(4):
            MM(b[:T, j:j + 1], qsq[:, j, :T], onesb[:, 0:1], [qsq, onesb], [b])
        TS(D_[:T, 12:16], b[:T, 0:4], EPS, None, ALU.add, ALU.bypass, [b], [D_])
        ACT(D_[:T, 12:16], D_[:T, 12:16], AF.Ln, [D_], [D_])
        ACT(D_[:T, 16:20], D_[:T, 12:16], AF.Exp, [D_], [D_], scale=-0.5)
        TS(D_[:T, 16:20], D_[:T, 16:20], float(128 ** -0.5), None, ALU.mult, ALU.bypass, [D_], [D_])
        TT(E_[:T, 0:4], D_[:T, 8:12], A_[:T, 12:16], ALU.mult, [D_, A_], [E_])
        TT(E_[:T, 4:8], E_[:T, 0:4], C_[:T, 0:4], ALU.mult, [E_, C_], [E_])
        TT(E_[:T, 8:12], D_[:T, 8:12], C_[:T, 4:8], ALU.mult, [D_, C_], [E_])
        TS(E_[:T, 12:16], E_[:T, 0:4], -1.0, None, ALU.mult, ALU.bypass, [E_], [E_])
        TS(E_[:T, 16:20], D_[:T, 8:12], -1.0, None, ALU.mult, ALU.bypass, [D_], [E_])
        CP_(LG[:T, 0:4], B_[:T, 0:4], [B_], [LG])
        STT(LG[:T, 4:8], D_[:T, 4:8], -0.5, B_[:T, 0:4], ALU.mult, ALU.subtract, [D_, B_], [LG])
        STT(LG[:T, 8:12], D_[:T, 4:8], -0.5, B_[:T, 0:4], ALU.mult, ALU.add, [D_, B_], [LG])
        TT(LG[:T, 8:12], LG[:T, 8:12], A_[:T, 16:20], ALU.add, [LG, A_], [LG])
        b = pget()
        TR(b[:12, 0:T], LG[:T, 0:12], identf[:T, :T], [LG, identf], [b])
        ACT(LGT[:, :T], b[:12, 0:T], AF.Copy, [b], [LGT])
        TT(r32(kbg[:T, :, :]), kv[:T, 0:4, :], bl(E_[:T, 4:8], 128), ALU.mult, [kv, E_], [kbg])
        TT(kdec[:T, :, :], kv[:T, 0:4, :], bl(E_[:T, 8:12], 128), ALU.mult, [kv, E_], [kdec])
        TT(r32(vb_[:T, :, :]), kv[:T, 4:8, :], bl(A_[:T, 12:16], 128), ALU.mult, [kv, A_], [vb_])
        bG, bA = pget(), pget()
        for j in range(4):
            MM(pf(bG, [4, T], p=T)[:, j, :], fT[:, 4 + j, :T], fT[:, 4 + j, :T], [fT], [bG])
        for j in range(4):
            MM(pf(bA, [4, T], p=T)[:, j, :], fT[:, 4 + j, :T], fT[:, j, :T], [fT], [bA])
        bcs = [pget(), pget(), pget()]
        for r in range(3):
            for j in range(4):
                MM(pf(bcs[r], [4, T], p=T)[:, j, :], identf[0:12, r * 4 + j:r * 4 + j + 1].to_broadcast([12, T]), LGT[:, :T],
                   [identf, LGT], [bcs[r]])
        be = pget()
        for j in range(4):
            MM(pf(be, [4, T])[:, j, :], identf[0:12, j:j + 1].to_broadcast([12, 128]), LGT[:, :T], [identf, LGT], [be])
        X, Y, R = Xs[0], Ys[0], Rs[0]
        kinds = ((1, ALU.add, cfg.nmSr, 0), (2, ALU.subtract, cfg.nmS, 1), (0, ALU.subtract, cfg.nmI, 2))
        for r, op0, msk, ki in kinds:
            for j in range(4):
                STT(ea[:T, j, :T], pf(bcs[r], [4, T], p=T)[:, j, :], B_[:T, j:j + 1], msk[:, :], op0, ALU.add, [bcs[r], B_, msk], [ea])
            Ek = E2[ki % 2]
            ACT(Ek[:T, :, :T], ea[:T, :, :T], AF.Exp, [ea], [Ek])
            for j in range(4):
                if ki == 0:
                    STT(r32(X[:T, j, :T]), pf(bG, [4, T], p=T)[:, j, :], E_[:T, 12 + j:13 + j], Ek[:T, j, :T], ALU.mult, ALU.mult, [bG, E_, Ek], [X])
                elif ki == 1:
                    STT(r32(Y[:T, j, :T]), pf(bG, [4, T], p=T)[:, j, :], E_[:T, 16 + j:17 + j], Ek[:T, j, :T], ALU.mult, ALU.mult, [bG, E_, Ek], [Y])
                else:
                    STT(attnT[:T, j, :T], pf(bA, [4, T], p=T)[:, j, :], D_[:T, 8 + j:9 + j], Ek[:T, j, :T], ALU.mult, ALU.mult, [bA, D_, Ek], [attnT])
        ACT(egb[:, :, :T], pf(be, [4, T]), AF.Exp, [be], [egb])
        TT(r32(R[:T, :, :T]), Y[:T, :, :T], bm(identf[:T, :T], 4), ALU.add, [Y, identf], [R])
        for lev in range(cfg.nlev):
            X2, Y2, R2 = Xs[(lev + 1) % 2], Ys[(lev + 1) % 2], Rs[(lev + 1) % 2]
            lastlev = (lev == cfg.nlev - 1)
            bX, bY, bR = pget(), (None if lastlev else pget()), pget()
            for j in range(4):
                MM(pf(bX, [4, T], p=T)[:, j, :], r32(Y[:T, j, :T]), r32(X[:T, j, :T]), [X, Y], [bX])
            if not lastlev:
                for j in range(4):
                    MM(pf(bY, [4, T], p=T)[:, j, :], r32(X[:T, j, :T]), r32(Y[:T, j, :T]), [X, Y], [bY])
            ACT(r32(X2[:T, :, :T]), pf(bX, [4, T], p=T), AF.Copy, [bX], [X2])
            if not lastlev:
                CP_(r32(Y2[:T, :, :T]), pf(bY, [4, T], p=T), [bY], [Y2])
            for j in range(4):
                MM(pf(bR, [4, T], p=T)[:, j, :], r32(X2[:T, j, :T]), r32(R[:T, j, :T]), [X2, R], [bR])
            TT(r32(R2[:T, :, :T]), R[:T, :, :T], pf(bR, [4, T], p=T), ALU.add, [R, bR], [R2])
            X, Y, R = X2, Y2, R2
        bU, bW = pget(), pget()
        for j in range(4):
            MM(pf(bU, [4, 128], p=T)[:, j, :], r32(R[:T, j, :T]), r32(vb_[:T, j, :]), [R, vb_], [bU])
        for j in range(4):
            MM(pf(bW, [4, T])[:, j, :], r32(kbg[:T, j, :]), r32(R[:T, j, :T]), [kbg, R], [bW])
        ACT(u_sb[:T, :, :], pf(bU, [4, 128], p=T), AF.Copy, [bU], [u_sb])
        TT(egb[:, :, :T], fT[:, 0:4, :T], egb[:, :, :T], ALU.mult, [fT, egb], [egb])
        vnb, ob = pget(), pget()
        ppin(vnb)
        ppin(ob)
        vn4, o4 = pf(vnb, [4, 128], p=T), pf(ob, [4, 128], p=T)
        if cfg.chain:
            if first:
                MSET(Sp_f[:], 0.0, [Sp_f])
                MSET(Sp_b[:], 0.0, [Sp_b])
            nwv = nwTm[:, 0:nch * 4 * T].rearrange("p (c j t) -> p c j t", c=nch, j=4)
            qdv = qdm[:, 0:nch * 4 * T].rearrange("p (c j t) -> p c j t", c=nch, j=4)
            for c in range(nch):
                TT(nwv[:, c], pf(bW, [4, T]), bm(cfg.ncmf[:, c, :], 4), ALU.mult, [bW, cfg.ncmf], [(nwTm, c)])
                TT(qdv[:, c], egb[:, :, :T], bm(cfg.cmf[:, c, :], 4), ALU.mult, [egb, cfg.cmf], [(qdm, c)], eng="pool")
            for c in range(nch):
                for j in range(4):
                    MM(vn4[:, j, :], nwv[:, c, j, :], Sp_b[:, j, :], [(nwTm, c), Sp_b], [vnb], start=(j == 0 and c == 0), stop=False, skip_group_check=True)
                for j in range(4):
                    MM(o4[:, j, :], qdv[:, c, j, :], Sp_b[:, j, :], [(qdm, c), Sp_b], [ob], start=(j == 0 and c == 0), stop=False, skip_group_check=True)
                TT(vnew[:T, :, :], vn4, u_sb[:T, :, :], ALU.add, [vnb, u_sb], [vnew])
                bs = pget()
                for j in range(4):
                    MM(pf(bs, [4, 128])[:, j, :], kdec[c * 64:(c + 1) * 64, j, :], vnew[c * 64:(c + 1) * 64, j, :], [kdec, vnew], [bs])
                tmpS = big[0][:, 512:1024].rearrange("p (j v) -> p j v", v=128)
                TT(tmpS, Sp_f[:, :, :], bl(dbcS[:, c * 4:(c + 1) * 4], 128), ALU.mult, [Sp_f, dbcS], [big[0]])
                TT(Sp_f[:, :, :], tmpS, pf(bs, [4, 128]), ALU.add, [big[0], bs], [Sp_f])
                ACT(Sp_b[:, :, :], Sp_f[:, :, :], AF.Copy, [Sp_f], [Sp_b])
            if is_last:
                STO(O["ngdn_p"][l, h0:h0 + 4].rearrange("h k v -> k h v"), Sp_f[:], Sp_f)
        else:
            nwv = nwTm[:, 0:nch * T].rearrange("p (c t) -> p c t", t=T)
            qdv = qdm[:, 0:nch * T].rearrange("p (c t) -> p c t", t=T)
            ppin(bW)
            for j in range(4):
                h = h0 + j
                LD(Ss_in[:], I["st_gdn"][l, :, h].rearrange("b k v -> k b v"), Ss_in)
                CP_(Ss_b[:], Ss_in[:], [Ss_in], [Ss_b], eng="pool")
                TT(nwv, bm(pf(bW, [4, T])[:, j, :], nch), cfg.ncmf[:, :, :], ALU.mult, [bW, cfg.ncmf], [nwTm])
                TT(qdv, bm(egb[:, j, :T], nch), cfg.cmf[:, :, :], ALU.mult, [egb, cfg.cmf], [qdm], eng="pool")
                for c in range(nch):
                    MM(vn4[:, j, :], nwv[:, c, :], Ss_b[:, c, :], [nwTm, Ss_b], [vnb], start=(j == 0 and c == 0), stop=False, skip_group_check=True)
                for c in range(nch):
                    MM(o4[:, j, :], qdv[:, c, :], Ss_b[:, c, :], [qdm, Ss_b], [ob], start=(j == 0 and c == 0), stop=False, skip_group_check=True)
                TT(vnew[:T, j, :], vn4[:, j, :], u_sb[:T, j, :], ALU.add, [vnb, u_sb], [vnew])
                tmpS = big[2][:, :].rearrange("p (c v) -> p c v", v=128)
                for hf in range(2):
                    TT(tmpS, Ss_in[:, hf * 8:(hf + 1) * 8, :],
                       bl(dbcS[:, 0:nch * 4].rearrange("p (c j) -> p c j", j=4)[:, hf * 8:(hf + 1) * 8, j], 128), ALU.mult, [Ss_in, dbcS], [big[2]])
                    for c4 in range(hf * 8, hf * 8 + 8, 4):
                        bs = pget()
                        for cc in range(4):
                            c = c4 + cc
                            sl = mslot[0] % 2
                            mslot[0] += 1
                            TS(kdm[:T, sl, :], kdec[:T, j, :], cfg.cmT[:, c:c + 1], 1.0, ALU.mult, ALU.mult, [kdec, cfg.cmT], [(kdm, sl)], eng="pool")
                            MM(pf(bs, [4, 128])[:, cc, :], kdm[:T, sl, :], vnew[:T, j, :], [(kdm, sl), vnew], [bs])
                        TT(Ss_out[:, c4:c4 + 4, :], tmpS[:, c4 - hf * 8:c4 - hf * 8 + 4, :], pf(bs, [4, 128]), ALU.add, [big[2], bs], [Ss_out])
                STO(O["ngdn_s"][l, :, h].rearrange("b k v -> k b v"), Ss_out[:], Ss_out)
            punpin(bW)
        for j in range(4):
            MM(o4[:, j, :], attnT[:T, j, :T], vnew[:T, j, :], [attnT, vnew], [ob], start=False, stop=True, skip_group_check=True)
        punpin(vnb)
        of = big[1]
        ACT(of[:T, 0:512], ob[:T, :], AF.Copy, [ob], [of])
        punpin(ob)
        if ti == 0 and h0 == 0:
            DBG("ogdn", of, of[:T, 0:512], [T, 512])
        TT(ksq[:T, 0:512], of[:T, 0:512], of[:T, 0:512], ALU.mult, [of], [ksq])
        RED(C_[:T, 8:12], ksq[:T, 0:512].rearrange("p (h d) -> p h d", d=128), [ksq], [C_])
        TT(C_[:T, 12:16], D_[:T, 16:20], D_[:T, 16:20], ALU.mult, [D_], [C_])
        TT(C_[:T, 8:12], C_[:T, 8:12], C_[:T, 12:16], ALU.mult, [C_], [C_])
        TS(C_[:T, 8:12], C_[:T, 8:12], 1.0 / 128, EPS, ALU.mult, ALU.add, [C_], [C_])
        ACT(C_[:T, 8:12], C_[:T, 8:12], AF.Ln, [C_], [C_])
        ACT(C_[:T, 8:12], C_[:T, 8:12], AF.Exp, [C_], [C_], scale=-0.5)
        TT(C_[:T, 8:12], C_[:T, 8:12], D_[:T, 16:20], ALU.mult, [C_, D_], [C_])
        b = pget()
        for k in range(8):
            MM(b[:T, :], hT[:, k, :T], WB[:, k, 0:512], [hT, WB], [b], start=(k == 0), stop=(k == 7))
        sz = big[2]
        ACT(sz[:T, 0:512], b[:T, :], AF.Silu, [b], [sz])
        TT(of[:T, 0:512].rearrange("p (h d) -> p h d", d=128), of[:T, 0:512].rearrange("p (h d) -> p h d", d=128), bl(C_[:T, 8:12], 128),
           ALU.mult, [of, C_], [of])
        onb = bigb[2]
        TT(onb[:T, 0:512], of[:T, 0:512], sz[:T, 0:512], ALU.mult, [of, sz], [onb])
        out_proj(cfg, onb, 4)

    def inproj_conv_B(cfg, l, h0, is_last, first):
        T, L, nseq = cfg.T, cfg.L, cfg.nseq
        rawv = raw[:, :, 0:nseq * (L + 3)].rearrange("p c (b l) -> p c b l", l=L + 3)
        if nseq == 1:
            if first:
                MSET(raw[:, :, 0:3], 0.0, [raw])
            else:
                CP_(raw[:, :, 0:3], raw[:, :, L:L + 3], [raw], [raw])
        else:
            for part in range(3):
                cs0 = part * 1024 + h0 * 128
                LD(cst_in[:, part * 512:(part + 1) * 512], I["st_gc"][l][:, :, cs0:cs0 + 512].rearrange("b k c -> (b k) c"), cst_in)
            for g0 in (0, 8):
                b = pget()
                ng = min(8, 12 - g0)
                for c in range(ng):
                    TR(pf(b, [8, 48])[:, c, :], cst_in[:, (g0 + c) * 128:(g0 + c + 1) * 128], identf[:48, :48], [cst_in, identf], [b])
                CP_(rawv[:, g0:g0 + ng, :, 0:3], pf(b, [8, 48])[:, 0:ng, :].rearrange("p c (b k) -> p c b k", k=3), [b], [raw])
        for g0 in range(0, 12, 4):
            b = pget()
            for c in range(4):
                for k in range(8):
                    MM(pf(b, [4, T])[:, c, :], PB["WB"][:, k, 512 + (g0 + c) * 128: 512 + (g0 + c + 1) * 128], hT[:, k, :T],
                       [PB["WB"], hT], [b], start=(k == 0), stop=(k == 7))
            if nseq == 1:
                ACT(raw[:, g0:g0 + 4, 3:3 + L], pf(b, [4, T]), AF.Copy, [b], [raw])
            else:
                for c in range(4):
                    ACT(rawv[:, g0 + c, :, 3:3 + L], pf(b, [4, T])[:, c, :].rearrange("p (b l) -> p b l", l=L), AF.Copy, [b], [raw])
            if is_last and limit.get("tail", True):
                for c in range(4):
                    ACT(tailf[:, g0 + c, 0:nseq * 3].rearrange("p (b k) -> p b k", k=3),
                        pf(b, [4, T])[:, c, :].rearrange("p (b l) -> p b l", l=L)[:, :, L - 3:L], AF.Copy, [b], [tailf])
        for g0 in range(0, 12, 4):
            b = pget()
            for c in range(4):
                o = pf(b, [4, T])[:, c, :]
                if nseq > 1:
                    o = o.rearrange("p (b l) -> p b l", l=L)
                for k in range(4):
                    MM(o, DIAG[:, g0 + c, k, :], rawv[:, g0 + c, :, k:k + L] if nseq > 1 else raw[:, g0 + c, k:k + L],
                       [DIAG, raw], [b], start=(k == 0), stop=(k == 3))
            ACT(fT[:, g0:g0 + 4, :T], pf(b, [4, T]), AF.Silu, [b], [fT])
        if is_last:
            n3 = nseq * 3
            for g0 in range(0, 12, 4):
                b = pget()
                for c in range(4):
                    TR(pf(b, [4, 128], p=n3)[:, c, :], tailf[:, g0 + c, 0:n3], identf[:, :], [tailf, identf], [b])
                CP_(tailo[:n3, g0 * 128:(g0 + 4) * 128], b[:n3, :], [b], [tailo])
            dst = O["ngc_p"] if nseq == 1 else O["ngc_s"]
            for part in range(3):
                cs0 = part * 1024 + h0 * 128
                STO(dst[l][:, cs0:cs0 + 512], tailo[:n3, part * 512:(part + 1) * 512], tailo)

    def bvec(dst_ap, src_row_ap, buf):
        LD(dst_ap, src_row_ap.partition_broadcast(128), buf)

    tiles = [(CP, ti) for ti in range(limit.get("ntiles", NT))] + ([(CS, NT)] if limit.get("sample", True) else [])
    all_bufs.extend(allb)
    layer_in = None
    for l in range(limit.get("layers", DEPTH)):
        do_mod(l)
        S.barrier(all_bufs)
        mod_stacks.pop().close()
        for ph in limit.get("phases", (0, 1, 2)):
            acc_in = layer_in if ph == 0 else scr[(ph - 1)]
            last_phase = (l == DEPTH - 1 and ph == 2)
            acc_out = scr[ph] if ph < 2 else (scr[2] if not last_phase else None)
            S.barrier(all_bufs)
            pes = contextlib.ExitStack()
            cur_es[0] = pes
            if ph == 0:
                alloc_A(pes, "_%d_%d" % (l, ph))
            else:
                alloc_B(pes, "_%d_%d" % (l, ph))
            for v_ in list(PA.values()) + list(PB.values()):
                if v_ is not None and v_ not in all_bufs:
                    all_bufs.append(v_)
            if ph == 0:
                load_cols(PA["WB"], 0, I["w_in"][l][:, 0:2576], 2576)
                LD(nwT[:], I["ssd_norm_w"][l].rearrange("(k p) -> p k", p=128), nwT, nonc=True)
                load_wo(I["w_out"][l][0:1024, :], 8, lambda k: nwT[:, k:k + 1])
                build_diag(I["ssd_conv_w"][l], 0, 12)
                LD(cbrow_f[0:1, :], I["ssd_conv_b"][l:l + 1, :], cbrow_f)
                CP_(cbrow[:], cbrow_f[0:1, :], [cbrow_f], [cbrow])
                bvec(vec16[:, 0, :], I["ssd_dt_bias"][l:l + 1, :], vec16)
                bvec(vec16[:, 1, :], I["ssd_a_log"][l:l + 1, :], vec16)
                bvec(vec16[:, 2, :], I["ssd_d"][l:l + 1, :], vec16)
                ACT(vec16[:, 1, :], vec16[:, 1, :], AF.Exp, [vec16], [vec16])
                TS(vec16[:, 1, :], vec16[:, 1, :], -1.0, None, ALU.mult, ALU.bypass, [vec16], [vec16])
            else:
                h0 = (ph - 1) * 4
                load_cols(PB["WB"], 0, I["w_in"][l][:, OFF_ZG + h0 * 128:OFF_ZG + h0 * 128 + 512], 512)
                for part in range(3):
                    c0 = OFF_QKV + part * 1024 + h0 * 128
                    load_cols(PB["WB"], 512 + part * 512, I["w_in"][l][:, c0:c0 + 512], 512)
                load_cols(PB["WB"], 2048, I["w_in"][l][:, OFF_A + h0:OFF_A + h0 + 4], 4)
                load_cols(PB["WB"], 2052, I["w_in"][l][:, OFF_B + h0:OFF_B + h0 + 4], 4)
                LD(nwT[:, 0:1], I["gdn_norm_w"][l].rearrange("(p o) -> p o", o=1), nwT, nonc=True)
                load_wo(I["w_out"][l][1024 + h0 * 128:1024 + h0 * 128 + 512, :], 4, lambda k: nwT[:, 0:1])
                for part in range(3):
                    for k_ in range(4):
                        LD(cwT[:, part * 4:part * 4 + 4, k_],
                           I["gdn_conv_w"][l][k_, part * 1024 + h0 * 128:part * 1024 + h0 * 128 + 512].rearrange("(c p) -> p c", p=128), cwT, nonc=True)
                for c in range(12):
                    TT(DIAG[:, c, :, :], bm(identf[:], 4), bl(cwT[:, c, :], 128), ALU.mult, [identf, cwT], [DIAG], eng="pool")
                bvec(vec8[:, 0, :], I["gdn_dt_bias"][l:l + 1, :], vec8)
                bvec(vec8[:, 1, :], I["gdn_a_log"][l:l + 1, :], vec8)
                ACT(vec8[:, 1, :], vec8[:, 1, :], AF.Exp, [vec8], [vec8])
                TS(vec8[:, 1, :], vec8[:, 1, :], -1.0, None, ALU.mult, ALU.bypass, [vec8], [vec8])

            def issue_loads(idx):
                cfg, ti = tiles[idx]
                r0, T = rows(ti)
                xt = xt_slots[idx % 2]
                ap, dep = src_ap(layer_in.t if layer_in is not None else None, ti)
                LD(xt[:T, :], ap, xt, R=[(layer_in, ti)] if layer_in is not None else None)
                if ph > 0:
                    xa = xa_slots[idx % 2]
                    LD(xa[:T, :], acc_in.t[r0:r0 + T, :], xa, R=[(acc_in, ti)])

            if last_phase:
                fnw_bc = stage[0]
                fnw_v = stage[0][:, :, :].rearrange("p a b -> p (a b)")
                LD(fnw_v, I["final_norm_w"].partition_broadcast(128), stage[0])
            issue_loads(0)
            for idx, (cfg, ti) in enumerate(tiles):
                if idx + 1 < len(tiles):
                    issue_loads(idx + 1)
                r0, T = rows(ti)
                xt = xt_slots[idx % 2]
                xa = xa_slots[idx % 2] if ph > 0 else xt
                first = (ti == 0)
                is_last = (ti == NT - 1) or (ti == NT)
                if ph == 0:
                    phaseA_tile(cfg, l, ti, xt, first, is_last)
                else:
                    phaseB_tile(cfg, l, ti, xt, (ph - 1) * 4, first, is_last)
                TT(xo[:T, :], big[3][:T, :], xa[:T, :], ALU.add, [big[3], xa], [xo], eng="pool")
                if not last_phase:
                    STO(acc_out.t[r0:r0 + T, :], xo[:T, :], xo, W=[(acc_out, ti)])
                else:
                    ACT(junk[:T, :], xo[:T, :], AF.Square, [xo], [junk, sm1], accum_out=sm1[:T, 4:5])
                    TS(sm1[:T, 4:5], sm1[:T, 4:5], 1.0 / D, EPS, ALU.mult, ALU.add, [sm1], [sm1])
                    ACT(sm1[:T, 4:5], sm1[:T, 4:5], AF.Ln, [sm1], [sm1])
                    ACT(sm1[:T, 4:5], sm1[:T, 4:5], AF.Exp, [sm1], [sm1], scale=-0.5)
                    STT(big[0][:T, :], xo[:T, :], sm1[:T, 4:5], fnw_v[:T, :], ALU.mult, ALU.mult, [xo, sm1, fnw_bc], [big[0]])
                    dsto = O["y_p"][r0:r0 + T, :] if ti < NT else O["y_s"][:, :]
                    STO(dsto, big[0][:T, :], big[0])
            pes.close()
            cur_es[0] = es
            PA.clear()
            PB.clear()
        layer_in = scr[2]
    if FILL_TABLE and not limit.get("nofill"):
        fill_rhs = CS.cmf[:, :, :].rearrange("p c t -> p (c t)")[:, 0:512]
        S.filler = lambda e: e.matmul(banks[7][:, :], lhsT=identb[:, :], rhs=fill_rhs, start=True, stop=True)
        S.fill_table = {int(kv.split(":")[0]): int(kv.split(":")[1]) for kv in FILL_TABLE.split(",") if kv}
    S.emit()
    build.pe_groups = S.pe_groups
    es.close()
    return nc, dbg_out


_CACHE = {}


def make_in_maps(inputs):
    g = {k: np.ascontiguousarray(np.asarray(v, dtype=np.float32)) for k, v in inputs.items()}
    consts = host_consts()
    maps = []
    for c in range(8):
        m = {}
        m["xp"] = g["x_prompt"][c]
        m["xs"] = g["x_sample"][c * NSB:(c + 1) * NSB].reshape(NSB * SL, D)
        m["call"] = np.concatenate([g["c_prompt"][c:c + 1], g["c_sample"][c * NSB:(c + 1) * NSB]], axis=0)
        m["st_sc"] = g["state_ssd_conv"][:, c * NSB:(c + 1) * NSB]
        m["st_ssm"] = g["state_ssm"][:, c * NSB:(c + 1) * NSB]
        m["st_gc"] = g["state_gdn_conv"][:, c * NSB:(c + 1) * NSB]
        m["st_gdn"] = g["state_gdn"][:, c * NSB:(c + 1) * NSB]
        for k in ("norm_w", "w_ada", "b_ada", "w_in", "ssd_conv_w", "ssd_conv_b", "ssd_dt_bias", "ssd_a_log", "ssd_d",
                  "ssd_norm_w", "gdn_conv_w", "gdn_dt_bias", "gdn_a_log", "gdn_norm_w", "w_out"):
            m[k] = g[k]
        m["final_norm_w"] = g["final_norm_w"].reshape(1, D)
        for k, v in consts.items():
            m["c_" + k] = v
        maps.append({k: np.ascontiguousarray(v) for k, v in m.items()})
    return maps


def kernel(**inputs):
    if "nc" not in _CACHE:
        _CACHE["nc"] = build()[0]
    nc = _CACHE["nc"]
    maps = make_in_maps(inputs)
    res = run_bass_kernel_spmd(nc, maps, core_ids=list(range(8)))
    R = res.results
    cat = lambda k, ax: np.concatenate([np.asarray(r[k]) for r in R], axis=ax)
    y_p = np.stack([np.asarray(r["y_p"]) for r in R], 0)
    y_s = np.stack([np.asarray(r["y_s"]).reshape(NSB, SL, D) for r in R], 0).reshape(8 * NSB, SL, D)
    ncs_p = np.stack([np.asarray(r["ncs_p"]) for r in R], 1)
    nssm_p = np.stack([np.asarray(r["nssm_p"]).reshape(DEPTH, 16, 64, 128) for r in R], 1)
    ngc_p = np.stack([np.asarray(r["ngc_p"]) for r in R], 1)
    ngdn_p = np.stack([np.asarray(r["ngdn_p"]) for r in R], 1)
    ncs_s = np.concatenate([np.asarray(r["ncs_s"]).reshape(DEPTH, NSB, 3, 1536) for r in R], 1)
    nssm_s = np.concatenate([np.asarray(r["nssm_s"]).reshape(DEPTH, NSB, 16, 64, 128) for r in R], 1)
    ngc_s = np.concatenate([np.asarray(r["ngc_s"]).reshape(DEPTH, NSB, 3, 3072) for r in R], 1)
    ngdn_s = np.concatenate([np.asarray(r["ngdn_s"]) for r in R], 1)
    outs = (y_p, y_s, ncs_p, nssm_p, ngc_p, ngdn_p, ncs_s, nssm_s, ngc_s, ngdn_s)
    return tuple(np.ascontiguousarray(o, dtype=np.float32) for o in outs)
```

```python
import contextlib
import numpy as np
import concourse.bass as bass
import concourse.mybir as mybir
from concourse.bass_utils import run_bass_kernel_spmd

F32 = mybir.dt.float32
F32R = mybir.dt.float32r
BF16 = mybir.dt.bfloat16
AF = mybir.ActivationFunctionType
ALU = mybir.AluOpType
AX = mybir.AxisListType

ENGS = ("pe", "dve", "act", "pool", "sp")


class Buf:
    def __init__(self, t, name):
        self.t = t
        self.name = name
        self.st = {}
        self.dma_sem = None
        self.dma_cnt = 0

    def __getitem__(self, k):
        return self.t[k]


class Instr:
    __slots__ = ("eng", "fn", "deps", "signal", "count", "is_dma", "buf_sem", "dma_count")

    def __init__(self, eng, fn):
        self.eng = eng
        self.fn = fn
        self.deps = []
        self.signal = False
        self.count = None
        self.is_dma = False
        self.buf_sem = None
        self.dma_count = None


class Sched:
    def __init__(self, nc):
        self.nc = nc
        self.streams = {e: [] for e in ENGS}
        self.all = []
        self.filler = None
        self.nfill = 0
        self.fill_table = {}
        self.pe_groups = 0

    def _states(self, buf, key):
        st = buf.st
        if key is None:
            if None not in st:
                st[None] = [None, {}]
            return list(st.values())
        if key not in st:
            if None in st:
                st[key] = [st[None][0], dict(st[None][1])]
            else:
                st[key] = [None, {}]
        res = [st[key]]
        if None in st:
            res.append(st[None])
        return res

    @staticmethod
    def _norm(lst):
        out = []
        for x in lst or []:
            out.append((x, None) if isinstance(x, Buf) else x)
        return out

    def add(self, eng, fn, reads=None, writes=None, dma_buf=None):
        ins = Instr(eng, fn)
        ins.is_dma = dma_buf is not None
        reads = self._norm(reads)
        writes = self._norm(writes)
        deps, raw = [], []
        for buf, key in reads:
            for s in self._states(buf, key):
                if s[0] is not None:
                    deps.append(s[0])
                    raw.append(s[0])
                if getattr(buf, "excl", False):
                    for r in s[1].values():
                        if r.eng != eng:
                            deps.append(r)
        for buf, key in writes:
            for s in self._states(buf, key):
                if s[0] is not None:
                    deps.append(s[0])
                deps.extend(s[1].values())
        fdeps = {}
        for d in deps:
            if d is ins:
                continue
            if (not ins.is_dma) and (not d.is_dma) and d.eng == eng:
                if eng == "pe" or not any(d is r for r in raw):
                    continue
            if d.is_dma:
                fdeps[id(d)] = (d, d.buf_sem.dma_cnt * 16)
            else:
                fdeps[id(d)] = (d, None)
        ins.deps = list(fdeps.values())
        if ins.is_dma:
            ins.buf_sem = dma_buf
            dma_buf.dma_cnt += 1
            ins.dma_count = dma_buf.dma_cnt * 16
        rkey = ("dma", id(ins)) if ins.is_dma else eng
        for buf, key in reads:
            for s in self._states(buf, key):
                s[1][rkey] = ins
        for buf, key in writes:
            for s in self._states(buf, key):
                s[0] = ins
                if key is None or s is not buf.st.get(None):
                    s[1] = {}
        self.streams[eng].append(ins)
        self.all.append(ins)
        return ins

    def barrier(self, dma_bufs_all):
        last = {}
        for e in ("pe", "dve", "act", "pool"):
            for ins in reversed(self.streams[e]):
                if not isinstance(ins, tuple) and not ins.is_dma:
                    last[e] = ins
                    ins.signal = True
                    break
        snap = [(b, b.dma_cnt * 16) for b in dma_bufs_all if b.dma_cnt > 0]
        mark = ("barrier", last, snap)
        for e in ENGS:
            self.streams[e].append(mark)

    def emit(self, final_wait_eng="sp"):
        nc = self.nc
        for ins in self.all:
            for d, _v in ins.deps:
                if not d.is_dma:
                    d.signal = True
        with contextlib.ExitStack() as es:
            eng_sem = {e: es.enter_context(nc.semaphore("s_" + e)) for e in ("pe", "dve", "act", "pool")}
            dma_bufs = []
            for ins in self.all:
                if isinstance(ins, tuple):
                    continue
                if ins.is_dma and ins.buf_sem.dma_sem is None:
                    ins.buf_sem.dma_sem = es.enter_context(nc.semaphore("d_%d" % len(dma_bufs)))
                    dma_bufs.append(ins.buf_sem)
            cnt = {e: 0 for e in eng_sem}
            for ins in self.all:
                if not ins.is_dma and ins.signal:
                    cnt[ins.eng] += 1
                    ins.count = cnt[ins.eng]
            block = es.enter_context(nc.Block())
            engobj = {"pe": block.tensor, "dve": block.vector, "act": block.scalar, "pool": block.gpsimd,
                      "sp": block.sync}
            sched = self

            def make(ename):
                def body(eng):
                    water = {}
                    for ins in sched.streams[ename]:
                        if isinstance(ins, tuple):
                            _, last, snap = ins
                            nw_ = 0
                            for e2, li in last.items():
                                if e2 != ename and water.get(("e", e2), 0) < li.count:
                                    water[("e", e2)] = li.count
                                    eng.wait_ge(eng_sem[e2], li.count)
                                    nw_ += 1
                            for b_, v_ in snap:
                                if water.get(("d", id(b_)), 0) < v_:
                                    water[("d", id(b_))] = v_
                                    eng.wait_ge(b_.dma_sem, v_)
                                    nw_ += 1
                            continue
                        need = {}
                        for d, dv in ins.deps:
                            if d.is_dma:
                                key, sem, val = ("d", id(d.buf_sem)), d.buf_sem.dma_sem, dv
                            else:
                                key, sem, val = ("e", d.eng), eng_sem[d.eng], d.count
                            if need.get(key, (None, 0))[1] < val:
                                need[key] = (sem, val)
                        todo = [(key, sem, val) for key, (sem, val) in need.items() if water.get(key, 0) < val]
                        if ename == "pe" and not ins.is_dma:
                            k_ = sched.pe_groups
                            sched.pe_groups += 1
                            if todo and sched.filler is not None:
                                for _ in range(sched.fill_table.get(k_, 0)):
                                    sched.filler(eng)
                        for key, sem, val in todo:
                            water[key] = val
                            eng.wait_ge(sem, val)
                        bi = ins.fn(eng)
                        if ins.is_dma:
                            bi.then_inc(ins.buf_sem.dma_sem, 16)
                        elif ins.signal:
                            bi.then_inc(eng_sem[ename], 1)
                    if ename == final_wait_eng:
                        for b in dma_bufs:
                            eng.wait_ge(b.dma_sem, b.dma_cnt * 16)
                return body

            for ename in ENGS:
                engobj[ename](make(ename))


FILL_TABLE = (
    "8:18,16:5,24:4,32:15,40:5,48:4,56:15,64:5,72:4,80:16,88:5,96:4,104:16,112:5,120:4,128:15,136:5,145:3,154:15,16"
    "3:3,172:3,181:14,190:3,199:3,208:1,210:1,212:80,220:7,316:3,384:11,387:1,417:22,437:11,457:46,465:3,481:14,489"
    ":12,585:3,653:11,656:1,686:21,706:11,726:44,734:3,750:12,758:12,854:3,922:11,925:1,955:21,975:11,995:44,1003:3"
    ",1019:12,1027:12,1123:3,1191:11,1194:1,1224:21,1244:11,1264:44,1272:3,1288:12,1296:12,1392:3,1460:11,1463:1,14"
    "93:21,1513:11,1533:44,1541:3,1557:12,1565:12,1661:3,1729:11,1732:1,1762:21,1782:11,1802:44,1810:3,1826:12,1834"
    ":12,1930:3,1998:11,2001:1,2031:21,2051:11,2071:44,2079:3,2095:12,2103:12,2199:3,2267:11,2270:1,2300:21,2320:11"
    ",2340:44,2348:3,2364:12,2372:12,2468:3,2536:11,2539:1,2569:17,2589:11,2609:44,2617:3,2633:12,2641:12,2737:3,28"
    "05:11,2808:1,2838:17,2858:11,2878:44,2886:3,2902:12,2910:12,3006:3,3074:11,3077:1,3107:17,3127:11,3147:44,3155"
    ":3,3171:12,3179:12,3275:3,3343:11,3346:1,3376:17,3396:11,3416:44,3424:3,3440:12,3448:12,3544:3,3612:11,3615:1,"
    "3645:17,3665:11,3685:44,3693:3,3709:12,3717:12,3813:3,3881:11,3884:1,3914:17,3934:11,3954:44,3962:3,3978:12,39"
    "86:12,4082:3,4150:11,4153:1,4183:17,4203:11,4223:44,4231:3,4247:12,4255:12,4351:3,4431:15,4434:2,4464:23,4484:"
    "11,4488:5,4496:10,4512:32,4520:3,4536:12,4544:22,4652:10,4732:11,4735:3,4765:24,4789:24,4793:5,4801:33,4809:25"
    ",4813:5,4821:35,4829:24,4833:5,4841:31,4849:25,4853:5,4861:32,4869:25,4873:5,4881:34,4889:23,4893:5,4901:35,49"
    "09:24,4913:5,4921:32,4929:25,4933:5,4941:33,4949:24,4953:5,4961:32,4969:25,4973:5,4981:31,4989:25,4993:5,5001:"
    "36,5009:24,5013:5,5021:32,5029:25,5033:5,5041:31,5049:24,5053:5,5061:34,5069:26,5073:5,5081:33,5089:25,5093:5,"
    "5101:10,5117:32,5125:1,5141:80,5149:8,5245:3,5301:19,5304:1,5313:19,5317:12,5342:8,5350:7,5362:1,5374:1,5386:1"
    ",5394:2,5398:2,5406:2,5414:2,5418:5,5426:2,5442:26,5446:2,5454:18,5462:12,5558:3,5614:19,5617:1,5626:17,5630:1"
    "1,5655:11,5663:9,5675:2,5687:1,5699:1,5707:3,5711:2,5719:2,5727:1,5731:4,5739:1,5755:26,5759:2,5767:17,5775:12"
    ",5871:3,5927:19,5930:1,5939:20,5943:11,5968:11,5976:7,5988:1,6020:2,6024:2,6032:2,6040:1,6044:6,6052:2,6068:26"
    ",6072:2,6080:18,6088:12,6184:3,6240:19,6243:1,6252:19,6256:12,6281:9,6289:8,6301:1,6313:2,6325:1,6333:2,6337:2"
    ",6345:2,6353:1,6357:4,6365:2,6381:26,6385:2,6393:18,6401:12,6497:3,6553:19,6556:1,6565:14,6569:12,6594:12,6602"
    ":4,6614:1,6626:1,6646:2,6650:2,6658:2,6666:1,6670:6,6678:2,6694:26,6698:2,6706:17,6714:12,6810:3,6866:19,6869:"
    "1,6878:16,6882:12,6907:8,6915:6,6927:1,6939:1,6951:1,6959:2,6963:2,6971:2,6979:1,6983:4,6991:1,7007:25,7011:2,"
    "7019:17,7027:12,7123:3,7179:19,7182:1,7191:18,7195:11,7220:8,7228:6,7240:1,7252:1,7264:2,7272:3,7276:2,7284:3,"
    "7292:2,7296:7,7304:2,7320:26,7324:2,7332:17,7340:12,7436:3,7492:19,7495:1,7504:16,7508:11,7533:15,7541:6,7553:"
    "1,7565:1,7577:1,7585:2,7589:2,7597:2,7605:1,7609:6,7617:2,7633:26,7637:2,7645:18,7653:12,7749:3,7805:19,7808:1"
    ",7817:20,7821:12,7846:9,7854:4,7898:2,7902:2,7910:2,7918:1,7922:5,7930:2,7946:27,7950:2,7958:18,7966:12,8062:3"
    ",8118:19,8121:1,8130:12,8134:12,8159:7,8167:9,8179:1,8191:1,8203:1,8211:3,8215:2,8223:2,8231:2,8235:7,8243:2,8"
    "259:25,8263:2,8271:18,8279:12,8375:3,8431:19,8434:1,8443:17,8447:12,8472:10,8480:7,8492:1,8524:2,8528:2,8536:2"
    ",8544:1,8548:5,8556:1,8572:26,8576:2,8584:18,8592:12,8688:3,8744:19,8747:1,8756:17,8760:11,8785:14,8793:7,8805"
    ":1,8817:1,8829:2,8837:3,8841:3,8849:3,8857:2,8861:7,8869:2,8885:25,8889:2,8897:18,8905:12,9001:3,9057:19,9060:"
    "1,9069:18,9073:12,9098:8,9106:7,9118:1,9130:1,9142:1,9150:2,9154:2,9162:2,9170:1,9174:5,9182:2,9198:26,9202:2,"
    "9210:18,9218:12,9314:3,9370:19,9373:1,9382:12,9386:12,9411:10,9419:4,9431:1,9443:1,9455:2,9463:3,9467:3,9475:3"
    ",9483:2,9487:7,9495:2,9511:25,9515:2,9523:18,9531:12,9627:3,9683:19,9686:1,9695:15,9699:12,9724:8,9732:8,9744:"
    "1,9756:1,9776:2,9780:2,9788:2,9796:1,9800:6,9808:2,9824:25,9828:2,9836:18,9844:12,9940:3,10008:20,10011:1,1002"
    "0:18,10024:12,10049:11,10057:8,10069:1,10101:2,10105:2,10113:2,10121:1,10125:6,10133:2,10149:26,10153:2,10161:"
    "18,10169:32,10277:7,10345:19,10348:1,10357:43,10361:11,10386:6,10390:4,10394:1,10402:3,10436:2,10445:3,10447:1"
    ",10450:77,10466:3,10482:1,10484:2,10493:3,10498:80,10514:2,10532:2,10534:1,10541:2,10545:1,10546:80,10562:2,10"
    "580:2,10589:2,10606:11,10610:1,10618:80,10626:8,10722:3,10778:19,10781:1,10790:18,10794:12,10819:8,10827:4,108"
    "71:2,10875:2,10883:2,10891:1,10895:4,10903:1,10919:25,10923:2,10931:18,10939:12,11035:3,11091:19,11094:1,11103"
    ":17,11107:12,11132:12,11140:4,11184:2,11188:2,11196:2,11204:1,11208:4,11216:2,11232:26,11236:2,11244:18,11252:"
    "12,11348:3,11404:19,11407:1,11416:16,11420:11,11445:9,11453:8,11465:1,11477:1,11489:1,11497:2,11501:2,11509:2,"
    "11517:1,11521:7,11529:2,11545:26,11549:2,11557:18,11565:12,11661:3,11717:19,11720:1,11729:16,11733:12,11758:12"
    ",11766:4,11810:2,11814:2,11822:2,11830:1,11834:5,11842:2,11858:26,11862:2,11870:18,11878:12,11974:3,12030:19,1"
    "2033:1,12042:16,12046:11,12071:10,12079:7,12091:1,12103:1,12115:1,12123:2,12127:2,12135:2,12143:1,12147:4,1215"
    "5:2,12171:23,12175:2,12183:18,12191:12,12287:3,12343:19,12346:1,12355:15,12359:12,12384:12,12392:4,12404:1,124"
    "36:2,12440:2,12448:2,12456:1,12460:4,12468:1,12484:25,12488:2,12496:18,12504:12,12600:3,12656:19,12659:1,12668"
    ":17,12672:11,12697:8,12705:8,12717:1,12729:1,12741:2,12749:3,12753:3,12761:3,12769:2,12773:6,12781:2,12797:22,"
    "12801:2,12809:18,12817:12,12913:3,12969:19,12972:1,12981:16,12985:11,13010:12,13018:10,13030:1,13042:2,13054:1"
    ",13062:3,13066:2,13074:2,13082:1,13086:5,13094:2,13110:26,13114:2,13122:18,13130:12,13226:3,13282:21,13285:1,1"
    "3294:16,13298:12,13323:16,13331:7,13343:1,13375:3,13379:2,13387:2,13395:1,13399:4,13407:1,13423:25,13427:2,134"
    "35:18,13443:12,13539:3,13595:19,13598:1,13607:17,13611:11,13636:7,13644:9,13656:1,13668:1,13680:2,13688:2,1369"
    "2:2,13700:2,13708:1,13712:4,13720:2,13736:26,13740:2,13748:18,13756:12,13852:3,13908:19,13911:1,13920:13,13924"
    ":12,13949:12,13957:9,13969:1,13981:1,14001:2,14005:2,14013:2,14021:1,14025:6,14033:2,14049:27,14053:2,14061:18"
    ",14069:12,14165:3,14221:19,14224:1,14233:16,14237:12,14262:12,14270:8,14282:1,14294:1,14314:2,14318:2,14326:2,"
    "14334:1,14338:4,14346:2,14362:26,14366:2,14374:18,14382:12,14478:3,14534:19,14537:1,14546:15,14550:12,14575:12"
    ",14583:4,14627:2,14631:2,14639:2,14647:1,14651:5,14659:1,14675:26,14679:2,14687:18,14695:12,14791:3,14847:19,1"
    "4850:1,14859:17,14863:11,14888:9,14896:8,14908:1,14920:1,14932:1,14940:2,14944:3,14952:2,14960:1,14964:6,14972"
    ":2,14988:26,14992:2,15000:18,15008:12,15104:3,15160:19,15163:1,15172:13,15176:12,15201:12,15209:7,15221:1,1523"
    "3:1,15253:2,15257:2,15265:2,15273:1,15277:5,15285:2,15301:26,15305:2,15313:18,15321:12,15417:3,15485:20,15488:"
    "1,15497:17,15501:12,15526:10,15534:8,15546:1,15558:2,15578:2,15582:2,15590:2,15598:1,15602:4,15610:2,15626:24,"
    "15630:2,15638:18,15646:30,15754:8,15822:19,15825:1,15834:44,15838:11,15863:7,15867:4,15871:1,15879:5,15913:2,1"
    "5922:3,15924:1,15926:1,15927:80,15943:3,15961:2,15970:2,15975:78,15991:2,16009:2,16011:1,16018:3,16023:78,1603"
    "9:3,16057:2,16066:3,16083:12,16087:1,16095:50,16103:30,16111:5,16119:4,16127:15,16135:5,16143:4,16151:16,16159"
    ":5,16167:4,16175:17,16183:5,16191:4,16199:15,16207:5,16215:5,16223:15,16231:5,16240:4,16249:14,16258:3,16267:3"
    ",16276:13,16285:3,16294:3,16303:1,16305:2,16307:80,16315:7,16411:3,16479:11,16482:1,16512:21,16532:11,16552:44"
    ",16560:3,16576:12,16584:12,16680:3,16748:11,16751:1,16781:23,16801:12,16821:44,16829:3,16845:12,16853:12,16949"
    ":3,17017:11,17020:1,17050:23,17070:12,17090:44,17098:3,17114:12,17122:12,17218:3,17286:11,17289:1,17319:23,173"
    "39:12,17359:44,17367:3,17383:12,17391:12,17487:3,17555:11,17558:1,17588:23,17608:12,17628:44,17636:3,17652:12,"
    "17660:12,17756:3,17824:11,17827:1,17857:23,17877:12,17897:44,17905:3,17921:12,17929:12,18025:3,18093:11,18096:"
    "1,18126:23,18146:11,18166:44,18174:3,18190:13,18198:12,18294:3,18362:11,18365:1,18395:21,18415:10,18435:44,184"
    "43:3,18459:12,18467:12,18563:3,18631:11,18634:1,18664:21,18684:11,18704:44,18712:3,18728:12,18736:12,18832:3,1"
    "8900:11,18903:1,18933:21,18953:11,18973:44,18981:3,18997:12,19005:12,19101:3,19169:11,19172:1,19202:17,19222:1"
    "1,19242:44,19250:3,19266:12,19274:12,19370:3,19438:11,19441:1,19471:17,19491:11,19511:44,19519:3,19535:12,1954"
    "3:12,19639:3,19707:11,19710:1,19740:17,19760:11,19780:44,19788:3,19804:12,19812:12,19908:3,19976:11,19979:1,20"
    "009:17,20029:11,20049:46,20057:3,20073:12,20081:12,20177:3,20245:11,20248:1,20278:16,20298:11,20318:44,20326:3"
    ",20342:12,20350:12,20446:3,20526:15,20529:2,20559:23,20579:11,20583:5,20591:10,20607:32,20615:3,20631:13,20639"
    ":22,20747:10,20827:11,20830:3,20860:24,20884:25,20888:5,20896:33,20904:24,20908:5,20916:34,20924:23,20928:5,20"
    "936:35,20944:24,20948:5,20956:31,20964:25,20968:5,20976:33,20984:22,20988:5,20996:36,21004:24,21008:5,21016:31"
    ",21024:25,21028:5,21036:32,21044:24,21048:5,21056:31,21064:24,21068:5,21076:32,21084:24,21088:5,21096:32,21104"
    ":25,21108:5,21116:31,21124:25,21128:5,21136:31,21144:25,21148:5,21156:32,21164:24,21168:5,21176:34,21184:25,21"
    "188:5,21196:10,21212:32,21220:2,21236:80,21244:9,21340:3,21396:19,21399:1,21408:17,21412:12,21437:11,21445:9,2"
    "1457:1,21489:2,21493:2,21501:2,21509:1,21513:4,21521:1,21537:26,21541:2,21549:17,21557:12,21653:3,21709:19,217"
    "12:1,21721:20,21725:12,21750:10,21758:9,21770:1,21782:1,21794:1,21802:3,21806:2,21814:3,21822:2,21826:6,21834:"
    "2,21850:24,21854:2,21862:18,21870:12,21966:3,22022:19,22025:1,22034:17,22038:12,22063:9,22071:7,22083:1,22095:"
    "1,22107:1,22115:2,22119:2,22127:2,22135:1,22139:6,22147:2,22163:26,22167:2,22175:18,22183:12,22279:3,22335:19,"
    "22338:1,22347:14,22351:12,22376:12,22384:9,22396:1,22408:1,22428:2,22432:2,22440:2,22448:1,22452:4,22460:2,224"
    "76:25,22480:2,22488:17,22496:13,22592:3,22648:19,22651:1,22660:15,22664:12,22689:12,22697:9,22709:1,22721:2,22"
    "733:1,22741:2,22745:2,22753:2,22761:1,22765:4,22773:2,22789:26,22793:2,22801:18,22809:12,22905:3,22961:19,2296"
    "4:1,22973:12,22977:12,23002:7,23010:7,23022:1,23034:1,23054:2,23058:2,23066:2,23074:1,23078:6,23086:2,23102:26"
    ",23106:2,23114:18,23122:12,23218:3,23274:19,23277:1,23286:20,23290:12,23315:9,23323:7,23335:1,23347:1,23359:2,"
    "23367:2,23371:2,23379:2,23387:1,23391:5,23399:2,23415:26,23419:2,23427:18,23435:12,23531:3,23587:19,23590:1,23"
    "599:17,23603:11,23628:9,23636:8,23648:1,23660:1,23672:1,23680:2,23684:2,23692:2,23700:1,23704:6,23712:2,23728:"
    "26,23732:2,23740:18,23748:12,23844:3,23900:19,23903:1,23912:15,23916:12,23941:8,23949:5,23961:1,23973:1,23985:"
    "1,23993:3,23997:2,24005:2,24013:1,24017:4,24025:1,24041:23,24045:2,24053:18,24061:12,24157:3,24213:19,24216:1,"
    "24225:17,24229:11,24254:10,24262:9,24274:1,24286:2,24298:1,24306:3,24310:3,24318:2,24326:1,24330:5,24338:2,243"
    "54:27,24358:2,24366:18,24374:12,24470:3,24526:19,24529:1,24538:16,24542:11,24567:16,24575:7,24587:1,24599:1,24"
    "619:2,24623:2,24631:2,24639:1,24643:4,24651:1,24667:22,24671:3,24679:17,24687:12,24783:3,24839:19,24842:1,2485"
    "1:16,24855:12,24880:7,24888:6,24900:1,24912:1,24924:1,24932:2,24936:2,24944:2,24952:1,24956:4,24964:2,24980:26"
    ",24984:2,24992:17,25000:12,25096:3,25152:19,25155:1,25164:18,25168:13,25193:11,25201:7,25213:2,25225:1,25237:1"
    ",25245:3,25249:2,25257:2,25265:1,25269:4,25277:1,25293:22,25297:2,25305:18,25313:12,25409:3,25465:19,25468:1,2"
    "5477:17,25481:11,25506:10,25514:9,25526:1,25538:2,25550:1,25558:3,25562:3,25570:2,25578:1,25582:5,25590:2,2560"
    "6:26,25610:2,25618:18,25626:12,25722:3,25778:19,25781:1,25790:16,25794:11,25819:9,25827:8,25839:1,25851:2,2586"
    "3:1,25871:2,25875:2,25883:2,25891:1,25895:4,25903:1,25919:23,25923:2,25931:18,25939:12,26035:3,26103:20,26106:"
    "1,26115:19,26119:12,26144:10,26152:8,26164:1,26176:2,26188:1,26196:3,26200:3,26208:3,26216:2,26220:6,26228:2,2"
    "6244:20,26248:2,26256:18,26264:29,26372:8,26440:19,26443:1,26452:41,26456:11,26481:8,26485:4,26489:1,26497:3,2"
    "6531:2,26540:2,26545:76,26561:3,26579:2,26588:2,26593:78,26609:3,26627:2,26629:1,26636:3,26641:79,26657:2,2667"
    "5:1,26684:2,26701:10,26705:1,26713:80,26721:7,26817:3,26873:19,26876:1,26885:19,26889:12,26914:9,26922:8,26934"
    ":1,26946:1,26958:2,26966:2,26970:2,26978:2,26986:1,26990:4,26998:2,27014:26,27018:2,27026:51,27034:11,27130:3,"
    "27186:19,27189:1,27198:17,27202:12,27227:10,27235:9,27247:1,27259:1,27279:3,27283:2,27291:3,27299:2,27303:6,27"
    "311:2,27327:26,27331:2,27339:52,27347:11,27443:3,27499:18,27502:1,27511:18,27515:12,27540:11,27548:8,27560:1,2"
    "7572:1,27592:2,27596:2,27604:2,27612:1,27616:5,27624:1,27640:26,27644:2,27652:51,27660:12,27756:3,27812:18,278"
    "15:1,27824:18,27828:12,27853:11,27861:7,27873:1,27885:1,27897:1,27905:2,27909:2,27917:2,27925:1,27929:4,27937:"
    "2,27953:26,27957:2,27965:51,27973:12,28069:3,28125:19,28128:1,28137:17,28141:11,28166:11,28174:9,28186:1,28198"
    ":1,28210:1,28218:3,28222:3,28230:3,28238:2,28242:7,28250:2,28266:26,28270:2,28278:52,28286:11,28382:3,28438:18"
    ",28441:3,28450:16,28454:11,28479:13,28487:4,28531:2,28535:2,28543:2,28551:1,28555:4,28563:2,28579:24,28583:2,2"
    "8591:51,28599:11,28695:3,28751:18,28754:1,28763:17,28767:11,28792:9,28800:9,28812:1,28824:1,28836:1,28844:2,28"
    "848:2,28856:2,28864:1,28868:4,28876:1,28892:26,28896:2,28904:52,28912:12,29008:3,29064:19,29067:1,29076:17,290"
    "80:11,29105:11,29113:9,29125:1,29137:1,29157:2,29161:2,29169:2,29177:1,29181:4,29189:1,29205:26,29209:2,29217:"
    "52,29225:10,29321:3,29377:19,29380:1,29389:20,29393:11,29418:14,29426:4,29438:1,29450:1,29470:2,29474:2,29482:"
    "2,29490:1,29494:4,29502:2,29518:25,29522:2,29530:52,29538:11,29634:3,29690:19,29693:1,29702:20,29706:11,29731:"
    "10,29739:7,29751:1,29763:1,29775:1,29783:2,29787:2,29795:2,29803:1,29807:4,29815:2,29831:25,29835:2,29843:51,2"
    "9851:11,29947:3,30003:18,30006:1,30015:17,30019:11,30044:9,30052:5,30064:1,30076:1,30088:1,30096:2,30100:2,301"
    "08:2,30116:1,30120:4,30128:2,30144:24,30148:2,30156:51,30164:11,30260:3,30316:18,30319:1,30328:17,30332:12,303"
    "57:10,30365:9,30377:1,30389:1,30409:2,30413:2,30421:2,30429:1,30433:5,30441:1,30457:25,30461:2,30469:52,30477:"
    "12,30573:3,30629:19,30632:1,30641:18,30645:12,30670:10,30678:7,30690:1,30702:1,30714:1,30722:2,30726:2,30734:2"
    ",30742:1,30746:5,30754:2,30770:22,30774:2,30782:52,30790:12,30886:3,30942:19,30945:1,30954:17,30958:11,30983:1"
    "1,30991:9,31003:1,31015:1,31027:1,31035:2,31039:2,31047:2,31055:1,31059:4,31067:2,31083:28,31087:2,31095:51,31"
    "103:12,31199:3,31255:19,31258:1,31267:17,31271:12,31296:10,31304:9,31316:1,31328:1,31348:2,31352:2,31360:2,313"
    "68:1,31372:6,31380:2,31396:26,31400:2,31408:52,31416:12,31512:3,31580:20,31583:1,31592:18,31596:12,31621:9,316"
    "29:7,31641:1,31653:1,31665:1,31673:2,31677:2,31685:2,31693:1,31697:6,31705:2,31721:26,31725:2,31733:52,31741:1"
    "6,31849:9,31917:19,31920:1,31929:45,31933:12,31958:12,31962:7,31966:1,31974:5,32008:2,32017:3,32019:1,32022:78"
    ",32038:3,32056:2,32065:3,32070:80,32086:1,32104:2,32106:1,32113:2,32118:77,32134:3,32152:2,32161:2,32178:11,32"
    "182:1"
)
D = 1024
SEQ = 2048
NT = SEQ // 128
NSB = 16
SL = 4
DEPTH = 2
IN_DIM = 6688
EPS = 1e-6
NEG = -30000.0
OFF_ZS, OFF_XBC, OFF_DT, OFF_ZG, OFF_QKV, OFF_A, OFF_B = 0, 1024, 2560, 2576, 3600, 6672, 6680


class Cfg:
    def __init__(self, name, T, nseq, L, Q, chain):
        self.name, self.T, self.nseq, self.L, self.Q, self.chain = name, T, nseq, L, Q, chain
        self.nch = T // Q
        self.nlev = {64: 5, 4: 1}[Q]


CP = Cfg("p", 128, 1, 128, 64, True)
CS = Cfg("s", 64, NSB, SL, 4, False)


def host_consts():
    c = {"ident": np.eye(128, dtype=np.float32), "ones": np.ones((128, 128), np.float32)}
    for cfg in (CP, CS):
        T, Q, nch = cfg.T, cfg.Q, cfg.nch
        ch = np.arange(T) // Q
        same = ch[:, None] == ch[None, :]
        idx = np.arange(T)
        le = idx[:, None] <= idx[None, :]
        lt = idx[:, None] < idx[None, :]
        n = cfg.name
        c["tri_" + n] = (same & le).astype(np.float32)
        c["blk_" + n] = same.astype(np.float32)
        c["nmI_" + n] = np.where(same & le, 0.0, NEG).astype(np.float32)
        c["nmS_" + n] = np.where(same & lt, 0.0, NEG).astype(np.float32)
        c["nmSr_" + n] = np.ascontiguousarray(c["nmS_" + n].T)
        cm = (ch[None, :] == np.arange(nch)[:, None]).astype(np.float32)
        c["cmf_" + n] = np.ascontiguousarray(np.broadcast_to(cm[None], (128, nch, T))).astype(np.float32)
        c["cmT_" + n] = np.ascontiguousarray(cm.T)
    e = np.zeros((17, 128), np.float32)
    e[0, :] = 1.0
    c["E_p"] = e
    e = np.zeros((17, 64), np.float32)
    for t in range(64):
        e[1 + t // SL, t] = 1.0
    c["E_s"] = e
    return c


def bl(ap, n):
    return ap.unsqueeze(len(ap.shape)).to_broadcast(list(ap.shape) + [n])


def bm(ap, n):
    return ap.unsqueeze(1).to_broadcast([ap.shape[0], n] + list(ap.shape[1:]))


def build(debug=(), limit=None):
    limit = limit or {}
    nc = bass.Bass("TRN2", target_bir_lowering=False)
    S = Sched(nc)
    es = contextlib.ExitStack()
    consts_np = host_consts()

    def din(name, shape):
        return nc.dram_tensor(name, list(shape), F32, kind="ExternalInput").ap()

    def dout(name, shape):
        return nc.dram_tensor(name, list(shape), F32, kind="ExternalOutput").ap()

    I = {}
    I["xp"] = din("xp", [SEQ, D])
    I["xs"] = din("xs", [NSB * SL, D])
    I["call"] = din("call", [17, D])
    I["st_sc"] = din("st_sc", [DEPTH, NSB, 3, 1536])
    I["st_ssm"] = din("st_ssm", [DEPTH, NSB, 16, 64, 128])
    I["st_gc"] = din("st_gc", [DEPTH, NSB, 3, 3072])
    I["st_gdn"] = din("st_gdn", [DEPTH, NSB, 8, 128, 128])
    wshapes = {"norm_w": [DEPTH, D], "w_ada": [DEPTH, D, 3 * D], "b_ada": [DEPTH, 3 * D], "w_in": [DEPTH, D, IN_DIM],
               "ssd_conv_w": [DEPTH, 4, 1536], "ssd_conv_b": [DEPTH, 1536], "ssd_dt_bias": [DEPTH, 16],
               "ssd_a_log": [DEPTH, 16], "ssd_d": [DEPTH, 16], "ssd_norm_w": [DEPTH, 1024],
               "gdn_conv_w": [DEPTH, 4, 3072], "gdn_dt_bias": [DEPTH, 8], "gdn_a_log": [DEPTH, 8],
               "gdn_norm_w": [DEPTH, 128], "w_out": [DEPTH, 2048, D], "final_norm_w": [1, D]}
    for k, shp in wshapes.items():
        I[k] = din(k, shp)
    for k, v in consts_np.items():
        I["c_" + k] = din("c_" + k, v.shape)
    O = {}
    O["y_p"] = dout("y_p", [SEQ, D])
    O["y_s"] = dout("y_s", [NSB * SL, D])
    O["ncs_p"] = dout("ncs_p", [DEPTH, 3, 1536])
    O["nssm_p"] = dout("nssm_p", [DEPTH, 1024, 128])
    O["ngc_p"] = dout("ngc_p", [DEPTH, 3, 3072])
    O["ngdn_p"] = dout("ngdn_p", [DEPTH, 8, 128, 128])
    O["ncs_s"] = dout("ncs_s", [DEPTH, NSB * 3, 1536])
    O["nssm_s"] = dout("nssm_s", [DEPTH, NSB, 1024, 128])
    O["ngc_s"] = dout("ngc_s", [DEPTH, NSB * 3, 3072])
    O["ngdn_s"] = dout("ngdn_s", [DEPTH, NSB, 8, 128, 128])
    NROW = SEQ + NSB * SL
    scr = [Buf(nc.dram_tensor("scr%d" % i, [NROW, D], F32, kind="Internal").ap(), "scr%d" % i) for i in range(3)]
    xin_ext = None

    allb = []

    def sb(name, shape, dt=F32):
        b_ = Buf(es.enter_context(nc.sbuf_tensor(name, list(shape), dt)), name)
        allb.append(b_)
        return b_

    banks = [Buf(es.enter_context(nc.psum_tensor("bank%d" % i, [128, 512], F32)), "bank%d" % i) for i in range(8)]
    for b_ in banks:
        b_.excl = True
    pstate = {"i": 0, "pinned": set()}
    NB = 7

    def pget():
        while True:
            i = pstate["i"] % NB
            pstate["i"] += 1
            if i not in pstate["pinned"]:
                return banks[i]

    def ppin(b):
        pstate["pinned"].add(banks.index(b))

    def punpin(b):
        pstate["pinned"].discard(banks.index(b))

    def pf(b, shape, p=128):
        n = int(np.prod(shape))
        ap = b[:p, 0:n]
        if len(shape) == 1:
            return ap
        names = " ".join("a%d" % i for i in range(len(shape)))
        kw = {"a%d" % i: shape[i] for i in range(1, len(shape))}
        return ap.rearrange("p (%s) -> p %s" % (names, names), **kw)

    def pb(b, shape, p=128):
        n = int(np.prod(shape))
        ap = b[:p, :].bitcast(BF16)[:, 0:n]
        if len(shape) == 1:
            return ap
        names = " ".join("a%d" % i for i in range(len(shape)))
        kw = {"a%d" % i: shape[i] for i in range(1, len(shape))}
        return ap.rearrange("p (%s) -> p %s" % (names, names), **kw)

    def MM(out, lhsT, rhs, R, W, start=True, stop=True, **kw):
        S.add("pe", lambda e: e.matmul(out, lhsT=lhsT, rhs=rhs, start=start, stop=stop, **kw), reads=R, writes=W)

    def TR(out, in_, ident, R, W):
        S.add("pe", lambda e: e.transpose(out=out, in_=in_, identity=ident), reads=R, writes=W)

    def ACT(out, in_, func, R, W, **kw):
        S.add("act", lambda e: e.activation(out=out, in_=in_, func=func, **kw), reads=R, writes=W)

    def TT(out, in0, in1, op, R, W, eng="dve"):
        S.add(eng, lambda e: e.tensor_tensor(out=out, in0=in0, in1=in1, op=op), reads=R, writes=W)

    def TS(out, in0, s1, s2, op0, op1, R, W, eng="dve"):
        S.add(eng, lambda e: e.tensor_scalar(out=out, in0=in0, scalar1=s1, scalar2=s2, op0=op0, op1=op1), reads=R, writes=W)

    def STT(out, in0, scalar, in1, op0, op1, R, W):
        S.add("dve", lambda e: e.scalar_tensor_tensor(out=out, in0=in0, scalar=scalar, in1=in1, op0=op0, op1=op1), reads=R, writes=W)

    def CP_(out, in_, R, W, eng="dve"):
        S.add(eng, lambda e: e.tensor_copy(out=out, in_=in_), reads=R, writes=W)

    def RECIP(out, in_, R, W):
        S.add("dve", lambda e: e.reciprocal(out=out, in_=in_), reads=R, writes=W)

    def RED(out, in_, R, W):
        S.add("dve", lambda e: e.tensor_reduce(out=out, in_=in_, axis=AX.X, op=ALU.add), reads=R, writes=W)

    def MSET(ap, val, W, eng="pool"):
        S.add(eng, lambda e: e.memset(ap, val), writes=W)

    def LD(out, in_, buf, R=None, nonc=False):
        if nonc:
            S.add("sp", lambda e: e.dma_start(out=out, in_=in_, allow_slow_non_contiguous=True), reads=R, writes=[buf], dma_buf=buf)
        else:
            S.add("sp", lambda e: e.dma_start(out=out, in_=in_), reads=R, writes=[buf], dma_buf=buf)

    sto_n = [0]
    sto_dummy = Buf(None, "sto_dummy")

    def STO(out, in_, buf, W=None, extra_reads=None):
        S.add("sp", lambda e: e.dma_start(out=out, in_=in_), reads=[buf] + (extra_reads or []), writes=W, dma_buf=buf)

    dbg_out = {}
    mod_stacks = []
    cur_es = [es]
    all_bufs = []

    def DBG(name, buf, ap, shape):
        if name in debug and name not in dbg_out:
            o = dout("dbg_" + name, shape)
            dbg_out[name] = o
            if ap.dtype != F32:
                tmp = Buf(cur_es[0].enter_context(nc.sbuf_tensor("dbgt_" + name, list(shape), F32)), "dbgt_" + name)
                all_bufs.append(tmp)
                CP_(tmp[:], ap, [buf], [tmp])
                STO(o, tmp[:], tmp)
            else:
                STO(o, ap, buf)

    C = {}
    for k, v in consts_np.items():
        if k.startswith("cmf_"):
            continue
        C[k] = sb("k_" + k, v.shape)
        LD(C[k][:], I["c_" + k], C[k])
    identf = C["ident"]
    onesf = C["ones"]
    identb = sb("identb", [128, 128], BF16)
    CP_(identb[:], identf[:], [identf], [identb])
    onesb = sb("onesb", [128, 128], BF16)
    CP_(onesb[:], onesf[:], [onesf], [onesb])
    cfg_pending = []
    for cfg in (CP, CS):
        n = cfg.name
        cfg.tri, cfg.blk, cfg.nmI, cfg.nmS, cfg.nmSr, cfg.cmT = (C[k + n] for k in ("tri_", "blk_", "nmI_", "nmS_", "nmSr_", "cmT_"))
        cfg.cmf = sb("cmfb_" + n, [128, cfg.nch, cfg.T], BF16)
        cfg.ncmf = sb("ncmfb_" + n, [128, cfg.nch, cfg.T], BF16)
        cfg_pending.append(cfg)
        cfg.cmTb = sb("cmTb_" + n, [cfg.T, cfg.nch], BF16)
        CP_(cfg.cmTb[:], cfg.cmT[:], [cfg.cmT], [cfg.cmTb])

    SW = 128
    stage = [sb("stage%d" % i, [128, 8, SW]) for i in range(2)]
    stage_i = [0]
    WO = sb("wo", [128, 8, D], BF16)
    DIAG = sb("diag", [128, 12, 4, 128], BF16)
    cwT = sb("cwT", [128, 12, 4])
    cbrow = sb("cbrow", [1, 1536], BF16)
    nwT = sb("nwT", [128, 8])
    scT = sb("scT", [128, 8, 17])
    shT = sb("shT", [128, 8, 17])
    gate_bc = {"p": sb("gate_p", [128, D]), "s": sb("gate_s", [64, D])}
    vec16 = sb("vec16", [128, 4, 16])
    vec8 = sb("vec8", [128, 2, 8])
    scTb = sb("scTb", [128, 8, 17], BF16)
    badaT = sb("badaT", [128, 24])
    nw_in = sb("nw_in", [128, 8])
    xt_slots = [sb("xt%d" % i, [128, D]) for i in range(2)]
    xa_slots = [sb("xa%d" % i, [128, D]) for i in range(2)]
    stage_all = stage + xt_slots + xa_slots
    junk = sb("junk", [128, D], BF16)
    xn = sb("xn", [128, D], BF16)
    hT = sb("hT", [128, 8, 128], BF16)
    sm1 = sb("sm1", [128, 8])
    raw = sb("raw", [128, 12, 132], BF16)
    tailf = sb("tailf", [128, 12, 48])
    tailo = sb("tailo", [48, 1536])
    cst_in = tailo
    fT = sb("fT", [128, 12, 128], BF16)
    big = [sb("big%d" % i, [128, D]) for i in range(4)]
    bigb = [sb("bigb%d" % i, [128, D], BF16) for i in range(3)]
    xo = big[3]
    call_sb, sil_c, bgate, gate17, cbrow_f = big[0], bigb[0], big[1], big[2], tailo
    hTf_t = big[0]
    for cfg in cfg_pending:
        n_ = cfg.nch * cfg.T
        tmpv = big[0][:, 0:n_].rearrange("p (c t) -> p c t", t=cfg.T)
        LD(tmpv, I["c_cmf_" + cfg.name], big[0])
        CP_(cfg.cmf[:], tmpv, [big[0]], [cfg.cmf])
        TS(cfg.ncmf[:], tmpv, -1.0, None, ALU.mult, ALU.bypass, [big[0]], [cfg.ncmf])

    def stg():
        i = stage_i[0] % len(stage_all)
        stage_i[0] += 1
        b_ = stage_all[i]
        v = b_[:, :, :] if b_ in stage else b_[:, :].rearrange("p (k n) -> p k n", k=8)
        return b_, v, ("pool", "act", "dve")[i % 3]

    def cast(eng, out, in_, R, W, scale=None):
        if eng == "act":
            if scale is None:
                ACT(out, in_, AF.Copy, R, W)
            else:
                ACT(out, in_, AF.Copy, R, W, scale=scale)
        elif scale is None:
            CP_(out, in_, R, W, eng=eng)
        else:
            TS(out, in_, scale, 1.0, ALU.mult, ALU.mult, R, W, eng=eng)

    def load_cols(dst, dcol, src, ncols):
        c0 = 0
        while c0 < ncols:
            n = min(SW, ncols - c0)
            sbuf_, st, eng = stg()
            LD(st[:, :, 0:n], src[:, c0:c0 + n].rearrange("(k p) n -> p k n", p=128), sbuf_)
            cast(eng, dst[:, :, dcol + c0:dcol + c0 + n], st[:, :, 0:n], [sbuf_], [dst])
            c0 += n

    def load_wo(src_rows, nk, scale_ap_fn):
        for half in range(D // SW):
            sbuf_, st, eng = stg()
            LD(st[:, 0:nk, :], src_rows[:, half * SW:(half + 1) * SW].rearrange("(k p) n -> p k n", p=128), sbuf_)
            for k in range(nk):
                cast(eng, WO[:, k, half * SW:(half + 1) * SW], st[:, k, :], [sbuf_, nwT], [WO], scale=scale_ap_fn(k))

    def build_diag(convw, c0, nchunks):
        for k_ in range(4):
            LD(cwT[:, 0:nchunks, k_], convw[k_, c0 * 128:(c0 + nchunks) * 128].rearrange("(c p) -> p c", p=128), cwT, nonc=True)
        for c in range(nchunks):
            TT(DIAG[:, c, :, :], bm(identf[:], 4), bl(cwT[:, c, :], 128), ALU.mult, [identf, cwT], [DIAG], eng="pool")

    def do_mod(l):
        LD(call_sb[:17, :], I["call"], call_sb)
        ACT(sil_c[:17, :], call_sb[:17, :], AF.Silu, [call_sb], [sil_c])
        b = pget()
        for c in range(8):
            TR(pb(b, [8, 32])[:, c, 0:17], sil_c[:17, c * 128:(c + 1) * 128], identb[:17, :17], [sil_c, identb], [b])
        CP_(scTb[:], pb(b, [8, 32])[:, :, 0:17], [b], [scTb])
        LD(badaT[:], I["b_ada"][l].rearrange("(c p) -> p c", p=128), badaT, nonc=True)
        LD(nw_in[:], I["norm_w"][l].rearrange("(c p) -> p c", p=128), nw_in, nonc=True)
        LD(bgate[0:1, :], I["b_ada"][l:l + 1, 2048:3072], bgate)
        nblk = 3 * D // SW
        per = D // SW
        modes = contextlib.ExitStack()
        wb = Buf(modes.enter_context(nc.sbuf_tensor("modwb%d" % l, [128, 8, SW], BF16)), "modwb")
        all_bufs.append(wb)
        for blk in range(nblk):
            sbuf_, st, eng = stg()
            LD(st, I["w_ada"][l][:, blk * SW:(blk + 1) * SW].rearrange("(k p) n -> p k n", p=128), sbuf_)
            cast(eng, wb[:, :, 0:SW], st, [sbuf_], [wb])
            if blk < 2 * per:
                b = pget()
                nj = SW // 128
                for j in range(nj):
                    for k in range(8):
                        MM(pf(b, [nj, 17])[:, j, :], wb[:, k, j * 128:(j + 1) * 128], scTb[:, k, :], [wb, scTb], [b],
                           start=(k == 0), stop=(k == 7))
                dst = shT if blk < per else scT
                cc = (blk % per) * nj
                TT(dst[:, cc:cc + nj, :], pf(b, [nj, 17]), bl(badaT[:, blk * nj:blk * nj + nj], 17), ALU.add, [b, badaT], [dst])
            else:
                g = pget()
                gc0 = (blk - 2 * per) * SW
                for k in range(8):
                    MM(g[:17, 0:SW], scTb[:, k, :], wb[:, k, 0:SW], [wb, scTb], [g], start=(k == 0), stop=False)
                MM(g[:17, 0:SW], onesf[0:1, 0:17], bgate[0:1, gc0:gc0 + SW], [onesf, bgate], [g], start=False, stop=True)
                CP_(gate17[:17, gc0:gc0 + SW], g[:17, 0:SW], [g], [gate17])
        TS(scT[:], scT[:], 1.0, None, ALU.add, ALU.bypass, [scT], [scT])
        TT(scT[:], scT[:], bl(nw_in[:], 17), ALU.mult, [scT, nw_in], [scT])
        for n, E, T in (("p", C["E_p"], 128), ("s", C["E_s"], 64)):
            for half in range(2):
                b = pget()
                MM(b[:T, :], E[:, :T], gate17[:17, half * 512:(half + 1) * 512], [E, gate17], [b])
                CP_(gate_bc[n][:T, half * 512:(half + 1) * 512], b[:T, :], [b], [gate_bc[n]])
        mod_stacks.append(modes)

    def rows(ti):
        return (ti * 128, 128) if ti < NT else (SEQ, 64)

    def src_ap(layer_in, ti):
        r0, T = rows(ti)
        if layer_in is None:
            return (I["xp"][r0:r0 + T, :] if ti < NT else I["xs"][:, :]), None
        return layer_in[r0:r0 + T, :], (layer_in, ti)

    def prologue(cfg, xt):
        T = cfg.T
        ACT(junk[:T, :], xt[:T, :], AF.Square, [xt], [junk, sm1], accum_out=sm1[:T, 0:1])
        TS(sm1[:T, 0:1], sm1[:T, 0:1], 1.0 / D, EPS, ALU.mult, ALU.add, [sm1], [sm1])
        ACT(sm1[:T, 0:1], sm1[:T, 0:1], AF.Ln, [sm1], [sm1])
        ACT(sm1[:T, 0:1], sm1[:T, 0:1], AF.Exp, [sm1], [sm1], scale=-0.5)
        ACT(xn[:T, :], xt[:T, :], AF.Copy, [xt, sm1], [xn], scale=sm1[:T, 0:1])
        b = pget()
        for c in range(8):
            TR(pb(b, [8, T])[:, c, :], xn[:T, c * 128:(c + 1) * 128], identb[:T, :T], [xn, identb], [b])
        if cfg.nseq == 1:
            sc, sh = bl(scT[:, :, 0], T), bl(shT[:, :, 0], T)
            o1, o2, i1 = hTf_t[:, 0:8 * T].rearrange("p (c t) -> p c t", t=T), hT[:, :, :T], pb(b, [8, T])
        else:
            sc, sh = bl(scT[:, :, 1:17], SL), bl(shT[:, :, 1:17], SL)
            o1 = hTf_t[:, 0:8 * T].rearrange("p (c b l) -> p c b l", c=8, l=SL)
            o2 = hT[:, :, :T].rearrange("p c (b l) -> p c b l", l=SL)
            i1 = pb(b, [8, NSB, SL])
        TT(o1, i1, sc, ALU.mult, [b, scT], [hTf_t])
        TT(o2, o1, sh, ALU.add, [hTf_t, shT], [hT])

    def inproj_conv(cfg, l, wcol0, nchunks, bias_row, st_conv_in, conv_out, conv_out_s, ch0, is_last, first):
        T, L, nseq = cfg.T, cfg.L, cfg.nseq
        rawv = raw[:, :, 0:nseq * (L + 3)].rearrange("p c (b l) -> p c b l", l=L + 3)
        if cfg.nseq == 1:
            if first:
                MSET(raw[:, :, 0:3], 0.0, [raw])
            else:
                CP_(raw[:, 0:nchunks, 0:3], raw[:, 0:nchunks, L:L + 3], [raw], [raw])
        else:
            LD(cst_in[:, 0:nchunks * 128], st_conv_in[:, :, ch0 * 128:(ch0 + nchunks) * 128].rearrange("b k c -> (b k) c"), cst_in)
            for g0 in range(0, nchunks, 8):
                b = pget()
                ng = min(8, nchunks - g0)
                for c in range(ng):
                    TR(pf(b, [8, 48])[:, c, :], cst_in[:, (g0 + c) * 128:(g0 + c + 1) * 128], identf[:48, :48], [cst_in, identf], [b])
                CP_(rawv[:, g0:g0 + ng, :, 0:3], pf(b, [8, 48])[:, 0:ng, :].rearrange("p c (b k) -> p c b k", k=3), [b], [raw])
        if limit.get("stopC", 99) <= 1:
            return
        for g0 in range(0, nchunks, 4):
            b = pget()
            for c in range(4):
                for k in range(8):
                    MM(pf(b, [4, T])[:, c, :], PA["WB"][:, k, wcol0 + (g0 + c) * 128: wcol0 + (g0 + c + 1) * 128], hT[:, k, :T],
                       [PA["WB"], hT], [b], start=(k == 0), stop=(k == 7))
            if nseq == 1:
                ACT(raw[:, g0:g0 + 4, 3:3 + L], pf(b, [4, T]), AF.Copy, [b], [raw])
            else:
                for c in range(4):
                    ACT(rawv[:, g0 + c, :, 3:3 + L], pf(b, [4, T])[:, c, :].rearrange("p (b l) -> p b l", l=L), AF.Copy, [b], [raw])
            if is_last and limit.get("tail", True):
                for c in range(4):
                    ACT(tailf[:, g0 + c, 0:nseq * 3].rearrange("p (b k) -> p b k", k=3),
                        pf(b, [4, T])[:, c, :].rearrange("p (b l) -> p b l", l=L)[:, :, L - 3:L], AF.Copy, [b], [tailf])
        if limit.get("stopC", 99) <= 2:
            return
        for g0 in range(0, nchunks, 4):
            b = pget()
            for c in range(4):
                o = pf(b, [4, T])[:, c, :]
                if nseq > 1:
                    o = o.rearrange("p (b l) -> p b l", l=L)
                for k in range(4):
                    MM(o, DIAG[:, g0 + c, k, :], rawv[:, g0 + c, :, k:k + L] if nseq > 1 else raw[:, g0 + c, k:k + L],
                       [DIAG, raw], [b], start=(k == 0), stop=(k == 3 and bias_row is None))
                if bias_row is not None:
                    MM(pf(b, [4, T])[:, c, :], bias_row[0:1, (g0 + c) * 128:(g0 + c + 1) * 128], onesb[0:1, :T], [bias_row, onesb], [b],
                       start=False, stop=True)
            ACT(fT[:, g0:g0 + 4, :T], pf(b, [4, T]), AF.Silu, [b], [fT])
        if limit.get("stopC", 99) <= 3:
            return
        if is_last:
            n3 = nseq * 3
            for g0 in range(0, nchunks, 4):
                b = pget()
                for c in range(4):
                    TR(pf(b, [4, 128], p=n3)[:, c, :], tailf[:, g0 + c, 0:n3], identf[:, :], [tailf, identf], [b])
                CP_(tailo[:n3, g0 * 128:(g0 + 4) * 128], b[:n3, :], [b], [tailo])
            dst = conv_out[l][:, ch0 * 128:(ch0 + nchunks) * 128] if nseq == 1 else conv_out_s[l][:, ch0 * 128:(ch0 + nchunks) * 128]
            STO(dst, tailo[:n3, 0:nchunks * 128], tailo)

    def cum_and_tot(cfg, a_ap, a_buf, nh, out_c, out_dbc, pw):
        T, nch = cfg.T, cfg.nch
        b = pget()
        MM(b[:T, 0:nh], cfg.tri[:, :], a_ap, [cfg.tri, a_buf], [b])
        MM(b[:T, nh:2 * nh], cfg.blk[:, :], a_ap, [cfg.blk, a_buf], [b])
        ACT(out_c[:T, 0:2 * nh], b[:T, 0:2 * nh], AF.Copy, [b], [out_c])
        b2 = pget()
        MM(b2[:nch, 0:nh], cfg.cmT[:, :], a_ap, [cfg.cmT, a_buf], [b2])
        CP_(tot[:nch, 0:nh], b2[:nch, 0:nh], [b2], [tot])
        TT(totx[:nch, 0:nch * nh].rearrange("p (c h) -> p c h", h=nh), bm(tot[:nch, 0:nh], nch), bl(identf[:nch, :nch], nh), ALU.mult,
           [tot, identf], [totx])
        b3 = pget()
        MM(b3[:pw, 0:nch * nh], onesf[:nch, :pw], totx[:nch, 0:nch * nh], [onesf, totx], [b3])
        ACT(out_dbc[:pw, 0:nch * nh], b3[:pw, 0:nch * nh], AF.Exp, [b3], [out_dbc])

    tot = sb("tot", [16, 16])
    totx = sb("totx", [16, 256])
    sm = {k: sb("sm_" + k, [128, 64]) for k in ("a", "b", "c", "d", "e")}
    dbc = sb("dbc", [128, 256])
    acT = sb("acT", [16, 128])

    PA = {}

    def alloc_A(pes, tag):
        def sbp(name, shape, dt=F32):
            return Buf(pes.enter_context(nc.sbuf_tensor(name + tag, list(shape), dt)), name)
        PA.update(stf=sbp("stf", [128, D]), stb=sbp("stb", [128, D], BF16), hn=sbp("hn", [128, 8, 128]),
                  stf_s=sbp("stf_s", [128, D]), stb_s=sbp("stb_s", [128, D], BF16), Gs=sbp("Gs", [128, 2, 128]),
                  earg=sbp("earg", [128, 4, 128]), Ee=sbp("Ee", [128, 4, 128]), WT=sbp("WT", [128, 16, 128], BF16),
                  Btok=sbp("Btok", [128, 256], BF16), Btm=sbp("Btm", [128, 2, 256], BF16), CTm=sbp("CTm", [128, 2, 2, 128], BF16),
                  WB=sbp("WB", [128, 8, 2576], BF16))
    mslot = [0]

    def ssd_state_update(cfg, c, st_in, stb_in, st_out, yo, xdec, first_c, last_c):
        T, nch = cfg.T, cfg.nch
        Btm, CTm, Btok = PA["Btm"], PA["CTm"], PA["Btok"]
        sl = mslot[0] % 2
        mslot[0] += 1
        TS(Btm[:T, sl, :], Btok[:T, :], cfg.cmT[:, c:c + 1], 1.0, ALU.mult, ALU.mult, [Btok, cfg.cmT], [(Btm, sl)], eng="pool")
        TT(CTm[:, sl, :, :T], fT[:, 10:12, :T], bm(cfg.cmf[:, c, :], 2), ALU.mult, [fT, cfg.cmf], [(CTm, sl)], eng="pool")
        for g in range(2):
            MM(yo[g][:T, :], CTm[:, sl, g, :T], stb_in[:, g * 512:(g + 1) * 512], [(CTm, sl), stb_in], [yo[g]], start=first_c, stop=last_c)
        sp_ = [pget(), pget()]
        for g in range(2):
            MM(sp_[g][:, :], Btm[:T, sl, g * 128:(g + 1) * 128], xdec[:T, g * 512:(g + 1) * 512], [(Btm, sl), xdec], [sp_[g]])
        TT(big[3][:, :].rearrange("p (h q) -> p h q", q=64), st_in[:, :].rearrange("p (h q) -> p h q", q=64),
           bl(dbc[:, c * 16:(c + 1) * 16], 64), ALU.mult, [st_in, dbc], [big[3]])
        for g in range(2):
            TT(st_out[:, g * 512:(g + 1) * 512], big[3][:, g * 512:(g + 1) * 512], sp_[g][:, :], ALU.add, [big[3], sp_[g]], [st_out])

    def phaseA_tile(cfg, l, ti, xt, first, is_last):
        T, L, nseq, nch = cfg.T, cfg.L, cfg.nseq, cfg.nch
        stf, stb, hn, stf_s, stb_s, Gs, earg, Ee, WT, Btok = (PA[k] for k in ("stf", "stb", "hn", "stf_s", "stb_s", "Gs", "earg", "Ee", "WT", "Btok"))
        prologue(cfg, xt)
        if limit.get("stopA", 99) <= 0:
            return
        inproj_conv(cfg, l, OFF_XBC, 12, cbrow, I["st_sc"][l], O["ncs_p"], O["ncs_s"], 0, is_last, first)
        if ti == 0:
            DBG("fT_A", fT, fT[:, :, :T], [128, 12, T])
        if limit.get("stopA", 99) <= 1:
            return
        b = pget()
        for k in range(8):
            MM(b[:T, 0:16], hT[:, k, :T], PA["WB"][:, k, OFF_DT:OFF_DT + 16], [hT, PA["WB"]], [b], start=(k == 0), stop=(k == 7))
        A_, B_, C_, D_, E_ = (sm[k] for k in "abcde")
        TT(A_[:T, 0:16], b[:T, 0:16], vec16[:T, 0, :], ALU.add, [b, vec16], [A_])
        ACT(A_[:T, 0:16], A_[:T, 0:16], AF.Exp, [A_], [A_])
        ACT(A_[:T, 16:32], A_[:T, 0:16], AF.Ln, [A_], [A_], bias=1.0)
        TT(A_[:T, 32:48], A_[:T, 16:32], vec16[:T, 1, :], ALU.mult, [A_, vec16], [A_])
        cum_and_tot(cfg, A_[:T, 32:48], A_, 16, B_, dbc, 128)
        TT(C_[:T, 0:16], B_[:T, 16:32], B_[:T, 0:16], ALU.subtract, [B_], [C_])
        ACT(C_[:T, 0:16], C_[:T, 0:16], AF.Exp, [C_], [C_])
        ACT(C_[:T, 16:32], B_[:T, 0:16], AF.Exp, [B_], [C_])
        b = pget()
        TR(b[:16, 0:T], B_[:T, 0:16], identf[:T, :T], [B_, identf], [b])
        ACT(acT[:, :T], b[:16, 0:T], AF.Copy, [b], [acT])
        if ti == 0:
            DBG("acum", B_, B_[:T, 0:32], [T, 32])
        if limit.get("stopA", 99) <= 2:
            return
        bx, bb = pget(), pget()
        for c in range(8):
            TR(pb(bx, [8, 128], p=T)[:, c, :], fT[:, c, :T], identb[:, :], [fT, identb], [bx])
        for c in range(2):
            TR(pb(bb, [2, 128], p=T)[:, c, :], fT[:, 8 + c, :T], identb[:, :], [fT, identb], [bb])
        xc, xdec, xsD = bigb[0], bigb[1], big[0]
        TT(xc[:T, :].rearrange("p (h q) -> p h q", q=64), pb(bx, [16, 64], p=T), bl(A_[:T, 16:32], 64), ALU.mult, [bx, A_], [xc])
        TT(xsD[:T, :].rearrange("p (h q) -> p h q", q=64), pb(bx, [16, 64], p=T), bl(vec16[:T, 2, :], 64), ALU.mult, [bx, vec16], [xsD])
        TT(xdec[:T, :].rearrange("p (h q) -> p h q", q=64), xc[:T, :].rearrange("p (h q) -> p h q", q=64), bl(C_[:T, 0:16], 64), ALU.mult,
           [xc, C_], [xdec], eng="pool")
        ACT(Btok[:T, :], pb(bb, [256], p=T), AF.Copy, [bb], [Btok])
        b = pget()
        for g in range(2):
            MM(pf(b, [2, T], p=T)[:, g, :], fT[:, 8 + g, :T], fT[:, 10 + g, :T], [fT], [b])
        ACT(Gs[:T, :, :T], pf(b, [2, T], p=T), AF.Copy, [b], [Gs])
        for q in range(4):
            b = pget()
            for j in range(4):
                h = q * 4 + j
                MM(pf(b, [4, T], p=T)[:, j, :], identf[0:16, h:h + 1].to_broadcast([16, T]), acT[:, :T], [identf, acT], [b])
            for j in range(4):
                h = q * 4 + j
                STT(earg[:T, j, :T], pf(b, [4, T], p=T)[:, j, :], B_[:T, h:h + 1], cfg.nmI[:, :], ALU.subtract, ALU.add,
                    [b, B_, cfg.nmI], [earg])
            ACT(Ee[:T, :, :T], earg[:T, :, :T], AF.Exp, [earg], [Ee])
            TT(WT[:T, q * 4:q * 4 + 4, :T], Ee[:T, :, :T], bm(Gs[:T, q // 2, :T], 4), ALU.mult, [Ee, Gs], [WT])
        if limit.get("stopA", 99) <= 3:
            return
        yp = [pget(), pget()]
        for h in range(16):
            MM(yp[h // 8][:T, (h % 8) * 64:(h % 8 + 1) * 64], WT[:T, h, :T], xc[:T, h * 64:(h + 1) * 64], [WT, xc], [yp[h // 8]])
        ppin(yp[0]); ppin(yp[1])
        yo = [pget(), pget()]
        ppin(yo[0]); ppin(yo[1])
        if cfg.chain:
            if first:
                MSET(stf[:, :], 0.0, [stf])
                MSET(stb[:, :], 0.0, [stb])
            for c in range(nch):
                ssd_state_update(cfg, c, stf, stb, stf, yo, xdec, c == 0, c == nch - 1)
                ACT(stb[:, :], stf[:, :], AF.Copy, [stf], [stb])
            if is_last:
                for half in range(2):
                    b = pget()
                    for k in range(4):
                        kk = half * 4 + k
                        TR(pf(b, [4, 128])[:, k, :], stf[:, :].rearrange("n (j k) -> n k j", k=8)[:, kk, :], identf[:, :], [stf, identf], [b])
                    CP_(hn[:, half * 4:half * 4 + 4, :], pf(b, [4, 128]), [b], [hn])
                STO(O["nssm_p"][l].rearrange("(p k) n -> p k n", k=8), hn[:], hn)
        else:
            for c in range(limit.get("nchs", nch)):
                LD(hn[:], I["st_ssm"][l, c].rearrange("h q n -> (h q) n").rearrange("(p k) n -> p k n", k=8), hn)
                if limit.get("sub", 9) <= 0:
                    continue
                for half in range(2):
                    b = pget()
                    for k in range(4):
                        kk = half * 4 + k
                        TR(pf(b, [4, 128])[:, k, :], hn[:, kk, :], identf[:, :], [hn, identf], [b])
                    if limit.get("sub", 9) <= 1:
                        continue
                    ACT(stf_s[:, :].rearrange("n (j k) -> n k j", k=8)[:, half * 4:half * 4 + 4, :], pf(b, [4, 128]), AF.Copy, [b], [stf_s])
                    if limit.get("sub", 9) <= 2:
                        continue
                    CP_(stb_s[:, :].rearrange("n (j k) -> n k j", k=8)[:, half * 4:half * 4 + 4, :], pf(b, [4, 128]), [b], [stb_s])
                if limit.get("noupd"):
                    continue
                ssd_state_update(cfg, c, stf_s, stb_s, stf_s, yo, xdec, c == 0, c == limit.get("nchs", nch) - 1)
                if limit.get("noback"):
                    continue
                for half in range(2):
                    b = pget()
                    for k in range(4):
                        kk = half * 4 + k
                        TR(pf(b, [4, 128])[:, k, :], stf_s[:, :].rearrange("n (j k) -> n k j", k=8)[:, kk, :], identf[:, :], [stf_s, identf], [b])
                    CP_(hn[:, half * 4:half * 4 + 4, :], pf(b, [4, 128]), [b], [hn])
                STO(O["nssm_s"][l, c].rearrange("(p k) n -> p k n", k=8), hn[:], hn)
        if limit.get("stopA", 99) <= 4:
            punpin(yp[0]); punpin(yp[1]); punpin(yo[0]); punpin(yo[1])
            return
        t1, t2 = big[1], big[2]
        for g in range(2):
            TT(t1[:T, g * 512:(g + 1) * 512].rearrange("p (h q) -> p h q", q=64), yo[g][:T, :].rearrange("p (h q) -> p h q", q=64),
               bl(C_[:T, 16 + g * 8:16 + g * 8 + 8], 64), ALU.mult, [yo[g], C_], [t1])
            TT(t2[:T, g * 512:(g + 1) * 512], yp[g][:T, :], t1[:T, g * 512:(g + 1) * 512], ALU.add, [yp[g], t1], [t2])
        for x_ in yp + yo:
            punpin(x_)
        TT(t2[:T, :], t2[:T, :], xsD[:T, :], ALU.add, [t2, xsD], [t2], eng="pool")
        if ti == 0:
            DBG("yssd", t2, t2[:T, :], [T, D])
        for g in range(2):
            b = pget()
            for k in range(8):
                MM(b[:T, :], hT[:, k, :T], PA["WB"][:, k, OFF_ZS + g * 512:OFF_ZS + (g + 1) * 512], [hT, PA["WB"]], [b], start=(k == 0), stop=(k == 7))
            ACT(t1[:T, g * 512:(g + 1) * 512], b[:T, :], AF.Silu, [b], [t1])
        TT(t2[:T, :], t2[:T, :], t1[:T, :], ALU.mult, [t2, t1], [t2])
        for g in range(2):
            ACT(junk[:T, g * 512:(g + 1) * 512], t2[:T, g * 512:(g + 1) * 512], AF.Square, [t2], [junk, sm1], accum_out=sm1[:T, 2 + g:3 + g])
        TS(sm1[:T, 2:4], sm1[:T, 2:4], 1.0 / 512, EPS, ALU.mult, ALU.add, [sm1], [sm1])
        ACT(sm1[:T, 2:4], sm1[:T, 2:4], AF.Ln, [sm1], [sm1])
        ACT(sm1[:T, 2:4], sm1[:T, 2:4], AF.Exp, [sm1], [sm1], scale=-0.5)
        ynb = bigb[2]
        for g in range(2):
            ACT(ynb[:T, g * 512:(g + 1) * 512], t2[:T, g * 512:(g + 1) * 512], AF.Copy, [t2, sm1], [ynb], scale=sm1[:T, 2 + g:3 + g])
        out_proj(cfg, ynb, 8)

    def out_proj(cfg, ynb, nk):
        T = cfg.T
        b = pget()
        for c in range(nk):
            TR(pb(b, [8, T])[:, c, :], ynb[:T, c * 128:(c + 1) * 128], identb[:T, :T], [ynb, identb], [b])
        CP_(hT[:, 0:nk, :T], pb(b, [8, T])[:, 0:nk, :], [b], [hT])
        for half in range(2):
            b = pget()
            for k in range(nk):
                MM(b[:T, :], hT[:, k, :T], WO[:, k, half * 512:(half + 1) * 512], [hT, WO], [b], start=(k == 0), stop=(k == nk - 1))
            TT(big[3][:T, half * 512:(half + 1) * 512], b[:T, :], gate_bc[cfg.name][:T, half * 512:(half + 1) * 512], ALU.mult,
               [b, gate_bc[cfg.name]], [big[3]])

    PB = {}

    def alloc_B(pes, tag):
        def sbp(name, shape, dt=F32):
            return Buf(pes.enter_context(nc.sbuf_tensor(name + tag, list(shape), dt)), name)
        PB.update(Sp_f=sbp("Sp_f", [128, 4, 128]), Sp_b=sbp("Sp_b", [128, 4, 128], BF16), Ss_in=sbp("Ss_in", [128, 16, 128]),
                  Ss_b=sbp("Ss_b", [128, 16, 128], BF16), kv=sbp("kv", [128, 8, 128], BF16), kbg=sbp("kbg", [128, 4, 128]),
                  vb_=sbp("vb", [128, 4, 128]), kdec=sbp("kdec", [128, 4, 128], BF16), kdm=sbp("kdm", [64, 2, 128], BF16),
                  LG=sbp("LG", [128, 12]), LGT=sbp("LGT", [12, 128]), ea=sbp("ea", [128, 4, 128]),
                  E2a=sbp("E2a", [128, 4, 128]), E2b=sbp("E2b", [128, 4, 128]),
                  X0=sbp("X0", [128, 4, 128]), X1=sbp("X1", [128, 4, 128]), Y0=sbp("Y0", [128, 4, 128]), Y1=sbp("Y1", [128, 4, 128]),
                  R0=sbp("R0", [128, 4, 128]), R1=sbp("R1", [128, 4, 128]), attnT=sbp("attnT", [128, 4, 128], BF16),
                  egb=sbp("egb", [128, 4, 128]), qsq=sbp("qsq", [128, 4, 128], BF16),
                  nwTm=sbp("nwTm", [128, 1024], BF16), qdm=sbp("qdm", [128, 1024], BF16), u_sb=sbp("u_sb", [128, 4, 128]),
                  vnew=sbp("vnew", [128, 4, 128], BF16), dbcS=sbp("dbcS", [128, 64]), WB=sbp("WB", [128, 8, 2056], BF16))
        PB["E2"] = None

    def r32(ap):
        return ap.bitcast(F32R)

    def phaseB_tile(cfg, l, ti, xt, h0, first, is_last):
        T, L, nseq, nch = cfg.T, cfg.L, cfg.nseq, cfg.nch
        (Sp_f, Sp_b, Ss_in, Ss_b, kv, kbg, vb_, kdec, kdm, LG, LGT, ea, attnT, egb, nwTm, qdm, u_sb, vnew, dbcS) = (
            PB[k] for k in ("Sp_f", "Sp_b", "Ss_in", "Ss_b", "kv", "kbg", "vb_", "kdec", "kdm", "LG", "LGT", "ea", "attnT",
                            "egb", "nwTm", "qdm", "u_sb", "vnew", "dbcS"))
        E2 = [PB["E2a"], PB["E2b"]]
        Ss_out = Ss_in
        Xs, Ys, Rs = [PB["X0"], PB["X1"]], [PB["Y0"], PB["Y1"]], [PB["R0"], PB["R1"]]
        WB = PB["WB"]
        prologue(cfg, xt)
        inproj_conv_B(cfg, l, h0, is_last, first)
        A_, B_, C_, D_, E_ = (sm[k] for k in "abcde")
        b = pget()
        for k in range(8):
            MM(b[:T, 0:8], hT[:, k, :T], WB[:, k, 2048:2056], [hT, WB], [b], start=(k == 0), stop=(k == 7))
        TT(A_[:T, 0:4], b[:T, 0:4], vec8[:T, 0, h0:h0 + 4], ALU.add, [b, vec8], [A_])
        ACT(A_[:T, 0:4], A_[:T, 0:4], AF.Exp, [A_], [A_])
        ACT(A_[:T, 0:4], A_[:T, 0:4], AF.Ln, [A_], [A_], bias=1.0)
        TT(A_[:T, 4:8], A_[:T, 0:4], vec8[:T, 1, h0:h0 + 4], ALU.mult, [A_, vec8], [A_])
        ACT(A_[:T, 8:12], b[:T, 4:8], AF.Exp, [b], [A_], scale=-1.0)
        ACT(A_[:T, 8:12], A_[:T, 8:12], AF.Ln, [A_], [A_], bias=1.0)
        ACT(A_[:T, 12:16], A_[:T, 8:12], AF.Exp, [A_], [A_], scale=-1.0)
        ACT(A_[:T, 16:20], A_[:T, 8:12], AF.Copy, [A_], [A_], scale=-1.0)
        cum_and_tot(cfg, A_[:T, 4:8], A_, 4, B_, dbcS, 128)
        ACT(C_[:T, 0:4], B_[:T, 0:4], AF.Exp, [B_], [C_])
        TT(C_[:T, 4:8], B_[:T, 4:8], B_[:T, 0:4], ALU.subtract, [B_], [C_])
        ACT(C_[:T, 4:8], C_[:T, 4:8], AF.Exp, [C_], [C_])
        b = pget()
        for c in range(8):
            TR(pb(b, [8, 128], p=T)[:, c, :], fT[:, 4 + c, :T], identb[:, :], [fT, identb], [b])
        ACT(kv[:T, :, :], pb(b, [8, 128], p=T), AF.Copy, [b], [kv])
        ksq = big[0]
        TT(ksq[:T, 0:512].rearrange("p (h d) -> p h d", d=128), kv[:T, 0:4, :], kv[:T, 0:4, :], ALU.mult, [kv], [ksq])
        RED(D_[:T, 0:4], ksq[:T, 0:512].rearrange("p (h d) -> p h d", d=128), [ksq], [D_])
        TS(D_[:T, 0:4], D_[:T, 0:4], EPS, None, ALU.add, ALU.bypass, [D_], [D_])
        ACT(D_[:T, 4:8], D_[:T, 0:4], AF.Ln, [D_], [D_])
        ACT(D_[:T, 8:12], D_[:T, 4:8], AF.Exp, [D_], [D_], scale=-0.5)
        qsq = PB["qsq"]
        ACT(qsq[:, :, :T], fT[:, 0:4, :T], AF.Square, [fT], [qsq])
        b = pget()
        for j in range(4):
            MM(b[:T, j:j + 1], qsq[:, j, :T], onesb[:, 0:1], [qsq, onesb], [b])
        TS(D_[:T, 12:16], b[:T, 0:4], EPS, None, ALU.add, ALU.bypass, [b], [D_])
        ACT(D_[:T, 12:16], D_[:T, 12:16], AF.Ln, [D_], [D_])
        ACT(D_[:T, 16:20], D_[:T, 12:16], AF.Exp, [D_], [D_], scale=-0.5)
        TS(D_[:T, 16:20], D_[:T, 16:20], float(128 ** -0.5), None, ALU.mult, ALU.bypass, [D_], [D_])
        TT(E_[:T, 0:4], D_[:T, 8:12], A_[:T, 12:16], ALU.mult, [D_, A_], [E_])
        TT(E_[:T, 4:8], E_[:T, 0:4], C_[:T, 0:4], ALU.mult, [E_, C_], [E_])
        TT(E_[:T, 8:12], D_[:T, 8:12], C_[:T, 4:8], ALU.mult, [D_, C_], [E_])
        TS(E_[:T, 12:16], E_[:T, 0:4], -1.0, None, ALU.mult, ALU.bypass, [E_], [E_])
        TS(E_[:T, 16:20], D_[:T, 8:12], -1.0, None, ALU.mult, ALU.bypass, [D_], [E_])
        CP_(LG[:T, 0:4], B_[:T, 0:4], [B_], [LG])
        STT(LG[:T, 4:8], D_[:T, 4:8], -0.5, B_[:T, 0:4], ALU.mult, ALU.subtract, [D_, B_], [LG])
        STT(LG[:T, 8:12], D_[:T, 4:8], -0.5, B_[:T, 0:4], ALU.mult, ALU.add, [D_, B_], [LG])
        TT(LG[:T, 8:12], LG[:T, 8:12], A_[:T, 16:20], ALU.add, [LG, A_], [LG])
        b = pget()
        TR(b[:12, 0:T], LG[:T, 0:12], identf[:T, :T], [LG, identf], [b])
        ACT(LGT[:, :T], b[:12, 0:T], AF.Copy, [b], [LGT])
        TT(r32(kbg[:T, :, :]), kv[:T, 0:4, :], bl(E_[:T, 4:8], 128), ALU.mult, [kv, E_], [kbg])
        TT(kdec[:T, :, :], kv[:T, 0:4, :], bl(E_[:T, 8:12], 128), ALU.mult, [kv, E_], [kdec])
        TT(r32(vb_[:T, :, :]), kv[:T, 4:8, :], bl(A_[:T, 12:16], 128), ALU.mult, [kv, A_], [vb_])
        bG, bA = pget(), pget()
        for j in range(4):
            MM(pf(bG, [4, T], p=T)[:, j, :], fT[:, 4 + j, :T], fT[:, 4 + j, :T], [fT], [bG])
        for j in range(4):
            MM(pf(bA, [4, T], p=T)[:, j, :], fT[:, 4 + j, :T], fT[:, j, :T], [fT], [bA])
        bcs = [pget(), pget(), pget()]
        for r in range(3):
            for j in range(4):
                MM(pf(bcs[r], [4, T], p=T)[:, j, :], identf[0:12, r * 4 + j:r * 4 + j + 1].to_broadcast([12, T]), LGT[:, :T],
                   [identf, LGT], [bcs[r]])
        be = pget()
        for j in range(4):
            MM(pf(be, [4, T])[:, j, :], identf[0:12, j:j + 1].to_broadcast([12, 128]), LGT[:, :T], [identf, LGT], [be])
        X, Y, R = Xs[0], Ys[0], Rs[0]
        kinds = ((1, ALU.add, cfg.nmSr, 0), (2, ALU.subtract, cfg.nmS, 1), (0, ALU.subtract, cfg.nmI, 2))
        for r, op0, msk, ki in kinds:
            for j in range(4):
                STT(ea[:T, j, :T], pf(bcs[r], [4, T], p=T)[:, j, :], B_[:T, j:j + 1], msk[:, :], op0, ALU.add, [bcs[r], B_, msk], [ea])
            Ek = E2[ki % 2]
            ACT(Ek[:T, :, :T], ea[:T, :, :T], AF.Exp, [ea], [Ek])
            for j in range(4):
                if ki == 0:
                    STT(r32(X[:T, j, :T]), pf(bG, [4, T], p=T)[:, j, :], E_[:T, 12 + j:13 + j], Ek[:T, j, :T], ALU.mult, ALU.mult, [bG, E_, Ek], [X])
                elif ki == 1:
                    STT(r32(Y[:T, j, :T]), pf(bG, [4, T], p=T)[:, j, :], E_[:T, 16 + j:17 + j], Ek[:T, j, :T], ALU.mult, ALU.mult, [bG, E_, Ek], [Y])
                else:
                    STT(attnT[:T, j, :T], pf(bA, [4, T], p=T)[:, j, :], D_[:T, 8 + j:9 + j], Ek[:T, j, :T], ALU.mult, ALU.mult, [bA, D_, Ek], [attnT])
        ACT(egb[:, :, :T], pf(be, [4, T]), AF.Exp, [be], [egb])
        TT(r32(R[:T, :, :T]), Y[:T, :, :T], bm(identf[:T, :T], 4), ALU.add, [Y, identf], [R])
        for lev in range(cfg.nlev):
            X2, Y2, R2 = Xs[(lev + 1) % 2], Ys[(lev + 1) % 2], Rs[(lev + 1) % 2]
            lastlev = (lev == cfg.nlev - 1)
            bX, bY, bR = pget(), (None if lastlev else pget()), pget()
            for j in range(4):
                MM(pf(bX, [4, T], p=T)[:, j, :], r32(Y[:T, j, :T]), r32(X[:T, j, :T]), [X, Y], [bX])
            if not lastlev:
                for j in range(4):
                    MM(pf(bY, [4, T], p=T)[:, j, :], r32(X[:T, j, :T]), r32(Y[:T, j, :T]), [X, Y], [bY])
            ACT(r32(X2[:T, :, :T]), pf(bX, [4, T], p=T), AF.Copy, [bX], [X2])
            if not lastlev:
                CP_(r32(Y2[:T, :, :T]), pf(bY, [4, T], p=T), [bY], [Y2])
            for j in range(4):
                MM(pf(bR, [4, T], p=T)[:, j, :], r32(X2[:T, j, :T]), r32(R[:T, j, :T]), [X2, R], [bR])
            TT(r32(R2[:T, :, :T]), R[:T, :, :T], pf(bR, [4, T], p=T), ALU.add, [R, bR], [R2])
            X, Y, R = X2, Y2, R2
        bU, bW = pget(), pget()
        for j in range(4):
            MM(pf(bU, [4, 128], p=T)[:, j, :], r32(R[:T, j, :T]), r32(vb_[:T, j, :]), [R, vb_], [bU])
        for j in range(4):
            MM(pf(bW, [4, T])[:, j, :], r32(kbg[:T, j, :]), r32(R[:T, j, :T]), [kbg, R], [bW])
        ACT(u_sb[:T, :, :], pf(bU, [4, 128], p=T), AF.Copy, [bU], [u_sb])
        TT(egb[:, :, :T], fT[:, 0:4, :T], egb[:, :, :T], ALU.mult, [fT, egb], [egb])
        vnb, ob = pget(), pget()
        ppin(vnb)
        ppin(ob)
        vn4, o4 = pf(vnb, [4, 128], p=T), pf(ob, [4, 128], p=T)
        if cfg.chain:
            if first:
                MSET(Sp_f[:], 0.0, [Sp_f])
                MSET(Sp_b[:], 0.0, [Sp_b])
            nwv = nwTm[:, 0:nch * 4 * T].rearrange("p (c j t) -> p c j t", c=nch, j=4)
            qdv = qdm[:, 0:nch * 4 * T].rearrange("p (c j t) -> p c j t", c=nch, j=4)
            for c in range(nch):
                TT(nwv[:, c], pf(bW, [4, T]), bm(cfg.ncmf[:, c, :], 4), ALU.mult, [bW, cfg.ncmf], [(nwTm, c)])
                TT(qdv[:, c], egb[:, :, :T], bm(cfg.cmf[:, c, :], 4), ALU.mult, [egb, cfg.cmf], [(qdm, c)], eng="pool")
            for c in range(nch):
                for j in range(4):
                    MM(vn4[:, j, :], nwv[:, c, j, :], Sp_b[:, j, :], [(nwTm, c), Sp_b], [vnb], start=(j == 0 and c == 0), stop=False, skip_group_check=True)
                for j in range(4):
                    MM(o4[:, j, :], qdv[:, c, j, :], Sp_b[:, j, :], [(qdm, c), Sp_b], [ob], start=(j == 0 and c == 0), stop=False, skip_group_check=True)
                TT(vnew[:T, :, :], vn4, u_sb[:T, :, :], ALU.add, [vnb, u_sb], [vnew])
                bs = pget()
                for j in range(4):
                    MM(pf(bs, [4, 128])[:, j, :], kdec[c * 64:(c + 1) * 64, j, :], vnew[c * 64:(c + 1) * 64, j, :], [kdec, vnew], [bs])
                tmpS = big[0][:, 512:1024].rearrange("p (j v) -> p j v", v=128)
                TT(tmpS, Sp_f[:, :, :], bl(dbcS[:, c * 4:(c + 1) * 4], 128), ALU.mult, [Sp_f, dbcS], [big[0]])
                TT(Sp_f[:, :, :], tmpS, pf(bs, [4, 128]), ALU.add, [big[0], bs], [Sp_f])
                ACT(Sp_b[:, :, :], Sp_f[:, :, :], AF.Copy, [Sp_f], [Sp_b])
            if is_last:
                STO(O["ngdn_p"][l, h0:h0 + 4].rearrange("h k v -> k h v"), Sp_f[:], Sp_f)
        else:
            nwv = nwTm[:, 0:nch * T].rearrange("p (c t) -> p c t", t=T)
            qdv = qdm[:, 0:nch * T].rearrange("p (c t) -> p c t", t=T)
            ppin(bW)
            for j in range(4):
                h = h0 + j
                LD(Ss_in[:], I["st_gdn"][l, :, h].rearrange("b k v -> k b v"), Ss_in)
                CP_(Ss_b[:], Ss_in[:], [Ss_in], [Ss_b], eng="pool")
                TT(nwv, bm(pf(bW, [4, T])[:, j, :], nch), cfg.ncmf[:, :, :], ALU.mult, [bW, cfg.ncmf], [nwTm])
                TT(qdv, bm(egb[:, j, :T], nch), cfg.cmf[:, :, :], ALU.mult, [egb, cfg.cmf], [qdm], eng="pool")
                for c in range(nch):
                    MM(vn4[:, j, :], nwv[:, c, :], Ss_b[:, c, :], [nwTm, Ss_b], [vnb], start=(j == 0 and c == 0), stop=False, skip_group_check=True)
                for c in range(nch):
                    MM(o4[:, j, :], qdv[:, c, :], Ss_b[:, c, :], [qdm, Ss_b], [ob], start=(j == 0 and c == 0), stop=False, skip_group_check=True)
                TT(vnew[:T, j, :], vn4[:, j, :], u_sb[:T, j, :], ALU.add, [vnb, u_sb], [vnew])
                tmpS = big[2][:, :].rearrange("p (c v) -> p c v", v=128)
                for hf in range(2):
                    TT(tmpS, Ss_in[:, hf * 8:(hf + 1) * 8, :],
                       bl(dbcS[:, 0:nch * 4].rearrange("p (c j) -> p c j", j=4)[:, hf * 8:(hf + 1) * 8, j], 128), ALU.mult, [Ss_in, dbcS], [big[2]])
                    for c4 in range(hf * 8, hf * 8 + 8, 4):
                        bs = pget()
                        for cc in range(4):
                            c = c4 + cc
                            sl = mslot[0] % 2
                            mslot[0] += 1
                            TS(kdm[:T, sl, :], kdec[:T, j, :], cfg.cmT[:, c:c + 1], 1.0, ALU.mult, ALU.mult, [kdec, cfg.cmT], [(kdm, sl)], eng="pool")
                            MM(pf(bs, [4, 128])[:, cc, :], kdm[:T, sl, :], vnew[:T, j, :], [(kdm, sl), vnew], [bs])
                        TT(Ss_out[:, c4:c4 + 4, :], tmpS[:, c4 - hf * 8:c4 - hf * 8 + 4, :], pf(bs, [4, 128]), ALU.add, [big[2], bs], [Ss_out])
                STO(O["ngdn_s"][l, :, h].rearrange("b k v -> k b v"), Ss_out[:], Ss_out)
            punpin(bW)
        for j in range(4):
            MM(o4[:, j, :], attnT[:T, j, :T], vnew[:T, j, :], [attnT, vnew], [ob], start=False, stop=True, skip_group_check=True)
        punpin(vnb)
        of = big[1]
        ACT(of[:T, 0:512], ob[:T, :], AF.Copy, [ob], [of])
        punpin(ob)
        if ti == 0 and h0 == 0:
            DBG("ogdn", of, of[:T, 0:512], [T, 512])
        TT(ksq[:T, 0:512], of[:T, 0:512], of[:T, 0:512], ALU.mult, [of], [ksq])
        RED(C_[:T, 8:12], ksq[:T, 0:512].rearrange("p (h d) -> p h d", d=128), [ksq], [C_])
        TT(C_[:T, 12:16], D_[:T, 16:20], D_[:T, 16:20], ALU.mult, [D_], [C_])
        TT(C_[:T, 8:12], C_[:T, 8:12], C_[:T, 12:16], ALU.mult, [C_], [C_])
        TS(C_[:T, 8:12], C_[:T, 8:12], 1.0 / 128, EPS, ALU.mult, ALU.add, [C_], [C_])
        ACT(C_[:T, 8:12], C_[:T, 8:12], AF.Ln, [C_], [C_])
        ACT(C_[:T, 8:12], C_[:T, 8:12], AF.Exp, [C_], [C_], scale=-0.5)
        TT(C_[:T, 8:12], C_[:T, 8:12], D_[:T, 16:20], ALU.mult, [C_, D_], [C_])
        b = pget()
        for k in range(8):
            MM(b[:T, :], hT[:, k, :T], WB[:, k, 0:512], [hT, WB], [b], start=(k == 0), stop=(k == 7))
        sz = big[2]
        ACT(sz[:T, 0:512], b[:T, :], AF.Silu, [b], [sz])
        TT(of[:T, 0:512].rearrange("p (h d) -> p h d", d=128), of[:T, 0:512].rearrange("p (h d) -> p h d", d=128), bl(C_[:T, 8:12], 128),
           ALU.mult, [of, C_], [of])
        onb = bigb[2]
        TT(onb[:T, 0:512], of[:T, 0:512], sz[:T, 0:512], ALU.mult, [of, sz], [onb])
        out_proj(cfg, onb, 4)

    def inproj_conv_B(cfg, l, h0, is_last, first):
        T, L, nseq = cfg.T, cfg.L, cfg.nseq
        rawv = raw[:, :, 0:nseq * (L + 3)].rearrange("p c (b l) -> p c b l", l=L + 3)
        if nseq == 1:
            if first:
                MSET(raw[:, :, 0:3], 0.0, [raw])
            else:
                CP_(raw[:, :, 0:3], raw[:, :, L:L + 3], [raw], [raw])
        else:
            for part in range(3):
                cs0 = part * 1024 + h0 * 128
                LD(cst_in[:, part * 512:(part + 1) * 512], I["st_gc"][l][:, :, cs0:cs0 + 512].rearrange("b k c -> (b k) c"), cst_in)
            for g0 in (0, 8):
                b = pget()
                ng = min(8, 12 - g0)
                for c in range(ng):
                    TR(pf(b, [8, 48])[:, c, :], cst_in[:, (g0 + c) * 128:(g0 + c + 1) * 128], identf[:48, :48], [cst_in, identf], [b])
                CP_(rawv[:, g0:g0 + ng, :, 0:3], pf(b, [8, 48])[:, 0:ng, :].rearrange("p c (b k) -> p c b k", k=3), [b], [raw])
        for g0 in range(0, 12, 4):
            b = pget()
            for c in range(4):
                for k in range(8):
                    MM(pf(b, [4, T])[:, c, :], PB["WB"][:, k, 512 + (g0 + c) * 128: 512 + (g0 + c + 1) * 128], hT[:, k, :T],
                       [PB["WB"], hT], [b], start=(k == 0), stop=(k == 7))
            if nseq == 1:
                ACT(raw[:, g0:g0 + 4, 3:3 + L], pf(b, [4, T]), AF.Copy, [b], [raw])
            else:
                for c in range(4):
                    ACT(rawv[:, g0 + c, :, 3:3 + L], pf(b, [4, T])[:, c, :].rearrange("p (b l) -> p b l", l=L), AF.Copy, [b], [raw])
            if is_last and limit.get("tail", True):
                for c in range(4):
                    ACT(tailf[:, g0 + c, 0:nseq * 3].rearrange("p (b k) -> p b k", k=3),
                        pf(b, [4, T])[:, c, :].rearrange("p (b l) -> p b l", l=L)[:, :, L - 3:L], AF.Copy, [b], [tailf])
        for g0 in range(0, 12, 4):
            b = pget()
            for c in range(4):
                o = pf(b, [4, T])[:, c, :]
                if nseq > 1:
                    o = o.rearrange("p (b l) -> p b l", l=L)
                for k in range(4):
                    MM(o, DIAG[:, g0 + c, k, :], rawv[:, g0 + c, :, k:k + L] if nseq > 1 else raw[:, g0 + c, k:k + L],
                       [DIAG, raw], [b], start=(k == 0), stop=(k == 3))
            ACT(fT[:, g0:g0 + 4, :T], pf(b, [4, T]), AF.Silu, [b], [fT])
        if is_last:
            n3 = nseq * 3
            for g0 in range(0, 12, 4):
                b = pget()
                for c in range(4):
                    TR(pf(b, [4, 128], p=n3)[:, c, :], tailf[:, g0 + c, 0:n3], identf[:, :], [tailf, identf], [b])
                CP_(tailo[:n3, g0 * 128:(g0 + 4) * 128], b[:n3, :], [b], [tailo])
            dst = O["ngc_p"] if nseq == 1 else O["ngc_s"]
            for part in range(3):
                cs0 = part * 1024 + h0 * 128
                STO(dst[l][:, cs0:cs0 + 512], tailo[:n3, part * 512:(part + 1) * 512], tailo)

    def bvec(dst_ap, src_row_ap, buf):
        LD(dst_ap, src_row_ap.partition_broadcast(128), buf)

    tiles = [(CP, ti) for ti in range(limit.get("ntiles", NT))] + ([(CS, NT)] if limit.get("sample", True) else [])
    all_bufs.extend(allb)
    layer_in = None
    for l in range(limit.get("layers", DEPTH)):
        do_mod(l)
        S.barrier(all_bufs)
        mod_stacks.pop().close()
        for ph in limit.get("phases", (0, 1, 2)):
            acc_in = layer_in if ph == 0 else scr[(ph - 1)]
            last_phase = (l == DEPTH - 1 and ph == 2)
            acc_out = scr[ph] if ph < 2 else (scr[2] if not last_phase else None)
            S.barrier(all_bufs)
            pes = contextlib.ExitStack()
            cur_es[0] = pes
            if ph == 0:
                alloc_A(pes, "_%d_%d" % (l, ph))
            else:
                alloc_B(pes, "_%d_%d" % (l, ph))
            for v_ in list(PA.values()) + list(PB.values()):
                if v_ is not None and v_ not in all_bufs:
                    all_bufs.append(v_)
            if ph == 0:
                load_cols(PA["WB"], 0, I["w_in"][l][:, 0:2576], 2576)
                LD(nwT[:], I["ssd_norm_w"][l].rearrange("(k p) -> p k", p=128), nwT, nonc=True)
                load_wo(I["w_out"][l][0:1024, :], 8, lambda k: nwT[:, k:k + 1])
                build_diag(I["ssd_conv_w"][l], 0, 12)
                LD(cbrow_f[0:1, :], I["ssd_conv_b"][l:l + 1, :], cbrow_f)
                CP_(cbrow[:], cbrow_f[0:1, :], [cbrow_f], [cbrow])
                bvec(vec16[:, 0, :], I["ssd_dt_bias"][l:l + 1, :], vec16)
                bvec(vec16[:, 1, :], I["ssd_a_log"][l:l + 1, :], vec16)
                bvec(vec16[:, 2, :], I["ssd_d"][l:l + 1, :], vec16)
                ACT(vec16[:, 1, :], vec16[:, 1, :], AF.Exp, [vec16], [vec16])
                TS(vec16[:, 1, :], vec16[:, 1, :], -1.0, None, ALU.mult, ALU.bypass, [vec16], [vec16])
            else:
                h0 = (ph - 1) * 4
                load_cols(PB["WB"], 0, I["w_in"][l][:, OFF_ZG + h0 * 128:OFF_ZG + h0 * 128 + 512], 512)
                for part in range(3):
                    c0 = OFF_QKV + part * 1024 + h0 * 128
                    load_cols(PB["WB"], 512 + part * 512, I["w_in"][l][:, c0:c0 + 512], 512)
                load_cols(PB["WB"], 2048, I["w_in"][l][:, OFF_A + h0:OFF_A + h0 + 4], 4)
                load_cols(PB["WB"], 2052, I["w_in"][l][:, OFF_B + h0:OFF_B + h0 + 4], 4)
                LD(nwT[:, 0:1], I["gdn_norm_w"][l].rearrange("(p o) -> p o", o=1), nwT, nonc=True)
                load_wo(I["w_out"][l][1024 + h0 * 128:1024 + h0 * 128 + 512, :], 4, lambda k: nwT[:, 0:1])
                for part in range(3):
                    for k_ in range(4):
                        LD(cwT[:, part * 4:part * 4 + 4, k_],
                           I["gdn_conv_w"][l][k_, part * 1024 + h0 * 128:part * 1024 + h0 * 128 + 512].rearrange("(c p) -> p c", p=128), cwT, nonc=True)
                for c in range(12):
                    TT(DIAG[:, c, :, :], bm(identf[:], 4), bl(cwT[:, c, :], 128), ALU.mult, [identf, cwT], [DIAG], eng="pool")
                bvec(vec8[:, 0, :], I["gdn_dt_bias"][l:l + 1, :], vec8)
                bvec(vec8[:, 1, :], I["gdn_a_log"][l:l + 1, :], vec8)
                ACT(vec8[:, 1, :], vec8[:, 1, :], AF.Exp, [vec8], [vec8])
                TS(vec8[:, 1, :], vec8[:, 1, :], -1.0, None, ALU.mult, ALU.bypass, [vec8], [vec8])

            def issue_loads(idx):
                cfg, ti = tiles[idx]
                r0, T = rows(ti)
                xt = xt_slots[idx % 2]
                ap, dep = src_ap(layer_in.t if layer_in is not None else None, ti)
                LD(xt[:T, :], ap, xt, R=[(layer_in, ti)] if layer_in is not None else None)
                if ph > 0:
                    xa = xa_slots[idx % 2]
                    LD(xa[:T, :], acc_in.t[r0:r0 + T, :], xa, R=[(acc_in, ti)])

            if last_phase:
                fnw_bc = stage[0]
                fnw_v = stage[0][:, :, :].rearrange("p a b -> p (a b)")
                LD(fnw_v, I["final_norm_w"].partition_broadcast(128), stage[0])
            issue_loads(0)
            for idx, (cfg, ti) in enumerate(tiles):
                if idx + 1 < len(tiles):
                    issue_loads(idx + 1)
                r0, T = rows(ti)
                xt = xt_slots[idx % 2]
                xa = xa_slots[idx % 2] if ph > 0 else xt
                first = (ti == 0)
                is_last = (ti == NT - 1) or (ti == NT)
                if ph == 0:
                    phaseA_tile(cfg, l, ti, xt, first, is_last)
                else:
                    phaseB_tile(cfg, l, ti, xt, (ph - 1) * 4, first, is_last)
                TT(xo[:T, :], big[3][:T, :], xa[:T, :], ALU.add, [big[3], xa], [xo], eng="pool")
                if not last_phase:
                    STO(acc_out.t[r0:r0 + T, :], xo[:T, :], xo, W=[(acc_out, ti)])
                else:
                    ACT(junk[:T, :], xo[:T, :], AF.Square, [xo], [junk, sm1], accum_out=sm1[:T, 4:5])
                    TS(sm1[:T, 4:5], sm1[:T, 4:5], 1.0 / D, EPS, ALU.mult, ALU.add, [sm1], [sm1])
                    ACT(sm1[:T, 4:5], sm1[:T, 4:5], AF.Ln, [sm1], [sm1])
                    ACT(sm1[:T, 4:5], sm1[:T, 4:5], AF.Exp, [sm1], [sm1], scale=-0.5)
                    STT(big[0][:T, :], xo[:T, :], sm1[:T, 4:5], fnw_v[:T, :], ALU.mult, ALU.mult, [xo, sm1, fnw_bc], [big[0]])
                    dsto = O["y_p"][r0:r0 + T, :] if ti < NT else O["y_s"][:, :]
                    STO(dsto, big[0][:T, :], big[0])
            pes.close()
            cur_es[0] = es
            PA.clear()
            PB.clear()
        layer_in = scr[2]
    if FILL_TABLE and not limit.get("nofill"):
        fill_rhs = CS.cmf[:, :, :].rearrange("p c t -> p (c t)")[:, 0:512]
        S.filler = lambda e: e.matmul(banks[7][:, :], lhsT=identb[:, :], rhs=fill_rhs, start=True, stop=True)
        S.fill_table = {int(kv.split(":")[0]): int(kv.split(":")[1]) for kv in FILL_TABLE.split(",") if kv}
    S.emit()
    build.pe_groups = S.pe_groups
    es.close()
    return nc, dbg_out


_CACHE = {}


def make_in_maps(inputs):
    g = {k: np.ascontiguousarray(np.asarray(v, dtype=np.float32)) for k, v in inputs.items()}
    consts = host_consts()
    maps = []
    for c in range(8):
        m = {}
        m["xp"] = g["x_prompt"][c]
        m["xs"] = g["x_sample"][c * NSB:(c + 1) * NSB].reshape(NSB * SL, D)
        m["call"] = np.concatenate([g["c_prompt"][c:c + 1], g["c_sample"][c * NSB:(c + 1) * NSB]], axis=0)
        m["st_sc"] = g["state_ssd_conv"][:, c * NSB:(c + 1) * NSB]
        m["st_ssm"] = g["state_ssm"][:, c * NSB:(c + 1) * NSB]
        m["st_gc"] = g["state_gdn_conv"][:, c * NSB:(c + 1) * NSB]
        m["st_gdn"] = g["state_gdn"][:, c * NSB:(c + 1) * NSB]
        for k in ("norm_w", "w_ada", "b_ada", "w_in", "ssd_conv_w", "ssd_conv_b", "ssd_dt_bias", "ssd_a_log", "ssd_d",
                  "ssd_norm_w", "gdn_conv_w", "gdn_dt_bias", "gdn_a_log", "gdn_norm_w", "w_out"):
            m[k] = g[k]
        m["final_norm_w"] = g["final_norm_w"].reshape(1, D)
        for k, v in consts.items():
            m["c_" + k] = v
        maps.append({k: np.ascontiguousarray(v) for k, v in m.items()})
    return maps


def kernel(**inputs):
    if "nc" not in _CACHE:
        _CACHE["nc"] = build()[0]
    nc = _CACHE["nc"]
    maps = make_in_maps(inputs)
    res = run_bass_kernel_spmd(nc, maps, core_ids=list(range(8)))
    R = res.results
    cat = lambda k, ax: np.concatenate([np.asarray(r[k]) for r in R], axis=ax)
    y_p = np.stack([np.asarray(r["y_p"]) for r in R], 0)
    y_s = np.stack([np.asarray(r["y_s"]).reshape(NSB, SL, D) for r in R], 0).reshape(8 * NSB, SL, D)
    ncs_p = np.stack([np.asarray(r["ncs_p"]) for r in R], 1)
    nssm_p = np.stack([np.asarray(r["nssm_p"]).reshape(DEPTH, 16, 64, 128) for r in R], 1)
    ngc_p = np.stack([np.asarray(r["ngc_p"]) for r in R], 1)
    ngdn_p = np.stack([np.asarray(r["ngdn_p"]) for r in R], 1)
    ncs_s = np.concatenate([np.asarray(r["ncs_s"]).reshape(DEPTH, NSB, 3, 1536) for r in R], 1)
    nssm_s = np.concatenate([np.asarray(r["nssm_s"]).reshape(DEPTH, NSB, 16, 64, 128) for r in R], 1)
    ngc_s = np.concatenate([np.asarray(r["ngc_s"]).reshape(DEPTH, NSB, 3, 3072) for r in R], 1)
    ngdn_s = np.concatenate([np.asarray(r["ngdn_s"]) for r in R], 1)
    outs = (y_p, y_s, ncs_p, nssm_p, ngc_p, ngdn_p, ncs_s, nssm_s, ngc_s, ngdn_s)
    return tuple(np.ascontiguousarray(o, dtype=np.float32) for o in outs)
```

```python
import contextlib
import numpy as np
import concourse.bass as bass
import concourse.mybir as mybir
from concourse.bass_utils import run_bass_kernel_spmd

F32 = mybir.dt.float32
F32R = mybir.dt.float32r
BF16 = mybir.dt.bfloat16
AF = mybir.ActivationFunctionType
ALU = mybir.AluOpType
AX = mybir.AxisListType

ENGS = ("pe", "dve", "act", "pool", "sp")


class Buf:
    def __init__(self, t, name):
        self.t = t
        self.name = name
        self.st = {}
        self.dma_sem = None
        self.dma_cnt = 0

    def __getitem__(self, k):
        return self.t[k]


class Instr:
    __slots__ = ("eng", "fn", "deps", "signal", "count", "is_dma", "buf_sem", "dma_count")

    def __init__(self, eng, fn):
        self.eng = eng
        self.fn = fn
        self.deps = []
        self.signal = False
        self.count = None
        self.is_dma = False
        self.buf_sem = None
        self.dma_count = None


class Sched:
    def __init__(self, nc):
        self.nc = nc
        self.streams = {e: [] for e in ENGS}
        self.all = []
        self.filler = None
        self.nfill = 0
        self.fill_table = {}
        self.pe_groups = 0

    def _states(self, buf, key):
        st = buf.st
        if key is None:
            if None not in st:
                st[None] = [None, {}]
            return list(st.values())
        if key not in st:
            if None in st:
                st[key] = [st[None][0], dict(st[None][1])]
            else:
                st[key] = [None, {}]
        res = [st[key]]
        if None in st:
            res.append(st[None])
        return res

    @staticmethod
    def _norm(lst):
        out = []
        for x in lst or []:
            out.append((x, None) if isinstance(x, Buf) else x)
        return out

    def add(self, eng, fn, reads=None, writes=None, dma_buf=None):
        ins = Instr(eng, fn)
        ins.is_dma = dma_buf is not None
        reads = self._norm(reads)
        writes = self._norm(writes)
        deps, raw = [], []
        for buf, key in reads:
            for s in self._states(buf, key):
                if s[0] is not None:
                    deps.append(s[0])
                    raw.append(s[0])
                if getattr(buf, "excl", False):
                    for r in s[1].values():
                        if r.eng != eng:
                            deps.append(r)
        for buf, key in writes:
            for s in self._states(buf, key):
                if s[0] is not None:
                    deps.append(s[0])
                deps.extend(s[1].values())
        fdeps = {}
        for d in deps:
            if d is ins:
                continue
            if (not ins.is_dma) and (not d.is_dma) and d.eng == eng:
                if eng == "pe" or not any(d is r for r in raw):
                    continue
            if d.is_dma:
                fdeps[id(d)] = (d, d.buf_sem.dma_cnt * 16)
            else:
                fdeps[id(d)] = (d, None)
        ins.deps = list(fdeps.values())
        if ins.is_dma:
            ins.buf_sem = dma_buf
            dma_buf.dma_cnt += 1
            ins.dma_count = dma_buf.dma_cnt * 16
        rkey = ("dma", id(ins)) if ins.is_dma else eng
        for buf, key in reads:
            for s in self._states(buf, key):
                s[1][rkey] = ins
        for buf, key in writes:
            for s in self._states(buf, key):
                s[0] = ins
                if key is None or s is not buf.st.get(None):
                    s[1] = {}
        self.streams[eng].append(ins)
        self.all.append(ins)
        return ins

    def barrier(self, dma_bufs_all):
        last = {}
        for e in ("pe", "dve", "act", "pool"):
            for ins in reversed(self.streams[e]):
                if not isinstance(ins, tuple) and not ins.is_dma:
                    last[e] = ins
                    ins.signal = True
                    break
        snap = [(b, b.dma_cnt * 16) for b in dma_bufs_all if b.dma_cnt > 0]
        mark = ("barrier", last, snap)
        for e in ENGS:
            self.streams[e].append(mark)

    def emit(self, final_wait_eng="sp"):
        nc = self.nc
        for ins in self.all:
            for d, _v in ins.deps:
                if not d.is_dma:
                    d.signal = True
        with contextlib.ExitStack() as es:
            eng_sem = {e: es.enter_context(nc.semaphore("s_" + e)) for e in ("pe", "dve", "act", "pool")}
            dma_bufs = []
            for ins in self.all:
                if isinstance(ins, tuple):
                    continue
                if ins.is_dma and ins.buf_sem.dma_sem is None:
                    ins.buf_sem.dma_sem = es.enter_context(nc.semaphore("d_%d" % len(dma_bufs)))
                    dma_bufs.append(ins.buf_sem)
            cnt = {e: 0 for e in eng_sem}
            for ins in self.all:
                if not ins.is_dma and ins.signal:
                    cnt[ins.eng] += 1
                    ins.count = cnt[ins.eng]
            block = es.enter_context(nc.Block())
            engobj = {"pe": block.tensor, "dve": block.vector, "act": block.scalar, "pool": block.gpsimd,
                      "sp": block.sync}
            sched = self

            def make(ename):
                def body(eng):
                    water = {}
                    for ins in sched.streams[ename]:
                        if isinstance(ins, tuple):
                            _, last, snap = ins
                            nw_ = 0
                            for e2, li in last.items():
                                if e2 != ename and water.get(("e", e2), 0) < li.count:
                                    water[("e", e2)] = li.count
                                    eng.wait_ge(eng_sem[e2], li.count)
                                    nw_ += 1
                            for b_, v_ in snap:
                                if water.get(("d", id(b_)), 0) < v_:
                                    water[("d", id(b_))] = v_
                                    eng.wait_ge(b_.dma_sem, v_)
                                    nw_ += 1
                            continue
                        need = {}
                        for d, dv in ins.deps:
                            if d.is_dma:
                                key, sem, val = ("d", id(d.buf_sem)), d.buf_sem.dma_sem, dv
                            else:
                                key, sem, val = ("e", d.eng), eng_sem[d.eng], d.count
                            if need.get(key, (None, 0))[1] < val:
                                need[key] = (sem, val)
                        todo = [(key, sem, val) for key, (sem, val) in need.items() if water.get(key, 0) < val]
                        if ename == "pe" and not ins.is_dma:
                            k_ = sched.pe_groups
                            sched.pe_groups += 1
                            if todo and sched.filler is not None:
                                for _ in range(sched.fill_table.get(k_, 0)):
                                    sched.filler(eng)
                        for key, sem, val in todo:
                            water[key] = val
                            eng.wait_ge(sem, val)
                        bi = ins.fn(eng)
                        if ins.is_dma:
                            bi.then_inc(ins.buf_sem.dma_sem, 16)
                        elif ins.signal:
                            bi.then_inc(eng_sem[ename], 1)
                    if ename == final_wait_eng:
                        for b in dma_bufs:
                            eng.wait_ge(b.dma_sem, b.dma_cnt * 16)
                return body

            for ename in ENGS:
                engobj[ename](make(ename))


FILL_TABLE = (
    "8:18,16:5,24:4,32:16,40:5,48:4,56:16,64:5,72:4,80:16,88:5,96:4,104:16,112:5,120:4,128:16,136:5,145:4,154:15,16"
    "3:3,172:4,181:15,190:3,199:4,208:1,210:1,212:80,220:7,316:3,384:11,387:2,417:25,437:11,457:47,465:3,481:14,489"
    ":12,585:3,653:11,656:2,686:24,706:11,726:45,734:3,750:12,758:12,854:3,922:11,925:2,955:24,975:11,995:45,1003:3"
    ",1019:12,1027:12,1123:3,1191:11,1194:2,1224:24,1244:11,1264:45,1272:3,1288:12,1296:12,1392:3,1460:11,1463:2,14"
    "93:24,1513:11,1533:45,1541:3,1557:12,1565:12,1661:3,1729:11,1732:2,1762:24,1782:11,1802:45,1810:3,1826:12,1834"
    ":12,1930:3,1998:11,2001:2,2031:24,2051:11,2071:45,2079:3,2095:12,2103:12,2199:3,2267:11,2270:2,2300:24,2320:11"
    ",2340:45,2348:3,2364:12,2372:12,2468:3,2536:11,2539:2,2569:23,2589:11,2609:45,2617:3,2633:14,2641:12,2737:3,28"
    "05:11,2808:2,2838:23,2858:11,2878:45,2886:3,2902:14,2910:12,3006:3,3074:11,3077:2,3107:23,3127:11,3147:45,3155"
    ":3,3171:14,3179:12,3275:3,3343:11,3346:2,3376:23,3396:11,3416:45,3424:3,3440:14,3448:12,3544:3,3612:11,3615:2,"
    "3645:17,3665:11,3685:45,3693:3,3709:14,3717:12,3813:3,3881:11,3884:2,3914:17,3934:11,3954:45,3962:3,3978:14,39"
    "86:12,4082:3,4150:11,4153:2,4183:17,4203:11,4223:45,4231:3,4247:14,4255:12,4351:3,4431:15,4434:2,4464:25,4484:"
    "11,4488:5,4496:10,4512:33,4520:3,4536:12,4544:22,4652:10,4732:11,4735:3,4765:26,4789:26,4793:5,4801:35,4809:26"
    ",4813:5,4821:35,4829:26,4833:5,4841:33,4849:26,4853:5,4861:33,4869:26,4873:5,4881:34,4889:25,4893:5,4901:35,49"
    "09:26,4913:5,4921:33,4929:26,4933:5,4941:33,4949:26,4953:5,4961:33,4969:26,4973:5,4981:32,4989:26,4993:5,5001:"
    "37,5009:26,5013:5,5021:33,5029:26,5033:7,5041:32,5049:26,5053:5,5061:34,5069:26,5073:5,5081:33,5089:26,5093:5,"
    "5101:11,5117:33,5125:1,5141:80,5149:8,5245:3,5301:20,5304:1,5313:20,5317:12,5342:13,5350:7,5362:1,5374:1,5386:"
    "3,5394:3,5398:3,5406:2,5414:2,5418:7,5426:2,5442:27,5446:2,5454:18,5462:12,5558:3,5614:20,5617:1,5626:20,5630:"
    "12,5655:11,5663:9,5675:3,5687:3,5699:1,5707:3,5711:2,5719:3,5727:2,5731:7,5739:2,5755:27,5759:2,5767:18,5775:1"
    "2,5871:3,5927:20,5930:1,5939:20,5943:12,5968:11,5976:7,5988:1,6000:1,6020:2,6024:2,6032:2,6040:1,6044:6,6052:2"
    ",6068:27,6072:2,6080:18,6088:12,6184:3,6240:20,6243:1,6252:20,6256:12,6281:9,6289:8,6301:1,6313:2,6325:1,6333:"
    "3,6337:3,6345:3,6353:2,6357:7,6365:2,6381:28,6385:2,6393:18,6401:12,6497:3,6553:20,6556:1,6565:19,6569:12,6594"
    ":18,6602:4,6614:1,6626:1,6646:2,6650:2,6658:2,6666:1,6670:6,6678:2,6694:28,6698:2,6706:18,6714:12,6810:3,6866:"
    "20,6869:1,6878:19,6882:12,6907:13,6915:6,6927:1,6939:1,6951:1,6959:2,6963:2,6971:2,6979:1,6983:4,6991:3,7007:2"
    "7,7011:2,7019:18,7027:12,7123:3,7179:20,7182:1,7191:21,7195:12,7220:8,7228:8,7240:1,7252:1,7264:2,7272:3,7276:"
    "3,7284:3,7292:2,7296:9,7304:2,7320:27,7324:2,7332:18,7340:12,7436:3,7492:20,7495:1,7504:19,7508:12,7533:15,754"
    "1:6,7553:1,7565:3,7577:1,7585:3,7589:3,7597:3,7605:1,7609:6,7617:2,7633:28,7637:2,7645:18,7653:12,7749:3,7805:"
    "20,7808:1,7817:20,7821:12,7846:9,7854:7,7866:1,7878:1,7898:2,7902:2,7910:2,7918:1,7922:5,7930:2,7946:27,7950:2"
    ",7958:18,7966:12,8062:3,8118:20,8121:1,8130:18,8134:12,8159:16,8167:9,8179:1,8191:1,8203:3,8211:3,8215:3,8223:"
    "3,8231:2,8235:9,8243:2,8259:27,8263:2,8271:18,8279:12,8375:3,8431:20,8434:1,8443:20,8447:12,8472:10,8480:8,849"
    "2:1,8524:2,8528:2,8536:2,8544:1,8548:5,8556:2,8572:27,8576:2,8584:18,8592:12,8688:3,8744:20,8747:1,8756:20,876"
    "0:12,8785:14,8793:7,8805:1,8817:3,8829:2,8837:3,8841:3,8849:3,8857:2,8861:7,8869:2,8885:27,8889:2,8897:18,8905"
    ":12,9001:3,9057:20,9060:1,9069:20,9073:12,9098:8,9106:8,9118:1,9130:1,9142:3,9150:2,9154:2,9162:2,9170:1,9174:"
    "5,9182:2,9198:27,9202:2,9210:18,9218:12,9314:3,9370:20,9373:1,9382:18,9386:12,9411:14,9419:4,9431:1,9443:1,945"
    "5:2,9463:3,9467:3,9475:3,9483:2,9487:7,9495:2,9511:27,9515:2,9523:18,9531:12,9627:3,9683:20,9686:1,9695:21,969"
    "9:12,9724:12,9732:10,9744:1,9756:1,9776:2,9780:2,9788:2,9796:1,9800:6,9808:2,9824:27,9828:2,9836:18,9844:12,99"
    "40:3,10008:20,10011:1,10020:21,10024:12,10049:14,10057:10,10069:1,10101:2,10105:2,10113:2,10121:1,10125:6,1013"
    "3:2,10149:27,10153:2,10161:18,10169:32,10277:10,10345:19,10348:2,10357:47,10361:12,10386:6,10390:4,10394:1,104"
    "02:5,10436:2,10445:3,10447:1,10450:79,10466:3,10482:1,10484:2,10493:3,10498:80,10514:3,10532:2,10534:1,10541:2"
    ",10545:1,10546:80,10562:3,10580:2,10589:2,10606:11,10610:1,10618:80,10626:8,10722:3,10778:20,10781:1,10790:20,"
    "10794:12,10819:8,10827:7,10839:1,10851:1,10871:2,10875:2,10883:2,10891:1,10895:4,10903:1,10919:27,10923:2,1093"
    "1:18,10939:12,11035:3,11091:20,11094:1,11103:20,11107:12,11132:12,11140:5,11152:1,11164:1,11184:2,11188:2,1119"
    "6:2,11204:1,11208:4,11216:2,11232:27,11236:2,11244:18,11252:12,11348:3,11404:20,11407:1,11416:19,11420:12,1144"
    "5:11,11453:10,11465:1,11477:3,11489:2,11497:3,11501:3,11509:3,11517:2,11521:7,11529:2,11545:28,11549:2,11557:1"
    "8,11565:12,11661:3,11717:20,11720:1,11729:19,11733:12,11758:12,11766:9,11778:1,11810:2,11814:2,11822:2,11830:1"
    ",11834:5,11842:2,11858:27,11862:2,11870:18,11878:12,11974:3,12030:20,12033:1,12042:19,12046:12,12071:12,12079:"
    "10,12091:1,12103:3,12115:1,12123:3,12127:3,12135:3,12143:2,12147:7,12155:2,12171:26,12175:2,12183:18,12191:12,"
    "12287:3,12343:20,12346:1,12355:21,12359:12,12384:12,12392:9,12404:1,12436:2,12440:2,12448:2,12456:1,12460:4,12"
    "468:1,12484:26,12488:2,12496:18,12504:12,12600:3,12656:20,12659:1,12668:20,12672:12,12697:11,12705:10,12717:1,"
    "12729:1,12741:2,12749:3,12753:3,12761:3,12769:2,12773:9,12781:2,12797:26,12801:2,12809:18,12817:12,12913:3,129"
    "69:20,12972:1,12981:19,12985:12,13010:12,13018:10,13030:1,13042:2,13054:1,13062:3,13066:2,13074:3,13082:3,1308"
    "6:7,13094:2,13110:28,13114:2,13122:18,13130:12,13226:3,13282:21,13285:1,13294:19,13298:13,13323:16,13331:7,133"
    "43:1,13355:1,13375:3,13379:2,13387:2,13395:1,13399:4,13407:1,13423:27,13427:2,13435:18,13443:12,13539:3,13595:"
    "20,13598:1,13607:20,13611:12,13636:10,13644:9,13656:1,13668:1,13680:2,13688:3,13692:3,13700:3,13708:2,13712:7,"
    "13720:2,13736:27,13740:2,13748:18,13756:12,13852:3,13908:20,13911:1,13920:18,13924:12,13949:12,13957:9,13969:1"
    ",13981:3,13993:1,14001:2,14005:2,14013:2,14021:1,14025:6,14033:2,14049:28,14053:2,14061:18,14069:12,14165:3,14"
    "221:20,14224:1,14233:19,14237:12,14262:13,14270:10,14282:1,14294:1,14314:2,14318:2,14326:2,14334:1,14338:4,143"
    "46:2,14362:28,14366:2,14374:18,14382:12,14478:3,14534:20,14537:1,14546:21,14550:12,14575:12,14583:6,14595:1,14"
    "627:2,14631:2,14639:2,14647:1,14651:5,14659:2,14675:27,14679:2,14687:18,14695:12,14791:3,14847:20,14850:1,1485"
    "9:20,14863:12,14888:9,14896:9,14908:1,14920:1,14932:1,14940:2,14944:3,14952:2,14960:1,14964:6,14972:2,14988:27"
    ",14992:2,15000:18,15008:12,15104:3,15160:20,15163:1,15172:18,15176:12,15201:13,15209:10,15221:1,15233:1,15253:"
    "2,15257:2,15265:2,15273:1,15277:5,15285:2,15301:27,15305:2,15313:18,15321:12,15417:3,15485:20,15488:1,15497:20"
    ",15501:12,15526:12,15534:10,15546:1,15558:2,15570:1,15578:3,15582:2,15590:2,15598:1,15602:4,15610:2,15626:27,1"
    "5630:2,15638:18,15646:31,15754:10,15822:19,15825:2,15834:46,15838:12,15863:10,15867:4,15871:1,15879:5,15913:2,"
    "15922:3,15924:1,15926:1,15927:80,15943:3,15961:2,15970:4,15975:80,15991:3,16009:2,16011:1,16018:3,16023:80,160"
    "39:3,16057:2,16066:3,16068:1,16083:14,16087:2,16095:51,16103:31,16111:5,16119:4,16127:16,16135:5,16143:4,16151"
    ":16,16159:5,16167:6,16175:17,16183:5,16191:6,16199:16,16207:5,16215:5,16223:16,16231:5,16240:4,16249:15,16258:"
    "3,16267:4,16276:14,16285:3,16294:4,16303:1,16305:2,16307:80,16315:7,16411:3,16479:11,16482:2,16512:25,16532:11"
    ",16552:45,16560:3,16576:12,16584:12,16680:3,16748:11,16751:2,16781:25,16801:12,16821:45,16829:3,16845:12,16853"
    ":12,16949:3,17017:11,17020:2,17050:25,17070:12,17090:45,17098:3,17114:12,17122:12,17218:3,17286:11,17289:2,173"
    "19:25,17339:12,17359:45,17367:3,17383:12,17391:12,17487:3,17555:11,17558:2,17588:25,17608:12,17628:45,17636:3,"
    "17652:12,17660:12,17756:3,17824:11,17827:2,17857:25,17877:12,17897:45,17905:3,17921:12,17929:12,18025:3,18093:"
    "11,18096:2,18126:25,18146:11,18166:45,18174:3,18190:13,18198:12,18294:3,18362:11,18365:2,18395:24,18415:13,184"
    "35:45,18443:3,18459:12,18467:12,18563:3,18631:11,18634:2,18664:24,18684:11,18704:45,18712:3,18728:12,18736:12,"
    "18832:3,18900:11,18903:2,18933:24,18953:11,18973:45,18981:3,18997:12,19005:12,19101:3,19169:11,19172:2,19202:2"
    "3,19222:11,19242:45,19250:3,19266:14,19274:12,19370:3,19438:11,19441:2,19471:17,19491:11,19511:45,19519:3,1953"
    "5:14,19543:12,19639:3,19707:11,19710:2,19740:17,19760:11,19780:45,19788:3,19804:14,19812:12,19908:3,19976:11,1"
    "9979:2,20009:17,20029:11,20049:47,20057:3,20073:12,20081:12,20177:3,20245:11,20248:2,20278:16,20298:11,20318:4"
    "5,20326:3,20342:12,20350:12,20446:3,20526:15,20529:2,20559:25,20579:11,20583:5,20591:10,20607:33,20615:3,20631"
    ":13,20639:22,20747:10,20827:13,20830:3,20860:24,20884:26,20888:5,20896:37,20904:26,20908:5,20916:34,20924:25,2"
    "0928:5,20936:37,20944:26,20948:5,20956:33,20964:26,20968:5,20976:33,20984:25,20988:5,20996:36,21004:26,21008:5"
    ",21016:33,21024:26,21028:5,21036:33,21044:26,21048:5,21056:33,21064:26,21068:5,21076:33,21084:26,21088:5,21096"
    ":33,21104:27,21108:5,21116:32,21124:26,21128:5,21136:32,21144:26,21148:5,21156:33,21164:26,21168:5,21176:34,21"
    "184:26,21188:5,21196:11,21212:33,21220:2,21236:80,21244:9,21340:3,21396:20,21399:1,21408:20,21412:12,21437:11,"
    "21445:9,21457:1,21469:1,21481:1,21489:2,21493:2,21501:2,21509:1,21513:4,21521:1,21537:26,21541:2,21549:18,2155"
    "7:12,21653:3,21709:20,21712:1,21721:20,21725:12,21750:10,21758:9,21770:1,21782:3,21794:1,21802:4,21806:3,21814"
    ":3,21822:2,21826:7,21834:2,21850:27,21854:2,21862:18,21870:12,21966:3,22022:20,22025:1,22034:20,22038:12,22063"
    ":11,22071:10,22083:1,22095:1,22107:3,22115:3,22119:3,22127:3,22135:2,22139:7,22147:2,22163:27,22167:2,22175:18"
    ",22183:12,22279:3,22335:20,22338:1,22347:20,22351:12,22376:12,22384:9,22396:1,22408:3,22420:1,22428:2,22432:2,"
    "22440:2,22448:1,22452:4,22460:2,22476:27,22480:2,22488:18,22496:13,22592:3,22648:20,22651:1,22660:21,22664:12,"
    "22689:12,22697:9,22709:1,22721:2,22733:1,22741:3,22745:3,22753:3,22761:2,22765:7,22773:2,22789:26,22793:2,2280"
    "1:18,22809:12,22905:3,22961:20,22964:1,22973:18,22977:12,23002:7,23010:10,23022:1,23034:1,23054:2,23058:2,2306"
    "6:2,23074:1,23078:6,23086:2,23102:28,23106:2,23114:18,23122:12,23218:3,23274:20,23277:1,23286:20,23290:12,2331"
    "5:14,23323:7,23335:1,23347:1,23359:2,23367:3,23371:3,23379:2,23387:1,23391:5,23399:2,23415:27,23419:2,23427:18"
    ",23435:12,23531:3,23587:20,23590:1,23599:20,23603:12,23628:9,23636:8,23648:1,23660:2,23672:1,23680:3,23684:3,2"
    "3692:3,23700:2,23704:7,23712:2,23728:27,23732:2,23740:18,23748:12,23844:3,23900:20,23903:1,23912:21,23916:12,2"
    "3941:8,23949:7,23961:1,23973:1,23985:3,23993:3,23997:3,24005:3,24013:2,24017:7,24025:3,24041:26,24045:2,24053:"
    "18,24061:12,24157:3,24213:20,24216:1,24225:20,24229:12,24254:12,24262:10,24274:1,24286:2,24298:1,24306:3,24310"
    ":3,24318:3,24326:3,24330:7,24338:2,24354:29,24358:2,24366:18,24374:12,24470:3,24526:20,24529:1,24538:19,24542:"
    "12,24567:16,24575:9,24587:1,24599:3,24619:2,24623:2,24631:2,24639:1,24643:4,24651:2,24667:28,24671:3,24679:18,"
    "24687:12,24783:3,24839:20,24842:1,24851:19,24855:12,24880:7,24888:9,24900:1,24912:1,24924:1,24932:2,24936:2,24"
    "944:2,24952:1,24956:4,24964:2,24980:27,24984:2,24992:18,25000:12,25096:3,25152:20,25155:1,25164:21,25168:13,25"
    "193:12,25201:10,25213:2,25225:1,25237:1,25245:3,25249:2,25257:3,25265:2,25269:7,25277:2,25293:28,25297:2,25305"
    ":18,25313:12,25409:3,25465:20,25468:1,25477:20,25481:12,25506:10,25514:9,25526:1,25538:2,25550:1,25558:3,25562"
    ":3,25570:3,25578:3,25582:7,25590:2,25606:28,25610:2,25618:18,25626:12,25722:3,25778:20,25781:1,25790:19,25794:"
    "12,25819:9,25827:8,25839:1,25851:2,25863:1,25871:3,25875:3,25883:3,25891:2,25895:7,25903:2,25919:26,25923:2,25"
    "931:18,25939:12,26035:3,26103:20,26106:1,26115:20,26119:12,26144:13,26152:10,26164:1,26176:2,26188:1,26196:3,2"
    "6200:3,26208:3,26216:2,26220:7,26228:2,26244:25,26248:2,26256:18,26264:30,26372:10,26440:19,26443:2,26452:45,2"
    "6456:12,26481:8,26485:4,26489:1,26497:5,26531:2,26540:2,26545:80,26561:3,26579:2,26588:2,26593:78,26609:3,2662"
    "7:2,26629:1,26636:3,26638:1,26641:80,26657:3,26675:1,26684:2,26701:12,26705:1,26713:80,26721:7,26817:3,26873:2"
    "0,26876:1,26885:20,26889:12,26914:9,26922:8,26934:2,26946:3,26958:2,26966:3,26970:3,26978:3,26986:2,26990:7,26"
    "998:2,27014:27,27018:2,27026:53,27034:12,27130:3,27186:20,27189:1,27198:20,27202:12,27227:12,27235:10,27247:1,"
    "27259:3,27279:3,27283:2,27291:3,27299:4,27303:7,27311:2,27327:27,27331:2,27339:54,27347:12,27443:3,27499:19,27"
    "502:1,27511:18,27515:12,27540:12,27548:10,27560:1,27572:1,27592:3,27596:2,27604:2,27612:1,27616:5,27624:1,2764"
    "0:27,27644:2,27652:53,27660:12,27756:3,27812:19,27815:1,27824:20,27828:12,27853:11,27861:8,27873:1,27885:3,278"
    "97:1,27905:2,27909:2,27917:2,27925:1,27929:4,27937:2,27953:27,27957:2,27965:53,27973:12,28069:3,28125:20,28128"
    ":1,28137:20,28141:12,28166:12,28174:10,28186:1,28198:3,28210:1,28218:3,28222:3,28230:3,28238:2,28242:7,28250:2"
    ",28266:28,28270:2,28278:54,28286:12,28382:3,28438:19,28441:3,28450:19,28454:12,28479:13,28487:5,28499:1,28531:"
    "2,28535:2,28543:2,28551:1,28555:4,28563:2,28579:24,28583:2,28591:53,28599:12,28695:3,28751:19,28754:1,28763:20"
    ",28767:12,28792:12,28800:10,28812:1,28824:3,28836:1,28844:3,28848:3,28856:3,28864:2,28868:7,28876:2,28892:28,2"
    "8896:2,28904:54,28912:12,29008:3,29064:20,29067:1,29076:20,29080:12,29105:12,29113:10,29125:1,29137:3,29157:2,"
    "29161:2,29169:2,29177:1,29181:4,29189:2,29205:27,29209:2,29217:54,29225:12,29321:3,29377:20,29380:1,29389:20,2"
    "9393:12,29418:14,29426:9,29438:1,29450:1,29470:2,29474:2,29482:2,29490:1,29494:4,29502:2,29518:27,29522:2,2953"
    "0:54,29538:12,29634:3,29690:20,29693:1,29702:20,29706:12,29731:12,29739:10,29751:1,29763:3,29775:1,29783:2,297"
    "87:2,29795:2,29803:1,29807:4,29815:2,29831:27,29835:2,29843:53,29851:12,29947:3,30003:19,30006:1,30015:20,3001"
    "9:12,30044:12,30052:9,30064:1,30076:1,30088:1,30096:2,30100:2,30108:2,30116:1,30120:4,30128:2,30144:26,30148:2"
    ",30156:53,30164:12,30260:3,30316:19,30319:1,30328:20,30332:12,30357:10,30365:9,30377:1,30389:3,30401:1,30409:2"
    ",30413:2,30421:2,30429:1,30433:5,30441:2,30457:27,30461:2,30469:54,30477:12,30573:3,30629:20,30632:1,30641:21,"
    "30645:12,30670:12,30678:10,30690:1,30702:1,30714:1,30722:2,30726:2,30734:2,30742:1,30746:5,30754:2,30770:26,30"
    "774:2,30782:54,30790:12,30886:3,30942:20,30945:1,30954:20,30958:12,30983:11,30991:9,31003:1,31015:3,31027:1,31"
    "035:2,31039:2,31047:2,31055:1,31059:4,31067:2,31083:29,31087:2,31095:53,31103:12,31199:3,31255:20,31258:1,3126"
    "7:20,31271:12,31296:12,31304:10,31316:1,31328:3,31348:2,31352:2,31360:2,31368:1,31372:6,31380:2,31396:27,31400"
    ":2,31408:54,31416:12,31512:3,31580:20,31583:1,31592:20,31596:12,31621:9,31629:8,31641:1,31653:1,31665:3,31673:"
    "2,31677:2,31685:2,31693:1,31697:6,31705:2,31721:27,31725:2,31733:53,31741:16,31849:10,31917:19,31920:2,31929:4"
    "7,31933:12,31958:14,31962:8,31966:1,31974:5,32008:2,32017:3,32019:1,32022:79,32038:3,32056:2,32065:3,32070:80,"
    "32086:3,32104:2,32106:1,32113:2,32118:80,32134:3,32152:2,32161:2,32178:13,32182:1"
)
D = 1024
SEQ = 2048
NT = SEQ // 128
NSB = 16
SL = 4
DEPTH = 2
IN_DIM = 6688
EPS = 1e-6
NEG = -30000.0
OFF_ZS, OFF_XBC, OFF_DT, OFF_ZG, OFF_QKV, OFF_A, OFF_B = 0, 1024, 2560, 2576, 3600, 6672, 6680


class Cfg:
    def __init__(self, name, T, nseq, L, Q, chain):
        self.name, self.T, self.nseq, self.L, self.Q, self.chain = name, T, nseq, L, Q, chain
        self.nch = T // Q
        self.nlev = {64: 5, 4: 1}[Q]


CP = Cfg("p", 128, 1, 128, 64, True)
CS = Cfg("s", 64, NSB, SL, 4, False)


def host_consts():
    c = {"ident": np.eye(128, dtype=np.float32), "ones": np.ones((128, 128), np.float32)}
    for cfg in (CP, CS):
        T, Q, nch = cfg.T, cfg.Q, cfg.nch
        ch = np.arange(T) // Q
        same = ch[:, None] == ch[None, :]
        idx = np.arange(T)
        le = idx[:, None] <= idx[None, :]
        lt = idx[:, None] < idx[None, :]
        n = cfg.name
        c["tri_" + n] = (same & le).astype(np.float32)
        c["blk_" + n] = same.astype(np.float32)
        c["nmI_" + n] = np.where(same & le, 0.0, NEG).astype(np.float32)
        c["nmS_" + n] = np.where(same & lt, 0.0, NEG).astype(np.float32)
        c["nmSr_" + n] = np.ascontiguousarray(c["nmS_" + n].T)
        cm = (ch[None, :] == np.arange(nch)[:, None]).astype(np.float32)
        c["cmf_" + n] = np.ascontiguousarray(np.broadcast_to(cm[None], (128, nch, T))).astype(np.float32)
        c["cmT_" + n] = np.ascontiguousarray(cm.T)
    e = np.zeros((17, 128), np.float32)
    e[0, :] = 1.0
    c["E_p"] = e
    e = np.zeros((17, 64), np.float32)
    for t in range(64):
        e[1 + t // SL, t] = 1.0
    c["E_s"] = e
    return c


def bl(ap, n):
    return ap.unsqueeze(len(ap.shape)).to_broadcast(list(ap.shape) + [n])


def bm(ap, n):
    return ap.unsqueeze(1).to_broadcast([ap.shape[0], n] + list(ap.shape[1:]))


def build(debug=(), limit=None):
    limit = limit or {}
    nc = bass.Bass("TRN2", target_bir_lowering=False)
    S = Sched(nc)
    es = contextlib.ExitStack()
    consts_np = host_consts()

    def din(name, shape):
        return nc.dram_tensor(name, list(shape), F32, kind="ExternalInput").ap()

    def dout(name, shape):
        return nc.dram_tensor(name, list(shape), F32, kind="ExternalOutput").ap()

    I = {}
    I["xp"] = din("xp", [SEQ, D])
    I["xs"] = din("xs", [NSB * SL, D])
    I["call"] = din("call", [17, D])
    I["st_sc"] = din("st_sc", [DEPTH, NSB, 3, 1536])
    I["st_ssm"] = din("st_ssm", [DEPTH, NSB, 16, 64, 128])
    I["st_gc"] = din("st_gc", [DEPTH, NSB, 3, 3072])
    I["st_gdn"] = din("st_gdn", [DEPTH, NSB, 8, 128, 128])
    wshapes = {"norm_w": [DEPTH, D], "w_ada": [DEPTH, D, 3 * D], "b_ada": [DEPTH, 3 * D], "w_in": [DEPTH, D, IN_DIM],
               "ssd_conv_w": [DEPTH, 4, 1536], "ssd_conv_b": [DEPTH, 1536], "ssd_dt_bias": [DEPTH, 16],
               "ssd_a_log": [DEPTH, 16], "ssd_d": [DEPTH, 16], "ssd_norm_w": [DEPTH, 1024],
               "gdn_conv_w": [DEPTH, 4, 3072], "gdn_dt_bias": [DEPTH, 8], "gdn_a_log": [DEPTH, 8],
               "gdn_norm_w": [DEPTH, 128], "w_out": [DEPTH, 2048, D], "final_norm_w": [1, D]}
    for k, shp in wshapes.items():
        I[k] = din(k, shp)
    for k, v in consts_np.items():
        I["c_" + k] = din("c_" + k, v.shape)
    O = {}
    O["y_p"] = dout("y_p", [SEQ, D])
    O["y_s"] = dout("y_s", [NSB * SL, D])
    O["ncs_p"] = dout("ncs_p", [DEPTH, 3, 1536])
    O["nssm_p"] = dout("nssm_p", [DEPTH, 1024, 128])
    O["ngc_p"] = dout("ngc_p", [DEPTH, 3, 3072])
    O["ngdn_p"] = dout("ngdn_p", [DEPTH, 8, 128, 128])
    O["ncs_s"] = dout("ncs_s", [DEPTH, NSB * 3, 1536])
    O["nssm_s"] = dout("nssm_s", [DEPTH, NSB, 1024, 128])
    O["ngc_s"] = dout("ngc_s", [DEPTH, NSB * 3, 3072])
    O["ngdn_s"] = dout("ngdn_s", [DEPTH, NSB, 8, 128, 128])
    NROW = SEQ + NSB * SL
    scr = [Buf(nc.dram_tensor("scr%d" % i, [NROW, D], F32, kind="Internal").ap(), "scr%d" % i) for i in range(3)]
    xin_ext = None

    allb = []

    def sb(name, shape, dt=F32):
        b_ = Buf(es.enter_context(nc.sbuf_tensor(name, list(shape), dt)), name)
        allb.append(b_)
        return b_

    banks = [Buf(es.enter_context(nc.psum_tensor("bank%d" % i, [128, 512], F32)), "bank%d" % i) for i in range(8)]
    for b_ in banks:
        b_.excl = True
    pstate = {"i": 0, "pinned": set()}
    NB = 7

    def pget():
        while True:
            i = pstate["i"] % NB
            pstate["i"] += 1
            if i not in pstate["pinned"]:
                return banks[i]

    def ppin(b):
        pstate["pinned"].add(banks.index(b))

    def punpin(b):
        pstate["pinned"].discard(banks.index(b))

    def pf(b, shape, p=128):
        n = int(np.prod(shape))
        ap = b[:p, 0:n]
        if len(shape) == 1:
            return ap
        names = " ".join("a%d" % i for i in range(len(shape)))
        kw = {"a%d" % i: shape[i] for i in range(1, len(shape))}
        return ap.rearrange("p (%s) -> p %s" % (names, names), **kw)

    def pb(b, shape, p=128):
        n = int(np.prod(shape))
        ap = b[:p, :].bitcast(BF16)[:, 0:n]
        if len(shape) == 1:
            return ap
        names = " ".join("a%d" % i for i in range(len(shape)))
        kw = {"a%d" % i: shape[i] for i in range(1, len(shape))}
        return ap.rearrange("p (%s) -> p %s" % (names, names), **kw)

    def MM(out, lhsT, rhs, R, W, start=True, stop=True, **kw):
        S.add("pe", lambda e: e.matmul(out, lhsT=lhsT, rhs=rhs, start=start, stop=stop, **kw), reads=R, writes=W)

    def TR(out, in_, ident, R, W):
        S.add("pe", lambda e: e.transpose(out=out, in_=in_, identity=ident), reads=R, writes=W)

    def ACT(out, in_, func, R, W, **kw):
        S.add("act", lambda e: e.activation(out=out, in_=in_, func=func, **kw), reads=R, writes=W)

    def TT(out, in0, in1, op, R, W, eng="dve"):
        S.add(eng, lambda e: e.tensor_tensor(out=out, in0=in0, in1=in1, op=op), reads=R, writes=W)

    def TS(out, in0, s1, s2, op0, op1, R, W, eng="dve"):
        S.add(eng, lambda e: e.tensor_scalar(out=out, in0=in0, scalar1=s1, scalar2=s2, op0=op0, op1=op1), reads=R, writes=W)

    def STT(out, in0, scalar, in1, op0, op1, R, W):
        S.add("dve", lambda e: e.scalar_tensor_tensor(out=out, in0=in0, scalar=scalar, in1=in1, op0=op0, op1=op1), reads=R, writes=W)

    def CP_(out, in_, R, W, eng="dve"):
        S.add(eng, lambda e: e.tensor_copy(out=out, in_=in_), reads=R, writes=W)

    def RECIP(out, in_, R, W):
        S.add("dve", lambda e: e.reciprocal(out=out, in_=in_), reads=R, writes=W)

    def RED(out, in_, R, W):
        S.add("dve", lambda e: e.tensor_reduce(out=out, in_=in_, axis=AX.X, op=ALU.add), reads=R, writes=W)

    def MSET(ap, val, W, eng="pool"):
        S.add(eng, lambda e: e.memset(ap, val), writes=W)

    def LD(out, in_, buf, R=None, nonc=False):
        if nonc:
            S.add("sp", lambda e: e.dma_start(out=out, in_=in_, allow_slow_non_contiguous=True), reads=R, writes=[buf], dma_buf=buf)
        else:
            S.add("sp", lambda e: e.dma_start(out=out, in_=in_), reads=R, writes=[buf], dma_buf=buf)

    sto_n = [0]
    sto_dummy = Buf(None, "sto_dummy")

    def STO(out, in_, buf, W=None, extra_reads=None):
        S.add("sp", lambda e: e.dma_start(out=out, in_=in_), reads=[buf] + (extra_reads or []), writes=W, dma_buf=buf)

    dbg_out = {}
    mod_stacks = []
    cur_es = [es]
    all_bufs = []

    def DBG(name, buf, ap, shape):
        if name in debug and name not in dbg_out:
            o = dout("dbg_" + name, shape)
            dbg_out[name] = o
            if ap.dtype != F32:
                tmp = Buf(cur_es[0].enter_context(nc.sbuf_tensor("dbgt_" + name, list(shape), F32)), "dbgt_" + name)
                all_bufs.append(tmp)
                CP_(tmp[:], ap, [buf], [tmp])
                STO(o, tmp[:], tmp)
            else:
                STO(o, ap, buf)

    C = {}
    for k, v in consts_np.items():
        if k.startswith("cmf_"):
            continue
        C[k] = sb("k_" + k, v.shape)
        LD(C[k][:], I["c_" + k], C[k])
    identf = C["ident"]
    onesf = C["ones"]
    identb = sb("identb", [128, 128], BF16)
    CP_(identb[:], identf[:], [identf], [identb])
    onesb = sb("onesb", [128, 128], BF16)
    CP_(onesb[:], onesf[:], [onesf], [onesb])
    cfg_pending = []
    for cfg in (CP, CS):
        n = cfg.name
        cfg.tri, cfg.blk, cfg.nmI, cfg.nmS, cfg.nmSr, cfg.cmT = (C[k + n] for k in ("tri_", "blk_", "nmI_", "nmS_", "nmSr_", "cmT_"))
        cfg.cmf = sb("cmfb_" + n, [128, cfg.nch, cfg.T], BF16)
        cfg.ncmf = sb("ncmfb_" + n, [128, cfg.nch, cfg.T], BF16)
        cfg_pending.append(cfg)
        cfg.cmTb = sb("cmTb_" + n, [cfg.T, cfg.nch], BF16)
        CP_(cfg.cmTb[:], cfg.cmT[:], [cfg.cmT], [cfg.cmTb])

    SW = 128
    stage = [sb("stage%d" % i, [128, 8, SW]) for i in range(2)]
    stage_i = [0]
    WO = sb("wo", [128, 8, D], BF16)
    DIAG = sb("diag", [128, 12, 4, 128], BF16)
    cwT = sb("cwT", [128, 12, 4])
    cbrow = sb("cbrow", [1, 1536], BF16)
    nwT = sb("nwT", [128, 8])
    scT = sb("scT", [128, 8, 17])
    shT = sb("shT", [128, 8, 17])
    gate_bc = {"p": sb("gate_p", [128, D]), "s": sb("gate_s", [64, D])}
    vec16 = sb("vec16", [128, 4, 16])
    vec8 = sb("vec8", [128, 2, 8])
    scTb = sb("scTb", [128, 8, 17], BF16)
    badaT = sb("badaT", [128, 24])
    nw_in = sb("nw_in", [128, 8])
    xt_slots = [sb("xt%d" % i, [128, D]) for i in range(2)]
    xa_slots = [sb("xa%d" % i, [128, D]) for i in range(2)]
    stage_all = stage + xt_slots + xa_slots
    junk = sb("junk", [128, D], BF16)
    xn = sb("xn", [128, D], BF16)
    hT = sb("hT", [128, 8, 128], BF16)
    sm1 = sb("sm1", [128, 8])
    raw = sb("raw", [128, 12, 132], BF16)
    tailf = sb("tailf", [128, 12, 48])
    tailo = sb("tailo", [48, 1536])
    cst_in = tailo
    fT = sb("fT", [128, 12, 128], BF16)
    big = [sb("big%d" % i, [128, D]) for i in range(4)]
    bigb = [sb("bigb%d" % i, [128, D], BF16) for i in range(3)]
    xo = big[3]
    call_sb, sil_c, bgate, gate17, cbrow_f = big[0], bigb[0], big[1], big[2], tailo
    hTf_t = big[0]
    for cfg in cfg_pending:
        n_ = cfg.nch * cfg.T
        tmpv = big[0][:, 0:n_].rearrange("p (c t) -> p c t", t=cfg.T)
        LD(tmpv, I["c_cmf_" + cfg.name], big[0])
        CP_(cfg.cmf[:], tmpv, [big[0]], [cfg.cmf])
        TS(cfg.ncmf[:], tmpv, -1.0, None, ALU.mult, ALU.bypass, [big[0]], [cfg.ncmf])

    def stg():
        i = stage_i[0] % len(stage_all)
        stage_i[0] += 1
        b_ = stage_all[i]
        v = b_[:, :, :] if b_ in stage else b_[:, :].rearrange("p (k n) -> p k n", k=8)
        return b_, v, ("pool", "act", "dve")[i % 3]

    def cast(eng, out, in_, R, W, scale=None):
        if eng == "act":
            if scale is None:
                ACT(out, in_, AF.Copy, R, W)
            else:
                ACT(out, in_, AF.Copy, R, W, scale=scale)
        elif scale is None:
            CP_(out, in_, R, W, eng=eng)
        else:
            TS(out, in_, scale, 1.0, ALU.mult, ALU.mult, R, W, eng=eng)

    def load_cols(dst, dcol, src, ncols):
        c0 = 0
        while c0 < ncols:
            n = min(SW, ncols - c0)
            sbuf_, st, eng = stg()
            LD(st[:, :, 0:n], src[:, c0:c0 + n].rearrange("(k p) n -> p k n", p=128), sbuf_)
            cast(eng, dst[:, :, dcol + c0:dcol + c0 + n], st[:, :, 0:n], [sbuf_], [dst])
            c0 += n

    def load_wo(src_rows, nk, scale_ap_fn):
        for half in range(D // SW):
            sbuf_, st, eng = stg()
            LD(st[:, 0:nk, :], src_rows[:, half * SW:(half + 1) * SW].rearrange("(k p) n -> p k n", p=128), sbuf_)
            for k in range(nk):
                cast(eng, WO[:, k, half * SW:(half + 1) * SW], st[:, k, :], [sbuf_, nwT], [WO], scale=scale_ap_fn(k))

    def build_diag(convw, c0, nchunks):
        for k_ in range(4):
            LD(cwT[:, 0:nchunks, k_], convw[k_, c0 * 128:(c0 + nchunks) * 128].rearrange("(c p) -> p c", p=128), cwT, nonc=True)
        for c in range(nchunks):
            TT(DIAG[:, c, :, :], bm(identf[:], 4), bl(cwT[:, c, :], 128), ALU.mult, [identf, cwT], [DIAG], eng="pool")

    def do_mod(l):
        LD(call_sb[:17, :], I["call"], call_sb)
        ACT(sil_c[:17, :], call_sb[:17, :], AF.Silu, [call_sb], [sil_c])
        b = pget()
        for c in range(8):
            TR(pb(b, [8, 32])[:, c, 0:17], sil_c[:17, c * 128:(c + 1) * 128], identb[:17, :17], [sil_c, identb], [b])
        CP_(scTb[:], pb(b, [8, 32])[:, :, 0:17], [b], [scTb])
        LD(badaT[:], I["b_ada"][l].rearrange("(c p) -> p c", p=128), badaT, nonc=True)
        LD(nw_in[:], I["norm_w"][l].rearrange("(c p) -> p c", p=128), nw_in, nonc=True)
        LD(bgate[0:1, :], I["b_ada"][l:l + 1, 2048:3072], bgate)
        nblk = 3 * D // SW
        per = D // SW
        modes = contextlib.ExitStack()
        wb = Buf(modes.enter_context(nc.sbuf_tensor("modwb%d" % l, [128, 8, SW], BF16)), "modwb")
        all_bufs.append(wb)
        for blk in range(nblk):
            sbuf_, st, eng = stg()
            LD(st, I["w_ada"][l][:, blk * SW:(blk + 1) * SW].rearrange("(k p) n -> p k n", p=128), sbuf_)
            cast(eng, wb[:, :, 0:SW], st, [sbuf_], [wb])
            if blk < 2 * per:
                b = pget()
                nj = SW // 128
                for j in range(nj):
                    for k in range(8):
                        MM(pf(b, [nj, 17])[:, j, :], wb[:, k, j * 128:(j + 1) * 128], scTb[:, k, :], [wb, scTb], [b],
                           start=(k == 0), stop=(k == 7))
                dst = shT if blk < per else scT
                cc = (blk % per) * nj
                TT(dst[:, cc:cc + nj, :], pf(b, [nj, 17]), bl(badaT[:, blk * nj:blk * nj + nj], 17), ALU.add, [b, badaT], [dst])
            else:
                g = pget()
                gc0 = (blk - 2 * per) * SW
                for k in range(8):
                    MM(g[:17, 0:SW], scTb[:, k, :], wb[:, k, 0:SW], [wb, scTb], [g], start=(k == 0), stop=False)
                MM(g[:17, 0:SW], onesf[0:1, 0:17], bgate[0:1, gc0:gc0 + SW], [onesf, bgate], [g], start=False, stop=True)
                CP_(gate17[:17, gc0:gc0 + SW], g[:17, 0:SW], [g], [gate17])
        TS(scT[:], scT[:], 1.0, None, ALU.add, ALU.bypass, [scT], [scT])
        TT(scT[:], scT[:], bl(nw_in[:], 17), ALU.mult, [scT, nw_in], [scT])
        for n, E, T in (("p", C["E_p"], 128), ("s", C["E_s"], 64)):
            for half in range(2):
                b = pget()
                MM(b[:T, :], E[:, :T], gate17[:17, half * 512:(half + 1) * 512], [E, gate17], [b])
                CP_(gate_bc[n][:T, half * 512:(half + 1) * 512], b[:T, :], [b], [gate_bc[n]])
        mod_stacks.append(modes)

    def rows(ti):
        return (ti * 128, 128) if ti < NT else (SEQ, 64)

    def src_ap(layer_in, ti):
        r0, T = rows(ti)
        if layer_in is None:
            return (I["xp"][r0:r0 + T, :] if ti < NT else I["xs"][:, :]), None
        return layer_in[r0:r0 + T, :], (layer_in, ti)

    def prologue(cfg, xt):
        T = cfg.T
        ACT(junk[:T, :], xt[:T, :], AF.Square, [xt], [junk, sm1], accum_out=sm1[:T, 0:1])
        TS(sm1[:T, 0:1], sm1[:T, 0:1], 1.0 / D, EPS, ALU.mult, ALU.add, [sm1], [sm1])
        ACT(sm1[:T, 0:1], sm1[:T, 0:1], AF.Ln, [sm1], [sm1])
        ACT(sm1[:T, 0:1], sm1[:T, 0:1], AF.Exp, [sm1], [sm1], scale=-0.5)
        ACT(xn[:T, :], xt[:T, :], AF.Copy, [xt, sm1], [xn], scale=sm1[:T, 0:1])
        b = pget()
        for c in range(8):
            TR(pb(b, [8, T])[:, c, :], xn[:T, c * 128:(c + 1) * 128], identb[:T, :T], [xn, identb], [b])
        if cfg.nseq == 1:
            sc, sh = bl(scT[:, :, 0], T), bl(shT[:, :, 0], T)
            o1, o2, i1 = hTf_t[:, 0:8 * T].rearrange("p (c t) -> p c t", t=T), hT[:, :, :T], pb(b, [8, T])
        else:
            sc, sh = bl(scT[:, :, 1:17], SL), bl(shT[:, :, 1:17], SL)
            o1 = hTf_t[:, 0:8 * T].rearrange("p (c b l) -> p c b l", c=8, l=SL)
            o2 = hT[:, :, :T].rearrange("p c (b l) -> p c b l", l=SL)
            i1 = pb(b, [8, NSB, SL])
        TT(o1, i1, sc, ALU.mult, [b, scT], [hTf_t])
        TT(o2, o1, sh, ALU.add, [hTf_t, shT], [hT])

    def inproj_conv(cfg, l, wcol0, nchunks, bias_row, st_conv_in, conv_out, conv_out_s, ch0, is_last, first):
        T, L, nseq = cfg.T, cfg.L, cfg.nseq
        rawv = raw[:, :, 0:nseq * (L + 3)].rearrange("p c (b l) -> p c b l", l=L + 3)
        if cfg.nseq == 1:
            if first:
                MSET(raw[:, :, 0:3], 0.0, [raw])
            else:
                CP_(raw[:, 0:nchunks, 0:3], raw[:, 0:nchunks, L:L + 3], [raw], [raw])
        else:
            LD(cst_in[:, 0:nchunks * 128], st_conv_in[:, :, ch0 * 128:(ch0 + nchunks) * 128].rearrange("b k c -> (b k) c"), cst_in)
            for g0 in range(0, nchunks, 8):
                b = pget()
                ng = min(8, nchunks - g0)
                for c in range(ng):
                    TR(pf(b, [8, 48])[:, c, :], cst_in[:, (g0 + c) * 128:(g0 + c + 1) * 128], identf[:48, :48], [cst_in, identf], [b])
                CP_(rawv[:, g0:g0 + ng, :, 0:3], pf(b, [8, 48])[:, 0:ng, :].rearrange("p c (b k) -> p c b k", k=3), [b], [raw])
        if limit.get("stopC", 99) <= 1:
            return
        for g0 in range(0, nchunks, 4):
            b = pget()
            for c in range(4):
                for k in range(8):
                    MM(pf(b, [4, T])[:, c, :], PA["WB"][:, k, wcol0 + (g0 + c) * 128: wcol0 + (g0 + c + 1) * 128], hT[:, k, :T],
                       [PA["WB"], hT], [b], start=(k == 0), stop=(k == 7))
            if nseq == 1:
                ACT(raw[:, g0:g0 + 4, 3:3 + L], pf(b, [4, T]), AF.Copy, [b], [raw])
            else:
                for c in range(4):
                    ACT(rawv[:, g0 + c, :, 3:3 + L], pf(b, [4, T])[:, c, :].rearrange("p (b l) -> p b l", l=L), AF.Copy, [b], [raw])
            if is_last and limit.get("tail", True):
                for c in range(4):
                    ACT(tailf[:, g0 + c, 0:nseq * 3].rearrange("p (b k) -> p b k", k=3),
                        pf(b, [4, T])[:, c, :].rearrange("p (b l) -> p b l", l=L)[:, :, L - 3:L], AF.Copy, [b], [tailf])
        if limit.get("stopC", 99) <= 2:
            return
        for g0 in range(0, nchunks, 4):
            b = pget()
            for c in range(4):
                o = pf(b, [4, T])[:, c, :]
                if nseq > 1:
                    o = o.rearrange("p (b l) -> p b l", l=L)
                for k in range(4):
                    MM(o, DIAG[:, g0 + c, k, :], rawv[:, g0 + c, :, k:k + L] if nseq > 1 else raw[:, g0 + c, k:k + L],
                       [DIAG, raw], [b], start=(k == 0), stop=(k == 3 and bias_row is None))
                if bias_row is not None:
                    MM(pf(b, [4, T])[:, c, :], bias_row[0:1, (g0 + c) * 128:(g0 + c + 1) * 128], onesb[0:1, :T], [bias_row, onesb], [b],
                       start=False, stop=True)
            ACT(fT[:, g0:g0 + 4, :T], pf(b, [4, T]), AF.Silu, [b], [fT])
        if limit.get("stopC", 99) <= 3:
            return
        if is_last:
            n3 = nseq * 3
            for g0 in range(0, nchunks, 4):
                b = pget()
                for c in range(4):
                    TR(pf(b, [4, 128], p=n3)[:, c, :], tailf[:, g0 + c, 0:n3], identf[:, :], [tailf, identf], [b])
                CP_(tailo[:n3, g0 * 128:(g0 + 4) * 128], b[:n3, :], [b], [tailo])
            dst = conv_out[l][:, ch0 * 128:(ch0 + nchunks) * 128] if nseq == 1 else conv_out_s[l][:, ch0 * 128:(ch0 + nchunks) * 128]
            STO(dst, tailo[:n3, 0:nchunks * 128], tailo)

    def cum_and_tot(cfg, a_ap, a_buf, nh, out_c, out_dbc, pw):
        T, nch = cfg.T, cfg.nch
        b = pget()
        MM(b[:T, 0:nh], cfg.tri[:, :], a_ap, [cfg.tri, a_buf], [b])
        MM(b[:T, nh:2 * nh], cfg.blk[:, :], a_ap, [cfg.blk, a_buf], [b])
        ACT(out_c[:T, 0:2 * nh], b[:T, 0:2 * nh], AF.Copy, [b], [out_c])
        b2 = pget()
        MM(b2[:nch, 0:nh], cfg.cmT[:, :], a_ap, [cfg.cmT, a_buf], [b2])
        CP_(tot[:nch, 0:nh], b2[:nch, 0:nh], [b2], [tot])
        TT(totx[:nch, 0:nch * nh].rearrange("p (c h) -> p c h", h=nh), bm(tot[:nch, 0:nh], nch), bl(identf[:nch, :nch], nh), ALU.mult,
           [tot, identf], [totx])
        b3 = pget()
        MM(b3[:pw, 0:nch * nh], onesf[:nch, :pw], totx[:nch, 0:nch * nh], [onesf, totx], [b3])
        ACT(out_dbc[:pw, 0:nch * nh], b3[:pw, 0:nch * nh], AF.Exp, [b3], [out_dbc])

    tot = sb("tot", [16, 16])
    totx = sb("totx", [16, 256])
    sm = {k: sb("sm_" + k, [128, 64]) for k in ("a", "b", "c", "d", "e")}
    dbc = sb("dbc", [128, 256])
    acT = sb("acT", [16, 128])

    PA = {}

    def alloc_A(pes, tag):
        def sbp(name, shape, dt=F32):
            return Buf(pes.enter_context(nc.sbuf_tensor(name + tag, list(shape), dt)), name)
        PA.update(stf=sbp("stf", [128, D]), stb=sbp("stb", [128, D], BF16), hn=sbp("hn", [128, 8, 128]),
                  stf_s=sbp("stf_s", [128, D]), stb_s=sbp("stb_s", [128, D], BF16), Gs=sbp("Gs", [128, 2, 128]),
                  earg=sbp("earg", [128, 4, 128]), Ee=sbp("Ee", [128, 4, 128]), WT=sbp("WT", [128, 16, 128], BF16),
                  Btok=sbp("Btok", [128, 256], BF16), Btm=sbp("Btm", [128, 2, 256], BF16), CTm=sbp("CTm", [128, 2, 2, 128], BF16),
                  WB=sbp("WB", [128, 8, 2576], BF16))
    mslot = [0]

    def ssd_state_update(cfg, c, st_in, stb_in, st_out, yo, xdec, first_c, last_c):
        T, nch = cfg.T, cfg.nch
        Btm, CTm, Btok = PA["Btm"], PA["CTm"], PA["Btok"]
        sl = mslot[0] % 2
        mslot[0] += 1
        TS(Btm[:T, sl, :], Btok[:T, :], cfg.cmT[:, c:c + 1], 1.0, ALU.mult, ALU.mult, [Btok, cfg.cmT], [(Btm, sl)], eng="pool")
        TT(CTm[:, sl, :, :T], fT[:, 10:12, :T], bm(cfg.cmf[:, c, :], 2), ALU.mult, [fT, cfg.cmf], [(CTm, sl)], eng="pool")
        for g in range(2):
            MM(yo[g][:T, :], CTm[:, sl, g, :T], stb_in[:, g * 512:(g + 1) * 512], [(CTm, sl), stb_in], [yo[g]], start=first_c, stop=last_c)
        sp_ = [pget(), pget()]
        for g in range(2):
            MM(sp_[g][:, :], Btm[:T, sl, g * 128:(g + 1) * 128], xdec[:T, g * 512:(g + 1) * 512], [(Btm, sl), xdec], [sp_[g]])
        TT(big[3][:, :].rearrange("p (h q) -> p h q", q=64), st_in[:, :].rearrange("p (h q) -> p h q", q=64),
           bl(dbc[:, c * 16:(c + 1) * 16], 64), ALU.mult, [st_in, dbc], [big[3]])
        for g in range(2):
            TT(st_out[:, g * 512:(g + 1) * 512], big[3][:, g * 512:(g + 1) * 512], sp_[g][:, :], ALU.add, [big[3], sp_[g]], [st_out])

    def phaseA_tile(cfg, l, ti, xt, first, is_last):
        T, L, nseq, nch = cfg.T, cfg.L, cfg.nseq, cfg.nch
        stf, stb, hn, stf_s, stb_s, Gs, earg, Ee, WT, Btok = (PA[k] for k in ("stf", "stb", "hn", "stf_s", "stb_s", "Gs", "earg", "Ee", "WT", "Btok"))
        prologue(cfg, xt)
        if limit.get("stopA", 99) <= 0:
            return
        inproj_conv(cfg, l, OFF_XBC, 12, cbrow, I["st_sc"][l], O["ncs_p"], O["ncs_s"], 0, is_last, first)
        if ti == 0:
            DBG("fT_A", fT, fT[:, :, :T], [128, 12, T])
        if limit.get("stopA", 99) <= 1:
            return
        b = pget()
        for k in range(8):
            MM(b[:T, 0:16], hT[:, k, :T], PA["WB"][:, k, OFF_DT:OFF_DT + 16], [hT, PA["WB"]], [b], start=(k == 0), stop=(k == 7))
        A_, B_, C_, D_, E_ = (sm[k] for k in "abcde")
        TT(A_[:T, 0:16], b[:T, 0:16], vec16[:T, 0, :], ALU.add, [b, vec16], [A_])
        ACT(A_[:T, 0:16], A_[:T, 0:16], AF.Exp, [A_], [A_])
        ACT(A_[:T, 16:32], A_[:T, 0:16], AF.Ln, [A_], [A_], bias=1.0)
        TT(A_[:T, 32:48], A_[:T, 16:32], vec16[:T, 1, :], ALU.mult, [A_, vec16], [A_])
        cum_and_tot(cfg, A_[:T, 32:48], A_, 16, B_, dbc, 128)
        TT(C_[:T, 0:16], B_[:T, 16:32], B_[:T, 0:16], ALU.subtract, [B_], [C_])
        ACT(C_[:T, 0:16], C_[:T, 0:16], AF.Exp, [C_], [C_])
        ACT(C_[:T, 16:32], B_[:T, 0:16], AF.Exp, [B_], [C_])
        b = pget()
        TR(b[:16, 0:T], B_[:T, 0:16], identf[:T, :T], [B_, identf], [b])
        ACT(acT[:, :T], b[:16, 0:T], AF.Copy, [b], [acT])
        if ti == 0:
            DBG("acum", B_, B_[:T, 0:32], [T, 32])
        if limit.get("stopA", 99) <= 2:
            return
        bx, bb = pget(), pget()
        for c in range(8):
            TR(pb(bx, [8, 128], p=T)[:, c, :], fT[:, c, :T], identb[:, :], [fT, identb], [bx])
        for c in range(2):
            TR(pb(bb, [2, 128], p=T)[:, c, :], fT[:, 8 + c, :T], identb[:, :], [fT, identb], [bb])
        xc, xdec, xsD = bigb[0], bigb[1], big[0]
        TT(xc[:T, :].rearrange("p (h q) -> p h q", q=64), pb(bx, [16, 64], p=T), bl(A_[:T, 16:32], 64), ALU.mult, [bx, A_], [xc])
        TT(xsD[:T, :].rearrange("p (h q) -> p h q", q=64), pb(bx, [16, 64], p=T), bl(vec16[:T, 2, :], 64), ALU.mult, [bx, vec16], [xsD])
        TT(xdec[:T, :].rearrange("p (h q) -> p h q", q=64), xc[:T, :].rearrange("p (h q) -> p h q", q=64), bl(C_[:T, 0:16], 64), ALU.mult,
           [xc, C_], [xdec], eng="pool")
        ACT(Btok[:T, :], pb(bb, [256], p=T), AF.Copy, [bb], [Btok])
        b = pget()
        for g in range(2):
            MM(pf(b, [2, T], p=T)[:, g, :], fT[:, 8 + g, :T], fT[:, 10 + g, :T], [fT], [b])
        ACT(Gs[:T, :, :T], pf(b, [2, T], p=T), AF.Copy, [b], [Gs])
        for q in range(4):
            b = pget()
            for j in range(4):
                h = q * 4 + j
                MM(pf(b, [4, T], p=T)[:, j, :], identf[0:16, h:h + 1].to_broadcast([16, T]), acT[:, :T], [identf, acT], [b])
            for j in range(4):
                h = q * 4 + j
                STT(earg[:T, j, :T], pf(b, [4, T], p=T)[:, j, :], B_[:T, h:h + 1], cfg.nmI[:, :], ALU.subtract, ALU.add,
                    [b, B_, cfg.nmI], [earg])
            ACT(Ee[:T, :, :T], earg[:T, :, :T], AF.Exp, [earg], [Ee])
            TT(WT[:T, q * 4:q * 4 + 4, :T], Ee[:T, :, :T], bm(Gs[:T, q // 2, :T], 4), ALU.mult, [Ee, Gs], [WT])
        if limit.get("stopA", 99) <= 3:
            return
        yp = [pget(), pget()]
        for h in range(16):
            MM(yp[h // 8][:T, (h % 8) * 64:(h % 8 + 1) * 64], WT[:T, h, :T], xc[:T, h * 64:(h + 1) * 64], [WT, xc], [yp[h // 8]])
        ppin(yp[0]); ppin(yp[1])
        yo = [pget(), pget()]
        ppin(yo[0]); ppin(yo[1])
        if cfg.chain:
            if first:
                MSET(stf[:, :], 0.0, [stf])
                MSET(stb[:, :], 0.0, [stb])
            for c in range(nch):
                ssd_state_update(cfg, c, stf, stb, stf, yo, xdec, c == 0, c == nch - 1)
                ACT(stb[:, :], stf[:, :], AF.Copy, [stf], [stb])
            if is_last:
                for half in range(2):
                    b = pget()
                    for k in range(4):
                        kk = half * 4 + k
                        TR(pf(b, [4, 128])[:, k, :], stf[:, :].rearrange("n (j k) -> n k j", k=8)[:, kk, :], identf[:, :], [stf, identf], [b])
                    CP_(hn[:, half * 4:half * 4 + 4, :], pf(b, [4, 128]), [b], [hn])
                STO(O["nssm_p"][l].rearrange("(p k) n -> p k n", k=8), hn[:], hn)
        else:
            for c in range(limit.get("nchs", nch)):
                LD(hn[:], I["st_ssm"][l, c].rearrange("h q n -> (h q) n").rearrange("(p k) n -> p k n", k=8), hn)
                if limit.get("sub", 9) <= 0:
                    continue
                for half in range(2):
                    b = pget()
                    for k in range(4):
                        kk = half * 4 + k
                        TR(pf(b, [4, 128])[:, k, :], hn[:, kk, :], identf[:, :], [hn, identf], [b])
                    if limit.get("sub", 9) <= 1:
                        continue
                    ACT(stf_s[:, :].rearrange("n (j k) -> n k j", k=8)[:, half * 4:half * 4 + 4, :], pf(b, [4, 128]), AF.Copy, [b], [stf_s])
                    if limit.get("sub", 9) <= 2:
                        continue
                    CP_(stb_s[:, :].rearrange("n (j k) -> n k j", k=8)[:, half * 4:half * 4 + 4, :], pf(b, [4, 128]), [b], [stb_s])
                if limit.get("noupd"):
                    continue
                ssd_state_update(cfg, c, stf_s, stb_s, stf_s, yo, xdec, c == 0, c == limit.get("nchs", nch) - 1)
                if limit.get("noback"):
                    continue
                for half in range(2):
                    b = pget()
                    for k in range(4):
                        kk = half * 4 + k
                        TR(pf(b, [4, 128])[:, k, :], stf_s[:, :].rearrange("n (j k) -> n k j", k=8)[:, kk, :], identf[:, :], [stf_s, identf], [b])
                    CP_(hn[:, half * 4:half * 4 + 4, :], pf(b, [4, 128]), [b], [hn])
                STO(O["nssm_s"][l, c].rearrange("(p k) n -> p k n", k=8), hn[:], hn)
        if limit.get("stopA", 99) <= 4:
            punpin(yp[0]); punpin(yp[1]); punpin(yo[0]); punpin(yo[1])
            return
        t1, t2 = big[1], big[2]
        for g in range(2):
            TT(t1[:T, g * 512:(g + 1) * 512].rearrange("p (h q) -> p h q", q=64), yo[g][:T, :].rearrange("p (h q) -> p h q", q=64),
               bl(C_[:T, 16 + g * 8:16 + g * 8 + 8], 64), ALU.mult, [yo[g], C_], [t1])
            TT(t2[:T, g * 512:(g + 1) * 512], yp[g][:T, :], t1[:T, g * 512:(g + 1) * 512], ALU.add, [yp[g], t1], [t2])
        for x_ in yp + yo:
            punpin(x_)
        TT(t2[:T, :], t2[:T, :], xsD[:T, :], ALU.add, [t2, xsD], [t2], eng="pool")
        if ti == 0:
            DBG("yssd", t2, t2[:T, :], [T, D])
        for g in range(2):
            b = pget()
            for k in range(8):
                MM(b[:T, :], hT[:, k, :T], PA["WB"][:, k, OFF_ZS + g * 512:OFF_ZS + (g + 1) * 512], [hT, PA["WB"]], [b], start=(k == 0), stop=(k == 7))
            ACT(t1[:T, g * 512:(g + 1) * 512], b[:T, :], AF.Silu, [b], [t1])
        TT(t2[:T, :], t2[:T, :], t1[:T, :], ALU.mult, [t2, t1], [t2])
        for g in range(2):
            ACT(junk[:T, g * 512:(g + 1) * 512], t2[:T, g * 512:(g + 1) * 512], AF.Square, [t2], [junk, sm1], accum_out=sm1[:T, 2 + g:3 + g])
        TS(sm1[:T, 2:4], sm1[:T, 2:4], 1.0 / 512, EPS, ALU.mult, ALU.add, [sm1], [sm1])
        ACT(sm1[:T, 2:4], sm1[:T, 2:4], AF.Ln, [sm1], [sm1])
        ACT(sm1[:T, 2:4], sm1[:T, 2:4], AF.Exp, [sm1], [sm1], scale=-0.5)
        ynb = bigb[2]
        for g in range(2):
            ACT(ynb[:T, g * 512:(g + 1) * 512], t2[:T, g * 512:(g + 1) * 512], AF.Copy, [t2, sm1], [ynb], scale=sm1[:T, 2 + g:3 + g])
        out_proj(cfg, ynb, 8)

    def out_proj(cfg, ynb, nk):
        T = cfg.T
        b = pget()
        for c in range(nk):
            TR(pb(b, [8, T])[:, c, :], ynb[:T, c * 128:(c + 1) * 128], identb[:T, :T], [ynb, identb], [b])
        CP_(hT[:, 0:nk, :T], pb(b, [8, T])[:, 0:nk, :], [b], [hT])
        for half in range(2):
            b = pget()
            for k in range(nk):
                MM(b[:T, :], hT[:, k, :T], WO[:, k, half * 512:(half + 1) * 512], [hT, WO], [b], start=(k == 0), stop=(k == nk - 1))
            TT(big[3][:T, half * 512:(half + 1) * 512], b[:T, :], gate_bc[cfg.name][:T, half * 512:(half + 1) * 512], ALU.mult,
               [b, gate_bc[cfg.name]], [big[3]])

    PB = {}

    def alloc_B(pes, tag):
        def sbp(name, shape, dt=F32):
            return Buf(pes.enter_context(nc.sbuf_tensor(name + tag, list(shape), dt)), name)
        PB.update(Sp_f=sbp("Sp_f", [128, 4, 128]), Sp_b=sbp("Sp_b", [128, 4, 128], BF16), Ss_in=sbp("Ss_in", [128, 16, 128]),
                  Ss_b=sbp("Ss_b", [128, 16, 128], BF16), kv=sbp("kv", [128, 8, 128], BF16), kbg=sbp("kbg", [128, 4, 128]),
                  vb_=sbp("vb", [128, 4, 128]), kdec=sbp("kdec", [128, 4, 128], BF16), kdm=sbp("kdm", [64, 2, 128], BF16),
                  LG=sbp("LG", [128, 12]), LGT=sbp("LGT", [12, 128]), ea=sbp("ea", [128, 4, 128]),
                  E2a=sbp("E2a", [128, 4, 128]), E2b=sbp("E2b", [128, 4, 128]),
                  X0=sbp("X0", [128, 4, 128]), X1=sbp("X1", [128, 4, 128]), Y0=sbp("Y0", [128, 4, 128]), Y1=sbp("Y1", [128, 4, 128]),
                  R0=sbp("R0", [128, 4, 128]), R1=sbp("R1", [128, 4, 128]), attnT=sbp("attnT", [128, 4, 128], BF16),
                  egb=sbp("egb", [128, 4, 128]), qsq=sbp("qsq", [128, 4, 128], BF16),
                  nwTm=sbp("nwTm", [128, 1024], BF16), qdm=sbp("qdm", [128, 1024], BF16), u_sb=sbp("u_sb", [128, 4, 128]),
                  vnew=sbp("vnew", [128, 4, 128], BF16), dbcS=sbp("dbcS", [128, 64]), WB=sbp("WB", [128, 8, 2056], BF16))
        PB["E2"] = None

    def r32(ap):
        return ap.bitcast(F32R)

    def phaseB_tile(cfg, l, ti, xt, h0, first, is_last):
        T, L, nseq, nch = cfg.T, cfg.L, cfg.nseq, cfg.nch
        (Sp_f, Sp_b, Ss_in, Ss_b, kv, kbg, vb_, kdec, kdm, LG, LGT, ea, attnT, egb, nwTm, qdm, u_sb, vnew, dbcS) = (
            PB[k] for k in ("Sp_f", "Sp_b", "Ss_in", "Ss_b", "kv", "kbg", "vb_", "kdec", "kdm", "LG", "LGT", "ea", "attnT",
                            "egb", "nwTm", "qdm", "u_sb", "vnew", "dbcS"))
        E2 = [PB["E2a"], PB["E2b"]]
        Ss_out = Ss_in
        Xs, Ys, Rs = [PB["X0"], PB["X1"]], [PB["Y0"], PB["Y1"]], [PB["R0"], PB["R1"]]
        WB = PB["WB"]
        prologue(cfg, xt)
        inproj_conv_B(cfg, l, h0, is_last, first)
        A_, B_, C_, D_, E_ = (sm[k] for k in "abcde")
        b = pget()
        for k in range(8):
            MM(b[:T, 0:8], hT[:, k, :T], WB[:, k, 2048:2056], [hT, WB], [b], start=(k == 0), stop=(k == 7))
        TT(A_[:T, 0:4], b[:T, 0:4], vec8[:T, 0, h0:h0 + 4], ALU.add, [b, vec8], [A_])
        ACT(A_[:T, 0:4], A_[:T, 0:4], AF.Exp, [A_], [A_])
        ACT(A_[:T, 0:4], A_[:T, 0:4], AF.Ln, [A_], [A_], bias=1.0)
        TT(A_[:T, 4:8], A_[:T, 0:4], vec8[:T, 1, h0:h0 + 4], ALU.mult, [A_, vec8], [A_])
        ACT(A_[:T, 8:12], b[:T, 4:8], AF.Exp, [b], [A_], scale=-1.0)
        ACT(A_[:T, 8:12], A_[:T, 8:12], AF.Ln, [A_], [A_], bias=1.0)
        ACT(A_[:T, 12:16], A_[:T, 8:12], AF.Exp, [A_], [A_], scale=-1.0)
        ACT(A_[:T, 16:20], A_[:T, 8:12], AF.Copy, [A_], [A_], scale=-1.0)
        cum_and_tot(cfg, A_[:T, 4:8], A_, 4, B_, dbcS, 128)
        ACT(C_[:T, 0:4], B_[:T, 0:4], AF.Exp, [B_], [C_])
        TT(C_[:T, 4:8], B_[:T, 4:8], B_[:T, 0:4], ALU.subtract, [B_], [C_])
        ACT(C_[:T, 4:8], C_[:T, 4:8], AF.Exp, [C_], [C_])
        b = pget()
        for c in range(8):
            TR(pb(b, [8, 128], p=T)[:, c, :], fT[:, 4 + c, :T], identb[:, :], [fT, identb], [b])
        ACT(kv[:T, :, :], pb(b, [8, 128], p=T), AF.Copy, [b], [kv])
        ksq = big[0]
        TT(ksq[:T, 0:512].rearrange("p (h d) -> p h d", d=128), kv[:T, 0:4, :], kv[:T, 0:4, :], ALU.mult, [kv], [ksq])
        RED(D_[:T, 0:4], ksq[:T, 0:512].rearrange("p (h d) -> p h d", d=128), [ksq], [D_])
        TS(D_[:T, 0:4], D_[:T, 0:4], EPS, None, ALU.add, ALU.bypass, [D_], [D_])
        ACT(D_[:T, 4:8], D_[:T, 0:4], AF.Ln, [D_], [D_])
        ACT(D_[:T, 8:12], D_[:T, 4:8], AF.Exp, [D_], [D_], scale=-0.5)
        qsq = PB["qsq"]
        ACT(qsq[:, :, :T], fT[:, 0:4, :T], AF.Square, [fT], [qsq])
        b = pget()
        for j in range(4):
            MM(b[:T, j:j + 1], qsq[:, j, :T], onesb[:, 0:1], [qsq, onesb], [b])
        TS(D_[:T, 12:16], b[:T, 0:4], EPS, None, ALU.add, ALU.bypass, [b], [D_])
        ACT(D_[:T, 12:16], D_[:T, 12:16], AF.Ln, [D_], [D_])
        ACT(D_[:T, 16:20], D_[:T, 12:16], AF.Exp, [D_], [D_], scale=-0.5)
        TS(D_[:T, 16:20], D_[:T, 16:20], float(128 ** -0.5), None, ALU.mult, ALU.bypass, [D_], [D_])
        TT(E_[:T, 0:4], D_[:T, 8:12], A_[:T, 12:16], ALU.mult, [D_, A_], [E_])
        TT(E_[:T, 4:8], E_[:T, 0:4], C_[:T, 0:4], ALU.mult, [E_, C_], [E_])
        TT(E_[:T, 8:12], D_[:T, 8:12], C_[:T, 4:8], ALU.mult, [D_, C_], [E_])
        TS(E_[:T, 12:16], E_[:T, 0:4], -1.0, None, ALU.mult, ALU.bypass, [E_], [E_])
        TS(E_[:T, 16:20], D_[:T, 8:12], -1.0, None, ALU.mult, ALU.bypass, [D_], [E_])
        CP_(LG[:T, 0:4], B_[:T, 0:4], [B_], [LG])
        STT(LG[:T, 4:8], D_[:T, 4:8], -0.5, B_[:T, 0:4], ALU.mult, ALU.subtract, [D_, B_], [LG])
        STT(LG[:T, 8:12], D_[:T, 4:8], -0.5, B_[:T, 0:4], ALU.mult, ALU.add, [D_, B_], [LG])
        TT(LG[:T, 8:12], LG[:T, 8:12], A_[:T, 16:20], ALU.add, [LG, A_], [LG])
        b = pget()
        TR(b[:12, 0:T], LG[:T, 0:12], identf[:T, :T], [LG, identf], [b])
        ACT(LGT[:, :T], b[:12, 0:T], AF.Copy, [b], [LGT])
        TT(r32(kbg[:T, :, :]), kv[:T, 0:4, :], bl(E_[:T, 4:8], 128), ALU.mult, [kv, E_], [kbg])
        TT(kdec[:T, :, :], kv[:T, 0:4, :], bl(E_[:T, 8:12], 128), ALU.mult, [kv, E_], [kdec])
        TT(r32(vb_[:T, :, :]), kv[:T, 4:8, :], bl(A_[:T, 12:16], 128), ALU.mult, [kv, A_], [vb_])
        bG, bA = pget(), pget()
        for j in range(4):
            MM(pf(bG, [4, T], p=T)[:, j, :], fT[:, 4 + j, :T], fT[:, 4 + j, :T], [fT], [bG])
        for j in range(4):
            MM(pf(bA, [4, T], p=T)[:, j, :], fT[:, 4 + j, :T], fT[:, j, :T], [fT], [bA])
        bcs = [pget(), pget(), pget()]
        for r in range(3):
            for j in range(4):
                MM(pf(bcs[r], [4, T], p=T)[:, j, :], identf[0:12, r * 4 + j:r * 4 + j + 1].to_broadcast([12, T]), LGT[:, :T],
                   [identf, LGT], [bcs[r]])
        be = pget()
        for j in range(4):
            MM(pf(be, [4, T])[:, j, :], identf[0:12, j:j + 1].to_broadcast([12, 128]), LGT[:, :T], [identf, LGT], [be])
        X, Y, R = Xs[0], Ys[0], Rs[0]
        kinds = ((1, ALU.add, cfg.nmSr, 0), (2, ALU.subtract, cfg.nmS, 1), (0, ALU.subtract, cfg.nmI, 2))
        for r, op0, msk, ki in kinds:
            for j in range(4):
                STT(ea[:T, j, :T], pf(bcs[r], [4, T], p=T)[:, j, :], B_[:T, j:j + 1], msk[:, :], op0, ALU.add, [bcs[r], B_, msk], [ea])
            Ek = E2[ki % 2]
            ACT(Ek[:T, :, :T], ea[:T, :, :T], AF.Exp, [ea], [Ek])
            for j in range(4):
                if ki == 0:
                    STT(r32(X[:T, j, :T]), pf(bG, [4, T], p=T)[:, j, :], E_[:T, 12 + j:13 + j], Ek[:T, j, :T], ALU.mult, ALU.mult, [bG, E_, Ek], [X])
                elif ki == 1:
                    STT(r32(Y[:T, j, :T]), pf(bG, [4, T], p=T)[:, j, :], E_[:T, 16 + j:17 + j], Ek[:T, j, :T], ALU.mult, ALU.mult, [bG, E_, Ek], [Y])
                else:
                    STT(attnT[:T, j, :T], pf(bA, [4, T], p=T)[:, j, :], D_[:T, 8 + j:9 + j], Ek[:T, j, :T], ALU.mult, ALU.mult, [bA, D_, Ek], [attnT])
        ACT(egb[:, :, :T], pf(be, [4, T]), AF.Exp, [be], [egb])
        TT(r32(R[:T, :, :T]), Y[:T, :, :T], bm(identf[:T, :T], 4), ALU.add, [Y, identf], [R])
        for lev in range(cfg.nlev):
            X2, Y2, R2 = Xs[(lev + 1) % 2], Ys[(lev + 1) % 2], Rs[(lev + 1) % 2]
            lastlev = (lev == cfg.nlev - 1)
            bX, bY, bR = pget(), (None if lastlev else pget()), pget()
            for j in range(4):
                MM(pf(bX, [4, T], p=T)[:, j, :], r32(Y[:T, j, :T]), r32(X[:T, j, :T]), [X, Y], [bX])
            if not lastlev:
                for j in range(4):
                    MM(pf(bY, [4, T], p=T)[:, j, :], r32(X[:T, j, :T]), r32(Y[:T, j, :T]), [X, Y], [bY])
            ACT(r32(X2[:T, :, :T]), pf(bX, [4, T], p=T), AF.Copy, [bX], [X2])
            if not lastlev:
                CP_(r32(Y2[:T, :, :T]), pf(bY, [4, T], p=T), [bY], [Y2])
            for j in range(4):
                MM(pf(bR, [4, T], p=T)[:, j, :], r32(X2[:T, j, :T]), r32(R[:T, j, :T]), [X2, R], [bR])
            TT(r32(R2[:T, :, :T]), R[:T, :, :T], pf(bR, [4, T], p=T), ALU.add, [R, bR], [R2])
            X, Y, R = X2, Y2, R2
        bU, bW = pget(), pget()
        for j in range(4):
            MM(pf(bU, [4, 128], p=T)[:, j, :], r32(R[:T, j, :T]), r32(vb_[:T, j, :]), [R, vb_], [bU])
        for j in range(4):
            MM(pf(bW, [4, T])[:, j, :], r32(kbg[:T, j, :]), r32(R[:T, j, :T]), [kbg, R], [bW])
        ACT(u_sb[:T, :, :], pf(bU, [4, 128], p=T), AF.Copy, [bU], [u_sb])
        TT(egb[:, :, :T], fT[:, 0:4, :T], egb[:, :, :T], ALU.mult, [fT, egb], [egb])
        vnb, ob = pget(), pget()
        ppin(vnb)
        ppin(ob)
        vn4, o4 = pf(vnb, [4, 128], p=T), pf(ob, [4, 128], p=T)
        if cfg.chain:
            if first:
                MSET(Sp_f[:], 0.0, [Sp_f])
                MSET(Sp_b[:], 0.0, [Sp_b])
            nwv = nwTm[:, 0:nch * 4 * T].rearrange("p (c j t) -> p c j t", c=nch, j=4)
            qdv = qdm[:, 0:nch * 4 * T].rearrange("p (c j t) -> p c j t", c=nch, j=4)
            for c in range(nch):
                TT(nwv[:, c], pf(bW, [4, T]), bm(cfg.ncmf[:, c, :], 4), ALU.mult, [bW, cfg.ncmf], [(nwTm, c)])
                TT(qdv[:, c], egb[:, :, :T], bm(cfg.cmf[:, c, :], 4), ALU.mult, [egb, cfg.cmf], [(qdm, c)], eng="pool")
            for c in range(nch):
                for j in range(4):
                    MM(vn4[:, j, :], nwv[:, c, j, :], Sp_b[:, j, :], [(nwTm, c), Sp_b], [vnb], start=(j == 0 and c == 0), stop=False, skip_group_check=True)
                for j in range(4):
                    MM(o4[:, j, :], qdv[:, c, j, :], Sp_b[:, j, :], [(qdm, c), Sp_b], [ob], start=(j == 0 and c == 0), stop=False, skip_group_check=True)
                TT(vnew[:T, :, :], vn4, u_sb[:T, :, :], ALU.add, [vnb, u_sb], [vnew])
                bs = pget()
                for j in range(4):
                    MM(pf(bs, [4, 128])[:, j, :], kdec[c * 64:(c + 1) * 64, j, :], vnew[c * 64:(c + 1) * 64, j, :], [kdec, vnew], [bs])
                tmpS = big[0][:, 512:1024].rearrange("p (j v) -> p j v", v=128)
                TT(tmpS, Sp_f[:, :, :], bl(dbcS[:, c * 4:(c + 1) * 4], 128), ALU.mult, [Sp_f, dbcS], [big[0]])
                TT(Sp_f[:, :, :], tmpS, pf(bs, [4, 128]), ALU.add, [big[0], bs], [Sp_f])
                ACT(Sp_b[:, :, :], Sp_f[:, :, :], AF.Copy, [Sp_f], [Sp_b])
            if is_last:
                STO(O["ngdn_p"][l, h0:h0 + 4].rearrange("h k v -> k h v"), Sp_f[:], Sp_f)
        else:
            nwv = nwTm[:, 0:nch * T].rearrange("p (c t) -> p c t", t=T)
            qdv = qdm[:, 0:nch * T].rearrange("p (c t) -> p c t", t=T)
            ppin(bW)
            for j in range(4):
                h = h0 + j
                LD(Ss_in[:], I["st_gdn"][l, :, h].rearrange("b k v -> k b v"), Ss_in)
                CP_(Ss_b[:], Ss_in[:], [Ss_in], [Ss_b], eng="pool")
                TT(nwv, bm(pf(bW, [4, T])[:, j, :], nch), cfg.ncmf[:, :, :], ALU.mult, [bW, cfg.ncmf], [nwTm])
                TT(qdv, bm(egb[:, j, :T], nch), cfg.cmf[:, :, :], ALU.mult, [egb, cfg.cmf], [qdm], eng="pool")
                for c in range(nch):
                    MM(vn4[:, j, :], nwv[:, c, :], Ss_b[:, c, :], [nwTm, Ss_b], [vnb], start=(j == 0 and c == 0), stop=False, skip_group_check=True)
                for c in range(nch):
                    MM(o4[:, j, :], qdv[:, c, :], Ss_b[:, c, :], [qdm, Ss_b], [ob], start=(j == 0 and c == 0), stop=False, skip_group_check=True)
                TT(vnew[:T, j, :], vn4[:, j, :], u_sb[:T, j, :], ALU.add, [vnb, u_sb], [vnew])
                tmpS = big[2][:, :].rearrange("p (c v) -> p c v", v=128)
                for hf in range(2):
                    TT(tmpS, Ss_in[:, hf * 8:(hf + 1) * 8, :],
                       bl(dbcS[:, 0:nch * 4].rearrange("p (c j) -> p c j", j=4)[:, hf * 8:(hf + 1) * 8, j], 128), ALU.mult, [Ss_in, dbcS], [big[2]])
                    for c4 in range(hf * 8, hf * 8 + 8, 4):
                        bs = pget()
                        for cc in range(4):
                            c = c4 + cc
                            sl = mslot[0] % 2
                            mslot[0] += 1
                            TS(kdm[:T, sl, :], kdec[:T, j, :], cfg.cmT[:, c:c + 1], 1.0, ALU.mult, ALU.mult, [kdec, cfg.cmT], [(kdm, sl)], eng="pool")
                            MM(pf(bs, [4, 128])[:, cc, :], kdm[:T, sl, :], vnew[:T, j, :], [(kdm, sl), vnew], [bs])
                        TT(Ss_out[:, c4:c4 + 4, :], tmpS[:, c4 - hf * 8:c4 - hf * 8 + 4, :], pf(bs, [4, 128]), ALU.add, [big[2], bs], [Ss_out])
                STO(O["ngdn_s"][l, :, h].rearrange("b k v -> k b v"), Ss_out[:], Ss_out)
            punpin(bW)
        for j in range(4):
            MM(o4[:, j, :], attnT[:T, j, :T], vnew[:T, j, :], [attnT, vnew], [ob], start=False, stop=True, skip_group_check=True)
        punpin(vnb)
        of = big[1]
        ACT(of[:T, 0:512], ob[:T, :], AF.Copy, [ob], [of])
        punpin(ob)
        if ti == 0 and h0 == 0:
            DBG("ogdn", of, of[:T, 0:512], [T, 512])
        TT(ksq[:T, 0:512], of[:T, 0:512], of[:T, 0:512], ALU.mult, [of], [ksq])
        RED(C_[:T, 8:12], ksq[:T, 0:512].rearrange("p (h d) -> p h d", d=128), [ksq], [C_])
        TT(C_[:T, 12:16], D_[:T, 16:20], D_[:T, 16:20], ALU.mult, [D_], [C_])
        TT(C_[:T, 8:12], C_[:T, 8:12], C_[:T, 12:16], ALU.mult, [C_], [C_])
        TS(C_[:T, 8:12], C_[:T, 8:12], 1.0 / 128, EPS, ALU.mult, ALU.add, [C_], [C_])
        ACT(C_[:T, 8:12], C_[:T, 8:12], AF.Ln, [C_], [C_])
        ACT(C_[:T, 8:12], C_[:T, 8:12], AF.Exp, [C_], [C_], scale=-0.5)
        TT(C_[:T, 8:12], C_[:T, 8:12], D_[:T, 16:20], ALU.mult, [C_, D_], [C_])
        b = pget()
        for k in range(8):
            MM(b[:T, :], hT[:, k, :T], WB[:, k, 0:512], [hT, WB], [b], start=(k == 0), stop=(k == 7))
        sz = big[2]
        ACT(sz[:T, 0:512], b[:T, :], AF.Silu, [b], [sz])
        TT(of[:T, 0:512].rearrange("p (h d) -> p h d", d=128), of[:T, 0:512].rearrange("p (h d) -> p h d", d=128), bl(C_[:T, 8:12], 128),
           ALU.mult, [of, C_], [of])
        onb = bigb[2]
        TT(onb[:T, 0:512], of[:T, 0:512], sz[:T, 0:512], ALU.mult, [of, sz], [onb])
        out_proj(cfg, onb, 4)

    def inproj_conv_B(cfg, l, h0, is_last, first):
        T, L, nseq = cfg.T, cfg.L, cfg.nseq
        rawv = raw[:, :, 0:nseq * (L + 3)].rearrange("p c (b l) -> p c b l", l=L + 3)
        if nseq == 1:
            if first:
                MSET(raw[:, :, 0:3], 0.0, [raw])
            else:
                CP_(raw[:, :, 0:3], raw[:, :, L:L + 3], [raw], [raw])
        else:
            for part in range(3):
                cs0 = part * 1024 + h0 * 128
                LD(cst_in[:, part * 512:(part + 1) * 512], I["st_gc"][l][:, :, cs0:cs0 + 512].rearrange("b k c -> (b k) c"), cst_in)
            for g0 in (0, 8):
                b = pget()
                ng = min(8, 12 - g0)
                for c in range(ng):
                    TR(pf(b, [8, 48])[:, c, :], cst_in[:, (g0 + c) * 128:(g0 + c + 1) * 128], identf[:48, :48], [cst_in, identf], [b])
                CP_(rawv[:, g0:g0 + ng, :, 0:3], pf(b, [8, 48])[:, 0:ng, :].rearrange("p c (b k) -> p c b k", k=3), [b], [raw])
        for g0 in range(0, 12, 4):
            b = pget()
            for c in range(4):
                for k in range(8):
                    MM(pf(b, [4, T])[:, c, :], PB["WB"][:, k, 512 + (g0 + c) * 128: 512 + (g0 + c + 1) * 128], hT[:, k, :T],
                       [PB["WB"], hT], [b], start=(k == 0), stop=(k == 7))
            if nseq == 1:
                ACT(raw[:, g0:g0 + 4, 3:3 + L], pf(b, [4, T]), AF.Copy, [b], [raw])
            else:
                for c in range(4):
                    ACT(rawv[:, g0 + c, :, 3:3 + L], pf(b, [4, T])[:, c, :].rearrange("p (b l) -> p b l", l=L), AF.Copy, [b], [raw])
            if is_last and limit.get("tail", True):
                for c in range(4):
                    ACT(tailf[:, g0 + c, 0:nseq * 3].rearrange("p (b k) -> p b k", k=3),
                        pf(b, [4, T])[:, c, :].rearrange("p (b l) -> p b l", l=L)[:, :, L - 3:L], AF.Copy, [b], [tailf])
        for g0 in range(0, 12, 4):
            b = pget()
            for c in range(4):
                o = pf(b, [4, T])[:, c, :]
                if nseq > 1:
                    o = o.rearrange("p (b l) -> p b l", l=L)
                for k in range(4):
                    MM(o, DIAG[:, g0 + c, k, :], rawv[:, g0 + c, :, k:k + L] if nseq > 1 else raw[:, g0 + c, k:k + L],
                       [DIAG, raw], [b], start=(k == 0), stop=(k == 3))
            ACT(fT[:, g0:g0 + 4, :T], pf(b, [4, T]), AF.Silu, [b], [fT])
        if is_last:
            n3 = nseq * 3
            for g0 in range(0, 12, 4):
                b = pget()
                for c in range(4):
                    TR(pf(b, [4, 128], p=n3)[:, c, :], tailf[:, g0 + c, 0:n3], identf[:, :], [tailf, identf], [b])
                CP_(tailo[:n3, g0 * 128:(g0 + 4) * 128], b[:n3, :], [b], [tailo])
            dst = O["ngc_p"] if nseq == 1 else O["ngc_s"]
            for part in range(3):
                cs0 = part * 1024 + h0 * 128
                STO(dst[l][:, cs0:cs0 + 512], tailo[:n3, part * 512:(part + 1) * 512], tailo)

    def bvec(dst_ap, src_row_ap, buf):
        LD(dst_ap, src_row_ap.partition_broadcast(128), buf)

    tiles = [(CP, ti) for ti in range(limit.get("ntiles", NT))] + ([(CS, NT)] if limit.get("sample", True) else [])
    all_bufs.extend(allb)
    layer_in = None
    for l in range(limit.get("layers", DEPTH)):
        do_mod(l)
        S.barrier(all_bufs)
        mod_stacks.pop().close()
        for ph in limit.get("phases", (0, 1, 2)):
            acc_in = layer_in if ph == 0 else scr[(ph - 1)]
            last_phase = (l == DEPTH - 1 and ph == 2)
            acc_out = scr[ph] if ph < 2 else (scr[2] if not last_phase else None)
            S.barrier(all_bufs)
            pes = contextlib.ExitStack()
            cur_es[0] = pes
            if ph == 0:
                alloc_A(pes, "_%d_%d" % (l, ph))
            else:
                alloc_B(pes, "_%d_%d" % (l, ph))
            for v_ in list(PA.values()) + list(PB.values()):
                if v_ is not None and v_ not in all_bufs:
                    all_bufs.append(v_)
            if ph == 0:
                load_cols(PA["WB"], 0, I["w_in"][l][:, 0:2576], 2576)
                LD(nwT[:], I["ssd_norm_w"][l].rearrange("(k p) -> p k", p=128), nwT, nonc=True)
                load_wo(I["w_out"][l][0:1024, :], 8, lambda k: nwT[:, k:k + 1])
                build_diag(I["ssd_conv_w"][l], 0, 12)
                LD(cbrow_f[0:1, :], I["ssd_conv_b"][l:l + 1, :], cbrow_f)
                CP_(cbrow[:], cbrow_f[0:1, :], [cbrow_f], [cbrow])
                bvec(vec16[:, 0, :], I["ssd_dt_bias"][l:l + 1, :], vec16)
                bvec(vec16[:, 1, :], I["ssd_a_log"][l:l + 1, :], vec16)
                bvec(vec16[:, 2, :], I["ssd_d"][l:l + 1, :], vec16)
                ACT(vec16[:, 1, :], vec16[:, 1, :], AF.Exp, [vec16], [vec16])
                TS(vec16[:, 1, :], vec16[:, 1, :], -1.0, None, ALU.mult, ALU.bypass, [vec16], [vec16])
            else:
                h0 = (ph - 1) * 4
                load_cols(PB["WB"], 0, I["w_in"][l][:, OFF_ZG + h0 * 128:OFF_ZG + h0 * 128 + 512], 512)
                for part in range(3):
                    c0 = OFF_QKV + part * 1024 + h0 * 128
                    load_cols(PB["WB"], 512 + part * 512, I["w_in"][l][:, c0:c0 + 512], 512)
                load_cols(PB["WB"], 2048, I["w_in"][l][:, OFF_A + h0:OFF_A + h0 + 4], 4)
                load_cols(PB["WB"], 2052, I["w_in"][l][:, OFF_B + h0:OFF_B + h0 + 4], 4)
                LD(nwT[:, 0:1], I["gdn_norm_w"][l].rearrange("(p o) -> p o", o=1), nwT, nonc=True)
                load_wo(I["w_out"][l][1024 + h0 * 128:1024 + h0 * 128 + 512, :], 4, lambda k: nwT[:, 0:1])
                for part in range(3):
                    for k_ in range(4):
                        LD(cwT[:, part * 4:part * 4 + 4, k_],
                           I["gdn_conv_w"][l][k_, part * 1024 + h0 * 128:part * 1024 + h0 * 128 + 512].rearrange("(c p) -> p c", p=128), cwT, nonc=True)
                for c in range(12):
                    TT(DIAG[:, c, :, :], bm(identf[:], 4), bl(cwT[:, c, :], 128), ALU.mult, [identf, cwT], [DIAG], eng="pool")
                bvec(vec8[:, 0, :], I["gdn_dt_bias"][l:l + 1, :], vec8)
                bvec(vec8[:, 1, :], I["gdn_a_log"][l:l + 1, :], vec8)
                ACT(vec8[:, 1, :], vec8[:, 1, :], AF.Exp, [vec8], [vec8])
                TS(vec8[:, 1, :], vec8[:, 1, :], -1.0, None, ALU.mult, ALU.bypass, [vec8], [vec8])

            def issue_loads(idx):
                cfg, ti = tiles[idx]
                r0, T = rows(ti)
                xt = xt_slots[idx % 2]
                ap, dep = src_ap(layer_in.t if layer_in is not None else None, ti)
                LD(xt[:T, :], ap, xt, R=[(layer_in, ti)] if layer_in is not None else None)
                if ph > 0:
                    xa = xa_slots[idx % 2]
                    LD(xa[:T, :], acc_in.t[r0:r0 + T, :], xa, R=[(acc_in, ti)])

            if last_phase:
                fnw_bc = stage[0]
                fnw_v = stage[0][:, :, :].rearrange("p a b -> p (a b)")
                LD(fnw_v, I["final_norm_w"].partition_broadcast(128), stage[0])
            issue_loads(0)
            for idx, (cfg, ti) in enumerate(tiles):
                if idx + 1 < len(tiles):
                    issue_loads(idx + 1)
                r0, T = rows(ti)
                xt = xt_slots[idx % 2]
                xa = xa_slots[idx % 2] if ph > 0 else xt
                first = (ti == 0)
                is_last = (ti == NT - 1) or (ti == NT)
                if ph == 0:
                    phaseA_tile(cfg, l, ti, xt, first, is_last)
                else:
                    phaseB_tile(cfg, l, ti, xt, (ph - 1) * 4, first, is_last)
                TT(xo[:T, :], big[3][:T, :], xa[:T, :], ALU.add, [big[3], xa], [xo], eng="pool")
                if not last_phase:
                    STO(acc_out.t[r0:r0 + T, :], xo[:T, :], xo, W=[(acc_out, ti)])
                else:
                    ACT(junk[:T, :], xo[:T, :], AF.Square, [xo], [junk, sm1], accum_out=sm1[:T, 4:5])
                    TS(sm1[:T, 4:5], sm1[:T, 4:5], 1.0 / D, EPS, ALU.mult, ALU.add, [sm1], [sm1])
                    ACT(sm1[:T, 4:5], sm1[:T, 4:5], AF.Ln, [sm1], [sm1])
                    ACT(sm1[:T, 4:5], sm1[:T, 4:5], AF.Exp, [sm1], [sm1], scale=-0.5)
                    STT(big[0][:T, :], xo[:T, :], sm1[:T, 4:5], fnw_v[:T, :], ALU.mult, ALU.mult, [xo, sm1, fnw_bc], [big[0]])
                    dsto = O["y_p"][r0:r0 + T, :] if ti < NT else O["y_s"][:, :]
                    STO(dsto, big[0][:T, :], big[0])
            pes.close()
            cur_es[0] = es
            PA.clear()
            PB.clear()
        layer_in = scr[2]
    if FILL_TABLE and not limit.get("nofill"):
        fill_rhs = CS.cmf[:, :, :].rearrange("p c t -> p (c t)")[:, 0:512]
        S.filler = lambda e: e.matmul(banks[7][:, :], lhsT=identb[:, :], rhs=fill_rhs, start=True, stop=True)
        S.fill_table = {int(kv.split(":")[0]): int(kv.split(":")[1]) for kv in FILL_TABLE.split(",") if kv}
    S.emit()
    build.pe_groups = S.pe_groups
    es.close()
    return nc, dbg_out


_CACHE = {}


def make_in_maps(inputs):
    g = {k: np.ascontiguousarray(np.asarray(v, dtype=np.float32)) for k, v in inputs.items()}
    consts = host_consts()
    maps = []
    for c in range(8):
        m = {}
        m["xp"] = g["x_prompt"][c]
        m["xs"] = g["x_sample"][c * NSB:(c + 1) * NSB].reshape(NSB * SL, D)
        m["call"] = np.concatenate([g["c_prompt"][c:c + 1], g["c_sample"][c * NSB:(c + 1) * NSB]], axis=0)
        m["st_sc"] = g["state_ssd_conv"][:, c * NSB:(c + 1) * NSB]
        m["st_ssm"] = g["state_ssm"][:, c * NSB:(c + 1) * NSB]
        m["st_gc"] = g["state_gdn_conv"][:, c * NSB:(c + 1) * NSB]
        m["st_gdn"] = g["state_gdn"][:, c * NSB:(c + 1) * NSB]
        for k in ("norm_w", "w_ada", "b_ada", "w_in", "ssd_conv_w", "ssd_conv_b", "ssd_dt_bias", "ssd_a_log", "ssd_d",
                  "ssd_norm_w", "gdn_conv_w", "gdn_dt_bias", "gdn_a_log", "gdn_norm_w", "w_out"):
            m[k] = g[k]
        m["final_norm_w"] = g["final_norm_w"].reshape(1, D)
        for k, v in consts.items():
            m["c_" + k] = v
        maps.append({k: np.ascontiguousarray(v) for k, v in m.items()})
    return maps


def kernel(**inputs):
    if "nc" not in _CACHE:
        _CACHE["nc"] = build()[0]
    nc = _CACHE["nc"]
    maps = make_in_maps(inputs)
    res = run_bass_kernel_spmd(nc, maps, core_ids=list(range(8)))
    R = res.results
    cat = lambda k, ax: np.concatenate([np.asarray(r[k]) for r in R], axis=ax)
    y_p = np.stack([np.asarray(r["y_p"]) for r in R], 0)
    y_s = np.stack([np.asarray(r["y_s"]).reshape(NSB, SL, D) for r in R], 0).reshape(8 * NSB, SL, D)
    ncs_p = np.stack([np.asarray(r["ncs_p"]) for r in R], 1)
    nssm_p = np.stack([np.asarray(r["nssm_p"]).reshape(DEPTH, 16, 64, 128) for r in R], 1)
    ngc_p = np.stack([np.asarray(r["ngc_p"]) for r in R], 1)
    ngdn_p = np.stack([np.asarray(r["ngdn_p"]) for r in R], 1)
    ncs_s = np.concatenate([np.asarray(r["ncs_s"]).reshape(DEPTH, NSB, 3, 1536) for r in R], 1)
    nssm_s = np.concatenate([np.asarray(r["nssm_s"]).reshape(DEPTH, NSB, 16, 64, 128) for r in R], 1)
    ngc_s = np.concatenate([np.asarray(r["ngc_s"]).reshape(DEPTH, NSB, 3, 3072) for r in R], 1)
    ngdn_s = np.concatenate([np.asarray(r["ngdn_s"]) for r in R], 1)
    outs = (y_p, y_s, ncs_p, nssm_p, ngc_p, ngdn_p, ncs_s, nssm_s, ngc_s, ngdn_s)
    return tuple(np.ascontiguousarray(o, dtype=np.float32) for o in outs)
```
